# Optimizing a Trainium2 kernel written in Bass

```python
import math
import jax, jax.numpy as jnp
from jax import lax
import numpy as np

D_MODEL = 1024
BATCH = 32
SEQ = 256
DEPTH = 2
DEC_BATCH = 2
DEC_SEQ = 1024
PAST_LEN = 256

GRID_W = 64
N_EVEN = (DEPTH + 1) // 2
N_ODD = DEPTH // 2
A_WIDTH = D_MODEL // 2
A_HEADS = 4
A_HEAD_DIM = A_WIDTH // A_HEADS
A_HALF = A_HEAD_DIM // 2
B_WIDTH = D_MODEL - A_WIDTH
B_HEAD = 64
B_HEADS = B_WIDTH // B_HEAD
LORA_W = 64
LORA_A = 64
LORA_G = 128
IN_B = 3 * B_WIDTH + 2 * LORA_W + 2 * LORA_A + LORA_G
IN_EVEN = 3 * A_WIDTH + IN_B
C_GROUPS = 8
C_GROUP_DIM = D_MODEL // C_GROUPS
D_FF = 2816
ROPE_BASE = 10000.0
Q_BLOCK = 128
RMS_EPS = 1e-6
LNX_EPS = 64e-5

kernel_name = "hybrid_diffattn_rwkv7_fnet_prefix_step"


def rmsnorm(x, g):
    xf = x.astype(jnp.float32)
    y = xf * lax.rsqrt(jnp.mean(xf * xf, axis=-1, keepdims=True) + RMS_EPS)
    return y.astype(x.dtype) * g


def axial_rope_tables(rows):
    row = jnp.repeat(jnp.arange(rows, dtype=jnp.float32), GRID_W)
    col = jnp.tile(jnp.arange(GRID_W, dtype=jnp.float32), rows)
    n_freq = A_HALF // 4
    inv = ROPE_BASE ** (-jnp.arange(n_freq, dtype=jnp.float32) / n_freq)
    ang = jnp.concatenate([row[:, None] * inv, col[:, None] * inv], axis=-1)
    return jnp.cos(ang), jnp.sin(ang)


def apply_rope(x, cos, sin):
    xf = x.astype(jnp.float32)
    x1, x2 = xf[..., 0::2], xf[..., 1::2]
    c = cos[:, None, None, :]
    s = sin[:, None, None, :]
    out = jnp.stack([x1 * c - x2 * s, x1 * s + x2 * c], axis=-1).reshape(x.shape)
    return out.astype(x.dtype)


def diff_attention(q, k, v, lam):
    b, lq = q.shape[0], q.shape[1]
    nblk = lq // Q_BLOCK
    qb = q.reshape(b, nblk, Q_BLOCK, A_HEADS, 2, A_HALF).transpose(1, 0, 2, 3, 4, 5)
    scale = A_HALF ** -0.5

    def one_block(qblk):
        s = jnp.einsum("bqhmd,bkhmd->bmhqk", qblk, k).astype(jnp.float32) * scale
        p = jax.nn.softmax(s, axis=-1)
        w = p[:, 0] - lam * p[:, 1]
        return jnp.einsum("bhqk,bkhd->bqhd", w.astype(v.dtype), v)

    out = lax.map(one_block, qb)
    return out.transpose(1, 0, 2, 3, 4).reshape(b, lq, A_HEADS, A_HEAD_DIM)


def token_shift_centred(f, mu):
    prev = jnp.pad(f, ((0, 0), (1, 0), (0, 0)))[:, :-1]
    nxt = jnp.pad(f, ((0, 0), (0, 1), (0, 0)))[:, 1:]
    return f + mu[0] * (prev - f) + mu[1] * (nxt - f)


def wkv_scan(state0, r, w, k, v, kk, a):
    def step(S, inp):
        r_t, w_t, k_t, v_t, kk_t, a_t = inp
        sa = jnp.einsum("bhij,bhj->bhi", S, -kk_t)
        S = (S * w_t[:, :, None, :] + sa[..., None] * (kk_t * a_t)[:, :, None, :]
             + v_t[..., None] * k_t[:, :, None, :])
        return S, jnp.einsum("bhij,bhj->bhi", S, r_t)

    xs = tuple(jnp.moveaxis(t, 1, 0) for t in (r, w, k, v, kk, a))
    S, y = lax.scan(step, state0.astype(jnp.float32), xs)
    return S, jnp.moveaxis(y, 0, 1)


def rwkv7_bidir(fb, s0, e, P):
    b, L, _ = fb.shape
    fb = token_shift_centred(fb, P["rwkv_shift_mu"][e])
    r = fb[..., :B_WIDTH]
    k = fb[..., B_WIDTH:2 * B_WIDTH]
    v = fb[..., 2 * B_WIDTH:3 * B_WIDTH]
    off = 3 * B_WIDTH
    wd = fb[..., off:off + 2 * LORA_W].reshape(b, L, 2, LORA_W)
    off = off + 2 * LORA_W
    ad = fb[..., off:off + 2 * LORA_A].reshape(b, L, 2, LORA_A)
    off = off + 2 * LORA_A
    gd = fb[..., off:]
    g = jax.nn.sigmoid(gd) @ P["rwkv_g2"][e]
    kvec = P["rwkv_kvec"][e]

    def heads(t):
        return t.reshape(b, L, B_HEADS, B_HEAD).astype(jnp.float32)

    rh, vh = heads(r), heads(v)
    kk = heads(k * kvec[0])
    kk = kk * lax.rsqrt(jnp.sum(kk * kk, axis=-1, keepdims=True) + 1e-12)
    r_k = kvec[2].reshape(B_HEADS, B_HEAD).astype(jnp.float32)
    ys = 0.0
    bonus = 0.0
    finals = []
    for d in range(2):
        w_raw = (P["rwkv_w0"][e, d] + jnp.tanh(wd[:, :, d]) @ P["rwkv_w2"][e, d]).astype(jnp.float32)
        decay = jnp.exp(-jnp.exp(-jax.nn.softplus(-w_raw) - 0.5))
        a = jax.nn.sigmoid(P["rwkv_a0"][e, d] + ad[:, :, d] @ P["rwkv_a2"][e, d])
        kd = k * (1 + (a - 1) * kvec[1])
        ah, kdh, wh = heads(a), heads(kd), heads(decay)
        inputs = (rh, wh, kdh, vh, kk, ah)
        if d == 1:
            inputs = tuple(jnp.flip(t, axis=1) for t in inputs)
        S, y = wkv_scan(s0[:, d], *inputs)
        if d == 1:
            y = jnp.flip(y, axis=1)
        ys = ys + y
        bonus = bonus + jnp.sum(rh * kdh * r_k, axis=-1, keepdims=True) * vh
        finals.append(S)
    mu = jnp.mean(ys, axis=-1, keepdims=True)
    var = jnp.mean((ys - mu) ** 2, axis=-1, keepdims=True)
    yn = ((ys - mu) * lax.rsqrt(var + LNX_EPS)).reshape(b, L, B_WIDTH)
    lnx = P["rwkv_lnx"][e]
    y = (yn * lnx[0] + lnx[1] + bonus.reshape(b, L, B_WIDTH)).astype(fb.dtype) * g
    return y, jnp.stack(finals, axis=1).astype(fb.dtype)


def fourier_mix(h, w_out):
    b, L, _ = h.shape
    hg = h.reshape(b, L, C_GROUPS, C_GROUP_DIM).astype(jnp.float32)
    f = jnp.fft.fft2(hg, axes=(1, 3), norm="ortho").real
    return f.reshape(b, L, D_MODEL).astype(h.dtype) @ w_out


def conv_ffn(h, w_in, conv_w, conv_b, w_out):
    L = h.shape[1]
    ug = h @ w_in
    u, g = ug[..., :D_FF], ug[..., D_FF:]
    gp = jnp.pad(g, ((0, 0), (1, 1), (0, 0)))
    g = gp[:, :L] * conv_w[0] + gp[:, 1:L + 1] * conv_w[1] + gp[:, 2:] * conv_w[2] + conv_b
    return (jax.nn.silu(g) * u) @ w_out


def even_mixer(h, l, rope, ctx, P):
    e = l // 2
    b, L, _ = h.shape
    proj = h @ P["w_in_even"][e]
    q = proj[..., :A_WIDTH].reshape(b, L, A_HEADS, 2, A_HALF)
    k = proj[..., A_WIDTH:2 * A_WIDTH].reshape(b, L, A_HEADS, 2, A_HALF)
    v = proj[..., 2 * A_WIDTH:3 * A_WIDTH].reshape(b, L, A_HEADS, A_HEAD_DIM)
    fb = proj[..., 3 * A_WIDTH:]
    if ctx is None:
        keys, vals = k, v
        s0 = jnp.zeros((b, 2, B_HEADS, B_HEAD, B_HEAD), jnp.float32)
    else:
        ck, cv, cs = ctx
        cos, sin = rope
        q = apply_rope(q, cos, sin)
        k_lat = apply_rope(k, cos, sin)
        keys = jnp.concatenate([ck.reshape(b, -1, A_HEADS, 2, A_HALF), k_lat], axis=1)
        vals = jnp.concatenate([cv, v], axis=1)
        s0 = cs
    lp = P["diff_lambda"][e].astype(jnp.float32)
    lam_init = 0.8 - 0.6 * math.exp(-0.3 * l)
    lam = jnp.exp(jnp.sum(lp[0] * lp[1])) - jnp.exp(jnp.sum(lp[2] * lp[3])) + lam_init
    ya = diff_attention(q, keys, vals, lam)
    ya = rmsnorm(ya, P["diff_subln"][e]) * (1.0 - lam_init)
    yb, s_final = rwkv7_bidir(fb, s0, e, P)
    m = jnp.concatenate([ya.reshape(b, L, A_WIDTH), yb], axis=-1) @ P["w_out_even"][e]
    new_ctx = (k.reshape(b, L, A_HEADS, A_HEAD_DIM), v, s_final) if ctx is None else None
    return m, new_ctx


def trunk_layer(l, x, mod, rope, ctx, P):
    shift_m, scale_m, gate_m, shift_f, scale_f, gate_f = jnp.split(mod, 6, axis=-1)
    gains = P["norm_gains"][l]
    h = rmsnorm(x, gains[0]) * (1 + scale_m) + shift_m
    if l % 2 == 0:
        m, new_ctx = even_mixer(h, l, rope, ctx, P)
    else:
        m, new_ctx = fourier_mix(h, P["w_out_odd"][l // 2]), None
    x = x + gate_m * rmsnorm(m, gains[1])
    h = rmsnorm(x, gains[2]) * (1 + scale_f) + shift_f
    f = conv_ffn(h, P["w_ffn_in"][l], P["ffn_conv"][l], P["ffn_conv_b"][l], P["w_ffn_out"][l])
    x = x + gate_f * rmsnorm(f, gains[3])
    return x, new_ctx


def setup_inputs(seed: int = 0) -> dict:
    key = jax.random.key(seed)
    ks = jax.random.split(key, 32)
    D = D_MODEL

    def nrm(k, shape, s):
        return jax.random.normal(k, shape, jnp.float32) * s

    return {
        "x_prompt": nrm(ks[0], (BATCH, SEQ, D), 1.0),
        "x_sample": nrm(ks[1], (DEC_BATCH, DEC_SEQ, D), 1.0),
        "cache_k": nrm(ks[2], (DEC_BATCH, N_EVEN, PAST_LEN, A_HEADS, A_HEAD_DIM), 1.0),
        "cache_v": nrm(ks[3], (DEC_BATCH, N_EVEN, PAST_LEN, A_HEADS, A_HEAD_DIM), 1.0),
        "state_wkv": nrm(ks[4], (DEC_BATCH, N_EVEN, 2, B_HEADS, B_HEAD, B_HEAD), 0.5),
        "c": nrm(ks[5], (DEC_BATCH, D), 1.0),
        "c_ctx": nrm(ks[6], (D,), 1.0),
        "w_ada": nrm(ks[7], (DEPTH, D, 6 * D), 0.5 * D ** -0.5),
        "b_ada": nrm(ks[8], (DEPTH, 6 * D), 0.02),
        "norm_gains": 1.0 + nrm(ks[9], (DEPTH, 4, D), 0.05),
        "w_in_even": nrm(ks[10], (N_EVEN, D, IN_EVEN), D ** -0.5),
        "w_out_even": nrm(ks[11], (N_EVEN, D, D), D ** -0.5),
        "diff_lambda": nrm(ks[12], (N_EVEN, 4, A_HALF), 0.1),
        "diff_subln": 1.0 + nrm(ks[13], (N_EVEN, A_HEAD_DIM), 0.05),
        "rwkv_shift_mu": jax.random.uniform(ks[14], (N_EVEN, 2, IN_B), jnp.float32, 0.0, 0.5),
        "rwkv_w0": jax.random.uniform(ks[15], (N_EVEN, 2, B_WIDTH), jnp.float32, -6.0, -1.0),
        "rwkv_w2": nrm(ks[16], (N_EVEN, 2, LORA_W, B_WIDTH), 0.1),
        "rwkv_a0": nrm(ks[17], (N_EVEN, 2, B_WIDTH), 0.1),
        "rwkv_a2": nrm(ks[18], (N_EVEN, 2, LORA_A, B_WIDTH), 0.1),
        "rwkv_g2": nrm(ks[19], (N_EVEN, LORA_G, B_WIDTH), LORA_G ** -0.5),
        "rwkv_kvec": jnp.array([0.85, 1.0, 0.0], jnp.float32)[None, :, None]
        + nrm(ks[20], (N_EVEN, 3, B_WIDTH), 0.1),
        "rwkv_lnx": jnp.array([1.0, 0.0], jnp.float32)[None, :, None]
        + nrm(ks[21], (N_EVEN, 2, B_WIDTH), 0.05),
        "w_out_odd": nrm(ks[22], (N_ODD, D, D), D ** -0.5),
        "w_ffn_in": nrm(ks[23], (DEPTH, D, 2 * D_FF), D ** -0.5),
        "ffn_conv": nrm(ks[24], (DEPTH, 3, D_FF), 3 ** -0.5),
        "ffn_conv_b": nrm(ks[25], (DEPTH, D_FF), 0.02),
        "w_ffn_out": nrm(ks[26], (DEPTH, D_FF, D), D_FF ** -0.5),
    }


def reference(x_prompt, x_sample, cache_k, cache_v, state_wkv, c, c_ctx, w_ada, b_ada,
              norm_gains, w_in_even, w_out_even, diff_lambda, diff_subln, rwkv_shift_mu,
              rwkv_w0, rwkv_w2, rwkv_a0, rwkv_a2, rwkv_g2, rwkv_kvec, rwkv_lnx, w_out_odd,
              w_ffn_in, ffn_conv, ffn_conv_b, w_ffn_out):
    P = {
        "norm_gains": norm_gains, "w_in_even": w_in_even, "w_out_even": w_out_even,
        "diff_lambda": diff_lambda, "diff_subln": diff_subln, "rwkv_shift_mu": rwkv_shift_mu,
        "rwkv_w0": rwkv_w0, "rwkv_w2": rwkv_w2, "rwkv_a0": rwkv_a0, "rwkv_a2": rwkv_a2,
        "rwkv_g2": rwkv_g2, "rwkv_kvec": rwkv_kvec, "rwkv_lnx": rwkv_lnx,
        "w_out_odd": w_out_odd, "w_ffn_in": w_ffn_in, "ffn_conv": ffn_conv,
        "ffn_conv_b": ffn_conv_b, "w_ffn_out": w_ffn_out,
    }
    x = x_prompt
    ks, vs, ss = [], [], []
    for l in range(DEPTH):
        mod = (jax.nn.silu(c_ctx) @ w_ada[l] + b_ada[l])[None, None, :]
        x, new_ctx = trunk_layer(l, x, mod, None, None, P)
        if new_ctx is not None:
            ks.append(new_ctx[0])
            vs.append(new_ctx[1])
            ss.append(new_ctx[2])
    y_prompt = x
    new_cache_k = jnp.stack(ks, axis=1)
    new_cache_v = jnp.stack(vs, axis=1)
    new_state_wkv = jnp.stack(ss, axis=1)

    rows = x_sample.shape[1] // GRID_W
    rope = axial_rope_tables(rows)
    x = x_sample
    for l in range(DEPTH):
        mod = (jax.nn.silu(c) @ w_ada[l] + b_ada[l])[:, None, :]
        if l % 2 == 0:
            e = l // 2
            ctx = (cache_k[:, e], cache_v[:, e], state_wkv[:, e])
        else:
            ctx = None
        x, _ = trunk_layer(l, x, mod, rope, ctx, P)
    y_sample = x
    return (y_prompt, y_sample, new_cache_k, new_cache_v, new_state_wkv)
```

```python
import contextlib
import math
import numpy as np
import concourse.bass as bass
import concourse.mybir as mybir
from concourse.bass_utils import run_bass_kernel_spmd

F32 = mybir.dt.float32
BF16 = mybir.dt.bfloat16
AF = mybir.ActivationFunctionType
ALU = mybir.AluOpType
AX = mybir.AxisListType

T = 1280
NT = 10
U = 5
D = 1024
KC = 8
DFF = 2816
FC = 22
NCORES = 8
ARENA_W = 27100
EXPM05 = math.exp(-0.5)


class Res:
    __slots__ = ("name", "writer", "readers", "excl")

    def __init__(self, name, excl=False):
        self.name = name
        self.writer = None
        self.readers = []
        self.excl = excl


class Sched:
    ENGS = ("pe", "act", "dve", "pool", "sp")
    NDMA = 6

    def __init__(self, nc):
        self.nc = nc
        self.prog = {e: [] for e in self.ENGS}
        self.signal = {e: set() for e in self.ENGS}
        self.ndma = {e: 0 for e in self.ENGS}

    def _collect(self, reads, writes, eng=None):
        deps = []
        for r in reads:
            if r.writer is not None:
                deps.append(r.writer)
            if r.excl:
                deps.extend(t for t in r.readers if t[1] != eng)
        for w in writes:
            if w.writer is not None:
                deps.append(w.writer)
            deps.extend(w.readers)
        return deps

    def _commit(self, tok, reads, writes):
        for r in reads:
            r.readers.append(tok)
        for w in writes:
            w.writer = tok
            w.readers = []

    def op(self, eng, fn, reads=(), writes=()):
        deps = self._collect(reads, writes, eng)
        idx = len(self.prog[eng])
        if eng == "pe":
            deps = [d for d in deps if not (d[0] == "c" and d[1] == "pe")]
        for d in deps:
            if d[0] == "c":
                self.signal[d[1]].add(d[2])
        self.prog[eng].append(dict(fn=fn, deps=deps, kind="c"))
        tok = ("c", eng, idx)
        self._commit(tok, reads, writes)
        return tok

    def dma(self, eng, out, in_, reads=(), writes=()):
        deps = self._collect(reads, writes, eng)
        n = self.ndma[eng]
        self.ndma[eng] += 1
        if n >= self.NDMA:
            deps.append(("d", eng, n - self.NDMA))
        for d in deps:
            if d[0] == "c":
                self.signal[d[1]].add(d[2])
        self.prog[eng].append(dict(out=out, in_=in_, deps=deps, kind="d", n=n))
        tok = ("d", eng, n)
        self._commit(tok, reads, writes)
        return tok

    def emit(self):
        nc = self.nc
        with contextlib.ExitStack() as st:
            csem = {e: st.enter_context(nc.semaphore("c_" + e)) for e in self.ENGS}
            dsem = {e: [st.enter_context(nc.semaphore("d_%s_%d" % (e, i))) for i in range(self.NDMA)]
                    for e in self.ENGS if self.ndma[e] > 0}
            sigval = {}
            for e in self.ENGS:
                cnt = 0
                m = {}
                for i in range(len(self.prog[e])):
                    if i in self.signal[e]:
                        cnt += 1
                        m[i] = cnt
                sigval[e] = m

            def resolve(tok):
                if tok[0] == "c":
                    return csem[tok[1]], sigval[tok[1]][tok[2]], ("c", tok[1])
                e, n = tok[1], tok[2]
                return dsem[e][n % self.NDMA], 16 * (n // self.NDMA + 1), ("d", e, n % self.NDMA)

            def run_engine(e, h):
                waited = {}
                for i, ins in enumerate(self.prog[e]):
                    need = {}
                    for d in ins["deps"]:
                        sem, val, key = resolve(d)
                        if waited.get(key, 0) >= val:
                            continue
                        if key not in need or need[key][1] < val:
                            need[key] = (sem, val)
                    for key, (sem, val) in need.items():
                        h.wait_ge(sem, val)
                        waited[key] = val
                    if ins["kind"] == "c":
                        bi = ins["fn"](h)
                        if i in self.signal[e]:
                            bi.then_inc(csem[e], 1)
                    else:
                        n = ins["n"]
                        h.dma_start(out=ins["out"], in_=ins["in_"]).then_inc(dsem[e][n % self.NDMA], 16)
                if self.ndma[e] > 0:
                    n = self.ndma[e]
                    for slot in range(self.NDMA):
                        cnt = (n - slot + self.NDMA - 1) // self.NDMA if n > slot else 0
                        if cnt > 0:
                            h.wait_ge(dsem[e][slot], 16 * cnt)

            with nc.Block() as block:
                @block.tensor
                def _(eng):
                    run_engine("pe", eng)

                @block.scalar
                def _(eng):
                    run_engine("act", eng)

                @block.vector
                def _(eng):
                    run_engine("dve", eng)

                @block.gpsimd
                def _(eng):
                    run_engine("pool", eng)

                @block.sync
                def _(eng):
                    run_engine("sp", eng)


def colp_layout():
    off = {}
    n = 0

    def add(name, cols):
        nonlocal n
        off[name] = n
        n += cols
    for l in range(2):
        add("bada%d" % l, 48)
        for g in range(4):
            add("gain%d_%d" % (l, g), 8)
        for k in range(3):
            add("conv%d_%d" % (l, k), FC)
        add("convb%d" % l, FC)
    add("mu0", 15)
    add("mu1", 15)
    for d in range(2):
        add("w0_%d" % d, 4)
        add("a0_%d" % d, 4)
    for k in range(3):
        add("kvec%d" % k, 4)
    return off, n


def rowp_layout():
    off = {}
    n = 0

    def add(name, cols):
        nonlocal n
        off[name] = n
        n += cols
    add("lam", 256)
    add("subln", 128)
    add("abias", 30)
    add("bndL", 5)
    add("bndR", 5)
    return off, n


def cols_of(vec):
    v = np.asarray(vec, np.float32).reshape(-1, 128)
    return np.ascontiguousarray(v.T)


def build(stage=99):
    nc = bass.Bass("TRN2", target_bir_lowering=False)
    COFF, NCOL = colp_layout()
    ROFF, NROW = rowp_layout()

    def din(name, shape):
        return nc.dram_tensor(name, list(shape), F32, kind="ExternalInput").ap()

    def dout(name, shape):
        return nc.dram_tensor(name, list(shape), F32, kind="ExternalOutput").ap()

    x_in = din("x_in", [T, D])
    condT = din("condT", [128, 16])
    colp = din("colp", [128, NCOL])
    rowp = din("rowp", [1, NROW])
    w_ada = din("w_ada", [2, D, 6 * D])
    w_in_even = din("w_in_even", [D, 3456])
    w_out_even = din("w_out_even", [D, D])
    w_out_odd = din("w_out_odd", [D, D])
    w_ffn_in = din("w_ffn_in", [2, D, 2 * DFF])
    w_ffn_out = din("w_ffn_out", [2, DFF, D])
    w2t_d = din("w2t", [128, 512])
    a2t_d = din("a2t", [128, 512])
    g2_d = din("g2", [128, 512])
    cacheKT = din("cacheKT", [512, 256])
    cacheV = din("cacheV", [256, 512])
    ropeC = din("ropeC", [128, T])
    ropeS = din("ropeS", [128, T])
    initS = din("initS", [2, 128, 256])
    dftC = din("dftC", [T, T])
    dftSn = din("dftSn", [T, T])
    chCS = din("chCS", [128, 256])
    cmat = din("cmat", [4, 128, 128])
    lnx_d = din("lnx", [2, 512])

    y_out = dout("y_out", [T, D])
    k_out = dout("k_out", [T, 512])
    v_out = dout("v_out", [T, 512])
    s_out = dout("s_out", [U, 2, 128, 256])

    import os
    sub = os.environ.get("SUB", "abcmqv")
    with contextlib.ExitStack() as st:
        s = Sched(nc)

        def sb(name, shape, dt=F32):
            return st.enter_context(nc.sbuf_tensor(name, list(shape), dt))

        x_sb = sb("x_sb", [128, NT, D])
        Rx = [Res("x%d" % i) for i in range(NT)]
        WB = 4096
        wbuf = [sb("wb%d" % i, [128, WB], BF16) for i in range(3)]
        Rw = [Res("wb%d" % i) for i in range(3)]
        wctr = [0]
        colp_sb = sb("colp_sb", [128, NCOL])
        Rcolp = Res("colp")
        rowp_sb = sb("rowp_sb", [128, NROW])
        Rrowp = Res("rowp")
        ident_f = sb("ident_f", [128, 128])
        ident_b = sb("ident_b", [128, 128], BF16)
        rotT_b = sb("rotT_b", [128, 128], BF16)
        bones_f = sb("bones_f", [128, 128])
        halfsel_f = sb("halfsel_f", [128, 128])
        Rconst = Res("const")
        gg_sb = sb("gg_sb", [128, 2, 2, D], BF16)
        Rgg = Res("gg")
        scond = sb("scond", [128, 16], BF16)
        condf = sb("condf", [128, 16])
        Rcond = Res("cond")
        modcol = sb("modcol", [128, 2, 96])
        Rmod = Res("modcol")
        scsh = sb("scsh", [128, 2, 2, 2, 16])
        Rscsh = Res("scsh")
        stat = sb("stat", [128, 64])
        Rstat = Res("stat")
        junk_b = sb("junk_b", [128, D], BF16)
        Rjunk = Res("junk")
        xn_b = sb("xn_b", [128, D], BF16)
        Rxn = Res("xn")
        stage_f = sb("stage_f", [128, 2, 512])
        Rstage = [Res("stage0"), Res("stage1")]
        stctr = [0]
        arena = sb("arena", [128, ARENA_W])
        Rar = {}

        def ares(name):
            if name not in Rar:
                Rar[name] = Res("ar_" + name)
            return Rar[name]

        def carve(off_w, nelem, dt):
            if dt == F32:
                return arena[:, off_w:off_w + nelem]
            return arena[:, off_w:off_w + (nelem + 1) // 2].bitcast(BF16)[:, 0:nelem]

        ps = [st.enter_context(nc.psum_tensor("ps%d" % b, [128, 512], F32)) for b in range(8)]
        psb = [p.bitcast(BF16) for p in ps]
        Rps = [Res("ps%d" % b, excl=True) for b in range(8)]
        bctr = [0]

        reserved = set()

        def bank():
            while True:
                b = bctr[0] % 8
                bctr[0] += 1
                if b not in reserved:
                    return b

        def barrier():
            allr = list(Rar.values()) + list(Rw)
            s.op("dve", lambda e: e.memset(stat[:, 63:64], 0.0), writes=allr + [Rstat])

        def wload(src_ap, kc, ncols):
            i = wctr[0] % 3
            wctr[0] += 1
            view = wbuf[i][:, 0:kc * ncols].rearrange("p (c n) -> p c n", c=kc)
            s.dma("pool", view, src_ap.rearrange("(c p) n -> p c n", p=128), writes=[Rw[i]])
            return view, Rw[i]

        s.dma("sp", colp_sb[:], colp, writes=[Rcolp])
        s.dma("sp", rowp_sb[:], rowp[0, :].partition_broadcast(128), writes=[Rrowp])
        s.dma("sp", ident_f[:], cmat[0], writes=[Rconst])
        s.dma("sp", bones_f[:], cmat[2], writes=[Rconst])
        s.dma("sp", halfsel_f[:], cmat[3], writes=[Rconst])
        s.dma("pool", ident_b[:], cmat[0], writes=[Rconst])
        s.dma("pool", rotT_b[:], cmat[1], writes=[Rconst])
        s.dma("sp", condf[:], condT, writes=[Rcond])
        for i in range(NT):
            s.dma("sp", x_sb[:, i, :], x_in[i * 128:(i + 1) * 128, :], writes=[Rx[i]])
        s.op("act", lambda e: e.activation(scond[:], condf[:], AF.Silu), reads=[Rcond], writes=[Rcond])

        def cp(name, c=0, n=1):
            o = COFF[name] + c
            return colp_sb[:, o:o + n]

        for l in range(2):
            b = bank()
            for v in range(6):
                for half in range(2):
                    wv, rw = wload(w_ada[l][:, v * 1024 + half * 512: v * 1024 + half * 512 + 512], KC, 512)
                    for j in range(4):
                        col = (v * 8 + half * 4 + j) * 2
                        for kc in range(KC):
                            s.op("pe", lambda e, wv=wv, j=j, kc=kc, col=col, b=b: e.matmul(
                                ps[b][:, col:col + 2], wv[:, kc, j * 128:(j + 1) * 128],
                                scond[:, kc * 2:kc * 2 + 2], start=(kc == 0), stop=(kc == KC - 1)),
                                reads=[rw, Rcond], writes=[Rps[b]])
            s.op("dve", lambda e, l=l, b=b: e.tensor_tensor(
                modcol[:, l, :].rearrange("p (c a) -> p c a", a=2),
                ps[b][:, 0:96].rearrange("p (c a) -> p c a", a=2),
                cp("bada%d" % l, 0, 48).unsqueeze(2).to_broadcast([128, 48, 2]), ALU.add),
                reads=[Rps[b], Rcolp], writes=[Rmod])

            def mv(v, l=l):
                return modcol[:, l, v * 16:(v + 1) * 16].rearrange("p (c a) -> p c a", a=2)

            def gain(g, l=l):
                return cp("gain%d_%d" % (l, g), 0, 8).unsqueeze(2).to_broadcast([128, 8, 2])
            for wi, (vs, vsh, g) in enumerate([(1, 0, 0), (4, 3, 2)]):
                sc = scsh[:, l, wi, 0, :].rearrange("p (c a) -> p c a", a=2)
                sh = scsh[:, l, wi, 1, :].rearrange("p (c a) -> p c a", a=2)
                s.op("dve", lambda e, sc=sc, vs=vs, mv=mv: e.tensor_scalar(sc, mv(vs), 1.0, None, ALU.add),
                     reads=[Rmod], writes=[Rscsh])
                s.op("dve", lambda e, sc=sc, g=g, gain=gain: e.tensor_tensor(sc, sc, gain(g), ALU.mult),
                     reads=[Rscsh, Rcolp], writes=[Rscsh])
                s.op("dve", lambda e, sh=sh, vsh=vsh, mv=mv: e.tensor_copy(sh, mv(vsh)),
                     reads=[Rmod], writes=[Rscsh])

        ggcol = sb("ggcol", [128, 2, 16])
        Rggcol = Res("ggcol")

        def make_gg(l):
            for wi, (vg, g) in enumerate([(2, 1), (5, 3)]):
                gc = ggcol[:, wi, :].rearrange("p (c a) -> p c a", a=2)
                s.op("dve", lambda e, gc=gc, vg=vg, g=g, l=l: e.tensor_tensor(
                    gc, modcol[:, l, vg * 16:(vg + 1) * 16].rearrange("p (c a) -> p c a", a=2),
                    cp("gain%d_%d" % (l, g), 0, 8).unsqueeze(2).to_broadcast([128, 8, 2]), ALU.mult),
                    reads=[Rmod, Rcolp], writes=[Rggcol])
                for a in range(2):
                    for hh in range(2):
                        b = bank()
                        for j in range(4):
                            c = hh * 4 + j
                            s.op("pe", lambda e, b=b, wi=wi, c=c, a=a, j=j: e.matmul(
                                ps[b][:, j * 128:(j + 1) * 128],
                                ggcol[:, wi, c * 2 + a:c * 2 + a + 1].to_broadcast([128, 128]),
                                ident_f[:], start=True, stop=True),
                                reads=[Rggcol, Rconst], writes=[Rps[b]])
                        s.op("act", lambda e, b=b, wi=wi, a=a, hh=hh: e.copy(
                            gg_sb[:, wi, a, hh * 512:(hh + 1) * 512], ps[b][:]),
                            reads=[Rps[b]], writes=[Rgg])

        hT = carve(0, 8 * T, BF16).rearrange("p (c t) -> p c t", c=8)
        RhT = ares("hT")

        def ab_of_tile(i):
            return 0 if i < 8 else 1

        def make_hT(l, wi):
            for i in range(NT):
                a = ab_of_tile(i)
                s.op("act", lambda e, i=i: e.activation(junk_b[:], x_sb[:, i, :], AF.Square, scale=1.0 / 32.0,
                                                        accum_out=stat[:, 0:1]),
                     reads=[Rx[i]], writes=[Rjunk, Rstat])
                s.op("dve", lambda e: e.tensor_scalar(stat[:, 1:2], stat[:, 0:1], 1e-6, None, ALU.add),
                     reads=[Rstat], writes=[Rstat])
                s.op("act", lambda e: e.activation(stat[:, 1:2], stat[:, 1:2], AF.Sqrt), reads=[Rstat], writes=[Rstat])
                s.op("dve", lambda e: e.reciprocal(stat[:, 2:3], stat[:, 1:2]), reads=[Rstat], writes=[Rstat])
                s.op("dve", lambda e, i=i: e.tensor_scalar(xn_b[:], x_sb[:, i, :], stat[:, 2:3], None, ALU.mult),
                     reads=[Rx[i], Rstat], writes=[Rxn])
                b = bank()
                for c in range(8):
                    s.op("pe", lambda e, b=b, c=c: e.transpose(psb[b][:, c * 128:(c + 1) * 128],
                                                               xn_b[:, c * 128:(c + 1) * 128], ident_b[:]),
                         reads=[Rxn, Rconst], writes=[Rps[b]])
                for c in range(8):
                    s.op("act", lambda e, b=b, c=c, i=i, a=a: e.activation(
                        hT[:, c, i * 128:(i + 1) * 128], psb[b][:, c * 128:(c + 1) * 128], AF.Identity,
                        bias=scsh[:, l, wi, 1, c * 2 + a:c * 2 + a + 1],
                        scale=scsh[:, l, wi, 0, c * 2 + a:c * 2 + a + 1]),
                        reads=[Rps[b], Rscsh], writes=[RhT])

        TOKCH = [(0, 512), (512, 512), (1024, 256)]

        def linear_fm(wv, rw, ncol0, actT, Ract, kc_n, evac):
            for (t0, tn) in TOKCH:
                b = bank()
                for kc in range(kc_n):
                    s.op("pe", lambda e, b=b, kc=kc, t0=t0, tn=tn: e.matmul(
                        ps[b][:, 0:tn], wv[:, kc, ncol0:ncol0 + 128], actT[:, kc, t0:t0 + tn],
                        start=(kc == 0), stop=(kc == kc_n - 1)),
                        reads=[rw, Ract], writes=[Rps[b]])
                evac(b, t0, tn)

        def linear_tm(wv, rw, ncols, actT, Ract, kc_n, i, b, col0=0):
            for kc in range(kc_n):
                s.op("pe", lambda e, kc=kc: e.matmul(
                    ps[b][:, col0:col0 + ncols], actT[:, kc, i * 128:(i + 1) * 128], wv[:, kc, 0:ncols],
                    start=(kc == 0), stop=(kc == kc_n - 1)),
                    reads=[rw, Ract], writes=[Rps[b]])

        def residual_update(i, banks, wi, fsrc=None, Rf=None):
            a = ab_of_tile(i)
            if fsrc is None:
                for hh, b in enumerate(banks):
                    s.op("act", lambda e, b=b, hh=hh: e.activation(junk_b[:, 0:512], ps[b][:], AF.Square,
                                                                   scale=1.0 / 32.0, accum_out=stat[:, 8 + hh:9 + hh]),
                         reads=[Rps[b]], writes=[Rjunk, Rstat])
                s.op("dve", lambda e: e.tensor_tensor(stat[:, 10:11], stat[:, 8:9], stat[:, 9:10], ALU.add),
                     reads=[Rstat], writes=[Rstat])
            else:
                s.op("act", lambda e: e.activation(junk_b[:], fsrc, AF.Square, scale=1.0 / 32.0,
                                                   accum_out=stat[:, 10:11]),
                     reads=[Rf], writes=[Rjunk, Rstat])
            s.op("dve", lambda e: e.tensor_scalar(stat[:, 11:12], stat[:, 10:11], 1e-6, None, ALU.add),
                 reads=[Rstat], writes=[Rstat])
            s.op("act", lambda e: e.activation(stat[:, 11:12], stat[:, 11:12], AF.Sqrt), reads=[Rstat], writes=[Rstat])
            s.op("dve", lambda e: e.reciprocal(stat[:, 12:13], stat[:, 11:12]), reads=[Rstat], writes=[Rstat])
            for hh in range(2):
                src = ps[banks[hh]][:] if fsrc is None else fsrc[:, hh * 512:(hh + 1) * 512]
                rr = [Rps[banks[hh]]] if fsrc is None else [Rf]
                si = stctr[0] % 2
                stctr[0] += 1
                s.op("dve", lambda e, src=src, hh=hh, si=si: e.scalar_tensor_tensor(
                    stage_f[:, si, :], src, stat[:, 12:13], gg_sb[:, wi, a, hh * 512:(hh + 1) * 512],
                    ALU.mult, ALU.mult),
                    reads=rr + [Rstat, Rgg], writes=[Rstage[si]])
                s.op("dve", lambda e, hh=hh, si=si, i=i: e.tensor_tensor(
                    x_sb[:, i, hh * 512:(hh + 1) * 512], x_sb[:, i, hh * 512:(hh + 1) * 512], stage_f[:, si, :], ALU.add),
                    reads=[Rstage[si], Rx[i]], writes=[Rx[i]])

        actT = carve(5120, FC * T, BF16).rearrange("p (c t) -> p c t", c=FC)
        RactT = ares("actT")
        graw = carve(19200, T + 2, F32)
        Rgraw = ares("graw")
        cbuf = carve(19200 + 1284, T, F32)
        Rcbuf = ares("cbuf")
        sbuf_s = carve(19200 + 1284 + 1280, T, F32)
        Rsbuf = ares("sbuf_s")
        fstageA = carve(19200, 5 * D, F32).rearrange("p (i n) -> p i n", i=5)
        fstageB = carve(0, 5 * D, F32).rearrange("p (i n) -> p i n", i=5)

        def fst(i):
            return fstageA[:, i, :] if i < 5 else fstageB[:, i - 5, :]

        def Rfst_of(i):
            return ares("fstage") if i < 5 else RhT
        bcorr = sb("bcorr", [128, 16])
        Rbcorr = Res("bcorr")

        def ffn(l):
            make_hT(l, 1)
            s.op("dve", lambda e: e.memset(graw[:, 0:1], 0.0), writes=[Rgraw])
            s.op("dve", lambda e: e.memset(graw[:, T + 1:T + 2], 0.0), writes=[Rgraw])
            for blk in range(0, FC, 2):
                wu, ru = wload(w_ffn_in[l][:, blk * 128: blk * 128 + 256], KC, 256)
                wg, rg = wload(w_ffn_in[l][:, DFF + blk * 128: DFF + blk * 128 + 256], KC, 256)
                for jj in range(2):
                    fc = blk + jj
                    def evac_g(b, t0, tn):
                        s.op("act", lambda e, b=b, t0=t0, tn=tn: e.copy(graw[:, 1 + t0:1 + t0 + tn], ps[b][:, 0:tn]),
                             reads=[Rps[b]], writes=[Rgraw])
                    linear_fm(wg, rg, jj * 128, hT, RhT, KC, evac_g)
                    w0 = cp("conv%d_0" % l, fc)
                    w1 = cp("conv%d_1" % l, fc)
                    w2 = cp("conv%d_2" % l, fc)
                    cb = cp("convb%d" % l, fc)
                    s.op("act", lambda e, w1=w1, cb=cb: e.activation(cbuf[:], graw[:, 1:T + 1], AF.Identity, bias=cb, scale=w1),
                         reads=[Rgraw, Rcolp], writes=[Rcbuf])
                    s.op("dve", lambda e, w0=w0: e.scalar_tensor_tensor(cbuf[:], graw[:, 0:T], w0, cbuf[:], ALU.mult, ALU.add),
                         reads=[Rgraw, Rcolp, Rcbuf], writes=[Rcbuf])
                    s.op("dve", lambda e, w2=w2: e.scalar_tensor_tensor(cbuf[:], graw[:, 2:T + 2], w2, cbuf[:], ALU.mult, ALU.add),
                         reads=[Rgraw, Rcolp, Rcbuf], writes=[Rcbuf])
                    gprev = graw[:, 256:256 + 1024].rearrange("p (u k) -> p u k", k=256)[:, :, 0]
                    gnext = graw[:, 257:257 + 1024].rearrange("p (u k) -> p u k", k=256)[:, :, 0]
                    s.op("dve", lambda e, gprev=gprev: e.tensor_tensor(bcorr[:, 0:4], gprev, rowp_sb[:, ROFF["bndL"] + 1:ROFF["bndL"] + 5], ALU.mult),
                         reads=[Rgraw, Rrowp], writes=[Rbcorr])
                    s.op("dve", lambda e, w0=w0: e.tensor_scalar(bcorr[:, 0:4], bcorr[:, 0:4], w0, None, ALU.mult),
                         reads=[Rbcorr, Rcolp], writes=[Rbcorr])
                    c_at = cbuf[:, 256:256 + 1024].rearrange("p (u k) -> p u k", k=256)[:, :, 0]
                    s.op("dve", lambda e, c_at=c_at: e.tensor_tensor(c_at, c_at, bcorr[:, 0:4], ALU.subtract),
                         reads=[Rbcorr, Rcbuf], writes=[Rcbuf])
                    s.op("dve", lambda e, gnext=gnext: e.tensor_tensor(bcorr[:, 4:8], gnext, rowp_sb[:, ROFF["bndL"] + 1:ROFF["bndL"] + 5], ALU.mult),
                         reads=[Rgraw, Rrowp], writes=[Rbcorr])
                    s.op("dve", lambda e, w2=w2: e.tensor_scalar(bcorr[:, 4:8], bcorr[:, 4:8], w2, None, ALU.mult),
                         reads=[Rbcorr, Rcolp], writes=[Rbcorr])
                    c_at2 = cbuf[:, 255:255 + 1024].rearrange("p (u k) -> p u k", k=256)[:, :, 0]
                    s.op("dve", lambda e, c_at2=c_at2: e.tensor_tensor(c_at2, c_at2, bcorr[:, 4:8], ALU.subtract),
                         reads=[Rbcorr, Rcbuf], writes=[Rcbuf])
                    s.op("act", lambda e: e.activation(sbuf_s[:], cbuf[:], AF.Silu), reads=[Rcbuf], writes=[Rsbuf])
                    def evac_u(b, t0, tn, fc=fc):
                        s.op("dve", lambda e, b=b, t0=t0, tn=tn: e.tensor_tensor(
                            actT[:, fc, t0:t0 + tn], ps[b][:, 0:tn], sbuf_s[:, t0:t0 + tn], ALU.mult),
                            reads=[Rps[b], Rsbuf], writes=[RactT])
                    linear_fm(wu, ru, jj * 128, hT, RhT, KC, evac_u)
            for nb in range(8):
                wv, rw = wload(w_ffn_out[l][:, nb * 128:(nb + 1) * 128], FC, 128)
                for i in range(NT):
                    b = bank()
                    linear_tm(wv, rw, 128, actT, RactT, FC, i, b)
                    s.op("act", lambda e, b=b, i=i, nb=nb: e.copy(fst(i)[:, nb * 128:(nb + 1) * 128], ps[b][:, 0:128]),
                         reads=[Rps[b]], writes=[Rfst_of(i)])
            for i in range(NT):
                residual_update(i, None, 1, fsrc=fst(i), Rf=Rfst_of(i))

        qT = carve(5120, 4 * T, BF16).rearrange("p (c t) -> p c t", c=4)
        RqT = ares("qT")
        kT = carve(7680, 4 * 1536, BF16).rearrange("p (c t) -> p c t", c=4)
        RkT = ares("kT")
        Vaug = carve(10752, 12 * 4 * 144, BF16).rearrange("p (k h d) -> p k h d", k=12, h=4)
        RV = ares("Vaug")
        yT = carve(0, 8 * T, BF16).rearrange("p (c t) -> p c t", c=8)
        fbT = carve(14208, 15 * (T + 2), BF16).rearrange("p (c t) -> p c t", c=15)
        RfbT = ares("fbT")
        RyT = RhT
        ropeC_sb = carve(23824, T, F32)
        ropeS_sb = carve(23824 + T, T, F32)
        Rrope = Res("rope")
        rtmp = sb("rtmp", [128, 2, 512])
        Rrtmp = Res("rtmp")
        raw_b = sb("raw_b", [128, 512], BF16)
        Rraw = Res("raw_b")

        def even_inproj():
            s.dma("sp", ropeC_sb[:], ropeC, writes=[Rrope])
            s.dma("sp", ropeS_sb[:], ropeS, writes=[Rrope])
            s.dma("pool", kT[:, :, 0:256], cacheKT.rearrange("(h p) t -> p h t", p=128), writes=[RkT])
            for kk_ in range(2):
                s.dma("pool", Vaug[:, kk_, :, 0:128],
                      cacheV[kk_ * 128:(kk_ + 1) * 128, :].rearrange("p (h d) -> p h d", h=4), writes=[RV])
            if "m" in sub:
                s.op("pool", lambda e: e.memset(Vaug[:, :, :, 128:129], 1.0), writes=[RV])
            for which, dst, doff in ((0, qT, 0), (1, kT, 256)):
                if "q" not in sub:
                    break
                wv, rw = wload(w_in_even[:, which * 512:(which + 1) * 512], KC, 512)
                Rdst = RqT if which == 0 else RkT
                for h in range(4):
                    def evac(b, t0, tn, h=h, dst=dst, doff=doff, Rdst=Rdst):
                        s.op("act", lambda e, b=b, tn=tn: e.copy(raw_b[:, 0:tn], ps[b][:, 0:tn]),
                             reads=[Rps[b]], writes=[Rraw])
                        b2 = bank()
                        s.op("pe", lambda e, b2=b2, tn=tn: e.matmul(ps[b2][:, 0:tn], rotT_b[:], raw_b[:, 0:tn], start=True, stop=True),
                             reads=[Rraw, Rconst], writes=[Rps[b2]])
                        s.op("dve", lambda e, t0=t0, tn=tn: e.tensor_tensor(rtmp[:, 0, 0:tn], raw_b[:, 0:tn], ropeC_sb[:, t0:t0 + tn], ALU.mult),
                             reads=[Rraw, Rrope], writes=[Rrtmp])
                        s.op("dve", lambda e, b2=b2, t0=t0, tn=tn: e.tensor_tensor(rtmp[:, 1, 0:tn], ps[b2][:, 0:tn], ropeS_sb[:, t0:t0 + tn], ALU.mult),
                             reads=[Rps[b2], Rrope], writes=[Rrtmp])
                        s.op("dve", lambda e, t0=t0, tn=tn: e.tensor_tensor(dst[:, h, doff + t0:doff + t0 + tn], rtmp[:, 0, 0:tn], rtmp[:, 1, 0:tn], ALU.add),
                             reads=[Rrtmp], writes=[Rdst])
                    linear_fm(wv, rw, h * 128, hT, RhT, KC, evac)
                if which == 1:
                    for i in range(NT):
                        b = bank()
                        linear_tm(wv, rw, 512, hT, RhT, KC, i, b)
                        si = stctr[0] % 2
                        stctr[0] += 1
                        s.op("act", lambda e, b=b, si=si: e.copy(stage_f[:, si, :], ps[b][:]), reads=[Rps[b]], writes=[Rstage[si]])
                        s.dma("sp", k_out[i * 128:(i + 1) * 128, :], stage_f[:, si, :], reads=[Rstage[si]])
            s.op("pool", lambda e: e.memset(fbT[:, :, 0:1], 0.0), writes=[RfbT])
            s.op("pool", lambda e: e.memset(fbT[:, :, T + 1:T + 2], 0.0), writes=[RfbT])
            for pc, (c0, ncols) in enumerate([(1536, 512), (2048, 512), (2560, 512), (3072, 384)]):
                wv, rw = wload(w_in_even[:, c0:c0 + ncols], KC, ncols)
                for j in range(ncols // 128):
                    ch = pc * 4 + j

                    def evac_fb(b, t0, tn, ch=ch):
                        s.op("act", lambda e, b=b, t0=t0, tn=tn: e.copy(fbT[:, ch, 1 + t0:1 + t0 + tn], ps[b][:, 0:tn]),
                             reads=[Rps[b]], writes=[RfbT])
                    linear_fm(wv, rw, j * 128, hT, RhT, KC, evac_fb)
            wv, rw = wload(w_in_even[:, 1024:1536], KC, 512)
            for i in range(NT if "v" in sub else 0):
                b = bank()
                linear_tm(wv, rw, 512, hT, RhT, KC, i, b)
                si = stctr[0] % 2
                stctr[0] += 1
                s.op("act", lambda e, b=b, si=si: e.copy(stage_f[:, si, :], ps[b][:]), reads=[Rps[b]], writes=[Rstage[si]])
                s.dma("sp", v_out[i * 128:(i + 1) * 128, :], stage_f[:, si, :], reads=[Rstage[si]])
                s.op("dve", lambda e, si=si, i=i: e.tensor_copy(Vaug[:, 2 + i, :, 0:128], stage_f[:, si, :].rearrange("p (h d) -> p h d", h=4)),
                     reads=[Rstage[si]], writes=[RV])


        PT = [[sb("PT%d%d" % (m, k), [128, 256], BF16) for k in range(2)] for m in range(2)]
        RPT = [[Res("PT%d%d" % (m, k)) for k in range(2)] for m in range(2)]
        ya_f = sb("ya_f", [128, 128])
        Rya = Res("ya_f")
        ya_b = sb("ya_b", [128, 128], BF16)
        Ryab = Res("ya_b")
        subln08 = sb("subln08", [128, 128])
        Rsub = Res("subln08")
        lamt = sb("lamt", [128, 64])
        Rlamt = Res("lamt")

        def attention():
            lo = ROFF["lam"]
            for k in range(2):
                s.op("dve", lambda e, k=k: e.tensor_tensor(lamt[:], rowp_sb[:, lo + 128 * k: lo + 128 * k + 64],
                                                          rowp_sb[:, lo + 128 * k + 64: lo + 128 * k + 128], ALU.mult),
                     reads=[Rrowp], writes=[Rlamt])
                s.op("dve", lambda e, k=k: e.reduce_sum(stat[:, 20 + k:21 + k], lamt[:], axis=AX.X),
                     reads=[Rlamt], writes=[Rstat])
            s.op("act", lambda e: e.activation(stat[:, 22:24], stat[:, 20:22], AF.Exp), reads=[Rstat], writes=[Rstat])
            s.op("dve", lambda e: e.tensor_tensor(stat[:, 24:25], stat[:, 22:23], stat[:, 23:24], ALU.subtract),
                 reads=[Rstat], writes=[Rstat])
            s.op("dve", lambda e: e.tensor_scalar(stat[:, 25:26], stat[:, 24:25], 0.2, -1.0, ALU.add, ALU.mult),
                 reads=[Rstat], writes=[Rstat])
            s.op("dve", lambda e: e.tensor_scalar(subln08[:], rowp_sb[:, ROFF["subln"]:ROFF["subln"] + 128], 0.8, None, ALU.mult),
                 reads=[Rrowp], writes=[Rsub])
            pctr = [0, 0]
            for h in range(4):
                for qu in range(5):
                    acc = [bank(), bank()]
                    for kt in range(12):
                        ku = 0 if kt < 2 else 1 + (kt - 2) // 2
                        bcol = ROFF["abias"] + ku * 5 + qu
                        for m in range(2):
                            bs = bank()
                            while bs in acc:
                                bs = bank()
                            s.op("pe", lambda e, bs=bs, m=m, h=h, kt=kt, qu=qu: e.matmul(
                                ps[bs][:, 0:256], kT[64 * m:64 * m + 64, h, kt * 128:(kt + 1) * 128],
                                qT[64 * m:64 * m + 64, h, qu * 256:(qu + 1) * 256], start=True, stop=True),
                                reads=[RkT, RqT], writes=[Rps[bs]])
                            pk = pctr[m] % 2
                            pctr[m] += 1
                            s.op("act", lambda e, bs=bs, m=m, pk=pk, bcol=bcol: e.activation(
                                PT[m][pk][:], ps[bs][:, 0:256], AF.Exp, bias=rowp_sb[:, bcol:bcol + 1], scale=0.125),
                                reads=[Rps[bs], Rrowp], writes=[RPT[m][pk]])
                            for qt in range(2):
                                s.op("pe", lambda e, m=m, pk=pk, qt=qt, kt=kt, h=h, acc=acc: e.matmul(
                                    ps[acc[m]][:, qt * 129:qt * 129 + 129], PT[m][pk][:, qt * 128:(qt + 1) * 128],
                                    Vaug[:, kt, h, 0:129], start=(kt == 0 and qt == 0), stop=(kt == 11),
                                    skip_group_check=True),
                                    reads=[RPT[m][pk], RV], writes=[Rps[acc[m]]])
                    for qt in range(2):
                        i = qu * 2 + qt
                        c0 = qt * 129
                        s.op("dve", lambda e, c0=c0, acc=acc: e.reciprocal(stat[:, 30:31], ps[acc[0]][:, c0 + 128:c0 + 129]),
                             reads=[Rps[acc[0]]], writes=[Rstat])
                        s.op("dve", lambda e, c0=c0, acc=acc: e.reciprocal(stat[:, 31:32], ps[acc[1]][:, c0 + 128:c0 + 129]),
                             reads=[Rps[acc[1]]], writes=[Rstat])
                        s.op("dve", lambda e: e.tensor_tensor(stat[:, 32:33], stat[:, 31:32], stat[:, 25:26], ALU.mult),
                             reads=[Rstat], writes=[Rstat])
                        s.op("dve", lambda e, c0=c0, acc=acc: e.tensor_scalar(ya_f[:], ps[acc[0]][:, c0:c0 + 128], stat[:, 30:31], None, ALU.mult),
                             reads=[Rps[acc[0]], Rstat], writes=[Rya])
                        s.op("dve", lambda e, c0=c0, acc=acc: e.scalar_tensor_tensor(ya_f[:], ps[acc[1]][:, c0:c0 + 128], stat[:, 32:33], ya_f[:], ALU.mult, ALU.add),
                             reads=[Rps[acc[1]], Rstat, Rya], writes=[Rya])
                        s.op("act", lambda e: e.activation(junk_b[:, 0:128], ya_f[:], AF.Square, scale=1.0 / math.sqrt(128.0),
                                                           accum_out=stat[:, 33:34]),
                             reads=[Rya], writes=[Rjunk, Rstat])
                        s.op("dve", lambda e: e.tensor_scalar(stat[:, 34:35], stat[:, 33:34], 1e-6, None, ALU.add),
                             reads=[Rstat], writes=[Rstat])
                        s.op("act", lambda e: e.activation(stat[:, 34:35], stat[:, 34:35], AF.Sqrt), reads=[Rstat], writes=[Rstat])
                        s.op("dve", lambda e: e.reciprocal(stat[:, 35:36], stat[:, 34:35]), reads=[Rstat], writes=[Rstat])
                        s.op("dve", lambda e: e.scalar_tensor_tensor(ya_b[:], ya_f[:], stat[:, 35:36], subln08[:], ALU.mult, ALU.mult),
                             reads=[Rya, Rstat, Rsub], writes=[Ryab])
                        bt = bank()
                        while bt in acc:
                            bt = bank()
                        s.op("pe", lambda e, bt=bt: e.transpose(psb[bt][:, 0:128], ya_b[:], ident_b[:]),
                             reads=[Ryab, Rconst], writes=[Rps[bt]])
                        s.op("act", lambda e, bt=bt, h=h, i=i: e.copy(yT[:, h, i * 128:(i + 1) * 128], psb[bt][:, 0:128]),
                             reads=[Rps[bt]], writes=[RyT])

        def out_proj(W, actTv, Ract):
            pieces = [wload(W[:, hh * 512:(hh + 1) * 512], KC, 512) for hh in range(2)]
            for i in range(NT):
                bb = [bank(), bank()]
                for hh in range(2):
                    linear_tm(pieces[hh][0], pieces[hh][1], 512, actTv, Ract, KC, i, bb[hh])
                residual_update(i, bb, 0)

        Xc = carve(5120, NT * 8 * 256, BF16).rearrange("p (i g n) -> p i g n", i=NT, g=8)
        RXc = ares("Xc")
        chCS_b = sb("chCS_b", [128, 256], BF16)
        Rch = Res("chCS")

        def fnet():
            s.dma("pool", chCS_b[:], chCS, writes=[Rch])
            for i in range(NT):
                for gp in range(4):
                    b = bank()
                    for g2_ in range(2):
                        g = gp * 2 + g2_
                        s.op("pe", lambda e, b=b, g=g, g2_=g2_, i=i: e.matmul(
                            ps[b][:, g2_ * 256:(g2_ + 1) * 256], hT[:, g, i * 128:(i + 1) * 128], chCS_b[:], start=True, stop=True),
                            reads=[RhT, Rch], writes=[Rps[b]])
                    s.op("act", lambda e, b=b, gp=gp, i=i: e.copy(
                        Xc[:, i, gp * 2:gp * 2 + 2, :], ps[b][:].rearrange("p (g n) -> p g n", g=2)),
                        reads=[Rps[b]], writes=[RXc])
            for (t0, tn) in [(0, 384), (384, 384), (768, 384), (1152, 128)]:
                wc, rc = wload(dftC[:, t0:t0 + tn], NT, tn)
                wsn, rsn = wload(dftSn[:, t0:t0 + tn], NT, tn)
                for g in range(8):
                    b = bank()
                    for tc in range(NT):
                        s.op("pe", lambda e, b=b, g=g, tc=tc, tn=tn, wc=wc: e.matmul(
                            ps[b][:, 0:tn], Xc[:, tc, g, 0:128], wc[:, tc, 0:tn], start=(tc == 0), stop=False),
                            reads=[RXc, rc], writes=[Rps[b]])
                        s.op("pe", lambda e, b=b, g=g, tc=tc, tn=tn, wsn=wsn: e.matmul(
                            ps[b][:, 0:tn], Xc[:, tc, g, 128:256], wsn[:, tc, 0:tn], start=False, stop=(tc == NT - 1)),
                            reads=[RXc, rsn], writes=[Rps[b]])
                    s.op("act", lambda e, b=b, g=g, t0=t0, tn=tn: e.copy(yT[:, g, t0:t0 + tn], ps[b][:, 0:tn]),
                         reads=[Rps[b]], writes=[RyT])


        V_tm = carve(5120, NT * 512, BF16).rearrange("p (i n) -> p i n", i=NT)
        RVtm = ares("V_tm")
        YSb = carve(7680, NT * 512, BF16).rearrange("p (i n) -> p i n", i=NT)
        RYS = ares("YSb")
        Ystage = carve(10240, 128 * 16, F32)[0:64, :].rearrange("p (n k) -> p n k", k=16)
        RYst = ares("Ystage")
        Sst = [carve(12288, 512, F32), carve(12800, 512, F32)]
        RS = [ares("S0"), ares("S1")]
        tmpA = carve(13312, 512, F32)
        RtA = ares("tmpA")
        tmpB = carve(23824, 512, F32)
        RtB = ares("tmpB")
        w2t_b = carve(24336, 512, BF16)
        a2t_b = carve(24592, 512, BF16)
        g2_b = carve(24848, 512, BF16)
        Rlora = ares("lora")
        PT0 = 25104
        NPT = 11

        def pt(k, n=1):
            return carve(PT0 + 128 * k, 128 * n, F32)
        Rpt = [ares("pt%d" % k) for k in range(NPT)]
        lnx0_b = carve(26512, 512, BF16)
        lnx1_b = carve(26768, 512, BF16)
        Rlnx = ares("lnx")
        tb0 = junk_b[:, 0:128]
        tb1 = junk_b[:, 128:256]
        Rtb0 = Res("tb0")
        Rtb1 = Res("tb1")
        colx = sb("colx", [128, 64])
        Rcolx = Res("colx")
        tiny = sb("tiny", [128, 8])
        Rtiny = Res("tiny")
        BS = sb("BS", [128, NT, 8])
        RBS = Res("BS")
        wbf = [w[:].bitcast(F32) for w in wbuf]
        TW = wbf[0][:, 0:1024].rearrange("p (k n) -> p k n", k=8)
        TKK = wbf[0][:, 1024:2048].rearrange("p (k n) -> p k n", k=8)
        TNK = wbf[1][:, 0:1024].rearrange("p (k n) -> p k n", k=8)
        TKD = wbf[1][:, 1024:2048].rearrange("p (k n) -> p k n", k=8)
        TR2 = wbf[2][:, 0:2048].rearrange("p (k n h) -> p k n h", k=8, h=2)
        CX = dict(cmu=0, nmu0=15, nmu1=30, omk1=45, keepL=49, keepR=54)

        def cx(name, c=0):
            o = CX[name] + c
            return colx[:, o:o + 1]

        def rwkv_setup():
            s.dma("pool", w2t_b, w2t_d, writes=[Rlora])
            s.dma("pool", a2t_b, a2t_d, writes=[Rlora])
            s.dma("pool", g2_b, g2_d, writes=[Rlora])
            s.dma("pool", lnx0_b, lnx_d[0, :].partition_broadcast(128), writes=[Rlnx])
            s.dma("pool", lnx1_b, lnx_d[1, :].partition_broadcast(128), writes=[Rlnx])
            mu0 = cp("mu0", 0, 15)
            mu1 = cp("mu1", 0, 15)
            s.op("dve", lambda e: e.tensor_tensor(colx[:, 0:15], mu0, mu1, ALU.add), reads=[Rcolp], writes=[Rcolx])
            s.op("dve", lambda e: e.tensor_scalar(colx[:, 0:15], colx[:, 0:15], -1.0, 1.0, ALU.mult, ALU.add),
                 reads=[Rcolx], writes=[Rcolx])
            s.op("dve", lambda e: e.tensor_scalar(colx[:, 15:30], mu0, -1.0, None, ALU.mult), reads=[Rcolp], writes=[Rcolx])
            s.op("dve", lambda e: e.tensor_scalar(colx[:, 30:45], mu1, -1.0, None, ALU.mult), reads=[Rcolp], writes=[Rcolx])
            s.op("dve", lambda e: e.tensor_scalar(colx[:, 45:49], cp("kvec1", 0, 4), -1.0, 1.0, ALU.mult, ALU.add),
                 reads=[Rcolp], writes=[Rcolx])
            s.op("dve", lambda e: e.tensor_scalar(colx[:, 49:54], rowp_sb[:, ROFF["bndL"]:ROFF["bndL"] + 5], -1.0, 1.0, ALU.mult, ALU.add),
                 reads=[Rrowp], writes=[Rcolx])
            s.op("dve", lambda e: e.tensor_scalar(colx[:, 54:59], rowp_sb[:, ROFF["bndR"]:ROFF["bndR"] + 5], -1.0, 1.0, ALU.mult, ALU.add),
                 reads=[Rrowp], writes=[Rcolx])
            s.op("dve", lambda e: e.memset(BS[:], 0.0), writes=[RBS])

        def shift(ch, ti, out, Rout):
            t0 = ti * 128
            f = fbT[:, ch, 1 + t0:1 + t0 + 128]
            fp = fbT[:, ch, t0:t0 + 128]
            fn = fbT[:, ch, 2 + t0:2 + t0 + 128]
            rr = [RfbT, Rcolp, Rcolx]
            s.op("dve", lambda e: e.tensor_scalar(out, f, cx("cmu", ch), None, ALU.mult), reads=rr, writes=[Rout])
            s.op("dve", lambda e: e.scalar_tensor_tensor(out, fp, cp("mu0", ch), out, ALU.mult, ALU.add),
                 reads=rr + [Rout], writes=[Rout])
            s.op("dve", lambda e: e.scalar_tensor_tensor(out, fn, cp("mu1", ch), out, ALU.mult, ALU.add),
                 reads=rr + [Rout], writes=[Rout])
            u = ti // 2
            if ti % 2 == 0:
                bl = rowp_sb[:, ROFF["bndL"] + u:ROFF["bndL"] + u + 1]
                s.op("dve", lambda e: e.tensor_tensor(tiny[:, 0:1], fbT[:, ch, t0:t0 + 1], bl, ALU.mult),
                     reads=[RfbT, Rrowp], writes=[Rtiny])
                s.op("dve", lambda e: e.scalar_tensor_tensor(out[:, 0:1], tiny[:, 0:1], cx("nmu0", ch), out[:, 0:1], ALU.mult, ALU.add),
                     reads=[Rtiny, Rcolx, Rout], writes=[Rout])
            else:
                br = rowp_sb[:, ROFF["bndR"] + u:ROFF["bndR"] + u + 1]
                s.op("dve", lambda e: e.tensor_tensor(tiny[:, 1:2], fbT[:, ch, 1 + t0 + 128:2 + t0 + 128], br, ALU.mult),
                     reads=[RfbT, Rrowp], writes=[Rtiny])
                s.op("dve", lambda e: e.scalar_tensor_tensor(out[:, 127:128], tiny[:, 1:2], cx("nmu1", ch), out[:, 127:128], ALU.mult, ALU.add),
                     reads=[Rtiny, Rcolx, Rout], writes=[Rout])

        def rwkv_prepass():
            for ti in range(NT):
                for c in range(4):
                    shift(8 + c, ti, pt(c), Rpt[c])
                    s.op("act", lambda e, c=c: e.copy(xn_b[:, c * 128:(c + 1) * 128], pt(c)), reads=[Rpt[c]], writes=[Rxn])
                b = bank()
                for c in range(4):
                    s.op("pe", lambda e, b=b, c=c: e.transpose(psb[b][:, c * 128:(c + 1) * 128], xn_b[:, c * 128:(c + 1) * 128], ident_b[:]),
                         reads=[Rxn, Rconst], writes=[Rps[b]])
                s.op("act", lambda e, b=b, ti=ti: e.copy(V_tm[:, ti, :], psb[b][:, 0:512]), reads=[Rps[b]], writes=[RVtm])

        def prep(d, ti):
            rev = (d == 1)

            def tab(T3, c):
                v = T3[:, d * 4 + c, :]
                return v[:, ::-1] if rev else v
            tabs_w = [Rw[0], Rw[1], Rw[2]]
            shift(12, ti, pt(0), Rpt[0])
            shift(13, ti, pt(1), Rpt[1])
            s.op("act", lambda e: e.activation(tb0, pt(0), AF.Tanh), reads=[Rpt[0]], writes=[Rtb0])
            s.op("act", lambda e: e.copy(tb1, pt(1)), reads=[Rpt[1]], writes=[Rtb1])
            lo, hi = 64 * d, 64 * d + 64
            for c in range(4):
                b = bank()
                s.op("pe", lambda e, b=b, c=c: e.matmul(ps[b][:, 0:128], w2t_b[lo:hi, c * 128:(c + 1) * 128], tb0[lo:hi, :], start=True, stop=True),
                     reads=[Rlora, Rtb0], writes=[Rps[b]])
                s.op("act", lambda e, b=b, c=c: e.activation(pt(2), ps[b][:, 0:128], AF.Sigmoid, bias=cp("w0_%d" % d, c)),
                     reads=[Rps[b], Rcolp], writes=[Rpt[2]])
                s.op("act", lambda e, c=c: e.activation(tab(TW, c), pt(2), AF.Exp, scale=-EXPM05),
                     reads=[Rpt[2]], writes=[Rw[0]])
                b = bank()
                s.op("pe", lambda e, b=b, c=c: e.matmul(ps[b][:, 0:128], a2t_b[lo:hi, c * 128:(c + 1) * 128], tb1[lo:hi, :], start=True, stop=True),
                     reads=[Rlora, Rtb1], writes=[Rps[b]])
                s.op("act", lambda e, b=b, c=c: e.activation(pt(3), ps[b][:, 0:128], AF.Sigmoid, bias=cp("a0_%d" % d, c)),
                     reads=[Rps[b], Rcolp], writes=[Rpt[3]])
                shift(4 + c, ti, pt(4), Rpt[4])
                s.op("dve", lambda e, c=c: e.tensor_scalar(pt(5), pt(4), cp("kvec0", c), None, ALU.mult),
                     reads=[Rpt[4], Rcolp], writes=[Rpt[5]])
                s.op("act", lambda e: e.activation(pt(6), pt(5), AF.Square), reads=[Rpt[5]], writes=[Rpt[6]])
                b = bank()
                s.op("pe", lambda e, b=b: e.matmul(ps[b][:, 0:128], bones_f[:], pt(6), start=True, stop=True),
                     reads=[Rconst, Rpt[6]], writes=[Rps[b]])
                s.op("dve", lambda e, b=b: e.tensor_scalar(pt(7), ps[b][:, 0:128], 1e-12, None, ALU.add),
                     reads=[Rps[b]], writes=[Rpt[7]])
                s.op("act", lambda e: e.activation(pt(7), pt(7), AF.Sqrt), reads=[Rpt[7]], writes=[Rpt[7]])
                s.op("dve", lambda e: e.reciprocal(pt(7), pt(7)), reads=[Rpt[7]], writes=[Rpt[7]])
                s.op("dve", lambda e: e.tensor_tensor(pt(8), pt(5), pt(7), ALU.mult), reads=[Rpt[5], Rpt[7]], writes=[Rpt[8]])
                s.op("act", lambda e, c=c: e.copy(tab(TKK, c), pt(8)), reads=[Rpt[8]], writes=[Rw[0]])
                s.op("dve", lambda e, c=c: e.scalar_tensor_tensor(tab(TNK, c), pt(8), -1.0, pt(3), ALU.mult, ALU.mult),
                     reads=[Rpt[8], Rpt[3]], writes=[Rw[1]])
                s.op("dve", lambda e, c=c: e.tensor_scalar(pt(9), pt(3), cp("kvec1", c), cx("omk1", c), ALU.mult, ALU.add),
                     reads=[Rpt[3], Rcolp, Rcolx], writes=[Rpt[9]])
                s.op("dve", lambda e: e.tensor_tensor(pt(9), pt(4), pt(9), ALU.mult), reads=[Rpt[4], Rpt[9]], writes=[Rpt[9]])
                s.op("act", lambda e, c=c: e.copy(tab(TKD, c), pt(9)), reads=[Rpt[9]], writes=[Rw[1]])
                shift(c, ti, pt(10), Rpt[10])
                for h2 in range(2):
                    o = TR2[:, d * 4 + c, :, h2]
                    if rev:
                        o = o[:, ::-1]
                    s.op("dve", lambda e, o=o, h2=h2: e.tensor_scalar(o, pt(10), halfsel_f[:, h2:h2 + 1], None, ALU.mult),
                         reads=[Rpt[10], Rconst], writes=[Rw[2]])
                s.op("dve", lambda e: e.tensor_tensor(pt(6), pt(10), pt(9), ALU.mult), reads=[Rpt[10], Rpt[9]], writes=[Rpt[6]])
                s.op("dve", lambda e, c=c: e.tensor_scalar(pt(6), pt(6), cp("kvec2", c), None, ALU.mult),
                     reads=[Rpt[6], Rcolp], writes=[Rpt[6]])
                b = bank()
                s.op("pe", lambda e, b=b: e.matmul(ps[b][:, 0:2], pt(6), halfsel_f[:, 0:2], start=True, stop=True),
                     reads=[Rpt[6], Rconst], writes=[Rps[b]])
                s.op("dve", lambda e, b=b, c=c, ti=ti: e.tensor_tensor(BS[:, ti, c * 2:c * 2 + 2], BS[:, ti, c * 2:c * 2 + 2], ps[b][:, 0:2], ALU.add),
                     reads=[Rps[b], RBS], writes=[RBS])

        yb = xn_b[:, 0:512]

        def finalize(ti, by):
            ys = pt(0, 4)
            Rys = [Rpt[0], Rpt[1], Rpt[2], Rpt[3]]
            t2 = pt(4, 4)
            Rt2 = [Rpt[4], Rpt[5], Rpt[6], Rpt[7]]
            ys3 = ys.rearrange("p (h i) -> p h i", h=8)
            t23 = t2.rearrange("p (h i) -> p h i", h=8)
            s.op("dve", lambda e: e.tensor_tensor(ys, YSb[:, ti, :], ps[by][:], ALU.add), reads=[RYS, Rps[by]], writes=Rys)
            s.op("dve", lambda e: e.reduce_sum(tiny[:, 0:8], ys3, axis=AX.X), reads=Rys, writes=[Rtiny])
            s.op("dve", lambda e: e.tensor_scalar(tiny[:, 0:8], tiny[:, 0:8], -1.0 / 64.0, None, ALU.mult), reads=[Rtiny], writes=[Rtiny])
            s.op("dve", lambda e: e.tensor_tensor(ys3, ys3, tiny[:, 0:8].unsqueeze(2).to_broadcast([128, 8, 64]), ALU.add),
                 reads=Rys + [Rtiny], writes=Rys)
            s.op("dve", lambda e: e.tensor_tensor(t2, ys, ys, ALU.mult), reads=Rys, writes=Rt2)
            s.op("dve", lambda e: e.reduce_sum(stat[:, 40:48], t23, axis=AX.X), reads=Rt2, writes=[Rstat])
            s.op("dve", lambda e: e.tensor_scalar(stat[:, 40:48], stat[:, 40:48], 1.0 / 64.0, 64e-5, ALU.mult, ALU.add),
                 reads=[Rstat], writes=[Rstat])
            s.op("act", lambda e: e.activation(stat[:, 40:48], stat[:, 40:48], AF.Sqrt), reads=[Rstat], writes=[Rstat])
            s.op("dve", lambda e: e.reciprocal(stat[:, 48:56], stat[:, 40:48]), reads=[Rstat], writes=[Rstat])
            s.op("dve", lambda e: e.tensor_tensor(ys3, ys3, stat[:, 48:56].unsqueeze(2).to_broadcast([128, 8, 64]), ALU.mult),
                 reads=Rys + [Rstat], writes=Rys)
            s.op("dve", lambda e: e.tensor_tensor(ys, ys, lnx0_b, ALU.mult), reads=Rys + [Rlnx], writes=Rys)
            s.op("dve", lambda e: e.tensor_tensor(ys, ys, lnx1_b, ALU.add), reads=Rys + [Rlnx], writes=Rys)
            s.op("dve", lambda e: e.tensor_tensor(t23, V_tm[:, ti, :].rearrange("p (h i) -> p h i", h=8),
                                                  BS[:, ti, :].unsqueeze(2).to_broadcast([128, 8, 64]), ALU.mult),
                 reads=[RVtm, RBS], writes=Rt2)
            s.op("dve", lambda e: e.tensor_tensor(ys, ys, t2, ALU.add), reads=Rys + Rt2, writes=Rys)
            shift(14, ti, pt(8), Rpt[8])
            s.op("act", lambda e: e.activation(tb0, pt(8), AF.Sigmoid), reads=[Rpt[8]], writes=[Rtb0])
            bg = bank()
            s.op("pe", lambda e, bg=bg: e.matmul(ps[bg][:], tb0, g2_b, start=True, stop=True),
                 reads=[Rtb0, Rlora], writes=[Rps[bg]])
            s.op("dve", lambda e, bg=bg: e.tensor_tensor(yb, ys, ps[bg][:], ALU.mult), reads=Rys + [Rps[bg]], writes=[Rxn])
            bt = bank()
            for c in range(4):
                s.op("pe", lambda e, bt=bt, c=c: e.transpose(psb[bt][:, c * 128:(c + 1) * 128], yb[:, c * 128:(c + 1) * 128], ident_b[:]),
                     reads=[Rxn, Rconst], writes=[Rps[bt]])
            s.op("act", lambda e, bt=bt, ti=ti: e.copy(yT[:, 4:8, ti * 128:(ti + 1) * 128],
                                                        psb[bt][:, 0:512].rearrange("p (c t) -> p c t", c=4)),
                 reads=[Rps[bt]], writes=[RyT])

        def rwkv_scan(nrounds=NT):
            S8 = [x_.rearrange("p (k i) -> p k i", k=8) for x_ in Sst]
            tA8 = tmpA.rearrange("p (k i) -> p k i", k=8)
            tB8 = tmpB.rearrange("p (k i) -> p k i", k=8)

            def bc(T3, n):
                return T3[:, :, n].unsqueeze(2).to_broadcast([128, 8, 64])
            for r in range(nrounds):
                tf, tbk = r, NT - 1 - r
                prep(0, tf)
                prep(1, tbk)
                if tf % 2 == 0:
                    u = tf // 2
                    if u == 0:
                        s.dma("sp", Sst[0][:, 0:256], initS[0], writes=[RS[0]])
                    else:
                        s.op("dve", lambda e, u=u: e.tensor_scalar(Sst[0][:, 0:256], Sst[0][:, 0:256], cx("keepL", u), None, ALU.mult),
                             reads=[RS[0], Rcolx], writes=[RS[0]])
                if tbk % 2 == 1:
                    u = tbk // 2
                    if u == 4:
                        s.op("dve", lambda e: e.memset(Sst[0][:, 256:512], 0.0), writes=[RS[0]])
                    elif u == 3:
                        s.dma("sp", tmpB[:, 0:256], initS[1], writes=[RtB])
                        s.op("dve", lambda e, u=u: e.scalar_tensor_tensor(Sst[0][:, 256:512], Sst[0][:, 256:512], cx("keepR", u),
                                                                          tmpB[:, 0:256], ALU.mult, ALU.add),
                             reads=[RS[0], Rcolx, RtB], writes=[RS[0]])
                    else:
                        s.op("dve", lambda e, u=u: e.tensor_scalar(Sst[0][:, 256:512], Sst[0][:, 256:512], cx("keepR", u), None, ALU.mult),
                             reads=[RS[0], Rcolx], writes=[RS[0]])
                by_ = None
                for n in range(128):
                    ci, ni = n % 2, (n + 1) % 2
                    Sc, Sn = S8[ci], S8[ni]
                    if n % 32 == 0:
                        by_ = bank()
                        reserved.add(by_)
                    s.op("dve", lambda e, n=n, Sc=Sc: e.tensor_tensor(tA8, Sc, bc(TKK, n), ALU.mult),
                         reads=[RS[ci], Rw[0]], writes=[RtA])
                    s.op("dve", lambda e, n=n, Sc=Sc, Sn=Sn: e.tensor_tensor(Sn, Sc, bc(TW, n), ALU.mult),
                         reads=[RS[ci], Rw[0]], writes=[RS[ni]])
                    bv = bank()
                    for d in range(2):
                        tile_d = tf if d == 0 else tbk
                        row = n if d == 0 else 127 - n
                        for h2 in range(2):
                            s.op("pe", lambda e, bv=bv, d=d, h2=h2, tile_d=tile_d, row=row: e.matmul(
                                ps[bv][64 * h2:64 * h2 + 64, d * 256:(d + 1) * 256].rearrange("p (c i) -> p c i", c=4),
                                ident_b[:, row:row + 1].to_broadcast([128, 64]),
                                V_tm[:, tile_d, :].rearrange("p (c h i) -> p c h i", c=4, h=2)[:, :, h2, :],
                                start=True, stop=True),
                                reads=[RVtm, Rconst], writes=[Rps[bv]])
                    s.op("dve", lambda e, n=n, bv=bv: e.tensor_tensor(tB8, ps[bv][:].rearrange("p (k i) -> p k i", k=8), bc(TKD, n), ALU.mult),
                         reads=[Rps[bv], Rw[1]], writes=[RtB])
                    bs_ = bank()
                    s.op("pe", lambda e, bs_=bs_: e.matmul(ps[bs_][:], bones_f[:], tmpA, start=True, stop=True),
                         reads=[RtA, Rconst], writes=[Rps[bs_]])
                    s.op("dve", lambda e, n=n, bs_=bs_: e.tensor_tensor(tA8, ps[bs_][:].rearrange("p (k i) -> p k i", k=8), bc(TNK, n), ALU.mult),
                         reads=[Rps[bs_], Rw[1]], writes=[RtA])
                    s.op("dve", lambda e, ni=ni: e.tensor_tensor(Sst[ni], Sst[ni], tmpA, ALU.add),
                         reads=[RS[ni], RtA], writes=[RS[ni]])
                    s.op("dve", lambda e, ni=ni: e.tensor_tensor(Sst[ni], Sst[ni], tmpB, ALU.add),
                         reads=[RS[ni], RtB], writes=[RS[ni]])
                    nl = n % 32
                    for dc in range(8):
                        s.op("pe", lambda e, by_=by_, dc=dc, nl=nl, n=n, ni=ni: e.matmul(
                            ps[by_][0:64, nl * 16 + dc * 2:nl * 16 + dc * 2 + 2], Sst[ni][:, dc * 64:(dc + 1) * 64], TR2[:, dc, n, :],
                            start=True, stop=True),
                            reads=[RS[ni], Rw[2]], writes=[Rps[by_]])
                    if nl == 31:
                        n0 = n - 31
                        src = ps[by_][0:64, :].rearrange("p (n k) -> p n k", k=16)
                        s.op("act", lambda e, src=src, n0=n0: e.copy(Ystage[:, n0:n0 + 32, 0:8], src[:, :, 0:8]),
                             reads=[Rps[by_]], writes=[RYst])
                        s.op("act", lambda e, src=src, n0=n0: e.copy(Ystage[:, 96 - n0:128 - n0, 8:16][:, ::-1, :], src[:, :, 8:16]),
                             reads=[Rps[by_]], writes=[RYst])
                        reserved.discard(by_)
                if tf % 2 == 1:
                    s.dma("sp", s_out[tf // 2, 0], Sst[0][:, 0:256], reads=[RS[0]])
                if tbk % 2 == 0:
                    s.dma("sp", s_out[tbk // 2, 1], Sst[0][:, 256:512], reads=[RS[0]])
                for d in range(2):
                    tile_d = tf if d == 0 else tbk
                    byy = bank()
                    for k8 in range(8):
                        s.op("pe", lambda e, byy=byy, k8=k8, d=d: e.matmul(
                            ps[byy][:, k8 * 64:(k8 + 1) * 64], Ystage[:, :, d * 8 + k8], ident_f[0:64, 0:64], start=True, stop=True),
                            reads=[RYst, Rconst], writes=[Rps[byy]])
                    first = (r < 5)
                    if first:
                        s.op("act", lambda e, byy=byy, tile_d=tile_d: e.copy(YSb[:, tile_d, :], ps[byy][:]),
                             reads=[Rps[byy]], writes=[RYS])
                    else:
                        finalize(tile_d, byy)

        if stage >= 1:
            if "a" in sub:
                make_gg(0)
            if "b" in sub:
                make_hT(0, 0)
            if "c" in sub:
                even_inproj()
        if stage >= 3:
            attention()
            barrier()
            if stage >= 5:
                rwkv_setup()
                rwkv_prepass()
                rwkv_scan(NT if stage >= 6 else 2)
            else:
                s.op("pool", lambda e: e.memset(yT[:, 4:8, :], 0.0), writes=[RyT])
            barrier()
            out_proj(w_out_even, yT, RyT)
        if stage >= 2:
            barrier()
            ffn(0)
            barrier()
        if stage >= 4:
            make_gg(1)
            make_hT(1, 0)
            fnet()
            out_proj(w_out_odd, yT, RyT)
            barrier()
            ffn(1)
        for i in range(NT):
            s.dma("sp", y_out[i * 128:(i + 1) * 128, :], x_sb[:, i, :], reads=[Rx[i]])
        s.emit()
    return nc


def _core_units(c):
    if c < 6:
        return [("p", 5 * c + u) for u in range(5)]
    b = c - 6
    return [("s", b, u) for u in range(4)] + [("p", 30 + b)]


def _host_prep(inp):
    f32 = np.float32
    COFF, NCOL = colp_layout()
    ROFF, NROW = rowp_layout()
    g = {k: np.asarray(v) for k, v in inp.items()}
    colp = np.zeros((128, NCOL), f32)

    def put(name, vec):
        c = cols_of(vec)
        colp[:, COFF[name]:COFF[name] + c.shape[1]] = c
    for l in range(2):
        put("bada%d" % l, g["b_ada"][l])
        for k in range(4):
            put("gain%d_%d" % (l, k), g["norm_gains"][l, k])
        for k in range(3):
            put("conv%d_%d" % (l, k), g["ffn_conv"][l, k])
        put("convb%d" % l, g["ffn_conv_b"][l])
    put("mu0", g["rwkv_shift_mu"][0, 0])
    put("mu1", g["rwkv_shift_mu"][0, 1])
    for d in range(2):
        put("w0_%d" % d, g["rwkv_w0"][0, d])
        put("a0_%d" % d, g["rwkv_a0"][0, d])
    for k in range(3):
        put("kvec%d" % k, g["rwkv_kvec"][0, k])

    cmat = np.zeros((4, 128, 128), f32)
    cmat[0] = np.eye(128, dtype=f32)
    for i in range(64):
        cmat[1][2 * i, 2 * i + 1] = 1.0
        cmat[1][2 * i + 1, 2 * i] = -1.0
    cmat[2][:64, :64] = 1.0
    cmat[2][64:, 64:] = 1.0
    cmat[3][:64, 0] = 1.0
    cmat[3][64:, 1] = 1.0
    cc = np.arange(128)
    chang = 2.0 * np.pi * ((cc[:, None] * cc[None, :]) % 128) / 128.0
    chCS = np.concatenate([np.cos(chang), np.sin(chang)], axis=1).astype(f32)

    inv = (10000.0 ** (-np.arange(16, dtype=np.float32) / 16)).astype(f32)

    shared = dict(
        colp=colp, w_ada=g["w_ada"], w_in_even=g["w_in_even"][0], w_out_even=g["w_out_even"][0],
        w_out_odd=g["w_out_odd"][0], w_ffn_in=g["w_ffn_in"], w_ffn_out=g["w_ffn_out"],
        w2t=np.ascontiguousarray(g["rwkv_w2"][0].reshape(128, 512)),
        a2t=np.ascontiguousarray(g["rwkv_a2"][0].reshape(128, 512)),
        g2=np.ascontiguousarray(g["rwkv_g2"][0]), chCS=chCS, cmat=cmat,
        lnx=np.ascontiguousarray(g["rwkv_lnx"][0]))
    maps = []
    for c in range(NCORES):
        units = _core_units(c)
        xs = []
        for un in units:
            if un[0] == "p":
                xs.append(g["x_prompt"][un[1]])
            else:
                xs.append(g["x_sample"][un[1], un[2] * 256:(un[2] + 1) * 256])
        x_in = np.ascontiguousarray(np.concatenate(xs, axis=0), dtype=f32)
        is_s = c >= 6
        condA = g["c"][c - 6] if is_s else g["c_ctx"]
        condB = g["c_ctx"]
        cond = np.stack([condA, condB], axis=0).astype(f32)
        condT = np.ascontiguousarray(cond.reshape(2, 8, 128).transpose(2, 1, 0).reshape(128, 16))
        rowp = np.zeros((1, NROW), f32)
        rowp[0, ROFF["lam"]:ROFF["lam"] + 256] = g["diff_lambda"][0].reshape(-1)
        rowp[0, ROFF["subln"]:ROFF["subln"] + 128] = g["diff_subln"][0]
        ab = np.full((6, 5), -30000.0, f32)
        if is_s:
            ab[0:5, 0:4] = 0.0
            ab[5, 4] = 0.0
            bndL = [1, 0, 0, 0, 1]
            bndR = [0, 0, 0, 1, 1]
        else:
            for u in range(5):
                ab[u + 1, u] = 0.0
            bndL = [1] * 5
            bndR = [1] * 5
        rowp[0, ROFF["abias"]:ROFF["abias"] + 30] = ab.reshape(-1)
        rowp[0, ROFF["bndL"]:ROFF["bndL"] + 5] = bndL
        rowp[0, ROFF["bndR"]:ROFF["bndR"] + 5] = bndR
        ropeC = np.ones((128, T), f32)
        ropeS = np.zeros((128, T), f32)
        if is_s:
            t = np.arange(1024)
            row = (t // 64).astype(f32)
            col = (t % 64).astype(f32)
            ang = np.concatenate([row[:, None] * inv[None, :], col[:, None] * inv[None, :]], axis=1).astype(f32)
            pidx = (np.arange(128) % 64) // 2
            ropeC[:, :1024] = np.cos(ang)[:, pidx].T
            ropeS[:, :1024] = np.sin(ang)[:, pidx].T
        cacheKT = np.zeros((512, 256), f32)
        cacheV = np.zeros((256, 512), f32)
        initS = np.zeros((2, 128, 256), f32)
        if is_s:
            b = c - 6
            cacheKT[:] = g["cache_k"][b, 0].reshape(256, 512).T
            cacheV[:] = g["cache_v"][b, 0].reshape(256, 512)
            st = g["state_wkv"][b, 0]
            initS[:] = st.reshape(2, 4, 2, 64, 64).transpose(0, 2, 4, 1, 3).reshape(2, 128, 256)
        dC = np.zeros((T, T), np.float64)
        dS = np.zeros((T, T), np.float64)
        blocks = [(0, 1024), (1024, 256)] if is_s else [(256 * u, 256) for u in range(5)]
        for (a0, L) in blocks:
            ll = np.arange(L)
            ang = 2.0 * np.pi * ((ll[:, None] * ll[None, :]) % L) / L
            sc = 1.0 / math.sqrt(L * 128.0)
            dC[a0:a0 + L, a0:a0 + L] = np.cos(ang) * sc
            dS[a0:a0 + L, a0:a0 + L] = -np.sin(ang) * sc
        m = dict(shared)
        m.update(x_in=x_in, condT=condT, rowp=rowp, ropeC=ropeC, ropeS=ropeS, cacheKT=cacheKT, cacheV=cacheV,
                 initS=initS, dftC=dC.astype(f32), dftSn=dS.astype(f32))
        maps.append(m)
    return maps


_NC_CACHE = {}


def kernel(**inputs):
    maps = _host_prep(inputs)
    if "nc" not in _NC_CACHE:
        _NC_CACHE["nc"] = build()
    nc = _NC_CACHE["nc"]
    import os
    ncr = int(os.environ.get("NCR", NCORES))
    res = run_bass_kernel_spmd(nc, maps[:ncr], core_ids=list(range(ncr)))
    outs = list(res.results) + [res.results[0]] * (NCORES - ncr)
    y_prompt = np.zeros((32, 256, D), np.float32)
    y_sample = np.zeros((2, 1024, D), np.float32)
    nk = np.zeros((32, 1, 256, 4, 128), np.float32)
    nv = np.zeros((32, 1, 256, 4, 128), np.float32)
    ns = np.zeros((32, 1, 2, 8, 64, 64), np.float32)
    for c in range(NCORES):
        r = outs[c]
        for u, un in enumerate(_core_units(c)):
            sl = slice(u * 256, (u + 1) * 256)
            if un[0] == "p":
                bi = un[1]
                y_prompt[bi] = r["y_out"][sl]
                nk[bi, 0] = r["k_out"][sl].reshape(256, 4, 128)
                nv[bi, 0] = r["v_out"][sl].reshape(256, 4, 128)
                stt = r["s_out"][u]
                ns[bi, 0] = stt.reshape(2, 2, 64, 4, 64).transpose(0, 3, 1, 4, 2).reshape(2, 8, 64, 64)
            else:
                y_sample[un[1], un[2] * 256:(un[2] + 1) * 256] = r["y_out"][sl]
    return (y_prompt, y_sample, nk, nv, ns)
```

```python
import contextlib
import math
import numpy as np
import concourse.bass as bass
import concourse.mybir as mybir
from concourse.bass_utils import run_bass_kernel_spmd

F32 = mybir.dt.float32
BF16 = mybir.dt.bfloat16
AF = mybir.ActivationFunctionType
ALU = mybir.AluOpType
AX = mybir.AxisListType

T = 1280
NT = 10
U = 5
D = 1024
KC = 8
DFF = 2816
FC = 22
NCORES = 8
ARENA_W = 28000
EXPM05 = math.exp(-0.5)


class Res:
    __slots__ = ("name", "writer", "readers", "excl")

    def __init__(self, name, excl=False):
        self.name = name
        self.writer = None
        self.readers = []
        self.excl = excl


class Sched:
    ENGS = ("pe", "act", "dve", "pool", "sp")
    NDMA = 6

    def __init__(self, nc):
        self.nc = nc
        self.prog = {e: [] for e in self.ENGS}
        self.signal = {e: set() for e in self.ENGS}
        self.ndma = {e: 0 for e in self.ENGS}

    def _collect(self, reads, writes, eng=None):
        deps = []
        for r in reads:
            if r.writer is not None:
                deps.append(r.writer)
            if r.excl:
                deps.extend(t for t in r.readers if t[1] != eng)
        for w in writes:
            if w.writer is not None:
                deps.append(w.writer)
            deps.extend(w.readers)
        return deps

    def _commit(self, tok, reads, writes):
        for r in reads:
            r.readers.append(tok)
        for w in writes:
            w.writer = tok
            w.readers = []

    def op(self, eng, fn, reads=(), writes=()):
        deps = self._collect(reads, writes, eng)
        idx = len(self.prog[eng])
        if eng == "pe":
            deps = [d for d in deps if not (d[0] == "c" and d[1] == "pe")]
        for d in deps:
            if d[0] == "c":
                self.signal[d[1]].add(d[2])
        self.prog[eng].append(dict(fn=fn, deps=deps, kind="c"))
        tok = ("c", eng, idx)
        self._commit(tok, reads, writes)
        return tok

    def dma(self, eng, out, in_, reads=(), writes=()):
        deps = self._collect(reads, writes, eng)
        n = self.ndma[eng]
        self.ndma[eng] += 1
        if n >= self.NDMA:
            deps.append(("d", eng, n - self.NDMA))
        for d in deps:
            if d[0] == "c":
                self.signal[d[1]].add(d[2])
        self.prog[eng].append(dict(out=out, in_=in_, deps=deps, kind="d", n=n))
        tok = ("d", eng, n)
        self._commit(tok, reads, writes)
        return tok

    def emit(self):
        nc = self.nc
        with contextlib.ExitStack() as st:
            csem = {e: st.enter_context(nc.semaphore("c_" + e)) for e in self.ENGS}
            dsem = {e: [st.enter_context(nc.semaphore("d_%s_%d" % (e, i))) for i in range(self.NDMA)]
                    for e in self.ENGS if self.ndma[e] > 0}
            sigval = {}
            for e in self.ENGS:
                cnt = 0
                m = {}
                for i in range(len(self.prog[e])):
                    if i in self.signal[e]:
                        cnt += 1
                        m[i] = cnt
                sigval[e] = m

            def resolve(tok):
                if tok[0] == "c":
                    return csem[tok[1]], sigval[tok[1]][tok[2]], ("c", tok[1])
                e, n = tok[1], tok[2]
                return dsem[e][n % self.NDMA], 16 * (n // self.NDMA + 1), ("d", e, n % self.NDMA)

            def run_engine(e, h):
                waited = {}
                for i, ins in enumerate(self.prog[e]):
                    need = {}
                    for d in ins["deps"]:
                        sem, val, key = resolve(d)
                        if waited.get(key, 0) >= val:
                            continue
                        if key not in need or need[key][1] < val:
                            need[key] = (sem, val)
                    for key, (sem, val) in need.items():
                        h.wait_ge(sem, val)
                        waited[key] = val
                    if ins["kind"] == "c":
                        bi = ins["fn"](h)
                        if i in self.signal[e]:
                            bi.then_inc(csem[e], 1)
                    else:
                        n = ins["n"]
                        h.dma_start(out=ins["out"], in_=ins["in_"]).then_inc(dsem[e][n % self.NDMA], 16)
                if self.ndma[e] > 0:
                    n = self.ndma[e]
                    for slot in range(self.NDMA):
                        cnt = (n - slot + self.NDMA - 1) // self.NDMA if n > slot else 0
                        if cnt > 0:
                            h.wait_ge(dsem[e][slot], 16 * cnt)

            with nc.Block() as block:
                @block.tensor
                def _(eng):
                    run_engine("pe", eng)

                @block.scalar
                def _(eng):
                    run_engine("act", eng)

                @block.vector
                def _(eng):
                    run_engine("dve", eng)

                @block.gpsimd
                def _(eng):
                    run_engine("pool", eng)

                @block.sync
                def _(eng):
                    run_engine("sp", eng)


def colp_layout():
    off = {}
    n = 0

    def add(name, cols):
        nonlocal n
        off[name] = n
        n += cols
    for l in range(2):
        add("bada%d" % l, 48)
        for g in range(4):
            add("gain%d_%d" % (l, g), 8)
        for k in range(3):
            add("conv%d_%d" % (l, k), FC)
        add("convb%d" % l, FC)
    add("mu0", 15)
    add("mu1", 15)
    for d in range(2):
        add("w0_%d" % d, 4)
        add("a0_%d" % d, 4)
    for k in range(3):
        add("kvec%d" % k, 4)
    return off, n


def rowp_layout():
    off = {}
    n = 0

    def add(name, cols):
        nonlocal n
        off[name] = n
        n += cols
    add("lam", 256)
    add("subln", 128)
    add("abias", 30)
    add("bndL", 5)
    add("bndR", 5)
    return off, n


def cols_of(vec):
    v = np.asarray(vec, np.float32).reshape(-1, 128)
    return np.ascontiguousarray(v.T)


def build(stage=99):
    nc = bass.Bass("TRN2", target_bir_lowering=False)
    COFF, NCOL = colp_layout()
    ROFF, NROW = rowp_layout()

    def din(name, shape):
        return nc.dram_tensor(name, list(shape), F32, kind="ExternalInput").ap()

    def dout(name, shape):
        return nc.dram_tensor(name, list(shape), F32, kind="ExternalOutput").ap()

    x_in = din("x_in", [T, D])
    condT = din("condT", [128, 16])
    colp = din("colp", [128, NCOL])
    rowp = din("rowp", [1, NROW])
    w_ada = din("w_ada", [2, D, 6 * D])
    w_in_even = din("w_in_even", [D, 3456])
    w_out_even = din("w_out_even", [D, D])
    w_out_odd = din("w_out_odd", [D, D])
    w_ffn_in = din("w_ffn_in", [2, D, 2 * DFF])
    w_ffn_out = din("w_ffn_out", [2, DFF, D])
    w2t_d = din("w2t", [128, 512])
    a2t_d = din("a2t", [128, 512])
    g2_d = din("g2", [128, 512])
    cacheKT = din("cacheKT", [512, 256])
    cacheV = din("cacheV", [256, 512])
    ropeC = din("ropeC", [128, T])
    ropeS = din("ropeS", [128, T])
    initS = din("initS", [2, 128, 256])
    dftC = din("dftC", [T, T])
    dftSn = din("dftSn", [T, T])
    chCS = din("chCS", [128, 256])
    cmat = din("cmat", [4, 128, 128])
    lnx_d = din("lnx", [2, 512])

    y_out = dout("y_out", [T, D])
    k_out = dout("k_out", [T, 512])
    v_out = dout("v_out", [T, 512])
    s_out = dout("s_out", [U, 2, 128, 256])

    import os
    sub = os.environ.get("SUB", "abcmqv")
    with contextlib.ExitStack() as st:
        s = Sched(nc)

        def sb(name, shape, dt=F32):
            return st.enter_context(nc.sbuf_tensor(name, list(shape), dt))

        x_sb = sb("x_sb", [128, NT, D])
        Rx = [Res("x%d" % i) for i in range(NT)]
        WB = 4096
        wbuf = [sb("wb%d" % i, [128, WB], BF16) for i in range(3)]
        Rw = [Res("wb%d" % i) for i in range(3)]
        wctr = [0]
        colp_sb = sb("colp_sb", [128, NCOL])
        Rcolp = Res("colp")
        rowp_sb = sb("rowp_sb", [128, NROW])
        Rrowp = Res("rowp")
        ident_f = sb("ident_f", [128, 128])
        ident_b = sb("ident_b", [128, 128], BF16)
        rotT_b = sb("rotT_b", [128, 128], BF16)
        bones_f = sb("bones_f", [128, 128])
        halfsel_f = sb("halfsel_f", [128, 128])
        bones_b = sb("bones_b", [128, 128], BF16)
        Rconst = Res("const")
        gg_sb = sb("gg_sb", [128, 2, 2, D], BF16)
        Rgg = Res("gg")
        scond = sb("scond", [128, 16], BF16)
        condf = sb("condf", [128, 16])
        Rcond = Res("cond")
        modcol = sb("modcol", [128, 2, 96])
        Rmod = Res("modcol")
        scsh = sb("scsh", [128, 2, 2, 2, 16])
        Rscsh = Res("scsh")
        stat = sb("stat", [128, 64])
        Rstat = Res("stat")
        junk_b = sb("junk_b", [128, D], BF16)
        Rjunk = Res("junk")
        xn_b = sb("xn_b", [128, D], BF16)
        Rxn = Res("xn")
        stage_f = sb("stage_f", [128, 2, 512])
        Rstage = [Res("stage0"), Res("stage1")]
        stctr = [0]
        arena = sb("arena", [128, ARENA_W])
        Rar = {}

        def ares(name):
            if name not in Rar:
                Rar[name] = Res("ar_" + name)
            return Rar[name]

        def carve(off_w, nelem, dt):
            if dt == F32:
                return arena[:, off_w:off_w + nelem]
            return arena[:, off_w:off_w + (nelem + 1) // 2].bitcast(BF16)[:, 0:nelem]

        ps = [st.enter_context(nc.psum_tensor("ps%d" % b, [128, 512], F32)) for b in range(8)]
        psb = [p.bitcast(BF16) for p in ps]
        Rps = [Res("ps%d" % b, excl=True) for b in range(8)]
        bctr = [0]

        reserved = set()

        def bank():
            while True:
                b = bctr[0] % 8
                bctr[0] += 1
                if b not in reserved:
                    return b

        def barrier():
            allr = list(Rar.values()) + list(Rw)
            s.op("dve", lambda e: e.memset(stat[:, 63:64], 0.0), writes=allr + [Rstat])

        def wload(src_ap, kc, ncols):
            i = wctr[0] % 3
            wctr[0] += 1
            view = wbuf[i][:, 0:kc * ncols].rearrange("p (c n) -> p c n", c=kc)
            s.dma("pool", view, src_ap.rearrange("(c p) n -> p c n", p=128), writes=[Rw[i]])
            return view, Rw[i]

        s.dma("sp", colp_sb[:], colp, writes=[Rcolp])
        s.dma("sp", rowp_sb[:], rowp[0, :].partition_broadcast(128), writes=[Rrowp])
        s.dma("sp", ident_f[:], cmat[0], writes=[Rconst])
        s.dma("sp", bones_f[:], cmat[2], writes=[Rconst])
        s.dma("sp", halfsel_f[:], cmat[3], writes=[Rconst])
        s.dma("pool", ident_b[:], cmat[0], writes=[Rconst])
        s.dma("pool", rotT_b[:], cmat[1], writes=[Rconst])
        s.dma("pool", bones_b[:], cmat[2], writes=[Rconst])
        s.dma("sp", condf[:], condT, writes=[Rcond])
        for i in range(NT):
            s.dma("sp", x_sb[:, i, :], x_in[i * 128:(i + 1) * 128, :], writes=[Rx[i]])
        s.op("act", lambda e: e.activation(scond[:], condf[:], AF.Silu), reads=[Rcond], writes=[Rcond])

        def cp(name, c=0, n=1):
            o = COFF[name] + c
            return colp_sb[:, o:o + n]

        for l in range(2):
            b = bank()
            for v in range(6):
                for half in range(2):
                    wv, rw = wload(w_ada[l][:, v * 1024 + half * 512: v * 1024 + half * 512 + 512], KC, 512)
                    for j in range(4):
                        col = (v * 8 + half * 4 + j) * 2
                        for kc in range(KC):
                            s.op("pe", lambda e, wv=wv, j=j, kc=kc, col=col, b=b: e.matmul(
                                ps[b][:, col:col + 2], wv[:, kc, j * 128:(j + 1) * 128],
                                scond[:, kc * 2:kc * 2 + 2], start=(kc == 0), stop=(kc == KC - 1)),
                                reads=[rw, Rcond], writes=[Rps[b]])
            s.op("dve", lambda e, l=l, b=b: e.tensor_tensor(
                modcol[:, l, :].rearrange("p (c a) -> p c a", a=2),
                ps[b][:, 0:96].rearrange("p (c a) -> p c a", a=2),
                cp("bada%d" % l, 0, 48).unsqueeze(2).to_broadcast([128, 48, 2]), ALU.add),
                reads=[Rps[b], Rcolp], writes=[Rmod])

            def mv(v, l=l):
                return modcol[:, l, v * 16:(v + 1) * 16].rearrange("p (c a) -> p c a", a=2)

            def gain(g, l=l):
                return cp("gain%d_%d" % (l, g), 0, 8).unsqueeze(2).to_broadcast([128, 8, 2])
            for wi, (vs, vsh, g) in enumerate([(1, 0, 0), (4, 3, 2)]):
                sc = scsh[:, l, wi, 0, :].rearrange("p (c a) -> p c a", a=2)
                sh = scsh[:, l, wi, 1, :].rearrange("p (c a) -> p c a", a=2)
                s.op("dve", lambda e, sc=sc, vs=vs, mv=mv: e.tensor_scalar(sc, mv(vs), 1.0, None, ALU.add),
                     reads=[Rmod], writes=[Rscsh])
                s.op("dve", lambda e, sc=sc, g=g, gain=gain: e.tensor_tensor(sc, sc, gain(g), ALU.mult),
                     reads=[Rscsh, Rcolp], writes=[Rscsh])
                s.op("dve", lambda e, sh=sh, vsh=vsh, mv=mv: e.tensor_copy(sh, mv(vsh)),
                     reads=[Rmod], writes=[Rscsh])

        ggcol = sb("ggcol", [128, 2, 16])
        Rggcol = Res("ggcol")

        def make_gg(l):
            for wi, (vg, g) in enumerate([(2, 1), (5, 3)]):
                gc = ggcol[:, wi, :].rearrange("p (c a) -> p c a", a=2)
                s.op("dve", lambda e, gc=gc, vg=vg, g=g, l=l: e.tensor_tensor(
                    gc, modcol[:, l, vg * 16:(vg + 1) * 16].rearrange("p (c a) -> p c a", a=2),
                    cp("gain%d_%d" % (l, g), 0, 8).unsqueeze(2).to_broadcast([128, 8, 2]), ALU.mult),
                    reads=[Rmod, Rcolp], writes=[Rggcol])
                for a in range(2):
                    for hh in range(2):
                        b = bank()
                        for j in range(4):
                            c = hh * 4 + j
                            s.op("pe", lambda e, b=b, wi=wi, c=c, a=a, j=j: e.matmul(
                                ps[b][:, j * 128:(j + 1) * 128],
                                ggcol[:, wi, c * 2 + a:c * 2 + a + 1].to_broadcast([128, 128]),
                                ident_f[:], start=True, stop=True),
                                reads=[Rggcol, Rconst], writes=[Rps[b]])
                        s.op("act", lambda e, b=b, wi=wi, a=a, hh=hh: e.copy(
                            gg_sb[:, wi, a, hh * 512:(hh + 1) * 512], ps[b][:]),
                            reads=[Rps[b]], writes=[Rgg])

        hT = carve(0, 8 * T, BF16).rearrange("p (c t) -> p c t", c=8)
        RhT = ares("hT")

        def ab_of_tile(i):
            return 0 if i < 8 else 1

        def make_hT(l, wi):
            for i in range(NT):
                a = ab_of_tile(i)
                s.op("act", lambda e, i=i: e.activation(junk_b[:], x_sb[:, i, :], AF.Square, scale=1.0 / 32.0,
                                                        accum_out=stat[:, 0:1]),
                     reads=[Rx[i]], writes=[Rjunk, Rstat])
                s.op("dve", lambda e: e.tensor_scalar(stat[:, 1:2], stat[:, 0:1], 1e-6, None, ALU.add),
                     reads=[Rstat], writes=[Rstat])
                s.op("act", lambda e: e.activation(stat[:, 1:2], stat[:, 1:2], AF.Sqrt), reads=[Rstat], writes=[Rstat])
                s.op("dve", lambda e: e.reciprocal(stat[:, 2:3], stat[:, 1:2]), reads=[Rstat], writes=[Rstat])
                s.op("dve", lambda e, i=i: e.tensor_scalar(xn_b[:], x_sb[:, i, :], stat[:, 2:3], None, ALU.mult),
                     reads=[Rx[i], Rstat], writes=[Rxn])
                b = bank()
                for c in range(8):
                    s.op("pe", lambda e, b=b, c=c: e.transpose(psb[b][:, c * 128:(c + 1) * 128],
                                                               xn_b[:, c * 128:(c + 1) * 128], ident_b[:]),
                         reads=[Rxn, Rconst], writes=[Rps[b]])
                for c in range(8):
                    s.op("act", lambda e, b=b, c=c, i=i, a=a: e.activation(
                        hT[:, c, i * 128:(i + 1) * 128], psb[b][:, c * 128:(c + 1) * 128], AF.Identity,
                        bias=scsh[:, l, wi, 1, c * 2 + a:c * 2 + a + 1],
                        scale=scsh[:, l, wi, 0, c * 2 + a:c * 2 + a + 1]),
                        reads=[Rps[b], Rscsh], writes=[RhT])

        TOKCH = [(0, 512), (512, 512), (1024, 256)]

        def linear_fm(wv, rw, ncol0, actT, Ract, kc_n, evac):
            for (t0, tn) in TOKCH:
                b = bank()
                for kc in range(kc_n):
                    s.op("pe", lambda e, b=b, kc=kc, t0=t0, tn=tn: e.matmul(
                        ps[b][:, 0:tn], wv[:, kc, ncol0:ncol0 + 128], actT[:, kc, t0:t0 + tn],
                        start=(kc == 0), stop=(kc == kc_n - 1)),
                        reads=[rw, Ract], writes=[Rps[b]])
                evac(b, t0, tn)

        def linear_tm(wv, rw, ncols, actT, Ract, kc_n, i, b, col0=0):
            for kc in range(kc_n):
                s.op("pe", lambda e, kc=kc: e.matmul(
                    ps[b][:, col0:col0 + ncols], actT[:, kc, i * 128:(i + 1) * 128], wv[:, kc, 0:ncols],
                    start=(kc == 0), stop=(kc == kc_n - 1)),
                    reads=[rw, Ract], writes=[Rps[b]])

        def residual_update(i, banks, wi, fsrc=None, Rf=None):
            a = ab_of_tile(i)
            if fsrc is None:
                for hh, b in enumerate(banks):
                    s.op("act", lambda e, b=b, hh=hh: e.activation(junk_b[:, 0:512], ps[b][:], AF.Square,
                                                                   scale=1.0 / 32.0, accum_out=stat[:, 8 + hh:9 + hh]),
                         reads=[Rps[b]], writes=[Rjunk, Rstat])
                s.op("dve", lambda e: e.tensor_tensor(stat[:, 10:11], stat[:, 8:9], stat[:, 9:10], ALU.add),
                     reads=[Rstat], writes=[Rstat])
            else:
                s.op("act", lambda e: e.activation(junk_b[:], fsrc, AF.Square, scale=1.0 / 32.0,
                                                   accum_out=stat[:, 10:11]),
                     reads=[Rf], writes=[Rjunk, Rstat])
            s.op("dve", lambda e: e.tensor_scalar(stat[:, 11:12], stat[:, 10:11], 1e-6, None, ALU.add),
                 reads=[Rstat], writes=[Rstat])
            s.op("act", lambda e: e.activation(stat[:, 11:12], stat[:, 11:12], AF.Sqrt), reads=[Rstat], writes=[Rstat])
            s.op("dve", lambda e: e.reciprocal(stat[:, 12:13], stat[:, 11:12]), reads=[Rstat], writes=[Rstat])
            for hh in range(2):
                src = ps[banks[hh]][:] if fsrc is None else fsrc[:, hh * 512:(hh + 1) * 512]
                rr = [Rps[banks[hh]]] if fsrc is None else [Rf]
                si = stctr[0] % 2
                stctr[0] += 1
                s.op("dve", lambda e, src=src, hh=hh, si=si: e.scalar_tensor_tensor(
                    stage_f[:, si, :], src, stat[:, 12:13], gg_sb[:, wi, a, hh * 512:(hh + 1) * 512],
                    ALU.mult, ALU.mult),
                    reads=rr + [Rstat, Rgg], writes=[Rstage[si]])
                s.op("dve", lambda e, hh=hh, si=si, i=i: e.tensor_tensor(
                    x_sb[:, i, hh * 512:(hh + 1) * 512], x_sb[:, i, hh * 512:(hh + 1) * 512], stage_f[:, si, :], ALU.add),
                    reads=[Rstage[si], Rx[i]], writes=[Rx[i]])

        actT = carve(5120, FC * T, BF16).rearrange("p (c t) -> p c t", c=FC)
        RactT = ares("actT")
        graw = carve(19200, T + 2, F32)
        Rgraw = ares("graw")
        cbuf = carve(19200 + 1284, T, F32)
        Rcbuf = ares("cbuf")
        sbuf_s = carve(19200 + 1284 + 1280, T, F32)
        Rsbuf = ares("sbuf_s")
        fstageA = carve(19200, 5 * D, F32).rearrange("p (i n) -> p i n", i=5)
        fstageB = carve(0, 5 * D, F32).rearrange("p (i n) -> p i n", i=5)

        def fst(i):
            return fstageA[:, i, :] if i < 5 else fstageB[:, i - 5, :]

        def Rfst_of(i):
            return ares("fstage") if i < 5 else RhT
        bcorr = sb("bcorr", [128, 16])
        Rbcorr = Res("bcorr")

        def ffn(l):
            make_hT(l, 1)
            s.op("dve", lambda e: e.memset(graw[:, 0:1], 0.0), writes=[Rgraw])
            s.op("dve", lambda e: e.memset(graw[:, T + 1:T + 2], 0.0), writes=[Rgraw])
            for blk in range(0, FC, 2):
                wu, ru = wload(w_ffn_in[l][:, blk * 128: blk * 128 + 256], KC, 256)
                wg, rg = wload(w_ffn_in[l][:, DFF + blk * 128: DFF + blk * 128 + 256], KC, 256)
                for jj in range(2):
                    fc = blk + jj
                    def evac_g(b, t0, tn):
                        s.op("act", lambda e, b=b, t0=t0, tn=tn: e.copy(graw[:, 1 + t0:1 + t0 + tn], ps[b][:, 0:tn]),
                             reads=[Rps[b]], writes=[Rgraw])
                    linear_fm(wg, rg, jj * 128, hT, RhT, KC, evac_g)
                    w0 = cp("conv%d_0" % l, fc)
                    w1 = cp("conv%d_1" % l, fc)
                    w2 = cp("conv%d_2" % l, fc)
                    cb = cp("convb%d" % l, fc)
                    s.op("act", lambda e, w1=w1, cb=cb: e.activation(cbuf[:], graw[:, 1:T + 1], AF.Identity, bias=cb, scale=w1),
                         reads=[Rgraw, Rcolp], writes=[Rcbuf])
                    s.op("dve", lambda e, w0=w0: e.scalar_tensor_tensor(cbuf[:], graw[:, 0:T], w0, cbuf[:], ALU.mult, ALU.add),
                         reads=[Rgraw, Rcolp, Rcbuf], writes=[Rcbuf])
                    s.op("dve", lambda e, w2=w2: e.scalar_tensor_tensor(cbuf[:], graw[:, 2:T + 2], w2, cbuf[:], ALU.mult, ALU.add),
                         reads=[Rgraw, Rcolp, Rcbuf], writes=[Rcbuf])
                    gprev = graw[:, 256:256 + 1024].rearrange("p (u k) -> p u k", k=256)[:, :, 0]
                    gnext = graw[:, 257:257 + 1024].rearrange("p (u k) -> p u k", k=256)[:, :, 0]
                    s.op("dve", lambda e, gprev=gprev: e.tensor_tensor(bcorr[:, 0:4], gprev, rowp_sb[:, ROFF["bndL"] + 1:ROFF["bndL"] + 5], ALU.mult),
                         reads=[Rgraw, Rrowp], writes=[Rbcorr])
                    s.op("dve", lambda e, w0=w0: e.tensor_scalar(bcorr[:, 0:4], bcorr[:, 0:4], w0, None, ALU.mult),
                         reads=[Rbcorr, Rcolp], writes=[Rbcorr])
                    c_at = cbuf[:, 256:256 + 1024].rearrange("p (u k) -> p u k", k=256)[:, :, 0]
                    s.op("dve", lambda e, c_at=c_at: e.tensor_tensor(c_at, c_at, bcorr[:, 0:4], ALU.subtract),
                         reads=[Rbcorr, Rcbuf], writes=[Rcbuf])
                    s.op("dve", lambda e, gnext=gnext: e.tensor_tensor(bcorr[:, 4:8], gnext, rowp_sb[:, ROFF["bndL"] + 1:ROFF["bndL"] + 5], ALU.mult),
                         reads=[Rgraw, Rrowp], writes=[Rbcorr])
                    s.op("dve", lambda e, w2=w2: e.tensor_scalar(bcorr[:, 4:8], bcorr[:, 4:8], w2, None, ALU.mult),
                         reads=[Rbcorr, Rcolp], writes=[Rbcorr])
                    c_at2 = cbuf[:, 255:255 + 1024].rearrange("p (u k) -> p u k", k=256)[:, :, 0]
                    s.op("dve", lambda e, c_at2=c_at2: e.tensor_tensor(c_at2, c_at2, bcorr[:, 4:8], ALU.subtract),
                         reads=[Rbcorr, Rcbuf], writes=[Rcbuf])
                    s.op("act", lambda e: e.activation(sbuf_s[:], cbuf[:], AF.Silu), reads=[Rcbuf], writes=[Rsbuf])
                    def evac_u(b, t0, tn, fc=fc):
                        s.op("dve", lambda e, b=b, t0=t0, tn=tn: e.tensor_tensor(
                            actT[:, fc, t0:t0 + tn], ps[b][:, 0:tn], sbuf_s[:, t0:t0 + tn], ALU.mult),
                            reads=[Rps[b], Rsbuf], writes=[RactT])
                    linear_fm(wu, ru, jj * 128, hT, RhT, KC, evac_u)
            for nb in range(8):
                wv, rw = wload(w_ffn_out[l][:, nb * 128:(nb + 1) * 128], FC, 128)
                for i in range(NT):
                    b = bank()
                    linear_tm(wv, rw, 128, actT, RactT, FC, i, b)
                    s.op("act", lambda e, b=b, i=i, nb=nb: e.copy(fst(i)[:, nb * 128:(nb + 1) * 128], ps[b][:, 0:128]),
                         reads=[Rps[b]], writes=[Rfst_of(i)])
            for i in range(NT):
                residual_update(i, None, 1, fsrc=fst(i), Rf=Rfst_of(i))

        qT = carve(5120, 4 * T, BF16).rearrange("p (c t) -> p c t", c=4)
        RqT = ares("qT")
        kT = carve(7680, 4 * 1536, BF16).rearrange("p (c t) -> p c t", c=4)
        RkT = ares("kT")
        Vaug = carve(10752, 12 * 4 * 144, BF16).rearrange("p (k h d) -> p k h d", k=12, h=4)
        RV = ares("Vaug")
        yT = carve(0, 8 * T, BF16).rearrange("p (c t) -> p c t", c=8)
        fbT = carve(14208, 15 * (T + 2), BF16).rearrange("p (c t) -> p c t", c=15)
        RfbT = ares("fbT")
        RyT = RhT
        ropeC_sb = carve(23824, T, F32)
        ropeS_sb = carve(23824 + T, T, F32)
        Rrope = Res("rope")
        rtmp = sb("rtmp", [128, 2, 512])
        Rrtmp = Res("rtmp")
        raw_b = sb("raw_b", [128, 512], BF16)
        Rraw = Res("raw_b")

        def even_inproj():
            s.dma("sp", ropeC_sb[:], ropeC, writes=[Rrope])
            s.dma("sp", ropeS_sb[:], ropeS, writes=[Rrope])
            s.dma("pool", kT[:, :, 0:256], cacheKT.rearrange("(h p) t -> p h t", p=128), writes=[RkT])
            for kk_ in range(2):
                s.dma("pool", Vaug[:, kk_, :, 0:128],
                      cacheV[kk_ * 128:(kk_ + 1) * 128, :].rearrange("p (h d) -> p h d", h=4), writes=[RV])
            if "m" in sub:
                s.op("pool", lambda e: e.memset(Vaug[:, :, :, 128:129], 1.0), writes=[RV])
            for which, dst, doff in ((0, qT, 0), (1, kT, 256)):
                if "q" not in sub:
                    break
                wv, rw = wload(w_in_even[:, which * 512:(which + 1) * 512], KC, 512)
                Rdst = RqT if which == 0 else RkT
                for h in range(4):
                    def evac(b, t0, tn, h=h, dst=dst, doff=doff, Rdst=Rdst):
                        s.op("act", lambda e, b=b, tn=tn: e.copy(raw_b[:, 0:tn], ps[b][:, 0:tn]),
                             reads=[Rps[b]], writes=[Rraw])
                        b2 = bank()
                        s.op("pe", lambda e, b2=b2, tn=tn: e.matmul(ps[b2][:, 0:tn], rotT_b[:], raw_b[:, 0:tn], start=True, stop=True),
                             reads=[Rraw, Rconst], writes=[Rps[b2]])
                        s.op("dve", lambda e, t0=t0, tn=tn: e.tensor_tensor(rtmp[:, 0, 0:tn], raw_b[:, 0:tn], ropeC_sb[:, t0:t0 + tn], ALU.mult),
                             reads=[Rraw, Rrope], writes=[Rrtmp])
                        s.op("dve", lambda e, b2=b2, t0=t0, tn=tn: e.tensor_tensor(rtmp[:, 1, 0:tn], ps[b2][:, 0:tn], ropeS_sb[:, t0:t0 + tn], ALU.mult),
                             reads=[Rps[b2], Rrope], writes=[Rrtmp])
                        s.op("dve", lambda e, t0=t0, tn=tn: e.tensor_tensor(dst[:, h, doff + t0:doff + t0 + tn], rtmp[:, 0, 0:tn], rtmp[:, 1, 0:tn], ALU.add),
                             reads=[Rrtmp], writes=[Rdst])
                    linear_fm(wv, rw, h * 128, hT, RhT, KC, evac)
                if which == 1:
                    for i in range(NT):
                        b = bank()
                        linear_tm(wv, rw, 512, hT, RhT, KC, i, b)
                        si = stctr[0] % 2
                        stctr[0] += 1
                        s.op("act", lambda e, b=b, si=si: e.copy(stage_f[:, si, :], ps[b][:]), reads=[Rps[b]], writes=[Rstage[si]])
                        s.dma("sp", k_out[i * 128:(i + 1) * 128, :], stage_f[:, si, :], reads=[Rstage[si]])
            s.op("pool", lambda e: e.memset(fbT[:, :, 0:1], 0.0), writes=[RfbT])
            s.op("pool", lambda e: e.memset(fbT[:, :, T + 1:T + 2], 0.0), writes=[RfbT])
            for pc, (c0, ncols) in enumerate([(1536, 512), (2048, 512), (2560, 512), (3072, 384)]):
                wv, rw = wload(w_in_even[:, c0:c0 + ncols], KC, ncols)
                for j in range(ncols // 128):
                    ch = pc * 4 + j

                    def evac_fb(b, t0, tn, ch=ch):
                        s.op("act", lambda e, b=b, t0=t0, tn=tn: e.copy(fbT[:, ch, 1 + t0:1 + t0 + tn], ps[b][:, 0:tn]),
                             reads=[Rps[b]], writes=[RfbT])
                    linear_fm(wv, rw, j * 128, hT, RhT, KC, evac_fb)
            wv, rw = wload(w_in_even[:, 1024:1536], KC, 512)
            for i in range(NT if "v" in sub else 0):
                b = bank()
                linear_tm(wv, rw, 512, hT, RhT, KC, i, b)
                si = stctr[0] % 2
                stctr[0] += 1
                s.op("act", lambda e, b=b, si=si: e.copy(stage_f[:, si, :], ps[b][:]), reads=[Rps[b]], writes=[Rstage[si]])
                s.dma("sp", v_out[i * 128:(i + 1) * 128, :], stage_f[:, si, :], reads=[Rstage[si]])
                s.op("dve", lambda e, si=si, i=i: e.tensor_copy(Vaug[:, 2 + i, :, 0:128], stage_f[:, si, :].rearrange("p (h d) -> p h d", h=4)),
                     reads=[Rstage[si]], writes=[RV])


        PT = [[sb("PT%d%d" % (m, k), [128, 256], BF16) for k in range(2)] for m in range(2)]
        RPT = [[Res("PT%d%d" % (m, k)) for k in range(2)] for m in range(2)]
        ya_f = sb("ya_f", [128, 128])
        Rya = Res("ya_f")
        ya_b = sb("ya_b", [128, 128], BF16)
        Ryab = Res("ya_b")
        subln08 = sb("subln08", [128, 128])
        Rsub = Res("subln08")
        lamt = sb("lamt", [128, 64])
        Rlamt = Res("lamt")

        def attention():
            lo = ROFF["lam"]
            for k in range(2):
                s.op("dve", lambda e, k=k: e.tensor_tensor(lamt[:], rowp_sb[:, lo + 128 * k: lo + 128 * k + 64],
                                                          rowp_sb[:, lo + 128 * k + 64: lo + 128 * k + 128], ALU.mult),
                     reads=[Rrowp], writes=[Rlamt])
                s.op("dve", lambda e, k=k: e.reduce_sum(stat[:, 20 + k:21 + k], lamt[:], axis=AX.X),
                     reads=[Rlamt], writes=[Rstat])
            s.op("act", lambda e: e.activation(stat[:, 22:24], stat[:, 20:22], AF.Exp), reads=[Rstat], writes=[Rstat])
            s.op("dve", lambda e: e.tensor_tensor(stat[:, 24:25], stat[:, 22:23], stat[:, 23:24], ALU.subtract),
                 reads=[Rstat], writes=[Rstat])
            s.op("dve", lambda e: e.tensor_scalar(stat[:, 25:26], stat[:, 24:25], 0.2, -1.0, ALU.add, ALU.mult),
                 reads=[Rstat], writes=[Rstat])
            s.op("dve", lambda e: e.tensor_scalar(subln08[:], rowp_sb[:, ROFF["subln"]:ROFF["subln"] + 128], 0.8, None, ALU.mult),
                 reads=[Rrowp], writes=[Rsub])
            pctr = [0, 0]
            for h in range(4):
                for qu in range(5):
                    acc = [bank(), bank()]
                    for kt in range(12):
                        ku = 0 if kt < 2 else 1 + (kt - 2) // 2
                        bcol = ROFF["abias"] + ku * 5 + qu
                        for m in range(2):
                            bs = bank()
                            while bs in acc:
                                bs = bank()
                            s.op("pe", lambda e, bs=bs, m=m, h=h, kt=kt, qu=qu: e.matmul(
                                ps[bs][:, 0:256], kT[64 * m:64 * m + 64, h, kt * 128:(kt + 1) * 128],
                                qT[64 * m:64 * m + 64, h, qu * 256:(qu + 1) * 256], start=True, stop=True),
                                reads=[RkT, RqT], writes=[Rps[bs]])
                            pk = pctr[m] % 2
                            pctr[m] += 1
                            s.op("act", lambda e, bs=bs, m=m, pk=pk, bcol=bcol: e.activation(
                                PT[m][pk][:], ps[bs][:, 0:256], AF.Exp, bias=rowp_sb[:, bcol:bcol + 1], scale=0.125),
                                reads=[Rps[bs], Rrowp], writes=[RPT[m][pk]])
                            for qt in range(2):
                                s.op("pe", lambda e, m=m, pk=pk, qt=qt, kt=kt, h=h, acc=acc: e.matmul(
                                    ps[acc[m]][:, qt * 129:qt * 129 + 129], PT[m][pk][:, qt * 128:(qt + 1) * 128],
                                    Vaug[:, kt, h, 0:129], start=(kt == 0 and qt == 0), stop=(kt == 11),
                                    skip_group_check=True),
                                    reads=[RPT[m][pk], RV], writes=[Rps[acc[m]]])
                    for qt in range(2):
                        i = qu * 2 + qt
                        c0 = qt * 129
                        s.op("dve", lambda e, c0=c0, acc=acc: e.reciprocal(stat[:, 30:31], ps[acc[0]][:, c0 + 128:c0 + 129]),
                             reads=[Rps[acc[0]]], writes=[Rstat])
                        s.op("dve", lambda e, c0=c0, acc=acc: e.reciprocal(stat[:, 31:32], ps[acc[1]][:, c0 + 128:c0 + 129]),
                             reads=[Rps[acc[1]]], writes=[Rstat])
                        s.op("dve", lambda e: e.tensor_tensor(stat[:, 32:33], stat[:, 31:32], stat[:, 25:26], ALU.mult),
                             reads=[Rstat], writes=[Rstat])
                        s.op("dve", lambda e, c0=c0, acc=acc: e.tensor_scalar(ya_f[:], ps[acc[0]][:, c0:c0 + 128], stat[:, 30:31], None, ALU.mult),
                             reads=[Rps[acc[0]], Rstat], writes=[Rya])
                        s.op("dve", lambda e, c0=c0, acc=acc: e.scalar_tensor_tensor(ya_f[:], ps[acc[1]][:, c0:c0 + 128], stat[:, 32:33], ya_f[:], ALU.mult, ALU.add),
                             reads=[Rps[acc[1]], Rstat, Rya], writes=[Rya])
                        s.op("act", lambda e: e.activation(junk_b[:, 0:128], ya_f[:], AF.Square, scale=1.0 / math.sqrt(128.0),
                                                           accum_out=stat[:, 33:34]),
                             reads=[Rya], writes=[Rjunk, Rstat])
                        s.op("dve", lambda e: e.tensor_scalar(stat[:, 34:35], stat[:, 33:34], 1e-6, None, ALU.add),
                             reads=[Rstat], writes=[Rstat])
                        s.op("act", lambda e: e.activation(stat[:, 34:35], stat[:, 34:35], AF.Sqrt), reads=[Rstat], writes=[Rstat])
                        s.op("dve", lambda e: e.reciprocal(stat[:, 35:36], stat[:, 34:35]), reads=[Rstat], writes=[Rstat])
                        s.op("dve", lambda e: e.scalar_tensor_tensor(ya_b[:], ya_f[:], stat[:, 35:36], subln08[:], ALU.mult, ALU.mult),
                             reads=[Rya, Rstat, Rsub], writes=[Ryab])
                        bt = bank()
                        while bt in acc:
                            bt = bank()
                        s.op("pe", lambda e, bt=bt: e.transpose(psb[bt][:, 0:128], ya_b[:], ident_b[:]),
                             reads=[Ryab, Rconst], writes=[Rps[bt]])
                        s.op("act", lambda e, bt=bt, h=h, i=i: e.copy(yT[:, h, i * 128:(i + 1) * 128], psb[bt][:, 0:128]),
                             reads=[Rps[bt]], writes=[RyT])

        def out_proj(W, actTv, Ract):
            pieces = [wload(W[:, hh * 512:(hh + 1) * 512], KC, 512) for hh in range(2)]
            for i in range(NT):
                bb = [bank(), bank()]
                for hh in range(2):
                    linear_tm(pieces[hh][0], pieces[hh][1], 512, actTv, Ract, KC, i, bb[hh])
                residual_update(i, bb, 0)

        Xc = carve(5120, NT * 8 * 256, BF16).rearrange("p (i g n) -> p i g n", i=NT, g=8)
        RXc = ares("Xc")
        chCS_b = sb("chCS_b", [128, 256], BF16)
        Rch = Res("chCS")

        def fnet():
            s.dma("pool", chCS_b[:], chCS, writes=[Rch])
            for i in range(NT):
                for gp in range(4):
                    b = bank()
                    for g2_ in range(2):
                        g = gp * 2 + g2_
                        s.op("pe", lambda e, b=b, g=g, g2_=g2_, i=i: e.matmul(
                            ps[b][:, g2_ * 256:(g2_ + 1) * 256], hT[:, g, i * 128:(i + 1) * 128], chCS_b[:], start=True, stop=True),
                            reads=[RhT, Rch], writes=[Rps[b]])
                    s.op("act", lambda e, b=b, gp=gp, i=i: e.copy(
                        Xc[:, i, gp * 2:gp * 2 + 2, :], ps[b][:].rearrange("p (g n) -> p g n", g=2)),
                        reads=[Rps[b]], writes=[RXc])
            for (t0, tn) in [(0, 384), (384, 384), (768, 384), (1152, 128)]:
                wc, rc = wload(dftC[:, t0:t0 + tn], NT, tn)
                wsn, rsn = wload(dftSn[:, t0:t0 + tn], NT, tn)
                for g in range(8):
                    b = bank()
                    for tc in range(NT):
                        s.op("pe", lambda e, b=b, g=g, tc=tc, tn=tn, wc=wc: e.matmul(
                            ps[b][:, 0:tn], Xc[:, tc, g, 0:128], wc[:, tc, 0:tn], start=(tc == 0), stop=False),
                            reads=[RXc, rc], writes=[Rps[b]])
                        s.op("pe", lambda e, b=b, g=g, tc=tc, tn=tn, wsn=wsn: e.matmul(
                            ps[b][:, 0:tn], Xc[:, tc, g, 128:256], wsn[:, tc, 0:tn], start=False, stop=(tc == NT - 1)),
                            reads=[RXc, rsn], writes=[Rps[b]])
                    s.op("act", lambda e, b=b, g=g, t0=t0, tn=tn: e.copy(yT[:, g, t0:t0 + tn], ps[b][:, 0:tn]),
                         reads=[Rps[b]], writes=[RyT])


        V_tm = carve(5120, NT * 512, BF16).rearrange("p (i n) -> p i n", i=NT)
        RVtm = ares("V_tm")
        YSb = carve(7680, NT * 512, BF16).rearrange("p (i n) -> p i n", i=NT)
        RYS = ares("YSb")
        Ystage = carve(10240, 128 * 16, F32)[0:64, :].rearrange("p (n k) -> p n k", k=16)
        RYst = ares("Ystage")
        Sst = [carve(12288, 512, F32), carve(12800, 512, F32)]
        RS = [ares("S0"), ares("S1")]
        tmpA = carve(13312, 512, F32)
        RtA = ares("tmpA")
        tmpB = carve(23824, 512, F32)
        RtB = ares("tmpB")
        w2t_b = carve(24336, 512, BF16)
        a2t_b = carve(24592, 512, BF16)
        g2_b = carve(24848, 512, BF16)
        Rlora = ares("lora")
        PT0 = 25104
        NPT = 11

        def pt(k, n=1):
            return carve(PT0 + 128 * k, 128 * n, F32)
        Rpt = [ares("pt%d" % k) for k in range(NPT)]
        lnx0_b = carve(26512, 512, BF16)
        lnx1_b = carve(26768, 512, BF16)
        Rlnx = ares("lnx")
        tb0 = junk_b[:, 0:128]
        tb1 = junk_b[:, 128:256]
        Rtb0 = Res("tb0")
        Rtb1 = Res("tb1")
        colx = sb("colx", [128, 64])
        Rcolx = Res("colx")
        tiny = sb("tiny", [128, 8])
        Rtiny = Res("tiny")
        BS = sb("BS", [128, NT, 8])
        RBS = Res("BS")
        wbf = [w[:].bitcast(F32) for w in wbuf]
        TW = wbf[0][:, 0:1024].rearrange("p (k n) -> p k n", k=8)
        TKK = wbf[0][:, 1024:2048].rearrange("p (k n) -> p k n", k=8)
        TNK = wbf[1][:, 0:1024].rearrange("p (k n) -> p k n", k=8)
        TKD = wbf[1][:, 1024:2048].rearrange("p (k n) -> p k n", k=8)
        TR2 = wbuf[2][:, 0:2048].rearrange("p (k n h) -> p k n h", k=8, h=2)
        tmpA_b = carve(27100, 512, BF16)
        RtAb = ares("tmpA_b")
        Sb16 = carve(27356, 512, BF16)
        RSb = ares("Sb16")
        CX = dict(cmu=0, nmu0=15, nmu1=30, omk1=45, keepL=49, keepR=54)

        def cx(name, c=0):
            o = CX[name] + c
            return colx[:, o:o + 1]

        def rwkv_setup():
            s.dma("pool", w2t_b, w2t_d, writes=[Rlora])
            s.dma("pool", a2t_b, a2t_d, writes=[Rlora])
            s.dma("pool", g2_b, g2_d, writes=[Rlora])
            s.dma("pool", lnx0_b, lnx_d[0, :].partition_broadcast(128), writes=[Rlnx])
            s.dma("pool", lnx1_b, lnx_d[1, :].partition_broadcast(128), writes=[Rlnx])
            mu0 = cp("mu0", 0, 15)
            mu1 = cp("mu1", 0, 15)
            s.op("dve", lambda e: e.tensor_tensor(colx[:, 0:15], mu0, mu1, ALU.add), reads=[Rcolp], writes=[Rcolx])
            s.op("dve", lambda e: e.tensor_scalar(colx[:, 0:15], colx[:, 0:15], -1.0, 1.0, ALU.mult, ALU.add),
                 reads=[Rcolx], writes=[Rcolx])
            s.op("dve", lambda e: e.tensor_scalar(colx[:, 15:30], mu0, -1.0, None, ALU.mult), reads=[Rcolp], writes=[Rcolx])
            s.op("dve", lambda e: e.tensor_scalar(colx[:, 30:45], mu1, -1.0, None, ALU.mult), reads=[Rcolp], writes=[Rcolx])
            s.op("dve", lambda e: e.tensor_scalar(colx[:, 45:49], cp("kvec1", 0, 4), -1.0, 1.0, ALU.mult, ALU.add),
                 reads=[Rcolp], writes=[Rcolx])
            s.op("dve", lambda e: e.tensor_scalar(colx[:, 49:54], rowp_sb[:, ROFF["bndL"]:ROFF["bndL"] + 5], -1.0, 1.0, ALU.mult, ALU.add),
                 reads=[Rrowp], writes=[Rcolx])
            s.op("dve", lambda e: e.tensor_scalar(colx[:, 54:59], rowp_sb[:, ROFF["bndR"]:ROFF["bndR"] + 5], -1.0, 1.0, ALU.mult, ALU.add),
                 reads=[Rrowp], writes=[Rcolx])
            s.op("dve", lambda e: e.memset(BS[:], 0.0), writes=[RBS])

        def shift(ch, ti, out, Rout):
            t0 = ti * 128
            f = fbT[:, ch, 1 + t0:1 + t0 + 128]
            fp = fbT[:, ch, t0:t0 + 128]
            fn = fbT[:, ch, 2 + t0:2 + t0 + 128]
            rr = [RfbT, Rcolp, Rcolx]
            s.op("dve", lambda e: e.tensor_scalar(out, f, cx("cmu", ch), None, ALU.mult), reads=rr, writes=[Rout])
            s.op("dve", lambda e: e.scalar_tensor_tensor(out, fp, cp("mu0", ch), out, ALU.mult, ALU.add),
                 reads=rr + [Rout], writes=[Rout])
            s.op("dve", lambda e: e.scalar_tensor_tensor(out, fn, cp("mu1", ch), out, ALU.mult, ALU.add),
                 reads=rr + [Rout], writes=[Rout])
            u = ti // 2
            if ti % 2 == 0:
                bl = rowp_sb[:, ROFF["bndL"] + u:ROFF["bndL"] + u + 1]
                s.op("dve", lambda e: e.tensor_tensor(tiny[:, 0:1], fbT[:, ch, t0:t0 + 1], bl, ALU.mult),
                     reads=[RfbT, Rrowp], writes=[Rtiny])
                s.op("dve", lambda e: e.scalar_tensor_tensor(out[:, 0:1], tiny[:, 0:1], cx("nmu0", ch), out[:, 0:1], ALU.mult, ALU.add),
                     reads=[Rtiny, Rcolx, Rout], writes=[Rout])
            else:
                br = rowp_sb[:, ROFF["bndR"] + u:ROFF["bndR"] + u + 1]
                s.op("dve", lambda e: e.tensor_tensor(tiny[:, 1:2], fbT[:, ch, 1 + t0 + 128:2 + t0 + 128], br, ALU.mult),
                     reads=[RfbT, Rrowp], writes=[Rtiny])
                s.op("dve", lambda e: e.scalar_tensor_tensor(out[:, 127:128], tiny[:, 1:2], cx("nmu1", ch), out[:, 127:128], ALU.mult, ALU.add),
                     reads=[Rtiny, Rcolx, Rout], writes=[Rout])

        def rwkv_prepass():
            for ti in range(NT):
                for c in range(4):
                    shift(8 + c, ti, pt(c), Rpt[c])
                    s.op("act", lambda e, c=c: e.copy(xn_b[:, c * 128:(c + 1) * 128], pt(c)), reads=[Rpt[c]], writes=[Rxn])
                b = bank()
                for c in range(4):
                    s.op("pe", lambda e, b=b, c=c: e.transpose(psb[b][:, c * 128:(c + 1) * 128], xn_b[:, c * 128:(c + 1) * 128], ident_b[:]),
                         reads=[Rxn, Rconst], writes=[Rps[b]])
                s.op("act", lambda e, b=b, ti=ti: e.copy(V_tm[:, ti, :], psb[b][:, 0:512]), reads=[Rps[b]], writes=[RVtm])

        def prep(d, ti):
            rev = (d == 1)

            def tab(T3, c):
                v = T3[:, d * 4 + c, :]
                return v[:, ::-1] if rev else v
            tabs_w = [Rw[0], Rw[1], Rw[2]]
            shift(12, ti, pt(0), Rpt[0])
            shift(13, ti, pt(1), Rpt[1])
            s.op("act", lambda e: e.activation(tb0, pt(0), AF.Tanh), reads=[Rpt[0]], writes=[Rtb0])
            s.op("act", lambda e: e.copy(tb1, pt(1)), reads=[Rpt[1]], writes=[Rtb1])
            lo, hi = 64 * d, 64 * d + 64
            for c in range(4):
                b = bank()
                s.op("pe", lambda e, b=b, c=c: e.matmul(ps[b][:, 0:128], w2t_b[lo:hi, c * 128:(c + 1) * 128], tb0[lo:hi, :], start=True, stop=True),
                     reads=[Rlora, Rtb0], writes=[Rps[b]])
                s.op("act", lambda e, b=b, c=c: e.activation(pt(2), ps[b][:, 0:128], AF.Sigmoid, bias=cp("w0_%d" % d, c)),
                     reads=[Rps[b], Rcolp], writes=[Rpt[2]])
                s.op("act", lambda e, c=c: e.activation(tab(TW, c), pt(2), AF.Exp, scale=-EXPM05),
                     reads=[Rpt[2]], writes=[Rw[0]])
                b = bank()
                s.op("pe", lambda e, b=b, c=c: e.matmul(ps[b][:, 0:128], a2t_b[lo:hi, c * 128:(c + 1) * 128], tb1[lo:hi, :], start=True, stop=True),
                     reads=[Rlora, Rtb1], writes=[Rps[b]])
                s.op("act", lambda e, b=b, c=c: e.activation(pt(3), ps[b][:, 0:128], AF.Sigmoid, bias=cp("a0_%d" % d, c)),
                     reads=[Rps[b], Rcolp], writes=[Rpt[3]])
                shift(4 + c, ti, pt(4), Rpt[4])
                s.op("dve", lambda e, c=c: e.tensor_scalar(pt(5), pt(4), cp("kvec0", c), None, ALU.mult),
                     reads=[Rpt[4], Rcolp], writes=[Rpt[5]])
                s.op("act", lambda e: e.activation(pt(6), pt(5), AF.Square), reads=[Rpt[5]], writes=[Rpt[6]])
                b = bank()
                s.op("pe", lambda e, b=b: e.matmul(ps[b][:, 0:128], bones_f[:], pt(6), start=True, stop=True),
                     reads=[Rconst, Rpt[6]], writes=[Rps[b]])
                s.op("dve", lambda e, b=b: e.tensor_scalar(pt(7), ps[b][:, 0:128], 1e-12, None, ALU.add),
                     reads=[Rps[b]], writes=[Rpt[7]])
                s.op("act", lambda e: e.activation(pt(7), pt(7), AF.Sqrt), reads=[Rpt[7]], writes=[Rpt[7]])
                s.op("dve", lambda e: e.reciprocal(pt(7), pt(7)), reads=[Rpt[7]], writes=[Rpt[7]])
                s.op("dve", lambda e: e.tensor_tensor(pt(8), pt(5), pt(7), ALU.mult), reads=[Rpt[5], Rpt[7]], writes=[Rpt[8]])
                s.op("act", lambda e, c=c: e.copy(tab(TKK, c), pt(8)), reads=[Rpt[8]], writes=[Rw[0]])
                s.op("dve", lambda e, c=c: e.scalar_tensor_tensor(tab(TNK, c), pt(8), -1.0, pt(3), ALU.mult, ALU.mult),
                     reads=[Rpt[8], Rpt[3]], writes=[Rw[1]])
                s.op("dve", lambda e, c=c: e.tensor_scalar(pt(9), pt(3), cp("kvec1", c), cx("omk1", c), ALU.mult, ALU.add),
                     reads=[Rpt[3], Rcolp, Rcolx], writes=[Rpt[9]])
                s.op("dve", lambda e: e.tensor_tensor(pt(9), pt(4), pt(9), ALU.mult), reads=[Rpt[4], Rpt[9]], writes=[Rpt[9]])
                s.op("act", lambda e, c=c: e.copy(tab(TKD, c), pt(9)), reads=[Rpt[9]], writes=[Rw[1]])
                shift(c, ti, pt(10), Rpt[10])
                for h2 in range(2):
                    o = TR2[:, d * 4 + c, :, h2]
                    if rev:
                        o = o[:, ::-1]
                    s.op("dve", lambda e, o=o, h2=h2: e.tensor_scalar(o, pt(10), halfsel_f[:, h2:h2 + 1], None, ALU.mult),
                         reads=[Rpt[10], Rconst], writes=[Rw[2]])
                s.op("dve", lambda e: e.tensor_tensor(pt(6), pt(10), pt(9), ALU.mult), reads=[Rpt[10], Rpt[9]], writes=[Rpt[6]])
                s.op("dve", lambda e, c=c: e.tensor_scalar(pt(6), pt(6), cp("kvec2", c), None, ALU.mult),
                     reads=[Rpt[6], Rcolp], writes=[Rpt[6]])
                b = bank()
                s.op("pe", lambda e, b=b: e.matmul(ps[b][:, 0:2], pt(6), halfsel_f[:, 0:2], start=True, stop=True),
                     reads=[Rpt[6], Rconst], writes=[Rps[b]])
                s.op("dve", lambda e, b=b, c=c, ti=ti: e.tensor_tensor(BS[:, ti, c * 2:c * 2 + 2], BS[:, ti, c * 2:c * 2 + 2], ps[b][:, 0:2], ALU.add),
                     reads=[Rps[b], RBS], writes=[RBS])

        yb = xn_b[:, 0:512]

        def finalize(ti, by):
            ys = pt(0, 4)
            Rys = [Rpt[0], Rpt[1], Rpt[2], Rpt[3]]
            t2 = pt(4, 4)
            Rt2 = [Rpt[4], Rpt[5], Rpt[6], Rpt[7]]
            ys3 = ys.rearrange("p (h i) -> p h i", h=8)
            t23 = t2.rearrange("p (h i) -> p h i", h=8)
            s.op("dve", lambda e: e.tensor_tensor(ys, YSb[:, ti, :], ps[by][:], ALU.add), reads=[RYS, Rps[by]], writes=Rys)
            s.op("dve", lambda e: e.reduce_sum(tiny[:, 0:8], ys3, axis=AX.X), reads=Rys, writes=[Rtiny])
            s.op("dve", lambda e: e.tensor_scalar(tiny[:, 0:8], tiny[:, 0:8], -1.0 / 64.0, None, ALU.mult), reads=[Rtiny], writes=[Rtiny])
            s.op("dve", lambda e: e.tensor_tensor(ys3, ys3, tiny[:, 0:8].unsqueeze(2).to_broadcast([128, 8, 64]), ALU.add),
                 reads=Rys + [Rtiny], writes=Rys)
            s.op("dve", lambda e: e.tensor_tensor(t2, ys, ys, ALU.mult), reads=Rys, writes=Rt2)
            s.op("dve", lambda e: e.reduce_sum(stat[:, 40:48], t23, axis=AX.X), reads=Rt2, writes=[Rstat])
            s.op("dve", lambda e: e.tensor_scalar(stat[:, 40:48], stat[:, 40:48], 1.0 / 64.0, 64e-5, ALU.mult, ALU.add),
                 reads=[Rstat], writes=[Rstat])
            s.op("act", lambda e: e.activation(stat[:, 40:48], stat[:, 40:48], AF.Sqrt), reads=[Rstat], writes=[Rstat])
            s.op("dve", lambda e: e.reciprocal(stat[:, 48:56], stat[:, 40:48]), reads=[Rstat], writes=[Rstat])
            s.op("dve", lambda e: e.tensor_tensor(ys3, ys3, stat[:, 48:56].unsqueeze(2).to_broadcast([128, 8, 64]), ALU.mult),
                 reads=Rys + [Rstat], writes=Rys)
            s.op("dve", lambda e: e.tensor_tensor(ys, ys, lnx0_b, ALU.mult), reads=Rys + [Rlnx], writes=Rys)
            s.op("dve", lambda e: e.tensor_tensor(ys, ys, lnx1_b, ALU.add), reads=Rys + [Rlnx], writes=Rys)
            s.op("dve", lambda e: e.tensor_tensor(t23, V_tm[:, ti, :].rearrange("p (h i) -> p h i", h=8),
                                                  BS[:, ti, :].unsqueeze(2).to_broadcast([128, 8, 64]), ALU.mult),
                 reads=[RVtm, RBS], writes=Rt2)
            s.op("dve", lambda e: e.tensor_tensor(ys, ys, t2, ALU.add), reads=Rys + Rt2, writes=Rys)
            shift(14, ti, pt(8), Rpt[8])
            s.op("act", lambda e: e.activation(tb0, pt(8), AF.Sigmoid), reads=[Rpt[8]], writes=[Rtb0])
            bg = bank()
            s.op("pe", lambda e, bg=bg: e.matmul(ps[bg][:], tb0, g2_b, start=True, stop=True),
                 reads=[Rtb0, Rlora], writes=[Rps[bg]])
            s.op("dve", lambda e, bg=bg: e.tensor_tensor(yb, ys, ps[bg][:], ALU.mult), reads=Rys + [Rps[bg]], writes=[Rxn])
            bt = bank()
            for c in range(4):
                s.op("pe", lambda e, bt=bt, c=c: e.transpose(psb[bt][:, c * 128:(c + 1) * 128], yb[:, c * 128:(c + 1) * 128], ident_b[:]),
                     reads=[Rxn, Rconst], writes=[Rps[bt]])
            s.op("act", lambda e, bt=bt, ti=ti: e.copy(yT[:, 4:8, ti * 128:(ti + 1) * 128],
                                                        psb[bt][:, 0:512].rearrange("p (c t) -> p c t", c=4)),
                 reads=[Rps[bt]], writes=[RyT])

        def rwkv_scan(nrounds=NT):
            S8 = [x_.rearrange("p (k i) -> p k i", k=8) for x_ in Sst]
            tA8 = tmpA.rearrange("p (k i) -> p k i", k=8)
            tB8 = tmpB.rearrange("p (k i) -> p k i", k=8)

            def bc(T3, n):
                return T3[:, :, n].unsqueeze(2).to_broadcast([128, 8, 64])
            for r in range(nrounds):
                tf, tbk = r, NT - 1 - r
                prep(0, tf)
                prep(1, tbk)
                if tf % 2 == 0:
                    u = tf // 2
                    if u == 0:
                        s.dma("sp", Sst[0][:, 0:256], initS[0], writes=[RS[0]])
                    else:
                        s.op("dve", lambda e, u=u: e.tensor_scalar(Sst[0][:, 0:256], Sst[0][:, 0:256], cx("keepL", u), None, ALU.mult),
                             reads=[RS[0], Rcolx], writes=[RS[0]])
                if tbk % 2 == 1:
                    u = tbk // 2
                    if u == 4:
                        s.op("dve", lambda e: e.memset(Sst[0][:, 256:512], 0.0), writes=[RS[0]])
                    elif u == 3:
                        s.dma("sp", tmpB[:, 0:256], initS[1], writes=[RtB])
                        s.op("dve", lambda e, u=u: e.scalar_tensor_tensor(Sst[0][:, 256:512], Sst[0][:, 256:512], cx("keepR", u),
                                                                          tmpB[:, 0:256], ALU.mult, ALU.add),
                             reads=[RS[0], Rcolx, RtB], writes=[RS[0]])
                    else:
                        s.op("dve", lambda e, u=u: e.tensor_scalar(Sst[0][:, 256:512], Sst[0][:, 256:512], cx("keepR", u), None, ALU.mult),
                             reads=[RS[0], Rcolx], writes=[RS[0]])
                by_ = None
                pending = []

                def flush():
                    for f_ in pending:
                        f_()
                    del pending[:]
                for n in range(128):
                    ci, ni = n % 2, (n + 1) % 2
                    Sc, Sn = S8[ci], S8[ni]
                    if n % 32 == 0:
                        by_ = bank()
                        reserved.add(by_)
                    tAb8 = tmpA_b.rearrange("p (k i) -> p k i", k=8)
                    s.op("dve", lambda e, n=n, Sc=Sc, tAb8=tAb8: e.tensor_tensor(tAb8, Sc, bc(TKK, n), ALU.mult),
                         reads=[RS[ci], Rw[0]], writes=[RtAb])
                    s.op("dve", lambda e, n=n, Sc=Sc, Sn=Sn: e.tensor_tensor(Sn, Sc, bc(TW, n), ALU.mult),
                         reads=[RS[ci], Rw[0]], writes=[RS[ni]])
                    bv = bank()
                    for d in range(2):
                        tile_d = tf if d == 0 else tbk
                        row = n if d == 0 else 127 - n
                        for h2 in range(2):
                            s.op("pe", lambda e, bv=bv, d=d, h2=h2, tile_d=tile_d, row=row: e.matmul(
                                ps[bv][64 * h2:64 * h2 + 64, d * 256:(d + 1) * 256].rearrange("p (c i) -> p c i", c=4),
                                ident_b[:, row:row + 1].to_broadcast([128, 64]),
                                V_tm[:, tile_d, :].rearrange("p (c h i) -> p c h i", c=4, h=2)[:, :, h2, :],
                                start=True, stop=True),
                                reads=[RVtm, Rconst], writes=[Rps[bv]])
                    bs_ = bank()
                    s.op("pe", lambda e, bs_=bs_: e.matmul(ps[bs_][:], bones_b[:], tmpA_b, start=True, stop=True),
                         reads=[RtAb, Rconst], writes=[Rps[bs_]])
                    flush()
                    s.op("dve", lambda e, n=n, bv=bv: e.tensor_tensor(tB8, ps[bv][:].rearrange("p (k i) -> p k i", k=8), bc(TKD, n), ALU.mult),
                         reads=[Rps[bv], Rw[1]], writes=[RtB])
                    s.op("dve", lambda e, ni=ni: e.tensor_tensor(Sst[ni], Sst[ni], tmpB, ALU.add),
                         reads=[RS[ni], RtB], writes=[RS[ni]])
                    s.op("dve", lambda e, n=n, bs_=bs_: e.tensor_tensor(tA8, ps[bs_][:].rearrange("p (k i) -> p k i", k=8), bc(TNK, n), ALU.mult),
                         reads=[Rps[bs_], Rw[1]], writes=[RtA])
                    s.op("dve", lambda e, ni=ni: e.tensor_tensor(Sst[ni], Sst[ni], tmpA, ALU.add),
                         reads=[RS[ni], RtA], writes=[RS[ni]])
                    s.op("act", lambda e, ni=ni: e.copy(Sb16, Sst[ni]), reads=[RS[ni]], writes=[RSb])
                    nl = n % 32

                    def ymm(by_=by_, nl=nl, n=n):
                        for dc in range(8):
                            s.op("pe", lambda e, dc=dc: e.matmul(
                                ps[by_][0:64, nl * 16 + dc * 2:nl * 16 + dc * 2 + 2], Sb16[:, dc * 64:(dc + 1) * 64], TR2[:, dc, n, :],
                                start=True, stop=True),
                                reads=[RSb, Rw[2]], writes=[Rps[by_]])
                    pending.append(ymm)
                    if nl == 31:
                        flush()
                        n0 = n - 31
                        src = ps[by_][0:64, :].rearrange("p (n k) -> p n k", k=16)
                        s.op("act", lambda e, src=src, n0=n0: e.copy(Ystage[:, n0:n0 + 32, 0:8], src[:, :, 0:8]),
                             reads=[Rps[by_]], writes=[RYst])
                        s.op("act", lambda e, src=src, n0=n0: e.copy(Ystage[:, 96 - n0:128 - n0, 8:16][:, ::-1, :], src[:, :, 8:16]),
                             reads=[Rps[by_]], writes=[RYst])
                        reserved.discard(by_)
                if tf % 2 == 1:
                    s.dma("sp", s_out[tf // 2, 0], Sst[0][:, 0:256], reads=[RS[0]])
                if tbk % 2 == 0:
                    s.dma("sp", s_out[tbk // 2, 1], Sst[0][:, 256:512], reads=[RS[0]])
                for d in range(2):
                    tile_d = tf if d == 0 else tbk
                    byy = bank()
                    for k8 in range(8):
                        s.op("pe", lambda e, byy=byy, k8=k8, d=d: e.matmul(
                            ps[byy][:, k8 * 64:(k8 + 1) * 64], Ystage[:, :, d * 8 + k8], ident_f[0:64, 0:64], start=True, stop=True),
                            reads=[RYst, Rconst], writes=[Rps[byy]])
                    first = (r < 5)
                    if first:
                        s.op("act", lambda e, byy=byy, tile_d=tile_d: e.copy(YSb[:, tile_d, :], ps[byy][:]),
                             reads=[Rps[byy]], writes=[RYS])
                    else:
                        finalize(tile_d, byy)

        if stage >= 1:
            if "a" in sub:
                make_gg(0)
            if "b" in sub:
                make_hT(0, 0)
            if "c" in sub:
                even_inproj()
        if stage >= 3:
            attention()
            barrier()
            if stage >= 5:
                rwkv_setup()
                rwkv_prepass()
                rwkv_scan(NT if stage >= 6 else 2)
            else:
                s.op("pool", lambda e: e.memset(yT[:, 4:8, :], 0.0), writes=[RyT])
            barrier()
            out_proj(w_out_even, yT, RyT)
        if stage >= 2:
            barrier()
            ffn(0)
            barrier()
        if stage >= 4:
            make_gg(1)
            make_hT(1, 0)
            fnet()
            out_proj(w_out_odd, yT, RyT)
            barrier()
            ffn(1)
        for i in range(NT):
            s.dma("sp", y_out[i * 128:(i + 1) * 128, :], x_sb[:, i, :], reads=[Rx[i]])
        s.emit()
    return nc


def _core_units(c):
    if c < 6:
        return [("p", 5 * c + u) for u in range(5)]
    b = c - 6
    return [("s", b, u) for u in range(4)] + [("p", 30 + b)]


def _host_prep(inp):
    f32 = np.float32
    COFF, NCOL = colp_layout()
    ROFF, NROW = rowp_layout()
    g = {k: np.asarray(v) for k, v in inp.items()}
    colp = np.zeros((128, NCOL), f32)

    def put(name, vec):
        c = cols_of(vec)
        colp[:, COFF[name]:COFF[name] + c.shape[1]] = c
    for l in range(2):
        put("bada%d" % l, g["b_ada"][l])
        for k in range(4):
            put("gain%d_%d" % (l, k), g["norm_gains"][l, k])
        for k in range(3):
            put("conv%d_%d" % (l, k), g["ffn_conv"][l, k])
        put("convb%d" % l, g["ffn_conv_b"][l])
    put("mu0", g["rwkv_shift_mu"][0, 0])
    put("mu1", g["rwkv_shift_mu"][0, 1])
    for d in range(2):
        put("w0_%d" % d, g["rwkv_w0"][0, d])
        put("a0_%d" % d, g["rwkv_a0"][0, d])
    for k in range(3):
        put("kvec%d" % k, g["rwkv_kvec"][0, k])

    cmat = np.zeros((4, 128, 128), f32)
    cmat[0] = np.eye(128, dtype=f32)
    for i in range(64):
        cmat[1][2 * i, 2 * i + 1] = 1.0
        cmat[1][2 * i + 1, 2 * i] = -1.0
    cmat[2][:64, :64] = 1.0
    cmat[2][64:, 64:] = 1.0
    cmat[3][:64, 0] = 1.0
    cmat[3][64:, 1] = 1.0
    cc = np.arange(128)
    chang = 2.0 * np.pi * ((cc[:, None] * cc[None, :]) % 128) / 128.0
    chCS = np.concatenate([np.cos(chang), np.sin(chang)], axis=1).astype(f32)

    inv = (10000.0 ** (-np.arange(16, dtype=np.float32) / 16)).astype(f32)

    shared = dict(
        colp=colp, w_ada=g["w_ada"], w_in_even=g["w_in_even"][0], w_out_even=g["w_out_even"][0],
        w_out_odd=g["w_out_odd"][0], w_ffn_in=g["w_ffn_in"], w_ffn_out=g["w_ffn_out"],
        w2t=np.ascontiguousarray(g["rwkv_w2"][0].reshape(128, 512)),
        a2t=np.ascontiguousarray(g["rwkv_a2"][0].reshape(128, 512)),
        g2=np.ascontiguousarray(g["rwkv_g2"][0]), chCS=chCS, cmat=cmat,
        lnx=np.ascontiguousarray(g["rwkv_lnx"][0]))
    maps = []
    for c in range(NCORES):
        units = _core_units(c)
        xs = []
        for un in units:
            if un[0] == "p":
                xs.append(g["x_prompt"][un[1]])
            else:
                xs.append(g["x_sample"][un[1], un[2] * 256:(un[2] + 1) * 256])
        x_in = np.ascontiguousarray(np.concatenate(xs, axis=0), dtype=f32)
        is_s = c >= 6
        condA = g["c"][c - 6] if is_s else g["c_ctx"]
        condB = g["c_ctx"]
        cond = np.stack([condA, condB], axis=0).astype(f32)
        condT = np.ascontiguousarray(cond.reshape(2, 8, 128).transpose(2, 1, 0).reshape(128, 16))
        rowp = np.zeros((1, NROW), f32)
        rowp[0, ROFF["lam"]:ROFF["lam"] + 256] = g["diff_lambda"][0].reshape(-1)
        rowp[0, ROFF["subln"]:ROFF["subln"] + 128] = g["diff_subln"][0]
        ab = np.full((6, 5), -30000.0, f32)
        if is_s:
            ab[0:5, 0:4] = 0.0
            ab[5, 4] = 0.0
            bndL = [1, 0, 0, 0, 1]
            bndR = [0, 0, 0, 1, 1]
        else:
            for u in range(5):
                ab[u + 1, u] = 0.0
            bndL = [1] * 5
            bndR = [1] * 5
        rowp[0, ROFF["abias"]:ROFF["abias"] + 30] = ab.reshape(-1)
        rowp[0, ROFF["bndL"]:ROFF["bndL"] + 5] = bndL
        rowp[0, ROFF["bndR"]:ROFF["bndR"] + 5] = bndR
        ropeC = np.ones((128, T), f32)
        ropeS = np.zeros((128, T), f32)
        if is_s:
            t = np.arange(1024)
            row = (t // 64).astype(f32)
            col = (t % 64).astype(f32)
            ang = np.concatenate([row[:, None] * inv[None, :], col[:, None] * inv[None, :]], axis=1).astype(f32)
            pidx = (np.arange(128) % 64) // 2
            ropeC[:, :1024] = np.cos(ang)[:, pidx].T
            ropeS[:, :1024] = np.sin(ang)[:, pidx].T
        cacheKT = np.zeros((512, 256), f32)
        cacheV = np.zeros((256, 512), f32)
        initS = np.zeros((2, 128, 256), f32)
        if is_s:
            b = c - 6
            cacheKT[:] = g["cache_k"][b, 0].reshape(256, 512).T
            cacheV[:] = g["cache_v"][b, 0].reshape(256, 512)
            st = g["state_wkv"][b, 0]
            initS[:] = st.reshape(2, 4, 2, 64, 64).transpose(0, 2, 4, 1, 3).reshape(2, 128, 256)
        dC = np.zeros((T, T), np.float64)
        dS = np.zeros((T, T), np.float64)
        blocks = [(0, 1024), (1024, 256)] if is_s else [(256 * u, 256) for u in range(5)]
        for (a0, L) in blocks:
            ll = np.arange(L)
            ang = 2.0 * np.pi * ((ll[:, None] * ll[None, :]) % L) / L
            sc = 1.0 / math.sqrt(L * 128.0)
            dC[a0:a0 + L, a0:a0 + L] = np.cos(ang) * sc
            dS[a0:a0 + L, a0:a0 + L] = -np.sin(ang) * sc
        m = dict(shared)
        m.update(x_in=x_in, condT=condT, rowp=rowp, ropeC=ropeC, ropeS=ropeS, cacheKT=cacheKT, cacheV=cacheV,
                 initS=initS, dftC=dC.astype(f32), dftSn=dS.astype(f32))
        maps.append(m)
    return maps


_NC_CACHE = {}


def kernel(**inputs):
    maps = _host_prep(inputs)
    if "nc" not in _NC_CACHE:
        _NC_CACHE["nc"] = build()
    nc = _NC_CACHE["nc"]
    import os
    ncr = int(os.environ.get("NCR", NCORES))
    res = run_bass_kernel_spmd(nc, maps[:ncr], core_ids=list(range(ncr)))
    outs = list(res.results) + [res.results[0]] * (NCORES - ncr)
    y_prompt = np.zeros((32, 256, D), np.float32)
    y_sample = np.zeros((2, 1024, D), np.float32)
    nk = np.zeros((32, 1, 256, 4, 128), np.float32)
    nv = np.zeros((32, 1, 256, 4, 128), np.float32)
    ns = np.zeros((32, 1, 2, 8, 64, 64), np.float32)
    for c in range(NCORES):
        r = outs[c]
        for u, un in enumerate(_core_units(c)):
            sl = slice(u * 256, (u + 1) * 256)
            if un[0] == "p":
                bi = un[1]
                y_prompt[bi] = r["y_out"][sl]
                nk[bi, 0] = r["k_out"][sl].reshape(256, 4, 128)
                nv[bi, 0] = r["v_out"][sl].reshape(256, 4, 128)
                stt = r["s_out"][u]
                ns[bi, 0] = stt.reshape(2, 2, 64, 4, 64).transpose(0, 3, 1, 4, 2).reshape(2, 8, 64, 64)
            else:
                y_sample[un[1], un[2] * 256:(un[2] + 1) * 256] = r["y_out"][sl]
    return (y_prompt, y_sample, nk, nv, ns)
```

```python
import contextlib
import math
import numpy as np
import concourse.bass as bass
import concourse.mybir as mybir
from concourse.bass_utils import run_bass_kernel_spmd

F32 = mybir.dt.float32
BF16 = mybir.dt.bfloat16
AF = mybir.ActivationFunctionType
ALU = mybir.AluOpType
AX = mybir.AxisListType

T = 1280
NT = 10
U = 5
D = 1024
KC = 8
DFF = 2816
FC = 22
NCORES = 8
ARENA_W = 28000
EXPM05 = math.exp(-0.5)


class Res:
    __slots__ = ("name", "writer", "readers", "excl")

    def __init__(self, name, excl=False):
        self.name = name
        self.writer = None
        self.readers = []
        self.excl = excl


class Sched:
    ENGS = ("pe", "act", "dve", "pool", "sp")
    NDMA = 6

    def __init__(self, nc):
        self.nc = nc
        self.prog = {e: [] for e in self.ENGS}
        self.signal = {e: set() for e in self.ENGS}
        self.ndma = {e: 0 for e in self.ENGS}

    def _collect(self, reads, writes, eng=None):
        deps = []
        for r in reads:
            if r.writer is not None:
                deps.append(r.writer)
            if r.excl:
                deps.extend(t for t in r.readers if t[1] != eng)
        for w in writes:
            if w.writer is not None:
                deps.append(w.writer)
            deps.extend(w.readers)
        return deps

    def _commit(self, tok, reads, writes):
        for r in reads:
            r.readers.append(tok)
        for w in writes:
            w.writer = tok
            w.readers = []

    def op(self, eng, fn, reads=(), writes=()):
        deps = self._collect(reads, writes, eng)
        idx = len(self.prog[eng])
        if eng == "pe":
            deps = [d for d in deps if not (d[0] == "c" and d[1] == "pe")]
        for d in deps:
            if d[0] == "c":
                self.signal[d[1]].add(d[2])
        self.prog[eng].append(dict(fn=fn, deps=deps, kind="c"))
        tok = ("c", eng, idx)
        self._commit(tok, reads, writes)
        return tok

    def dma(self, eng, out, in_, reads=(), writes=()):
        deps = self._collect(reads, writes, eng)
        n = self.ndma[eng]
        self.ndma[eng] += 1
        if n >= self.NDMA:
            deps.append(("d", eng, n - self.NDMA))
        for d in deps:
            if d[0] == "c":
                self.signal[d[1]].add(d[2])
        self.prog[eng].append(dict(out=out, in_=in_, deps=deps, kind="d", n=n))
        tok = ("d", eng, n)
        self._commit(tok, reads, writes)
        return tok

    def emit(self):
        nc = self.nc
        with contextlib.ExitStack() as st:
            csem = {e: st.enter_context(nc.semaphore("c_" + e)) for e in self.ENGS}
            dsem = {e: [st.enter_context(nc.semaphore("d_%s_%d" % (e, i))) for i in range(self.NDMA)]
                    for e in self.ENGS if self.ndma[e] > 0}
            sigval = {}
            for e in self.ENGS:
                cnt = 0
                m = {}
                for i in range(len(self.prog[e])):
                    if i in self.signal[e]:
                        cnt += 1
                        m[i] = cnt
                sigval[e] = m

            def resolve(tok):
                if tok[0] == "c":
                    return csem[tok[1]], sigval[tok[1]][tok[2]], ("c", tok[1])
                e, n = tok[1], tok[2]
                return dsem[e][n % self.NDMA], 16 * (n // self.NDMA + 1), ("d", e, n % self.NDMA)

            def run_engine(e, h):
                waited = {}
                for i, ins in enumerate(self.prog[e]):
                    need = {}
                    for d in ins["deps"]:
                        sem, val, key = resolve(d)
                        if waited.get(key, 0) >= val:
                            continue
                        if key not in need or need[key][1] < val:
                            need[key] = (sem, val)
                    for key, (sem, val) in need.items():
                        h.wait_ge(sem, val)
                        waited[key] = val
                    if ins["kind"] == "c":
                        bi = ins["fn"](h)
                        if i in self.signal[e]:
                            bi.then_inc(csem[e], 1)
                    else:
                        n = ins["n"]
                        h.dma_start(out=ins["out"], in_=ins["in_"]).then_inc(dsem[e][n % self.NDMA], 16)
                if self.ndma[e] > 0:
                    n = self.ndma[e]
                    for slot in range(self.NDMA):
                        cnt = (n - slot + self.NDMA - 1) // self.NDMA if n > slot else 0
                        if cnt > 0:
                            h.wait_ge(dsem[e][slot], 16 * cnt)

            with nc.Block() as block:
                @block.tensor
                def _(eng):
                    run_engine("pe", eng)

                @block.scalar
                def _(eng):
                    run_engine("act", eng)

                @block.vector
                def _(eng):
                    run_engine("dve", eng)

                @block.gpsimd
                def _(eng):
                    run_engine("pool", eng)

                @block.sync
                def _(eng):
                    run_engine("sp", eng)


def colp_layout():
    off = {}
    n = 0

    def add(name, cols):
        nonlocal n
        off[name] = n
        n += cols
    for l in range(2):
        add("bada%d" % l, 48)
        for g in range(4):
            add("gain%d_%d" % (l, g), 8)
        for k in range(3):
            add("conv%d_%d" % (l, k), FC)
        add("convb%d" % l, FC)
    add("mu0", 15)
    add("mu1", 15)
    for d in range(2):
        add("w0_%d" % d, 4)
        add("a0_%d" % d, 4)
    for k in range(3):
        add("kvec%d" % k, 4)
    return off, n


def rowp_layout():
    off = {}
    n = 0

    def add(name, cols):
        nonlocal n
        off[name] = n
        n += cols
    add("lam", 256)
    add("subln", 128)
    add("abias", 30)
    add("bndL", 5)
    add("bndR", 5)
    return off, n


def cols_of(vec):
    v = np.asarray(vec, np.float32).reshape(-1, 128)
    return np.ascontiguousarray(v.T)


def build(stage=99):
    nc = bass.Bass("TRN2", target_bir_lowering=False)
    COFF, NCOL = colp_layout()
    ROFF, NROW = rowp_layout()

    def din(name, shape):
        return nc.dram_tensor(name, list(shape), F32, kind="ExternalInput").ap()

    def dout(name, shape):
        return nc.dram_tensor(name, list(shape), F32, kind="ExternalOutput").ap()

    x_in = din("x_in", [T, D])
    condT = din("condT", [128, 16])
    colp = din("colp", [128, NCOL])
    rowp = din("rowp", [1, NROW])
    w_ada = din("w_ada", [2, D, 6 * D])
    w_in_even = din("w_in_even", [D, 3456])
    w_out_even = din("w_out_even", [D, D])
    w_out_odd = din("w_out_odd", [D, D])
    w_ffn_in = din("w_ffn_in", [2, D, 2 * DFF])
    w_ffn_out = din("w_ffn_out", [2, DFF, D])
    w2t_d = din("w2t", [128, 512])
    a2t_d = din("a2t", [128, 512])
    g2_d = din("g2", [128, 512])
    cacheKT = din("cacheKT", [512, 256])
    cacheV = din("cacheV", [256, 512])
    ropeC = din("ropeC", [128, T])
    ropeS = din("ropeS", [128, T])
    initS = din("initS", [2, 128, 256])
    dftC = din("dftC", [T, T])
    dftSn = din("dftSn", [T, T])
    chCS = din("chCS", [128, 256])
    cmat = din("cmat", [8, 128, 128])
    lnx_d = din("lnx", [2, 512])

    y_out = dout("y_out", [T, D])
    k_out = dout("k_out", [T, 512])
    v_out = dout("v_out", [T, 512])
    s_out = dout("s_out", [U, 2, 128, 256])

    import os
    sub = os.environ.get("SUB", "abcmqv")
    with contextlib.ExitStack() as st:
        s = Sched(nc)

        def sb(name, shape, dt=F32):
            return st.enter_context(nc.sbuf_tensor(name, list(shape), dt))

        x_sb = sb("x_sb", [128, NT, D])
        Rx = [Res("x%d" % i) for i in range(NT)]
        WB = 4096
        wbuf = [sb("wb%d" % i, [128, WB], BF16) for i in range(3)]
        Rw = [Res("wb%d" % i) for i in range(3)]
        wctr = [0]
        colp_sb = sb("colp_sb", [128, NCOL])
        Rcolp = Res("colp")
        rowp_sb = sb("rowp_sb", [128, NROW])
        Rrowp = Res("rowp")
        ident_f = sb("ident_f", [128, 128])
        ident_b = sb("ident_b", [128, 128], BF16)
        rotT_b = sb("rotT_b", [128, 128], BF16)
        bones_f = sb("bones_f", [128, 128])
        halfsel_f = sb("halfsel_f", [128, 128])
        bones_b = sb("bones_b", [128, 128], BF16)
        Rconst = Res("const")
        gg_sb = sb("gg_sb", [128, 2, 2, D], BF16)
        Rgg = Res("gg")
        scond = sb("scond", [128, 16], BF16)
        condf = sb("condf", [128, 16])
        Rcond = Res("cond")
        modcol = sb("modcol", [128, 2, 96])
        Rmod = Res("modcol")
        scsh = sb("scsh", [128, 2, 2, 2, 16])
        Rscsh = Res("scsh")
        stat = sb("stat", [128, 64])
        Rstat = Res("stat")
        junk_b = sb("junk_b", [128, D], BF16)
        Rjunk = Res("junk")
        xn_b = sb("xn_b", [128, D], BF16)
        Rxn = Res("xn")
        stage_f = sb("stage_f", [128, 2, 512])
        Rstage = [Res("stage0"), Res("stage1")]
        stctr = [0]
        arena = sb("arena", [128, ARENA_W])
        Rar = {}

        def ares(name):
            if name not in Rar:
                Rar[name] = Res("ar_" + name)
            return Rar[name]

        def carve(off_w, nelem, dt):
            if dt == F32:
                return arena[:, off_w:off_w + nelem]
            return arena[:, off_w:off_w + (nelem + 1) // 2].bitcast(BF16)[:, 0:nelem]

        ps = [st.enter_context(nc.psum_tensor("ps%d" % b, [128, 512], F32)) for b in range(8)]
        psb = [p.bitcast(BF16) for p in ps]
        Rps = [Res("ps%d" % b, excl=True) for b in range(8)]
        bctr = [0]

        reserved = set()

        def bank():
            while True:
                b = bctr[0] % 8
                bctr[0] += 1
                if b not in reserved:
                    return b

        def barrier():
            allr = list(Rar.values()) + list(Rw)
            s.op("dve", lambda e: e.memset(stat[:, 63:64], 0.0), writes=allr + [Rstat])

        def wload(src_ap, kc, ncols):
            i = wctr[0] % 3
            wctr[0] += 1
            view = wbuf[i][:, 0:kc * ncols].rearrange("p (c n) -> p c n", c=kc)
            s.dma("pool", view, src_ap.rearrange("(c p) n -> p c n", p=128), writes=[Rw[i]])
            return view, Rw[i]

        s.dma("sp", colp_sb[:], colp, writes=[Rcolp])
        s.dma("sp", rowp_sb[:], rowp[0, :].partition_broadcast(128), writes=[Rrowp])
        s.dma("sp", ident_f[:], cmat[0], writes=[Rconst])
        s.dma("sp", bones_f[:], cmat[2], writes=[Rconst])
        s.dma("sp", halfsel_f[:], cmat[3], writes=[Rconst])
        s.dma("pool", ident_b[:], cmat[0], writes=[Rconst])
        s.dma("pool", rotT_b[:], cmat[1], writes=[Rconst])
        s.dma("pool", bones_b[:], cmat[2], writes=[Rconst])
        s.dma("sp", condf[:], condT, writes=[Rcond])
        for i in range(NT):
            s.dma("sp", x_sb[:, i, :], x_in[i * 128:(i + 1) * 128, :], writes=[Rx[i]])
        s.op("act", lambda e: e.activation(scond[:], condf[:], AF.Silu), reads=[Rcond], writes=[Rcond])

        def cp(name, c=0, n=1):
            o = COFF[name] + c
            return colp_sb[:, o:o + n]

        for l in range(2):
            b = bank()
            for v in range(6):
                for half in range(2):
                    wv, rw = wload(w_ada[l][:, v * 1024 + half * 512: v * 1024 + half * 512 + 512], KC, 512)
                    for j in range(4):
                        col = (v * 8 + half * 4 + j) * 2
                        for kc in range(KC):
                            s.op("pe", lambda e, wv=wv, j=j, kc=kc, col=col, b=b: e.matmul(
                                ps[b][:, col:col + 2], wv[:, kc, j * 128:(j + 1) * 128],
                                scond[:, kc * 2:kc * 2 + 2], start=(kc == 0), stop=(kc == KC - 1)),
                                reads=[rw, Rcond], writes=[Rps[b]])
            s.op("dve", lambda e, l=l, b=b: e.tensor_tensor(
                modcol[:, l, :].rearrange("p (c a) -> p c a", a=2),
                ps[b][:, 0:96].rearrange("p (c a) -> p c a", a=2),
                cp("bada%d" % l, 0, 48).unsqueeze(2).to_broadcast([128, 48, 2]), ALU.add),
                reads=[Rps[b], Rcolp], writes=[Rmod])

            def mv(v, l=l):
                return modcol[:, l, v * 16:(v + 1) * 16].rearrange("p (c a) -> p c a", a=2)

            def gain(g, l=l):
                return cp("gain%d_%d" % (l, g), 0, 8).unsqueeze(2).to_broadcast([128, 8, 2])
            for wi, (vs, vsh, g) in enumerate([(1, 0, 0), (4, 3, 2)]):
                sc = scsh[:, l, wi, 0, :].rearrange("p (c a) -> p c a", a=2)
                sh = scsh[:, l, wi, 1, :].rearrange("p (c a) -> p c a", a=2)
                s.op("dve", lambda e, sc=sc, vs=vs, mv=mv: e.tensor_scalar(sc, mv(vs), 1.0, None, ALU.add),
                     reads=[Rmod], writes=[Rscsh])
                s.op("dve", lambda e, sc=sc, g=g, gain=gain: e.tensor_tensor(sc, sc, gain(g), ALU.mult),
                     reads=[Rscsh, Rcolp], writes=[Rscsh])
                s.op("dve", lambda e, sh=sh, vsh=vsh, mv=mv: e.tensor_copy(sh, mv(vsh)),
                     reads=[Rmod], writes=[Rscsh])

        ggcol = sb("ggcol", [128, 2, 16])
        Rggcol = Res("ggcol")

        def make_gg(l):
            for wi, (vg, g) in enumerate([(2, 1), (5, 3)]):
                gc = ggcol[:, wi, :].rearrange("p (c a) -> p c a", a=2)
                s.op("dve", lambda e, gc=gc, vg=vg, g=g, l=l: e.tensor_tensor(
                    gc, modcol[:, l, vg * 16:(vg + 1) * 16].rearrange("p (c a) -> p c a", a=2),
                    cp("gain%d_%d" % (l, g), 0, 8).unsqueeze(2).to_broadcast([128, 8, 2]), ALU.mult),
                    reads=[Rmod, Rcolp], writes=[Rggcol])
                for a in range(2):
                    for hh in range(2):
                        b = bank()
                        for j in range(4):
                            c = hh * 4 + j
                            s.op("pe", lambda e, b=b, wi=wi, c=c, a=a, j=j: e.matmul(
                                ps[b][:, j * 128:(j + 1) * 128],
                                ggcol[:, wi, c * 2 + a:c * 2 + a + 1].to_broadcast([128, 128]),
                                ident_f[:], start=True, stop=True),
                                reads=[Rggcol, Rconst], writes=[Rps[b]])
                        s.op("act", lambda e, b=b, wi=wi, a=a, hh=hh: e.copy(
                            gg_sb[:, wi, a, hh * 512:(hh + 1) * 512], ps[b][:]),
                            reads=[Rps[b]], writes=[Rgg])

        hT = carve(0, 8 * T, BF16).rearrange("p (c t) -> p c t", c=8)
        RhT = ares("hT")

        def ab_of_tile(i):
            return 0 if i < 8 else 1

        def make_hT(l, wi):
            for i in range(NT):
                a = ab_of_tile(i)
                s.op("act", lambda e, i=i: e.activation(junk_b[:], x_sb[:, i, :], AF.Square, scale=1.0 / 32.0,
                                                        accum_out=stat[:, 0:1]),
                     reads=[Rx[i]], writes=[Rjunk, Rstat])
                s.op("dve", lambda e: e.tensor_scalar(stat[:, 1:2], stat[:, 0:1], 1e-6, None, ALU.add),
                     reads=[Rstat], writes=[Rstat])
                s.op("act", lambda e: e.activation(stat[:, 1:2], stat[:, 1:2], AF.Sqrt), reads=[Rstat], writes=[Rstat])
                s.op("dve", lambda e: e.reciprocal(stat[:, 2:3], stat[:, 1:2]), reads=[Rstat], writes=[Rstat])
                s.op("dve", lambda e, i=i: e.tensor_scalar(xn_b[:], x_sb[:, i, :], stat[:, 2:3], None, ALU.mult),
                     reads=[Rx[i], Rstat], writes=[Rxn])
                b = bank()
                for c in range(8):
                    s.op("pe", lambda e, b=b, c=c: e.transpose(psb[b][:, c * 128:(c + 1) * 128],
                                                               xn_b[:, c * 128:(c + 1) * 128], ident_b[:]),
                         reads=[Rxn, Rconst], writes=[Rps[b]])
                for c in range(8):
                    s.op("act", lambda e, b=b, c=c, i=i, a=a: e.activation(
                        hT[:, c, i * 128:(i + 1) * 128], psb[b][:, c * 128:(c + 1) * 128], AF.Identity,
                        bias=scsh[:, l, wi, 1, c * 2 + a:c * 2 + a + 1],
                        scale=scsh[:, l, wi, 0, c * 2 + a:c * 2 + a + 1]),
                        reads=[Rps[b], Rscsh], writes=[RhT])

        TOKCH = [(0, 512), (512, 512), (1024, 256)]

        def linear_fm(wv, rw, ncol0, actT, Ract, kc_n, evac):
            for (t0, tn) in TOKCH:
                b = bank()
                for kc in range(kc_n):
                    s.op("pe", lambda e, b=b, kc=kc, t0=t0, tn=tn: e.matmul(
                        ps[b][:, 0:tn], wv[:, kc, ncol0:ncol0 + 128], actT[:, kc, t0:t0 + tn],
                        start=(kc == 0), stop=(kc == kc_n - 1)),
                        reads=[rw, Ract], writes=[Rps[b]])
                evac(b, t0, tn)

        def linear_tm(wv, rw, ncols, actT, Ract, kc_n, i, b, col0=0):
            for kc in range(kc_n):
                s.op("pe", lambda e, kc=kc: e.matmul(
                    ps[b][:, col0:col0 + ncols], actT[:, kc, i * 128:(i + 1) * 128], wv[:, kc, 0:ncols],
                    start=(kc == 0), stop=(kc == kc_n - 1)),
                    reads=[rw, Ract], writes=[Rps[b]])

        def residual_update(i, banks, wi, fsrc=None, Rf=None):
            a = ab_of_tile(i)
            if fsrc is None:
                for hh, b in enumerate(banks):
                    s.op("act", lambda e, b=b, hh=hh: e.activation(junk_b[:, 0:512], ps[b][:], AF.Square,
                                                                   scale=1.0 / 32.0, accum_out=stat[:, 8 + hh:9 + hh]),
                         reads=[Rps[b]], writes=[Rjunk, Rstat])
                s.op("dve", lambda e: e.tensor_tensor(stat[:, 10:11], stat[:, 8:9], stat[:, 9:10], ALU.add),
                     reads=[Rstat], writes=[Rstat])
            else:
                s.op("act", lambda e: e.activation(junk_b[:], fsrc, AF.Square, scale=1.0 / 32.0,
                                                   accum_out=stat[:, 10:11]),
                     reads=[Rf], writes=[Rjunk, Rstat])
            s.op("dve", lambda e: e.tensor_scalar(stat[:, 11:12], stat[:, 10:11], 1e-6, None, ALU.add),
                 reads=[Rstat], writes=[Rstat])
            s.op("act", lambda e: e.activation(stat[:, 11:12], stat[:, 11:12], AF.Sqrt), reads=[Rstat], writes=[Rstat])
            s.op("dve", lambda e: e.reciprocal(stat[:, 12:13], stat[:, 11:12]), reads=[Rstat], writes=[Rstat])
            for hh in range(2):
                src = ps[banks[hh]][:] if fsrc is None else fsrc[:, hh * 512:(hh + 1) * 512]
                rr = [Rps[banks[hh]]] if fsrc is None else [Rf]
                si = stctr[0] % 2
                stctr[0] += 1
                s.op("dve", lambda e, src=src, hh=hh, si=si: e.scalar_tensor_tensor(
                    stage_f[:, si, :], src, stat[:, 12:13], gg_sb[:, wi, a, hh * 512:(hh + 1) * 512],
                    ALU.mult, ALU.mult),
                    reads=rr + [Rstat, Rgg], writes=[Rstage[si]])
                s.op("dve", lambda e, hh=hh, si=si, i=i: e.tensor_tensor(
                    x_sb[:, i, hh * 512:(hh + 1) * 512], x_sb[:, i, hh * 512:(hh + 1) * 512], stage_f[:, si, :], ALU.add),
                    reads=[Rstage[si], Rx[i]], writes=[Rx[i]])

        actT = carve(5120, FC * T, BF16).rearrange("p (c t) -> p c t", c=FC)
        RactT = ares("actT")
        graw = carve(19200, T + 2, F32)
        Rgraw = ares("graw")
        cbuf = carve(19200 + 1284, T, F32)
        Rcbuf = ares("cbuf")
        sbuf_s = carve(19200 + 1284 + 1280, T, F32)
        Rsbuf = ares("sbuf_s")
        fstageA = carve(19200, 5 * D, F32).rearrange("p (i n) -> p i n", i=5)
        fstageB = carve(0, 5 * D, F32).rearrange("p (i n) -> p i n", i=5)

        def fst(i):
            return fstageA[:, i, :] if i < 5 else fstageB[:, i - 5, :]

        def Rfst_of(i):
            return ares("fstage") if i < 5 else RhT
        bcorr = sb("bcorr", [128, 16])
        Rbcorr = Res("bcorr")

        def ffn(l):
            make_hT(l, 1)
            s.op("dve", lambda e: e.memset(graw[:, 0:1], 0.0), writes=[Rgraw])
            s.op("dve", lambda e: e.memset(graw[:, T + 1:T + 2], 0.0), writes=[Rgraw])
            for blk in range(0, FC, 2):
                wu, ru = wload(w_ffn_in[l][:, blk * 128: blk * 128 + 256], KC, 256)
                wg, rg = wload(w_ffn_in[l][:, DFF + blk * 128: DFF + blk * 128 + 256], KC, 256)
                for jj in range(2):
                    fc = blk + jj
                    def evac_g(b, t0, tn):
                        s.op("act", lambda e, b=b, t0=t0, tn=tn: e.copy(graw[:, 1 + t0:1 + t0 + tn], ps[b][:, 0:tn]),
                             reads=[Rps[b]], writes=[Rgraw])
                    linear_fm(wg, rg, jj * 128, hT, RhT, KC, evac_g)
                    w0 = cp("conv%d_0" % l, fc)
                    w1 = cp("conv%d_1" % l, fc)
                    w2 = cp("conv%d_2" % l, fc)
                    cb = cp("convb%d" % l, fc)
                    s.op("act", lambda e, w1=w1, cb=cb: e.activation(cbuf[:], graw[:, 1:T + 1], AF.Identity, bias=cb, scale=w1),
                         reads=[Rgraw, Rcolp], writes=[Rcbuf])
                    s.op("dve", lambda e, w0=w0: e.scalar_tensor_tensor(cbuf[:], graw[:, 0:T], w0, cbuf[:], ALU.mult, ALU.add),
                         reads=[Rgraw, Rcolp, Rcbuf], writes=[Rcbuf])
                    s.op("dve", lambda e, w2=w2: e.scalar_tensor_tensor(cbuf[:], graw[:, 2:T + 2], w2, cbuf[:], ALU.mult, ALU.add),
                         reads=[Rgraw, Rcolp, Rcbuf], writes=[Rcbuf])
                    gprev = graw[:, 256:256 + 1024].rearrange("p (u k) -> p u k", k=256)[:, :, 0]
                    gnext = graw[:, 257:257 + 1024].rearrange("p (u k) -> p u k", k=256)[:, :, 0]
                    s.op("dve", lambda e, gprev=gprev: e.tensor_tensor(bcorr[:, 0:4], gprev, rowp_sb[:, ROFF["bndL"] + 1:ROFF["bndL"] + 5], ALU.mult),
                         reads=[Rgraw, Rrowp], writes=[Rbcorr])
                    s.op("dve", lambda e, w0=w0: e.tensor_scalar(bcorr[:, 0:4], bcorr[:, 0:4], w0, None, ALU.mult),
                         reads=[Rbcorr, Rcolp], writes=[Rbcorr])
                    c_at = cbuf[:, 256:256 + 1024].rearrange("p (u k) -> p u k", k=256)[:, :, 0]
                    s.op("dve", lambda e, c_at=c_at: e.tensor_tensor(c_at, c_at, bcorr[:, 0:4], ALU.subtract),
                         reads=[Rbcorr, Rcbuf], writes=[Rcbuf])
                    s.op("dve", lambda e, gnext=gnext: e.tensor_tensor(bcorr[:, 4:8], gnext, rowp_sb[:, ROFF["bndL"] + 1:ROFF["bndL"] + 5], ALU.mult),
                         reads=[Rgraw, Rrowp], writes=[Rbcorr])
                    s.op("dve", lambda e, w2=w2: e.tensor_scalar(bcorr[:, 4:8], bcorr[:, 4:8], w2, None, ALU.mult),
                         reads=[Rbcorr, Rcolp], writes=[Rbcorr])
                    c_at2 = cbuf[:, 255:255 + 1024].rearrange("p (u k) -> p u k", k=256)[:, :, 0]
                    s.op("dve", lambda e, c_at2=c_at2: e.tensor_tensor(c_at2, c_at2, bcorr[:, 4:8], ALU.subtract),
                         reads=[Rbcorr, Rcbuf], writes=[Rcbuf])
                    s.op("act", lambda e: e.activation(sbuf_s[:], cbuf[:], AF.Silu), reads=[Rcbuf], writes=[Rsbuf])
                    def evac_u(b, t0, tn, fc=fc):
                        s.op("dve", lambda e, b=b, t0=t0, tn=tn: e.tensor_tensor(
                            actT[:, fc, t0:t0 + tn], ps[b][:, 0:tn], sbuf_s[:, t0:t0 + tn], ALU.mult),
                            reads=[Rps[b], Rsbuf], writes=[RactT])
                    linear_fm(wu, ru, jj * 128, hT, RhT, KC, evac_u)
            for nb in range(8):
                wv, rw = wload(w_ffn_out[l][:, nb * 128:(nb + 1) * 128], FC, 128)
                for i in range(NT):
                    b = bank()
                    linear_tm(wv, rw, 128, actT, RactT, FC, i, b)
                    s.op("act", lambda e, b=b, i=i, nb=nb: e.copy(fst(i)[:, nb * 128:(nb + 1) * 128], ps[b][:, 0:128]),
                         reads=[Rps[b]], writes=[Rfst_of(i)])
            for i in range(NT):
                residual_update(i, None, 1, fsrc=fst(i), Rf=Rfst_of(i))

        qT = carve(5120, 4 * T, BF16).rearrange("p (c t) -> p c t", c=4)
        RqT = ares("qT")
        kT = carve(7680, 4 * 1536, BF16).rearrange("p (c t) -> p c t", c=4)
        RkT = ares("kT")
        Vaug = carve(10752, 12 * 4 * 144, BF16).rearrange("p (k h d) -> p k h d", k=12, h=4)
        RV = ares("Vaug")
        yT = carve(0, 8 * T, BF16).rearrange("p (c t) -> p c t", c=8)
        fbT = carve(14208, 15 * (T + 2), BF16).rearrange("p (c t) -> p c t", c=15)
        RfbT = ares("fbT")
        RyT = RhT
        ropeC_sb = carve(23824, T, F32)
        ropeS_sb = carve(23824 + T, T, F32)
        Rrope = Res("rope")
        rtmp = sb("rtmp", [128, 2, 512])
        Rrtmp = Res("rtmp")
        raw_b = sb("raw_b", [128, 512], BF16)
        Rraw = Res("raw_b")

        def even_inproj():
            s.dma("sp", ropeC_sb[:], ropeC, writes=[Rrope])
            s.dma("sp", ropeS_sb[:], ropeS, writes=[Rrope])
            s.dma("pool", kT[:, :, 0:256], cacheKT.rearrange("(h p) t -> p h t", p=128), writes=[RkT])
            for kk_ in range(2):
                s.dma("pool", Vaug[:, kk_, :, 0:128],
                      cacheV[kk_ * 128:(kk_ + 1) * 128, :].rearrange("p (h d) -> p h d", h=4), writes=[RV])
            if "m" in sub:
                s.op("pool", lambda e: e.memset(Vaug[:, :, :, 128:129], 1.0), writes=[RV])
            for which, dst, doff in ((0, qT, 0), (1, kT, 256)):
                if "q" not in sub:
                    break
                wv, rw = wload(w_in_even[:, which * 512:(which + 1) * 512], KC, 512)
                Rdst = RqT if which == 0 else RkT
                for h in range(4):
                    def evac(b, t0, tn, h=h, dst=dst, doff=doff, Rdst=Rdst):
                        s.op("act", lambda e, b=b, tn=tn: e.copy(raw_b[:, 0:tn], ps[b][:, 0:tn]),
                             reads=[Rps[b]], writes=[Rraw])
                        b2 = bank()
                        s.op("pe", lambda e, b2=b2, tn=tn: e.matmul(ps[b2][:, 0:tn], rotT_b[:], raw_b[:, 0:tn], start=True, stop=True),
                             reads=[Rraw, Rconst], writes=[Rps[b2]])
                        s.op("dve", lambda e, t0=t0, tn=tn: e.tensor_tensor(rtmp[:, 0, 0:tn], raw_b[:, 0:tn], ropeC_sb[:, t0:t0 + tn], ALU.mult),
                             reads=[Rraw, Rrope], writes=[Rrtmp])
                        s.op("dve", lambda e, b2=b2, t0=t0, tn=tn: e.tensor_tensor(rtmp[:, 1, 0:tn], ps[b2][:, 0:tn], ropeS_sb[:, t0:t0 + tn], ALU.mult),
                             reads=[Rps[b2], Rrope], writes=[Rrtmp])
                        s.op("dve", lambda e, t0=t0, tn=tn: e.tensor_tensor(dst[:, h, doff + t0:doff + t0 + tn], rtmp[:, 0, 0:tn], rtmp[:, 1, 0:tn], ALU.add),
                             reads=[Rrtmp], writes=[Rdst])
                    linear_fm(wv, rw, h * 128, hT, RhT, KC, evac)
                if which == 1:
                    for i in range(NT):
                        b = bank()
                        linear_tm(wv, rw, 512, hT, RhT, KC, i, b)
                        si = stctr[0] % 2
                        stctr[0] += 1
                        s.op("act", lambda e, b=b, si=si: e.copy(stage_f[:, si, :], ps[b][:]), reads=[Rps[b]], writes=[Rstage[si]])
                        s.dma("sp", k_out[i * 128:(i + 1) * 128, :], stage_f[:, si, :], reads=[Rstage[si]])
            s.op("pool", lambda e: e.memset(fbT[:, :, 0:1], 0.0), writes=[RfbT])
            s.op("pool", lambda e: e.memset(fbT[:, :, T + 1:T + 2], 0.0), writes=[RfbT])
            for pc, (c0, ncols) in enumerate([(1536, 512), (2048, 512), (2560, 512), (3072, 384)]):
                wv, rw = wload(w_in_even[:, c0:c0 + ncols], KC, ncols)
                for j in range(ncols // 128):
                    ch = pc * 4 + j

                    def evac_fb(b, t0, tn, ch=ch):
                        s.op("act", lambda e, b=b, t0=t0, tn=tn: e.copy(fbT[:, ch, 1 + t0:1 + t0 + tn], ps[b][:, 0:tn]),
                             reads=[Rps[b]], writes=[RfbT])
                    linear_fm(wv, rw, j * 128, hT, RhT, KC, evac_fb)
            wv, rw = wload(w_in_even[:, 1024:1536], KC, 512)
            for i in range(NT if "v" in sub else 0):
                b = bank()
                linear_tm(wv, rw, 512, hT, RhT, KC, i, b)
                si = stctr[0] % 2
                stctr[0] += 1
                s.op("act", lambda e, b=b, si=si: e.copy(stage_f[:, si, :], ps[b][:]), reads=[Rps[b]], writes=[Rstage[si]])
                s.dma("sp", v_out[i * 128:(i + 1) * 128, :], stage_f[:, si, :], reads=[Rstage[si]])
                s.op("dve", lambda e, si=si, i=i: e.tensor_copy(Vaug[:, 2 + i, :, 0:128], stage_f[:, si, :].rearrange("p (h d) -> p h d", h=4)),
                     reads=[Rstage[si]], writes=[RV])


        PT = [[sb("PT%d%d" % (m, k), [128, 256], BF16) for k in range(2)] for m in range(2)]
        RPT = [[Res("PT%d%d" % (m, k)) for k in range(2)] for m in range(2)]
        ya_f = sb("ya_f", [128, 128])
        Rya = Res("ya_f")
        ya_b = sb("ya_b", [128, 128], BF16)
        Ryab = Res("ya_b")
        subln08 = sb("subln08", [128, 128])
        Rsub = Res("subln08")
        lamt = sb("lamt", [128, 64])
        Rlamt = Res("lamt")

        def attention():
            lo = ROFF["lam"]
            for k in range(2):
                s.op("dve", lambda e, k=k: e.tensor_tensor(lamt[:], rowp_sb[:, lo + 128 * k: lo + 128 * k + 64],
                                                          rowp_sb[:, lo + 128 * k + 64: lo + 128 * k + 128], ALU.mult),
                     reads=[Rrowp], writes=[Rlamt])
                s.op("dve", lambda e, k=k: e.reduce_sum(stat[:, 20 + k:21 + k], lamt[:], axis=AX.X),
                     reads=[Rlamt], writes=[Rstat])
            s.op("act", lambda e: e.activation(stat[:, 22:24], stat[:, 20:22], AF.Exp), reads=[Rstat], writes=[Rstat])
            s.op("dve", lambda e: e.tensor_tensor(stat[:, 24:25], stat[:, 22:23], stat[:, 23:24], ALU.subtract),
                 reads=[Rstat], writes=[Rstat])
            s.op("dve", lambda e: e.tensor_scalar(stat[:, 25:26], stat[:, 24:25], 0.2, -1.0, ALU.add, ALU.mult),
                 reads=[Rstat], writes=[Rstat])
            s.op("dve", lambda e: e.tensor_scalar(subln08[:], rowp_sb[:, ROFF["subln"]:ROFF["subln"] + 128], 0.8, None, ALU.mult),
                 reads=[Rrowp], writes=[Rsub])
            pctr = [0, 0]
            for h in range(4):
                for qu in range(5):
                    acc = [bank(), bank()]
                    for kt in range(12):
                        ku = 0 if kt < 2 else 1 + (kt - 2) // 2
                        bcol = ROFF["abias"] + ku * 5 + qu
                        for m in range(2):
                            bs = bank()
                            while bs in acc:
                                bs = bank()
                            s.op("pe", lambda e, bs=bs, m=m, h=h, kt=kt, qu=qu: e.matmul(
                                ps[bs][:, 0:256], kT[64 * m:64 * m + 64, h, kt * 128:(kt + 1) * 128],
                                qT[64 * m:64 * m + 64, h, qu * 256:(qu + 1) * 256], start=True, stop=True),
                                reads=[RkT, RqT], writes=[Rps[bs]])
                            pk = pctr[m] % 2
                            pctr[m] += 1
                            s.op("act", lambda e, bs=bs, m=m, pk=pk, bcol=bcol: e.activation(
                                PT[m][pk][:], ps[bs][:, 0:256], AF.Exp, bias=rowp_sb[:, bcol:bcol + 1], scale=0.125),
                                reads=[Rps[bs], Rrowp], writes=[RPT[m][pk]])
                            for qt in range(2):
                                s.op("pe", lambda e, m=m, pk=pk, qt=qt, kt=kt, h=h, acc=acc: e.matmul(
                                    ps[acc[m]][:, qt * 129:qt * 129 + 129], PT[m][pk][:, qt * 128:(qt + 1) * 128],
                                    Vaug[:, kt, h, 0:129], start=(kt == 0 and qt == 0), stop=(kt == 11),
                                    skip_group_check=True),
                                    reads=[RPT[m][pk], RV], writes=[Rps[acc[m]]])
                    for qt in range(2):
                        i = qu * 2 + qt
                        c0 = qt * 129
                        s.op("dve", lambda e, c0=c0, acc=acc: e.reciprocal(stat[:, 30:31], ps[acc[0]][:, c0 + 128:c0 + 129]),
                             reads=[Rps[acc[0]]], writes=[Rstat])
                        s.op("dve", lambda e, c0=c0, acc=acc: e.reciprocal(stat[:, 31:32], ps[acc[1]][:, c0 + 128:c0 + 129]),
                             reads=[Rps[acc[1]]], writes=[Rstat])
                        s.op("dve", lambda e: e.tensor_tensor(stat[:, 32:33], stat[:, 31:32], stat[:, 25:26], ALU.mult),
                             reads=[Rstat], writes=[Rstat])
                        s.op("dve", lambda e, c0=c0, acc=acc: e.tensor_scalar(ya_f[:], ps[acc[0]][:, c0:c0 + 128], stat[:, 30:31], None, ALU.mult),
                             reads=[Rps[acc[0]], Rstat], writes=[Rya])
                        s.op("dve", lambda e, c0=c0, acc=acc: e.scalar_tensor_tensor(ya_f[:], ps[acc[1]][:, c0:c0 + 128], stat[:, 32:33], ya_f[:], ALU.mult, ALU.add),
                             reads=[Rps[acc[1]], Rstat, Rya], writes=[Rya])
                        s.op("act", lambda e: e.activation(junk_b[:, 0:128], ya_f[:], AF.Square, scale=1.0 / math.sqrt(128.0),
                                                           accum_out=stat[:, 33:34]),
                             reads=[Rya], writes=[Rjunk, Rstat])
                        s.op("dve", lambda e: e.tensor_scalar(stat[:, 34:35], stat[:, 33:34], 1e-6, None, ALU.add),
                             reads=[Rstat], writes=[Rstat])
                        s.op("act", lambda e: e.activation(stat[:, 34:35], stat[:, 34:35], AF.Sqrt), reads=[Rstat], writes=[Rstat])
                        s.op("dve", lambda e: e.reciprocal(stat[:, 35:36], stat[:, 34:35]), reads=[Rstat], writes=[Rstat])
                        s.op("dve", lambda e: e.scalar_tensor_tensor(ya_b[:], ya_f[:], stat[:, 35:36], subln08[:], ALU.mult, ALU.mult),
                             reads=[Rya, Rstat, Rsub], writes=[Ryab])
                        bt = bank()
                        while bt in acc:
                            bt = bank()
                        s.op("pe", lambda e, bt=bt: e.transpose(psb[bt][:, 0:128], ya_b[:], ident_b[:]),
                             reads=[Ryab, Rconst], writes=[Rps[bt]])
                        s.op("act", lambda e, bt=bt, h=h, i=i: e.copy(yT[:, h, i * 128:(i + 1) * 128], psb[bt][:, 0:128]),
                             reads=[Rps[bt]], writes=[RyT])

        def out_proj(W, actTv, Ract):
            pieces = [wload(W[:, hh * 512:(hh + 1) * 512], KC, 512) for hh in range(2)]
            for i in range(NT):
                bb = [bank(), bank()]
                for hh in range(2):
                    linear_tm(pieces[hh][0], pieces[hh][1], 512, actTv, Ract, KC, i, bb[hh])
                residual_update(i, bb, 0)

        Xc = carve(5120, NT * 8 * 256, BF16).rearrange("p (i g n) -> p i g n", i=NT, g=8)
        RXc = ares("Xc")
        chCS_b = sb("chCS_b", [128, 256], BF16)
        Rch = Res("chCS")

        def fnet():
            s.dma("pool", chCS_b[:], chCS, writes=[Rch])
            for i in range(NT):
                for gp in range(4):
                    b = bank()
                    for g2_ in range(2):
                        g = gp * 2 + g2_
                        s.op("pe", lambda e, b=b, g=g, g2_=g2_, i=i: e.matmul(
                            ps[b][:, g2_ * 256:(g2_ + 1) * 256], hT[:, g, i * 128:(i + 1) * 128], chCS_b[:], start=True, stop=True),
                            reads=[RhT, Rch], writes=[Rps[b]])
                    s.op("act", lambda e, b=b, gp=gp, i=i: e.copy(
                        Xc[:, i, gp * 2:gp * 2 + 2, :], ps[b][:].rearrange("p (g n) -> p g n", g=2)),
                        reads=[Rps[b]], writes=[RXc])
            for (t0, tn) in [(0, 384), (384, 384), (768, 384), (1152, 128)]:
                wc, rc = wload(dftC[:, t0:t0 + tn], NT, tn)
                wsn, rsn = wload(dftSn[:, t0:t0 + tn], NT, tn)
                for g in range(8):
                    b = bank()
                    for tc in range(NT):
                        s.op("pe", lambda e, b=b, g=g, tc=tc, tn=tn, wc=wc: e.matmul(
                            ps[b][:, 0:tn], Xc[:, tc, g, 0:128], wc[:, tc, 0:tn], start=(tc == 0), stop=False),
                            reads=[RXc, rc], writes=[Rps[b]])
                        s.op("pe", lambda e, b=b, g=g, tc=tc, tn=tn, wsn=wsn: e.matmul(
                            ps[b][:, 0:tn], Xc[:, tc, g, 128:256], wsn[:, tc, 0:tn], start=False, stop=(tc == NT - 1)),
                            reads=[RXc, rsn], writes=[Rps[b]])
                    s.op("act", lambda e, b=b, g=g, t0=t0, tn=tn: e.copy(yT[:, g, t0:t0 + tn], ps[b][:, 0:tn]),
                         reads=[Rps[b]], writes=[RyT])


        V_tm = carve(5120, NT * 512, BF16).rearrange("p (i n) -> p i n", i=NT)
        RVtm = ares("V_tm")
        YSb = carve(7680, NT * 512, BF16).rearrange("p (i n) -> p i n", i=NT)
        RYS = ares("YSb")
        Ystage = carve(10240, 128 * 16, F32)[0:64, :].rearrange("p (n k) -> p n k", k=16)
        RYst = ares("Ystage")
        Sst = [carve(12288, 512, F32), carve(12800, 512, F32)]
        RS = [ares("S0"), ares("S1")]
        tmpA = carve(13312, 512, F32)
        RtA = ares("tmpA")
        tmpB = carve(23824, 512, F32)
        RtB = ares("tmpB")
        w2t_b = carve(24336, 512, BF16)
        a2t_b = carve(24592, 512, BF16)
        g2_b = carve(24848, 512, BF16)
        Rlora = ares("lora")
        PT0 = 25104
        NPT = 12

        def pt(k, n=1):
            return carve(PT0 + 128 * k, 128 * n, F32)
        Rpt = [ares("pt%d" % k) for k in range(NPT)]
        lnx0_b = carve(26640, 512, BF16)
        lnx1_b = carve(26896, 512, BF16)
        Rlnx = ares("lnx")
        tb0 = junk_b[:, 0:128]
        tb1 = junk_b[:, 128:256]
        Rtb0 = Res("tb0")
        Rtb1 = Res("tb1")
        colx = sb("colx", [128, 64])
        Rcolx = Res("colx")
        tiny = sb("tiny", [128, 8])
        Rtiny = Res("tiny")
        BS = sb("BS", [128, NT, 8])
        RBS = Res("BS")
        wbf = [w[:].bitcast(F32) for w in wbuf]
        TW = wbf[0][:, 0:1024].rearrange("p (k n) -> p k n", k=8)
        TKK = wbf[0][:, 1024:2048].rearrange("p (k n) -> p k n", k=8)
        TNK = wbf[1][:, 0:1024].rearrange("p (k n) -> p k n", k=8)
        TKD = wbf[1][:, 1024:2048].rearrange("p (k n) -> p k n", k=8)
        TR2 = wbuf[2][:, 0:2048].rearrange("p (k n h) -> p k n h", k=8, h=2)
        tmpA_b = carve(27100, 512, BF16)
        RtAb = ares("tmpA_b")
        Sb16 = carve(27356, 512, BF16)
        RSb = ares("Sb16")
        CX = dict(cmu=0, nmu0=15, nmu1=30, omk1=45, keepL=49, keepR=54)

        def cx(name, c=0):
            o = CX[name] + c
            return colx[:, o:o + 1]

        def rwkv_setup():
            s.dma("pool", w2t_b, w2t_d, writes=[Rlora])
            s.dma("pool", a2t_b, a2t_d, writes=[Rlora])
            s.dma("pool", g2_b, g2_d, writes=[Rlora])
            s.dma("pool", lnx0_b, lnx_d[0, :].partition_broadcast(128), writes=[Rlnx])
            s.dma("pool", lnx1_b, lnx_d[1, :].partition_broadcast(128), writes=[Rlnx])
            mu0 = cp("mu0", 0, 15)
            mu1 = cp("mu1", 0, 15)
            s.op("dve", lambda e: e.tensor_tensor(colx[:, 0:15], mu0, mu1, ALU.add), reads=[Rcolp], writes=[Rcolx])
            s.op("dve", lambda e: e.tensor_scalar(colx[:, 0:15], colx[:, 0:15], -1.0, 1.0, ALU.mult, ALU.add),
                 reads=[Rcolx], writes=[Rcolx])
            s.op("dve", lambda e: e.tensor_scalar(colx[:, 15:30], mu0, -1.0, None, ALU.mult), reads=[Rcolp], writes=[Rcolx])
            s.op("dve", lambda e: e.tensor_scalar(colx[:, 30:45], mu1, -1.0, None, ALU.mult), reads=[Rcolp], writes=[Rcolx])
            s.op("dve", lambda e: e.tensor_scalar(colx[:, 45:49], cp("kvec1", 0, 4), -1.0, 1.0, ALU.mult, ALU.add),
                 reads=[Rcolp], writes=[Rcolx])
            s.op("dve", lambda e: e.tensor_scalar(colx[:, 49:54], rowp_sb[:, ROFF["bndL"]:ROFF["bndL"] + 5], -1.0, 1.0, ALU.mult, ALU.add),
                 reads=[Rrowp], writes=[Rcolx])
            s.op("dve", lambda e: e.tensor_scalar(colx[:, 54:59], rowp_sb[:, ROFF["bndR"]:ROFF["bndR"] + 5], -1.0, 1.0, ALU.mult, ALU.add),
                 reads=[Rrowp], writes=[Rcolx])
            s.op("dve", lambda e: e.memset(BS[:], 0.0), writes=[RBS])

        def shift(ch, ti, out, Rout):
            t0 = ti * 128
            f = fbT[:, ch, 1 + t0:1 + t0 + 128]
            fp = fbT[:, ch, t0:t0 + 128]
            fn = fbT[:, ch, 2 + t0:2 + t0 + 128]
            rr = [RfbT, Rcolp, Rcolx]
            s.op("dve", lambda e: e.tensor_scalar(out, f, cx("cmu", ch), None, ALU.mult), reads=rr, writes=[Rout])
            s.op("dve", lambda e: e.scalar_tensor_tensor(out, fp, cp("mu0", ch), out, ALU.mult, ALU.add),
                 reads=rr + [Rout], writes=[Rout])
            s.op("dve", lambda e: e.scalar_tensor_tensor(out, fn, cp("mu1", ch), out, ALU.mult, ALU.add),
                 reads=rr + [Rout], writes=[Rout])
            u = ti // 2
            if ti % 2 == 0:
                bl = rowp_sb[:, ROFF["bndL"] + u:ROFF["bndL"] + u + 1]
                s.op("dve", lambda e: e.tensor_tensor(tiny[:, 0:1], fbT[:, ch, t0:t0 + 1], bl, ALU.mult),
                     reads=[RfbT, Rrowp], writes=[Rtiny])
                s.op("dve", lambda e: e.scalar_tensor_tensor(out[:, 0:1], tiny[:, 0:1], cx("nmu0", ch), out[:, 0:1], ALU.mult, ALU.add),
                     reads=[Rtiny, Rcolx, Rout], writes=[Rout])
            else:
                br = rowp_sb[:, ROFF["bndR"] + u:ROFF["bndR"] + u + 1]
                s.op("dve", lambda e: e.tensor_tensor(tiny[:, 1:2], fbT[:, ch, 1 + t0 + 128:2 + t0 + 128], br, ALU.mult),
                     reads=[RfbT, Rrowp], writes=[Rtiny])
                s.op("dve", lambda e: e.scalar_tensor_tensor(out[:, 127:128], tiny[:, 1:2], cx("nmu1", ch), out[:, 127:128], ALU.mult, ALU.add),
                     reads=[Rtiny, Rcolx, Rout], writes=[Rout])

        def rwkv_prepass():
            for ti in range(NT):
                for c in range(4):
                    shift(8 + c, ti, pt(c), Rpt[c])
                    s.op("act", lambda e, c=c: e.copy(xn_b[:, c * 128:(c + 1) * 128], pt(c)), reads=[Rpt[c]], writes=[Rxn])
                b = bank()
                for c in range(4):
                    s.op("pe", lambda e, b=b, c=c: e.transpose(psb[b][:, c * 128:(c + 1) * 128], xn_b[:, c * 128:(c + 1) * 128], ident_b[:]),
                         reads=[Rxn, Rconst], writes=[Rps[b]])
                s.op("act", lambda e, b=b, ti=ti: e.copy(V_tm[:, ti, :], psb[b][:, 0:512]), reads=[Rps[b]], writes=[RVtm])

        def prep(d, ti):
            rev = (d == 1)

            def tab(T3, c):
                v = T3[:, d * 4 + c, :]
                return v[:, ::-1] if rev else v
            tabs_w = [Rw[0], Rw[1], Rw[2]]
            shift(12, ti, pt(0), Rpt[0])
            shift(13, ti, pt(1), Rpt[1])
            s.op("act", lambda e: e.activation(tb0, pt(0), AF.Tanh), reads=[Rpt[0]], writes=[Rtb0])
            s.op("act", lambda e: e.copy(tb1, pt(1)), reads=[Rpt[1]], writes=[Rtb1])
            lo, hi = 64 * d, 64 * d + 64
            for c in range(4):
                b = bank()
                s.op("pe", lambda e, b=b, c=c: e.matmul(ps[b][:, 0:128], w2t_b[lo:hi, c * 128:(c + 1) * 128], tb0[lo:hi, :], start=True, stop=True),
                     reads=[Rlora, Rtb0], writes=[Rps[b]])
                s.op("act", lambda e, b=b, c=c: e.activation(pt(2), ps[b][:, 0:128], AF.Sigmoid, bias=cp("w0_%d" % d, c)),
                     reads=[Rps[b], Rcolp], writes=[Rpt[2]])
                s.op("act", lambda e, c=c: e.activation(tab(TW, c), pt(2), AF.Exp, scale=-EXPM05),
                     reads=[Rpt[2]], writes=[Rw[0]])
                b = bank()
                s.op("pe", lambda e, b=b, c=c: e.matmul(ps[b][:, 0:128], a2t_b[lo:hi, c * 128:(c + 1) * 128], tb1[lo:hi, :], start=True, stop=True),
                     reads=[Rlora, Rtb1], writes=[Rps[b]])
                s.op("act", lambda e, b=b, c=c: e.activation(pt(3), ps[b][:, 0:128], AF.Sigmoid, bias=cp("a0_%d" % d, c)),
                     reads=[Rps[b], Rcolp], writes=[Rpt[3]])
                shift(4 + c, ti, pt(4), Rpt[4])
                s.op("dve", lambda e, c=c: e.tensor_scalar(pt(5), pt(4), cp("kvec0", c), None, ALU.mult),
                     reads=[Rpt[4], Rcolp], writes=[Rpt[5]])
                s.op("act", lambda e: e.activation(pt(6), pt(5), AF.Square), reads=[Rpt[5]], writes=[Rpt[6]])
                b = bank()
                s.op("pe", lambda e, b=b: e.matmul(ps[b][:, 0:128], bones_f[:], pt(6), start=True, stop=True),
                     reads=[Rconst, Rpt[6]], writes=[Rps[b]])
                s.op("dve", lambda e, b=b: e.tensor_scalar(pt(7), ps[b][:, 0:128], 1e-12, None, ALU.add),
                     reads=[Rps[b]], writes=[Rpt[7]])
                s.op("act", lambda e: e.activation(pt(7), pt(7), AF.Sqrt), reads=[Rpt[7]], writes=[Rpt[7]])
                s.op("dve", lambda e: e.reciprocal(pt(7), pt(7)), reads=[Rpt[7]], writes=[Rpt[7]])
                s.op("dve", lambda e: e.tensor_tensor(pt(8), pt(5), pt(7), ALU.mult), reads=[Rpt[5], Rpt[7]], writes=[Rpt[8]])
                s.op("act", lambda e, c=c: e.copy(tab(TKK, c), pt(8)), reads=[Rpt[8]], writes=[Rw[0]])
                s.op("dve", lambda e, c=c: e.scalar_tensor_tensor(tab(TNK, c), pt(8), -1.0, pt(3), ALU.mult, ALU.mult),
                     reads=[Rpt[8], Rpt[3]], writes=[Rw[1]])
                s.op("dve", lambda e, c=c: e.tensor_scalar(pt(9), pt(3), cp("kvec1", c), cx("omk1", c), ALU.mult, ALU.add),
                     reads=[Rpt[3], Rcolp, Rcolx], writes=[Rpt[9]])
                s.op("dve", lambda e: e.tensor_tensor(pt(9), pt(4), pt(9), ALU.mult), reads=[Rpt[4], Rpt[9]], writes=[Rpt[9]])
                s.op("act", lambda e, c=c: e.copy(tab(TKD, c), pt(9)), reads=[Rpt[9]], writes=[Rw[1]])
                shift(c, ti, pt(10), Rpt[10])
                for h2 in range(2):
                    o = TR2[:, d * 4 + c, :, h2]
                    if rev:
                        o = o[:, ::-1]
                    s.op("dve", lambda e, o=o, h2=h2: e.tensor_scalar(o, pt(10), halfsel_f[:, h2:h2 + 1], None, ALU.mult),
                         reads=[Rpt[10], Rconst], writes=[Rw[2]])
                s.op("dve", lambda e: e.tensor_tensor(pt(6), pt(10), pt(9), ALU.mult), reads=[Rpt[10], Rpt[9]], writes=[Rpt[6]])
                s.op("dve", lambda e, c=c: e.tensor_scalar(pt(6), pt(6), cp("kvec2", c), None, ALU.mult),
                     reads=[Rpt[6], Rcolp], writes=[Rpt[6]])
                b = bank()
                s.op("pe", lambda e, b=b: e.matmul(ps[b][:, 0:2], pt(6), halfsel_f[:, 0:2], start=True, stop=True),
                     reads=[Rpt[6], Rconst], writes=[Rps[b]])
                s.op("dve", lambda e, b=b, c=c, ti=ti: e.tensor_tensor(BS[:, ti, c * 2:c * 2 + 2], BS[:, ti, c * 2:c * 2 + 2], ps[b][:, 0:2], ALU.add),
                     reads=[Rps[b], RBS], writes=[RBS])

        yb = xn_b[:, 0:512]

        def finalize(ti, by):
            ys = pt(0, 4)
            Rys = [Rpt[0], Rpt[1], Rpt[2], Rpt[3]]
            t2 = pt(4, 4)
            Rt2 = [Rpt[4], Rpt[5], Rpt[6], Rpt[7]]
            ys3 = ys.rearrange("p (h i) -> p h i", h=8)
            t23 = t2.rearrange("p (h i) -> p h i", h=8)
            s.op("dve", lambda e: e.tensor_tensor(ys, YSb[:, ti, :], ps[by][:], ALU.add), reads=[RYS, Rps[by]], writes=Rys)
            s.op("dve", lambda e: e.reduce_sum(tiny[:, 0:8], ys3, axis=AX.X), reads=Rys, writes=[Rtiny])
            s.op("dve", lambda e: e.tensor_scalar(tiny[:, 0:8], tiny[:, 0:8], -1.0 / 64.0, None, ALU.mult), reads=[Rtiny], writes=[Rtiny])
            s.op("dve", lambda e: e.tensor_tensor(ys3, ys3, tiny[:, 0:8].unsqueeze(2).to_broadcast([128, 8, 64]), ALU.add),
                 reads=Rys + [Rtiny], writes=Rys)
            s.op("dve", lambda e: e.tensor_tensor(t2, ys, ys, ALU.mult), reads=Rys, writes=Rt2)
            s.op("dve", lambda e: e.reduce_sum(stat[:, 40:48], t23, axis=AX.X), reads=Rt2, writes=[Rstat])
            s.op("dve", lambda e: e.tensor_scalar(stat[:, 40:48], stat[:, 40:48], 1.0 / 64.0, 64e-5, ALU.mult, ALU.add),
                 reads=[Rstat], writes=[Rstat])
            s.op("act", lambda e: e.activation(stat[:, 40:48], stat[:, 40:48], AF.Sqrt), reads=[Rstat], writes=[Rstat])
            s.op("dve", lambda e: e.reciprocal(stat[:, 48:56], stat[:, 40:48]), reads=[Rstat], writes=[Rstat])
            s.op("dve", lambda e: e.tensor_tensor(ys3, ys3, stat[:, 48:56].unsqueeze(2).to_broadcast([128, 8, 64]), ALU.mult),
                 reads=Rys + [Rstat], writes=Rys)
            s.op("dve", lambda e: e.tensor_tensor(ys, ys, lnx0_b, ALU.mult), reads=Rys + [Rlnx], writes=Rys)
            s.op("dve", lambda e: e.tensor_tensor(ys, ys, lnx1_b, ALU.add), reads=Rys + [Rlnx], writes=Rys)
            s.op("dve", lambda e: e.tensor_tensor(t23, V_tm[:, ti, :].rearrange("p (h i) -> p h i", h=8),
                                                  BS[:, ti, :].unsqueeze(2).to_broadcast([128, 8, 64]), ALU.mult),
                 reads=[RVtm, RBS], writes=Rt2)
            s.op("dve", lambda e: e.tensor_tensor(ys, ys, t2, ALU.add), reads=Rys + Rt2, writes=Rys)
            shift(14, ti, pt(8), Rpt[8])
            s.op("act", lambda e: e.activation(tb0, pt(8), AF.Sigmoid), reads=[Rpt[8]], writes=[Rtb0])
            bg = bank()
            s.op("pe", lambda e, bg=bg: e.matmul(ps[bg][:], tb0, g2_b, start=True, stop=True),
                 reads=[Rtb0, Rlora], writes=[Rps[bg]])
            s.op("dve", lambda e, bg=bg: e.tensor_tensor(yb, ys, ps[bg][:], ALU.mult), reads=Rys + [Rps[bg]], writes=[Rxn])
            bt = bank()
            for c in range(4):
                s.op("pe", lambda e, bt=bt, c=c: e.transpose(psb[bt][:, c * 128:(c + 1) * 128], yb[:, c * 128:(c + 1) * 128], ident_b[:]),
                     reads=[Rxn, Rconst], writes=[Rps[bt]])
            s.op("act", lambda e, bt=bt, ti=ti: e.copy(yT[:, 4:8, ti * 128:(ti + 1) * 128],
                                                        psb[bt][:, 0:512].rearrange("p (c t) -> p c t", c=4)),
                 reads=[Rps[bt]], writes=[RyT])

        def rwkv_scan(nrounds=NT):
            S8 = [x_.rearrange("p (k i) -> p k i", k=8) for x_ in Sst]
            tA8 = tmpA.rearrange("p (k i) -> p k i", k=8)
            tB8 = tmpB.rearrange("p (k i) -> p k i", k=8)

            def bc(T3, n):
                return T3[:, :, n].unsqueeze(2).to_broadcast([128, 8, 64])
            for r in range(nrounds):
                tf, tbk = r, NT - 1 - r
                prep(0, tf)
                prep(1, tbk)
                if tf % 2 == 0:
                    u = tf // 2
                    if u == 0:
                        s.dma("sp", Sst[0][:, 0:256], initS[0], writes=[RS[0]])
                    else:
                        s.op("dve", lambda e, u=u: e.tensor_scalar(Sst[0][:, 0:256], Sst[0][:, 0:256], cx("keepL", u), None, ALU.mult),
                             reads=[RS[0], Rcolx], writes=[RS[0]])
                if tbk % 2 == 1:
                    u = tbk // 2
                    if u == 4:
                        s.op("dve", lambda e: e.memset(Sst[0][:, 256:512], 0.0), writes=[RS[0]])
                    elif u == 3:
                        s.dma("sp", tmpB[:, 0:256], initS[1], writes=[RtB])
                        s.op("dve", lambda e, u=u: e.scalar_tensor_tensor(Sst[0][:, 256:512], Sst[0][:, 256:512], cx("keepR", u),
                                                                          tmpB[:, 0:256], ALU.mult, ALU.add),
                             reads=[RS[0], Rcolx, RtB], writes=[RS[0]])
                    else:
                        s.op("dve", lambda e, u=u: e.tensor_scalar(Sst[0][:, 256:512], Sst[0][:, 256:512], cx("keepR", u), None, ALU.mult),
                             reads=[RS[0], Rcolx], writes=[RS[0]])
                by_ = None
                pending = []

                def flush():
                    for f_ in pending:
                        f_()
                    del pending[:]
                for n in range(128):
                    ci, ni = n % 2, (n + 1) % 2
                    Sc, Sn = S8[ci], S8[ni]
                    if n % 32 == 0:
                        by_ = bank()
                        reserved.add(by_)
                    tAb8 = tmpA_b.rearrange("p (k i) -> p k i", k=8)
                    s.op("dve", lambda e, n=n, Sc=Sc, tAb8=tAb8: e.tensor_tensor(tAb8, Sc, bc(TKK, n), ALU.mult),
                         reads=[RS[ci], Rw[0]], writes=[RtAb])
                    s.op("dve", lambda e, n=n, Sc=Sc, Sn=Sn: e.tensor_tensor(Sn, Sc, bc(TW, n), ALU.mult),
                         reads=[RS[ci], Rw[0]], writes=[RS[ni]])
                    bv = bank()
                    for d in range(2):
                        tile_d = tf if d == 0 else tbk
                        row = n if d == 0 else 127 - n
                        for h2 in range(2):
                            s.op("pe", lambda e, bv=bv, d=d, h2=h2, tile_d=tile_d, row=row: e.matmul(
                                ps[bv][64 * h2:64 * h2 + 64, d * 256:(d + 1) * 256].rearrange("p (c i) -> p c i", c=4),
                                ident_b[:, row:row + 1].to_broadcast([128, 64]),
                                V_tm[:, tile_d, :].rearrange("p (c h i) -> p c h i", c=4, h=2)[:, :, h2, :],
                                start=True, stop=True),
                                reads=[RVtm, Rconst], writes=[Rps[bv]])
                    bs_ = bank()
                    s.op("pe", lambda e, bs_=bs_: e.matmul(ps[bs_][:], bones_b[:], tmpA_b, start=True, stop=True),
                         reads=[RtAb, Rconst], writes=[Rps[bs_]])
                    flush()
                    s.op("dve", lambda e, n=n, bv=bv: e.tensor_tensor(tB8, ps[bv][:].rearrange("p (k i) -> p k i", k=8), bc(TKD, n), ALU.mult),
                         reads=[Rps[bv], Rw[1]], writes=[RtB])
                    s.op("dve", lambda e, ni=ni: e.tensor_tensor(Sst[ni], Sst[ni], tmpB, ALU.add),
                         reads=[RS[ni], RtB], writes=[RS[ni]])
                    s.op("dve", lambda e, n=n, bs_=bs_: e.tensor_tensor(tA8, ps[bs_][:].rearrange("p (k i) -> p k i", k=8), bc(TNK, n), ALU.mult),
                         reads=[Rps[bs_], Rw[1]], writes=[RtA])
                    s.op("dve", lambda e, ni=ni: e.tensor_tensor(Sst[ni], Sst[ni], tmpA, ALU.add),
                         reads=[RS[ni], RtA], writes=[RS[ni]])
                    s.op("act", lambda e, ni=ni: e.copy(Sb16, Sst[ni]), reads=[RS[ni]], writes=[RSb])
                    nl = n % 32

                    def ymm(by_=by_, nl=nl, n=n):
                        for dc in range(8):
                            s.op("pe", lambda e, dc=dc: e.matmul(
                                ps[by_][0:64, nl * 16 + dc * 2:nl * 16 + dc * 2 + 2], Sb16[:, dc * 64:(dc + 1) * 64], TR2[:, dc, n, :],
                                start=True, stop=True),
                                reads=[RSb, Rw[2]], writes=[Rps[by_]])
                    pending.append(ymm)
                    if nl == 31:
                        flush()
                        n0 = n - 31
                        src = ps[by_][0:64, :].rearrange("p (n k) -> p n k", k=16)
                        s.op("act", lambda e, src=src, n0=n0: e.copy(Ystage[:, n0:n0 + 32, 0:8], src[:, :, 0:8]),
                             reads=[Rps[by_]], writes=[RYst])
                        s.op("act", lambda e, src=src, n0=n0: e.copy(Ystage[:, 96 - n0:128 - n0, 8:16][:, ::-1, :], src[:, :, 8:16]),
                             reads=[Rps[by_]], writes=[RYst])
                        reserved.discard(by_)
                if tf % 2 == 1:
                    s.dma("sp", s_out[tf // 2, 0], Sst[0][:, 0:256], reads=[RS[0]])
                if tbk % 2 == 0:
                    s.dma("sp", s_out[tbk // 2, 1], Sst[0][:, 256:512], reads=[RS[0]])
                for d in range(2):
                    tile_d = tf if d == 0 else tbk
                    byy = bank()
                    for k8 in range(8):
                        s.op("pe", lambda e, byy=byy, k8=k8, d=d: e.matmul(
                            ps[byy][:, k8 * 64:(k8 + 1) * 64], Ystage[:, :, d * 8 + k8], ident_f[0:64, 0:64], start=True, stop=True),
                            reads=[RYst, Rconst], writes=[Rps[byy]])
                    first = (r < 5)
                    if first:
                        s.op("act", lambda e, byy=byy, tile_d=tile_d: e.copy(YSb[:, tile_d, :], ps[byy][:]),
                             reads=[Rps[byy]], writes=[RYS])
                    else:
                        finalize(tile_d, byy)


        def wbw(i, off_w, nelem):
            return wbuf[i][:, 2 * off_w:2 * off_w + nelem]
        Ast = carve(13200, 512, F32)
        RAst = ares("Ast")
        Abf = carve(13712, 512, BF16)
        RAbf = ares("Abf")
        WS = []
        d_ = {}
        for k_, nm in enumerate(["Lt0", "Lt1", "L0", "L1", "Z0"]):
            d_[nm] = carve(10240 + 512 * k_, 512, F32)
        d_["MT"] = carve(12800, 256, BF16)
        d_["RT"] = carve(12928, 512, BF16)
        d_["Z1"] = wbuf[0][:, 0:1024].bitcast(F32)
        for k_, nm in enumerate(["Lakt", "Mrbt", "Mrkt", "BKtm", "Zfb"]):
            d_[nm] = wbw(0, 512 + 256 * k_, 512)
        for nm in ["Lt0", "Lt1", "L0", "L1", "Z0", "Z1", "Lakt", "Mrbt", "Mrkt", "BKtm", "Zfb", "MT", "RT"]:
            d_["R" + nm] = ares("ws_" + nm)
        WS = [d_, d_]
        masks = wbw(1, 896, 4 * 4 * 128).rearrange("p (m q t) -> p m q t", m=4, q=4)
        Rmask = ares("masks")
        TBL = []
        for par in range(2):
            d_ = {}
            for k_, nm in enumerate(["RH", "AH", "BH", "KH"]):
                d_[nm] = wbw(2, par * 1024 + 256 * k_, 512).rearrange("p (c t) -> p c t", c=4)
                d_["R" + nm] = ares("tb%d_%s" % (par, nm))
            TBL.append(d_)
        pcs = sb("pcs", [128, 2, 4])
        Rpcs = [Res("pcs0"), Res("pcs1")]

        def prep_chunk(d, ti, par):
            TB = TBL[par]
            shift(12, ti, pt(0), Rpt[0])
            shift(13, ti, pt(1), Rpt[1])
            s.op("act", lambda e: e.activation(tb0, pt(0), AF.Tanh), reads=[Rpt[0]], writes=[Rtb0])
            s.op("act", lambda e: e.copy(tb1, pt(1)), reads=[Rpt[1]], writes=[Rtb1])
            lo, hi = 64 * d, 64 * d + 64
            for c in range(4):
                b = bank()
                s.op("pe", lambda e, b=b, c=c: e.matmul(ps[b][:, 0:128], w2t_b[lo:hi, c * 128:(c + 1) * 128], tb0[lo:hi, :], start=True, stop=True),
                     reads=[Rlora, Rtb0], writes=[Rps[b]])
                s.op("act", lambda e, b=b, c=c: e.activation(pt(2), ps[b][:, 0:128], AF.Sigmoid, bias=cp("w0_%d" % d, c)),
                     reads=[Rps[b], Rcolp], writes=[Rpt[2]])
                s.op("dve", lambda e: e.tensor_scalar(pt(2), pt(2), -EXPM05, None, ALU.mult), reads=[Rpt[2]], writes=[Rpt[2]])
                if d == 0:
                    s.op("dve", lambda e: e.tensor_tensor_scan(pt(0), pt(11), pt(2), 0.0, ALU.mult, ALU.add),
                         reads=[Rpt[2], Rpt[11]], writes=[Rpt[0]])
                else:
                    s.op("dve", lambda e: e.tensor_tensor_scan(pt(0)[:, ::-1], pt(11), pt(2)[:, ::-1], 0.0, ALU.mult, ALU.add),
                         reads=[Rpt[2], Rpt[11]], writes=[Rpt[0]])
                s.op("act", lambda e: e.activation(pt(1), pt(0), AF.Exp), reads=[Rpt[0]], writes=[Rpt[1]])
                s.op("dve", lambda e: e.tensor_tensor(pt(2), pt(0), pt(2), ALU.subtract), reads=[Rpt[0], Rpt[2]], writes=[Rpt[2]])
                s.op("act", lambda e: e.activation(pt(2), pt(2), AF.Exp), reads=[Rpt[2]], writes=[Rpt[2]])
                s.op("act", lambda e: e.activation(pt(0), pt(0), AF.Exp, scale=-1.0), reads=[Rpt[0]], writes=[Rpt[0]])
                pcol = 127 if d == 0 else 0
                s.op("dve", lambda e, c=c, pcol=pcol: e.tensor_copy(pcs[:, par, c:c + 1], pt(1)[:, pcol:pcol + 1]),
                     reads=[Rpt[1]], writes=[Rpcs[par]])
                b = bank()
                s.op("pe", lambda e, b=b, c=c: e.matmul(ps[b][:, 0:128], a2t_b[lo:hi, c * 128:(c + 1) * 128], tb1[lo:hi, :], start=True, stop=True),
                     reads=[Rlora, Rtb1], writes=[Rps[b]])
                s.op("act", lambda e, b=b, c=c: e.activation(pt(3), ps[b][:, 0:128], AF.Sigmoid, bias=cp("a0_%d" % d, c)),
                     reads=[Rps[b], Rcolp], writes=[Rpt[3]])
                shift(4 + c, ti, pt(4), Rpt[4])
                s.op("dve", lambda e, c=c: e.tensor_scalar(pt(5), pt(4), cp("kvec0", c), None, ALU.mult),
                     reads=[Rpt[4], Rcolp], writes=[Rpt[5]])
                s.op("act", lambda e: e.activation(pt(6), pt(5), AF.Square), reads=[Rpt[5]], writes=[Rpt[6]])
                b = bank()
                s.op("pe", lambda e, b=b: e.matmul(ps[b][:, 0:128], bones_f[:], pt(6), start=True, stop=True),
                     reads=[Rconst, Rpt[6]], writes=[Rps[b]])
                s.op("dve", lambda e, b=b: e.tensor_scalar(pt(7), ps[b][:, 0:128], 1e-12, None, ALU.add),
                     reads=[Rps[b]], writes=[Rpt[7]])
                s.op("act", lambda e: e.activation(pt(7), pt(7), AF.Sqrt), reads=[Rpt[7]], writes=[Rpt[7]])
                s.op("dve", lambda e: e.reciprocal(pt(7), pt(7)), reads=[Rpt[7]], writes=[Rpt[7]])
                s.op("dve", lambda e: e.tensor_tensor(pt(8), pt(5), pt(7), ALU.mult), reads=[Rpt[5], Rpt[7]], writes=[Rpt[8]])
                s.op("dve", lambda e, c=c: e.scalar_tensor_tensor(TB["AH"][:, c, :], pt(8), -1.0, pt(2), ALU.mult, ALU.mult),
                     reads=[Rpt[8], Rpt[2]], writes=[TB["RAH"]])
                s.op("dve", lambda e: e.tensor_tensor(pt(6), pt(8), pt(3), ALU.mult), reads=[Rpt[8], Rpt[3]], writes=[Rpt[6]])
                s.op("dve", lambda e, c=c: e.tensor_tensor(TB["BH"][:, c, :], pt(6), pt(0), ALU.mult),
                     reads=[Rpt[6], Rpt[0]], writes=[TB["RBH"]])
                s.op("dve", lambda e, c=c: e.tensor_scalar(pt(9), pt(3), cp("kvec1", c), cx("omk1", c), ALU.mult, ALU.add),
                     reads=[Rpt[3], Rcolp, Rcolx], writes=[Rpt[9]])
                s.op("dve", lambda e: e.tensor_tensor(pt(9), pt(4), pt(9), ALU.mult), reads=[Rpt[4], Rpt[9]], writes=[Rpt[9]])
                s.op("dve", lambda e, c=c: e.tensor_tensor(TB["KH"][:, c, :], pt(9), pt(0), ALU.mult),
                     reads=[Rpt[9], Rpt[0]], writes=[TB["RKH"]])
                shift(c, ti, pt(10), Rpt[10])
                s.op("dve", lambda e, c=c: e.tensor_tensor(TB["RH"][:, c, :], pt(10), pt(1), ALU.mult),
                     reads=[Rpt[10], Rpt[1]], writes=[TB["RRH"]])
                s.op("dve", lambda e: e.tensor_tensor(pt(6), pt(10), pt(9), ALU.mult), reads=[Rpt[10], Rpt[9]], writes=[Rpt[6]])
                s.op("dve", lambda e, c=c: e.tensor_scalar(pt(6), pt(6), cp("kvec2", c), None, ALU.mult),
                     reads=[Rpt[6], Rcolp], writes=[Rpt[6]])
                b = bank()
                s.op("pe", lambda e, b=b: e.matmul(ps[b][:, 0:2], pt(6), halfsel_f[:, 0:2], start=True, stop=True),
                     reads=[Rpt[6], Rconst], writes=[Rps[b]])
                s.op("dve", lambda e, b=b, c=c, ti=ti: e.tensor_tensor(BS[:, ti, c * 2:c * 2 + 2], BS[:, ti, c * 2:c * 2 + 2], ps[b][:, 0:2], ALU.add),
                     reads=[Rps[b], RBS], writes=[RBS])

        def rwkv_chunked(ntiles=NT):
            cut = int(os.environ.get("CUT", 99))
            for m in range(4):
                for q in range(4):
                    s.dma("pool", masks[:, m, q, :], cmat[4 + m], writes=[Rmask])
            s.op("dve", lambda e: e.memset(pt(11), 1.0), writes=[Rpt[11]])
            MSK = {0: (0, 1, 2), 1: (1, 0, 3)}
            gctr = [0]
            pctr2 = [0]
            for d in range(2):
                order = list(range(NT)) if d == 0 else list(range(NT - 1, -1, -1))
                order = order[:ntiles]
                Ad = Ast[:, d * 256:(d + 1) * 256]
                Ad3 = Ad.rearrange("p (c i) -> p c i", c=4)
                Abd = Abf[:, d * 256:(d + 1) * 256]
                Abd3 = Abd.rearrange("p (c i) -> p c i", c=4)
                mTs, mS, mTi = MSK[d]
                def tile_body(ti, d=d, Ad=Ad, Ad3=Ad3, Abd=Abd, Abd3=Abd3, mTs=mTs, mS=mS, mTi=mTi):
                    par = pctr2[0] % 2
                    pctr2[0] += 1
                    TB = TBL[par]
                    prep_chunk(d, ti, par)
                    if os.environ.get("DBG") == "1" and d == 0 and ti == 0:
                        s.dma("sp", s_out[1, 0][:, 0:128], pt(0), reads=[Rpt[0]])
                        s.dma("sp", s_out[1, 0][:, 128:256], pt(1), reads=[Rpt[1]])
                        s.dma("sp", s_out[1, 1][:, 0:128], pt(2), reads=[Rpt[2]])
                        s.dma("sp", s_out[1, 1][:, 128:256], pt(11), reads=[Rpt[11]])
                        s.dma("sp", s_out[2, 0][:, 0:8], pcs[:, :, :].rearrange("p a b -> p (a b)"), reads=Rpcs)
                    u = ti // 2
                    isstart = (ti % 2 == 0) if d == 0 else (ti % 2 == 1)
                    if isstart:
                        if d == 0:
                            if u == 0:
                                s.dma("sp", Ad, initS[0], writes=[RAst])
                            else:
                                s.op("dve", lambda e, u=u: e.tensor_scalar(Ad, Ad, cx("keepL", u), None, ALU.mult),
                                     reads=[RAst, Rcolx], writes=[RAst])
                        else:
                            if u == 4:
                                s.op("dve", lambda e: e.memset(Ad, 0.0), writes=[RAst])
                            elif u == 3:
                                s.dma("sp", tmpB[:, 0:256], initS[1], writes=[RtB])
                                s.op("dve", lambda e, u=u: e.scalar_tensor_tensor(Ad, Ad, cx("keepR", u), tmpB[:, 0:256], ALU.mult, ALU.add),
                                     reads=[RAst, Rcolx, RtB], writes=[RAst])
                            else:
                                s.op("dve", lambda e, u=u: e.tensor_scalar(Ad, Ad, cx("keepR", u), None, ALU.mult),
                                     reads=[RAst, Rcolx], writes=[RAst])
                        s.op("act", lambda e: e.copy(Abd, Ad), reads=[RAst], writes=[RAbf])
                    if cut < 2:
                        return
                    fy = bank()
                    reserved.add(fy)
                    first_fy = [True]
                    def group_body(g):
                        W = WS[gctr[0] % 2]
                        gctr[0] += 1
                        heads = [(c_, g) for c_ in range(4)]

                        def fm(T3, c, h2):
                            return T3[64 * h2:64 * h2 + 64, c, :]

                        def q4(tl):
                            return tl.rearrange("p (q t) -> p q t", q=4)
                        specs = [("Lt0", "BH", "AH", mTs), ("L0", "AH", "BH", mS), ("Lakt", "KH", "AH", mTs),
                                 ("Mrbt", "BH", "RH", mTi), ("Mrkt", "KH", "RH", mTi)]
                        lim = int(os.environ.get("LIM", 99))
                        for (dst, la, rb, mk) in specs[:lim]:
                            b = bank()
                            for q, (c, h2) in enumerate(heads[:int(os.environ.get("LIMH", 4))]):
                                s.op("pe", lambda e, b=b, q=q, c=c, h2=h2, la=la, rb=rb: e.matmul(
                                    ps[b][:, q * 128:(q + 1) * 128], fm(TB[la], c, h2), fm(TB[rb], c, h2), start=True, stop=True),
                                    reads=[TB["R" + la], TB["R" + rb]], writes=[Rps[b]])
                            if os.environ.get("NOEV") == "1":
                                continue
                            if os.environ.get("NOEV") == "2":
                                s.op("dve", lambda e, b=b, dst=dst, mk=mk: e.tensor_copy(W[dst], ps[b][:]),
                                     reads=[Rps[b], Rmask], writes=[W["R" + dst]])
                                continue
                            s.op("dve", lambda e, b=b, dst=dst, mk=mk: e.tensor_tensor(
                                q4(W[dst]), ps[b][:].rearrange("p (q t) -> p q t", q=4), masks[:, mk, :, :], ALU.mult),
                                reads=[Rps[b], Rmask], writes=[W["R" + dst]])
                        if os.environ.get("DBG") == "3" and d == 0 and ti == 0 and g == 0:
                            s.dma("pool", s_out[1, 0][:, 0:128], TB["AH"][:, 0, :], reads=[TB["RAH"]])
                            s.dma("pool", s_out[1, 0][:, 128:256], TB["BH"][:, 0, :], reads=[TB["RBH"]])
                            s.dma("pool", s_out[1, 1][:, 0:128], TB["KH"][:, 0, :], reads=[TB["RKH"]])
                            s.dma("pool", s_out[1, 1][:, 128:256], TB["RH"][:, 0, :], reads=[TB["RRH"]])
                            s.dma("pool", s_out[2, 0][:, 0:128], q4(W["Lt0"])[:, 0, :], reads=[W["RLt0"]])
                            s.dma("pool", s_out[2, 0][:, 128:256], q4(W["L0"])[:, 0, :], reads=[W["RL0"]])
                            s.dma("pool", s_out[3, 0][:, 0:64], V_tm[:, 0, 0:64], reads=[RVtm])
                        if cut < 3:
                            return
                        b = bank()
                        for q, (c, h2) in enumerate(heads):
                            idn = ident_b[64 * h2:64 * h2 + 64, 64 * h2:64 * h2 + 64]
                            for k_, nm in enumerate(["AH", "BH", "KH"]):
                                s.op("pe", lambda e, b=b, q=q, c=c, h2=h2, nm=nm, k_=k_, idn=idn: e.transpose(
                                    psb[b][:, k_ * 256 + q * 64:k_ * 256 + q * 64 + 64], fm(TB[nm], c, h2), idn),
                                    reads=[TB["R" + nm], Rconst], writes=[Rps[b]])
                        Z0q = q4(W["Z0"])
                        BKq = q4(W["BKtm"])
                        s.op("act", lambda e, b=b, Z0q=Z0q: e.copy(Z0q[:, :, 0:64], psb[b][:, 0:256].rearrange("p (q j) -> p q j", q=4)),
                             reads=[Rps[b]], writes=[W["RZ0"]])
                        s.op("act", lambda e, b=b, BKq=BKq: e.copy(BKq[:, :, 0:64], psb[b][:, 256:512].rearrange("p (q j) -> p q j", q=4)),
                             reads=[Rps[b]], writes=[W["RBKtm"]])
                        s.op("act", lambda e, b=b, BKq=BKq: e.copy(BKq[:, :, 64:128], psb[b][:, 512:768].rearrange("p (q j) -> p q j", q=4)),
                             reads=[Rps[b]], writes=[W["RBKtm"]])

                        def Vq(c, h2):
                            return V_tm[:, ti, (c * 2 + h2) * 64:(c * 2 + h2) * 64 + 64]
                        if cut < 4:
                            return
                        b = bank()
                        Lakq = q4(W["Lakt"])
                        for q, (c, h2) in enumerate(heads):
                            s.op("pe", lambda e, b=b, q=q, c=c, h2=h2: e.matmul(
                                ps[b][:, q * 64:(q + 1) * 64], Lakq[:, q, :], Vq(c, h2), start=True, stop=True),
                                reads=[W["RLakt"], RVtm], writes=[Rps[b]])
                        s.op("act", lambda e, b=b, Z0q=Z0q: e.copy(Z0q[:, :, 64:128], ps[b][:, 0:256].rearrange("p (q j) -> p q j", q=4)),
                             reads=[Rps[b]], writes=[W["RZ0"]])
                        if os.environ.get("DBG") == "3" and d == 0 and ti == 0 and g == 0:
                            s.dma("pool", s_out[2, 1][:, 0:128], Z0q[:, 0, :], reads=[W["RZ0"]])
                        if cut < 5:
                            return
                        zc, lc = 0, 0
                        for k in range(7):
                            Zc, Zn = W["Z%d" % zc], W["Z%d" % (1 - zc)]
                            RZc, RZn = W["RZ%d" % zc], W["RZ%d" % (1 - zc)]
                            Ltc, Lc = W["Lt%d" % lc], W["L%d" % lc]
                            RLtc, RLc = W["RLt%d" % lc], W["RL%d" % lc]
                            b = bank()
                            for q in range(4):
                                s.op("pe", lambda e, b=b, q=q, Zc=Zc: e.matmul(
                                    ps[b][:, q * 128:(q + 1) * 128], ident_f[:], q4(Zc)[:, q, :], start=True, stop=False),
                                    reads=[RZc, Rconst], writes=[Rps[b]])
                                s.op("pe", lambda e, b=b, q=q, Zc=Zc, Ltc=Ltc: e.matmul(
                                    ps[b][:, q * 128:(q + 1) * 128], q4(Ltc)[:, q, :], q4(Zc)[:, q, :], start=False, stop=True,
                                    skip_group_check=True),
                                    reads=[RZc, RLtc], writes=[Rps[b]])
                            s.op("act", lambda e, b=b, Zn=Zn: e.copy(Zn, ps[b][:]), reads=[Rps[b]], writes=[RZn])
                            if k < 6:
                                Ltn, Ln = W["Lt%d" % (1 - lc)], W["L%d" % (1 - lc)]
                                RLtn, RLn = W["RLt%d" % (1 - lc)], W["RL%d" % (1 - lc)]
                                b2 = bank()
                                for q in range(4):
                                    s.op("pe", lambda e, b2=b2, q=q, Lc=Lc, Ltc=Ltc: e.matmul(
                                        ps[b2][:, q * 128:(q + 1) * 128], q4(Lc)[:, q, :], q4(Ltc)[:, q, :], start=True, stop=True),
                                        reads=[RLc, RLtc], writes=[Rps[b2]])
                                s.op("dve", lambda e, b2=b2, Ltn=Ltn: e.tensor_copy(Ltn, ps[b2][:]), reads=[Rps[b2]], writes=[RLtn])
                                if k < 5:
                                    b3 = bank()
                                    for q in range(4):
                                        s.op("pe", lambda e, b3=b3, q=q, Lc=Lc, Ltc=Ltc: e.matmul(
                                            ps[b3][:, q * 128:(q + 1) * 128], q4(Ltc)[:, q, :], q4(Lc)[:, q, :], start=True, stop=True),
                                            reads=[RLc, RLtc], writes=[Rps[b3]])
                                    s.op("dve", lambda e, b3=b3, Ln=Ln: e.tensor_copy(Ln, ps[b3][:]), reads=[Rps[b3]], writes=[RLn])
                                lc = 1 - lc
                            zc = 1 - zc
                        s.op("act", lambda e, zc=zc: e.copy(W["Zfb"], W["Z%d" % zc]), reads=[W["RZ%d" % zc]], writes=[W["RZfb"]])
                        Zf = q4(W["Zfb"])
                        RZf = W["RZfb"]
                        Mrbq = q4(W["Mrbt"])
                        Mrkq = q4(W["Mrkt"])
                        if cut < 6:
                            return
                        bM = bank()
                        bN = bank()
                        bR = bank()
                        for q, (c, h2) in enumerate(heads):
                            cl_ = c
                            prt = slice(64 * h2, 64 * h2 + 64)
                            s.op("pe", lambda e, q=q, cl_=cl_, prt=prt: e.matmul(
                                ps[bM][prt, cl_ * 64:(cl_ + 1) * 64], Zf[:, q, 0:64], BKq[:, q, 0:64],
                                start=(q == 0), stop=False, skip_group_check=True),
                                reads=[RZf, W["RBKtm"]], writes=[Rps[bM]])
                            s.op("pe", lambda e, q=q, cl_=cl_, prt=prt: e.matmul(
                                ps[bM][prt, cl_ * 64:(cl_ + 1) * 64], ident_b[0:64, 0:64], ident_b[0:64, 0:64],
                                start=False, stop=True, skip_group_check=True),
                                reads=[Rconst], writes=[Rps[bM]])
                            s.op("pe", lambda e, q=q, cl_=cl_, prt=prt: e.matmul(
                                ps[bN][prt, cl_ * 64:(cl_ + 1) * 64], BKq[:, q, 0:64], Zf[:, q, 64:128],
                                start=(q == 0), stop=False, skip_group_check=True),
                                reads=[RZf, W["RBKtm"]], writes=[Rps[bN]])
                            s.op("pe", lambda e, q=q, cl_=cl_, prt=prt, c=c, h2=h2: e.matmul(
                                ps[bN][prt, cl_ * 64:(cl_ + 1) * 64], BKq[:, q, 64:128], Vq(c, h2),
                                start=False, stop=False, skip_group_check=True),
                                reads=[W["RBKtm"], RVtm], writes=[Rps[bN]])
                            s.op("pe", lambda e, q=q, cl_=cl_, prt=prt: e.matmul(
                                ps[bR][prt, cl_ * 128:(cl_ + 1) * 128], Zf[:, q, 0:64], Mrbq[:, q, :],
                                start=(q == 0), stop=False, skip_group_check=True),
                                reads=[RZf, W["RMrbt"]], writes=[Rps[bR]])
                            s.op("pe", lambda e, q=q, cl_=cl_, prt=prt, c=c, h2=h2: e.matmul(
                                ps[bR][prt, cl_ * 128:(cl_ + 1) * 128], ident_b[prt, prt], fm(TB["RH"], c, h2),
                                start=False, stop=True, skip_group_check=True),
                                reads=[Rconst, TB["RRH"]], writes=[Rps[bR]])
                            st_ = first_fy[0]
                            first_fy[0] = False
                            hh = c * 2 + h2
                            s.op("pe", lambda e, q=q, hh=hh, st_=st_: e.matmul(
                                ps[fy][:, hh * 64:(hh + 1) * 64], Mrbq[:, q, :], Zf[:, q, 64:128],
                                start=st_, stop=False, skip_group_check=True),
                                reads=[RZf, W["RMrbt"]], writes=[Rps[fy]])
                            s.op("pe", lambda e, q=q, hh=hh, c=c, h2=h2: e.matmul(
                                ps[fy][:, hh * 64:(hh + 1) * 64], Mrkq[:, q, :], Vq(c, h2),
                                start=False, stop=False, skip_group_check=True),
                                reads=[W["RMrkt"], RVtm], writes=[Rps[fy]])
                        MTv = W["MT"].rearrange("p (c j) -> p c j", c=4)
                        RTv = W["RT"].rearrange("p (c t) -> p c t", c=4)
                        pg = slice(64 * g, 64 * g + 64)
                        s.op("act", lambda e, MTv=MTv, pg=pg: e.copy(MTv[pg], ps[bM][pg, 0:256].rearrange("p (c j) -> p c j", c=4)),
                             reads=[Rps[bM]], writes=[W["RMT"]])
                        s.op("act", lambda e, RTv=RTv, pg=pg: e.copy(RTv[pg], ps[bR][pg, 0:512].rearrange("p (c t) -> p c t", c=4)),
                             reads=[Rps[bR]], writes=[W["RRT"]])
                        for q, (c, h2) in enumerate(heads):
                            cl_ = c
                            prt = slice(64 * h2, 64 * h2 + 64)
                            hh = c * 2 + h2
                            s.op("pe", lambda e, cl_=cl_, prt=prt, hh=hh, c=c, RTv=RTv: e.matmul(
                                ps[fy][:, hh * 64:(hh + 1) * 64], RTv[prt, cl_, :], Abd3[prt, c, :],
                                start=False, stop=True, skip_group_check=True),
                                reads=[W["RRT"], RAbf], writes=[Rps[fy]])
                            s.op("pe", lambda e, cl_=cl_, prt=prt, c=c, MTv=MTv: e.matmul(
                                ps[bN][prt, cl_ * 64:(cl_ + 1) * 64], MTv[prt, cl_, :], Abd3[prt, c, :],
                                start=False, stop=True, skip_group_check=True),
                                reads=[W["RMT"], RAbf], writes=[Rps[bN]])
                        if os.environ.get("DBG") == "3" and d == 0 and ti == 0 and g == 0:
                            s.op("act", lambda e: e.copy(stage_f[:, 0, 0:256], ps[bN][:, 0:256]), reads=[Rps[bN]], writes=[Rstage[0]])
                            s.dma("sp", s_out[3, 1], stage_f[:, 0, 0:256], reads=[Rstage[0]])
                            s.dma("pool", s_out[2, 1][:, 128:256], Zf[:, 0, :], reads=[RZf])
                            s.dma("pool", s_out[4, 1], W["MT"], reads=[W["RMT"]])
                        s.op("dve", lambda e, pg=pg: e.tensor_tensor(
                            Ad3[pg, :, :], ps[bN][pg, 0:256].rearrange("p (c i) -> p c i", c=4),
                            pcs[pg, par, :].unsqueeze(2).to_broadcast([64, 4, 64]), ALU.mult),
                            reads=[Rps[bN], Rpcs[par], RAbf], writes=[RAst])
                    for g in range(2):
                        group_body(g)
                    if cut < 6:
                        reserved.discard(fy)
                        return
                    if os.environ.get("DBG") == "3" and d == 0 and ti == 0:
                        s.dma("sp", s_out[4, 0], Ad, reads=[RAst])
                    s.op("act", lambda e: e.copy(Abd, Ad), reads=[RAst], writes=[RAbf])
                    isend = (ti % 2 == 1) if d == 0 else (ti % 2 == 0)
                    if isend:
                        s.dma("sp", s_out[u, d], Ad, reads=[RAst])
                    reserved.discard(fy)
                    if d == 0:
                        s.op("act", lambda e, ti=ti: e.copy(YSb[:, ti, :], ps[fy][:]), reads=[Rps[fy]], writes=[RYS])
                    else:
                        finalize(ti, fy)
                for ti in order:
                    tile_body(ti)

        if stage >= 1:
            if "a" in sub:
                make_gg(0)
            if "b" in sub:
                make_hT(0, 0)
            if "c" in sub:
                even_inproj()
        if stage >= 3:
            attention()
            barrier()
            if stage >= 5:
                rwkv_setup()
                rwkv_prepass()
                rwkv_chunked(NT if stage >= 6 else 2)
            else:
                s.op("pool", lambda e: e.memset(yT[:, 4:8, :], 0.0), writes=[RyT])
            barrier()
            out_proj(w_out_even, yT, RyT)
        if stage >= 2:
            barrier()
            ffn(0)
            barrier()
        if stage >= 4:
            make_gg(1)
            make_hT(1, 0)
            fnet()
            out_proj(w_out_odd, yT, RyT)
            barrier()
            ffn(1)
        for i in range(NT):
            s.dma("sp", y_out[i * 128:(i + 1) * 128, :], x_sb[:, i, :], reads=[Rx[i]])
        s.emit()
    return nc


def _core_units(c):
    if c < 6:
        return [("p", 5 * c + u) for u in range(5)]
    b = c - 6
    return [("s", b, u) for u in range(4)] + [("p", 30 + b)]


def _host_prep(inp):
    f32 = np.float32
    COFF, NCOL = colp_layout()
    ROFF, NROW = rowp_layout()
    g = {k: np.asarray(v) for k, v in inp.items()}
    colp = np.zeros((128, NCOL), f32)

    def put(name, vec):
        c = cols_of(vec)
        colp[:, COFF[name]:COFF[name] + c.shape[1]] = c
    for l in range(2):
        put("bada%d" % l, g["b_ada"][l])
        for k in range(4):
            put("gain%d_%d" % (l, k), g["norm_gains"][l, k])
        for k in range(3):
            put("conv%d_%d" % (l, k), g["ffn_conv"][l, k])
        put("convb%d" % l, g["ffn_conv_b"][l])
    put("mu0", g["rwkv_shift_mu"][0, 0])
    put("mu1", g["rwkv_shift_mu"][0, 1])
    for d in range(2):
        put("w0_%d" % d, g["rwkv_w0"][0, d])
        put("a0_%d" % d, g["rwkv_a0"][0, d])
    for k in range(3):
        put("kvec%d" % k, g["rwkv_kvec"][0, k])

    cmat = np.zeros((8, 128, 128), f32)
    cmat[0] = np.eye(128, dtype=f32)
    for i in range(64):
        cmat[1][2 * i, 2 * i + 1] = 1.0
        cmat[1][2 * i + 1, 2 * i] = -1.0
    cmat[2][:64, :64] = 1.0
    cmat[2][64:, 64:] = 1.0
    cmat[3][:64, 0] = 1.0
    cmat[3][64:, 1] = 1.0
    rr_, cc_ = np.meshgrid(np.arange(128), np.arange(128), indexing="ij")
    cmat[4] = (rr_ < cc_)
    cmat[5] = (rr_ > cc_)
    cmat[6] = (rr_ <= cc_)
    cmat[7] = (rr_ >= cc_)
    cc = np.arange(128)
    chang = 2.0 * np.pi * ((cc[:, None] * cc[None, :]) % 128) / 128.0
    chCS = np.concatenate([np.cos(chang), np.sin(chang)], axis=1).astype(f32)

    inv = (10000.0 ** (-np.arange(16, dtype=np.float32) / 16)).astype(f32)

    shared = dict(
        colp=colp, w_ada=g["w_ada"], w_in_even=g["w_in_even"][0], w_out_even=g["w_out_even"][0],
        w_out_odd=g["w_out_odd"][0], w_ffn_in=g["w_ffn_in"], w_ffn_out=g["w_ffn_out"],
        w2t=np.ascontiguousarray(g["rwkv_w2"][0].reshape(128, 512)),
        a2t=np.ascontiguousarray(g["rwkv_a2"][0].reshape(128, 512)),
        g2=np.ascontiguousarray(g["rwkv_g2"][0]), chCS=chCS, cmat=cmat,
        lnx=np.ascontiguousarray(g["rwkv_lnx"][0]))
    maps = []
    for c in range(NCORES):
        units = _core_units(c)
        xs = []
        for un in units:
            if un[0] == "p":
                xs.append(g["x_prompt"][un[1]])
            else:
                xs.append(g["x_sample"][un[1], un[2] * 256:(un[2] + 1) * 256])
        x_in = np.ascontiguousarray(np.concatenate(xs, axis=0), dtype=f32)
        is_s = c >= 6
        condA = g["c"][c - 6] if is_s else g["c_ctx"]
        condB = g["c_ctx"]
        cond = np.stack([condA, condB], axis=0).astype(f32)
        condT = np.ascontiguousarray(cond.reshape(2, 8, 128).transpose(2, 1, 0).reshape(128, 16))
        rowp = np.zeros((1, NROW), f32)
        rowp[0, ROFF["lam"]:ROFF["lam"] + 256] = g["diff_lambda"][0].reshape(-1)
        rowp[0, ROFF["subln"]:ROFF["subln"] + 128] = g["diff_subln"][0]
        ab = np.full((6, 5), -30000.0, f32)
        if is_s:
            ab[0:5, 0:4] = 0.0
            ab[5, 4] = 0.0
            bndL = [1, 0, 0, 0, 1]
            bndR = [0, 0, 0, 1, 1]
        else:
            for u in range(5):
                ab[u + 1, u] = 0.0
            bndL = [1] * 5
            bndR = [1] * 5
        rowp[0, ROFF["abias"]:ROFF["abias"] + 30] = ab.reshape(-1)
        rowp[0, ROFF["bndL"]:ROFF["bndL"] + 5] = bndL
        rowp[0, ROFF["bndR"]:ROFF["bndR"] + 5] = bndR
        ropeC = np.ones((128, T), f32)
        ropeS = np.zeros((128, T), f32)
        if is_s:
            t = np.arange(1024)
            row = (t // 64).astype(f32)
            col = (t % 64).astype(f32)
            ang = np.concatenate([row[:, None] * inv[None, :], col[:, None] * inv[None, :]], axis=1).astype(f32)
            pidx = (np.arange(128) % 64) // 2
            ropeC[:, :1024] = np.cos(ang)[:, pidx].T
            ropeS[:, :1024] = np.sin(ang)[:, pidx].T
        cacheKT = np.zeros((512, 256), f32)
        cacheV = np.zeros((256, 512), f32)
        initS = np.zeros((2, 128, 256), f32)
        if is_s:
            b = c - 6
            cacheKT[:] = g["cache_k"][b, 0].reshape(256, 512).T
            cacheV[:] = g["cache_v"][b, 0].reshape(256, 512)
            st = g["state_wkv"][b, 0]
            initS[:] = st.reshape(2, 4, 2, 64, 64).transpose(0, 2, 4, 1, 3).reshape(2, 128, 256)
        dC = np.zeros((T, T), np.float64)
        dS = np.zeros((T, T), np.float64)
        blocks = [(0, 1024), (1024, 256)] if is_s else [(256 * u, 256) for u in range(5)]
        for (a0, L) in blocks:
            ll = np.arange(L)
            ang = 2.0 * np.pi * ((ll[:, None] * ll[None, :]) % L) / L
            sc = 1.0 / math.sqrt(L * 128.0)
            dC[a0:a0 + L, a0:a0 + L] = np.cos(ang) * sc
            dS[a0:a0 + L, a0:a0 + L] = -np.sin(ang) * sc
        m = dict(shared)
        m.update(x_in=x_in, condT=condT, rowp=rowp, ropeC=ropeC, ropeS=ropeS, cacheKT=cacheKT, cacheV=cacheV,
                 initS=initS, dftC=dC.astype(f32), dftSn=dS.astype(f32))
        maps.append(m)
    return maps


_NC_CACHE = {}


def kernel(**inputs):
    maps = _host_prep(inputs)
    if "nc" not in _NC_CACHE:
        _NC_CACHE["nc"] = build()
    nc = _NC_CACHE["nc"]
    import os
    ncr = int(os.environ.get("NCR", NCORES))
    res = run_bass_kernel_spmd(nc, maps[:ncr], core_ids=list(range(ncr)))
    outs = list(res.results) + [res.results[0]] * (NCORES - ncr)
    y_prompt = np.zeros((32, 256, D), np.float32)
    y_sample = np.zeros((2, 1024, D), np.float32)
    nk = np.zeros((32, 1, 256, 4, 128), np.float32)
    nv = np.zeros((32, 1, 256, 4, 128), np.float32)
    ns = np.zeros((32, 1, 2, 8, 64, 64), np.float32)
    for c in range(NCORES):
        r = outs[c]
        for u, un in enumerate(_core_units(c)):
            sl = slice(u * 256, (u + 1) * 256)
            if un[0] == "p":
                bi = un[1]
                y_prompt[bi] = r["y_out"][sl]
                nk[bi, 0] = r["k_out"][sl].reshape(256, 4, 128)
                nv[bi, 0] = r["v_out"][sl].reshape(256, 4, 128)
                stt = r["s_out"][u]
                ns[bi, 0] = stt.reshape(2, 2, 64, 4, 64).transpose(0, 3, 1, 4, 2).reshape(2, 8, 64, 64)
            else:
                y_sample[un[1], un[2] * 256:(un[2] + 1) * 256] = r["y_out"][sl]
    return (y_prompt, y_sample, nk, nv, ns)
```

```python
import contextlib
import math
import numpy as np
import concourse.bass as bass
import concourse.mybir as mybir
from concourse.bass_utils import run_bass_kernel_spmd

F32 = mybir.dt.float32
BF16 = mybir.dt.bfloat16
AF = mybir.ActivationFunctionType
ALU = mybir.AluOpType
AX = mybir.AxisListType

T = 1280
NT = 10
U = 5
D = 1024
KC = 8
DFF = 2816
FC = 22
NCORES = 8
ARENA_W = 27200
EXPM05 = math.exp(-0.5)


class Res:
    __slots__ = ("name", "writer", "readers", "excl")

    def __init__(self, name, excl=False):
        self.name = name
        self.writer = None
        self.readers = []
        self.excl = excl


class Sched:
    ENGS = ("pe", "act", "dve", "pool", "sp")
    NDMA = 6

    def __init__(self, nc):
        self.nc = nc
        self.prog = {e: [] for e in self.ENGS}
        self.signal = {e: set() for e in self.ENGS}
        self.ndma = {e: 0 for e in self.ENGS}

    def _collect(self, reads, writes, eng=None):
        deps = []
        for r in reads:
            if r.writer is not None:
                deps.append(r.writer)
            if r.excl:
                deps.extend(t for t in r.readers if t[1] != eng)
        for w in writes:
            if w.writer is not None:
                deps.append(w.writer)
            deps.extend(w.readers)
        return deps

    def _commit(self, tok, reads, writes):
        for r in reads:
            r.readers.append(tok)
        for w in writes:
            w.writer = tok
            w.readers = []

    def op(self, eng, fn, reads=(), writes=()):
        deps = self._collect(reads, writes, eng)
        idx = len(self.prog[eng])
        if eng == "pe":
            deps = [d for d in deps if not (d[0] == "c" and d[1] == "pe")]
        for d in deps:
            if d[0] == "c":
                self.signal[d[1]].add(d[2])
        self.prog[eng].append(dict(fn=fn, deps=deps, kind="c"))
        tok = ("c", eng, idx)
        self._commit(tok, reads, writes)
        return tok

    def dma(self, eng, out, in_, reads=(), writes=()):
        deps = self._collect(reads, writes, eng)
        n = self.ndma[eng]
        self.ndma[eng] += 1
        if n >= self.NDMA:
            deps.append(("d", eng, n - self.NDMA))
        for d in deps:
            if d[0] == "c":
                self.signal[d[1]].add(d[2])
        self.prog[eng].append(dict(out=out, in_=in_, deps=deps, kind="d", n=n))
        tok = ("d", eng, n)
        self._commit(tok, reads, writes)
        return tok

    def emit(self):
        nc = self.nc
        with contextlib.ExitStack() as st:
            csem = {e: st.enter_context(nc.semaphore("c_" + e)) for e in self.ENGS}
            dsem = {e: [st.enter_context(nc.semaphore("d_%s_%d" % (e, i))) for i in range(self.NDMA)]
                    for e in self.ENGS if self.ndma[e] > 0}
            sigval = {}
            for e in self.ENGS:
                cnt = 0
                m = {}
                for i in range(len(self.prog[e])):
                    if i in self.signal[e]:
                        cnt += 1
                        m[i] = cnt
                sigval[e] = m

            def resolve(tok):
                if tok[0] == "c":
                    return csem[tok[1]], sigval[tok[1]][tok[2]], ("c", tok[1])
                e, n = tok[1], tok[2]
                return dsem[e][n % self.NDMA], 16 * (n // self.NDMA + 1), ("d", e, n % self.NDMA)

            def run_engine(e, h):
                waited = {}
                for i, ins in enumerate(self.prog[e]):
                    need = {}
                    for d in ins["deps"]:
                        sem, val, key = resolve(d)
                        if waited.get(key, 0) >= val:
                            continue
                        if key not in need or need[key][1] < val:
                            need[key] = (sem, val)
                    for key, (sem, val) in need.items():
                        h.wait_ge(sem, val)
                        waited[key] = val
                    if ins["kind"] == "c":
                        bi = ins["fn"](h)
                        if i in self.signal[e]:
                            bi.then_inc(csem[e], 1)
                    else:
                        n = ins["n"]
                        h.dma_start(out=ins["out"], in_=ins["in_"]).then_inc(dsem[e][n % self.NDMA], 16)
                if self.ndma[e] > 0:
                    n = self.ndma[e]
                    for slot in range(self.NDMA):
                        cnt = (n - slot + self.NDMA - 1) // self.NDMA if n > slot else 0
                        if cnt > 0:
                            h.wait_ge(dsem[e][slot], 16 * cnt)

            with nc.Block() as block:
                @block.tensor
                def _(eng):
                    run_engine("pe", eng)

                @block.scalar
                def _(eng):
                    run_engine("act", eng)

                @block.vector
                def _(eng):
                    run_engine("dve", eng)

                @block.gpsimd
                def _(eng):
                    run_engine("pool", eng)

                @block.sync
                def _(eng):
                    run_engine("sp", eng)


def colp_layout():
    off = {}
    n = 0

    def add(name, cols):
        nonlocal n
        off[name] = n
        n += cols
    for l in range(2):
        add("bada%d" % l, 48)
        for g in range(4):
            add("gain%d_%d" % (l, g), 8)
        for k in range(3):
            add("conv%d_%d" % (l, k), FC)
        add("convb%d" % l, FC)
    add("mu0", 15)
    add("mu1", 15)
    for d in range(2):
        add("w0_%d" % d, 4)
        add("a0_%d" % d, 4)
    for k in range(3):
        add("kvec%d" % k, 4)
    return off, n


def rowp_layout():
    off = {}
    n = 0

    def add(name, cols):
        nonlocal n
        off[name] = n
        n += cols
    add("lam", 256)
    add("subln", 128)
    add("abias", 30)
    add("bndL", 5)
    add("bndR", 5)
    return off, n


def cols_of(vec):
    v = np.asarray(vec, np.float32).reshape(-1, 128)
    return np.ascontiguousarray(v.T)


def build(stage=99):
    nc = bass.Bass("TRN2", target_bir_lowering=False)
    COFF, NCOL = colp_layout()
    ROFF, NROW = rowp_layout()

    def din(name, shape):
        return nc.dram_tensor(name, list(shape), F32, kind="ExternalInput").ap()

    def dout(name, shape):
        return nc.dram_tensor(name, list(shape), F32, kind="ExternalOutput").ap()

    x_in = din("x_in", [T, D])
    condT = din("condT", [128, 16])
    colp = din("colp", [128, NCOL])
    rowp = din("rowp", [1, NROW])
    w_ada = din("w_ada", [2, D, 6 * D])
    w_in_even = din("w_in_even", [D, 3456])
    w_out_even = din("w_out_even", [D, D])
    w_out_odd = din("w_out_odd", [D, D])
    w_ffn_in = din("w_ffn_in", [2, D, 2 * DFF])
    w_ffn_out = din("w_ffn_out", [2, DFF, D])
    w2t_d = din("w2t", [128, 512])
    a2t_d = din("a2t", [128, 512])
    g2_d = din("g2", [128, 512])
    cacheKT = din("cacheKT", [512, 256])
    cacheV = din("cacheV", [256, 512])
    ropeC = din("ropeC", [128, T])
    ropeS = din("ropeS", [128, T])
    initS = din("initS", [2, 128, 256])
    dftC = din("dftC", [T, T])
    dftSn = din("dftSn", [T, T])
    chCS = din("chCS", [128, 256])
    cmat = din("cmat", [8, 128, 128])
    lnx_d = din("lnx", [2, 512])

    y_out = dout("y_out", [T, D])
    k_out = dout("k_out", [T, 512])
    v_out = dout("v_out", [T, 512])
    s_out = dout("s_out", [U, 2, 128, 256])

    import os
    sub = os.environ.get("SUB", "abcmqv")
    with contextlib.ExitStack() as st:
        s = Sched(nc)

        def sb(name, shape, dt=F32):
            return st.enter_context(nc.sbuf_tensor(name, list(shape), dt))

        x_sb = sb("x_sb", [128, NT, D])
        Rx = [Res("x%d" % i) for i in range(NT)]
        WB = 4096
        wbuf = [sb("wb%d" % i, [128, WB], BF16) for i in range(3)]
        Rw = [Res("wb%d" % i) for i in range(3)]
        wctr = [0]
        colp_sb = sb("colp_sb", [128, NCOL])
        Rcolp = Res("colp")
        rowp_sb = sb("rowp_sb", [128, NROW])
        Rrowp = Res("rowp")
        ident_f = sb("ident_f", [128, 128])
        ident_b = sb("ident_b", [128, 128], BF16)
        rotT_b = sb("rotT_b", [128, 128], BF16)
        bones_f = sb("bones_f", [128, 128])
        halfsel_f = sb("halfsel_f", [128, 128])
        bones_b = sb("bones_b", [128, 128], BF16)
        Rconst = Res("const")
        gg_sb = sb("gg_sb", [128, 2, 2, D], BF16)
        Rgg = Res("gg")
        scond = sb("scond", [128, 16], BF16)
        condf = sb("condf", [128, 16])
        Rcond = Res("cond")
        modcol = sb("modcol", [128, 2, 96])
        Rmod = Res("modcol")
        scsh = sb("scsh", [128, 2, 2, 2, 16])
        Rscsh = Res("scsh")
        stat = sb("stat", [128, 64])
        Rstat = Res("stat")
        junk_b = sb("junk_b", [128, D], BF16)
        Rjunk = Res("junk")
        xn_b = sb("xn_b", [128, D], BF16)
        Rxn = Res("xn")
        stage_f = sb("stage_f", [128, 2, 512])
        Rstage = [Res("stage0"), Res("stage1")]
        stctr = [0]
        arena = sb("arena", [128, ARENA_W])
        Rar = {}

        def ares(name):
            if name not in Rar:
                Rar[name] = Res("ar_" + name)
            return Rar[name]

        def carve(off_w, nelem, dt):
            if dt == F32:
                return arena[:, off_w:off_w + nelem]
            return arena[:, off_w:off_w + (nelem + 1) // 2].bitcast(BF16)[:, 0:nelem]

        ps = [st.enter_context(nc.psum_tensor("ps%d" % b, [128, 512], F32)) for b in range(8)]
        psb = [p.bitcast(BF16) for p in ps]
        Rps = [Res("ps%d" % b, excl=True) for b in range(8)]
        bctr = [0]

        reserved = set()

        def bank():
            while True:
                b = bctr[0] % 8
                bctr[0] += 1
                if b not in reserved:
                    return b

        def barrier():
            allr = list(Rar.values()) + list(Rw)
            s.op("dve", lambda e: e.memset(stat[:, 63:64], 0.0), writes=allr + [Rstat])

        def wload(src_ap, kc, ncols):
            i = wctr[0] % 3
            wctr[0] += 1
            view = wbuf[i][:, 0:kc * ncols].rearrange("p (c n) -> p c n", c=kc)
            s.dma("pool", view, src_ap.rearrange("(c p) n -> p c n", p=128), writes=[Rw[i]])
            return view, Rw[i]

        s.dma("sp", colp_sb[:], colp, writes=[Rcolp])
        s.dma("sp", rowp_sb[:], rowp[0, :].partition_broadcast(128), writes=[Rrowp])
        s.dma("sp", ident_f[:], cmat[0], writes=[Rconst])
        s.dma("sp", bones_f[:], cmat[2], writes=[Rconst])
        s.dma("sp", halfsel_f[:], cmat[3], writes=[Rconst])
        s.dma("pool", ident_b[:], cmat[0], writes=[Rconst])
        s.dma("pool", rotT_b[:], cmat[1], writes=[Rconst])
        s.dma("pool", bones_b[:], cmat[2], writes=[Rconst])
        s.dma("sp", condf[:], condT, writes=[Rcond])
        for i in range(NT):
            s.dma("sp", x_sb[:, i, :], x_in[i * 128:(i + 1) * 128, :], writes=[Rx[i]])
        s.op("act", lambda e: e.activation(scond[:], condf[:], AF.Silu), reads=[Rcond], writes=[Rcond])

        def cp(name, c=0, n=1):
            o = COFF[name] + c
            return colp_sb[:, o:o + n]

        for l in range(2):
            b = bank()
            for v in range(6):
                for half in range(2):
                    wv, rw = wload(w_ada[l][:, v * 1024 + half * 512: v * 1024 + half * 512 + 512], KC, 512)
                    for j in range(4):
                        col = (v * 8 + half * 4 + j) * 2
                        for kc in range(KC):
                            s.op("pe", lambda e, wv=wv, j=j, kc=kc, col=col, b=b: e.matmul(
                                ps[b][:, col:col + 2], wv[:, kc, j * 128:(j + 1) * 128],
                                scond[:, kc * 2:kc * 2 + 2], start=(kc == 0), stop=(kc == KC - 1)),
                                reads=[rw, Rcond], writes=[Rps[b]])
            s.op("dve", lambda e, l=l, b=b: e.tensor_tensor(
                modcol[:, l, :].rearrange("p (c a) -> p c a", a=2),
                ps[b][:, 0:96].rearrange("p (c a) -> p c a", a=2),
                cp("bada%d" % l, 0, 48).unsqueeze(2).to_broadcast([128, 48, 2]), ALU.add),
                reads=[Rps[b], Rcolp], writes=[Rmod])

            def mv(v, l=l):
                return modcol[:, l, v * 16:(v + 1) * 16].rearrange("p (c a) -> p c a", a=2)

            def gain(g, l=l):
                return cp("gain%d_%d" % (l, g), 0, 8).unsqueeze(2).to_broadcast([128, 8, 2])
            for wi, (vs, vsh, g) in enumerate([(1, 0, 0), (4, 3, 2)]):
                sc = scsh[:, l, wi, 0, :].rearrange("p (c a) -> p c a", a=2)
                sh = scsh[:, l, wi, 1, :].rearrange("p (c a) -> p c a", a=2)
                s.op("dve", lambda e, sc=sc, vs=vs, mv=mv: e.tensor_scalar(sc, mv(vs), 1.0, None, ALU.add),
                     reads=[Rmod], writes=[Rscsh])
                s.op("dve", lambda e, sc=sc, g=g, gain=gain: e.tensor_tensor(sc, sc, gain(g), ALU.mult),
                     reads=[Rscsh, Rcolp], writes=[Rscsh])
                s.op("dve", lambda e, sh=sh, vsh=vsh, mv=mv: e.tensor_copy(sh, mv(vsh)),
                     reads=[Rmod], writes=[Rscsh])

        ggcol = sb("ggcol", [128, 2, 16])
        Rggcol = Res("ggcol")

        def make_gg(l):
            for wi, (vg, g) in enumerate([(2, 1), (5, 3)]):
                gc = ggcol[:, wi, :].rearrange("p (c a) -> p c a", a=2)
                s.op("dve", lambda e, gc=gc, vg=vg, g=g, l=l: e.tensor_tensor(
                    gc, modcol[:, l, vg * 16:(vg + 1) * 16].rearrange("p (c a) -> p c a", a=2),
                    cp("gain%d_%d" % (l, g), 0, 8).unsqueeze(2).to_broadcast([128, 8, 2]), ALU.mult),
                    reads=[Rmod, Rcolp], writes=[Rggcol])
                for a in range(2):
                    for hh in range(2):
                        b = bank()
                        for j in range(4):
                            c = hh * 4 + j
                            s.op("pe", lambda e, b=b, wi=wi, c=c, a=a, j=j: e.matmul(
                                ps[b][:, j * 128:(j + 1) * 128],
                                ggcol[:, wi, c * 2 + a:c * 2 + a + 1].to_broadcast([128, 128]),
                                ident_f[:], start=True, stop=True),
                                reads=[Rggcol, Rconst], writes=[Rps[b]])
                        s.op("act", lambda e, b=b, wi=wi, a=a, hh=hh: e.copy(
                            gg_sb[:, wi, a, hh * 512:(hh + 1) * 512], ps[b][:]),
                            reads=[Rps[b]], writes=[Rgg])

        hT = carve(0, 8 * T, BF16).rearrange("p (c t) -> p c t", c=8)
        RhT = ares("hT")

        def ab_of_tile(i):
            return 0 if i < 8 else 1

        def make_hT(l, wi):
            for i in range(NT):
                a = ab_of_tile(i)
                s.op("act", lambda e, i=i: e.activation(junk_b[:], x_sb[:, i, :], AF.Square, scale=1.0 / 32.0,
                                                        accum_out=stat[:, 0:1]),
                     reads=[Rx[i]], writes=[Rjunk, Rstat])
                s.op("dve", lambda e: e.tensor_scalar(stat[:, 1:2], stat[:, 0:1], 1e-6, None, ALU.add),
                     reads=[Rstat], writes=[Rstat])
                s.op("act", lambda e: e.activation(stat[:, 1:2], stat[:, 1:2], AF.Sqrt), reads=[Rstat], writes=[Rstat])
                s.op("dve", lambda e: e.reciprocal(stat[:, 2:3], stat[:, 1:2]), reads=[Rstat], writes=[Rstat])
                s.op("dve", lambda e, i=i: e.tensor_scalar(xn_b[:], x_sb[:, i, :], stat[:, 2:3], None, ALU.mult),
                     reads=[Rx[i], Rstat], writes=[Rxn])
                b = bank()
                for c in range(8):
                    s.op("pe", lambda e, b=b, c=c: e.transpose(psb[b][:, c * 128:(c + 1) * 128],
                                                               xn_b[:, c * 128:(c + 1) * 128], ident_b[:]),
                         reads=[Rxn, Rconst], writes=[Rps[b]])
                for c in range(8):
                    s.op("act", lambda e, b=b, c=c, i=i, a=a: e.activation(
                        hT[:, c, i * 128:(i + 1) * 128], psb[b][:, c * 128:(c + 1) * 128], AF.Identity,
                        bias=scsh[:, l, wi, 1, c * 2 + a:c * 2 + a + 1],
                        scale=scsh[:, l, wi, 0, c * 2 + a:c * 2 + a + 1]),
                        reads=[Rps[b], Rscsh], writes=[RhT])

        TOKCH = [(0, 512), (512, 512), (1024, 256)]

        def linear_fm(wv, rw, ncol0, actT, Ract, kc_n, evac):
            for (t0, tn) in TOKCH:
                b = bank()
                for kc in range(kc_n):
                    s.op("pe", lambda e, b=b, kc=kc, t0=t0, tn=tn: e.matmul(
                        ps[b][:, 0:tn], wv[:, kc, ncol0:ncol0 + 128], actT[:, kc, t0:t0 + tn],
                        start=(kc == 0), stop=(kc == kc_n - 1)),
                        reads=[rw, Ract], writes=[Rps[b]])
                evac(b, t0, tn)

        def linear_tm(wv, rw, ncols, actT, Ract, kc_n, i, b, col0=0):
            for kc in range(kc_n):
                s.op("pe", lambda e, kc=kc: e.matmul(
                    ps[b][:, col0:col0 + ncols], actT[:, kc, i * 128:(i + 1) * 128], wv[:, kc, 0:ncols],
                    start=(kc == 0), stop=(kc == kc_n - 1)),
                    reads=[rw, Ract], writes=[Rps[b]])

        def residual_update(i, banks, wi, fsrc=None, Rf=None):
            a = ab_of_tile(i)
            if fsrc is None:
                for hh, b in enumerate(banks):
                    s.op("act", lambda e, b=b, hh=hh: e.activation(junk_b[:, 0:512], ps[b][:], AF.Square,
                                                                   scale=1.0 / 32.0, accum_out=stat[:, 8 + hh:9 + hh]),
                         reads=[Rps[b]], writes=[Rjunk, Rstat])
                s.op("dve", lambda e: e.tensor_tensor(stat[:, 10:11], stat[:, 8:9], stat[:, 9:10], ALU.add),
                     reads=[Rstat], writes=[Rstat])
            else:
                s.op("act", lambda e: e.activation(junk_b[:], fsrc, AF.Square, scale=1.0 / 32.0,
                                                   accum_out=stat[:, 10:11]),
                     reads=[Rf], writes=[Rjunk, Rstat])
            s.op("dve", lambda e: e.tensor_scalar(stat[:, 11:12], stat[:, 10:11], 1e-6, None, ALU.add),
                 reads=[Rstat], writes=[Rstat])
            s.op("act", lambda e: e.activation(stat[:, 11:12], stat[:, 11:12], AF.Sqrt), reads=[Rstat], writes=[Rstat])
            s.op("dve", lambda e: e.reciprocal(stat[:, 12:13], stat[:, 11:12]), reads=[Rstat], writes=[Rstat])
            for hh in range(2):
                src = ps[banks[hh]][:] if fsrc is None else fsrc[:, hh * 512:(hh + 1) * 512]
                rr = [Rps[banks[hh]]] if fsrc is None else [Rf]
                si = stctr[0] % 2
                stctr[0] += 1
                s.op("dve", lambda e, src=src, hh=hh, si=si: e.scalar_tensor_tensor(
                    stage_f[:, si, :], src, stat[:, 12:13], gg_sb[:, wi, a, hh * 512:(hh + 1) * 512],
                    ALU.mult, ALU.mult),
                    reads=rr + [Rstat, Rgg], writes=[Rstage[si]])
                s.op("dve", lambda e, hh=hh, si=si, i=i: e.tensor_tensor(
                    x_sb[:, i, hh * 512:(hh + 1) * 512], x_sb[:, i, hh * 512:(hh + 1) * 512], stage_f[:, si, :], ALU.add),
                    reads=[Rstage[si], Rx[i]], writes=[Rx[i]])

        actT = carve(5120, FC * T, BF16).rearrange("p (c t) -> p c t", c=FC)
        RactT = ares("actT")
        graw = carve(19200, T + 2, F32)
        Rgraw = ares("graw")
        cbuf = carve(19200 + 1284, T, F32)
        Rcbuf = ares("cbuf")
        sbuf_s = carve(19200 + 1284 + 1280, T, F32)
        Rsbuf = ares("sbuf_s")
        fstageA = carve(19200, 5 * D, F32).rearrange("p (i n) -> p i n", i=5)
        fstageB = carve(0, 5 * D, F32).rearrange("p (i n) -> p i n", i=5)

        def fst(i):
            return fstageA[:, i, :] if i < 5 else fstageB[:, i - 5, :]

        def Rfst_of(i):
            return ares("fstage") if i < 5 else RhT
        bcorr = sb("bcorr", [128, 16])
        Rbcorr = Res("bcorr")

        def ffn(l):
            make_hT(l, 1)
            s.op("dve", lambda e: e.memset(graw[:, 0:1], 0.0), writes=[Rgraw])
            s.op("dve", lambda e: e.memset(graw[:, T + 1:T + 2], 0.0), writes=[Rgraw])
            for blk in range(0, FC, 2):
                wu, ru = wload(w_ffn_in[l][:, blk * 128: blk * 128 + 256], KC, 256)
                wg, rg = wload(w_ffn_in[l][:, DFF + blk * 128: DFF + blk * 128 + 256], KC, 256)
                for jj in range(2):
                    fc = blk + jj
                    def evac_g(b, t0, tn):
                        s.op("act", lambda e, b=b, t0=t0, tn=tn: e.copy(graw[:, 1 + t0:1 + t0 + tn], ps[b][:, 0:tn]),
                             reads=[Rps[b]], writes=[Rgraw])
                    linear_fm(wg, rg, jj * 128, hT, RhT, KC, evac_g)
                    w0 = cp("conv%d_0" % l, fc)
                    w1 = cp("conv%d_1" % l, fc)
                    w2 = cp("conv%d_2" % l, fc)
                    cb = cp("convb%d" % l, fc)
                    s.op("act", lambda e, w1=w1, cb=cb: e.activation(cbuf[:], graw[:, 1:T + 1], AF.Identity, bias=cb, scale=w1),
                         reads=[Rgraw, Rcolp], writes=[Rcbuf])
                    s.op("dve", lambda e, w0=w0: e.scalar_tensor_tensor(cbuf[:], graw[:, 0:T], w0, cbuf[:], ALU.mult, ALU.add),
                         reads=[Rgraw, Rcolp, Rcbuf], writes=[Rcbuf])
                    s.op("dve", lambda e, w2=w2: e.scalar_tensor_tensor(cbuf[:], graw[:, 2:T + 2], w2, cbuf[:], ALU.mult, ALU.add),
                         reads=[Rgraw, Rcolp, Rcbuf], writes=[Rcbuf])
                    gprev = graw[:, 256:256 + 1024].rearrange("p (u k) -> p u k", k=256)[:, :, 0]
                    gnext = graw[:, 257:257 + 1024].rearrange("p (u k) -> p u k", k=256)[:, :, 0]
                    s.op("dve", lambda e, gprev=gprev: e.tensor_tensor(bcorr[:, 0:4], gprev, rowp_sb[:, ROFF["bndL"] + 1:ROFF["bndL"] + 5], ALU.mult),
                         reads=[Rgraw, Rrowp], writes=[Rbcorr])
                    s.op("dve", lambda e, w0=w0: e.tensor_scalar(bcorr[:, 0:4], bcorr[:, 0:4], w0, None, ALU.mult),
                         reads=[Rbcorr, Rcolp], writes=[Rbcorr])
                    c_at = cbuf[:, 256:256 + 1024].rearrange("p (u k) -> p u k", k=256)[:, :, 0]
                    s.op("dve", lambda e, c_at=c_at: e.tensor_tensor(c_at, c_at, bcorr[:, 0:4], ALU.subtract),
                         reads=[Rbcorr, Rcbuf], writes=[Rcbuf])
                    s.op("dve", lambda e, gnext=gnext: e.tensor_tensor(bcorr[:, 4:8], gnext, rowp_sb[:, ROFF["bndL"] + 1:ROFF["bndL"] + 5], ALU.mult),
                         reads=[Rgraw, Rrowp], writes=[Rbcorr])
                    s.op("dve", lambda e, w2=w2: e.tensor_scalar(bcorr[:, 4:8], bcorr[:, 4:8], w2, None, ALU.mult),
                         reads=[Rbcorr, Rcolp], writes=[Rbcorr])
                    c_at2 = cbuf[:, 255:255 + 1024].rearrange("p (u k) -> p u k", k=256)[:, :, 0]
                    s.op("dve", lambda e, c_at2=c_at2: e.tensor_tensor(c_at2, c_at2, bcorr[:, 4:8], ALU.subtract),
                         reads=[Rbcorr, Rcbuf], writes=[Rcbuf])
                    s.op("act", lambda e: e.activation(sbuf_s[:], cbuf[:], AF.Silu), reads=[Rcbuf], writes=[Rsbuf])
                    def evac_u(b, t0, tn, fc=fc):
                        s.op("dve", lambda e, b=b, t0=t0, tn=tn: e.tensor_tensor(
                            actT[:, fc, t0:t0 + tn], ps[b][:, 0:tn], sbuf_s[:, t0:t0 + tn], ALU.mult),
                            reads=[Rps[b], Rsbuf], writes=[RactT])
                    linear_fm(wu, ru, jj * 128, hT, RhT, KC, evac_u)
            for nb in range(8):
                wv, rw = wload(w_ffn_out[l][:, nb * 128:(nb + 1) * 128], FC, 128)
                for i in range(NT):
                    b = bank()
                    linear_tm(wv, rw, 128, actT, RactT, FC, i, b)
                    s.op("act", lambda e, b=b, i=i, nb=nb: e.copy(fst(i)[:, nb * 128:(nb + 1) * 128], ps[b][:, 0:128]),
                         reads=[Rps[b]], writes=[Rfst_of(i)])
            for i in range(NT):
                residual_update(i, None, 1, fsrc=fst(i), Rf=Rfst_of(i))

        qT = carve(5120, 4 * T, BF16).rearrange("p (c t) -> p c t", c=4)
        RqT = ares("qT")
        kT = carve(7680, 4 * 1536, BF16).rearrange("p (c t) -> p c t", c=4)
        RkT = ares("kT")
        Vaug = carve(10752, 12 * 4 * 144, BF16).rearrange("p (k h d) -> p k h d", k=12, h=4)
        RV = ares("Vaug")
        yT = carve(0, 8 * T, BF16).rearrange("p (c t) -> p c t", c=8)
        fbT = carve(14208, 15 * (T + 2), BF16).rearrange("p (c t) -> p c t", c=15)
        RfbT = ares("fbT")
        RyT = RhT
        ropeC_sb = carve(23824, T, F32)
        ropeS_sb = carve(23824 + T, T, F32)
        Rrope = Res("rope")
        rtmp = sb("rtmp", [128, 2, 512])
        Rrtmp = Res("rtmp")
        raw_b = sb("raw_b", [128, 512], BF16)
        Rraw = Res("raw_b")

        def even_inproj():
            s.dma("sp", ropeC_sb[:], ropeC, writes=[Rrope])
            s.dma("sp", ropeS_sb[:], ropeS, writes=[Rrope])
            s.dma("pool", kT[:, :, 0:256], cacheKT.rearrange("(h p) t -> p h t", p=128), writes=[RkT])
            for kk_ in range(2):
                s.dma("pool", Vaug[:, kk_, :, 0:128],
                      cacheV[kk_ * 128:(kk_ + 1) * 128, :].rearrange("p (h d) -> p h d", h=4), writes=[RV])
            if "m" in sub:
                s.op("pool", lambda e: e.memset(Vaug[:, :, :, 128:129], 1.0), writes=[RV])
            for which, dst, doff in ((0, qT, 0), (1, kT, 256)):
                if "q" not in sub:
                    break
                wv, rw = wload(w_in_even[:, which * 512:(which + 1) * 512], KC, 512)
                Rdst = RqT if which == 0 else RkT
                for h in range(4):
                    def evac(b, t0, tn, h=h, dst=dst, doff=doff, Rdst=Rdst):
                        s.op("act", lambda e, b=b, tn=tn: e.copy(raw_b[:, 0:tn], ps[b][:, 0:tn]),
                             reads=[Rps[b]], writes=[Rraw])
                        b2 = bank()
                        s.op("pe", lambda e, b2=b2, tn=tn: e.matmul(ps[b2][:, 0:tn], rotT_b[:], raw_b[:, 0:tn], start=True, stop=True),
                             reads=[Rraw, Rconst], writes=[Rps[b2]])
                        s.op("dve", lambda e, t0=t0, tn=tn: e.tensor_tensor(rtmp[:, 0, 0:tn], raw_b[:, 0:tn], ropeC_sb[:, t0:t0 + tn], ALU.mult),
                             reads=[Rraw, Rrope], writes=[Rrtmp])
                        s.op("dve", lambda e, b2=b2, t0=t0, tn=tn: e.tensor_tensor(rtmp[:, 1, 0:tn], ps[b2][:, 0:tn], ropeS_sb[:, t0:t0 + tn], ALU.mult),
                             reads=[Rps[b2], Rrope], writes=[Rrtmp])
                        s.op("dve", lambda e, t0=t0, tn=tn: e.tensor_tensor(dst[:, h, doff + t0:doff + t0 + tn], rtmp[:, 0, 0:tn], rtmp[:, 1, 0:tn], ALU.add),
                             reads=[Rrtmp], writes=[Rdst])
                    linear_fm(wv, rw, h * 128, hT, RhT, KC, evac)
                if which == 1:
                    for i in range(NT):
                        b = bank()
                        linear_tm(wv, rw, 512, hT, RhT, KC, i, b)
                        si = stctr[0] % 2
                        stctr[0] += 1
                        s.op("act", lambda e, b=b, si=si: e.copy(stage_f[:, si, :], ps[b][:]), reads=[Rps[b]], writes=[Rstage[si]])
                        s.dma("sp", k_out[i * 128:(i + 1) * 128, :], stage_f[:, si, :], reads=[Rstage[si]])
            s.op("pool", lambda e: e.memset(fbT[:, :, 0:1], 0.0), writes=[RfbT])
            s.op("pool", lambda e: e.memset(fbT[:, :, T + 1:T + 2], 0.0), writes=[RfbT])
            for pc, (c0, ncols) in enumerate([(1536, 512), (2048, 512), (2560, 512), (3072, 384)]):
                wv, rw = wload(w_in_even[:, c0:c0 + ncols], KC, ncols)
                for j in range(ncols // 128):
                    ch = pc * 4 + j

                    def evac_fb(b, t0, tn, ch=ch):
                        s.op("act", lambda e, b=b, t0=t0, tn=tn: e.copy(fbT[:, ch, 1 + t0:1 + t0 + tn], ps[b][:, 0:tn]),
                             reads=[Rps[b]], writes=[RfbT])
                    linear_fm(wv, rw, j * 128, hT, RhT, KC, evac_fb)
            wv, rw = wload(w_in_even[:, 1024:1536], KC, 512)
            for i in range(NT if "v" in sub else 0):
                b = bank()
                linear_tm(wv, rw, 512, hT, RhT, KC, i, b)
                si = stctr[0] % 2
                stctr[0] += 1
                s.op("act", lambda e, b=b, si=si: e.copy(stage_f[:, si, :], ps[b][:]), reads=[Rps[b]], writes=[Rstage[si]])
                s.dma("sp", v_out[i * 128:(i + 1) * 128, :], stage_f[:, si, :], reads=[Rstage[si]])
                s.op("dve", lambda e, si=si, i=i: e.tensor_copy(Vaug[:, 2 + i, :, 0:128], stage_f[:, si, :].rearrange("p (h d) -> p h d", h=4)),
                     reads=[Rstage[si]], writes=[RV])


        NPB = 4
        PT = [[sb("PT%d%d" % (m, k), [128, 256], BF16) for k in range(NPB)] for m in range(2)]
        RPT = [[Res("PT%d%d" % (m, k)) for k in range(NPB)] for m in range(2)]
        ya_f = sb("ya_f", [128, 128])
        Rya = Res("ya_f")
        ya_b = sb("ya_b", [128, 128], BF16)
        Ryab = Res("ya_b")
        subln08 = sb("subln08", [128, 128])
        Rsub = Res("subln08")
        lamt = sb("lamt", [128, 64])
        Rlamt = Res("lamt")

        def attention():
            lo = ROFF["lam"]
            for k in range(2):
                s.op("dve", lambda e, k=k: e.tensor_tensor(lamt[:], rowp_sb[:, lo + 128 * k: lo + 128 * k + 64],
                                                          rowp_sb[:, lo + 128 * k + 64: lo + 128 * k + 128], ALU.mult),
                     reads=[Rrowp], writes=[Rlamt])
                s.op("dve", lambda e, k=k: e.reduce_sum(stat[:, 20 + k:21 + k], lamt[:], axis=AX.X),
                     reads=[Rlamt], writes=[Rstat])
            s.op("act", lambda e: e.activation(stat[:, 22:24], stat[:, 20:22], AF.Exp), reads=[Rstat], writes=[Rstat])
            s.op("dve", lambda e: e.tensor_tensor(stat[:, 24:25], stat[:, 22:23], stat[:, 23:24], ALU.subtract),
                 reads=[Rstat], writes=[Rstat])
            s.op("dve", lambda e: e.tensor_scalar(stat[:, 25:26], stat[:, 24:25], 0.2, -1.0, ALU.add, ALU.mult),
                 reads=[Rstat], writes=[Rstat])
            s.op("dve", lambda e: e.tensor_scalar(subln08[:], rowp_sb[:, ROFF["subln"]:ROFF["subln"] + 128], 0.8, None, ALU.mult),
                 reads=[Rrowp], writes=[Rsub])
            pctr = [0, 0]
            for h in range(4):
                for qu in range(5):
                    acc = [bank(), bank()]
                    reserved.update(acc)
                    pend = []
                    for kt in range(12):
                        ku = 0 if kt < 2 else 1 + (kt - 2) // 2
                        bcol = ROFF["abias"] + ku * 5 + qu
                        for m in range(2):
                            bs = bank()
                            s.op("pe", lambda e, bs=bs, m=m, h=h, kt=kt, qu=qu: e.matmul(
                                ps[bs][:, 0:256], kT[64 * m:64 * m + 64, h, kt * 128:(kt + 1) * 128],
                                qT[64 * m:64 * m + 64, h, qu * 256:(qu + 1) * 256], start=True, stop=True),
                                reads=[RkT, RqT], writes=[Rps[bs]])
                            pk = pctr[m] % NPB
                            pctr[m] += 1
                            s.op("act", lambda e, bs=bs, m=m, pk=pk, bcol=bcol: e.activation(
                                PT[m][pk][:], ps[bs][:, 0:256], AF.Exp, bias=rowp_sb[:, bcol:bcol + 1], scale=0.125),
                                reads=[Rps[bs], Rrowp], writes=[RPT[m][pk]])

                            def pv(m=m, pk=pk, kt=kt, h=h, acc=acc):
                                for qt in range(2):
                                    s.op("pe", lambda e, qt=qt: e.matmul(
                                        ps[acc[m]][:, qt * 129:qt * 129 + 129], PT[m][pk][:, qt * 128:(qt + 1) * 128],
                                        Vaug[:, kt, h, 0:129], start=(kt == 0 and qt == 0), stop=(kt == 11),
                                        skip_group_check=True),
                                        reads=[RPT[m][pk], RV], writes=[Rps[acc[m]]])
                            pend.append(pv)
                            if len(pend) > 2:
                                pend.pop(0)()
                    while pend:
                        pend.pop(0)()
                    reserved.difference_update(acc)
                    for qt in range(2):
                        i = qu * 2 + qt
                        c0 = qt * 129
                        s.op("dve", lambda e, c0=c0, acc=acc: e.reciprocal(stat[:, 30:31], ps[acc[0]][:, c0 + 128:c0 + 129]),
                             reads=[Rps[acc[0]]], writes=[Rstat])
                        s.op("dve", lambda e, c0=c0, acc=acc: e.reciprocal(stat[:, 31:32], ps[acc[1]][:, c0 + 128:c0 + 129]),
                             reads=[Rps[acc[1]]], writes=[Rstat])
                        s.op("dve", lambda e: e.tensor_tensor(stat[:, 32:33], stat[:, 31:32], stat[:, 25:26], ALU.mult),
                             reads=[Rstat], writes=[Rstat])
                        s.op("dve", lambda e, c0=c0, acc=acc: e.tensor_scalar(ya_f[:], ps[acc[0]][:, c0:c0 + 128], stat[:, 30:31], None, ALU.mult),
                             reads=[Rps[acc[0]], Rstat], writes=[Rya])
                        s.op("dve", lambda e, c0=c0, acc=acc: e.scalar_tensor_tensor(ya_f[:], ps[acc[1]][:, c0:c0 + 128], stat[:, 32:33], ya_f[:], ALU.mult, ALU.add),
                             reads=[Rps[acc[1]], Rstat, Rya], writes=[Rya])
                        s.op("act", lambda e: e.activation(junk_b[:, 0:128], ya_f[:], AF.Square, scale=1.0 / math.sqrt(128.0),
                                                           accum_out=stat[:, 33:34]),
                             reads=[Rya], writes=[Rjunk, Rstat])
                        s.op("dve", lambda e: e.tensor_scalar(stat[:, 34:35], stat[:, 33:34], 1e-6, None, ALU.add),
                             reads=[Rstat], writes=[Rstat])
                        s.op("act", lambda e: e.activation(stat[:, 34:35], stat[:, 34:35], AF.Sqrt), reads=[Rstat], writes=[Rstat])
                        s.op("dve", lambda e: e.reciprocal(stat[:, 35:36], stat[:, 34:35]), reads=[Rstat], writes=[Rstat])
                        s.op("dve", lambda e: e.scalar_tensor_tensor(ya_b[:], ya_f[:], stat[:, 35:36], subln08[:], ALU.mult, ALU.mult),
                             reads=[Rya, Rstat, Rsub], writes=[Ryab])
                        reserved.update(acc)
                        bt = bank()
                        reserved.difference_update(acc)
                        s.op("pe", lambda e, bt=bt: e.transpose(psb[bt][:, 0:128], ya_b[:], ident_b[:]),
                             reads=[Ryab, Rconst], writes=[Rps[bt]])
                        s.op("act", lambda e, bt=bt, h=h, i=i: e.copy(yT[:, h, i * 128:(i + 1) * 128], psb[bt][:, 0:128]),
                             reads=[Rps[bt]], writes=[RyT])

        def out_proj(W, actTv, Ract):
            pieces = [wload(W[:, hh * 512:(hh + 1) * 512], KC, 512) for hh in range(2)]
            for i in range(NT):
                bb = [bank(), bank()]
                for hh in range(2):
                    linear_tm(pieces[hh][0], pieces[hh][1], 512, actTv, Ract, KC, i, bb[hh])
                residual_update(i, bb, 0)

        Xc = carve(5120, NT * 8 * 256, BF16).rearrange("p (i g n) -> p i g n", i=NT, g=8)
        RXc = ares("Xc")
        chCS_b = sb("chCS_b", [128, 256], BF16)
        Rch = Res("chCS")

        def fnet():
            s.dma("pool", chCS_b[:], chCS, writes=[Rch])
            for i in range(NT):
                for gp in range(4):
                    b = bank()
                    for g2_ in range(2):
                        g = gp * 2 + g2_
                        s.op("pe", lambda e, b=b, g=g, g2_=g2_, i=i: e.matmul(
                            ps[b][:, g2_ * 256:(g2_ + 1) * 256], hT[:, g, i * 128:(i + 1) * 128], chCS_b[:], start=True, stop=True),
                            reads=[RhT, Rch], writes=[Rps[b]])
                    s.op("act", lambda e, b=b, gp=gp, i=i: e.copy(
                        Xc[:, i, gp * 2:gp * 2 + 2, :], ps[b][:].rearrange("p (g n) -> p g n", g=2)),
                        reads=[Rps[b]], writes=[RXc])
            for (t0, tn) in [(0, 384), (384, 384), (768, 384), (1152, 128)]:
                wc, rc = wload(dftC[:, t0:t0 + tn], NT, tn)
                wsn, rsn = wload(dftSn[:, t0:t0 + tn], NT, tn)
                for g in range(8):
                    b = bank()
                    for tc in range(NT):
                        s.op("pe", lambda e, b=b, g=g, tc=tc, tn=tn, wc=wc: e.matmul(
                            ps[b][:, 0:tn], Xc[:, tc, g, 0:128], wc[:, tc, 0:tn], start=(tc == 0), stop=False),
                            reads=[RXc, rc], writes=[Rps[b]])
                        s.op("pe", lambda e, b=b, g=g, tc=tc, tn=tn, wsn=wsn: e.matmul(
                            ps[b][:, 0:tn], Xc[:, tc, g, 128:256], wsn[:, tc, 0:tn], start=False, stop=(tc == NT - 1)),
                            reads=[RXc, rsn], writes=[Rps[b]])
                    s.op("act", lambda e, b=b, g=g, t0=t0, tn=tn: e.copy(yT[:, g, t0:t0 + tn], ps[b][:, 0:tn]),
                         reads=[Rps[b]], writes=[RyT])


        V_tm = carve(5120, NT * 512, BF16).rearrange("p (i n) -> p i n", i=NT)
        RVtm = ares("V_tm")
        YSb = carve(7680, NT * 512, BF16).rearrange("p (i n) -> p i n", i=NT)
        RYS = ares("YSb")
        Ystage = carve(10240, 128 * 16, F32)[0:64, :].rearrange("p (n k) -> p n k", k=16)
        RYst = ares("Ystage")
        Sst = [carve(12288, 512, F32), carve(12800, 512, F32)]
        RS = [ares("S0"), ares("S1")]
        tmpA = carve(13312, 512, F32)
        RtA = ares("tmpA")
        tmpB = carve(23824, 512, F32)
        RtB = ares("tmpB")
        w2t_b = carve(24336, 512, BF16)
        a2t_b = carve(24592, 512, BF16)
        g2_b = carve(24848, 512, BF16)
        Rlora = ares("lora")
        PT0 = 25104
        NPT = 12

        def pt(k, n=1):
            return carve(PT0 + 128 * k, 128 * n, F32)
        Rpt = [ares("pt%d" % k) for k in range(NPT)]
        lnx0_b = carve(26640, 512, BF16)
        lnx1_b = carve(26896, 512, BF16)
        Rlnx = ares("lnx")
        tb0 = junk_b[:, 0:128]
        tb1 = junk_b[:, 128:256]
        Rtb0 = Res("tb0")
        Rtb1 = Res("tb1")
        colx = sb("colx", [128, 64])
        Rcolx = Res("colx")
        tiny = sb("tiny", [128, 8])
        Rtiny = Res("tiny")
        BS = sb("BS", [128, NT, 8])
        RBS = Res("BS")
        wbf = [w[:].bitcast(F32) for w in wbuf]
        TW = wbf[0][:, 0:1024].rearrange("p (k n) -> p k n", k=8)
        TKK = wbf[0][:, 1024:2048].rearrange("p (k n) -> p k n", k=8)
        TNK = wbf[1][:, 0:1024].rearrange("p (k n) -> p k n", k=8)
        TKD = wbf[1][:, 1024:2048].rearrange("p (k n) -> p k n", k=8)
        TR2 = wbuf[2][:, 0:2048].rearrange("p (k n h) -> p k n h", k=8, h=2)
        tmpA_b = carve(10240, 512, BF16)
        RtAb = ares("tmpA_b")
        Sb16 = carve(10496, 512, BF16)
        RSb = ares("Sb16")
        CX = dict(cmu=0, nmu0=15, nmu1=30, omk1=45, keepL=49, keepR=54)

        def cx(name, c=0):
            o = CX[name] + c
            return colx[:, o:o + 1]

        def rwkv_setup():
            s.dma("pool", w2t_b, w2t_d, writes=[Rlora])
            s.dma("pool", a2t_b, a2t_d, writes=[Rlora])
            s.dma("pool", g2_b, g2_d, writes=[Rlora])
            s.dma("pool", lnx0_b, lnx_d[0, :].partition_broadcast(128), writes=[Rlnx])
            s.dma("pool", lnx1_b, lnx_d[1, :].partition_broadcast(128), writes=[Rlnx])
            mu0 = cp("mu0", 0, 15)
            mu1 = cp("mu1", 0, 15)
            s.op("dve", lambda e: e.tensor_tensor(colx[:, 0:15], mu0, mu1, ALU.add), reads=[Rcolp], writes=[Rcolx])
            s.op("dve", lambda e: e.tensor_scalar(colx[:, 0:15], colx[:, 0:15], -1.0, 1.0, ALU.mult, ALU.add),
                 reads=[Rcolx], writes=[Rcolx])
            s.op("dve", lambda e: e.tensor_scalar(colx[:, 15:30], mu0, -1.0, None, ALU.mult), reads=[Rcolp], writes=[Rcolx])
            s.op("dve", lambda e: e.tensor_scalar(colx[:, 30:45], mu1, -1.0, None, ALU.mult), reads=[Rcolp], writes=[Rcolx])
            s.op("dve", lambda e: e.tensor_scalar(colx[:, 45:49], cp("kvec1", 0, 4), -1.0, 1.0, ALU.mult, ALU.add),
                 reads=[Rcolp], writes=[Rcolx])
            s.op("dve", lambda e: e.tensor_scalar(colx[:, 49:54], rowp_sb[:, ROFF["bndL"]:ROFF["bndL"] + 5], -1.0, 1.0, ALU.mult, ALU.add),
                 reads=[Rrowp], writes=[Rcolx])
            s.op("dve", lambda e: e.tensor_scalar(colx[:, 54:59], rowp_sb[:, ROFF["bndR"]:ROFF["bndR"] + 5], -1.0, 1.0, ALU.mult, ALU.add),
                 reads=[Rrowp], writes=[Rcolx])
            s.op("dve", lambda e: e.memset(BS[:], 0.0), writes=[RBS])

        def shift(ch, ti, out, Rout):
            t0 = ti * 128
            f = fbT[:, ch, 1 + t0:1 + t0 + 128]
            fp = fbT[:, ch, t0:t0 + 128]
            fn = fbT[:, ch, 2 + t0:2 + t0 + 128]
            rr = [RfbT, Rcolp, Rcolx]
            s.op("dve", lambda e: e.tensor_scalar(out, f, cx("cmu", ch), None, ALU.mult), reads=rr, writes=[Rout])
            s.op("dve", lambda e: e.scalar_tensor_tensor(out, fp, cp("mu0", ch), out, ALU.mult, ALU.add),
                 reads=rr + [Rout], writes=[Rout])
            s.op("dve", lambda e: e.scalar_tensor_tensor(out, fn, cp("mu1", ch), out, ALU.mult, ALU.add),
                 reads=rr + [Rout], writes=[Rout])
            u = ti // 2
            if ti % 2 == 0:
                bl = rowp_sb[:, ROFF["bndL"] + u:ROFF["bndL"] + u + 1]
                s.op("dve", lambda e: e.tensor_tensor(tiny[:, 0:1], fbT[:, ch, t0:t0 + 1], bl, ALU.mult),
                     reads=[RfbT, Rrowp], writes=[Rtiny])
                s.op("dve", lambda e: e.scalar_tensor_tensor(out[:, 0:1], tiny[:, 0:1], cx("nmu0", ch), out[:, 0:1], ALU.mult, ALU.add),
                     reads=[Rtiny, Rcolx, Rout], writes=[Rout])
            else:
                br = rowp_sb[:, ROFF["bndR"] + u:ROFF["bndR"] + u + 1]
                s.op("dve", lambda e: e.tensor_tensor(tiny[:, 1:2], fbT[:, ch, 1 + t0 + 128:2 + t0 + 128], br, ALU.mult),
                     reads=[RfbT, Rrowp], writes=[Rtiny])
                s.op("dve", lambda e: e.scalar_tensor_tensor(out[:, 127:128], tiny[:, 1:2], cx("nmu1", ch), out[:, 127:128], ALU.mult, ALU.add),
                     reads=[Rtiny, Rcolx, Rout], writes=[Rout])

        def rwkv_prepass():
            for ti in range(NT):
                for c in range(4):
                    shift(8 + c, ti, pt(c), Rpt[c])
                    s.op("act", lambda e, c=c: e.copy(xn_b[:, c * 128:(c + 1) * 128], pt(c)), reads=[Rpt[c]], writes=[Rxn])
                b = bank()
                for c in range(4):
                    s.op("pe", lambda e, b=b, c=c: e.transpose(psb[b][:, c * 128:(c + 1) * 128], xn_b[:, c * 128:(c + 1) * 128], ident_b[:]),
                         reads=[Rxn, Rconst], writes=[Rps[b]])
                s.op("act", lambda e, b=b, ti=ti: e.copy(V_tm[:, ti, :], psb[b][:, 0:512]), reads=[Rps[b]], writes=[RVtm])

        def prep(d, ti):
            rev = (d == 1)

            def tab(T3, c):
                v = T3[:, d * 4 + c, :]
                return v[:, ::-1] if rev else v
            tabs_w = [Rw[0], Rw[1], Rw[2]]
            shift(12, ti, pt(0), Rpt[0])
            shift(13, ti, pt(1), Rpt[1])
            s.op("act", lambda e: e.activation(tb0, pt(0), AF.Tanh), reads=[Rpt[0]], writes=[Rtb0])
            s.op("act", lambda e: e.copy(tb1, pt(1)), reads=[Rpt[1]], writes=[Rtb1])
            lo, hi = 64 * d, 64 * d + 64
            for c in range(4):
                b = bank()
                s.op("pe", lambda e, b=b, c=c: e.matmul(ps[b][:, 0:128], w2t_b[lo:hi, c * 128:(c + 1) * 128], tb0[lo:hi, :], start=True, stop=True),
                     reads=[Rlora, Rtb0], writes=[Rps[b]])
                s.op("act", lambda e, b=b, c=c: e.activation(pt(2), ps[b][:, 0:128], AF.Sigmoid, bias=cp("w0_%d" % d, c)),
                     reads=[Rps[b], Rcolp], writes=[Rpt[2]])
                s.op("act", lambda e, c=c: e.activation(tab(TW, c), pt(2), AF.Exp, scale=-EXPM05),
                     reads=[Rpt[2]], writes=[Rw[0]])
                b = bank()
                s.op("pe", lambda e, b=b, c=c: e.matmul(ps[b][:, 0:128], a2t_b[lo:hi, c * 128:(c + 1) * 128], tb1[lo:hi, :], start=True, stop=True),
                     reads=[Rlora, Rtb1], writes=[Rps[b]])
                s.op("act", lambda e, b=b, c=c: e.activation(pt(3), ps[b][:, 0:128], AF.Sigmoid, bias=cp("a0_%d" % d, c)),
                     reads=[Rps[b], Rcolp], writes=[Rpt[3]])
                shift(4 + c, ti, pt(4), Rpt[4])
                s.op("dve", lambda e, c=c: e.tensor_scalar(pt(5), pt(4), cp("kvec0", c), None, ALU.mult),
                     reads=[Rpt[4], Rcolp], writes=[Rpt[5]])
                s.op("act", lambda e: e.activation(pt(6), pt(5), AF.Square), reads=[Rpt[5]], writes=[Rpt[6]])
                b = bank()
                s.op("pe", lambda e, b=b: e.matmul(ps[b][:, 0:128], bones_f[:], pt(6), start=True, stop=True),
                     reads=[Rconst, Rpt[6]], writes=[Rps[b]])
                s.op("dve", lambda e, b=b: e.tensor_scalar(pt(7), ps[b][:, 0:128], 1e-12, None, ALU.add),
                     reads=[Rps[b]], writes=[Rpt[7]])
                s.op("act", lambda e: e.activation(pt(7), pt(7), AF.Sqrt), reads=[Rpt[7]], writes=[Rpt[7]])
                s.op("dve", lambda e: e.reciprocal(pt(7), pt(7)), reads=[Rpt[7]], writes=[Rpt[7]])
                s.op("dve", lambda e: e.tensor_tensor(pt(8), pt(5), pt(7), ALU.mult), reads=[Rpt[5], Rpt[7]], writes=[Rpt[8]])
                s.op("act", lambda e, c=c: e.copy(tab(TKK, c), pt(8)), reads=[Rpt[8]], writes=[Rw[0]])
                s.op("dve", lambda e, c=c: e.scalar_tensor_tensor(tab(TNK, c), pt(8), -1.0, pt(3), ALU.mult, ALU.mult),
                     reads=[Rpt[8], Rpt[3]], writes=[Rw[1]])
                s.op("dve", lambda e, c=c: e.tensor_scalar(pt(9), pt(3), cp("kvec1", c), cx("omk1", c), ALU.mult, ALU.add),
                     reads=[Rpt[3], Rcolp, Rcolx], writes=[Rpt[9]])
                s.op("dve", lambda e: e.tensor_tensor(pt(9), pt(4), pt(9), ALU.mult), reads=[Rpt[4], Rpt[9]], writes=[Rpt[9]])
                s.op("act", lambda e, c=c: e.copy(tab(TKD, c), pt(9)), reads=[Rpt[9]], writes=[Rw[1]])
                shift(c, ti, pt(10), Rpt[10])
                for h2 in range(2):
                    o = TR2[:, d * 4 + c, :, h2]
                    if rev:
                        o = o[:, ::-1]
                    s.op("dve", lambda e, o=o, h2=h2: e.tensor_scalar(o, pt(10), halfsel_f[:, h2:h2 + 1], None, ALU.mult),
                         reads=[Rpt[10], Rconst], writes=[Rw[2]])
                s.op("dve", lambda e: e.tensor_tensor(pt(6), pt(10), pt(9), ALU.mult), reads=[Rpt[10], Rpt[9]], writes=[Rpt[6]])
                s.op("dve", lambda e, c=c: e.tensor_scalar(pt(6), pt(6), cp("kvec2", c), None, ALU.mult),
                     reads=[Rpt[6], Rcolp], writes=[Rpt[6]])
                b = bank()
                s.op("pe", lambda e, b=b: e.matmul(ps[b][:, 0:2], pt(6), halfsel_f[:, 0:2], start=True, stop=True),
                     reads=[Rpt[6], Rconst], writes=[Rps[b]])
                s.op("dve", lambda e, b=b, c=c, ti=ti: e.tensor_tensor(BS[:, ti, c * 2:c * 2 + 2], BS[:, ti, c * 2:c * 2 + 2], ps[b][:, 0:2], ALU.add),
                     reads=[Rps[b], RBS], writes=[RBS])

        yb = xn_b[:, 0:512]

        def finalize(ti, by):
            ys = pt(0, 4)
            Rys = [Rpt[0], Rpt[1], Rpt[2], Rpt[3]]
            t2 = pt(4, 4)
            Rt2 = [Rpt[4], Rpt[5], Rpt[6], Rpt[7]]
            ys3 = ys.rearrange("p (h i) -> p h i", h=8)
            t23 = t2.rearrange("p (h i) -> p h i", h=8)
            s.op("dve", lambda e: e.tensor_tensor(ys, YSb[:, ti, :], ps[by][:], ALU.add), reads=[RYS, Rps[by]], writes=Rys)
            s.op("dve", lambda e: e.reduce_sum(tiny[:, 0:8], ys3, axis=AX.X), reads=Rys, writes=[Rtiny])
            s.op("dve", lambda e: e.tensor_scalar(tiny[:, 0:8], tiny[:, 0:8], -1.0 / 64.0, None, ALU.mult), reads=[Rtiny], writes=[Rtiny])
            s.op("dve", lambda e: e.tensor_tensor(ys3, ys3, tiny[:, 0:8].unsqueeze(2).to_broadcast([128, 8, 64]), ALU.add),
                 reads=Rys + [Rtiny], writes=Rys)
            s.op("dve", lambda e: e.tensor_tensor(t2, ys, ys, ALU.mult), reads=Rys, writes=Rt2)
            s.op("dve", lambda e: e.reduce_sum(stat[:, 40:48], t23, axis=AX.X), reads=Rt2, writes=[Rstat])
            s.op("dve", lambda e: e.tensor_scalar(stat[:, 40:48], stat[:, 40:48], 1.0 / 64.0, 64e-5, ALU.mult, ALU.add),
                 reads=[Rstat], writes=[Rstat])
            s.op("act", lambda e: e.activation(stat[:, 40:48], stat[:, 40:48], AF.Sqrt), reads=[Rstat], writes=[Rstat])
            s.op("dve", lambda e: e.reciprocal(stat[:, 48:56], stat[:, 40:48]), reads=[Rstat], writes=[Rstat])
            s.op("dve", lambda e: e.tensor_tensor(ys3, ys3, stat[:, 48:56].unsqueeze(2).to_broadcast([128, 8, 64]), ALU.mult),
                 reads=Rys + [Rstat], writes=Rys)
            s.op("dve", lambda e: e.tensor_tensor(ys, ys, lnx0_b, ALU.mult), reads=Rys + [Rlnx], writes=Rys)
            s.op("dve", lambda e: e.tensor_tensor(ys, ys, lnx1_b, ALU.add), reads=Rys + [Rlnx], writes=Rys)
            s.op("dve", lambda e: e.tensor_tensor(t23, V_tm[:, ti, :].rearrange("p (h i) -> p h i", h=8),
                                                  BS[:, ti, :].unsqueeze(2).to_broadcast([128, 8, 64]), ALU.mult),
                 reads=[RVtm, RBS], writes=Rt2)
            s.op("dve", lambda e: e.tensor_tensor(ys, ys, t2, ALU.add), reads=Rys + Rt2, writes=Rys)
            shift(14, ti, pt(8), Rpt[8])
            s.op("act", lambda e: e.activation(tb0, pt(8), AF.Sigmoid), reads=[Rpt[8]], writes=[Rtb0])
            bg = bank()
            s.op("pe", lambda e, bg=bg: e.matmul(ps[bg][:], tb0, g2_b, start=True, stop=True),
                 reads=[Rtb0, Rlora], writes=[Rps[bg]])
            s.op("dve", lambda e, bg=bg: e.tensor_tensor(yb, ys, ps[bg][:], ALU.mult), reads=Rys + [Rps[bg]], writes=[Rxn])
            bt = bank()
            for c in range(4):
                s.op("pe", lambda e, bt=bt, c=c: e.transpose(psb[bt][:, c * 128:(c + 1) * 128], yb[:, c * 128:(c + 1) * 128], ident_b[:]),
                     reads=[Rxn, Rconst], writes=[Rps[bt]])
            s.op("act", lambda e, bt=bt, ti=ti: e.copy(yT[:, 4:8, ti * 128:(ti + 1) * 128],
                                                        psb[bt][:, 0:512].rearrange("p (c t) -> p c t", c=4)),
                 reads=[Rps[bt]], writes=[RyT])

        def rwkv_scan(nrounds=NT):
            S8 = [x_.rearrange("p (k i) -> p k i", k=8) for x_ in Sst]
            tA8 = tmpA.rearrange("p (k i) -> p k i", k=8)
            tB8 = tmpB.rearrange("p (k i) -> p k i", k=8)

            def bc(T3, n):
                return T3[:, :, n].unsqueeze(2).to_broadcast([128, 8, 64])
            for r in range(nrounds):
                tf, tbk = r, NT - 1 - r
                prep(0, tf)
                prep(1, tbk)
                if tf % 2 == 0:
                    u = tf // 2
                    if u == 0:
                        s.dma("sp", Sst[0][:, 0:256], initS[0], writes=[RS[0]])
                    else:
                        s.op("dve", lambda e, u=u: e.tensor_scalar(Sst[0][:, 0:256], Sst[0][:, 0:256], cx("keepL", u), None, ALU.mult),
                             reads=[RS[0], Rcolx], writes=[RS[0]])
                if tbk % 2 == 1:
                    u = tbk // 2
                    if u == 4:
                        s.op("dve", lambda e: e.memset(Sst[0][:, 256:512], 0.0), writes=[RS[0]])
                    elif u == 3:
                        s.dma("sp", tmpB[:, 0:256], initS[1], writes=[RtB])
                        s.op("dve", lambda e, u=u: e.scalar_tensor_tensor(Sst[0][:, 256:512], Sst[0][:, 256:512], cx("keepR", u),
                                                                          tmpB[:, 0:256], ALU.mult, ALU.add),
                             reads=[RS[0], Rcolx, RtB], writes=[RS[0]])
                    else:
                        s.op("dve", lambda e, u=u: e.tensor_scalar(Sst[0][:, 256:512], Sst[0][:, 256:512], cx("keepR", u), None, ALU.mult),
                             reads=[RS[0], Rcolx], writes=[RS[0]])
                by_ = None
                pending = []

                def flush():
                    for f_ in pending:
                        f_()
                    del pending[:]
                for n in range(128):
                    ci, ni = n % 2, (n + 1) % 2
                    Sc, Sn = S8[ci], S8[ni]
                    if n % 32 == 0:
                        by_ = bank()
                        reserved.add(by_)
                    tAb8 = tmpA_b.rearrange("p (k i) -> p k i", k=8)
                    s.op("dve", lambda e, n=n, Sc=Sc, tAb8=tAb8: e.tensor_tensor(tAb8, Sc, bc(TKK, n), ALU.mult),
                         reads=[RS[ci], Rw[0]], writes=[RtAb])
                    s.op("dve", lambda e, n=n, Sc=Sc, Sn=Sn: e.tensor_tensor(Sn, Sc, bc(TW, n), ALU.mult),
                         reads=[RS[ci], Rw[0]], writes=[RS[ni]])
                    bv = bank()
                    for d in range(2):
                        tile_d = tf if d == 0 else tbk
                        row = n if d == 0 else 127 - n
                        for h2 in range(2):
                            s.op("pe", lambda e, bv=bv, d=d, h2=h2, tile_d=tile_d, row=row: e.matmul(
                                ps[bv][64 * h2:64 * h2 + 64, d * 256:(d + 1) * 256].rearrange("p (c i) -> p c i", c=4),
                                ident_b[:, row:row + 1].to_broadcast([128, 64]),
                                V_tm[:, tile_d, :].rearrange("p (c h i) -> p c h i", c=4, h=2)[:, :, h2, :],
                                start=True, stop=True),
                                reads=[RVtm, Rconst], writes=[Rps[bv]])
                    bs_ = bank()
                    s.op("pe", lambda e, bs_=bs_: e.matmul(ps[bs_][:], bones_b[:], tmpA_b, start=True, stop=True),
                         reads=[RtAb, Rconst], writes=[Rps[bs_]])
                    flush()
                    s.op("dve", lambda e, n=n, bv=bv: e.tensor_tensor(tB8, ps[bv][:].rearrange("p (k i) -> p k i", k=8), bc(TKD, n), ALU.mult),
                         reads=[Rps[bv], Rw[1]], writes=[RtB])
                    s.op("dve", lambda e, ni=ni: e.tensor_tensor(Sst[ni], Sst[ni], tmpB, ALU.add),
                         reads=[RS[ni], RtB], writes=[RS[ni]])
                    s.op("dve", lambda e, n=n, bs_=bs_: e.tensor_tensor(tA8, ps[bs_][:].rearrange("p (k i) -> p k i", k=8), bc(TNK, n), ALU.mult),
                         reads=[Rps[bs_], Rw[1]], writes=[RtA])
                    s.op("dve", lambda e, ni=ni: e.tensor_tensor(Sst[ni], Sst[ni], tmpA, ALU.add),
                         reads=[RS[ni], RtA], writes=[RS[ni]])
                    s.op("act", lambda e, ni=ni: e.copy(Sb16, Sst[ni]), reads=[RS[ni]], writes=[RSb])
                    nl = n % 32

                    def ymm(by_=by_, nl=nl, n=n):
                        for dc in range(8):
                            s.op("pe", lambda e, dc=dc: e.matmul(
                                ps[by_][0:64, nl * 16 + dc * 2:nl * 16 + dc * 2 + 2], Sb16[:, dc * 64:(dc + 1) * 64], TR2[:, dc, n, :],
                                start=True, stop=True),
                                reads=[RSb, Rw[2]], writes=[Rps[by_]])
                    pending.append(ymm)
                    if nl == 31:
                        flush()
                        n0 = n - 31
                        src = ps[by_][0:64, :].rearrange("p (n k) -> p n k", k=16)
                        s.op("act", lambda e, src=src, n0=n0: e.copy(Ystage[:, n0:n0 + 32, 0:8], src[:, :, 0:8]),
                             reads=[Rps[by_]], writes=[RYst])
                        s.op("act", lambda e, src=src, n0=n0: e.copy(Ystage[:, 96 - n0:128 - n0, 8:16][:, ::-1, :], src[:, :, 8:16]),
                             reads=[Rps[by_]], writes=[RYst])
                        reserved.discard(by_)
                if tf % 2 == 1:
                    s.dma("sp", s_out[tf // 2, 0], Sst[0][:, 0:256], reads=[RS[0]])
                if tbk % 2 == 0:
                    s.dma("sp", s_out[tbk // 2, 1], Sst[0][:, 256:512], reads=[RS[0]])
                for d in range(2):
                    tile_d = tf if d == 0 else tbk
                    byy = bank()
                    for k8 in range(8):
                        s.op("pe", lambda e, byy=byy, k8=k8, d=d: e.matmul(
                            ps[byy][:, k8 * 64:(k8 + 1) * 64], Ystage[:, :, d * 8 + k8], ident_f[0:64, 0:64], start=True, stop=True),
                            reads=[RYst, Rconst], writes=[Rps[byy]])
                    first = (r < 5)
                    if first:
                        s.op("act", lambda e, byy=byy, tile_d=tile_d: e.copy(YSb[:, tile_d, :], ps[byy][:]),
                             reads=[Rps[byy]], writes=[RYS])
                    else:
                        finalize(tile_d, byy)


        def wbw(i, off_w, nelem):
            return wbuf[i][:, 2 * off_w:2 * off_w + nelem]
        Ast = carve(13200, 512, F32)
        RAst = ares("Ast")
        Abf = carve(13712, 512, BF16)
        RAbf = ares("Abf")
        WS = []
        d_ = {}
        for k_, nm in enumerate(["Lt0", "Lt1", "L0", "L1", "Z0"]):
            d_[nm] = carve(10240 + 512 * k_, 512, F32)
        d_["MT"] = carve(12800, 256, BF16)
        d_["RT"] = carve(12928, 512, BF16)
        d_["Z1"] = wbuf[0][:, 0:1024].bitcast(F32)
        for k_, nm in enumerate(["Lakt", "Mrbt", "Mrkt", "BKtm", "Zfb"]):
            d_[nm] = wbw(0, 512 + 256 * k_, 512)
        for nm in ["Lt0", "Lt1", "L0", "L1", "Z0", "Z1", "Lakt", "Mrbt", "Mrkt", "BKtm", "Zfb", "MT", "RT"]:
            d_["R" + nm] = ares("ws_" + nm)
        WS = [d_, d_]
        masks = wbw(1, 896, 4 * 4 * 128).rearrange("p (m q t) -> p m q t", m=4, q=4)
        Rmask = ares("masks")
        TBL = []
        for par in range(2):
            d_ = {}
            for k_, nm in enumerate(["RH", "AH", "BH", "KH"]):
                d_[nm] = wbw(2, par * 1024 + 256 * k_, 512).rearrange("p (c t) -> p c t", c=4)
                d_["R" + nm] = ares("tb%d_%s" % (par, nm))
            TBL.append(d_)
        pcs = sb("pcs", [128, 2, 4])
        Rpcs = [Res("pcs0"), Res("pcs1")]

        def prep_chunk(d, ti, par):
            TB = TBL[par]
            shift(12, ti, pt(0), Rpt[0])
            shift(13, ti, pt(1), Rpt[1])
            s.op("act", lambda e: e.activation(tb0, pt(0), AF.Tanh), reads=[Rpt[0]], writes=[Rtb0])
            s.op("act", lambda e: e.copy(tb1, pt(1)), reads=[Rpt[1]], writes=[Rtb1])
            lo, hi = 64 * d, 64 * d + 64
            for c in range(4):
                b = bank()
                s.op("pe", lambda e, b=b, c=c: e.matmul(ps[b][:, 0:128], w2t_b[lo:hi, c * 128:(c + 1) * 128], tb0[lo:hi, :], start=True, stop=True),
                     reads=[Rlora, Rtb0], writes=[Rps[b]])
                s.op("act", lambda e, b=b, c=c: e.activation(pt(2), ps[b][:, 0:128], AF.Sigmoid, bias=cp("w0_%d" % d, c)),
                     reads=[Rps[b], Rcolp], writes=[Rpt[2]])
                s.op("dve", lambda e: e.tensor_scalar(pt(2), pt(2), -EXPM05, None, ALU.mult), reads=[Rpt[2]], writes=[Rpt[2]])
                if d == 0:
                    s.op("dve", lambda e: e.tensor_tensor_scan(pt(0), pt(11), pt(2), 0.0, ALU.mult, ALU.add),
                         reads=[Rpt[2], Rpt[11]], writes=[Rpt[0]])
                else:
                    s.op("dve", lambda e: e.tensor_tensor_scan(pt(0)[:, ::-1], pt(11), pt(2)[:, ::-1], 0.0, ALU.mult, ALU.add),
                         reads=[Rpt[2], Rpt[11]], writes=[Rpt[0]])
                s.op("act", lambda e: e.activation(pt(1), pt(0), AF.Exp), reads=[Rpt[0]], writes=[Rpt[1]])
                s.op("dve", lambda e: e.tensor_tensor(pt(2), pt(0), pt(2), ALU.subtract), reads=[Rpt[0], Rpt[2]], writes=[Rpt[2]])
                s.op("act", lambda e: e.activation(pt(2), pt(2), AF.Exp), reads=[Rpt[2]], writes=[Rpt[2]])
                s.op("act", lambda e: e.activation(pt(0), pt(0), AF.Exp, scale=-1.0), reads=[Rpt[0]], writes=[Rpt[0]])
                pcol = 127 if d == 0 else 0
                s.op("dve", lambda e, c=c, pcol=pcol: e.tensor_copy(pcs[:, par, c:c + 1], pt(1)[:, pcol:pcol + 1]),
                     reads=[Rpt[1]], writes=[Rpcs[par]])
                b = bank()
                s.op("pe", lambda e, b=b, c=c: e.matmul(ps[b][:, 0:128], a2t_b[lo:hi, c * 128:(c + 1) * 128], tb1[lo:hi, :], start=True, stop=True),
                     reads=[Rlora, Rtb1], writes=[Rps[b]])
                s.op("act", lambda e, b=b, c=c: e.activation(pt(3), ps[b][:, 0:128], AF.Sigmoid, bias=cp("a0_%d" % d, c)),
                     reads=[Rps[b], Rcolp], writes=[Rpt[3]])
                shift(4 + c, ti, pt(4), Rpt[4])
                s.op("dve", lambda e, c=c: e.tensor_scalar(pt(5), pt(4), cp("kvec0", c), None, ALU.mult),
                     reads=[Rpt[4], Rcolp], writes=[Rpt[5]])
                s.op("act", lambda e: e.activation(pt(6), pt(5), AF.Square), reads=[Rpt[5]], writes=[Rpt[6]])
                b = bank()
                s.op("pe", lambda e, b=b: e.matmul(ps[b][:, 0:128], bones_f[:], pt(6), start=True, stop=True),
                     reads=[Rconst, Rpt[6]], writes=[Rps[b]])
                s.op("dve", lambda e, b=b: e.tensor_scalar(pt(7), ps[b][:, 0:128], 1e-12, None, ALU.add),
                     reads=[Rps[b]], writes=[Rpt[7]])
                s.op("act", lambda e: e.activation(pt(7), pt(7), AF.Sqrt), reads=[Rpt[7]], writes=[Rpt[7]])
                s.op("dve", lambda e: e.reciprocal(pt(7), pt(7)), reads=[Rpt[7]], writes=[Rpt[7]])
                s.op("dve", lambda e: e.tensor_tensor(pt(8), pt(5), pt(7), ALU.mult), reads=[Rpt[5], Rpt[7]], writes=[Rpt[8]])
                s.op("dve", lambda e, c=c: e.scalar_tensor_tensor(TB["AH"][:, c, :], pt(8), -1.0, pt(2), ALU.mult, ALU.mult),
                     reads=[Rpt[8], Rpt[2]], writes=[TB["RAH"]])
                s.op("dve", lambda e: e.tensor_tensor(pt(6), pt(8), pt(3), ALU.mult), reads=[Rpt[8], Rpt[3]], writes=[Rpt[6]])
                s.op("dve", lambda e, c=c: e.tensor_tensor(TB["BH"][:, c, :], pt(6), pt(0), ALU.mult),
                     reads=[Rpt[6], Rpt[0]], writes=[TB["RBH"]])
                s.op("dve", lambda e, c=c: e.tensor_scalar(pt(9), pt(3), cp("kvec1", c), cx("omk1", c), ALU.mult, ALU.add),
                     reads=[Rpt[3], Rcolp, Rcolx], writes=[Rpt[9]])
                s.op("dve", lambda e: e.tensor_tensor(pt(9), pt(4), pt(9), ALU.mult), reads=[Rpt[4], Rpt[9]], writes=[Rpt[9]])
                s.op("dve", lambda e, c=c: e.tensor_tensor(TB["KH"][:, c, :], pt(9), pt(0), ALU.mult),
                     reads=[Rpt[9], Rpt[0]], writes=[TB["RKH"]])
                shift(c, ti, pt(10), Rpt[10])
                s.op("dve", lambda e, c=c: e.tensor_tensor(TB["RH"][:, c, :], pt(10), pt(1), ALU.mult),
                     reads=[Rpt[10], Rpt[1]], writes=[TB["RRH"]])
                s.op("dve", lambda e: e.tensor_tensor(pt(6), pt(10), pt(9), ALU.mult), reads=[Rpt[10], Rpt[9]], writes=[Rpt[6]])
                s.op("dve", lambda e, c=c: e.tensor_scalar(pt(6), pt(6), cp("kvec2", c), None, ALU.mult),
                     reads=[Rpt[6], Rcolp], writes=[Rpt[6]])
                b = bank()
                s.op("pe", lambda e, b=b: e.matmul(ps[b][:, 0:2], pt(6), halfsel_f[:, 0:2], start=True, stop=True),
                     reads=[Rpt[6], Rconst], writes=[Rps[b]])
                s.op("dve", lambda e, b=b, c=c, ti=ti: e.tensor_tensor(BS[:, ti, c * 2:c * 2 + 2], BS[:, ti, c * 2:c * 2 + 2], ps[b][:, 0:2], ALU.add),
                     reads=[Rps[b], RBS], writes=[RBS])

        def rwkv_chunked(ntiles=NT):
            cut = int(os.environ.get("CUT", 99))
            for m in range(4):
                for q in range(4):
                    s.dma("pool", masks[:, m, q, :], cmat[4 + m], writes=[Rmask])
            s.op("dve", lambda e: e.memset(pt(11), 1.0), writes=[Rpt[11]])
            MSK = {0: (0, 1, 2), 1: (1, 0, 3)}
            gctr = [0]
            pctr2 = [0]
            for d in range(2):
                order = list(range(NT)) if d == 0 else list(range(NT - 1, -1, -1))
                order = order[:ntiles]
                Ad = Ast[:, d * 256:(d + 1) * 256]
                Ad3 = Ad.rearrange("p (c i) -> p c i", c=4)
                Abd = Abf[:, d * 256:(d + 1) * 256]
                Abd3 = Abd.rearrange("p (c i) -> p c i", c=4)
                mTs, mS, mTi = MSK[d]
                def tile_body(ti, d=d, Ad=Ad, Ad3=Ad3, Abd=Abd, Abd3=Abd3, mTs=mTs, mS=mS, mTi=mTi):
                    par = pctr2[0] % 2
                    pctr2[0] += 1
                    TB = TBL[par]
                    prep_chunk(d, ti, par)
                    if os.environ.get("DBG") == "1" and d == 0 and ti == 0:
                        s.dma("sp", s_out[1, 0][:, 0:128], pt(0), reads=[Rpt[0]])
                        s.dma("sp", s_out[1, 0][:, 128:256], pt(1), reads=[Rpt[1]])
                        s.dma("sp", s_out[1, 1][:, 0:128], pt(2), reads=[Rpt[2]])
                        s.dma("sp", s_out[1, 1][:, 128:256], pt(11), reads=[Rpt[11]])
                        s.dma("sp", s_out[2, 0][:, 0:8], pcs[:, :, :].rearrange("p a b -> p (a b)"), reads=Rpcs)
                    u = ti // 2
                    isstart = (ti % 2 == 0) if d == 0 else (ti % 2 == 1)
                    if isstart:
                        if d == 0:
                            if u == 0:
                                s.dma("sp", Ad, initS[0], writes=[RAst])
                            else:
                                s.op("dve", lambda e, u=u: e.tensor_scalar(Ad, Ad, cx("keepL", u), None, ALU.mult),
                                     reads=[RAst, Rcolx], writes=[RAst])
                        else:
                            if u == 4:
                                s.op("dve", lambda e: e.memset(Ad, 0.0), writes=[RAst])
                            elif u == 3:
                                s.dma("sp", tmpB[:, 0:256], initS[1], writes=[RtB])
                                s.op("dve", lambda e, u=u: e.scalar_tensor_tensor(Ad, Ad, cx("keepR", u), tmpB[:, 0:256], ALU.mult, ALU.add),
                                     reads=[RAst, Rcolx, RtB], writes=[RAst])
                            else:
                                s.op("dve", lambda e, u=u: e.tensor_scalar(Ad, Ad, cx("keepR", u), None, ALU.mult),
                                     reads=[RAst, Rcolx], writes=[RAst])
                        s.op("act", lambda e: e.copy(Abd, Ad), reads=[RAst], writes=[RAbf])
                    if cut < 2:
                        return
                    fy = bank()
                    reserved.add(fy)
                    first_fy = [True]
                    def group_body(g):
                        W = WS[gctr[0] % 2]
                        gctr[0] += 1
                        heads = [(c_, g) for c_ in range(4)]

                        def fm(T3, c, h2):
                            return T3[64 * h2:64 * h2 + 64, c, :]

                        def q4(tl):
                            return tl.rearrange("p (q t) -> p q t", q=4)
                        specs = [("Lt0", "BH", "AH", mTs), ("L0", "AH", "BH", mS), ("Lakt", "KH", "AH", mTs),
                                 ("Mrbt", "BH", "RH", mTi), ("Mrkt", "KH", "RH", mTi)]
                        lim = int(os.environ.get("LIM", 99))
                        for (dst, la, rb, mk) in specs[:lim]:
                            b = bank()
                            for q, (c, h2) in enumerate(heads[:int(os.environ.get("LIMH", 4))]):
                                s.op("pe", lambda e, b=b, q=q, c=c, h2=h2, la=la, rb=rb: e.matmul(
                                    ps[b][:, q * 128:(q + 1) * 128], fm(TB[la], c, h2), fm(TB[rb], c, h2), start=True, stop=True),
                                    reads=[TB["R" + la], TB["R" + rb]], writes=[Rps[b]])
                            if os.environ.get("NOEV") == "1":
                                continue
                            if os.environ.get("NOEV") == "2":
                                s.op("dve", lambda e, b=b, dst=dst, mk=mk: e.tensor_copy(W[dst], ps[b][:]),
                                     reads=[Rps[b], Rmask], writes=[W["R" + dst]])
                                continue
                            s.op("dve", lambda e, b=b, dst=dst, mk=mk: e.tensor_tensor(
                                q4(W[dst]), ps[b][:].rearrange("p (q t) -> p q t", q=4), masks[:, mk, :, :], ALU.mult),
                                reads=[Rps[b], Rmask], writes=[W["R" + dst]])
                        if os.environ.get("DBG") == "3" and d == 0 and ti == 0 and g == 0:
                            s.dma("pool", s_out[1, 0][:, 0:128], TB["AH"][:, 0, :], reads=[TB["RAH"]])
                            s.dma("pool", s_out[1, 0][:, 128:256], TB["BH"][:, 0, :], reads=[TB["RBH"]])
                            s.dma("pool", s_out[1, 1][:, 0:128], TB["KH"][:, 0, :], reads=[TB["RKH"]])
                            s.dma("pool", s_out[1, 1][:, 128:256], TB["RH"][:, 0, :], reads=[TB["RRH"]])
                            s.dma("pool", s_out[2, 0][:, 0:128], q4(W["Lt0"])[:, 0, :], reads=[W["RLt0"]])
                            s.dma("pool", s_out[2, 0][:, 128:256], q4(W["L0"])[:, 0, :], reads=[W["RL0"]])
                            s.dma("pool", s_out[3, 0][:, 0:64], V_tm[:, 0, 0:64], reads=[RVtm])
                        if cut < 3:
                            return
                        b = bank()
                        for q, (c, h2) in enumerate(heads):
                            idn = ident_b[64 * h2:64 * h2 + 64, 64 * h2:64 * h2 + 64]
                            for k_, nm in enumerate(["AH", "BH", "KH"]):
                                s.op("pe", lambda e, b=b, q=q, c=c, h2=h2, nm=nm, k_=k_, idn=idn: e.transpose(
                                    psb[b][:, k_ * 256 + q * 64:k_ * 256 + q * 64 + 64], fm(TB[nm], c, h2), idn),
                                    reads=[TB["R" + nm], Rconst], writes=[Rps[b]])
                        Z0q = q4(W["Z0"])
                        BKq = q4(W["BKtm"])
                        s.op("act", lambda e, b=b, Z0q=Z0q: e.copy(Z0q[:, :, 0:64], psb[b][:, 0:256].rearrange("p (q j) -> p q j", q=4)),
                             reads=[Rps[b]], writes=[W["RZ0"]])
                        s.op("act", lambda e, b=b, BKq=BKq: e.copy(BKq[:, :, 0:64], psb[b][:, 256:512].rearrange("p (q j) -> p q j", q=4)),
                             reads=[Rps[b]], writes=[W["RBKtm"]])
                        s.op("act", lambda e, b=b, BKq=BKq: e.copy(BKq[:, :, 64:128], psb[b][:, 512:768].rearrange("p (q j) -> p q j", q=4)),
                             reads=[Rps[b]], writes=[W["RBKtm"]])

                        def Vq(c, h2):
                            return V_tm[:, ti, (c * 2 + h2) * 64:(c * 2 + h2) * 64 + 64]
                        if cut < 4:
                            return
                        b = bank()
                        Lakq = q4(W["Lakt"])
                        for q, (c, h2) in enumerate(heads):
                            s.op("pe", lambda e, b=b, q=q, c=c, h2=h2: e.matmul(
                                ps[b][:, q * 64:(q + 1) * 64], Lakq[:, q, :], Vq(c, h2), start=True, stop=True),
                                reads=[W["RLakt"], RVtm], writes=[Rps[b]])
                        s.op("act", lambda e, b=b, Z0q=Z0q: e.copy(Z0q[:, :, 64:128], ps[b][:, 0:256].rearrange("p (q j) -> p q j", q=4)),
                             reads=[Rps[b]], writes=[W["RZ0"]])
                        if os.environ.get("DBG") == "3" and d == 0 and ti == 0 and g == 0:
                            s.dma("pool", s_out[2, 1][:, 0:128], Z0q[:, 0, :], reads=[W["RZ0"]])
                        if cut < 5:
                            return
                        zc, lc = 0, 0
                        for k in range(7):
                            Zc, Zn = W["Z%d" % zc], W["Z%d" % (1 - zc)]
                            RZc, RZn = W["RZ%d" % zc], W["RZ%d" % (1 - zc)]
                            Ltc, Lc = W["Lt%d" % lc], W["L%d" % lc]
                            RLtc, RLc = W["RLt%d" % lc], W["RL%d" % lc]
                            b = bank()
                            for q in range(4):
                                s.op("pe", lambda e, b=b, q=q, Zc=Zc: e.matmul(
                                    ps[b][:, q * 128:(q + 1) * 128], ident_f[:], q4(Zc)[:, q, :], start=True, stop=False),
                                    reads=[RZc, Rconst], writes=[Rps[b]])
                                s.op("pe", lambda e, b=b, q=q, Zc=Zc, Ltc=Ltc: e.matmul(
                                    ps[b][:, q * 128:(q + 1) * 128], q4(Ltc)[:, q, :], q4(Zc)[:, q, :], start=False, stop=True,
                                    skip_group_check=True),
                                    reads=[RZc, RLtc], writes=[Rps[b]])
                            s.op("act", lambda e, b=b, Zn=Zn: e.copy(Zn, ps[b][:]), reads=[Rps[b]], writes=[RZn])
                            if k < 6:
                                Ltn, Ln = W["Lt%d" % (1 - lc)], W["L%d" % (1 - lc)]
                                RLtn, RLn = W["RLt%d" % (1 - lc)], W["RL%d" % (1 - lc)]
                                b2 = bank()
                                for q in range(4):
                                    s.op("pe", lambda e, b2=b2, q=q, Lc=Lc, Ltc=Ltc: e.matmul(
                                        ps[b2][:, q * 128:(q + 1) * 128], q4(Lc)[:, q, :], q4(Ltc)[:, q, :], start=True, stop=True),
                                        reads=[RLc, RLtc], writes=[Rps[b2]])
                                s.op("dve", lambda e, b2=b2, Ltn=Ltn: e.tensor_copy(Ltn, ps[b2][:]), reads=[Rps[b2]], writes=[RLtn])
                                if k < 5:
                                    b3 = bank()
                                    for q in range(4):
                                        s.op("pe", lambda e, b3=b3, q=q, Lc=Lc, Ltc=Ltc: e.matmul(
                                            ps[b3][:, q * 128:(q + 1) * 128], q4(Ltc)[:, q, :], q4(Lc)[:, q, :], start=True, stop=True),
                                            reads=[RLc, RLtc], writes=[Rps[b3]])
                                    s.op("dve", lambda e, b3=b3, Ln=Ln: e.tensor_copy(Ln, ps[b3][:]), reads=[Rps[b3]], writes=[RLn])
                                lc = 1 - lc
                            zc = 1 - zc
                        s.op("act", lambda e, zc=zc: e.copy(W["Zfb"], W["Z%d" % zc]), reads=[W["RZ%d" % zc]], writes=[W["RZfb"]])
                        Zf = q4(W["Zfb"])
                        RZf = W["RZfb"]
                        Mrbq = q4(W["Mrbt"])
                        Mrkq = q4(W["Mrkt"])
                        if cut < 6:
                            return
                        bM = bank()
                        bN = bank()
                        bR = bank()
                        for q, (c, h2) in enumerate(heads):
                            cl_ = c
                            prt = slice(64 * h2, 64 * h2 + 64)
                            s.op("pe", lambda e, q=q, cl_=cl_, prt=prt: e.matmul(
                                ps[bM][prt, cl_ * 64:(cl_ + 1) * 64], Zf[:, q, 0:64], BKq[:, q, 0:64],
                                start=(q == 0), stop=False, skip_group_check=True),
                                reads=[RZf, W["RBKtm"]], writes=[Rps[bM]])
                            s.op("pe", lambda e, q=q, cl_=cl_, prt=prt: e.matmul(
                                ps[bM][prt, cl_ * 64:(cl_ + 1) * 64], ident_b[0:64, 0:64], ident_b[0:64, 0:64],
                                start=False, stop=True, skip_group_check=True),
                                reads=[Rconst], writes=[Rps[bM]])
                            s.op("pe", lambda e, q=q, cl_=cl_, prt=prt: e.matmul(
                                ps[bN][prt, cl_ * 64:(cl_ + 1) * 64], BKq[:, q, 0:64], Zf[:, q, 64:128],
                                start=(q == 0), stop=False, skip_group_check=True),
                                reads=[RZf, W["RBKtm"]], writes=[Rps[bN]])
                            s.op("pe", lambda e, q=q, cl_=cl_, prt=prt, c=c, h2=h2: e.matmul(
                                ps[bN][prt, cl_ * 64:(cl_ + 1) * 64], BKq[:, q, 64:128], Vq(c, h2),
                                start=False, stop=False, skip_group_check=True),
                                reads=[W["RBKtm"], RVtm], writes=[Rps[bN]])
                            s.op("pe", lambda e, q=q, cl_=cl_, prt=prt: e.matmul(
                                ps[bR][prt, cl_ * 128:(cl_ + 1) * 128], Zf[:, q, 0:64], Mrbq[:, q, :],
                                start=(q == 0), stop=False, skip_group_check=True),
                                reads=[RZf, W["RMrbt"]], writes=[Rps[bR]])
                            s.op("pe", lambda e, q=q, cl_=cl_, prt=prt, c=c, h2=h2: e.matmul(
                                ps[bR][prt, cl_ * 128:(cl_ + 1) * 128], ident_b[prt, prt], fm(TB["RH"], c, h2),
                                start=False, stop=True, skip_group_check=True),
                                reads=[Rconst, TB["RRH"]], writes=[Rps[bR]])
                            st_ = first_fy[0]
                            first_fy[0] = False
                            hh = c * 2 + h2
                            s.op("pe", lambda e, q=q, hh=hh, st_=st_: e.matmul(
                                ps[fy][:, hh * 64:(hh + 1) * 64], Mrbq[:, q, :], Zf[:, q, 64:128],
                                start=st_, stop=False, skip_group_check=True),
                                reads=[RZf, W["RMrbt"]], writes=[Rps[fy]])
                            s.op("pe", lambda e, q=q, hh=hh, c=c, h2=h2: e.matmul(
                                ps[fy][:, hh * 64:(hh + 1) * 64], Mrkq[:, q, :], Vq(c, h2),
                                start=False, stop=False, skip_group_check=True),
                                reads=[W["RMrkt"], RVtm], writes=[Rps[fy]])
                        MTv = W["MT"].rearrange("p (c j) -> p c j", c=4)
                        RTv = W["RT"].rearrange("p (c t) -> p c t", c=4)
                        pg = slice(64 * g, 64 * g + 64)
                        s.op("act", lambda e, MTv=MTv, pg=pg: e.copy(MTv[pg], ps[bM][pg, 0:256].rearrange("p (c j) -> p c j", c=4)),
                             reads=[Rps[bM]], writes=[W["RMT"]])
                        s.op("act", lambda e, RTv=RTv, pg=pg: e.copy(RTv[pg], ps[bR][pg, 0:512].rearrange("p (c t) -> p c t", c=4)),
                             reads=[Rps[bR]], writes=[W["RRT"]])
                        for q, (c, h2) in enumerate(heads):
                            cl_ = c
                            prt = slice(64 * h2, 64 * h2 + 64)
                            hh = c * 2 + h2
                            s.op("pe", lambda e, cl_=cl_, prt=prt, hh=hh, c=c, RTv=RTv: e.matmul(
                                ps[fy][:, hh * 64:(hh + 1) * 64], RTv[prt, cl_, :], Abd3[prt, c, :],
                                start=False, stop=True, skip_group_check=True),
                                reads=[W["RRT"], RAbf], writes=[Rps[fy]])
                            s.op("pe", lambda e, cl_=cl_, prt=prt, c=c, MTv=MTv: e.matmul(
                                ps[bN][prt, cl_ * 64:(cl_ + 1) * 64], MTv[prt, cl_, :], Abd3[prt, c, :],
                                start=False, stop=True, skip_group_check=True),
                                reads=[W["RMT"], RAbf], writes=[Rps[bN]])
                        if os.environ.get("DBG") == "3" and d == 0 and ti == 0 and g == 0:
                            s.op("act", lambda e: e.copy(stage_f[:, 0, 0:256], ps[bN][:, 0:256]), reads=[Rps[bN]], writes=[Rstage[0]])
                            s.dma("sp", s_out[3, 1], stage_f[:, 0, 0:256], reads=[Rstage[0]])
                            s.dma("pool", s_out[2, 1][:, 128:256], Zf[:, 0, :], reads=[RZf])
                            s.dma("pool", s_out[4, 1], W["MT"], reads=[W["RMT"]])
                        s.op("dve", lambda e, pg=pg: e.tensor_tensor(
                            Ad3[pg, :, :], ps[bN][pg, 0:256].rearrange("p (c i) -> p c i", c=4),
                            pcs[pg, par, :].unsqueeze(2).to_broadcast([64, 4, 64]), ALU.mult),
                            reads=[Rps[bN], Rpcs[par], RAbf], writes=[RAst])
                    for g in range(2):
                        group_body(g)
                    if cut < 6:
                        reserved.discard(fy)
                        return
                    if os.environ.get("DBG") == "3" and d == 0 and ti == 0:
                        s.dma("sp", s_out[4, 0], Ad, reads=[RAst])
                    s.op("act", lambda e: e.copy(Abd, Ad), reads=[RAst], writes=[RAbf])
                    isend = (ti % 2 == 1) if d == 0 else (ti % 2 == 0)
                    if isend:
                        s.dma("sp", s_out[u, d], Ad, reads=[RAst])
                    reserved.discard(fy)
                    if d == 0:
                        s.op("act", lambda e, ti=ti: e.copy(YSb[:, ti, :], ps[fy][:]), reads=[Rps[fy]], writes=[RYS])
                    else:
                        finalize(ti, fy)
                for ti in order:
                    tile_body(ti)

        if stage >= 1:
            if "a" in sub:
                make_gg(0)
            if "b" in sub:
                make_hT(0, 0)
            if "c" in sub:
                even_inproj()
        if stage >= 3:
            attention()
            barrier()
            if stage >= 5:
                rwkv_setup()
                rwkv_prepass()
                rwkv_chunked(NT if stage >= 6 else 2)
            else:
                s.op("pool", lambda e: e.memset(yT[:, 4:8, :], 0.0), writes=[RyT])
            barrier()
            out_proj(w_out_even, yT, RyT)
        if stage >= 2:
            barrier()
            ffn(0)
            barrier()
        if stage >= 4:
            make_gg(1)
            make_hT(1, 0)
            fnet()
            out_proj(w_out_odd, yT, RyT)
            barrier()
            ffn(1)
        for i in range(NT):
            s.dma("sp", y_out[i * 128:(i + 1) * 128, :], x_sb[:, i, :], reads=[Rx[i]])
        s.emit()
    return nc


def _core_units(c):
    if c < 6:
        return [("p", 5 * c + u) for u in range(5)]
    b = c - 6
    return [("s", b, u) for u in range(4)] + [("p", 30 + b)]


def _host_prep(inp):
    f32 = np.float32
    COFF, NCOL = colp_layout()
    ROFF, NROW = rowp_layout()
    g = {k: np.asarray(v) for k, v in inp.items()}
    colp = np.zeros((128, NCOL), f32)

    def put(name, vec):
        c = cols_of(vec)
        colp[:, COFF[name]:COFF[name] + c.shape[1]] = c
    for l in range(2):
        put("bada%d" % l, g["b_ada"][l])
        for k in range(4):
            put("gain%d_%d" % (l, k), g["norm_gains"][l, k])
        for k in range(3):
            put("conv%d_%d" % (l, k), g["ffn_conv"][l, k])
        put("convb%d" % l, g["ffn_conv_b"][l])
    put("mu0", g["rwkv_shift_mu"][0, 0])
    put("mu1", g["rwkv_shift_mu"][0, 1])
    for d in range(2):
        put("w0_%d" % d, g["rwkv_w0"][0, d])
        put("a0_%d" % d, g["rwkv_a0"][0, d])
    for k in range(3):
        put("kvec%d" % k, g["rwkv_kvec"][0, k])

    cmat = np.zeros((8, 128, 128), f32)
    cmat[0] = np.eye(128, dtype=f32)
    for i in range(64):
        cmat[1][2 * i, 2 * i + 1] = 1.0
        cmat[1][2 * i + 1, 2 * i] = -1.0
    cmat[2][:64, :64] = 1.0
    cmat[2][64:, 64:] = 1.0
    cmat[3][:64, 0] = 1.0
    cmat[3][64:, 1] = 1.0
    rr_, cc_ = np.meshgrid(np.arange(128), np.arange(128), indexing="ij")
    cmat[4] = (rr_ < cc_)
    cmat[5] = (rr_ > cc_)
    cmat[6] = (rr_ <= cc_)
    cmat[7] = (rr_ >= cc_)
    cc = np.arange(128)
    chang = 2.0 * np.pi * ((cc[:, None] * cc[None, :]) % 128) / 128.0
    chCS = np.concatenate([np.cos(chang), np.sin(chang)], axis=1).astype(f32)

    inv = (10000.0 ** (-np.arange(16, dtype=np.float32) / 16)).astype(f32)

    shared = dict(
        colp=colp, w_ada=g["w_ada"], w_in_even=g["w_in_even"][0], w_out_even=g["w_out_even"][0],
        w_out_odd=g["w_out_odd"][0], w_ffn_in=g["w_ffn_in"], w_ffn_out=g["w_ffn_out"],
        w2t=np.ascontiguousarray(g["rwkv_w2"][0].reshape(128, 512)),
        a2t=np.ascontiguousarray(g["rwkv_a2"][0].reshape(128, 512)),
        g2=np.ascontiguousarray(g["rwkv_g2"][0]), chCS=chCS, cmat=cmat,
        lnx=np.ascontiguousarray(g["rwkv_lnx"][0]))
    maps = []
    for c in range(NCORES):
        units = _core_units(c)
        xs = []
        for un in units:
            if un[0] == "p":
                xs.append(g["x_prompt"][un[1]])
            else:
                xs.append(g["x_sample"][un[1], un[2] * 256:(un[2] + 1) * 256])
        x_in = np.ascontiguousarray(np.concatenate(xs, axis=0), dtype=f32)
        is_s = c >= 6
        condA = g["c"][c - 6] if is_s else g["c_ctx"]
        condB = g["c_ctx"]
        cond = np.stack([condA, condB], axis=0).astype(f32)
        condT = np.ascontiguousarray(cond.reshape(2, 8, 128).transpose(2, 1, 0).reshape(128, 16))
        rowp = np.zeros((1, NROW), f32)
        rowp[0, ROFF["lam"]:ROFF["lam"] + 256] = g["diff_lambda"][0].reshape(-1)
        rowp[0, ROFF["subln"]:ROFF["subln"] + 128] = g["diff_subln"][0]
        ab = np.full((6, 5), -30000.0, f32)
        if is_s:
            ab[0:5, 0:4] = 0.0
            ab[5, 4] = 0.0
            bndL = [1, 0, 0, 0, 1]
            bndR = [0, 0, 0, 1, 1]
        else:
            for u in range(5):
                ab[u + 1, u] = 0.0
            bndL = [1] * 5
            bndR = [1] * 5
        rowp[0, ROFF["abias"]:ROFF["abias"] + 30] = ab.reshape(-1)
        rowp[0, ROFF["bndL"]:ROFF["bndL"] + 5] = bndL
        rowp[0, ROFF["bndR"]:ROFF["bndR"] + 5] = bndR
        ropeC = np.ones((128, T), f32)
        ropeS = np.zeros((128, T), f32)
        if is_s:
            t = np.arange(1024)
            row = (t // 64).astype(f32)
            col = (t % 64).astype(f32)
            ang = np.concatenate([row[:, None] * inv[None, :], col[:, None] * inv[None, :]], axis=1).astype(f32)
            pidx = (np.arange(128) % 64) // 2
            ropeC[:, :1024] = np.cos(ang)[:, pidx].T
            ropeS[:, :1024] = np.sin(ang)[:, pidx].T
        cacheKT = np.zeros((512, 256), f32)
        cacheV = np.zeros((256, 512), f32)
        initS = np.zeros((2, 128, 256), f32)
        if is_s:
            b = c - 6
            cacheKT[:] = g["cache_k"][b, 0].reshape(256, 512).T
            cacheV[:] = g["cache_v"][b, 0].reshape(256, 512)
            st = g["state_wkv"][b, 0]
            initS[:] = st.reshape(2, 4, 2, 64, 64).transpose(0, 2, 4, 1, 3).reshape(2, 128, 256)
        dC = np.zeros((T, T), np.float64)
        dS = np.zeros((T, T), np.float64)
        blocks = [(0, 1024), (1024, 256)] if is_s else [(256 * u, 256) for u in range(5)]
        for (a0, L) in blocks:
            ll = np.arange(L)
            ang = 2.0 * np.pi * ((ll[:, None] * ll[None, :]) % L) / L
            sc = 1.0 / math.sqrt(L * 128.0)
            dC[a0:a0 + L, a0:a0 + L] = np.cos(ang) * sc
            dS[a0:a0 + L, a0:a0 + L] = -np.sin(ang) * sc
        m = dict(shared)
        m.update(x_in=x_in, condT=condT, rowp=rowp, ropeC=ropeC, ropeS=ropeS, cacheKT=cacheKT, cacheV=cacheV,
                 initS=initS, dftC=dC.astype(f32), dftSn=dS.astype(f32))
        maps.append(m)
    return maps


_NC_CACHE = {}


def kernel(**inputs):
    maps = _host_prep(inputs)
    if "nc" not in _NC_CACHE:
        _NC_CACHE["nc"] = build()
    nc = _NC_CACHE["nc"]
    import os
    ncr = int(os.environ.get("NCR", NCORES))
    res = run_bass_kernel_spmd(nc, maps[:ncr], core_ids=list(range(ncr)))
    outs = list(res.results) + [res.results[0]] * (NCORES - ncr)
    y_prompt = np.zeros((32, 256, D), np.float32)
    y_sample = np.zeros((2, 1024, D), np.float32)
    nk = np.zeros((32, 1, 256, 4, 128), np.float32)
    nv = np.zeros((32, 1, 256, 4, 128), np.float32)
    ns = np.zeros((32, 1, 2, 8, 64, 64), np.float32)
    for c in range(NCORES):
        r = outs[c]
        for u, un in enumerate(_core_units(c)):
            sl = slice(u * 256, (u + 1) * 256)
            if un[0] == "p":
                bi = un[1]
                y_prompt[bi] = r["y_out"][sl]
                nk[bi, 0] = r["k_out"][sl].reshape(256, 4, 128)
                nv[bi, 0] = r["v_out"][sl].reshape(256, 4, 128)
                stt = r["s_out"][u]
                ns[bi, 0] = stt.reshape(2, 2, 64, 4, 64).transpose(0, 3, 1, 4, 2).reshape(2, 8, 64, 64)
            else:
                y_sample[un[1], un[2] * 256:(un[2] + 1) * 256] = r["y_out"][sl]
    return (y_prompt, y_sample, nk, nv, ns)
```

```python
import contextlib
import math
import numpy as np
import concourse.bass as bass
import concourse.mybir as mybir
from concourse.bass_utils import run_bass_kernel_spmd

F32 = mybir.dt.float32
BF16 = mybir.dt.bfloat16
AF = mybir.ActivationFunctionType
ALU = mybir.AluOpType
AX = mybir.AxisListType

T = 1280
NT = 10
U = 5
D = 1024
KC = 8
DFF = 2816
FC = 22
NCORES = 8
ARENA_W = 27200
EXPM05 = math.exp(-0.5)


class Res:
    __slots__ = ("name", "writer", "readers", "excl")

    def __init__(self, name, excl=False):
        self.name = name
        self.writer = None
        self.readers = []
        self.excl = excl


class Sched:
    ENGS = ("pe", "act", "dve", "pool", "sp")
    NDMA = 6

    def __init__(self, nc):
        self.nc = nc
        self.prog = {e: [] for e in self.ENGS}
        self.signal = {e: set() for e in self.ENGS}
        self.ndma = {e: 0 for e in self.ENGS}

    def _collect(self, reads, writes, eng=None):
        deps = []
        for r in reads:
            if r.writer is not None:
                deps.append(r.writer)
            if r.excl:
                deps.extend(t for t in r.readers if t[1] != eng)
        for w in writes:
            if w.writer is not None:
                deps.append(w.writer)
            deps.extend(w.readers)
        return deps

    def _commit(self, tok, reads, writes):
        for r in reads:
            r.readers.append(tok)
        for w in writes:
            w.writer = tok
            w.readers = []

    def op(self, eng, fn, reads=(), writes=()):
        deps = self._collect(reads, writes, eng)
        idx = len(self.prog[eng])
        if eng == "pe":
            deps = [d for d in deps if not (d[0] == "c" and d[1] == "pe")]
        for d in deps:
            if d[0] == "c":
                self.signal[d[1]].add(d[2])
        self.prog[eng].append(dict(fn=fn, deps=deps, kind="c"))
        tok = ("c", eng, idx)
        self._commit(tok, reads, writes)
        return tok

    def dma(self, eng, out, in_, reads=(), writes=()):
        deps = self._collect(reads, writes, eng)
        n = self.ndma[eng]
        self.ndma[eng] += 1
        if n >= self.NDMA:
            deps.append(("d", eng, n - self.NDMA))
        for d in deps:
            if d[0] == "c":
                self.signal[d[1]].add(d[2])
        self.prog[eng].append(dict(out=out, in_=in_, deps=deps, kind="d", n=n))
        tok = ("d", eng, n)
        self._commit(tok, reads, writes)
        return tok

    def emit(self):
        nc = self.nc
        with contextlib.ExitStack() as st:
            csem = {e: st.enter_context(nc.semaphore("c_" + e)) for e in self.ENGS}
            dsem = {e: [st.enter_context(nc.semaphore("d_%s_%d" % (e, i))) for i in range(self.NDMA)]
                    for e in self.ENGS if self.ndma[e] > 0}
            sigval = {}
            for e in self.ENGS:
                cnt = 0
                m = {}
                for i in range(len(self.prog[e])):
                    if i in self.signal[e]:
                        cnt += 1
                        m[i] = cnt
                sigval[e] = m

            def resolve(tok):
                if tok[0] == "c":
                    return csem[tok[1]], sigval[tok[1]][tok[2]], ("c", tok[1])
                e, n = tok[1], tok[2]
                return dsem[e][n % self.NDMA], 16 * (n // self.NDMA + 1), ("d", e, n % self.NDMA)

            def run_engine(e, h):
                waited = {}
                for i, ins in enumerate(self.prog[e]):
                    need = {}
                    for d in ins["deps"]:
                        sem, val, key = resolve(d)
                        if waited.get(key, 0) >= val:
                            continue
                        if key not in need or need[key][1] < val:
                            need[key] = (sem, val)
                    for key, (sem, val) in need.items():
                        h.wait_ge(sem, val)
                        waited[key] = val
                    if ins["kind"] == "c":
                        bi = ins["fn"](h)
                        if i in self.signal[e]:
                            bi.then_inc(csem[e], 1)
                    else:
                        n = ins["n"]
                        h.dma_start(out=ins["out"], in_=ins["in_"]).then_inc(dsem[e][n % self.NDMA], 16)
                if self.ndma[e] > 0:
                    n = self.ndma[e]
                    for slot in range(self.NDMA):
                        cnt = (n - slot + self.NDMA - 1) // self.NDMA if n > slot else 0
                        if cnt > 0:
                            h.wait_ge(dsem[e][slot], 16 * cnt)

            with nc.Block() as block:
                @block.tensor
                def _(eng):
                    run_engine("pe", eng)

                @block.scalar
                def _(eng):
                    run_engine("act", eng)

                @block.vector
                def _(eng):
                    run_engine("dve", eng)

                @block.gpsimd
                def _(eng):
                    run_engine("pool", eng)

                @block.sync
                def _(eng):
                    run_engine("sp", eng)


def colp_layout():
    off = {}
    n = 0

    def add(name, cols):
        nonlocal n
        off[name] = n
        n += cols
    for l in range(2):
        add("bada%d" % l, 48)
        for g in range(4):
            add("gain%d_%d" % (l, g), 8)
        for k in range(3):
            add("conv%d_%d" % (l, k), FC)
        add("convb%d" % l, FC)
    add("mu0", 15)
    add("mu1", 15)
    for d in range(2):
        add("w0_%d" % d, 4)
        add("a0_%d" % d, 4)
    for k in range(3):
        add("kvec%d" % k, 4)
    return off, n


def rowp_layout():
    off = {}
    n = 0

    def add(name, cols):
        nonlocal n
        off[name] = n
        n += cols
    add("lam", 256)
    add("subln", 128)
    add("abias", 30)
    add("bndL", 5)
    add("bndR", 5)
    return off, n


def cols_of(vec):
    v = np.asarray(vec, np.float32).reshape(-1, 128)
    return np.ascontiguousarray(v.T)


def build(stage=99):
    nc = bass.Bass("TRN2", target_bir_lowering=False)
    COFF, NCOL = colp_layout()
    ROFF, NROW = rowp_layout()

    def din(name, shape):
        return nc.dram_tensor(name, list(shape), F32, kind="ExternalInput").ap()

    def dout(name, shape):
        return nc.dram_tensor(name, list(shape), F32, kind="ExternalOutput").ap()

    x_in = din("x_in", [T, D])
    condT = din("condT", [128, 16])
    colp = din("colp", [128, NCOL])
    rowp = din("rowp", [1, NROW])
    w_ada = din("w_ada", [2, D, 6 * D])
    w_in_even = din("w_in_even", [D, 3456])
    w_out_even = din("w_out_even", [D, D])
    w_out_odd = din("w_out_odd", [D, D])
    w_ffn_in = din("w_ffn_in", [2, D, 2 * DFF])
    w_ffn_out = din("w_ffn_out", [2, DFF, D])
    w2t_d = din("w2t", [128, 512])
    a2t_d = din("a2t", [128, 512])
    g2_d = din("g2", [128, 512])
    cacheKT = din("cacheKT", [512, 256])
    cacheV = din("cacheV", [256, 512])
    ropeC = din("ropeC", [128, T])
    ropeS = din("ropeS", [128, T])
    initS = din("initS", [2, 128, 256])
    dftC = din("dftC", [T, T])
    dftSn = din("dftSn", [T, T])
    chCS = din("chCS", [128, 256])
    cmat = din("cmat", [8, 128, 128])
    lnx_d = din("lnx", [2, 512])

    y_out = dout("y_out", [T, D])
    k_out = dout("k_out", [T, 512])
    v_out = dout("v_out", [T, 512])
    s_out = dout("s_out", [U, 2, 128, 256])

    import os
    sub = os.environ.get("SUB", "abcmqv")
    with contextlib.ExitStack() as st:
        s = Sched(nc)

        def sb(name, shape, dt=F32):
            return st.enter_context(nc.sbuf_tensor(name, list(shape), dt))

        x_sb = sb("x_sb", [128, NT, D])
        Rx = [Res("x%d" % i) for i in range(NT)]
        WB = 4096
        wbuf = [sb("wb%d" % i, [128, WB], BF16) for i in range(3)]
        Rw = [Res("wb%d" % i) for i in range(3)]
        wctr = [0]
        colp_sb = sb("colp_sb", [128, NCOL])
        Rcolp = Res("colp")
        rowp_sb = sb("rowp_sb", [128, NROW])
        Rrowp = Res("rowp")
        ident_f = sb("ident_f", [128, 128])
        ident_b = sb("ident_b", [128, 128], BF16)
        rotT_b = sb("rotT_b", [128, 128], BF16)
        bones_f = sb("bones_f", [128, 128])
        halfsel_f = sb("halfsel_f", [128, 128])
        bones_b = sb("bones_b", [128, 128], BF16)
        Rconst = Res("const")
        gg_sb = sb("gg_sb", [128, 2, 2, D], BF16)
        Rgg = Res("gg")
        scond = sb("scond", [128, 16], BF16)
        condf = sb("condf", [128, 16])
        Rcond = Res("cond")
        modcol = sb("modcol", [128, 2, 96])
        Rmod = Res("modcol")
        scsh = sb("scsh", [128, 2, 2, 2, 16])
        Rscsh = Res("scsh")
        stat = sb("stat", [128, 64])
        Rstat = Res("stat")
        junk_b = sb("junk_b", [128, D], BF16)
        Rjunk = Res("junk")
        xn_b = sb("xn_b", [128, D], BF16)
        Rxn = Res("xn")
        stage_f = sb("stage_f", [128, 2, 512])
        Rstage = [Res("stage0"), Res("stage1")]
        stctr = [0]
        arena = sb("arena", [128, ARENA_W])
        Rar = {}

        def ares(name):
            if name not in Rar:
                Rar[name] = Res("ar_" + name)
            return Rar[name]

        def carve(off_w, nelem, dt):
            if dt == F32:
                return arena[:, off_w:off_w + nelem]
            return arena[:, off_w:off_w + (nelem + 1) // 2].bitcast(BF16)[:, 0:nelem]

        ps = [st.enter_context(nc.psum_tensor("ps%d" % b, [128, 512], F32)) for b in range(8)]
        psb = [p.bitcast(BF16) for p in ps]
        Rps = [Res("ps%d" % b, excl=True) for b in range(8)]
        bctr = [0]

        reserved = set()

        def bank():
            while True:
                b = bctr[0] % 8
                bctr[0] += 1
                if b not in reserved:
                    return b

        def barrier():
            allr = list(Rar.values()) + list(Rw)
            s.op("dve", lambda e: e.memset(stat[:, 63:64], 0.0), writes=allr + [Rstat])

        def wload(src_ap, kc, ncols):
            i = wctr[0] % 3
            wctr[0] += 1
            view = wbuf[i][:, 0:kc * ncols].rearrange("p (c n) -> p c n", c=kc)
            s.dma("pool", view, src_ap.rearrange("(c p) n -> p c n", p=128), writes=[Rw[i]])
            return view, Rw[i]

        s.dma("sp", colp_sb[:], colp, writes=[Rcolp])
        s.dma("sp", rowp_sb[:], rowp[0, :].partition_broadcast(128), writes=[Rrowp])
        s.dma("sp", ident_f[:], cmat[0], writes=[Rconst])
        s.dma("sp", bones_f[:], cmat[2], writes=[Rconst])
        s.dma("sp", halfsel_f[:], cmat[3], writes=[Rconst])
        s.dma("pool", ident_b[:], cmat[0], writes=[Rconst])
        s.dma("pool", rotT_b[:], cmat[1], writes=[Rconst])
        s.dma("pool", bones_b[:], cmat[2], writes=[Rconst])
        s.dma("sp", condf[:], condT, writes=[Rcond])
        for i in range(NT):
            s.dma("sp", x_sb[:, i, :], x_in[i * 128:(i + 1) * 128, :], writes=[Rx[i]])
        s.op("act", lambda e: e.activation(scond[:], condf[:], AF.Silu), reads=[Rcond], writes=[Rcond])

        def cp(name, c=0, n=1):
            o = COFF[name] + c
            return colp_sb[:, o:o + n]

        for l in range(2):
            b = bank()
            for v in range(6):
                for half in range(2):
                    wv, rw = wload(w_ada[l][:, v * 1024 + half * 512: v * 1024 + half * 512 + 512], KC, 512)
                    for j in range(4):
                        col = (v * 8 + half * 4 + j) * 2
                        for kc in range(KC):
                            s.op("pe", lambda e, wv=wv, j=j, kc=kc, col=col, b=b: e.matmul(
                                ps[b][:, col:col + 2], wv[:, kc, j * 128:(j + 1) * 128],
                                scond[:, kc * 2:kc * 2 + 2], start=(kc == 0), stop=(kc == KC - 1)),
                                reads=[rw, Rcond], writes=[Rps[b]])
            s.op("dve", lambda e, l=l, b=b: e.tensor_tensor(
                modcol[:, l, :].rearrange("p (c a) -> p c a", a=2),
                ps[b][:, 0:96].rearrange("p (c a) -> p c a", a=2),
                cp("bada%d" % l, 0, 48).unsqueeze(2).to_broadcast([128, 48, 2]), ALU.add),
                reads=[Rps[b], Rcolp], writes=[Rmod])

            def mv(v, l=l):
                return modcol[:, l, v * 16:(v + 1) * 16].rearrange("p (c a) -> p c a", a=2)

            def gain(g, l=l):
                return cp("gain%d_%d" % (l, g), 0, 8).unsqueeze(2).to_broadcast([128, 8, 2])
            for wi, (vs, vsh, g) in enumerate([(1, 0, 0), (4, 3, 2)]):
                sc = scsh[:, l, wi, 0, :].rearrange("p (c a) -> p c a", a=2)
                sh = scsh[:, l, wi, 1, :].rearrange("p (c a) -> p c a", a=2)
                s.op("dve", lambda e, sc=sc, vs=vs, mv=mv: e.tensor_scalar(sc, mv(vs), 1.0, None, ALU.add),
                     reads=[Rmod], writes=[Rscsh])
                s.op("dve", lambda e, sc=sc, g=g, gain=gain: e.tensor_tensor(sc, sc, gain(g), ALU.mult),
                     reads=[Rscsh, Rcolp], writes=[Rscsh])
                s.op("dve", lambda e, sh=sh, vsh=vsh, mv=mv: e.tensor_copy(sh, mv(vsh)),
                     reads=[Rmod], writes=[Rscsh])

        ggcol = sb("ggcol", [128, 2, 16])
        Rggcol = Res("ggcol")

        def make_gg(l):
            for wi, (vg, g) in enumerate([(2, 1), (5, 3)]):
                gc = ggcol[:, wi, :].rearrange("p (c a) -> p c a", a=2)
                s.op("dve", lambda e, gc=gc, vg=vg, g=g, l=l: e.tensor_tensor(
                    gc, modcol[:, l, vg * 16:(vg + 1) * 16].rearrange("p (c a) -> p c a", a=2),
                    cp("gain%d_%d" % (l, g), 0, 8).unsqueeze(2).to_broadcast([128, 8, 2]), ALU.mult),
                    reads=[Rmod, Rcolp], writes=[Rggcol])
                for a in range(2):
                    for hh in range(2):
                        b = bank()
                        for j in range(4):
                            c = hh * 4 + j
                            s.op("pe", lambda e, b=b, wi=wi, c=c, a=a, j=j: e.matmul(
                                ps[b][:, j * 128:(j + 1) * 128],
                                ggcol[:, wi, c * 2 + a:c * 2 + a + 1].to_broadcast([128, 128]),
                                ident_f[:], start=True, stop=True),
                                reads=[Rggcol, Rconst], writes=[Rps[b]])
                        s.op("act", lambda e, b=b, wi=wi, a=a, hh=hh: e.copy(
                            gg_sb[:, wi, a, hh * 512:(hh + 1) * 512], ps[b][:]),
                            reads=[Rps[b]], writes=[Rgg])

        hT = carve(0, 8 * T, BF16).rearrange("p (c t) -> p c t", c=8)
        RhT = ares("hT")

        def ab_of_tile(i):
            return 0 if i < 8 else 1

        def make_hT(l, wi):
            for i in range(NT):
                a = ab_of_tile(i)
                s.op("act", lambda e, i=i: e.activation(junk_b[:], x_sb[:, i, :], AF.Square, scale=1.0 / 32.0,
                                                        accum_out=stat[:, 0:1]),
                     reads=[Rx[i]], writes=[Rjunk, Rstat])
                s.op("dve", lambda e: e.tensor_scalar(stat[:, 1:2], stat[:, 0:1], 1e-6, None, ALU.add),
                     reads=[Rstat], writes=[Rstat])
                s.op("act", lambda e: e.activation(stat[:, 1:2], stat[:, 1:2], AF.Sqrt), reads=[Rstat], writes=[Rstat])
                s.op("dve", lambda e: e.reciprocal(stat[:, 2:3], stat[:, 1:2]), reads=[Rstat], writes=[Rstat])
                s.op("dve", lambda e, i=i: e.tensor_scalar(xn_b[:], x_sb[:, i, :], stat[:, 2:3], None, ALU.mult),
                     reads=[Rx[i], Rstat], writes=[Rxn])
                b = bank()
                for c in range(8):
                    s.op("pe", lambda e, b=b, c=c: e.transpose(psb[b][:, c * 128:(c + 1) * 128],
                                                               xn_b[:, c * 128:(c + 1) * 128], ident_b[:]),
                         reads=[Rxn, Rconst], writes=[Rps[b]])
                for c in range(8):
                    s.op("act", lambda e, b=b, c=c, i=i, a=a: e.activation(
                        hT[:, c, i * 128:(i + 1) * 128], psb[b][:, c * 128:(c + 1) * 128], AF.Identity,
                        bias=scsh[:, l, wi, 1, c * 2 + a:c * 2 + a + 1],
                        scale=scsh[:, l, wi, 0, c * 2 + a:c * 2 + a + 1]),
                        reads=[Rps[b], Rscsh], writes=[RhT])

        TOKCH = [(0, 512), (512, 512), (1024, 256)]

        def linear_fm(wv, rw, ncol0, actT, Ract, kc_n, evac):
            for (t0, tn) in TOKCH:
                b = bank()
                for kc in range(kc_n):
                    s.op("pe", lambda e, b=b, kc=kc, t0=t0, tn=tn: e.matmul(
                        ps[b][:, 0:tn], wv[:, kc, ncol0:ncol0 + 128], actT[:, kc, t0:t0 + tn],
                        start=(kc == 0), stop=(kc == kc_n - 1)),
                        reads=[rw, Ract], writes=[Rps[b]])
                evac(b, t0, tn)

        def linear_tm(wv, rw, ncols, actT, Ract, kc_n, i, b, col0=0):
            for kc in range(kc_n):
                s.op("pe", lambda e, kc=kc: e.matmul(
                    ps[b][:, col0:col0 + ncols], actT[:, kc, i * 128:(i + 1) * 128], wv[:, kc, 0:ncols],
                    start=(kc == 0), stop=(kc == kc_n - 1)),
                    reads=[rw, Ract], writes=[Rps[b]])

        def residual_update(i, banks, wi, fsrc=None, Rf=None):
            a = ab_of_tile(i)
            if fsrc is None:
                for hh, b in enumerate(banks):
                    s.op("act", lambda e, b=b, hh=hh: e.activation(junk_b[:, 0:512], ps[b][:], AF.Square,
                                                                   scale=1.0 / 32.0, accum_out=stat[:, 8 + hh:9 + hh]),
                         reads=[Rps[b]], writes=[Rjunk, Rstat])
                s.op("dve", lambda e: e.tensor_tensor(stat[:, 10:11], stat[:, 8:9], stat[:, 9:10], ALU.add),
                     reads=[Rstat], writes=[Rstat])
            else:
                s.op("act", lambda e: e.activation(junk_b[:], fsrc, AF.Square, scale=1.0 / 32.0,
                                                   accum_out=stat[:, 10:11]),
                     reads=[Rf], writes=[Rjunk, Rstat])
            s.op("dve", lambda e: e.tensor_scalar(stat[:, 11:12], stat[:, 10:11], 1e-6, None, ALU.add),
                 reads=[Rstat], writes=[Rstat])
            s.op("act", lambda e: e.activation(stat[:, 11:12], stat[:, 11:12], AF.Sqrt), reads=[Rstat], writes=[Rstat])
            s.op("dve", lambda e: e.reciprocal(stat[:, 12:13], stat[:, 11:12]), reads=[Rstat], writes=[Rstat])
            for hh in range(2):
                src = ps[banks[hh]][:] if fsrc is None else fsrc[:, hh * 512:(hh + 1) * 512]
                rr = [Rps[banks[hh]]] if fsrc is None else [Rf]
                si = stctr[0] % 2
                stctr[0] += 1
                s.op("dve", lambda e, src=src, hh=hh, si=si: e.scalar_tensor_tensor(
                    stage_f[:, si, :], src, stat[:, 12:13], gg_sb[:, wi, a, hh * 512:(hh + 1) * 512],
                    ALU.mult, ALU.mult),
                    reads=rr + [Rstat, Rgg], writes=[Rstage[si]])
                s.op("dve", lambda e, hh=hh, si=si, i=i: e.tensor_tensor(
                    x_sb[:, i, hh * 512:(hh + 1) * 512], x_sb[:, i, hh * 512:(hh + 1) * 512], stage_f[:, si, :], ALU.add),
                    reads=[Rstage[si], Rx[i]], writes=[Rx[i]])

        actT = carve(5120, FC * T, BF16).rearrange("p (c t) -> p c t", c=FC)
        RactT = ares("actT")
        graw = carve(19200, T + 2, F32)
        Rgraw = ares("graw")
        cbuf = carve(19200 + 1284, T, F32)
        Rcbuf = ares("cbuf")
        sbuf_s = carve(19200 + 1284 + 1280, T, F32)
        Rsbuf = ares("sbuf_s")
        fstageA = carve(19200, 5 * D, F32).rearrange("p (i n) -> p i n", i=5)
        fstageB = carve(0, 5 * D, F32).rearrange("p (i n) -> p i n", i=5)

        def fst(i):
            return fstageA[:, i, :] if i < 5 else fstageB[:, i - 5, :]

        def Rfst_of(i):
            return ares("fstage") if i < 5 else RhT
        bcorr = sb("bcorr", [128, 16])
        Rbcorr = Res("bcorr")

        def ffn(l):
            make_hT(l, 1)
            s.op("dve", lambda e: e.memset(graw[:, 0:1], 0.0), writes=[Rgraw])
            s.op("dve", lambda e: e.memset(graw[:, T + 1:T + 2], 0.0), writes=[Rgraw])
            for blk in range(0, FC, 2):
                wu, ru = wload(w_ffn_in[l][:, blk * 128: blk * 128 + 256], KC, 256)
                wg, rg = wload(w_ffn_in[l][:, DFF + blk * 128: DFF + blk * 128 + 256], KC, 256)
                for jj in range(2):
                    fc = blk + jj
                    def evac_g(b, t0, tn):
                        s.op("act", lambda e, b=b, t0=t0, tn=tn: e.copy(graw[:, 1 + t0:1 + t0 + tn], ps[b][:, 0:tn]),
                             reads=[Rps[b]], writes=[Rgraw])
                    linear_fm(wg, rg, jj * 128, hT, RhT, KC, evac_g)
                    w0 = cp("conv%d_0" % l, fc)
                    w1 = cp("conv%d_1" % l, fc)
                    w2 = cp("conv%d_2" % l, fc)
                    cb = cp("convb%d" % l, fc)
                    s.op("act", lambda e, w1=w1, cb=cb: e.activation(cbuf[:], graw[:, 1:T + 1], AF.Identity, bias=cb, scale=w1),
                         reads=[Rgraw, Rcolp], writes=[Rcbuf])
                    s.op("dve", lambda e, w0=w0: e.scalar_tensor_tensor(cbuf[:], graw[:, 0:T], w0, cbuf[:], ALU.mult, ALU.add),
                         reads=[Rgraw, Rcolp, Rcbuf], writes=[Rcbuf])
                    s.op("dve", lambda e, w2=w2: e.scalar_tensor_tensor(cbuf[:], graw[:, 2:T + 2], w2, cbuf[:], ALU.mult, ALU.add),
                         reads=[Rgraw, Rcolp, Rcbuf], writes=[Rcbuf])
                    gprev = graw[:, 256:256 + 1024].rearrange("p (u k) -> p u k", k=256)[:, :, 0]
                    gnext = graw[:, 257:257 + 1024].rearrange("p (u k) -> p u k", k=256)[:, :, 0]
                    s.op("dve", lambda e, gprev=gprev: e.tensor_tensor(bcorr[:, 0:4], gprev, rowp_sb[:, ROFF["bndL"] + 1:ROFF["bndL"] + 5], ALU.mult),
                         reads=[Rgraw, Rrowp], writes=[Rbcorr])
                    s.op("dve", lambda e, w0=w0: e.tensor_scalar(bcorr[:, 0:4], bcorr[:, 0:4], w0, None, ALU.mult),
                         reads=[Rbcorr, Rcolp], writes=[Rbcorr])
                    c_at = cbuf[:, 256:256 + 1024].rearrange("p (u k) -> p u k", k=256)[:, :, 0]
                    s.op("dve", lambda e, c_at=c_at: e.tensor_tensor(c_at, c_at, bcorr[:, 0:4], ALU.subtract),
                         reads=[Rbcorr, Rcbuf], writes=[Rcbuf])
                    s.op("dve", lambda e, gnext=gnext: e.tensor_tensor(bcorr[:, 4:8], gnext, rowp_sb[:, ROFF["bndL"] + 1:ROFF["bndL"] + 5], ALU.mult),
                         reads=[Rgraw, Rrowp], writes=[Rbcorr])
                    s.op("dve", lambda e, w2=w2: e.tensor_scalar(bcorr[:, 4:8], bcorr[:, 4:8], w2, None, ALU.mult),
                         reads=[Rbcorr, Rcolp], writes=[Rbcorr])
                    c_at2 = cbuf[:, 255:255 + 1024].rearrange("p (u k) -> p u k", k=256)[:, :, 0]
                    s.op("dve", lambda e, c_at2=c_at2: e.tensor_tensor(c_at2, c_at2, bcorr[:, 4:8], ALU.subtract),
                         reads=[Rbcorr, Rcbuf], writes=[Rcbuf])
                    s.op("act", lambda e: e.activation(sbuf_s[:], cbuf[:], AF.Silu), reads=[Rcbuf], writes=[Rsbuf])
                    def evac_u(b, t0, tn, fc=fc):
                        s.op("dve", lambda e, b=b, t0=t0, tn=tn: e.tensor_tensor(
                            actT[:, fc, t0:t0 + tn], ps[b][:, 0:tn], sbuf_s[:, t0:t0 + tn], ALU.mult),
                            reads=[Rps[b], Rsbuf], writes=[RactT])
                    linear_fm(wu, ru, jj * 128, hT, RhT, KC, evac_u)
            for nb in range(8):
                wv, rw = wload(w_ffn_out[l][:, nb * 128:(nb + 1) * 128], FC, 128)
                for i in range(NT):
                    b = bank()
                    linear_tm(wv, rw, 128, actT, RactT, FC, i, b)
                    s.op("act", lambda e, b=b, i=i, nb=nb: e.copy(fst(i)[:, nb * 128:(nb + 1) * 128], ps[b][:, 0:128]),
                         reads=[Rps[b]], writes=[Rfst_of(i)])
            for i in range(NT):
                residual_update(i, None, 1, fsrc=fst(i), Rf=Rfst_of(i))

        qT = carve(5120, 4 * T, BF16).rearrange("p (c t) -> p c t", c=4)
        RqT = ares("qT")
        kT = carve(7680, 4 * 1536, BF16).rearrange("p (c t) -> p c t", c=4)
        RkT = ares("kT")
        Vaug = carve(10752, 12 * 4 * 144, BF16).rearrange("p (k h d) -> p k h d", k=12, h=4)
        RV = ares("Vaug")
        yT = carve(0, 8 * T, BF16).rearrange("p (c t) -> p c t", c=8)
        fbT = carve(14208, 15 * (T + 2), BF16).rearrange("p (c t) -> p c t", c=15)
        RfbT = ares("fbT")
        RyT = RhT
        ropeC_sb = carve(23824, T, F32)
        ropeS_sb = carve(23824 + T, T, F32)
        Rrope = Res("rope")
        rtmp = sb("rtmp", [128, 2, 512])
        Rrtmp = Res("rtmp")
        raw_b = sb("raw_b", [128, 512], BF16)
        Rraw = Res("raw_b")

        def even_inproj():
            s.dma("sp", ropeC_sb[:], ropeC, writes=[Rrope])
            s.dma("sp", ropeS_sb[:], ropeS, writes=[Rrope])
            s.dma("pool", kT[:, :, 0:256], cacheKT.rearrange("(h p) t -> p h t", p=128), writes=[RkT])
            for kk_ in range(2):
                s.dma("pool", Vaug[:, kk_, :, 0:128],
                      cacheV[kk_ * 128:(kk_ + 1) * 128, :].rearrange("p (h d) -> p h d", h=4), writes=[RV])
            if "m" in sub:
                s.op("pool", lambda e: e.memset(Vaug[:, :, :, 128:129], 1.0), writes=[RV])
            for which, dst, doff in ((0, qT, 0), (1, kT, 256)):
                if "q" not in sub:
                    break
                wv, rw = wload(w_in_even[:, which * 512:(which + 1) * 512], KC, 512)
                Rdst = RqT if which == 0 else RkT
                for h in range(4):
                    def evac(b, t0, tn, h=h, dst=dst, doff=doff, Rdst=Rdst):
                        s.op("act", lambda e, b=b, tn=tn: e.copy(raw_b[:, 0:tn], ps[b][:, 0:tn]),
                             reads=[Rps[b]], writes=[Rraw])
                        b2 = bank()
                        s.op("pe", lambda e, b2=b2, tn=tn: e.matmul(ps[b2][:, 0:tn], rotT_b[:], raw_b[:, 0:tn], start=True, stop=True),
                             reads=[Rraw, Rconst], writes=[Rps[b2]])
                        s.op("dve", lambda e, t0=t0, tn=tn: e.tensor_tensor(rtmp[:, 0, 0:tn], raw_b[:, 0:tn], ropeC_sb[:, t0:t0 + tn], ALU.mult),
                             reads=[Rraw, Rrope], writes=[Rrtmp])
                        s.op("dve", lambda e, b2=b2, t0=t0, tn=tn: e.tensor_tensor(rtmp[:, 1, 0:tn], ps[b2][:, 0:tn], ropeS_sb[:, t0:t0 + tn], ALU.mult),
                             reads=[Rps[b2], Rrope], writes=[Rrtmp])
                        s.op("dve", lambda e, t0=t0, tn=tn: e.tensor_tensor(dst[:, h, doff + t0:doff + t0 + tn], rtmp[:, 0, 0:tn], rtmp[:, 1, 0:tn], ALU.add),
                             reads=[Rrtmp], writes=[Rdst])
                    linear_fm(wv, rw, h * 128, hT, RhT, KC, evac)
                if which == 1:
                    for i in range(NT):
                        b = bank()
                        linear_tm(wv, rw, 512, hT, RhT, KC, i, b)
                        si = stctr[0] % 2
                        stctr[0] += 1
                        s.op("act", lambda e, b=b, si=si: e.copy(stage_f[:, si, :], ps[b][:]), reads=[Rps[b]], writes=[Rstage[si]])
                        s.dma("sp", k_out[i * 128:(i + 1) * 128, :], stage_f[:, si, :], reads=[Rstage[si]])
            s.op("pool", lambda e: e.memset(fbT[:, :, 0:1], 0.0), writes=[RfbT])
            s.op("pool", lambda e: e.memset(fbT[:, :, T + 1:T + 2], 0.0), writes=[RfbT])
            for pc, (c0, ncols) in enumerate([(1536, 512), (2048, 512), (2560, 512), (3072, 384)]):
                wv, rw = wload(w_in_even[:, c0:c0 + ncols], KC, ncols)
                for j in range(ncols // 128):
                    ch = pc * 4 + j

                    def evac_fb(b, t0, tn, ch=ch):
                        s.op("act", lambda e, b=b, t0=t0, tn=tn: e.copy(fbT[:, ch, 1 + t0:1 + t0 + tn], ps[b][:, 0:tn]),
                             reads=[Rps[b]], writes=[RfbT])
                    linear_fm(wv, rw, j * 128, hT, RhT, KC, evac_fb)
            wv, rw = wload(w_in_even[:, 1024:1536], KC, 512)
            for i in range(NT if "v" in sub else 0):
                b = bank()
                linear_tm(wv, rw, 512, hT, RhT, KC, i, b)
                si = stctr[0] % 2
                stctr[0] += 1
                s.op("act", lambda e, b=b, si=si: e.copy(stage_f[:, si, :], ps[b][:]), reads=[Rps[b]], writes=[Rstage[si]])
                s.dma("sp", v_out[i * 128:(i + 1) * 128, :], stage_f[:, si, :], reads=[Rstage[si]])
                s.op("dve", lambda e, si=si, i=i: e.tensor_copy(Vaug[:, 2 + i, :, 0:128], stage_f[:, si, :].rearrange("p (h d) -> p h d", h=4)),
                     reads=[Rstage[si]], writes=[RV])


        NPB = 4
        PT = [[sb("PT%d%d" % (m, k), [128, 256], BF16) for k in range(NPB)] for m in range(2)]
        RPT = [[Res("PT%d%d" % (m, k)) for k in range(NPB)] for m in range(2)]
        ya_f = sb("ya_f", [128, 128])
        Rya = Res("ya_f")
        ya_b = sb("ya_b", [128, 128], BF16)
        Ryab = Res("ya_b")
        subln08 = sb("subln08", [128, 128])
        Rsub = Res("subln08")
        lamt = sb("lamt", [128, 64])
        Rlamt = Res("lamt")

        def attention():
            lo = ROFF["lam"]
            for k in range(2):
                s.op("dve", lambda e, k=k: e.tensor_tensor(lamt[:], rowp_sb[:, lo + 128 * k: lo + 128 * k + 64],
                                                          rowp_sb[:, lo + 128 * k + 64: lo + 128 * k + 128], ALU.mult),
                     reads=[Rrowp], writes=[Rlamt])
                s.op("dve", lambda e, k=k: e.reduce_sum(stat[:, 20 + k:21 + k], lamt[:], axis=AX.X),
                     reads=[Rlamt], writes=[Rstat])
            s.op("act", lambda e: e.activation(stat[:, 22:24], stat[:, 20:22], AF.Exp), reads=[Rstat], writes=[Rstat])
            s.op("dve", lambda e: e.tensor_tensor(stat[:, 24:25], stat[:, 22:23], stat[:, 23:24], ALU.subtract),
                 reads=[Rstat], writes=[Rstat])
            s.op("dve", lambda e: e.tensor_scalar(stat[:, 25:26], stat[:, 24:25], 0.2, -1.0, ALU.add, ALU.mult),
                 reads=[Rstat], writes=[Rstat])
            s.op("dve", lambda e: e.tensor_scalar(subln08[:], rowp_sb[:, ROFF["subln"]:ROFF["subln"] + 128], 0.8, None, ALU.mult),
                 reads=[Rrowp], writes=[Rsub])
            pctr = [0, 0]
            for h in range(4):
                for qu in range(5):
                    acc = [bank(), bank()]
                    reserved.update(acc)
                    pend = []
                    for kt in range(12):
                        ku = 0 if kt < 2 else 1 + (kt - 2) // 2
                        bcol = ROFF["abias"] + ku * 5 + qu
                        for m in range(2):
                            bs = bank()
                            s.op("pe", lambda e, bs=bs, m=m, h=h, kt=kt, qu=qu: e.matmul(
                                ps[bs][:, 0:256], kT[64 * m:64 * m + 64, h, kt * 128:(kt + 1) * 128],
                                qT[64 * m:64 * m + 64, h, qu * 256:(qu + 1) * 256], start=True, stop=True),
                                reads=[RkT, RqT], writes=[Rps[bs]])
                            pk = pctr[m] % NPB
                            pctr[m] += 1
                            s.op("act", lambda e, bs=bs, m=m, pk=pk, bcol=bcol: e.activation(
                                PT[m][pk][:], ps[bs][:, 0:256], AF.Exp, bias=rowp_sb[:, bcol:bcol + 1], scale=0.125),
                                reads=[Rps[bs], Rrowp], writes=[RPT[m][pk]])

                            def pv(m=m, pk=pk, kt=kt, h=h, acc=acc):
                                for qt in range(2):
                                    s.op("pe", lambda e, qt=qt: e.matmul(
                                        ps[acc[m]][:, qt * 129:qt * 129 + 129], PT[m][pk][:, qt * 128:(qt + 1) * 128],
                                        Vaug[:, kt, h, 0:129], start=(kt == 0 and qt == 0), stop=(kt == 11),
                                        skip_group_check=True),
                                        reads=[RPT[m][pk], RV], writes=[Rps[acc[m]]])
                            pend.append(pv)
                            if len(pend) > 2:
                                pend.pop(0)()
                    while pend:
                        pend.pop(0)()
                    reserved.difference_update(acc)
                    for qt in range(2):
                        i = qu * 2 + qt
                        c0 = qt * 129
                        s.op("dve", lambda e, c0=c0, acc=acc: e.reciprocal(stat[:, 30:31], ps[acc[0]][:, c0 + 128:c0 + 129]),
                             reads=[Rps[acc[0]]], writes=[Rstat])
                        s.op("dve", lambda e, c0=c0, acc=acc: e.reciprocal(stat[:, 31:32], ps[acc[1]][:, c0 + 128:c0 + 129]),
                             reads=[Rps[acc[1]]], writes=[Rstat])
                        s.op("dve", lambda e: e.tensor_tensor(stat[:, 32:33], stat[:, 31:32], stat[:, 25:26], ALU.mult),
                             reads=[Rstat], writes=[Rstat])
                        s.op("dve", lambda e, c0=c0, acc=acc: e.tensor_scalar(ya_f[:], ps[acc[0]][:, c0:c0 + 128], stat[:, 30:31], None, ALU.mult),
                             reads=[Rps[acc[0]], Rstat], writes=[Rya])
                        s.op("dve", lambda e, c0=c0, acc=acc: e.scalar_tensor_tensor(ya_f[:], ps[acc[1]][:, c0:c0 + 128], stat[:, 32:33], ya_f[:], ALU.mult, ALU.add),
                             reads=[Rps[acc[1]], Rstat, Rya], writes=[Rya])
                        s.op("act", lambda e: e.activation(junk_b[:, 0:128], ya_f[:], AF.Square, scale=1.0 / math.sqrt(128.0),
                                                           accum_out=stat[:, 33:34]),
                             reads=[Rya], writes=[Rjunk, Rstat])
                        s.op("dve", lambda e: e.tensor_scalar(stat[:, 34:35], stat[:, 33:34], 1e-6, None, ALU.add),
                             reads=[Rstat], writes=[Rstat])
                        s.op("act", lambda e: e.activation(stat[:, 34:35], stat[:, 34:35], AF.Sqrt), reads=[Rstat], writes=[Rstat])
                        s.op("dve", lambda e: e.reciprocal(stat[:, 35:36], stat[:, 34:35]), reads=[Rstat], writes=[Rstat])
                        s.op("dve", lambda e: e.scalar_tensor_tensor(ya_b[:], ya_f[:], stat[:, 35:36], subln08[:], ALU.mult, ALU.mult),
                             reads=[Rya, Rstat, Rsub], writes=[Ryab])
                        reserved.update(acc)
                        bt = bank()
                        reserved.difference_update(acc)
                        s.op("pe", lambda e, bt=bt: e.transpose(psb[bt][:, 0:128], ya_b[:], ident_b[:]),
                             reads=[Ryab, Rconst], writes=[Rps[bt]])
                        s.op("act", lambda e, bt=bt, h=h, i=i: e.copy(yT[:, h, i * 128:(i + 1) * 128], psb[bt][:, 0:128]),
                             reads=[Rps[bt]], writes=[RyT])

        def out_proj(W, actTv, Ract):
            pieces = [wload(W[:, hh * 512:(hh + 1) * 512], KC, 512) for hh in range(2)]
            for i in range(NT):
                bb = [bank(), bank()]
                for hh in range(2):
                    linear_tm(pieces[hh][0], pieces[hh][1], 512, actTv, Ract, KC, i, bb[hh])
                residual_update(i, bb, 0)

        Xc = carve(5120, NT * 8 * 256, BF16).rearrange("p (i g n) -> p i g n", i=NT, g=8)
        RXc = ares("Xc")
        chCS_b = sb("chCS_b", [128, 256], BF16)
        Rch = Res("chCS")

        def fnet():
            s.dma("pool", chCS_b[:], chCS, writes=[Rch])
            for i in range(NT):
                for gp in range(4):
                    b = bank()
                    for g2_ in range(2):
                        g = gp * 2 + g2_
                        s.op("pe", lambda e, b=b, g=g, g2_=g2_, i=i: e.matmul(
                            ps[b][:, g2_ * 256:(g2_ + 1) * 256], hT[:, g, i * 128:(i + 1) * 128], chCS_b[:], start=True, stop=True),
                            reads=[RhT, Rch], writes=[Rps[b]])
                    s.op("act", lambda e, b=b, gp=gp, i=i: e.copy(
                        Xc[:, i, gp * 2:gp * 2 + 2, :], ps[b][:].rearrange("p (g n) -> p g n", g=2)),
                        reads=[Rps[b]], writes=[RXc])
            for (t0, tn) in [(0, 384), (384, 384), (768, 384), (1152, 128)]:
                wc, rc = wload(dftC[:, t0:t0 + tn], NT, tn)
                wsn, rsn = wload(dftSn[:, t0:t0 + tn], NT, tn)
                for g in range(8):
                    b = bank()
                    for tc in range(NT):
                        s.op("pe", lambda e, b=b, g=g, tc=tc, tn=tn, wc=wc: e.matmul(
                            ps[b][:, 0:tn], Xc[:, tc, g, 0:128], wc[:, tc, 0:tn], start=(tc == 0), stop=False),
                            reads=[RXc, rc], writes=[Rps[b]])
                        s.op("pe", lambda e, b=b, g=g, tc=tc, tn=tn, wsn=wsn: e.matmul(
                            ps[b][:, 0:tn], Xc[:, tc, g, 128:256], wsn[:, tc, 0:tn], start=False, stop=(tc == NT - 1)),
                            reads=[RXc, rsn], writes=[Rps[b]])
                    s.op("act", lambda e, b=b, g=g, t0=t0, tn=tn: e.copy(yT[:, g, t0:t0 + tn], ps[b][:, 0:tn]),
                         reads=[Rps[b]], writes=[RyT])


        V_tm = carve(5120, NT * 512, BF16).rearrange("p (i n) -> p i n", i=NT)
        RVtm = ares("V_tm")
        YSb = carve(7680, NT * 512, BF16).rearrange("p (i n) -> p i n", i=NT)
        RYS = ares("YSb")
        Ystage = carve(10240, 128 * 16, F32)[0:64, :].rearrange("p (n k) -> p n k", k=16)
        RYst = ares("Ystage")
        Sst = [carve(12288, 512, F32), carve(12800, 512, F32)]
        RS = [ares("S0"), ares("S1")]
        tmpA = carve(13312, 512, F32)
        RtA = ares("tmpA")
        tmpB = carve(23824, 512, F32)
        RtB = ares("tmpB")
        w2t_b = carve(24336, 512, BF16)
        a2t_b = carve(24592, 512, BF16)
        g2_b = carve(24848, 512, BF16)
        Rlora = ares("lora")
        PT0 = 25104
        NPT = 12

        def pt(k, n=1):
            return carve(PT0 + 128 * k, 128 * n, F32)
        Rpt = [ares("pt%d" % k) for k in range(NPT)]
        lnx0_b = carve(26640, 512, BF16)
        lnx1_b = carve(26896, 512, BF16)
        Rlnx = ares("lnx")
        tb0 = junk_b[:, 0:128]
        tb1 = junk_b[:, 128:256]
        Rtb0 = Res("tb0")
        Rtb1 = Res("tb1")
        colx = sb("colx", [128, 64])
        Rcolx = Res("colx")
        tiny = sb("tiny", [128, 8])
        Rtiny = Res("tiny")
        BS = sb("BS", [128, NT, 8])
        RBS = Res("BS")
        wbf = [w[:].bitcast(F32) for w in wbuf]
        TW = wbf[0][:, 0:1024].rearrange("p (k n) -> p k n", k=8)
        TKK = wbf[0][:, 1024:2048].rearrange("p (k n) -> p k n", k=8)
        TNK = wbf[1][:, 0:1024].rearrange("p (k n) -> p k n", k=8)
        TKD = wbf[1][:, 1024:2048].rearrange("p (k n) -> p k n", k=8)
        TR2 = wbuf[2][:, 0:2048].rearrange("p (k n h) -> p k n h", k=8, h=2)
        tmpA_b = carve(10240, 512, BF16)
        RtAb = ares("tmpA_b")
        Sb16 = carve(10496, 512, BF16)
        RSb = ares("Sb16")
        CX = dict(cmu=0, nmu0=15, nmu1=30, omk1=45, keepL=49, keepR=54)

        def cx(name, c=0):
            o = CX[name] + c
            return colx[:, o:o + 1]

        def rwkv_setup():
            s.dma("pool", w2t_b, w2t_d, writes=[Rlora])
            s.dma("pool", a2t_b, a2t_d, writes=[Rlora])
            s.dma("pool", g2_b, g2_d, writes=[Rlora])
            s.dma("pool", lnx0_b, lnx_d[0, :].partition_broadcast(128), writes=[Rlnx])
            s.dma("pool", lnx1_b, lnx_d[1, :].partition_broadcast(128), writes=[Rlnx])
            mu0 = cp("mu0", 0, 15)
            mu1 = cp("mu1", 0, 15)
            s.op("dve", lambda e: e.tensor_tensor(colx[:, 0:15], mu0, mu1, ALU.add), reads=[Rcolp], writes=[Rcolx])
            s.op("dve", lambda e: e.tensor_scalar(colx[:, 0:15], colx[:, 0:15], -1.0, 1.0, ALU.mult, ALU.add),
                 reads=[Rcolx], writes=[Rcolx])
            s.op("dve", lambda e: e.tensor_scalar(colx[:, 15:30], mu0, -1.0, None, ALU.mult), reads=[Rcolp], writes=[Rcolx])
            s.op("dve", lambda e: e.tensor_scalar(colx[:, 30:45], mu1, -1.0, None, ALU.mult), reads=[Rcolp], writes=[Rcolx])
            s.op("dve", lambda e: e.tensor_scalar(colx[:, 45:49], cp("kvec1", 0, 4), -1.0, 1.0, ALU.mult, ALU.add),
                 reads=[Rcolp], writes=[Rcolx])
            s.op("dve", lambda e: e.tensor_scalar(colx[:, 49:54], rowp_sb[:, ROFF["bndL"]:ROFF["bndL"] + 5], -1.0, 1.0, ALU.mult, ALU.add),
                 reads=[Rrowp], writes=[Rcolx])
            s.op("dve", lambda e: e.tensor_scalar(colx[:, 54:59], rowp_sb[:, ROFF["bndR"]:ROFF["bndR"] + 5], -1.0, 1.0, ALU.mult, ALU.add),
                 reads=[Rrowp], writes=[Rcolx])
            s.op("dve", lambda e: e.memset(BS[:], 0.0), writes=[RBS])

        def shift(ch, ti, out, Rout):
            t0 = ti * 128
            f = fbT[:, ch, 1 + t0:1 + t0 + 128]
            fp = fbT[:, ch, t0:t0 + 128]
            fn = fbT[:, ch, 2 + t0:2 + t0 + 128]
            rr = [RfbT, Rcolp, Rcolx]
            s.op("dve", lambda e: e.tensor_scalar(out, f, cx("cmu", ch), None, ALU.mult), reads=rr, writes=[Rout])
            s.op("dve", lambda e: e.scalar_tensor_tensor(out, fp, cp("mu0", ch), out, ALU.mult, ALU.add),
                 reads=rr + [Rout], writes=[Rout])
            s.op("dve", lambda e: e.scalar_tensor_tensor(out, fn, cp("mu1", ch), out, ALU.mult, ALU.add),
                 reads=rr + [Rout], writes=[Rout])
            u = ti // 2
            if ti % 2 == 0:
                bl = rowp_sb[:, ROFF["bndL"] + u:ROFF["bndL"] + u + 1]
                s.op("dve", lambda e: e.tensor_tensor(tiny[:, 0:1], fbT[:, ch, t0:t0 + 1], bl, ALU.mult),
                     reads=[RfbT, Rrowp], writes=[Rtiny])
                s.op("dve", lambda e: e.scalar_tensor_tensor(out[:, 0:1], tiny[:, 0:1], cx("nmu0", ch), out[:, 0:1], ALU.mult, ALU.add),
                     reads=[Rtiny, Rcolx, Rout], writes=[Rout])
            else:
                br = rowp_sb[:, ROFF["bndR"] + u:ROFF["bndR"] + u + 1]
                s.op("dve", lambda e: e.tensor_tensor(tiny[:, 1:2], fbT[:, ch, 1 + t0 + 128:2 + t0 + 128], br, ALU.mult),
                     reads=[RfbT, Rrowp], writes=[Rtiny])
                s.op("dve", lambda e: e.scalar_tensor_tensor(out[:, 127:128], tiny[:, 1:2], cx("nmu1", ch), out[:, 127:128], ALU.mult, ALU.add),
                     reads=[Rtiny, Rcolx, Rout], writes=[Rout])

        def rwkv_prepass():
            for ti in range(NT):
                for c in range(4):
                    shift(8 + c, ti, pt(c), Rpt[c])
                    s.op("act", lambda e, c=c: e.copy(xn_b[:, c * 128:(c + 1) * 128], pt(c)), reads=[Rpt[c]], writes=[Rxn])
                b = bank()
                for c in range(4):
                    s.op("pe", lambda e, b=b, c=c: e.transpose(psb[b][:, c * 128:(c + 1) * 128], xn_b[:, c * 128:(c + 1) * 128], ident_b[:]),
                         reads=[Rxn, Rconst], writes=[Rps[b]])
                s.op("act", lambda e, b=b, ti=ti: e.copy(V_tm[:, ti, :], psb[b][:, 0:512]), reads=[Rps[b]], writes=[RVtm])

        def prep(d, ti):
            rev = (d == 1)

            def tab(T3, c):
                v = T3[:, d * 4 + c, :]
                return v[:, ::-1] if rev else v
            tabs_w = [Rw[0], Rw[1], Rw[2]]
            shift(12, ti, pt(0), Rpt[0])
            shift(13, ti, pt(1), Rpt[1])
            s.op("act", lambda e: e.activation(tb0, pt(0), AF.Tanh), reads=[Rpt[0]], writes=[Rtb0])
            s.op("act", lambda e: e.copy(tb1, pt(1)), reads=[Rpt[1]], writes=[Rtb1])
            lo, hi = 64 * d, 64 * d + 64
            for c in range(4):
                b = bank()
                s.op("pe", lambda e, b=b, c=c: e.matmul(ps[b][:, 0:128], w2t_b[lo:hi, c * 128:(c + 1) * 128], tb0[lo:hi, :], start=True, stop=True),
                     reads=[Rlora, Rtb0], writes=[Rps[b]])
                s.op("act", lambda e, b=b, c=c: e.activation(pt(2), ps[b][:, 0:128], AF.Sigmoid, bias=cp("w0_%d" % d, c)),
                     reads=[Rps[b], Rcolp], writes=[Rpt[2]])
                s.op("act", lambda e, c=c: e.activation(tab(TW, c), pt(2), AF.Exp, scale=-EXPM05),
                     reads=[Rpt[2]], writes=[Rw[0]])
                b = bank()
                s.op("pe", lambda e, b=b, c=c: e.matmul(ps[b][:, 0:128], a2t_b[lo:hi, c * 128:(c + 1) * 128], tb1[lo:hi, :], start=True, stop=True),
                     reads=[Rlora, Rtb1], writes=[Rps[b]])
                s.op("act", lambda e, b=b, c=c: e.activation(pt(3), ps[b][:, 0:128], AF.Sigmoid, bias=cp("a0_%d" % d, c)),
                     reads=[Rps[b], Rcolp], writes=[Rpt[3]])
                shift(4 + c, ti, pt(4), Rpt[4])
                s.op("dve", lambda e, c=c: e.tensor_scalar(pt(5), pt(4), cp("kvec0", c), None, ALU.mult),
                     reads=[Rpt[4], Rcolp], writes=[Rpt[5]])
                s.op("act", lambda e: e.activation(pt(6), pt(5), AF.Square), reads=[Rpt[5]], writes=[Rpt[6]])
                b = bank()
                s.op("pe", lambda e, b=b: e.matmul(ps[b][:, 0:128], bones_f[:], pt(6), start=True, stop=True),
                     reads=[Rconst, Rpt[6]], writes=[Rps[b]])
                s.op("dve", lambda e, b=b: e.tensor_scalar(pt(7), ps[b][:, 0:128], 1e-12, None, ALU.add),
                     reads=[Rps[b]], writes=[Rpt[7]])
                s.op("act", lambda e: e.activation(pt(7), pt(7), AF.Sqrt), reads=[Rpt[7]], writes=[Rpt[7]])
                s.op("dve", lambda e: e.reciprocal(pt(7), pt(7)), reads=[Rpt[7]], writes=[Rpt[7]])
                s.op("dve", lambda e: e.tensor_tensor(pt(8), pt(5), pt(7), ALU.mult), reads=[Rpt[5], Rpt[7]], writes=[Rpt[8]])
                s.op("act", lambda e, c=c: e.copy(tab(TKK, c), pt(8)), reads=[Rpt[8]], writes=[Rw[0]])
                s.op("dve", lambda e, c=c: e.scalar_tensor_tensor(tab(TNK, c), pt(8), -1.0, pt(3), ALU.mult, ALU.mult),
                     reads=[Rpt[8], Rpt[3]], writes=[Rw[1]])
                s.op("dve", lambda e, c=c: e.tensor_scalar(pt(9), pt(3), cp("kvec1", c), cx("omk1", c), ALU.mult, ALU.add),
                     reads=[Rpt[3], Rcolp, Rcolx], writes=[Rpt[9]])
                s.op("dve", lambda e: e.tensor_tensor(pt(9), pt(4), pt(9), ALU.mult), reads=[Rpt[4], Rpt[9]], writes=[Rpt[9]])
                s.op("act", lambda e, c=c: e.copy(tab(TKD, c), pt(9)), reads=[Rpt[9]], writes=[Rw[1]])
                shift(c, ti, pt(10), Rpt[10])
                for h2 in range(2):
                    o = TR2[:, d * 4 + c, :, h2]
                    if rev:
                        o = o[:, ::-1]
                    s.op("dve", lambda e, o=o, h2=h2: e.tensor_scalar(o, pt(10), halfsel_f[:, h2:h2 + 1], None, ALU.mult),
                         reads=[Rpt[10], Rconst], writes=[Rw[2]])
                s.op("dve", lambda e: e.tensor_tensor(pt(6), pt(10), pt(9), ALU.mult), reads=[Rpt[10], Rpt[9]], writes=[Rpt[6]])
                s.op("dve", lambda e, c=c: e.tensor_scalar(pt(6), pt(6), cp("kvec2", c), None, ALU.mult),
                     reads=[Rpt[6], Rcolp], writes=[Rpt[6]])
                b = bank()
                s.op("pe", lambda e, b=b: e.matmul(ps[b][:, 0:2], pt(6), halfsel_f[:, 0:2], start=True, stop=True),
                     reads=[Rpt[6], Rconst], writes=[Rps[b]])
                s.op("dve", lambda e, b=b, c=c, ti=ti: e.tensor_tensor(BS[:, ti, c * 2:c * 2 + 2], BS[:, ti, c * 2:c * 2 + 2], ps[b][:, 0:2], ALU.add),
                     reads=[Rps[b], RBS], writes=[RBS])

        yb = xn_b[:, 0:512]

        def finalize(ti, by):
            ys = pt(0, 4)
            Rys = [Rpt[0], Rpt[1], Rpt[2], Rpt[3]]
            t2 = pt(4, 4)
            Rt2 = [Rpt[4], Rpt[5], Rpt[6], Rpt[7]]
            ys3 = ys.rearrange("p (h i) -> p h i", h=8)
            t23 = t2.rearrange("p (h i) -> p h i", h=8)
            s.op("dve", lambda e: e.tensor_tensor(ys, YSb[:, ti, :], ps[by][:], ALU.add), reads=[RYS, Rps[by]], writes=Rys)
            s.op("dve", lambda e: e.reduce_sum(tiny[:, 0:8], ys3, axis=AX.X), reads=Rys, writes=[Rtiny])
            s.op("dve", lambda e: e.tensor_scalar(tiny[:, 0:8], tiny[:, 0:8], -1.0 / 64.0, None, ALU.mult), reads=[Rtiny], writes=[Rtiny])
            s.op("dve", lambda e: e.tensor_tensor(ys3, ys3, tiny[:, 0:8].unsqueeze(2).to_broadcast([128, 8, 64]), ALU.add),
                 reads=Rys + [Rtiny], writes=Rys)
            s.op("dve", lambda e: e.tensor_tensor(t2, ys, ys, ALU.mult), reads=Rys, writes=Rt2)
            s.op("dve", lambda e: e.reduce_sum(stat[:, 40:48], t23, axis=AX.X), reads=Rt2, writes=[Rstat])
            s.op("dve", lambda e: e.tensor_scalar(stat[:, 40:48], stat[:, 40:48], 1.0 / 64.0, 64e-5, ALU.mult, ALU.add),
                 reads=[Rstat], writes=[Rstat])
            s.op("act", lambda e: e.activation(stat[:, 40:48], stat[:, 40:48], AF.Sqrt), reads=[Rstat], writes=[Rstat])
            s.op("dve", lambda e: e.reciprocal(stat[:, 48:56], stat[:, 40:48]), reads=[Rstat], writes=[Rstat])
            s.op("dve", lambda e: e.tensor_tensor(ys3, ys3, stat[:, 48:56].unsqueeze(2).to_broadcast([128, 8, 64]), ALU.mult),
                 reads=Rys + [Rstat], writes=Rys)
            s.op("dve", lambda e: e.tensor_tensor(ys, ys, lnx0_b, ALU.mult), reads=Rys + [Rlnx], writes=Rys)
            s.op("dve", lambda e: e.tensor_tensor(ys, ys, lnx1_b, ALU.add), reads=Rys + [Rlnx], writes=Rys)
            s.op("dve", lambda e: e.tensor_tensor(t23, V_tm[:, ti, :].rearrange("p (h i) -> p h i", h=8),
                                                  BS[:, ti, :].unsqueeze(2).to_broadcast([128, 8, 64]), ALU.mult),
                 reads=[RVtm, RBS], writes=Rt2)
            s.op("dve", lambda e: e.tensor_tensor(ys, ys, t2, ALU.add), reads=Rys + Rt2, writes=Rys)
            shift(14, ti, pt(8), Rpt[8])
            s.op("act", lambda e: e.activation(tb0, pt(8), AF.Sigmoid), reads=[Rpt[8]], writes=[Rtb0])
            bg = bank()
            s.op("pe", lambda e, bg=bg: e.matmul(ps[bg][:], tb0, g2_b, start=True, stop=True),
                 reads=[Rtb0, Rlora], writes=[Rps[bg]])
            s.op("dve", lambda e, bg=bg: e.tensor_tensor(yb, ys, ps[bg][:], ALU.mult), reads=Rys + [Rps[bg]], writes=[Rxn])
            bt = bank()
            for c in range(4):
                s.op("pe", lambda e, bt=bt, c=c: e.transpose(psb[bt][:, c * 128:(c + 1) * 128], yb[:, c * 128:(c + 1) * 128], ident_b[:]),
                     reads=[Rxn, Rconst], writes=[Rps[bt]])
            s.op("act", lambda e, bt=bt, ti=ti: e.copy(yT[:, 4:8, ti * 128:(ti + 1) * 128],
                                                        psb[bt][:, 0:512].rearrange("p (c t) -> p c t", c=4)),
                 reads=[Rps[bt]], writes=[RyT])

        def rwkv_scan(nrounds=NT):
            S8 = [x_.rearrange("p (k i) -> p k i", k=8) for x_ in Sst]
            tA8 = tmpA.rearrange("p (k i) -> p k i", k=8)
            tB8 = tmpB.rearrange("p (k i) -> p k i", k=8)

            def bc(T3, n):
                return T3[:, :, n].unsqueeze(2).to_broadcast([128, 8, 64])
            for r in range(nrounds):
                tf, tbk = r, NT - 1 - r
                prep(0, tf)
                prep(1, tbk)
                if tf % 2 == 0:
                    u = tf // 2
                    if u == 0:
                        s.dma("sp", Sst[0][:, 0:256], initS[0], writes=[RS[0]])
                    else:
                        s.op("dve", lambda e, u=u: e.tensor_scalar(Sst[0][:, 0:256], Sst[0][:, 0:256], cx("keepL", u), None, ALU.mult),
                             reads=[RS[0], Rcolx], writes=[RS[0]])
                if tbk % 2 == 1:
                    u = tbk // 2
                    if u == 4:
                        s.op("dve", lambda e: e.memset(Sst[0][:, 256:512], 0.0), writes=[RS[0]])
                    elif u == 3:
                        s.dma("sp", tmpB[:, 0:256], initS[1], writes=[RtB])
                        s.op("dve", lambda e, u=u: e.scalar_tensor_tensor(Sst[0][:, 256:512], Sst[0][:, 256:512], cx("keepR", u),
                                                                          tmpB[:, 0:256], ALU.mult, ALU.add),
                             reads=[RS[0], Rcolx, RtB], writes=[RS[0]])
                    else:
                        s.op("dve", lambda e, u=u: e.tensor_scalar(Sst[0][:, 256:512], Sst[0][:, 256:512], cx("keepR", u), None, ALU.mult),
                             reads=[RS[0], Rcolx], writes=[RS[0]])
                by_ = None
                pending = []

                def flush():
                    for f_ in pending:
                        f_()
                    del pending[:]
                for n in range(128):
                    ci, ni = n % 2, (n + 1) % 2
                    Sc, Sn = S8[ci], S8[ni]
                    if n % 32 == 0:
                        by_ = bank()
                        reserved.add(by_)
                    tAb8 = tmpA_b.rearrange("p (k i) -> p k i", k=8)
                    s.op("dve", lambda e, n=n, Sc=Sc, tAb8=tAb8: e.tensor_tensor(tAb8, Sc, bc(TKK, n), ALU.mult),
                         reads=[RS[ci], Rw[0]], writes=[RtAb])
                    s.op("dve", lambda e, n=n, Sc=Sc, Sn=Sn: e.tensor_tensor(Sn, Sc, bc(TW, n), ALU.mult),
                         reads=[RS[ci], Rw[0]], writes=[RS[ni]])
                    bv = bank()
                    for d in range(2):
                        tile_d = tf if d == 0 else tbk
                        row = n if d == 0 else 127 - n
                        for h2 in range(2):
                            s.op("pe", lambda e, bv=bv, d=d, h2=h2, tile_d=tile_d, row=row: e.matmul(
                                ps[bv][64 * h2:64 * h2 + 64, d * 256:(d + 1) * 256].rearrange("p (c i) -> p c i", c=4),
                                ident_b[:, row:row + 1].to_broadcast([128, 64]),
                                V_tm[:, tile_d, :].rearrange("p (c h i) -> p c h i", c=4, h=2)[:, :, h2, :],
                                start=True, stop=True),
                                reads=[RVtm, Rconst], writes=[Rps[bv]])
                    bs_ = bank()
                    s.op("pe", lambda e, bs_=bs_: e.matmul(ps[bs_][:], bones_b[:], tmpA_b, start=True, stop=True),
                         reads=[RtAb, Rconst], writes=[Rps[bs_]])
                    flush()
                    s.op("dve", lambda e, n=n, bv=bv: e.tensor_tensor(tB8, ps[bv][:].rearrange("p (k i) -> p k i", k=8), bc(TKD, n), ALU.mult),
                         reads=[Rps[bv], Rw[1]], writes=[RtB])
                    s.op("dve", lambda e, ni=ni: e.tensor_tensor(Sst[ni], Sst[ni], tmpB, ALU.add),
                         reads=[RS[ni], RtB], writes=[RS[ni]])
                    s.op("dve", lambda e, n=n, bs_=bs_: e.tensor_tensor(tA8, ps[bs_][:].rearrange("p (k i) -> p k i", k=8), bc(TNK, n), ALU.mult),
                         reads=[Rps[bs_], Rw[1]], writes=[RtA])
                    s.op("dve", lambda e, ni=ni: e.tensor_tensor(Sst[ni], Sst[ni], tmpA, ALU.add),
                         reads=[RS[ni], RtA], writes=[RS[ni]])
                    s.op("act", lambda e, ni=ni: e.copy(Sb16, Sst[ni]), reads=[RS[ni]], writes=[RSb])
                    nl = n % 32

                    def ymm(by_=by_, nl=nl, n=n):
                        for dc in range(8):
                            s.op("pe", lambda e, dc=dc: e.matmul(
                                ps[by_][0:64, nl * 16 + dc * 2:nl * 16 + dc * 2 + 2], Sb16[:, dc * 64:(dc + 1) * 64], TR2[:, dc, n, :],
                                start=True, stop=True),
                                reads=[RSb, Rw[2]], writes=[Rps[by_]])
                    pending.append(ymm)
                    if nl == 31:
                        flush()
                        n0 = n - 31
                        src = ps[by_][0:64, :].rearrange("p (n k) -> p n k", k=16)
                        s.op("act", lambda e, src=src, n0=n0: e.copy(Ystage[:, n0:n0 + 32, 0:8], src[:, :, 0:8]),
                             reads=[Rps[by_]], writes=[RYst])
                        s.op("act", lambda e, src=src, n0=n0: e.copy(Ystage[:, 96 - n0:128 - n0, 8:16][:, ::-1, :], src[:, :, 8:16]),
                             reads=[Rps[by_]], writes=[RYst])
                        reserved.discard(by_)
                if tf % 2 == 1:
                    s.dma("sp", s_out[tf // 2, 0], Sst[0][:, 0:256], reads=[RS[0]])
                if tbk % 2 == 0:
                    s.dma("sp", s_out[tbk // 2, 1], Sst[0][:, 256:512], reads=[RS[0]])
                for d in range(2):
                    tile_d = tf if d == 0 else tbk
                    byy = bank()
                    for k8 in range(8):
                        s.op("pe", lambda e, byy=byy, k8=k8, d=d: e.matmul(
                            ps[byy][:, k8 * 64:(k8 + 1) * 64], Ystage[:, :, d * 8 + k8], ident_f[0:64, 0:64], start=True, stop=True),
                            reads=[RYst, Rconst], writes=[Rps[byy]])
                    first = (r < 5)
                    if first:
                        s.op("act", lambda e, byy=byy, tile_d=tile_d: e.copy(YSb[:, tile_d, :], ps[byy][:]),
                             reads=[Rps[byy]], writes=[RYS])
                    else:
                        finalize(tile_d, byy)


        def wbw(i, off_w, nelem):
            return wbuf[i][:, 2 * off_w:2 * off_w + nelem]
        Ast = carve(13200, 512, F32)
        RAst = ares("Ast")
        Abf = carve(13712, 512, BF16)
        RAbf = ares("Abf")
        WS = []
        fB = 14208 + 8 * 641
        for wsi in range(2):
            d_ = {}
            if wsi == 0:
                for k_, nm in enumerate(["Lt0", "Lt1", "L0", "L1", "Z0"]):
                    d_[nm] = carve(10240 + 512 * k_, 512, F32)
                d_["MT"] = carve(12800, 256, BF16)
                d_["RT"] = carve(12928, 512, BF16)
                d_["Z1"] = wbuf[0][:, 0:1024].bitcast(F32)
                for k_, nm in enumerate(["Lakt", "Mrbt", "Mrkt", "BKtm"]):
                    d_[nm] = wbw(0, 512 + 256 * k_, 512)
                d_["Zfb"] = carve(10240, 512, BF16)
            else:
                for k_, nm in enumerate(["Lt0", "Lt1", "L0", "L1", "Z0"]):
                    d_[nm] = carve(fB + 512 * k_, 512, F32)
                d_["Z1"] = wbuf[1][:, 0:1024].bitcast(F32)
                d_["Lakt"] = wbw(0, 1536, 512)
                d_["Mrbt"] = wbw(0, 1792, 512)
                d_["Mrkt"] = wbw(1, 512, 512)
                d_["MT"] = wbw(1, 768, 256)
                d_["BKtm"] = carve(23824, 512, BF16)
                d_["RT"] = carve(24080, 512, BF16)
                d_["Zfb"] = carve(fB, 512, BF16)
            for nm in ["Lt0", "Lt1", "L0", "L1", "Z0", "Z1", "Lakt", "Mrbt", "Mrkt", "BKtm", "MT", "RT"]:
                d_["R" + nm] = ares("ws%d_%s" % (wsi, nm))
            d_["RZfb"] = d_["RLt0"]
            WS.append(d_)
        masks = wbw(1, 896, 4 * 4 * 128).rearrange("p (m q t) -> p m q t", m=4, q=4)
        Rmask = ares("masks")
        TBL = []
        for par in range(2):
            d_ = {}
            for k_, nm in enumerate(["RH", "AH", "BH", "KH"]):
                d_[nm] = wbw(2, par * 1024 + 256 * k_, 512).rearrange("p (c t) -> p c t", c=4)
                d_["R" + nm] = ares("tb%d_%s" % (par, nm))
            TBL.append(d_)
        pcs = sb("pcs", [128, 2, 4])
        Rpcs = [Res("pcs0"), Res("pcs1")]

        def prep_chunk(d, ti, par):
            TB = TBL[par]
            shift(12, ti, pt(0), Rpt[0])
            shift(13, ti, pt(1), Rpt[1])
            s.op("act", lambda e: e.activation(tb0, pt(0), AF.Tanh), reads=[Rpt[0]], writes=[Rtb0])
            s.op("act", lambda e: e.copy(tb1, pt(1)), reads=[Rpt[1]], writes=[Rtb1])
            lo, hi = 64 * d, 64 * d + 64
            for c in range(4):
                b = bank()
                s.op("pe", lambda e, b=b, c=c: e.matmul(ps[b][:, 0:128], w2t_b[lo:hi, c * 128:(c + 1) * 128], tb0[lo:hi, :], start=True, stop=True),
                     reads=[Rlora, Rtb0], writes=[Rps[b]])
                s.op("act", lambda e, b=b, c=c: e.activation(pt(2), ps[b][:, 0:128], AF.Sigmoid, bias=cp("w0_%d" % d, c)),
                     reads=[Rps[b], Rcolp], writes=[Rpt[2]])
                s.op("dve", lambda e: e.tensor_scalar(pt(2), pt(2), -EXPM05, None, ALU.mult), reads=[Rpt[2]], writes=[Rpt[2]])
                if d == 0:
                    s.op("dve", lambda e: e.tensor_tensor_scan(pt(0), pt(11), pt(2), 0.0, ALU.mult, ALU.add),
                         reads=[Rpt[2], Rpt[11]], writes=[Rpt[0]])
                else:
                    s.op("dve", lambda e: e.tensor_tensor_scan(pt(0)[:, ::-1], pt(11), pt(2)[:, ::-1], 0.0, ALU.mult, ALU.add),
                         reads=[Rpt[2], Rpt[11]], writes=[Rpt[0]])
                s.op("act", lambda e: e.activation(pt(1), pt(0), AF.Exp), reads=[Rpt[0]], writes=[Rpt[1]])
                s.op("dve", lambda e: e.tensor_tensor(pt(2), pt(0), pt(2), ALU.subtract), reads=[Rpt[0], Rpt[2]], writes=[Rpt[2]])
                s.op("act", lambda e: e.activation(pt(2), pt(2), AF.Exp), reads=[Rpt[2]], writes=[Rpt[2]])
                s.op("act", lambda e: e.activation(pt(0), pt(0), AF.Exp, scale=-1.0), reads=[Rpt[0]], writes=[Rpt[0]])
                pcol = 127 if d == 0 else 0
                s.op("dve", lambda e, c=c, pcol=pcol: e.tensor_copy(pcs[:, par, c:c + 1], pt(1)[:, pcol:pcol + 1]),
                     reads=[Rpt[1]], writes=[Rpcs[par]])
                b = bank()
                s.op("pe", lambda e, b=b, c=c: e.matmul(ps[b][:, 0:128], a2t_b[lo:hi, c * 128:(c + 1) * 128], tb1[lo:hi, :], start=True, stop=True),
                     reads=[Rlora, Rtb1], writes=[Rps[b]])
                s.op("act", lambda e, b=b, c=c: e.activation(pt(3), ps[b][:, 0:128], AF.Sigmoid, bias=cp("a0_%d" % d, c)),
                     reads=[Rps[b], Rcolp], writes=[Rpt[3]])
                shift(4 + c, ti, pt(4), Rpt[4])
                s.op("dve", lambda e, c=c: e.tensor_scalar(pt(5), pt(4), cp("kvec0", c), None, ALU.mult),
                     reads=[Rpt[4], Rcolp], writes=[Rpt[5]])
                s.op("act", lambda e: e.activation(pt(6), pt(5), AF.Square), reads=[Rpt[5]], writes=[Rpt[6]])
                b = bank()
                s.op("pe", lambda e, b=b: e.matmul(ps[b][:, 0:128], bones_f[:], pt(6), start=True, stop=True),
                     reads=[Rconst, Rpt[6]], writes=[Rps[b]])
                s.op("dve", lambda e, b=b: e.tensor_scalar(pt(7), ps[b][:, 0:128], 1e-12, None, ALU.add),
                     reads=[Rps[b]], writes=[Rpt[7]])
                s.op("act", lambda e: e.activation(pt(7), pt(7), AF.Sqrt), reads=[Rpt[7]], writes=[Rpt[7]])
                s.op("dve", lambda e: e.reciprocal(pt(7), pt(7)), reads=[Rpt[7]], writes=[Rpt[7]])
                s.op("dve", lambda e: e.tensor_tensor(pt(8), pt(5), pt(7), ALU.mult), reads=[Rpt[5], Rpt[7]], writes=[Rpt[8]])
                s.op("dve", lambda e, c=c: e.scalar_tensor_tensor(TB["AH"][:, c, :], pt(8), -1.0, pt(2), ALU.mult, ALU.mult),
                     reads=[Rpt[8], Rpt[2]], writes=[TB["RAH"]])
                s.op("dve", lambda e: e.tensor_tensor(pt(6), pt(8), pt(3), ALU.mult), reads=[Rpt[8], Rpt[3]], writes=[Rpt[6]])
                s.op("dve", lambda e, c=c: e.tensor_tensor(TB["BH"][:, c, :], pt(6), pt(0), ALU.mult),
                     reads=[Rpt[6], Rpt[0]], writes=[TB["RBH"]])
                s.op("dve", lambda e, c=c: e.tensor_scalar(pt(9), pt(3), cp("kvec1", c), cx("omk1", c), ALU.mult, ALU.add),
                     reads=[Rpt[3], Rcolp, Rcolx], writes=[Rpt[9]])
                s.op("dve", lambda e: e.tensor_tensor(pt(9), pt(4), pt(9), ALU.mult), reads=[Rpt[4], Rpt[9]], writes=[Rpt[9]])
                s.op("dve", lambda e, c=c: e.tensor_tensor(TB["KH"][:, c, :], pt(9), pt(0), ALU.mult),
                     reads=[Rpt[9], Rpt[0]], writes=[TB["RKH"]])
                shift(c, ti, pt(10), Rpt[10])
                s.op("dve", lambda e, c=c: e.tensor_tensor(TB["RH"][:, c, :], pt(10), pt(1), ALU.mult),
                     reads=[Rpt[10], Rpt[1]], writes=[TB["RRH"]])
                s.op("dve", lambda e: e.tensor_tensor(pt(6), pt(10), pt(9), ALU.mult), reads=[Rpt[10], Rpt[9]], writes=[Rpt[6]])
                s.op("dve", lambda e, c=c: e.tensor_scalar(pt(6), pt(6), cp("kvec2", c), None, ALU.mult),
                     reads=[Rpt[6], Rcolp], writes=[Rpt[6]])
                b = bank()
                s.op("pe", lambda e, b=b: e.matmul(ps[b][:, 0:2], pt(6), halfsel_f[:, 0:2], start=True, stop=True),
                     reads=[Rpt[6], Rconst], writes=[Rps[b]])
                s.op("dve", lambda e, b=b, c=c, ti=ti: e.tensor_tensor(BS[:, ti, c * 2:c * 2 + 2], BS[:, ti, c * 2:c * 2 + 2], ps[b][:, 0:2], ALU.add),
                     reads=[Rps[b], RBS], writes=[RBS])

        def rwkv_chunked(ntiles=NT):
            cut = int(os.environ.get("CUT", 99))
            for m in range(4):
                for q in range(4):
                    s.dma("pool", masks[:, m, q, :], cmat[4 + m], writes=[Rmask])
            s.op("dve", lambda e: e.memset(pt(11), 1.0), writes=[Rpt[11]])
            MSK = {0: (0, 1, 2), 1: (1, 0, 3)}
            gctr = [0]
            pctr2 = [0]
            for d in range(2):
                order = list(range(NT)) if d == 0 else list(range(NT - 1, -1, -1))
                order = order[:ntiles]
                Ad = Ast[:, d * 256:(d + 1) * 256]
                Ad3 = Ad.rearrange("p (c i) -> p c i", c=4)
                Abd = Abf[:, d * 256:(d + 1) * 256]
                Abd3 = Abd.rearrange("p (c i) -> p c i", c=4)
                mTs, mS, mTi = MSK[d]
                def tile_body(ti, d=d, Ad=Ad, Ad3=Ad3, Abd=Abd, Abd3=Abd3, mTs=mTs, mS=mS, mTi=mTi):
                    par = pctr2[0] % 2
                    pctr2[0] += 1
                    TB = TBL[par]
                    prep_chunk(d, ti, par)
                    u = ti // 2
                    isstart = (ti % 2 == 0) if d == 0 else (ti % 2 == 1)
                    if isstart:
                        if d == 0:
                            if u == 0:
                                s.dma("sp", Ad, initS[0], writes=[RAst])
                            else:
                                s.op("dve", lambda e, u=u: e.tensor_scalar(Ad, Ad, cx("keepL", u), None, ALU.mult),
                                     reads=[RAst, Rcolx], writes=[RAst])
                        else:
                            if u == 4:
                                s.op("dve", lambda e: e.memset(Ad, 0.0), writes=[RAst])
                            elif u == 3:
                                s.dma("sp", stage_f[:, 0, 0:256], initS[1], writes=[Rstage[0]])
                                s.op("dve", lambda e, u=u: e.scalar_tensor_tensor(Ad, Ad, cx("keepR", u), stage_f[:, 0, 0:256], ALU.mult, ALU.add),
                                     reads=[RAst, Rcolx, Rstage[0]], writes=[RAst])
                            else:
                                s.op("dve", lambda e, u=u: e.tensor_scalar(Ad, Ad, cx("keepR", u), None, ALU.mult),
                                     reads=[RAst, Rcolx], writes=[RAst])
                        s.op("act", lambda e: e.copy(Abd, Ad), reads=[RAst], writes=[RAbf])
                    if cut < 2:
                        return
                    fy = bank()
                    reserved.add(fy)
                    first_fy = [True]
                    def group_body(g):
                        W = WS[g]
                        heads = [(c_, g) for c_ in range(4)]

                        def fm(T3, c, h2):
                            return T3[64 * h2:64 * h2 + 64, c, :]

                        def q4(tl):
                            return tl.rearrange("p (q t) -> p q t", q=4)
                        specs = [("Lt0", "BH", "AH", mTs), ("L0", "AH", "BH", mS), ("Lakt", "KH", "AH", mTs),
                                 ("Mrbt", "BH", "RH", mTi), ("Mrkt", "KH", "RH", mTi)]
                        lim = int(os.environ.get("LIM", 99))
                        for (dst, la, rb, mk) in specs[:lim]:
                            b = bank()
                            for q, (c, h2) in enumerate(heads[:int(os.environ.get("LIMH", 4))]):
                                s.op("pe", lambda e, b=b, q=q, c=c, h2=h2, la=la, rb=rb: e.matmul(
                                    ps[b][:, q * 128:(q + 1) * 128], fm(TB[la], c, h2), fm(TB[rb], c, h2), start=True, stop=True),
                                    reads=[TB["R" + la], TB["R" + rb]], writes=[Rps[b]])
                            if os.environ.get("NOEV") == "1":
                                continue
                            if os.environ.get("NOEV") == "2":
                                s.op("dve", lambda e, b=b, dst=dst, mk=mk: e.tensor_copy(W[dst], ps[b][:]),
                                     reads=[Rps[b], Rmask], writes=[W["R" + dst]])
                                continue
                            s.op("dve", lambda e, b=b, dst=dst, mk=mk: e.tensor_tensor(
                                q4(W[dst]), ps[b][:].rearrange("p (q t) -> p q t", q=4), masks[:, mk, :, :], ALU.mult),
                                reads=[Rps[b], Rmask], writes=[W["R" + dst]])
                        yield
                        if cut < 3:
                            return
                        b = bank()
                        for q, (c, h2) in enumerate(heads):
                            idn = ident_b[64 * h2:64 * h2 + 64, 64 * h2:64 * h2 + 64]
                            for k_, nm in enumerate(["AH", "BH", "KH"]):
                                s.op("pe", lambda e, b=b, q=q, c=c, h2=h2, nm=nm, k_=k_, idn=idn: e.transpose(
                                    psb[b][:, k_ * 256 + q * 64:k_ * 256 + q * 64 + 64], fm(TB[nm], c, h2), idn),
                                    reads=[TB["R" + nm], Rconst], writes=[Rps[b]])
                        Z0q = q4(W["Z0"])
                        BKq = q4(W["BKtm"])
                        s.op("act", lambda e, b=b, Z0q=Z0q: e.copy(Z0q[:, :, 0:64], psb[b][:, 0:256].rearrange("p (q j) -> p q j", q=4)),
                             reads=[Rps[b]], writes=[W["RZ0"]])
                        s.op("act", lambda e, b=b, BKq=BKq: e.copy(BKq[:, :, 0:64], psb[b][:, 256:512].rearrange("p (q j) -> p q j", q=4)),
                             reads=[Rps[b]], writes=[W["RBKtm"]])
                        s.op("act", lambda e, b=b, BKq=BKq: e.copy(BKq[:, :, 64:128], psb[b][:, 512:768].rearrange("p (q j) -> p q j", q=4)),
                             reads=[Rps[b]], writes=[W["RBKtm"]])

                        def Vq(c, h2):
                            return V_tm[:, ti, (c * 2 + h2) * 64:(c * 2 + h2) * 64 + 64]
                        if cut < 4:
                            return
                        b = bank()
                        Lakq = q4(W["Lakt"])
                        for q, (c, h2) in enumerate(heads):
                            s.op("pe", lambda e, b=b, q=q, c=c, h2=h2: e.matmul(
                                ps[b][:, q * 64:(q + 1) * 64], Lakq[:, q, :], Vq(c, h2), start=True, stop=True),
                                reads=[W["RLakt"], RVtm], writes=[Rps[b]])
                        s.op("act", lambda e, b=b, Z0q=Z0q: e.copy(Z0q[:, :, 64:128], ps[b][:, 0:256].rearrange("p (q j) -> p q j", q=4)),
                             reads=[Rps[b]], writes=[W["RZ0"]])
                        yield
                        if cut < 5:
                            return
                        zc, lc = 0, 0
                        for k in range(7):
                            Zc, Zn = W["Z%d" % zc], W["Z%d" % (1 - zc)]
                            RZc, RZn = W["RZ%d" % zc], W["RZ%d" % (1 - zc)]
                            Ltc, Lc = W["Lt%d" % lc], W["L%d" % lc]
                            RLtc, RLc = W["RLt%d" % lc], W["RL%d" % lc]
                            b = bank()
                            for q in range(4):
                                s.op("pe", lambda e, b=b, q=q, Zc=Zc, Ltc=Ltc: e.matmul(
                                    ps[b][:, q * 128:(q + 1) * 128], q4(Ltc)[:, q, :], q4(Zc)[:, q, :], start=True, stop=True),
                                    reads=[RZc, RLtc], writes=[Rps[b]])
                            s.op("dve", lambda e, b=b, Zn=Zn, Zc=Zc: e.tensor_tensor(Zn, ps[b][:], Zc, ALU.add),
                                 reads=[Rps[b], RZc], writes=[RZn])
                            if k < 6:
                                Ltn, Ln = W["Lt%d" % (1 - lc)], W["L%d" % (1 - lc)]
                                RLtn, RLn = W["RLt%d" % (1 - lc)], W["RL%d" % (1 - lc)]
                                b2 = bank()
                                for q in range(4):
                                    s.op("pe", lambda e, b2=b2, q=q, Lc=Lc, Ltc=Ltc: e.matmul(
                                        ps[b2][:, q * 128:(q + 1) * 128], q4(Lc)[:, q, :], q4(Ltc)[:, q, :], start=True, stop=True),
                                        reads=[RLc, RLtc], writes=[Rps[b2]])
                                s.op("act", lambda e, b2=b2, Ltn=Ltn: e.copy(Ltn, ps[b2][:]), reads=[Rps[b2]], writes=[RLtn])
                                if k < 5:
                                    b3 = bank()
                                    for q in range(4):
                                        s.op("pe", lambda e, b3=b3, q=q, Lc=Lc, Ltc=Ltc: e.matmul(
                                            ps[b3][:, q * 128:(q + 1) * 128], q4(Ltc)[:, q, :], q4(Lc)[:, q, :], start=True, stop=True),
                                            reads=[RLc, RLtc], writes=[Rps[b3]])
                                    s.op("act", lambda e, b3=b3, Ln=Ln: e.copy(Ln, ps[b3][:]), reads=[Rps[b3]], writes=[RLn])
                                lc = 1 - lc
                            zc = 1 - zc
                            yield
                        s.op("act", lambda e, zc=zc: e.copy(W["Zfb"], W["Z%d" % zc]), reads=[W["RZ%d" % zc]], writes=[W["RZfb"]])
                        Zf = q4(W["Zfb"])
                        RZf = W["RZfb"]
                        Mrbq = q4(W["Mrbt"])
                        Mrkq = q4(W["Mrkt"])
                        if cut < 6:
                            return
                        bM = bank()
                        bN = bank()
                        bR = bank()
                        for q, (c, h2) in enumerate(heads):
                            cl_ = c
                            prt = slice(64 * h2, 64 * h2 + 64)
                            s.op("pe", lambda e, q=q, cl_=cl_, prt=prt: e.matmul(
                                ps[bM][prt, cl_ * 64:(cl_ + 1) * 64], Zf[:, q, 0:64], BKq[:, q, 0:64],
                                start=(q == 0), stop=False, skip_group_check=True),
                                reads=[RZf, W["RBKtm"]], writes=[Rps[bM]])
                            s.op("pe", lambda e, q=q, cl_=cl_, prt=prt: e.matmul(
                                ps[bM][prt, cl_ * 64:(cl_ + 1) * 64], ident_b[0:64, 0:64], ident_b[0:64, 0:64],
                                start=False, stop=True, skip_group_check=True),
                                reads=[Rconst], writes=[Rps[bM]])
                            s.op("pe", lambda e, q=q, cl_=cl_, prt=prt: e.matmul(
                                ps[bN][prt, cl_ * 64:(cl_ + 1) * 64], BKq[:, q, 0:64], Zf[:, q, 64:128],
                                start=(q == 0), stop=False, skip_group_check=True),
                                reads=[RZf, W["RBKtm"]], writes=[Rps[bN]])
                            s.op("pe", lambda e, q=q, cl_=cl_, prt=prt, c=c, h2=h2: e.matmul(
                                ps[bN][prt, cl_ * 64:(cl_ + 1) * 64], BKq[:, q, 64:128], Vq(c, h2),
                                start=False, stop=False, skip_group_check=True),
                                reads=[W["RBKtm"], RVtm], writes=[Rps[bN]])
                            s.op("pe", lambda e, q=q, cl_=cl_, prt=prt: e.matmul(
                                ps[bR][prt, cl_ * 128:(cl_ + 1) * 128], Zf[:, q, 0:64], Mrbq[:, q, :],
                                start=(q == 0), stop=False, skip_group_check=True),
                                reads=[RZf, W["RMrbt"]], writes=[Rps[bR]])
                            s.op("pe", lambda e, q=q, cl_=cl_, prt=prt, c=c, h2=h2: e.matmul(
                                ps[bR][prt, cl_ * 128:(cl_ + 1) * 128], ident_b[prt, prt], fm(TB["RH"], c, h2),
                                start=False, stop=True, skip_group_check=True),
                                reads=[Rconst, TB["RRH"]], writes=[Rps[bR]])
                            st_ = first_fy[0]
                            first_fy[0] = False
                            hh = c * 2 + h2
                            s.op("pe", lambda e, q=q, hh=hh, st_=st_: e.matmul(
                                ps[fy][:, hh * 64:(hh + 1) * 64], Mrbq[:, q, :], Zf[:, q, 64:128],
                                start=st_, stop=False, skip_group_check=True),
                                reads=[RZf, W["RMrbt"]], writes=[Rps[fy]])
                            s.op("pe", lambda e, q=q, hh=hh, c=c, h2=h2: e.matmul(
                                ps[fy][:, hh * 64:(hh + 1) * 64], Mrkq[:, q, :], Vq(c, h2),
                                start=False, stop=False, skip_group_check=True),
                                reads=[W["RMrkt"], RVtm], writes=[Rps[fy]])
                        MTv = W["MT"].rearrange("p (c j) -> p c j", c=4)
                        RTv = W["RT"].rearrange("p (c t) -> p c t", c=4)
                        pg = slice(64 * g, 64 * g + 64)
                        s.op("act", lambda e, MTv=MTv, pg=pg: e.copy(MTv[pg], ps[bM][pg, 0:256].rearrange("p (c j) -> p c j", c=4)),
                             reads=[Rps[bM]], writes=[W["RMT"]])
                        s.op("act", lambda e, RTv=RTv, pg=pg: e.copy(RTv[pg], ps[bR][pg, 0:512].rearrange("p (c t) -> p c t", c=4)),
                             reads=[Rps[bR]], writes=[W["RRT"]])
                        for q, (c, h2) in enumerate(heads):
                            cl_ = c
                            prt = slice(64 * h2, 64 * h2 + 64)
                            hh = c * 2 + h2
                            s.op("pe", lambda e, cl_=cl_, prt=prt, hh=hh, c=c, RTv=RTv: e.matmul(
                                ps[fy][:, hh * 64:(hh + 1) * 64], RTv[prt, cl_, :], Abd3[prt, c, :],
                                start=False, stop=True, skip_group_check=True),
                                reads=[W["RRT"], RAbf], writes=[Rps[fy]])
                            s.op("pe", lambda e, cl_=cl_, prt=prt, c=c, MTv=MTv: e.matmul(
                                ps[bN][prt, cl_ * 64:(cl_ + 1) * 64], MTv[prt, cl_, :], Abd3[prt, c, :],
                                start=False, stop=True, skip_group_check=True),
                                reads=[W["RMT"], RAbf], writes=[Rps[bN]])
                        s.op("dve", lambda e, pg=pg: e.tensor_tensor(
                            Ad3[pg, :, :], ps[bN][pg, 0:256].rearrange("p (c i) -> p c i", c=4),
                            pcs[pg, par, :].unsqueeze(2).to_broadcast([64, 4, 64]), ALU.mult),
                            reads=[Rps[bN], Rpcs[par], RAbf], writes=[RAst])
                    gens = [group_body(0), group_body(1)]
                    while gens:
                        for gn in list(gens):
                            try:
                                next(gn)
                            except StopIteration:
                                gens.remove(gn)
                    if cut < 6:
                        reserved.discard(fy)
                        return
                    s.op("act", lambda e: e.copy(Abd, Ad), reads=[RAst], writes=[RAbf])
                    isend = (ti % 2 == 1) if d == 0 else (ti % 2 == 0)
                    if isend:
                        s.dma("sp", s_out[u, d], Ad, reads=[RAst])
                    reserved.discard(fy)
                    if d == 0:
                        s.op("act", lambda e, ti=ti: e.copy(YSb[:, ti, :], ps[fy][:]), reads=[Rps[fy]], writes=[RYS])
                    else:
                        finalize(ti, fy)
                for ti in order:
                    tile_body(ti)

        if stage >= 1:
            if "a" in sub:
                make_gg(0)
            if "b" in sub:
                make_hT(0, 0)
            if "c" in sub:
                even_inproj()
        if stage >= 3:
            attention()
            barrier()
            if stage >= 5:
                rwkv_setup()
                rwkv_prepass()
                rwkv_chunked(NT if stage >= 6 else 2)
            else:
                s.op("pool", lambda e: e.memset(yT[:, 4:8, :], 0.0), writes=[RyT])
            barrier()
            out_proj(w_out_even, yT, RyT)
        if stage >= 2:
            barrier()
            ffn(0)
            barrier()
        if stage >= 4:
            make_gg(1)
            make_hT(1, 0)
            fnet()
            out_proj(w_out_odd, yT, RyT)
            barrier()
            ffn(1)
        for i in range(NT):
            s.dma("sp", y_out[i * 128:(i + 1) * 128, :], x_sb[:, i, :], reads=[Rx[i]])
        s.emit()
    return nc


def _core_units(c):
    if c < 6:
        return [("p", 5 * c + u) for u in range(5)]
    b = c - 6
    return [("s", b, u) for u in range(4)] + [("p", 30 + b)]


def _host_prep(inp):
    f32 = np.float32
    COFF, NCOL = colp_layout()
    ROFF, NROW = rowp_layout()
    g = {k: np.asarray(v) for k, v in inp.items()}
    colp = np.zeros((128, NCOL), f32)

    def put(name, vec):
        c = cols_of(vec)
        colp[:, COFF[name]:COFF[name] + c.shape[1]] = c
    for l in range(2):
        put("bada%d" % l, g["b_ada"][l])
        for k in range(4):
            put("gain%d_%d" % (l, k), g["norm_gains"][l, k])
        for k in range(3):
            put("conv%d_%d" % (l, k), g["ffn_conv"][l, k])
        put("convb%d" % l, g["ffn_conv_b"][l])
    put("mu0", g["rwkv_shift_mu"][0, 0])
    put("mu1", g["rwkv_shift_mu"][0, 1])
    for d in range(2):
        put("w0_%d" % d, g["rwkv_w0"][0, d])
        put("a0_%d" % d, g["rwkv_a0"][0, d])
    for k in range(3):
        put("kvec%d" % k, g["rwkv_kvec"][0, k])

    cmat = np.zeros((8, 128, 128), f32)
    cmat[0] = np.eye(128, dtype=f32)
    for i in range(64):
        cmat[1][2 * i, 2 * i + 1] = 1.0
        cmat[1][2 * i + 1, 2 * i] = -1.0
    cmat[2][:64, :64] = 1.0
    cmat[2][64:, 64:] = 1.0
    cmat[3][:64, 0] = 1.0
    cmat[3][64:, 1] = 1.0
    rr_, cc_ = np.meshgrid(np.arange(128), np.arange(128), indexing="ij")
    cmat[4] = (rr_ < cc_)
    cmat[5] = (rr_ > cc_)
    cmat[6] = (rr_ <= cc_)
    cmat[7] = (rr_ >= cc_)
    cc = np.arange(128)
    chang = 2.0 * np.pi * ((cc[:, None] * cc[None, :]) % 128) / 128.0
    chCS = np.concatenate([np.cos(chang), np.sin(chang)], axis=1).astype(f32)

    inv = (10000.0 ** (-np.arange(16, dtype=np.float32) / 16)).astype(f32)

    shared = dict(
        colp=colp, w_ada=g["w_ada"], w_in_even=g["w_in_even"][0], w_out_even=g["w_out_even"][0],
        w_out_odd=g["w_out_odd"][0], w_ffn_in=g["w_ffn_in"], w_ffn_out=g["w_ffn_out"],
        w2t=np.ascontiguousarray(g["rwkv_w2"][0].reshape(128, 512)),
        a2t=np.ascontiguousarray(g["rwkv_a2"][0].reshape(128, 512)),
        g2=np.ascontiguousarray(g["rwkv_g2"][0]), chCS=chCS, cmat=cmat,
        lnx=np.ascontiguousarray(g["rwkv_lnx"][0]))
    maps = []
    for c in range(NCORES):
        units = _core_units(c)
        xs = []
        for un in units:
            if un[0] == "p":
                xs.append(g["x_prompt"][un[1]])
            else:
                xs.append(g["x_sample"][un[1], un[2] * 256:(un[2] + 1) * 256])
        x_in = np.ascontiguousarray(np.concatenate(xs, axis=0), dtype=f32)
        is_s = c >= 6
        condA = g["c"][c - 6] if is_s else g["c_ctx"]
        condB = g["c_ctx"]
        cond = np.stack([condA, condB], axis=0).astype(f32)
        condT = np.ascontiguousarray(cond.reshape(2, 8, 128).transpose(2, 1, 0).reshape(128, 16))
        rowp = np.zeros((1, NROW), f32)
        rowp[0, ROFF["lam"]:ROFF["lam"] + 256] = g["diff_lambda"][0].reshape(-1)
        rowp[0, ROFF["subln"]:ROFF["subln"] + 128] = g["diff_subln"][0]
        ab = np.full((6, 5), -30000.0, f32)
        if is_s:
            ab[0:5, 0:4] = 0.0
            ab[5, 4] = 0.0
            bndL = [1, 0, 0, 0, 1]
            bndR = [0, 0, 0, 1, 1]
        else:
            for u in range(5):
                ab[u + 1, u] = 0.0
            bndL = [1] * 5
            bndR = [1] * 5
        rowp[0, ROFF["abias"]:ROFF["abias"] + 30] = ab.reshape(-1)
        rowp[0, ROFF["bndL"]:ROFF["bndL"] + 5] = bndL
        rowp[0, ROFF["bndR"]:ROFF["bndR"] + 5] = bndR
        ropeC = np.ones((128, T), f32)
        ropeS = np.zeros((128, T), f32)
        if is_s:
            t = np.arange(1024)
            row = (t // 64).astype(f32)
            col = (t % 64).astype(f32)
            ang = np.concatenate([row[:, None] * inv[None, :], col[:, None] * inv[None, :]], axis=1).astype(f32)
            pidx = (np.arange(128) % 64) // 2
            ropeC[:, :1024] = np.cos(ang)[:, pidx].T
            ropeS[:, :1024] = np.sin(ang)[:, pidx].T
        cacheKT = np.zeros((512, 256), f32)
        cacheV = np.zeros((256, 512), f32)
        initS = np.zeros((2, 128, 256), f32)
        if is_s:
            b = c - 6
            cacheKT[:] = g["cache_k"][b, 0].reshape(256, 512).T
            cacheV[:] = g["cache_v"][b, 0].reshape(256, 512)
            st = g["state_wkv"][b, 0]
            initS[:] = st.reshape(2, 4, 2, 64, 64).transpose(0, 2, 4, 1, 3).reshape(2, 128, 256)
        dC = np.zeros((T, T), np.float64)
        dS = np.zeros((T, T), np.float64)
        blocks = [(0, 1024), (1024, 256)] if is_s else [(256 * u, 256) for u in range(5)]
        for (a0, L) in blocks:
            ll = np.arange(L)
            ang = 2.0 * np.pi * ((ll[:, None] * ll[None, :]) % L) / L
            sc = 1.0 / math.sqrt(L * 128.0)
            dC[a0:a0 + L, a0:a0 + L] = np.cos(ang) * sc
            dS[a0:a0 + L, a0:a0 + L] = -np.sin(ang) * sc
        m = dict(shared)
        m.update(x_in=x_in, condT=condT, rowp=rowp, ropeC=ropeC, ropeS=ropeS, cacheKT=cacheKT, cacheV=cacheV,
                 initS=initS, dftC=dC.astype(f32), dftSn=dS.astype(f32))
        maps.append(m)
    return maps


_NC_CACHE = {}


def kernel(**inputs):
    maps = _host_prep(inputs)
    if "nc" not in _NC_CACHE:
        _NC_CACHE["nc"] = build()
    nc = _NC_CACHE["nc"]
    import os
    ncr = int(os.environ.get("NCR", NCORES))
    res = run_bass_kernel_spmd(nc, maps[:ncr], core_ids=list(range(ncr)))
    outs = list(res.results) + [res.results[0]] * (NCORES - ncr)
    y_prompt = np.zeros((32, 256, D), np.float32)
    y_sample = np.zeros((2, 1024, D), np.float32)
    nk = np.zeros((32, 1, 256, 4, 128), np.float32)
    nv = np.zeros((32, 1, 256, 4, 128), np.float32)
    ns = np.zeros((32, 1, 2, 8, 64, 64), np.float32)
    for c in range(NCORES):
        r = outs[c]
        for u, un in enumerate(_core_units(c)):
            sl = slice(u * 256, (u + 1) * 256)
            if un[0] == "p":
                bi = un[1]
                y_prompt[bi] = r["y_out"][sl]
                nk[bi, 0] = r["k_out"][sl].reshape(256, 4, 128)
                nv[bi, 0] = r["v_out"][sl].reshape(256, 4, 128)
                stt = r["s_out"][u]
                ns[bi, 0] = stt.reshape(2, 2, 64, 4, 64).transpose(0, 3, 1, 4, 2).reshape(2, 8, 64, 64)
            else:
                y_sample[un[1], un[2] * 256:(un[2] + 1) * 256] = r["y_out"][sl]
    return (y_prompt, y_sample, nk, nv, ns)
```

```python
import contextlib
import math
import numpy as np
import concourse.bass as bass
import concourse.mybir as mybir
from concourse.bass_utils import run_bass_kernel_spmd

F32 = mybir.dt.float32
BF16 = mybir.dt.bfloat16
AF = mybir.ActivationFunctionType
ALU = mybir.AluOpType
AX = mybir.AxisListType

T = 1280
NT = 10
U = 5
D = 1024
KC = 8
DFF = 2816
FC = 22
NCORES = 8
ARENA_W = 27200
EXPM05 = math.exp(-0.5)


class Res:
    __slots__ = ("name", "writer", "readers", "excl")

    def __init__(self, name, excl=False):
        self.name = name
        self.writer = None
        self.readers = []
        self.excl = excl


class Sched:
    ENGS = ("pe", "act", "dve", "pool", "sp")
    NDMA = 6

    def __init__(self, nc):
        self.nc = nc
        self.prog = {e: [] for e in self.ENGS}
        self.signal = {e: set() for e in self.ENGS}
        self.ndma = {e: 0 for e in self.ENGS}

    def _collect(self, reads, writes, eng=None):
        deps = []
        for r in reads:
            if r.writer is not None:
                deps.append(r.writer)
            if r.excl:
                deps.extend(t for t in r.readers if t[1] != eng)
        for w in writes:
            if w.writer is not None:
                deps.append(w.writer)
            deps.extend(w.readers)
        return deps

    def _commit(self, tok, reads, writes):
        for r in reads:
            r.readers.append(tok)
        for w in writes:
            w.writer = tok
            w.readers = []

    def op(self, eng, fn, reads=(), writes=()):
        deps = self._collect(reads, writes, eng)
        idx = len(self.prog[eng])
        if eng == "pe":
            deps = [d for d in deps if not (d[0] == "c" and d[1] == "pe")]
        for d in deps:
            if d[0] == "c":
                self.signal[d[1]].add(d[2])
        self.prog[eng].append(dict(fn=fn, deps=deps, kind="c"))
        tok = ("c", eng, idx)
        self._commit(tok, reads, writes)
        return tok

    def dma(self, eng, out, in_, reads=(), writes=()):
        deps = self._collect(reads, writes, eng)
        n = self.ndma[eng]
        self.ndma[eng] += 1
        if n >= self.NDMA:
            deps.append(("d", eng, n - self.NDMA))
        for d in deps:
            if d[0] == "c":
                self.signal[d[1]].add(d[2])
        self.prog[eng].append(dict(out=out, in_=in_, deps=deps, kind="d", n=n))
        tok = ("d", eng, n)
        self._commit(tok, reads, writes)
        return tok

    def emit(self):
        nc = self.nc
        with contextlib.ExitStack() as st:
            csem = {e: st.enter_context(nc.semaphore("c_" + e)) for e in self.ENGS}
            dsem = {e: [st.enter_context(nc.semaphore("d_%s_%d" % (e, i))) for i in range(self.NDMA)]
                    for e in self.ENGS if self.ndma[e] > 0}
            sigval = {}
            for e in self.ENGS:
                cnt = 0
                m = {}
                for i in range(len(self.prog[e])):
                    if i in self.signal[e]:
                        cnt += 1
                        m[i] = cnt
                sigval[e] = m

            def resolve(tok):
                if tok[0] == "c":
                    return csem[tok[1]], sigval[tok[1]][tok[2]], ("c", tok[1])
                e, n = tok[1], tok[2]
                return dsem[e][n % self.NDMA], 16 * (n // self.NDMA + 1), ("d", e, n % self.NDMA)

            def run_engine(e, h):
                waited = {}
                for i, ins in enumerate(self.prog[e]):
                    need = {}
                    for d in ins["deps"]:
                        sem, val, key = resolve(d)
                        if waited.get(key, 0) >= val:
                            continue
                        if key not in need or need[key][1] < val:
                            need[key] = (sem, val)
                    for key, (sem, val) in need.items():
                        h.wait_ge(sem, val)
                        waited[key] = val
                    if ins["kind"] == "c":
                        bi = ins["fn"](h)
                        if i in self.signal[e]:
                            bi.then_inc(csem[e], 1)
                    else:
                        n = ins["n"]
                        h.dma_start(out=ins["out"], in_=ins["in_"]).then_inc(dsem[e][n % self.NDMA], 16)
                if self.ndma[e] > 0:
                    n = self.ndma[e]
                    for slot in range(self.NDMA):
                        cnt = (n - slot + self.NDMA - 1) // self.NDMA if n > slot else 0
                        if cnt > 0:
                            h.wait_ge(dsem[e][slot], 16 * cnt)

            with nc.Block() as block:
                @block.tensor
                def _(eng):
                    run_engine("pe", eng)

                @block.scalar
                def _(eng):
                    run_engine("act", eng)

                @block.vector
                def _(eng):
                    run_engine("dve", eng)

                @block.gpsimd
                def _(eng):
                    run_engine("pool", eng)

                @block.sync
                def _(eng):
                    run_engine("sp", eng)


def colp_layout():
    off = {}
    n = 0

    def add(name, cols):
        nonlocal n
        off[name] = n
        n += cols
    for l in range(2):
        add("bada%d" % l, 48)
        for g in range(4):
            add("gain%d_%d" % (l, g), 8)
        for k in range(3):
            add("conv%d_%d" % (l, k), FC)
        add("convb%d" % l, FC)
    add("mu0", 15)
    add("mu1", 15)
    for d in range(2):
        add("w0_%d" % d, 4)
        add("a0_%d" % d, 4)
    for k in range(3):
        add("kvec%d" % k, 4)
    return off, n


def rowp_layout():
    off = {}
    n = 0

    def add(name, cols):
        nonlocal n
        off[name] = n
        n += cols
    add("lam", 256)
    add("subln", 128)
    add("abias", 30)
    add("bndL", 5)
    add("bndR", 5)
    return off, n


def cols_of(vec):
    v = np.asarray(vec, np.float32).reshape(-1, 128)
    return np.ascontiguousarray(v.T)


def build(stage=99):
    nc = bass.Bass("TRN2", target_bir_lowering=False)
    COFF, NCOL = colp_layout()
    ROFF, NROW = rowp_layout()

    def din(name, shape):
        return nc.dram_tensor(name, list(shape), F32, kind="ExternalInput").ap()

    def dout(name, shape):
        return nc.dram_tensor(name, list(shape), F32, kind="ExternalOutput").ap()

    x_in = din("x_in", [T, D])
    condT = din("condT", [128, 16])
    colp = din("colp", [128, NCOL])
    rowp = din("rowp", [1, NROW])
    w_ada = din("w_ada", [2, D, 6 * D])
    w_in_even = din("w_in_even", [D, 3456])
    w_out_even = din("w_out_even", [D, D])
    w_out_odd = din("w_out_odd", [D, D])
    w_ffn_in = din("w_ffn_in", [2, D, 2 * DFF])
    w_ffn_out = din("w_ffn_out", [2, DFF, D])
    w2t_d = din("w2t", [128, 512])
    a2t_d = din("a2t", [128, 512])
    g2_d = din("g2", [128, 512])
    cacheKT = din("cacheKT", [512, 256])
    cacheV = din("cacheV", [256, 512])
    ropeC = din("ropeC", [128, T])
    ropeS = din("ropeS", [128, T])
    initS = din("initS", [2, 128, 256])
    dftC = din("dftC", [T, T])
    dftSn = din("dftSn", [T, T])
    chCS = din("chCS", [128, 256])
    cmat = din("cmat", [8, 128, 128])
    lnx_d = din("lnx", [2, 512])

    y_out = dout("y_out", [T, D])
    k_out = dout("k_out", [T, 512])
    v_out = dout("v_out", [T, 512])
    s_out = dout("s_out", [U, 2, 128, 256])

    import os
    sub = os.environ.get("SUB", "abcmqv")
    with contextlib.ExitStack() as st:
        s = Sched(nc)

        def sb(name, shape, dt=F32):
            return st.enter_context(nc.sbuf_tensor(name, list(shape), dt))

        x_sb = sb("x_sb", [128, NT, D])
        Rx = [Res("x%d" % i) for i in range(NT)]
        WB = 4096
        wbuf = [sb("wb%d" % i, [128, WB], BF16) for i in range(3)]
        Rw = [Res("wb%d" % i) for i in range(3)]
        wctr = [0]
        colp_sb = sb("colp_sb", [128, NCOL])
        Rcolp = Res("colp")
        rowp_sb = sb("rowp_sb", [128, NROW])
        Rrowp = Res("rowp")
        ident_f = sb("ident_f", [128, 128])
        ident_b = sb("ident_b", [128, 128], BF16)
        rotT_b = sb("rotT_b", [128, 128], BF16)
        bones_f = sb("bones_f", [128, 128])
        halfsel_f = sb("halfsel_f", [128, 128])
        bones_b = sb("bones_b", [128, 128], BF16)
        Rconst = Res("const")
        gg_sb = sb("gg_sb", [128, 2, 2, D], BF16)
        Rgg = Res("gg")
        scond = sb("scond", [128, 16], BF16)
        condf = sb("condf", [128, 16])
        Rcond = Res("cond")
        modcol = sb("modcol", [128, 2, 96])
        Rmod = Res("modcol")
        scsh = sb("scsh", [128, 2, 2, 2, 16])
        Rscsh = Res("scsh")
        stat = sb("stat", [128, 64])
        Rstat = Res("stat")
        junk_b = sb("junk_b", [128, D], BF16)
        Rjunk = Res("junk")
        xn_b = sb("xn_b", [128, D], BF16)
        Rxn = Res("xn")
        stage_f = sb("stage_f", [128, 2, 512])
        Rstage = [Res("stage0"), Res("stage1")]
        stctr = [0]
        arena = sb("arena", [128, ARENA_W])
        Rar = {}

        def ares(name):
            if name not in Rar:
                Rar[name] = Res("ar_" + name)
            return Rar[name]

        def carve(off_w, nelem, dt):
            if dt == F32:
                return arena[:, off_w:off_w + nelem]
            return arena[:, off_w:off_w + (nelem + 1) // 2].bitcast(BF16)[:, 0:nelem]

        ps = [st.enter_context(nc.psum_tensor("ps%d" % b, [128, 512], F32)) for b in range(8)]
        psb = [p.bitcast(BF16) for p in ps]
        Rps = [Res("ps%d" % b, excl=True) for b in range(8)]
        bctr = [0]

        reserved = set()

        def bank():
            while True:
                b = bctr[0] % 8
                bctr[0] += 1
                if b not in reserved:
                    return b

        def barrier():
            allr = list(Rar.values()) + list(Rw)
            s.op("dve", lambda e: e.memset(stat[:, 63:64], 0.0), writes=allr + [Rstat])

        def wload(src_ap, kc, ncols):
            i = wctr[0] % 3
            wctr[0] += 1
            view = wbuf[i][:, 0:kc * ncols].rearrange("p (c n) -> p c n", c=kc)
            s.dma("pool", view, src_ap.rearrange("(c p) n -> p c n", p=128), writes=[Rw[i]])
            return view, Rw[i]

        s.dma("sp", colp_sb[:], colp, writes=[Rcolp])
        s.dma("sp", rowp_sb[:], rowp[0, :].partition_broadcast(128), writes=[Rrowp])
        s.dma("sp", ident_f[:], cmat[0], writes=[Rconst])
        s.dma("sp", bones_f[:], cmat[2], writes=[Rconst])
        s.dma("sp", halfsel_f[:], cmat[3], writes=[Rconst])
        s.dma("pool", ident_b[:], cmat[0], writes=[Rconst])
        s.dma("pool", rotT_b[:], cmat[1], writes=[Rconst])
        s.dma("pool", bones_b[:], cmat[2], writes=[Rconst])
        s.dma("sp", condf[:], condT, writes=[Rcond])
        for i in range(NT):
            s.dma("sp", x_sb[:, i, :], x_in[i * 128:(i + 1) * 128, :], writes=[Rx[i]])
        s.op("act", lambda e: e.activation(scond[:], condf[:], AF.Silu), reads=[Rcond], writes=[Rcond])

        def cp(name, c=0, n=1):
            o = COFF[name] + c
            return colp_sb[:, o:o + n]

        for l in range(2):
            b = bank()
            for v in range(6):
                for half in range(2):
                    wv, rw = wload(w_ada[l][:, v * 1024 + half * 512: v * 1024 + half * 512 + 512], KC, 512)
                    for j in range(4):
                        col = (v * 8 + half * 4 + j) * 2
                        for kc in range(KC):
                            s.op("pe", lambda e, wv=wv, j=j, kc=kc, col=col, b=b: e.matmul(
                                ps[b][:, col:col + 2], wv[:, kc, j * 128:(j + 1) * 128],
                                scond[:, kc * 2:kc * 2 + 2], start=(kc == 0), stop=(kc == KC - 1)),
                                reads=[rw, Rcond], writes=[Rps[b]])
            s.op("dve", lambda e, l=l, b=b: e.tensor_tensor(
                modcol[:, l, :].rearrange("p (c a) -> p c a", a=2),
                ps[b][:, 0:96].rearrange("p (c a) -> p c a", a=2),
                cp("bada%d" % l, 0, 48).unsqueeze(2).to_broadcast([128, 48, 2]), ALU.add),
                reads=[Rps[b], Rcolp], writes=[Rmod])

            def mv(v, l=l):
                return modcol[:, l, v * 16:(v + 1) * 16].rearrange("p (c a) -> p c a", a=2)

            def gain(g, l=l):
                return cp("gain%d_%d" % (l, g), 0, 8).unsqueeze(2).to_broadcast([128, 8, 2])
            for wi, (vs, vsh, g) in enumerate([(1, 0, 0), (4, 3, 2)]):
                sc = scsh[:, l, wi, 0, :].rearrange("p (c a) -> p c a", a=2)
                sh = scsh[:, l, wi, 1, :].rearrange("p (c a) -> p c a", a=2)
                s.op("dve", lambda e, sc=sc, vs=vs, mv=mv: e.tensor_scalar(sc, mv(vs), 1.0, None, ALU.add),
                     reads=[Rmod], writes=[Rscsh])
                s.op("dve", lambda e, sc=sc, g=g, gain=gain: e.tensor_tensor(sc, sc, gain(g), ALU.mult),
                     reads=[Rscsh, Rcolp], writes=[Rscsh])
                s.op("dve", lambda e, sh=sh, vsh=vsh, mv=mv: e.tensor_copy(sh, mv(vsh)),
                     reads=[Rmod], writes=[Rscsh])

        ggcol = sb("ggcol", [128, 2, 16])
        Rggcol = Res("ggcol")

        def make_gg(l):
            for wi, (vg, g) in enumerate([(2, 1), (5, 3)]):
                gc = ggcol[:, wi, :].rearrange("p (c a) -> p c a", a=2)
                s.op("dve", lambda e, gc=gc, vg=vg, g=g, l=l: e.tensor_tensor(
                    gc, modcol[:, l, vg * 16:(vg + 1) * 16].rearrange("p (c a) -> p c a", a=2),
                    cp("gain%d_%d" % (l, g), 0, 8).unsqueeze(2).to_broadcast([128, 8, 2]), ALU.mult),
                    reads=[Rmod, Rcolp], writes=[Rggcol])
                for a in range(2):
                    for hh in range(2):
                        b = bank()
                        for j in range(4):
                            c = hh * 4 + j
                            s.op("pe", lambda e, b=b, wi=wi, c=c, a=a, j=j: e.matmul(
                                ps[b][:, j * 128:(j + 1) * 128],
                                ggcol[:, wi, c * 2 + a:c * 2 + a + 1].to_broadcast([128, 128]),
                                ident_f[:], start=True, stop=True),
                                reads=[Rggcol, Rconst], writes=[Rps[b]])
                        s.op("act", lambda e, b=b, wi=wi, a=a, hh=hh: e.copy(
                            gg_sb[:, wi, a, hh * 512:(hh + 1) * 512], ps[b][:]),
                            reads=[Rps[b]], writes=[Rgg])

        hT = carve(0, 8 * T, BF16).rearrange("p (c t) -> p c t", c=8)
        RhT = ares("hT")

        def ab_of_tile(i):
            return 0 if i < 8 else 1

        def make_hT(l, wi):
            for i in range(NT):
                a = ab_of_tile(i)
                s.op("act", lambda e, i=i: e.activation(junk_b[:], x_sb[:, i, :], AF.Square, scale=1.0 / 32.0,
                                                        accum_out=stat[:, 0:1]),
                     reads=[Rx[i]], writes=[Rjunk, Rstat])
                s.op("dve", lambda e: e.tensor_scalar(stat[:, 1:2], stat[:, 0:1], 1e-6, None, ALU.add),
                     reads=[Rstat], writes=[Rstat])
                s.op("act", lambda e: e.activation(stat[:, 1:2], stat[:, 1:2], AF.Sqrt), reads=[Rstat], writes=[Rstat])
                s.op("dve", lambda e: e.reciprocal(stat[:, 2:3], stat[:, 1:2]), reads=[Rstat], writes=[Rstat])
                s.op("dve", lambda e, i=i: e.tensor_scalar(xn_b[:], x_sb[:, i, :], stat[:, 2:3], None, ALU.mult),
                     reads=[Rx[i], Rstat], writes=[Rxn])
                b = bank()
                for c in range(8):
                    s.op("pe", lambda e, b=b, c=c: e.transpose(psb[b][:, c * 128:(c + 1) * 128],
                                                               xn_b[:, c * 128:(c + 1) * 128], ident_b[:]),
                         reads=[Rxn, Rconst], writes=[Rps[b]])
                for c in range(8):
                    s.op("act", lambda e, b=b, c=c, i=i, a=a: e.activation(
                        hT[:, c, i * 128:(i + 1) * 128], psb[b][:, c * 128:(c + 1) * 128], AF.Identity,
                        bias=scsh[:, l, wi, 1, c * 2 + a:c * 2 + a + 1],
                        scale=scsh[:, l, wi, 0, c * 2 + a:c * 2 + a + 1]),
                        reads=[Rps[b], Rscsh], writes=[RhT])

        TOKCH = [(0, 512), (512, 512), (1024, 256)]

        def linear_fm(wv, rw, ncol0, actT, Ract, kc_n, evac):
            for (t0, tn) in TOKCH:
                b = bank()
                for kc in range(kc_n):
                    s.op("pe", lambda e, b=b, kc=kc, t0=t0, tn=tn: e.matmul(
                        ps[b][:, 0:tn], wv[:, kc, ncol0:ncol0 + 128], actT[:, kc, t0:t0 + tn],
                        start=(kc == 0), stop=(kc == kc_n - 1)),
                        reads=[rw, Ract], writes=[Rps[b]])
                evac(b, t0, tn)

        def linear_tm(wv, rw, ncols, actT, Ract, kc_n, i, b, col0=0):
            for kc in range(kc_n):
                s.op("pe", lambda e, kc=kc: e.matmul(
                    ps[b][:, col0:col0 + ncols], actT[:, kc, i * 128:(i + 1) * 128], wv[:, kc, 0:ncols],
                    start=(kc == 0), stop=(kc == kc_n - 1)),
                    reads=[rw, Ract], writes=[Rps[b]])

        def residual_update(i, banks, wi, fsrc=None, Rf=None):
            a = ab_of_tile(i)
            if fsrc is None:
                for hh, b in enumerate(banks):
                    s.op("act", lambda e, b=b, hh=hh: e.activation(junk_b[:, 0:512], ps[b][:], AF.Square,
                                                                   scale=1.0 / 32.0, accum_out=stat[:, 8 + hh:9 + hh]),
                         reads=[Rps[b]], writes=[Rjunk, Rstat])
                s.op("dve", lambda e: e.tensor_tensor(stat[:, 10:11], stat[:, 8:9], stat[:, 9:10], ALU.add),
                     reads=[Rstat], writes=[Rstat])
            else:
                s.op("act", lambda e: e.activation(junk_b[:], fsrc, AF.Square, scale=1.0 / 32.0,
                                                   accum_out=stat[:, 10:11]),
                     reads=[Rf], writes=[Rjunk, Rstat])
            s.op("dve", lambda e: e.tensor_scalar(stat[:, 11:12], stat[:, 10:11], 1e-6, None, ALU.add),
                 reads=[Rstat], writes=[Rstat])
            s.op("act", lambda e: e.activation(stat[:, 11:12], stat[:, 11:12], AF.Sqrt), reads=[Rstat], writes=[Rstat])
            s.op("dve", lambda e: e.reciprocal(stat[:, 12:13], stat[:, 11:12]), reads=[Rstat], writes=[Rstat])
            for hh in range(2):
                src = ps[banks[hh]][:] if fsrc is None else fsrc[:, hh * 512:(hh + 1) * 512]
                rr = [Rps[banks[hh]]] if fsrc is None else [Rf]
                si = stctr[0] % 2
                stctr[0] += 1
                s.op("dve", lambda e, src=src, hh=hh, si=si: e.scalar_tensor_tensor(
                    stage_f[:, si, :], src, stat[:, 12:13], gg_sb[:, wi, a, hh * 512:(hh + 1) * 512],
                    ALU.mult, ALU.mult),
                    reads=rr + [Rstat, Rgg], writes=[Rstage[si]])
                s.op("dve", lambda e, hh=hh, si=si, i=i: e.tensor_tensor(
                    x_sb[:, i, hh * 512:(hh + 1) * 512], x_sb[:, i, hh * 512:(hh + 1) * 512], stage_f[:, si, :], ALU.add),
                    reads=[Rstage[si], Rx[i]], writes=[Rx[i]])

        actT = carve(5120, FC * T, BF16).rearrange("p (c t) -> p c t", c=FC)
        RactT = ares("actT")
        graw = carve(19200, T + 2, F32)
        Rgraw = ares("graw")
        cbuf = carve(19200 + 1284, T, F32)
        Rcbuf = ares("cbuf")
        sbuf_s = carve(19200 + 1284 + 1280, T, F32)
        Rsbuf = ares("sbuf_s")
        fstageA = carve(19200, 5 * D, F32).rearrange("p (i n) -> p i n", i=5)
        fstageB = carve(0, 5 * D, F32).rearrange("p (i n) -> p i n", i=5)

        def fst(i):
            return fstageA[:, i, :] if i < 5 else fstageB[:, i - 5, :]

        def Rfst_of(i):
            return ares("fstage") if i < 5 else RhT
        bcorr = sb("bcorr", [128, 16])
        Rbcorr = Res("bcorr")

        def ffn(l):
            make_hT(l, 1)
            s.op("dve", lambda e: e.memset(graw[:, 0:1], 0.0), writes=[Rgraw])
            s.op("dve", lambda e: e.memset(graw[:, T + 1:T + 2], 0.0), writes=[Rgraw])
            for blk in range(0, FC, 2):
                wu, ru = wload(w_ffn_in[l][:, blk * 128: blk * 128 + 256], KC, 256)
                wg, rg = wload(w_ffn_in[l][:, DFF + blk * 128: DFF + blk * 128 + 256], KC, 256)
                for jj in range(2):
                    fc = blk + jj
                    def evac_g(b, t0, tn):
                        s.op("act", lambda e, b=b, t0=t0, tn=tn: e.copy(graw[:, 1 + t0:1 + t0 + tn], ps[b][:, 0:tn]),
                             reads=[Rps[b]], writes=[Rgraw])
                    linear_fm(wg, rg, jj * 128, hT, RhT, KC, evac_g)
                    w0 = cp("conv%d_0" % l, fc)
                    w1 = cp("conv%d_1" % l, fc)
                    w2 = cp("conv%d_2" % l, fc)
                    cb = cp("convb%d" % l, fc)
                    s.op("act", lambda e, w1=w1, cb=cb: e.activation(cbuf[:], graw[:, 1:T + 1], AF.Identity, bias=cb, scale=w1),
                         reads=[Rgraw, Rcolp], writes=[Rcbuf])
                    s.op("dve", lambda e, w0=w0: e.scalar_tensor_tensor(cbuf[:], graw[:, 0:T], w0, cbuf[:], ALU.mult, ALU.add),
                         reads=[Rgraw, Rcolp, Rcbuf], writes=[Rcbuf])
                    s.op("dve", lambda e, w2=w2: e.scalar_tensor_tensor(cbuf[:], graw[:, 2:T + 2], w2, cbuf[:], ALU.mult, ALU.add),
                         reads=[Rgraw, Rcolp, Rcbuf], writes=[Rcbuf])
                    gprev = graw[:, 256:256 + 1024].rearrange("p (u k) -> p u k", k=256)[:, :, 0]
                    gnext = graw[:, 257:257 + 1024].rearrange("p (u k) -> p u k", k=256)[:, :, 0]
                    s.op("dve", lambda e, gprev=gprev: e.tensor_tensor(bcorr[:, 0:4], gprev, rowp_sb[:, ROFF["bndL"] + 1:ROFF["bndL"] + 5], ALU.mult),
                         reads=[Rgraw, Rrowp], writes=[Rbcorr])
                    s.op("dve", lambda e, w0=w0: e.tensor_scalar(bcorr[:, 0:4], bcorr[:, 0:4], w0, None, ALU.mult),
                         reads=[Rbcorr, Rcolp], writes=[Rbcorr])
                    c_at = cbuf[:, 256:256 + 1024].rearrange("p (u k) -> p u k", k=256)[:, :, 0]
                    s.op("dve", lambda e, c_at=c_at: e.tensor_tensor(c_at, c_at, bcorr[:, 0:4], ALU.subtract),
                         reads=[Rbcorr, Rcbuf], writes=[Rcbuf])
                    s.op("dve", lambda e, gnext=gnext: e.tensor_tensor(bcorr[:, 4:8], gnext, rowp_sb[:, ROFF["bndL"] + 1:ROFF["bndL"] + 5], ALU.mult),
                         reads=[Rgraw, Rrowp], writes=[Rbcorr])
                    s.op("dve", lambda e, w2=w2: e.tensor_scalar(bcorr[:, 4:8], bcorr[:, 4:8], w2, None, ALU.mult),
                         reads=[Rbcorr, Rcolp], writes=[Rbcorr])
                    c_at2 = cbuf[:, 255:255 + 1024].rearrange("p (u k) -> p u k", k=256)[:, :, 0]
                    s.op("dve", lambda e, c_at2=c_at2: e.tensor_tensor(c_at2, c_at2, bcorr[:, 4:8], ALU.subtract),
                         reads=[Rbcorr, Rcbuf], writes=[Rcbuf])
                    s.op("act", lambda e: e.activation(sbuf_s[:], cbuf[:], AF.Silu), reads=[Rcbuf], writes=[Rsbuf])
                    def evac_u(b, t0, tn, fc=fc):
                        s.op("dve", lambda e, b=b, t0=t0, tn=tn: e.tensor_tensor(
                            actT[:, fc, t0:t0 + tn], ps[b][:, 0:tn], sbuf_s[:, t0:t0 + tn], ALU.mult),
                            reads=[Rps[b], Rsbuf], writes=[RactT])
                    linear_fm(wu, ru, jj * 128, hT, RhT, KC, evac_u)
            for nb in range(8):
                wv, rw = wload(w_ffn_out[l][:, nb * 128:(nb + 1) * 128], FC, 128)
                for i in range(NT):
                    b = bank()
                    linear_tm(wv, rw, 128, actT, RactT, FC, i, b)
                    s.op("act", lambda e, b=b, i=i, nb=nb: e.copy(fst(i)[:, nb * 128:(nb + 1) * 128], ps[b][:, 0:128]),
                         reads=[Rps[b]], writes=[Rfst_of(i)])
            for i in range(NT):
                residual_update(i, None, 1, fsrc=fst(i), Rf=Rfst_of(i))

        qT = carve(5120, 4 * T, BF16).rearrange("p (c t) -> p c t", c=4)
        RqT = ares("qT")
        kT = carve(7680, 4 * 1536, BF16).rearrange("p (c t) -> p c t", c=4)
        RkT = ares("kT")
        Vaug = carve(10752, 12 * 4 * 144, BF16).rearrange("p (k h d) -> p k h d", k=12, h=4)
        RV = ares("Vaug")
        yT = carve(0, 8 * T, BF16).rearrange("p (c t) -> p c t", c=8)
        fbT = carve(14208, 15 * (T + 2), BF16).rearrange("p (c t) -> p c t", c=15)
        RfbT = ares("fbT")
        RyT = RhT
        ropeC_sb = carve(23824, T, F32)
        ropeS_sb = carve(23824 + T, T, F32)
        Rrope = Res("rope")
        rtmp = sb("rtmp", [128, 2, 512])
        Rrtmp = Res("rtmp")
        raw_b = sb("raw_b", [128, 512], BF16)
        Rraw = Res("raw_b")

        def even_inproj():
            s.dma("sp", ropeC_sb[:], ropeC, writes=[Rrope])
            s.dma("sp", ropeS_sb[:], ropeS, writes=[Rrope])
            s.dma("pool", kT[:, :, 0:256], cacheKT.rearrange("(h p) t -> p h t", p=128), writes=[RkT])
            for kk_ in range(2):
                s.dma("pool", Vaug[:, kk_, :, 0:128],
                      cacheV[kk_ * 128:(kk_ + 1) * 128, :].rearrange("p (h d) -> p h d", h=4), writes=[RV])
            if "m" in sub:
                s.op("pool", lambda e: e.memset(Vaug[:, :, :, 128:129], 1.0), writes=[RV])
            for which, dst, doff in ((0, qT, 0), (1, kT, 256)):
                if "q" not in sub:
                    break
                wv, rw = wload(w_in_even[:, which * 512:(which + 1) * 512], KC, 512)
                Rdst = RqT if which == 0 else RkT
                for h in range(4):
                    def evac(b, t0, tn, h=h, dst=dst, doff=doff, Rdst=Rdst):
                        s.op("act", lambda e, b=b, tn=tn: e.copy(raw_b[:, 0:tn], ps[b][:, 0:tn]),
                             reads=[Rps[b]], writes=[Rraw])
                        b2 = bank()
                        s.op("pe", lambda e, b2=b2, tn=tn: e.matmul(ps[b2][:, 0:tn], rotT_b[:], raw_b[:, 0:tn], start=True, stop=True),
                             reads=[Rraw, Rconst], writes=[Rps[b2]])
                        s.op("dve", lambda e, t0=t0, tn=tn: e.tensor_tensor(rtmp[:, 0, 0:tn], raw_b[:, 0:tn], ropeC_sb[:, t0:t0 + tn], ALU.mult),
                             reads=[Rraw, Rrope], writes=[Rrtmp])
                        s.op("dve", lambda e, b2=b2, t0=t0, tn=tn: e.tensor_tensor(rtmp[:, 1, 0:tn], ps[b2][:, 0:tn], ropeS_sb[:, t0:t0 + tn], ALU.mult),
                             reads=[Rps[b2], Rrope], writes=[Rrtmp])
                        s.op("dve", lambda e, t0=t0, tn=tn: e.tensor_tensor(dst[:, h, doff + t0:doff + t0 + tn], rtmp[:, 0, 0:tn], rtmp[:, 1, 0:tn], ALU.add),
                             reads=[Rrtmp], writes=[Rdst])
                    linear_fm(wv, rw, h * 128, hT, RhT, KC, evac)
                if which == 1:
                    for i in range(NT):
                        b = bank()
                        linear_tm(wv, rw, 512, hT, RhT, KC, i, b)
                        si = stctr[0] % 2
                        stctr[0] += 1
                        s.op("act", lambda e, b=b, si=si: e.copy(stage_f[:, si, :], ps[b][:]), reads=[Rps[b]], writes=[Rstage[si]])
                        s.dma("sp", k_out[i * 128:(i + 1) * 128, :], stage_f[:, si, :], reads=[Rstage[si]])
            s.op("pool", lambda e: e.memset(fbT[:, :, 0:1], 0.0), writes=[RfbT])
            s.op("pool", lambda e: e.memset(fbT[:, :, T + 1:T + 2], 0.0), writes=[RfbT])
            for pc, (c0, ncols) in enumerate([(1536, 512), (2048, 512), (2560, 512), (3072, 384)]):
                wv, rw = wload(w_in_even[:, c0:c0 + ncols], KC, ncols)
                for j in range(ncols // 128):
                    ch = pc * 4 + j

                    def evac_fb(b, t0, tn, ch=ch):
                        s.op("act", lambda e, b=b, t0=t0, tn=tn: e.copy(fbT[:, ch, 1 + t0:1 + t0 + tn], ps[b][:, 0:tn]),
                             reads=[Rps[b]], writes=[RfbT])
                    linear_fm(wv, rw, j * 128, hT, RhT, KC, evac_fb)
            wv, rw = wload(w_in_even[:, 1024:1536], KC, 512)
            for i in range(NT if "v" in sub else 0):
                b = bank()
                linear_tm(wv, rw, 512, hT, RhT, KC, i, b)
                si = stctr[0] % 2
                stctr[0] += 1
                s.op("act", lambda e, b=b, si=si: e.copy(stage_f[:, si, :], ps[b][:]), reads=[Rps[b]], writes=[Rstage[si]])
                s.dma("sp", v_out[i * 128:(i + 1) * 128, :], stage_f[:, si, :], reads=[Rstage[si]])
                s.op("dve", lambda e, si=si, i=i: e.tensor_copy(Vaug[:, 2 + i, :, 0:128], stage_f[:, si, :].rearrange("p (h d) -> p h d", h=4)),
                     reads=[Rstage[si]], writes=[RV])


        NPB = 4
        PT = [[sb("PT%d%d" % (m, k), [128, 256], BF16) for k in range(NPB)] for m in range(2)]
        RPT = [[Res("PT%d%d" % (m, k)) for k in range(NPB)] for m in range(2)]
        ya_f = sb("ya_f", [128, 128])
        Rya = Res("ya_f")
        ya_b = sb("ya_b", [128, 128], BF16)
        Ryab = Res("ya_b")
        subln08 = sb("subln08", [128, 128])
        Rsub = Res("subln08")
        lamt = sb("lamt", [128, 64])
        Rlamt = Res("lamt")

        def attention():
            lo = ROFF["lam"]
            for k in range(2):
                s.op("dve", lambda e, k=k: e.tensor_tensor(lamt[:], rowp_sb[:, lo + 128 * k: lo + 128 * k + 64],
                                                          rowp_sb[:, lo + 128 * k + 64: lo + 128 * k + 128], ALU.mult),
                     reads=[Rrowp], writes=[Rlamt])
                s.op("dve", lambda e, k=k: e.reduce_sum(stat[:, 20 + k:21 + k], lamt[:], axis=AX.X),
                     reads=[Rlamt], writes=[Rstat])
            s.op("act", lambda e: e.activation(stat[:, 22:24], stat[:, 20:22], AF.Exp), reads=[Rstat], writes=[Rstat])
            s.op("dve", lambda e: e.tensor_tensor(stat[:, 24:25], stat[:, 22:23], stat[:, 23:24], ALU.subtract),
                 reads=[Rstat], writes=[Rstat])
            s.op("dve", lambda e: e.tensor_scalar(stat[:, 25:26], stat[:, 24:25], 0.2, -1.0, ALU.add, ALU.mult),
                 reads=[Rstat], writes=[Rstat])
            s.op("dve", lambda e: e.tensor_scalar(subln08[:], rowp_sb[:, ROFF["subln"]:ROFF["subln"] + 128], 0.8, None, ALU.mult),
                 reads=[Rrowp], writes=[Rsub])
            pctr = [0, 0]
            for h in range(4):
                for qu in range(5):
                    acc = [bank(), bank()]
                    reserved.update(acc)
                    pend = []
                    for kt in range(12):
                        ku = 0 if kt < 2 else 1 + (kt - 2) // 2
                        bcol = ROFF["abias"] + ku * 5 + qu
                        for m in range(2):
                            bs = bank()
                            s.op("pe", lambda e, bs=bs, m=m, h=h, kt=kt, qu=qu: e.matmul(
                                ps[bs][:, 0:256], kT[64 * m:64 * m + 64, h, kt * 128:(kt + 1) * 128],
                                qT[64 * m:64 * m + 64, h, qu * 256:(qu + 1) * 256], start=True, stop=True),
                                reads=[RkT, RqT], writes=[Rps[bs]])
                            pk = pctr[m] % NPB
                            pctr[m] += 1
                            s.op("act", lambda e, bs=bs, m=m, pk=pk, bcol=bcol: e.activation(
                                PT[m][pk][:], ps[bs][:, 0:256], AF.Exp, bias=rowp_sb[:, bcol:bcol + 1], scale=0.125),
                                reads=[Rps[bs], Rrowp], writes=[RPT[m][pk]])

                            def pv(m=m, pk=pk, kt=kt, h=h, acc=acc):
                                for qt in range(2):
                                    s.op("pe", lambda e, qt=qt: e.matmul(
                                        ps[acc[m]][:, qt * 129:qt * 129 + 129], PT[m][pk][:, qt * 128:(qt + 1) * 128],
                                        Vaug[:, kt, h, 0:129], start=(kt == 0 and qt == 0), stop=(kt == 11),
                                        skip_group_check=True),
                                        reads=[RPT[m][pk], RV], writes=[Rps[acc[m]]])
                            pend.append(pv)
                            if len(pend) > 2:
                                pend.pop(0)()
                    while pend:
                        pend.pop(0)()
                    reserved.difference_update(acc)
                    for qt in range(2):
                        i = qu * 2 + qt
                        c0 = qt * 129
                        s.op("dve", lambda e, c0=c0, acc=acc: e.reciprocal(stat[:, 30:31], ps[acc[0]][:, c0 + 128:c0 + 129]),
                             reads=[Rps[acc[0]]], writes=[Rstat])
                        s.op("dve", lambda e, c0=c0, acc=acc: e.reciprocal(stat[:, 31:32], ps[acc[1]][:, c0 + 128:c0 + 129]),
                             reads=[Rps[acc[1]]], writes=[Rstat])
                        s.op("dve", lambda e: e.tensor_tensor(stat[:, 32:33], stat[:, 31:32], stat[:, 25:26], ALU.mult),
                             reads=[Rstat], writes=[Rstat])
                        s.op("dve", lambda e, c0=c0, acc=acc: e.tensor_scalar(ya_f[:], ps[acc[0]][:, c0:c0 + 128], stat[:, 30:31], None, ALU.mult),
                             reads=[Rps[acc[0]], Rstat], writes=[Rya])
                        s.op("dve", lambda e, c0=c0, acc=acc: e.scalar_tensor_tensor(ya_f[:], ps[acc[1]][:, c0:c0 + 128], stat[:, 32:33], ya_f[:], ALU.mult, ALU.add),
                             reads=[Rps[acc[1]], Rstat, Rya], writes=[Rya])
                        s.op("act", lambda e: e.activation(junk_b[:, 0:128], ya_f[:], AF.Square, scale=1.0 / math.sqrt(128.0),
                                                           accum_out=stat[:, 33:34]),
                             reads=[Rya], writes=[Rjunk, Rstat])
                        s.op("dve", lambda e: e.tensor_scalar(stat[:, 34:35], stat[:, 33:34], 1e-6, None, ALU.add),
                             reads=[Rstat], writes=[Rstat])
                        s.op("act", lambda e: e.activation(stat[:, 34:35], stat[:, 34:35], AF.Sqrt), reads=[Rstat], writes=[Rstat])
                        s.op("dve", lambda e: e.reciprocal(stat[:, 35:36], stat[:, 34:35]), reads=[Rstat], writes=[Rstat])
                        s.op("dve", lambda e: e.scalar_tensor_tensor(ya_b[:], ya_f[:], stat[:, 35:36], subln08[:], ALU.mult, ALU.mult),
                             reads=[Rya, Rstat, Rsub], writes=[Ryab])
                        reserved.update(acc)
                        bt = bank()
                        reserved.difference_update(acc)
                        s.op("pe", lambda e, bt=bt: e.transpose(psb[bt][:, 0:128], ya_b[:], ident_b[:]),
                             reads=[Ryab, Rconst], writes=[Rps[bt]])
                        s.op("act", lambda e, bt=bt, h=h, i=i: e.copy(yT[:, h, i * 128:(i + 1) * 128], psb[bt][:, 0:128]),
                             reads=[Rps[bt]], writes=[RyT])

        def out_proj(W, actTv, Ract):
            pieces = [wload(W[:, hh * 512:(hh + 1) * 512], KC, 512) for hh in range(2)]
            for i in range(NT):
                bb = [bank(), bank()]
                for hh in range(2):
                    linear_tm(pieces[hh][0], pieces[hh][1], 512, actTv, Ract, KC, i, bb[hh])
                residual_update(i, bb, 0)

        Xc = carve(5120, NT * 8 * 256, BF16).rearrange("p (i g n) -> p i g n", i=NT, g=8)
        RXc = ares("Xc")
        chCS_b = sb("chCS_b", [128, 256], BF16)
        Rch = Res("chCS")

        def fnet():
            s.dma("pool", chCS_b[:], chCS, writes=[Rch])
            for i in range(NT):
                for gp in range(4):
                    b = bank()
                    for g2_ in range(2):
                        g = gp * 2 + g2_
                        s.op("pe", lambda e, b=b, g=g, g2_=g2_, i=i: e.matmul(
                            ps[b][:, g2_ * 256:(g2_ + 1) * 256], hT[:, g, i * 128:(i + 1) * 128], chCS_b[:], start=True, stop=True),
                            reads=[RhT, Rch], writes=[Rps[b]])
                    s.op("act", lambda e, b=b, gp=gp, i=i: e.copy(
                        Xc[:, i, gp * 2:gp * 2 + 2, :], ps[b][:].rearrange("p (g n) -> p g n", g=2)),
                        reads=[Rps[b]], writes=[RXc])
            for (t0, tn) in [(0, 384), (384, 384), (768, 384), (1152, 128)]:
                wc, rc = wload(dftC[:, t0:t0 + tn], NT, tn)
                wsn, rsn = wload(dftSn[:, t0:t0 + tn], NT, tn)
                for g in range(8):
                    b = bank()
                    for tc in range(NT):
                        s.op("pe", lambda e, b=b, g=g, tc=tc, tn=tn, wc=wc: e.matmul(
                            ps[b][:, 0:tn], Xc[:, tc, g, 0:128], wc[:, tc, 0:tn], start=(tc == 0), stop=False),
                            reads=[RXc, rc], writes=[Rps[b]])
                        s.op("pe", lambda e, b=b, g=g, tc=tc, tn=tn, wsn=wsn: e.matmul(
                            ps[b][:, 0:tn], Xc[:, tc, g, 128:256], wsn[:, tc, 0:tn], start=False, stop=(tc == NT - 1)),
                            reads=[RXc, rsn], writes=[Rps[b]])
                    s.op("act", lambda e, b=b, g=g, t0=t0, tn=tn: e.copy(yT[:, g, t0:t0 + tn], ps[b][:, 0:tn]),
                         reads=[Rps[b]], writes=[RyT])


        V_tm = carve(5120, NT * 512, BF16).rearrange("p (i n) -> p i n", i=NT)
        RVtm = ares("V_tm")
        YSb = carve(7680, NT * 512, BF16).rearrange("p (i n) -> p i n", i=NT)
        RYS = ares("YSb")
        Ystage = carve(10240, 128 * 16, F32)[0:64, :].rearrange("p (n k) -> p n k", k=16)
        RYst = ares("Ystage")
        Sst = [carve(12288, 512, F32), carve(12800, 512, F32)]
        RS = [ares("S0"), ares("S1")]
        tmpA = carve(13312, 512, F32)
        RtA = ares("tmpA")
        tmpB = carve(23824, 512, F32)
        RtB = ares("tmpB")
        w2t_b = carve(24336, 512, BF16)
        a2t_b = carve(24592, 512, BF16)
        g2_b = carve(24848, 512, BF16)
        Rlora = ares("lora")
        PT0 = 25104
        NPT = 12

        def pt(k, n=1):
            return carve(PT0 + 128 * k, 128 * n, F32)
        Rpt = [ares("pt%d" % k) for k in range(NPT)]
        lnx0_b = carve(26640, 512, BF16)
        lnx1_b = carve(26896, 512, BF16)
        Rlnx = ares("lnx")
        tb0 = junk_b[:, 0:128]
        tb1 = junk_b[:, 128:256]
        Rtb0 = Res("tb0")
        Rtb1 = Res("tb1")
        colx = sb("colx", [128, 64])
        Rcolx = Res("colx")
        tiny = sb("tiny", [128, 8])
        Rtiny = Res("tiny")
        BS = sb("BS", [128, NT, 8])
        RBS = Res("BS")
        wbf = [w[:].bitcast(F32) for w in wbuf]
        TW = wbf[0][:, 0:1024].rearrange("p (k n) -> p k n", k=8)
        TKK = wbf[0][:, 1024:2048].rearrange("p (k n) -> p k n", k=8)
        TNK = wbf[1][:, 0:1024].rearrange("p (k n) -> p k n", k=8)
        TKD = wbf[1][:, 1024:2048].rearrange("p (k n) -> p k n", k=8)
        TR2 = wbuf[2][:, 0:2048].rearrange("p (k n h) -> p k n h", k=8, h=2)
        tmpA_b = carve(10240, 512, BF16)
        RtAb = ares("tmpA_b")
        Sb16 = carve(10496, 512, BF16)
        RSb = ares("Sb16")
        CX = dict(cmu=0, nmu0=15, nmu1=30, omk1=45, keepL=49, keepR=54)

        def cx(name, c=0):
            o = CX[name] + c
            return colx[:, o:o + 1]

        def rwkv_setup():
            s.dma("pool", w2t_b, w2t_d, writes=[Rlora])
            s.dma("pool", a2t_b, a2t_d, writes=[Rlora])
            s.dma("pool", g2_b, g2_d, writes=[Rlora])
            s.dma("pool", lnx0_b, lnx_d[0, :].partition_broadcast(128), writes=[Rlnx])
            s.dma("pool", lnx1_b, lnx_d[1, :].partition_broadcast(128), writes=[Rlnx])
            mu0 = cp("mu0", 0, 15)
            mu1 = cp("mu1", 0, 15)
            s.op("dve", lambda e: e.tensor_tensor(colx[:, 0:15], mu0, mu1, ALU.add), reads=[Rcolp], writes=[Rcolx])
            s.op("dve", lambda e: e.tensor_scalar(colx[:, 0:15], colx[:, 0:15], -1.0, 1.0, ALU.mult, ALU.add),
                 reads=[Rcolx], writes=[Rcolx])
            s.op("dve", lambda e: e.tensor_scalar(colx[:, 15:30], mu0, -1.0, None, ALU.mult), reads=[Rcolp], writes=[Rcolx])
            s.op("dve", lambda e: e.tensor_scalar(colx[:, 30:45], mu1, -1.0, None, ALU.mult), reads=[Rcolp], writes=[Rcolx])
            s.op("dve", lambda e: e.tensor_scalar(colx[:, 45:49], cp("kvec1", 0, 4), -1.0, 1.0, ALU.mult, ALU.add),
                 reads=[Rcolp], writes=[Rcolx])
            s.op("dve", lambda e: e.tensor_scalar(colx[:, 49:54], rowp_sb[:, ROFF["bndL"]:ROFF["bndL"] + 5], -1.0, 1.0, ALU.mult, ALU.add),
                 reads=[Rrowp], writes=[Rcolx])
            s.op("dve", lambda e: e.tensor_scalar(colx[:, 54:59], rowp_sb[:, ROFF["bndR"]:ROFF["bndR"] + 5], -1.0, 1.0, ALU.mult, ALU.add),
                 reads=[Rrowp], writes=[Rcolx])
            s.op("dve", lambda e: e.memset(BS[:], 0.0), writes=[RBS])

        def shift(ch, ti, out, Rout):
            t0 = ti * 128
            f = fbT[:, ch, 1 + t0:1 + t0 + 128]
            fp = fbT[:, ch, t0:t0 + 128]
            fn = fbT[:, ch, 2 + t0:2 + t0 + 128]
            rr = [RfbT, Rcolp, Rcolx]
            s.op("dve", lambda e: e.tensor_scalar(out, f, cx("cmu", ch), None, ALU.mult), reads=rr, writes=[Rout])
            s.op("dve", lambda e: e.scalar_tensor_tensor(out, fp, cp("mu0", ch), out, ALU.mult, ALU.add),
                 reads=rr + [Rout], writes=[Rout])
            s.op("dve", lambda e: e.scalar_tensor_tensor(out, fn, cp("mu1", ch), out, ALU.mult, ALU.add),
                 reads=rr + [Rout], writes=[Rout])
            u = ti // 2
            if ti % 2 == 0:
                bl = rowp_sb[:, ROFF["bndL"] + u:ROFF["bndL"] + u + 1]
                s.op("dve", lambda e: e.tensor_tensor(tiny[:, 0:1], fbT[:, ch, t0:t0 + 1], bl, ALU.mult),
                     reads=[RfbT, Rrowp], writes=[Rtiny])
                s.op("dve", lambda e: e.scalar_tensor_tensor(out[:, 0:1], tiny[:, 0:1], cx("nmu0", ch), out[:, 0:1], ALU.mult, ALU.add),
                     reads=[Rtiny, Rcolx, Rout], writes=[Rout])
            else:
                br = rowp_sb[:, ROFF["bndR"] + u:ROFF["bndR"] + u + 1]
                s.op("dve", lambda e: e.tensor_tensor(tiny[:, 1:2], fbT[:, ch, 1 + t0 + 128:2 + t0 + 128], br, ALU.mult),
                     reads=[RfbT, Rrowp], writes=[Rtiny])
                s.op("dve", lambda e: e.scalar_tensor_tensor(out[:, 127:128], tiny[:, 1:2], cx("nmu1", ch), out[:, 127:128], ALU.mult, ALU.add),
                     reads=[Rtiny, Rcolx, Rout], writes=[Rout])

        def rwkv_prepass():
            for ti in range(NT):
                for c in range(4):
                    shift(8 + c, ti, pt(c), Rpt[c])
                    s.op("act", lambda e, c=c: e.copy(xn_b[:, c * 128:(c + 1) * 128], pt(c)), reads=[Rpt[c]], writes=[Rxn])
                b = bank()
                for c in range(4):
                    s.op("pe", lambda e, b=b, c=c: e.transpose(psb[b][:, c * 128:(c + 1) * 128], xn_b[:, c * 128:(c + 1) * 128], ident_b[:]),
                         reads=[Rxn, Rconst], writes=[Rps[b]])
                s.op("act", lambda e, b=b, ti=ti: e.copy(V_tm[:, ti, :], psb[b][:, 0:512]), reads=[Rps[b]], writes=[RVtm])

        def prep(d, ti):
            rev = (d == 1)

            def tab(T3, c):
                v = T3[:, d * 4 + c, :]
                return v[:, ::-1] if rev else v
            tabs_w = [Rw[0], Rw[1], Rw[2]]
            shift(12, ti, pt(0), Rpt[0])
            shift(13, ti, pt(1), Rpt[1])
            s.op("act", lambda e: e.activation(tb0, pt(0), AF.Tanh), reads=[Rpt[0]], writes=[Rtb0])
            s.op("act", lambda e: e.copy(tb1, pt(1)), reads=[Rpt[1]], writes=[Rtb1])
            lo, hi = 64 * d, 64 * d + 64
            for c in range(4):
                b = bank()
                s.op("pe", lambda e, b=b, c=c: e.matmul(ps[b][:, 0:128], w2t_b[lo:hi, c * 128:(c + 1) * 128], tb0[lo:hi, :], start=True, stop=True),
                     reads=[Rlora, Rtb0], writes=[Rps[b]])
                s.op("act", lambda e, b=b, c=c: e.activation(pt(2), ps[b][:, 0:128], AF.Sigmoid, bias=cp("w0_%d" % d, c)),
                     reads=[Rps[b], Rcolp], writes=[Rpt[2]])
                s.op("act", lambda e, c=c: e.activation(tab(TW, c), pt(2), AF.Exp, scale=-EXPM05),
                     reads=[Rpt[2]], writes=[Rw[0]])
                b = bank()
                s.op("pe", lambda e, b=b, c=c: e.matmul(ps[b][:, 0:128], a2t_b[lo:hi, c * 128:(c + 1) * 128], tb1[lo:hi, :], start=True, stop=True),
                     reads=[Rlora, Rtb1], writes=[Rps[b]])
                s.op("act", lambda e, b=b, c=c: e.activation(pt(3), ps[b][:, 0:128], AF.Sigmoid, bias=cp("a0_%d" % d, c)),
                     reads=[Rps[b], Rcolp], writes=[Rpt[3]])
                shift(4 + c, ti, pt(4), Rpt[4])
                s.op("dve", lambda e, c=c: e.tensor_scalar(pt(5), pt(4), cp("kvec0", c), None, ALU.mult),
                     reads=[Rpt[4], Rcolp], writes=[Rpt[5]])
                s.op("act", lambda e: e.activation(pt(6), pt(5), AF.Square), reads=[Rpt[5]], writes=[Rpt[6]])
                b = bank()
                s.op("pe", lambda e, b=b: e.matmul(ps[b][:, 0:128], bones_f[:], pt(6), start=True, stop=True),
                     reads=[Rconst, Rpt[6]], writes=[Rps[b]])
                s.op("dve", lambda e, b=b: e.tensor_scalar(pt(7), ps[b][:, 0:128], 1e-12, None, ALU.add),
                     reads=[Rps[b]], writes=[Rpt[7]])
                s.op("act", lambda e: e.activation(pt(7), pt(7), AF.Sqrt), reads=[Rpt[7]], writes=[Rpt[7]])
                s.op("dve", lambda e: e.reciprocal(pt(7), pt(7)), reads=[Rpt[7]], writes=[Rpt[7]])
                s.op("dve", lambda e: e.tensor_tensor(pt(8), pt(5), pt(7), ALU.mult), reads=[Rpt[5], Rpt[7]], writes=[Rpt[8]])
                s.op("act", lambda e, c=c: e.copy(tab(TKK, c), pt(8)), reads=[Rpt[8]], writes=[Rw[0]])
                s.op("dve", lambda e, c=c: e.scalar_tensor_tensor(tab(TNK, c), pt(8), -1.0, pt(3), ALU.mult, ALU.mult),
                     reads=[Rpt[8], Rpt[3]], writes=[Rw[1]])
                s.op("dve", lambda e, c=c: e.tensor_scalar(pt(9), pt(3), cp("kvec1", c), cx("omk1", c), ALU.mult, ALU.add),
                     reads=[Rpt[3], Rcolp, Rcolx], writes=[Rpt[9]])
                s.op("dve", lambda e: e.tensor_tensor(pt(9), pt(4), pt(9), ALU.mult), reads=[Rpt[4], Rpt[9]], writes=[Rpt[9]])
                s.op("act", lambda e, c=c: e.copy(tab(TKD, c), pt(9)), reads=[Rpt[9]], writes=[Rw[1]])
                shift(c, ti, pt(10), Rpt[10])
                for h2 in range(2):
                    o = TR2[:, d * 4 + c, :, h2]
                    if rev:
                        o = o[:, ::-1]
                    s.op("dve", lambda e, o=o, h2=h2: e.tensor_scalar(o, pt(10), halfsel_f[:, h2:h2 + 1], None, ALU.mult),
                         reads=[Rpt[10], Rconst], writes=[Rw[2]])
                s.op("dve", lambda e: e.tensor_tensor(pt(6), pt(10), pt(9), ALU.mult), reads=[Rpt[10], Rpt[9]], writes=[Rpt[6]])
                s.op("dve", lambda e, c=c: e.tensor_scalar(pt(6), pt(6), cp("kvec2", c), None, ALU.mult),
                     reads=[Rpt[6], Rcolp], writes=[Rpt[6]])
                b = bank()
                s.op("pe", lambda e, b=b: e.matmul(ps[b][:, 0:2], pt(6), halfsel_f[:, 0:2], start=True, stop=True),
                     reads=[Rpt[6], Rconst], writes=[Rps[b]])
                s.op("dve", lambda e, b=b, c=c, ti=ti: e.tensor_tensor(BS[:, ti, c * 2:c * 2 + 2], BS[:, ti, c * 2:c * 2 + 2], ps[b][:, 0:2], ALU.add),
                     reads=[Rps[b], RBS], writes=[RBS])

        yb = xn_b[:, 0:512]

        def finalize(ti, by):
            ys = pt(0, 4)
            Rys = [Rpt[0], Rpt[1], Rpt[2], Rpt[3]]
            t2 = pt(4, 4)
            Rt2 = [Rpt[4], Rpt[5], Rpt[6], Rpt[7]]
            ys3 = ys.rearrange("p (h i) -> p h i", h=8)
            t23 = t2.rearrange("p (h i) -> p h i", h=8)
            s.op("dve", lambda e: e.tensor_tensor(ys, YSb[:, ti, :], ps[by][:], ALU.add), reads=[RYS, Rps[by]], writes=Rys)
            s.op("dve", lambda e: e.reduce_sum(tiny[:, 0:8], ys3, axis=AX.X), reads=Rys, writes=[Rtiny])
            s.op("dve", lambda e: e.tensor_scalar(tiny[:, 0:8], tiny[:, 0:8], -1.0 / 64.0, None, ALU.mult), reads=[Rtiny], writes=[Rtiny])
            s.op("dve", lambda e: e.tensor_tensor(ys3, ys3, tiny[:, 0:8].unsqueeze(2).to_broadcast([128, 8, 64]), ALU.add),
                 reads=Rys + [Rtiny], writes=Rys)
            s.op("dve", lambda e: e.tensor_tensor(t2, ys, ys, ALU.mult), reads=Rys, writes=Rt2)
            s.op("dve", lambda e: e.reduce_sum(stat[:, 40:48], t23, axis=AX.X), reads=Rt2, writes=[Rstat])
            s.op("dve", lambda e: e.tensor_scalar(stat[:, 40:48], stat[:, 40:48], 1.0 / 64.0, 64e-5, ALU.mult, ALU.add),
                 reads=[Rstat], writes=[Rstat])
            s.op("act", lambda e: e.activation(stat[:, 40:48], stat[:, 40:48], AF.Sqrt), reads=[Rstat], writes=[Rstat])
            s.op("dve", lambda e: e.reciprocal(stat[:, 48:56], stat[:, 40:48]), reads=[Rstat], writes=[Rstat])
            s.op("dve", lambda e: e.tensor_tensor(ys3, ys3, stat[:, 48:56].unsqueeze(2).to_broadcast([128, 8, 64]), ALU.mult),
                 reads=Rys + [Rstat], writes=Rys)
            s.op("dve", lambda e: e.tensor_tensor(ys, ys, lnx0_b, ALU.mult), reads=Rys + [Rlnx], writes=Rys)
            s.op("dve", lambda e: e.tensor_tensor(ys, ys, lnx1_b, ALU.add), reads=Rys + [Rlnx], writes=Rys)
            s.op("dve", lambda e: e.tensor_tensor(t23, V_tm[:, ti, :].rearrange("p (h i) -> p h i", h=8),
                                                  BS[:, ti, :].unsqueeze(2).to_broadcast([128, 8, 64]), ALU.mult),
                 reads=[RVtm, RBS], writes=Rt2)
            s.op("dve", lambda e: e.tensor_tensor(ys, ys, t2, ALU.add), reads=Rys + Rt2, writes=Rys)
            shift(14, ti, pt(8), Rpt[8])
            s.op("act", lambda e: e.activation(tb0, pt(8), AF.Sigmoid), reads=[Rpt[8]], writes=[Rtb0])
            bg = bank()
            s.op("pe", lambda e, bg=bg: e.matmul(ps[bg][:], tb0, g2_b, start=True, stop=True),
                 reads=[Rtb0, Rlora], writes=[Rps[bg]])
            s.op("dve", lambda e, bg=bg: e.tensor_tensor(yb, ys, ps[bg][:], ALU.mult), reads=Rys + [Rps[bg]], writes=[Rxn])
            bt = bank()
            for c in range(4):
                s.op("pe", lambda e, bt=bt, c=c: e.transpose(psb[bt][:, c * 128:(c + 1) * 128], yb[:, c * 128:(c + 1) * 128], ident_b[:]),
                     reads=[Rxn, Rconst], writes=[Rps[bt]])
            s.op("act", lambda e, bt=bt, ti=ti: e.copy(yT[:, 4:8, ti * 128:(ti + 1) * 128],
                                                        psb[bt][:, 0:512].rearrange("p (c t) -> p c t", c=4)),
                 reads=[Rps[bt]], writes=[RyT])

        def rwkv_scan(nrounds=NT):
            S8 = [x_.rearrange("p (k i) -> p k i", k=8) for x_ in Sst]
            tA8 = tmpA.rearrange("p (k i) -> p k i", k=8)
            tB8 = tmpB.rearrange("p (k i) -> p k i", k=8)

            def bc(T3, n):
                return T3[:, :, n].unsqueeze(2).to_broadcast([128, 8, 64])
            for r in range(nrounds):
                tf, tbk = r, NT - 1 - r
                prep(0, tf)
                prep(1, tbk)
                if tf % 2 == 0:
                    u = tf // 2
                    if u == 0:
                        s.dma("sp", Sst[0][:, 0:256], initS[0], writes=[RS[0]])
                    else:
                        s.op("dve", lambda e, u=u: e.tensor_scalar(Sst[0][:, 0:256], Sst[0][:, 0:256], cx("keepL", u), None, ALU.mult),
                             reads=[RS[0], Rcolx], writes=[RS[0]])
                if tbk % 2 == 1:
                    u = tbk // 2
                    if u == 4:
                        s.op("dve", lambda e: e.memset(Sst[0][:, 256:512], 0.0), writes=[RS[0]])
                    elif u == 3:
                        s.dma("sp", tmpB[:, 0:256], initS[1], writes=[RtB])
                        s.op("dve", lambda e, u=u: e.scalar_tensor_tensor(Sst[0][:, 256:512], Sst[0][:, 256:512], cx("keepR", u),
                                                                          tmpB[:, 0:256], ALU.mult, ALU.add),
                             reads=[RS[0], Rcolx, RtB], writes=[RS[0]])
                    else:
                        s.op("dve", lambda e, u=u: e.tensor_scalar(Sst[0][:, 256:512], Sst[0][:, 256:512], cx("keepR", u), None, ALU.mult),
                             reads=[RS[0], Rcolx], writes=[RS[0]])
                by_ = None
                pending = []

                def flush():
                    for f_ in pending:
                        f_()
                    del pending[:]
                for n in range(128):
                    ci, ni = n % 2, (n + 1) % 2
                    Sc, Sn = S8[ci], S8[ni]
                    if n % 32 == 0:
                        by_ = bank()
                        reserved.add(by_)
                    tAb8 = tmpA_b.rearrange("p (k i) -> p k i", k=8)
                    s.op("dve", lambda e, n=n, Sc=Sc, tAb8=tAb8: e.tensor_tensor(tAb8, Sc, bc(TKK, n), ALU.mult),
                         reads=[RS[ci], Rw[0]], writes=[RtAb])
                    s.op("dve", lambda e, n=n, Sc=Sc, Sn=Sn: e.tensor_tensor(Sn, Sc, bc(TW, n), ALU.mult),
                         reads=[RS[ci], Rw[0]], writes=[RS[ni]])
                    bv = bank()
                    for d in range(2):
                        tile_d = tf if d == 0 else tbk
                        row = n if d == 0 else 127 - n
                        for h2 in range(2):
                            s.op("pe", lambda e, bv=bv, d=d, h2=h2, tile_d=tile_d, row=row: e.matmul(
                                ps[bv][64 * h2:64 * h2 + 64, d * 256:(d + 1) * 256].rearrange("p (c i) -> p c i", c=4),
                                ident_b[:, row:row + 1].to_broadcast([128, 64]),
                                V_tm[:, tile_d, :].rearrange("p (c h i) -> p c h i", c=4, h=2)[:, :, h2, :],
                                start=True, stop=True),
                                reads=[RVtm, Rconst], writes=[Rps[bv]])
                    bs_ = bank()
                    s.op("pe", lambda e, bs_=bs_: e.matmul(ps[bs_][:], bones_b[:], tmpA_b, start=True, stop=True),
                         reads=[RtAb, Rconst], writes=[Rps[bs_]])
                    flush()
                    s.op("dve", lambda e, n=n, bv=bv: e.tensor_tensor(tB8, ps[bv][:].rearrange("p (k i) -> p k i", k=8), bc(TKD, n), ALU.mult),
                         reads=[Rps[bv], Rw[1]], writes=[RtB])
                    s.op("dve", lambda e, ni=ni: e.tensor_tensor(Sst[ni], Sst[ni], tmpB, ALU.add),
                         reads=[RS[ni], RtB], writes=[RS[ni]])
                    s.op("dve", lambda e, n=n, bs_=bs_: e.tensor_tensor(tA8, ps[bs_][:].rearrange("p (k i) -> p k i", k=8), bc(TNK, n), ALU.mult),
                         reads=[Rps[bs_], Rw[1]], writes=[RtA])
                    s.op("dve", lambda e, ni=ni: e.tensor_tensor(Sst[ni], Sst[ni], tmpA, ALU.add),
                         reads=[RS[ni], RtA], writes=[RS[ni]])
                    s.op("act", lambda e, ni=ni: e.copy(Sb16, Sst[ni]), reads=[RS[ni]], writes=[RSb])
                    nl = n % 32

                    def ymm(by_=by_, nl=nl, n=n):
                        for dc in range(8):
                            s.op("pe", lambda e, dc=dc: e.matmul(
                                ps[by_][0:64, nl * 16 + dc * 2:nl * 16 + dc * 2 + 2], Sb16[:, dc * 64:(dc + 1) * 64], TR2[:, dc, n, :],
                                start=True, stop=True),
                                reads=[RSb, Rw[2]], writes=[Rps[by_]])
                    pending.append(ymm)
                    if nl == 31:
                        flush()
                        n0 = n - 31
                        src = ps[by_][0:64, :].rearrange("p (n k) -> p n k", k=16)
                        s.op("act", lambda e, src=src, n0=n0: e.copy(Ystage[:, n0:n0 + 32, 0:8], src[:, :, 0:8]),
                             reads=[Rps[by_]], writes=[RYst])
                        s.op("act", lambda e, src=src, n0=n0: e.copy(Ystage[:, 96 - n0:128 - n0, 8:16][:, ::-1, :], src[:, :, 8:16]),
                             reads=[Rps[by_]], writes=[RYst])
                        reserved.discard(by_)
                if tf % 2 == 1:
                    s.dma("sp", s_out[tf // 2, 0], Sst[0][:, 0:256], reads=[RS[0]])
                if tbk % 2 == 0:
                    s.dma("sp", s_out[tbk // 2, 1], Sst[0][:, 256:512], reads=[RS[0]])
                for d in range(2):
                    tile_d = tf if d == 0 else tbk
                    byy = bank()
                    for k8 in range(8):
                        s.op("pe", lambda e, byy=byy, k8=k8, d=d: e.matmul(
                            ps[byy][:, k8 * 64:(k8 + 1) * 64], Ystage[:, :, d * 8 + k8], ident_f[0:64, 0:64], start=True, stop=True),
                            reads=[RYst, Rconst], writes=[Rps[byy]])
                    first = (r < 5)
                    if first:
                        s.op("act", lambda e, byy=byy, tile_d=tile_d: e.copy(YSb[:, tile_d, :], ps[byy][:]),
                             reads=[Rps[byy]], writes=[RYS])
                    else:
                        finalize(tile_d, byy)


        def wbw(i, off_w, nelem):
            return wbuf[i][:, 2 * off_w:2 * off_w + nelem]
        Ast = carve(13200, 512, F32)
        RAst = ares("Ast")
        Abf = carve(13712, 512, BF16)
        RAbf = ares("Abf")
        WS = []
        fB = 14208 + 8 * 641
        for wsi in range(2):
            d_ = {}
            if wsi == 0:
                for k_, nm in enumerate(["Lt0", "Lt1", "L0", "L1", "Z0"]):
                    d_[nm] = carve(10240 + 512 * k_, 512, F32)
                d_["MT"] = carve(12800, 256, BF16)
                d_["RT"] = carve(12928, 512, BF16)
                d_["Z1"] = wbuf[0][:, 0:1024].bitcast(F32)
                for k_, nm in enumerate(["Lakt", "Mrbt", "Mrkt", "BKtm"]):
                    d_[nm] = wbw(0, 512 + 256 * k_, 512)
                d_["Zfb"] = carve(10240, 512, BF16)
            else:
                for k_, nm in enumerate(["Lt0", "Lt1", "L0", "L1", "Z0"]):
                    d_[nm] = carve(fB + 512 * k_, 512, F32)
                d_["Z1"] = wbuf[1][:, 0:1024].bitcast(F32)
                d_["Lakt"] = wbw(0, 1536, 512)
                d_["Mrbt"] = wbw(0, 1792, 512)
                d_["Mrkt"] = wbw(1, 512, 512)
                d_["MT"] = wbw(1, 768, 256)
                d_["BKtm"] = carve(23824, 512, BF16)
                d_["RT"] = carve(24080, 512, BF16)
                d_["Zfb"] = carve(fB, 512, BF16)
            for nm in ["Lt0", "Lt1", "L0", "L1", "Z0", "Z1", "Lakt", "Mrbt", "Mrkt", "BKtm", "MT", "RT"]:
                d_["R" + nm] = ares("ws%d_%s" % (wsi, nm))
            d_["RZfb"] = d_["RLt0"]
            WS.append(d_)
        masks = wbw(1, 896, 4 * 4 * 128).rearrange("p (m q t) -> p m q t", m=4, q=4)
        Rmask = ares("masks")
        TBL = []
        for par in range(2):
            d_ = {}
            for k_, nm in enumerate(["RH", "AH", "BH", "KH"]):
                d_[nm] = wbw(2, par * 1024 + 256 * k_, 512).rearrange("p (c t) -> p c t", c=4)
                d_["R" + nm] = ares("tb%d_%s" % (par, nm))
            TBL.append(d_)
        pcs = sb("pcs", [128, 2, 4])
        Rpcs = [Res("pcs0"), Res("pcs1")]

        def prep_chunk(d, ti, par):
            TB = TBL[par]
            shift(12, ti, pt(0), Rpt[0])
            shift(13, ti, pt(1), Rpt[1])
            s.op("act", lambda e: e.activation(tb0, pt(0), AF.Tanh), reads=[Rpt[0]], writes=[Rtb0])
            s.op("act", lambda e: e.copy(tb1, pt(1)), reads=[Rpt[1]], writes=[Rtb1])
            lo, hi = 64 * d, 64 * d + 64
            for c in range(4):
                b = bank()
                s.op("pe", lambda e, b=b, c=c: e.matmul(ps[b][:, 0:128], w2t_b[lo:hi, c * 128:(c + 1) * 128], tb0[lo:hi, :], start=True, stop=True),
                     reads=[Rlora, Rtb0], writes=[Rps[b]])
                s.op("act", lambda e, b=b, c=c: e.activation(pt(2), ps[b][:, 0:128], AF.Sigmoid, bias=cp("w0_%d" % d, c)),
                     reads=[Rps[b], Rcolp], writes=[Rpt[2]])
                b = bank()
                s.op("pe", lambda e, b=b, c=c: e.matmul(ps[b][:, 0:128], a2t_b[lo:hi, c * 128:(c + 1) * 128], tb1[lo:hi, :], start=True, stop=True),
                     reads=[Rlora, Rtb1], writes=[Rps[b]])
                s.op("act", lambda e, b=b, c=c: e.activation(pt(3), ps[b][:, 0:128], AF.Sigmoid, bias=cp("a0_%d" % d, c)),
                     reads=[Rps[b], Rcolp], writes=[Rpt[3]])
                s.op("dve", lambda e: e.tensor_scalar(pt(2), pt(2), -EXPM05, None, ALU.mult), reads=[Rpt[2]], writes=[Rpt[2]])
                if d == 0:
                    s.op("dve", lambda e: e.tensor_tensor_scan(pt(0), pt(11), pt(2), 0.0, ALU.mult, ALU.add),
                         reads=[Rpt[2], Rpt[11]], writes=[Rpt[0]])
                else:
                    s.op("dve", lambda e: e.tensor_tensor_scan(pt(0)[:, ::-1], pt(11), pt(2)[:, ::-1], 0.0, ALU.mult, ALU.add),
                         reads=[Rpt[2], Rpt[11]], writes=[Rpt[0]])
                s.op("act", lambda e: e.activation(pt(1), pt(0), AF.Exp), reads=[Rpt[0]], writes=[Rpt[1]])
                s.op("dve", lambda e: e.tensor_tensor(pt(2), pt(0), pt(2), ALU.subtract), reads=[Rpt[0], Rpt[2]], writes=[Rpt[2]])
                s.op("act", lambda e: e.activation(pt(2), pt(2), AF.Exp), reads=[Rpt[2]], writes=[Rpt[2]])
                s.op("act", lambda e: e.activation(pt(0), pt(0), AF.Exp, scale=-1.0), reads=[Rpt[0]], writes=[Rpt[0]])
                yield
                pcol = 127 if d == 0 else 0
                s.op("dve", lambda e, c=c, pcol=pcol: e.tensor_copy(pcs[:, par, c:c + 1], pt(1)[:, pcol:pcol + 1]),
                     reads=[Rpt[1]], writes=[Rpcs[par]])
                yield
                shift(4 + c, ti, pt(4), Rpt[4])
                s.op("dve", lambda e, c=c: e.tensor_scalar(pt(5), pt(4), cp("kvec0", c), None, ALU.mult),
                     reads=[Rpt[4], Rcolp], writes=[Rpt[5]])
                s.op("act", lambda e: e.activation(pt(6), pt(5), AF.Square), reads=[Rpt[5]], writes=[Rpt[6]])
                b = bank()
                s.op("pe", lambda e, b=b: e.matmul(ps[b][:, 0:128], bones_f[:], pt(6), start=True, stop=True),
                     reads=[Rconst, Rpt[6]], writes=[Rps[b]])
                s.op("dve", lambda e, b=b: e.tensor_scalar(pt(7), ps[b][:, 0:128], 1e-12, None, ALU.add),
                     reads=[Rps[b]], writes=[Rpt[7]])
                s.op("act", lambda e: e.activation(pt(7), pt(7), AF.Ln), reads=[Rpt[7]], writes=[Rpt[7]])
                s.op("act", lambda e: e.activation(pt(7), pt(7), AF.Exp, scale=-0.5), reads=[Rpt[7]], writes=[Rpt[7]])
                s.op("dve", lambda e: e.tensor_tensor(pt(8), pt(5), pt(7), ALU.mult), reads=[Rpt[5], Rpt[7]], writes=[Rpt[8]])
                yield
                s.op("dve", lambda e, c=c: e.scalar_tensor_tensor(TB["AH"][:, c, :], pt(8), -1.0, pt(2), ALU.mult, ALU.mult),
                     reads=[Rpt[8], Rpt[2]], writes=[TB["RAH"]])
                s.op("dve", lambda e: e.tensor_tensor(pt(6), pt(8), pt(3), ALU.mult), reads=[Rpt[8], Rpt[3]], writes=[Rpt[6]])
                s.op("dve", lambda e, c=c: e.tensor_tensor(TB["BH"][:, c, :], pt(6), pt(0), ALU.mult),
                     reads=[Rpt[6], Rpt[0]], writes=[TB["RBH"]])
                yield
                s.op("dve", lambda e, c=c: e.tensor_scalar(pt(9), pt(3), cp("kvec1", c), cx("omk1", c), ALU.mult, ALU.add),
                     reads=[Rpt[3], Rcolp, Rcolx], writes=[Rpt[9]])
                s.op("dve", lambda e: e.tensor_tensor(pt(9), pt(4), pt(9), ALU.mult), reads=[Rpt[4], Rpt[9]], writes=[Rpt[9]])
                s.op("dve", lambda e, c=c: e.tensor_tensor(TB["KH"][:, c, :], pt(9), pt(0), ALU.mult),
                     reads=[Rpt[9], Rpt[0]], writes=[TB["RKH"]])
                yield
                shift(c, ti, pt(10), Rpt[10])
                s.op("dve", lambda e, c=c: e.tensor_tensor(TB["RH"][:, c, :], pt(10), pt(1), ALU.mult),
                     reads=[Rpt[10], Rpt[1]], writes=[TB["RRH"]])
                yield
                s.op("dve", lambda e: e.tensor_tensor(pt(6), pt(10), pt(9), ALU.mult), reads=[Rpt[10], Rpt[9]], writes=[Rpt[6]])
                s.op("dve", lambda e, c=c: e.tensor_scalar(pt(6), pt(6), cp("kvec2", c), None, ALU.mult),
                     reads=[Rpt[6], Rcolp], writes=[Rpt[6]])
                b = bank()
                s.op("pe", lambda e, b=b: e.matmul(ps[b][:, 0:2], pt(6), halfsel_f[:, 0:2], start=True, stop=True),
                     reads=[Rpt[6], Rconst], writes=[Rps[b]])
                s.op("dve", lambda e, b=b, c=c, ti=ti: e.tensor_tensor(BS[:, ti, c * 2:c * 2 + 2], BS[:, ti, c * 2:c * 2 + 2], ps[b][:, 0:2], ALU.add),
                     reads=[Rps[b], RBS], writes=[RBS])
                yield

        def rwkv_chunked(ntiles=NT):
            cut = int(os.environ.get("CUT", 99))
            for m in range(4):
                for q in range(4):
                    s.dma("pool", masks[:, m, q, :], cmat[4 + m], writes=[Rmask])
            s.op("dve", lambda e: e.memset(pt(11), 1.0), writes=[Rpt[11]])
            MSK = {0: (0, 1, 2), 1: (1, 0, 3)}
            seq = []
            for d in range(2):
                order = list(range(NT)) if d == 0 else list(range(NT - 1, -1, -1))
                seq += [(d, ti) for ti in order[:ntiles]]
            for _ in prep_chunk(seq[0][0], seq[0][1], 0):
                pass
            if True:
                def tile_body(n):
                    d, ti = seq[n]
                    par = n % 2
                    Ad = Ast[:, d * 256:(d + 1) * 256]
                    Ad3 = Ad.rearrange("p (c i) -> p c i", c=4)
                    Abd = Abf[:, d * 256:(d + 1) * 256]
                    Abd3 = Abd.rearrange("p (c i) -> p c i", c=4)
                    mTs, mS, mTi = MSK[d]
                    TB = TBL[par]
                    nxt = prep_chunk(seq[n + 1][0], seq[n + 1][1], (n + 1) % 2) if n + 1 < len(seq) else iter(())
                    u = ti // 2
                    isstart = (ti % 2 == 0) if d == 0 else (ti % 2 == 1)
                    if isstart:
                        if d == 0:
                            if u == 0:
                                s.dma("sp", Ad, initS[0], writes=[RAst])
                            else:
                                s.op("dve", lambda e, u=u: e.tensor_scalar(Ad, Ad, cx("keepL", u), None, ALU.mult),
                                     reads=[RAst, Rcolx], writes=[RAst])
                        else:
                            if u == 4:
                                s.op("dve", lambda e: e.memset(Ad, 0.0), writes=[RAst])
                            elif u == 3:
                                s.dma("sp", stage_f[:, 0, 0:256], initS[1], writes=[Rstage[0]])
                                s.op("dve", lambda e, u=u: e.scalar_tensor_tensor(Ad, Ad, cx("keepR", u), stage_f[:, 0, 0:256], ALU.mult, ALU.add),
                                     reads=[RAst, Rcolx, Rstage[0]], writes=[RAst])
                            else:
                                s.op("dve", lambda e, u=u: e.tensor_scalar(Ad, Ad, cx("keepR", u), None, ALU.mult),
                                     reads=[RAst, Rcolx], writes=[RAst])
                        s.op("act", lambda e: e.copy(Abd, Ad), reads=[RAst], writes=[RAbf])
                    if cut < 2:
                        return
                    fy = bank()
                    reserved.add(fy)
                    first_fy = [True]
                    def group_body(g):
                        W = WS[g]
                        heads = [(c_, g) for c_ in range(4)]

                        def fm(T3, c, h2):
                            return T3[64 * h2:64 * h2 + 64, c, :]

                        def q4(tl):
                            return tl.rearrange("p (q t) -> p q t", q=4)
                        specs = [("Lt0", "BH", "AH", mTs), ("L0", "AH", "BH", mS), ("Lakt", "KH", "AH", mTs),
                                 ("Mrbt", "BH", "RH", mTi), ("Mrkt", "KH", "RH", mTi)]
                        lim = int(os.environ.get("LIM", 99))
                        for (dst, la, rb, mk) in specs[:lim]:
                            b = bank()
                            for q, (c, h2) in enumerate(heads[:int(os.environ.get("LIMH", 4))]):
                                s.op("pe", lambda e, b=b, q=q, c=c, h2=h2, la=la, rb=rb: e.matmul(
                                    ps[b][:, q * 128:(q + 1) * 128], fm(TB[la], c, h2), fm(TB[rb], c, h2), start=True, stop=True),
                                    reads=[TB["R" + la], TB["R" + rb]], writes=[Rps[b]])
                            if os.environ.get("NOEV") == "1":
                                continue
                            if os.environ.get("NOEV") == "2":
                                s.op("dve", lambda e, b=b, dst=dst, mk=mk: e.tensor_copy(W[dst], ps[b][:]),
                                     reads=[Rps[b], Rmask], writes=[W["R" + dst]])
                                continue
                            s.op("dve", lambda e, b=b, dst=dst, mk=mk: e.tensor_tensor(
                                q4(W[dst]), ps[b][:].rearrange("p (q t) -> p q t", q=4), masks[:, mk, :, :], ALU.mult),
                                reads=[Rps[b], Rmask], writes=[W["R" + dst]])
                        yield
                        if cut < 3:
                            return
                        b = bank()
                        for q, (c, h2) in enumerate(heads):
                            idn = ident_b[64 * h2:64 * h2 + 64, 64 * h2:64 * h2 + 64]
                            for k_, nm in enumerate(["AH", "BH", "KH"]):
                                s.op("pe", lambda e, b=b, q=q, c=c, h2=h2, nm=nm, k_=k_, idn=idn: e.transpose(
                                    psb[b][:, k_ * 256 + q * 64:k_ * 256 + q * 64 + 64], fm(TB[nm], c, h2), idn),
                                    reads=[TB["R" + nm], Rconst], writes=[Rps[b]])
                        Z0q = q4(W["Z0"])
                        BKq = q4(W["BKtm"])
                        s.op("act", lambda e, b=b, Z0q=Z0q: e.copy(Z0q[:, :, 0:64], psb[b][:, 0:256].rearrange("p (q j) -> p q j", q=4)),
                             reads=[Rps[b]], writes=[W["RZ0"]])
                        s.op("act", lambda e, b=b, BKq=BKq: e.copy(BKq[:, :, 0:64], psb[b][:, 256:512].rearrange("p (q j) -> p q j", q=4)),
                             reads=[Rps[b]], writes=[W["RBKtm"]])
                        s.op("act", lambda e, b=b, BKq=BKq: e.copy(BKq[:, :, 64:128], psb[b][:, 512:768].rearrange("p (q j) -> p q j", q=4)),
                             reads=[Rps[b]], writes=[W["RBKtm"]])

                        def Vq(c, h2):
                            return V_tm[:, ti, (c * 2 + h2) * 64:(c * 2 + h2) * 64 + 64]
                        if cut < 4:
                            return
                        b = bank()
                        Lakq = q4(W["Lakt"])
                        for q, (c, h2) in enumerate(heads):
                            s.op("pe", lambda e, b=b, q=q, c=c, h2=h2: e.matmul(
                                ps[b][:, q * 64:(q + 1) * 64], Lakq[:, q, :], Vq(c, h2), start=True, stop=True),
                                reads=[W["RLakt"], RVtm], writes=[Rps[b]])
                        s.op("act", lambda e, b=b, Z0q=Z0q: e.copy(Z0q[:, :, 64:128], ps[b][:, 0:256].rearrange("p (q j) -> p q j", q=4)),
                             reads=[Rps[b]], writes=[W["RZ0"]])
                        yield
                        if cut < 5:
                            return
                        zc, lc = 0, 0
                        for k in range(7):
                            Zc, Zn = W["Z%d" % zc], W["Z%d" % (1 - zc)]
                            RZc, RZn = W["RZ%d" % zc], W["RZ%d" % (1 - zc)]
                            Ltc, Lc = W["Lt%d" % lc], W["L%d" % lc]
                            RLtc, RLc = W["RLt%d" % lc], W["RL%d" % lc]
                            b = bank()
                            for q in range(4):
                                s.op("pe", lambda e, b=b, q=q, Zc=Zc, Ltc=Ltc: e.matmul(
                                    ps[b][:, q * 128:(q + 1) * 128], q4(Ltc)[:, q, :], q4(Zc)[:, q, :], start=True, stop=True),
                                    reads=[RZc, RLtc], writes=[Rps[b]])
                            s.op("dve", lambda e, b=b, Zn=Zn, Zc=Zc: e.tensor_tensor(Zn, ps[b][:], Zc, ALU.add),
                                 reads=[Rps[b], RZc], writes=[RZn])
                            if k < 6:
                                Ltn, Ln = W["Lt%d" % (1 - lc)], W["L%d" % (1 - lc)]
                                RLtn, RLn = W["RLt%d" % (1 - lc)], W["RL%d" % (1 - lc)]
                                b2 = bank()
                                for q in range(4):
                                    s.op("pe", lambda e, b2=b2, q=q, Lc=Lc, Ltc=Ltc: e.matmul(
                                        ps[b2][:, q * 128:(q + 1) * 128], q4(Lc)[:, q, :], q4(Ltc)[:, q, :], start=True, stop=True),
                                        reads=[RLc, RLtc], writes=[Rps[b2]])
                                s.op("act", lambda e, b2=b2, Ltn=Ltn: e.copy(Ltn, ps[b2][:]), reads=[Rps[b2]], writes=[RLtn])
                                if k < 5:
                                    b3 = bank()
                                    for q in range(4):
                                        s.op("pe", lambda e, b3=b3, q=q, Lc=Lc, Ltc=Ltc: e.matmul(
                                            ps[b3][:, q * 128:(q + 1) * 128], q4(Ltc)[:, q, :], q4(Lc)[:, q, :], start=True, stop=True),
                                            reads=[RLc, RLtc], writes=[Rps[b3]])
                                    s.op("act", lambda e, b3=b3, Ln=Ln: e.copy(Ln, ps[b3][:]), reads=[Rps[b3]], writes=[RLn])
                                lc = 1 - lc
                            zc = 1 - zc
                            yield
                        s.op("act", lambda e, zc=zc: e.copy(W["Zfb"], W["Z%d" % zc]), reads=[W["RZ%d" % zc]], writes=[W["RZfb"]])
                        Zf = q4(W["Zfb"])
                        RZf = W["RZfb"]
                        Mrbq = q4(W["Mrbt"])
                        Mrkq = q4(W["Mrkt"])
                        if cut < 6:
                            return
                        bM = bank()
                        bN = bank()
                        bR = bank()
                        for q, (c, h2) in enumerate(heads):
                            cl_ = c
                            prt = slice(64 * h2, 64 * h2 + 64)
                            s.op("pe", lambda e, q=q, cl_=cl_, prt=prt: e.matmul(
                                ps[bM][prt, cl_ * 64:(cl_ + 1) * 64], Zf[:, q, 0:64], BKq[:, q, 0:64],
                                start=(q == 0), stop=False, skip_group_check=True),
                                reads=[RZf, W["RBKtm"]], writes=[Rps[bM]])
                            s.op("pe", lambda e, q=q, cl_=cl_, prt=prt: e.matmul(
                                ps[bM][prt, cl_ * 64:(cl_ + 1) * 64], ident_b[0:64, 0:64], ident_b[0:64, 0:64],
                                start=False, stop=True, skip_group_check=True),
                                reads=[Rconst], writes=[Rps[bM]])
                            s.op("pe", lambda e, q=q, cl_=cl_, prt=prt: e.matmul(
                                ps[bN][prt, cl_ * 64:(cl_ + 1) * 64], BKq[:, q, 0:64], Zf[:, q, 64:128],
                                start=(q == 0), stop=False, skip_group_check=True),
                                reads=[RZf, W["RBKtm"]], writes=[Rps[bN]])
                            s.op("pe", lambda e, q=q, cl_=cl_, prt=prt, c=c, h2=h2: e.matmul(
                                ps[bN][prt, cl_ * 64:(cl_ + 1) * 64], BKq[:, q, 64:128], Vq(c, h2),
                                start=False, stop=False, skip_group_check=True),
                                reads=[W["RBKtm"], RVtm], writes=[Rps[bN]])
                            s.op("pe", lambda e, q=q, cl_=cl_, prt=prt: e.matmul(
                                ps[bR][prt, cl_ * 128:(cl_ + 1) * 128], Zf[:, q, 0:64], Mrbq[:, q, :],
                                start=(q == 0), stop=False, skip_group_check=True),
                                reads=[RZf, W["RMrbt"]], writes=[Rps[bR]])
                            s.op("pe", lambda e, q=q, cl_=cl_, prt=prt, c=c, h2=h2: e.matmul(
                                ps[bR][prt, cl_ * 128:(cl_ + 1) * 128], ident_b[prt, prt], fm(TB["RH"], c, h2),
                                start=False, stop=True, skip_group_check=True),
                                reads=[Rconst, TB["RRH"]], writes=[Rps[bR]])
                            st_ = first_fy[0]
                            first_fy[0] = False
                            hh = c * 2 + h2
                            s.op("pe", lambda e, q=q, hh=hh, st_=st_: e.matmul(
                                ps[fy][:, hh * 64:(hh + 1) * 64], Mrbq[:, q, :], Zf[:, q, 64:128],
                                start=st_, stop=False, skip_group_check=True),
                                reads=[RZf, W["RMrbt"]], writes=[Rps[fy]])
                            s.op("pe", lambda e, q=q, hh=hh, c=c, h2=h2: e.matmul(
                                ps[fy][:, hh * 64:(hh + 1) * 64], Mrkq[:, q, :], Vq(c, h2),
                                start=False, stop=False, skip_group_check=True),
                                reads=[W["RMrkt"], RVtm], writes=[Rps[fy]])
                        MTv = W["MT"].rearrange("p (c j) -> p c j", c=4)
                        RTv = W["RT"].rearrange("p (c t) -> p c t", c=4)
                        pg = slice(64 * g, 64 * g + 64)
                        s.op("act", lambda e, MTv=MTv, pg=pg: e.copy(MTv[pg], ps[bM][pg, 0:256].rearrange("p (c j) -> p c j", c=4)),
                             reads=[Rps[bM]], writes=[W["RMT"]])
                        s.op("act", lambda e, RTv=RTv, pg=pg: e.copy(RTv[pg], ps[bR][pg, 0:512].rearrange("p (c t) -> p c t", c=4)),
                             reads=[Rps[bR]], writes=[W["RRT"]])
                        for q, (c, h2) in enumerate(heads):
                            cl_ = c
                            prt = slice(64 * h2, 64 * h2 + 64)
                            hh = c * 2 + h2
                            s.op("pe", lambda e, cl_=cl_, prt=prt, hh=hh, c=c, RTv=RTv: e.matmul(
                                ps[fy][:, hh * 64:(hh + 1) * 64], RTv[prt, cl_, :], Abd3[prt, c, :],
                                start=False, stop=True, skip_group_check=True),
                                reads=[W["RRT"], RAbf], writes=[Rps[fy]])
                            s.op("pe", lambda e, cl_=cl_, prt=prt, c=c, MTv=MTv: e.matmul(
                                ps[bN][prt, cl_ * 64:(cl_ + 1) * 64], MTv[prt, cl_, :], Abd3[prt, c, :],
                                start=False, stop=True, skip_group_check=True),
                                reads=[W["RMT"], RAbf], writes=[Rps[bN]])
                        s.op("dve", lambda e, pg=pg: e.tensor_tensor(
                            Ad3[pg, :, :], ps[bN][pg, 0:256].rearrange("p (c i) -> p c i", c=4),
                            pcs[pg, par, :].unsqueeze(2).to_broadcast([64, 4, 64]), ALU.mult),
                            reads=[Rps[bN], Rpcs[par], RAbf], writes=[RAst])
                    gens = [group_body(0), group_body(1), nxt]
                    while gens:
                        for gn in list(gens):
                            try:
                                next(gn)
                            except StopIteration:
                                gens.remove(gn)
                    if cut < 6:
                        reserved.discard(fy)
                        return
                    s.op("act", lambda e: e.copy(Abd, Ad), reads=[RAst], writes=[RAbf])
                    isend = (ti % 2 == 1) if d == 0 else (ti % 2 == 0)
                    if isend:
                        s.dma("sp", s_out[u, d], Ad, reads=[RAst])
                    reserved.discard(fy)
                    if d == 0:
                        s.op("act", lambda e, ti=ti: e.copy(YSb[:, ti, :], ps[fy][:]), reads=[Rps[fy]], writes=[RYS])
                    else:
                        finalize(ti, fy)
                for n_ in range(len(seq)):
                    tile_body(n_)

        if stage >= 1:
            if "a" in sub:
                make_gg(0)
            if "b" in sub:
                make_hT(0, 0)
            if "c" in sub:
                even_inproj()
        if stage >= 3:
            attention()
            barrier()
            if stage >= 5:
                rwkv_setup()
                rwkv_prepass()
                rwkv_chunked(NT if stage >= 6 else 2)
            else:
                s.op("pool", lambda e: e.memset(yT[:, 4:8, :], 0.0), writes=[RyT])
            barrier()
            out_proj(w_out_even, yT, RyT)
        if stage >= 2:
            barrier()
            ffn(0)
            barrier()
        if stage >= 4:
            make_gg(1)
            make_hT(1, 0)
            fnet()
            out_proj(w_out_odd, yT, RyT)
            barrier()
            ffn(1)
        for i in range(NT):
            s.dma("sp", y_out[i * 128:(i + 1) * 128, :], x_sb[:, i, :], reads=[Rx[i]])
        s.emit()
    return nc


def _core_units(c):
    if c < 6:
        return [("p", 5 * c + u) for u in range(5)]
    b = c - 6
    return [("s", b, u) for u in range(4)] + [("p", 30 + b)]


def _host_prep(inp):
    f32 = np.float32
    COFF, NCOL = colp_layout()
    ROFF, NROW = rowp_layout()
    g = {k: np.asarray(v) for k, v in inp.items()}
    colp = np.zeros((128, NCOL), f32)

    def put(name, vec):
        c = cols_of(vec)
        colp[:, COFF[name]:COFF[name] + c.shape[1]] = c
    for l in range(2):
        put("bada%d" % l, g["b_ada"][l])
        for k in range(4):
            put("gain%d_%d" % (l, k), g["norm_gains"][l, k])
        for k in range(3):
            put("conv%d_%d" % (l, k), g["ffn_conv"][l, k])
        put("convb%d" % l, g["ffn_conv_b"][l])
    put("mu0", g["rwkv_shift_mu"][0, 0])
    put("mu1", g["rwkv_shift_mu"][0, 1])
    for d in range(2):
        put("w0_%d" % d, g["rwkv_w0"][0, d])
        put("a0_%d" % d, g["rwkv_a0"][0, d])
    for k in range(3):
        put("kvec%d" % k, g["rwkv_kvec"][0, k])

    cmat = np.zeros((8, 128, 128), f32)
    cmat[0] = np.eye(128, dtype=f32)
    for i in range(64):
        cmat[1][2 * i, 2 * i + 1] = 1.0
        cmat[1][2 * i + 1, 2 * i] = -1.0
    cmat[2][:64, :64] = 1.0
    cmat[2][64:, 64:] = 1.0
    cmat[3][:64, 0] = 1.0
    cmat[3][64:, 1] = 1.0
    rr_, cc_ = np.meshgrid(np.arange(128), np.arange(128), indexing="ij")
    cmat[4] = (rr_ < cc_)
    cmat[5] = (rr_ > cc_)
    cmat[6] = (rr_ <= cc_)
    cmat[7] = (rr_ >= cc_)
    cc = np.arange(128)
    chang = 2.0 * np.pi * ((cc[:, None] * cc[None, :]) % 128) / 128.0
    chCS = np.concatenate([np.cos(chang), np.sin(chang)], axis=1).astype(f32)

    inv = (10000.0 ** (-np.arange(16, dtype=np.float32) / 16)).astype(f32)

    shared = dict(
        colp=colp, w_ada=g["w_ada"], w_in_even=g["w_in_even"][0], w_out_even=g["w_out_even"][0],
        w_out_odd=g["w_out_odd"][0], w_ffn_in=g["w_ffn_in"], w_ffn_out=g["w_ffn_out"],
        w2t=np.ascontiguousarray(g["rwkv_w2"][0].reshape(128, 512)),
        a2t=np.ascontiguousarray(g["rwkv_a2"][0].reshape(128, 512)),
        g2=np.ascontiguousarray(g["rwkv_g2"][0]), chCS=chCS, cmat=cmat,
        lnx=np.ascontiguousarray(g["rwkv_lnx"][0]))
    maps = []
    for c in range(NCORES):
        units = _core_units(c)
        xs = []
        for un in units:
            if un[0] == "p":
                xs.append(g["x_prompt"][un[1]])
            else:
                xs.append(g["x_sample"][un[1], un[2] * 256:(un[2] + 1) * 256])
        x_in = np.ascontiguousarray(np.concatenate(xs, axis=0), dtype=f32)
        is_s = c >= 6
        condA = g["c"][c - 6] if is_s else g["c_ctx"]
        condB = g["c_ctx"]
        cond = np.stack([condA, condB], axis=0).astype(f32)
        condT = np.ascontiguousarray(cond.reshape(2, 8, 128).transpose(2, 1, 0).reshape(128, 16))
        rowp = np.zeros((1, NROW), f32)
        rowp[0, ROFF["lam"]:ROFF["lam"] + 256] = g["diff_lambda"][0].reshape(-1)
        rowp[0, ROFF["subln"]:ROFF["subln"] + 128] = g["diff_subln"][0]
        ab = np.full((6, 5), -30000.0, f32)
        if is_s:
            ab[0:5, 0:4] = 0.0
            ab[5, 4] = 0.0
            bndL = [1, 0, 0, 0, 1]
            bndR = [0, 0, 0, 1, 1]
        else:
            for u in range(5):
                ab[u + 1, u] = 0.0
            bndL = [1] * 5
            bndR = [1] * 5
        rowp[0, ROFF["abias"]:ROFF["abias"] + 30] = ab.reshape(-1)
        rowp[0, ROFF["bndL"]:ROFF["bndL"] + 5] = bndL
        rowp[0, ROFF["bndR"]:ROFF["bndR"] + 5] = bndR
        ropeC = np.ones((128, T), f32)
        ropeS = np.zeros((128, T), f32)
        if is_s:
            t = np.arange(1024)
            row = (t // 64).astype(f32)
            col = (t % 64).astype(f32)
            ang = np.concatenate([row[:, None] * inv[None, :], col[:, None] * inv[None, :]], axis=1).astype(f32)
            pidx = (np.arange(128) % 64) // 2
            ropeC[:, :1024] = np.cos(ang)[:, pidx].T
            ropeS[:, :1024] = np.sin(ang)[:, pidx].T
        cacheKT = np.zeros((512, 256), f32)
        cacheV = np.zeros((256, 512), f32)
        initS = np.zeros((2, 128, 256), f32)
        if is_s:
            b = c - 6
            cacheKT[:] = g["cache_k"][b, 0].reshape(256, 512).T
            cacheV[:] = g["cache_v"][b, 0].reshape(256, 512)
            st = g["state_wkv"][b, 0]
            initS[:] = st.reshape(2, 4, 2, 64, 64).transpose(0, 2, 4, 1, 3).reshape(2, 128, 256)
        dC = np.zeros((T, T), np.float64)
        dS = np.zeros((T, T), np.float64)
        blocks = [(0, 1024), (1024, 256)] if is_s else [(256 * u, 256) for u in range(5)]
        for (a0, L) in blocks:
            ll = np.arange(L)
            ang = 2.0 * np.pi * ((ll[:, None] * ll[None, :]) % L) / L
            sc = 1.0 / math.sqrt(L * 128.0)
            dC[a0:a0 + L, a0:a0 + L] = np.cos(ang) * sc
            dS[a0:a0 + L, a0:a0 + L] = -np.sin(ang) * sc
        m = dict(shared)
        m.update(x_in=x_in, condT=condT, rowp=rowp, ropeC=ropeC, ropeS=ropeS, cacheKT=cacheKT, cacheV=cacheV,
                 initS=initS, dftC=dC.astype(f32), dftSn=dS.astype(f32))
        maps.append(m)
    return maps


_NC_CACHE = {}


def kernel(**inputs):
    maps = _host_prep(inputs)
    if "nc" not in _NC_CACHE:
        _NC_CACHE["nc"] = build()
    nc = _NC_CACHE["nc"]
    import os
    ncr = int(os.environ.get("NCR", NCORES))
    res = run_bass_kernel_spmd(nc, maps[:ncr], core_ids=list(range(ncr)))
    outs = list(res.results) + [res.results[0]] * (NCORES - ncr)
    y_prompt = np.zeros((32, 256, D), np.float32)
    y_sample = np.zeros((2, 1024, D), np.float32)
    nk = np.zeros((32, 1, 256, 4, 128), np.float32)
    nv = np.zeros((32, 1, 256, 4, 128), np.float32)
    ns = np.zeros((32, 1, 2, 8, 64, 64), np.float32)
    for c in range(NCORES):
        r = outs[c]
        for u, un in enumerate(_core_units(c)):
            sl = slice(u * 256, (u + 1) * 256)
            if un[0] == "p":
                bi = un[1]
                y_prompt[bi] = r["y_out"][sl]
                nk[bi, 0] = r["k_out"][sl].reshape(256, 4, 128)
                nv[bi, 0] = r["v_out"][sl].reshape(256, 4, 128)
                stt = r["s_out"][u]
                ns[bi, 0] = stt.reshape(2, 2, 64, 4, 64).transpose(0, 3, 1, 4, 2).reshape(2, 8, 64, 64)
            else:
                y_sample[un[1], un[2] * 256:(un[2] + 1) * 256] = r["y_out"][sl]
    return (y_prompt, y_sample, nk, nv, ns)
```

```python
import contextlib
import math
import numpy as np
import concourse.bass as bass
import concourse.mybir as mybir
from concourse.bass_utils import run_bass_kernel_spmd

F32 = mybir.dt.float32
BF16 = mybir.dt.bfloat16
AF = mybir.ActivationFunctionType
ALU = mybir.AluOpType
AX = mybir.AxisListType

T = 1280
NT = 10
U = 5
D = 1024
KC = 8
DFF = 2816
FC = 22
NCORES = 8
ARENA_W = 27200
EXPM05 = math.exp(-0.5)


class Res:
    __slots__ = ("name", "writer", "readers", "excl")

    def __init__(self, name, excl=False):
        self.name = name
        self.writer = None
        self.readers = []
        self.excl = excl


class Sched:
    ENGS = ("pe", "act", "dve", "pool", "sp")
    NDMA = 6

    def __init__(self, nc):
        self.nc = nc
        self.prog = {e: [] for e in self.ENGS}
        self.signal = {e: set() for e in self.ENGS}
        self.ndma = {e: 0 for e in self.ENGS}

    def _collect(self, reads, writes, eng=None):
        deps = []
        for r in reads:
            if r.writer is not None:
                deps.append(r.writer)
            if r.excl:
                deps.extend(t for t in r.readers if t[1] != eng)
        for w in writes:
            if w.writer is not None:
                deps.append(w.writer)
            deps.extend(w.readers)
        return deps

    def _commit(self, tok, reads, writes):
        for r in reads:
            r.readers.append(tok)
        for w in writes:
            w.writer = tok
            w.readers = []

    def op(self, eng, fn, reads=(), writes=()):
        deps = self._collect(reads, writes, eng)
        idx = len(self.prog[eng])
        if eng == "pe":
            deps = [d for d in deps if not (d[0] == "c" and d[1] == "pe")]
        for d in deps:
            if d[0] == "c":
                self.signal[d[1]].add(d[2])
        self.prog[eng].append(dict(fn=fn, deps=deps, kind="c"))
        tok = ("c", eng, idx)
        self._commit(tok, reads, writes)
        return tok

    def dma(self, eng, out, in_, reads=(), writes=()):
        deps = self._collect(reads, writes, eng)
        n = self.ndma[eng]
        self.ndma[eng] += 1
        if n >= self.NDMA:
            deps.append(("d", eng, n - self.NDMA))
        for d in deps:
            if d[0] == "c":
                self.signal[d[1]].add(d[2])
        self.prog[eng].append(dict(out=out, in_=in_, deps=deps, kind="d", n=n))
        tok = ("d", eng, n)
        self._commit(tok, reads, writes)
        return tok

    def emit(self):
        nc = self.nc
        with contextlib.ExitStack() as st:
            csem = {e: st.enter_context(nc.semaphore("c_" + e)) for e in self.ENGS}
            dsem = {e: [st.enter_context(nc.semaphore("d_%s_%d" % (e, i))) for i in range(self.NDMA)]
                    for e in self.ENGS if self.ndma[e] > 0}
            sigval = {}
            for e in self.ENGS:
                cnt = 0
                m = {}
                for i in range(len(self.prog[e])):
                    if i in self.signal[e]:
                        cnt += 1
                        m[i] = cnt
                sigval[e] = m

            def resolve(tok):
                if tok[0] == "c":
                    return csem[tok[1]], sigval[tok[1]][tok[2]], ("c", tok[1])
                e, n = tok[1], tok[2]
                return dsem[e][n % self.NDMA], 16 * (n // self.NDMA + 1), ("d", e, n % self.NDMA)

            def run_engine(e, h):
                waited = {}
                for i, ins in enumerate(self.prog[e]):
                    need = {}
                    for d in ins["deps"]:
                        sem, val, key = resolve(d)
                        if waited.get(key, 0) >= val:
                            continue
                        if key not in need or need[key][1] < val:
                            need[key] = (sem, val)
                    for key, (sem, val) in need.items():
                        h.wait_ge(sem, val)
                        waited[key] = val
                    if ins["kind"] == "c":
                        bi = ins["fn"](h)
                        if i in self.signal[e]:
                            bi.then_inc(csem[e], 1)
                    else:
                        n = ins["n"]
                        h.dma_start(out=ins["out"], in_=ins["in_"]).then_inc(dsem[e][n % self.NDMA], 16)
                if self.ndma[e] > 0:
                    n = self.ndma[e]
                    for slot in range(self.NDMA):
                        cnt = (n - slot + self.NDMA - 1) // self.NDMA if n > slot else 0
                        if cnt > 0:
                            h.wait_ge(dsem[e][slot], 16 * cnt)

            with nc.Block() as block:
                @block.tensor
                def _(eng):
                    run_engine("pe", eng)

                @block.scalar
                def _(eng):
                    run_engine("act", eng)

                @block.vector
                def _(eng):
                    run_engine("dve", eng)

                @block.gpsimd
                def _(eng):
                    run_engine("pool", eng)

                @block.sync
                def _(eng):
                    run_engine("sp", eng)


def colp_layout():
    off = {}
    n = 0

    def add(name, cols):
        nonlocal n
        off[name] = n
        n += cols
    for l in range(2):
        add("bada%d" % l, 48)
        for g in range(4):
            add("gain%d_%d" % (l, g), 8)
        for k in range(3):
            add("conv%d_%d" % (l, k), FC)
        add("convb%d" % l, FC)
    add("mu0", 15)
    add("mu1", 15)
    for d in range(2):
        add("w0_%d" % d, 4)
        add("a0_%d" % d, 4)
    for k in range(3):
        add("kvec%d" % k, 4)
    return off, n


def rowp_layout():
    off = {}
    n = 0

    def add(name, cols):
        nonlocal n
        off[name] = n
        n += cols
    add("lam", 256)
    add("subln", 128)
    add("abias", 30)
    add("bndL", 5)
    add("bndR", 5)
    return off, n


def cols_of(vec):
    v = np.asarray(vec, np.float32).reshape(-1, 128)
    return np.ascontiguousarray(v.T)


def build(stage=99):
    nc = bass.Bass("TRN2", target_bir_lowering=False)
    COFF, NCOL = colp_layout()
    ROFF, NROW = rowp_layout()

    def din(name, shape):
        return nc.dram_tensor(name, list(shape), F32, kind="ExternalInput").ap()

    def dout(name, shape):
        return nc.dram_tensor(name, list(shape), F32, kind="ExternalOutput").ap()

    x_in = din("x_in", [T, D])
    condT = din("condT", [128, 16])
    colp = din("colp", [128, NCOL])
    rowp = din("rowp", [1, NROW])
    w_ada = din("w_ada", [2, D, 6 * D])
    w_in_even = din("w_in_even", [D, 3456])
    w_out_even = din("w_out_even", [D, D])
    w_out_odd = din("w_out_odd", [D, D])
    w_ffn_in = din("w_ffn_in", [2, D, 2 * DFF])
    w_ffn_out = din("w_ffn_out", [2, DFF, D])
    w2t_d = din("w2t", [128, 512])
    a2t_d = din("a2t", [128, 512])
    g2_d = din("g2", [128, 512])
    cacheKT = din("cacheKT", [512, 256])
    cacheV = din("cacheV", [256, 512])
    ropeC = din("ropeC", [128, T])
    ropeS = din("ropeS", [128, T])
    initS = din("initS", [2, 128, 256])
    dftC = din("dftC", [T, T])
    dftSn = din("dftSn", [T, T])
    chCS = din("chCS", [128, 256])
    cmat = din("cmat", [8, 128, 128])
    lnx_d = din("lnx", [2, 512])

    y_out = dout("y_out", [T, D])
    k_out = dout("k_out", [T, 512])
    v_out = dout("v_out", [T, 512])
    s_out = dout("s_out", [U, 2, 128, 256])

    import os
    sub = os.environ.get("SUB", "abcmqv")
    with contextlib.ExitStack() as st:
        s = Sched(nc)

        def sb(name, shape, dt=F32):
            return st.enter_context(nc.sbuf_tensor(name, list(shape), dt))

        x_sb = sb("x_sb", [128, NT, D])
        Rx = [Res("x%d" % i) for i in range(NT)]
        WB = 4096
        wbuf = [sb("wb%d" % i, [128, WB], BF16) for i in range(3)]
        Rw = [Res("wb%d" % i) for i in range(3)]
        wctr = [0]
        colp_sb = sb("colp_sb", [128, NCOL])
        Rcolp = Res("colp")
        rowp_sb = sb("rowp_sb", [128, NROW])
        Rrowp = Res("rowp")
        ident_f = sb("ident_f", [128, 128])
        ident_b = sb("ident_b", [128, 128], BF16)
        rotT_b = sb("rotT_b", [128, 128], BF16)
        bones_f = sb("bones_f", [128, 128])
        halfsel_f = sb("halfsel_f", [128, 128])
        bones_b = sb("bones_b", [128, 128], BF16)
        Rconst = Res("const")
        gg_sb = sb("gg_sb", [128, 2, 2, D], BF16)
        Rgg = Res("gg")
        scond = sb("scond", [128, 16], BF16)
        condf = sb("condf", [128, 16])
        Rcond = Res("cond")
        modcol = sb("modcol", [128, 2, 96])
        Rmod = Res("modcol")
        scsh = sb("scsh", [128, 2, 2, 2, 16])
        Rscsh = Res("scsh")
        stat = sb("stat", [128, 64])
        Rstat = Res("stat")
        junk_b = sb("junk_b", [128, D], BF16)
        Rjunk = Res("junk")
        xn_b = sb("xn_b", [128, D], BF16)
        Rxn = Res("xn")
        stage_f = sb("stage_f", [128, 2, 512])
        Rstage = [Res("stage0"), Res("stage1")]
        stctr = [0]
        arena = sb("arena", [128, ARENA_W])
        Rar = {}

        def ares(name):
            if name not in Rar:
                Rar[name] = Res("ar_" + name)
            return Rar[name]

        def carve(off_w, nelem, dt):
            if dt == F32:
                return arena[:, off_w:off_w + nelem]
            return arena[:, off_w:off_w + (nelem + 1) // 2].bitcast(BF16)[:, 0:nelem]

        ps = [st.enter_context(nc.psum_tensor("ps%d" % b, [128, 512], F32)) for b in range(8)]
        psb = [p.bitcast(BF16) for p in ps]
        Rps = [Res("ps%d" % b, excl=True) for b in range(8)]
        bctr = [0]

        reserved = set()

        def bank():
            while True:
                b = bctr[0] % 8
                bctr[0] += 1
                if b not in reserved:
                    return b

        def barrier():
            allr = list(Rar.values()) + list(Rw)
            s.op("dve", lambda e: e.memset(stat[:, 63:64], 0.0), writes=allr + [Rstat])

        def wload(src_ap, kc, ncols):
            i = wctr[0] % 3
            wctr[0] += 1
            view = wbuf[i][:, 0:kc * ncols].rearrange("p (c n) -> p c n", c=kc)
            s.dma("pool", view, src_ap.rearrange("(c p) n -> p c n", p=128), writes=[Rw[i]])
            return view, Rw[i]

        s.dma("sp", colp_sb[:], colp, writes=[Rcolp])
        s.dma("sp", rowp_sb[:], rowp[0, :].partition_broadcast(128), writes=[Rrowp])
        s.dma("sp", ident_f[:], cmat[0], writes=[Rconst])
        s.dma("sp", bones_f[:], cmat[2], writes=[Rconst])
        s.dma("sp", halfsel_f[:], cmat[3], writes=[Rconst])
        s.dma("pool", ident_b[:], cmat[0], writes=[Rconst])
        s.dma("pool", rotT_b[:], cmat[1], writes=[Rconst])
        s.dma("pool", bones_b[:], cmat[2], writes=[Rconst])
        s.dma("sp", condf[:], condT, writes=[Rcond])
        for i in range(NT):
            s.dma("sp", x_sb[:, i, :], x_in[i * 128:(i + 1) * 128, :], writes=[Rx[i]])
        s.op("act", lambda e: e.activation(scond[:], condf[:], AF.Silu), reads=[Rcond], writes=[Rcond])

        def cp(name, c=0, n=1):
            o = COFF[name] + c
            return colp_sb[:, o:o + n]

        for l in range(2):
            b = bank()
            for v in range(6):
                for half in range(2):
                    wv, rw = wload(w_ada[l][:, v * 1024 + half * 512: v * 1024 + half * 512 + 512], KC, 512)
                    for j in range(4):
                        col = (v * 8 + half * 4 + j) * 2
                        for kc in range(KC):
                            s.op("pe", lambda e, wv=wv, j=j, kc=kc, col=col, b=b: e.matmul(
                                ps[b][:, col:col + 2], wv[:, kc, j * 128:(j + 1) * 128],
                                scond[:, kc * 2:kc * 2 + 2], start=(kc == 0), stop=(kc == KC - 1)),
                                reads=[rw, Rcond], writes=[Rps[b]])
            s.op("dve", lambda e, l=l, b=b: e.tensor_tensor(
                modcol[:, l, :].rearrange("p (c a) -> p c a", a=2),
                ps[b][:, 0:96].rearrange("p (c a) -> p c a", a=2),
                cp("bada%d" % l, 0, 48).unsqueeze(2).to_broadcast([128, 48, 2]), ALU.add),
                reads=[Rps[b], Rcolp], writes=[Rmod])

            def mv(v, l=l):
                return modcol[:, l, v * 16:(v + 1) * 16].rearrange("p (c a) -> p c a", a=2)

            def gain(g, l=l):
                return cp("gain%d_%d" % (l, g), 0, 8).unsqueeze(2).to_broadcast([128, 8, 2])
            for wi, (vs, vsh, g) in enumerate([(1, 0, 0), (4, 3, 2)]):
                sc = scsh[:, l, wi, 0, :].rearrange("p (c a) -> p c a", a=2)
                sh = scsh[:, l, wi, 1, :].rearrange("p (c a) -> p c a", a=2)
                s.op("dve", lambda e, sc=sc, vs=vs, mv=mv: e.tensor_scalar(sc, mv(vs), 1.0, None, ALU.add),
                     reads=[Rmod], writes=[Rscsh])
                s.op("dve", lambda e, sc=sc, g=g, gain=gain: e.tensor_tensor(sc, sc, gain(g), ALU.mult),
                     reads=[Rscsh, Rcolp], writes=[Rscsh])
                s.op("dve", lambda e, sh=sh, vsh=vsh, mv=mv: e.tensor_copy(sh, mv(vsh)),
                     reads=[Rmod], writes=[Rscsh])

        ggcol = sb("ggcol", [128, 2, 16])
        Rggcol = Res("ggcol")

        def make_gg(l):
            for wi, (vg, g) in enumerate([(2, 1), (5, 3)]):
                gc = ggcol[:, wi, :].rearrange("p (c a) -> p c a", a=2)
                s.op("dve", lambda e, gc=gc, vg=vg, g=g, l=l: e.tensor_tensor(
                    gc, modcol[:, l, vg * 16:(vg + 1) * 16].rearrange("p (c a) -> p c a", a=2),
                    cp("gain%d_%d" % (l, g), 0, 8).unsqueeze(2).to_broadcast([128, 8, 2]), ALU.mult),
                    reads=[Rmod, Rcolp], writes=[Rggcol])
                for a in range(2):
                    for hh in range(2):
                        b = bank()
                        for j in range(4):
                            c = hh * 4 + j
                            s.op("pe", lambda e, b=b, wi=wi, c=c, a=a, j=j: e.matmul(
                                ps[b][:, j * 128:(j + 1) * 128],
                                ggcol[:, wi, c * 2 + a:c * 2 + a + 1].to_broadcast([128, 128]),
                                ident_f[:], start=True, stop=True),
                                reads=[Rggcol, Rconst], writes=[Rps[b]])
                        s.op("act", lambda e, b=b, wi=wi, a=a, hh=hh: e.copy(
                            gg_sb[:, wi, a, hh * 512:(hh + 1) * 512], ps[b][:]),
                            reads=[Rps[b]], writes=[Rgg])

        hT = carve(0, 8 * T, BF16).rearrange("p (c t) -> p c t", c=8)
        RhT = ares("hT")

        def ab_of_tile(i):
            return 0 if i < 8 else 1

        def make_hT(l, wi):
            for i in range(NT):
                a = ab_of_tile(i)
                s.op("act", lambda e, i=i: e.activation(junk_b[:], x_sb[:, i, :], AF.Square, scale=1.0 / 32.0,
                                                        accum_out=stat[:, 0:1]),
                     reads=[Rx[i]], writes=[Rjunk, Rstat])
                s.op("dve", lambda e: e.tensor_scalar(stat[:, 1:2], stat[:, 0:1], 1e-6, None, ALU.add),
                     reads=[Rstat], writes=[Rstat])
                s.op("act", lambda e: e.activation(stat[:, 1:2], stat[:, 1:2], AF.Sqrt), reads=[Rstat], writes=[Rstat])
                s.op("dve", lambda e: e.reciprocal(stat[:, 2:3], stat[:, 1:2]), reads=[Rstat], writes=[Rstat])
                s.op("dve", lambda e, i=i: e.tensor_scalar(xn_b[:], x_sb[:, i, :], stat[:, 2:3], None, ALU.mult),
                     reads=[Rx[i], Rstat], writes=[Rxn])
                b = bank()
                for c in range(8):
                    s.op("pe", lambda e, b=b, c=c: e.transpose(psb[b][:, c * 128:(c + 1) * 128],
                                                               xn_b[:, c * 128:(c + 1) * 128], ident_b[:]),
                         reads=[Rxn, Rconst], writes=[Rps[b]])
                for c in range(8):
                    s.op("act", lambda e, b=b, c=c, i=i, a=a: e.activation(
                        hT[:, c, i * 128:(i + 1) * 128], psb[b][:, c * 128:(c + 1) * 128], AF.Identity,
                        bias=scsh[:, l, wi, 1, c * 2 + a:c * 2 + a + 1],
                        scale=scsh[:, l, wi, 0, c * 2 + a:c * 2 + a + 1]),
                        reads=[Rps[b], Rscsh], writes=[RhT])

        TOKCH = [(0, 512), (512, 512), (1024, 256)]

        def linear_fm(wv, rw, ncol0, actT, Ract, kc_n, evac):
            for (t0, tn) in TOKCH:
                b = bank()
                for kc in range(kc_n):
                    s.op("pe", lambda e, b=b, kc=kc, t0=t0, tn=tn: e.matmul(
                        ps[b][:, 0:tn], wv[:, kc, ncol0:ncol0 + 128], actT[:, kc, t0:t0 + tn],
                        start=(kc == 0), stop=(kc == kc_n - 1)),
                        reads=[rw, Ract], writes=[Rps[b]])
                evac(b, t0, tn)

        def linear_tm(wv, rw, ncols, actT, Ract, kc_n, i, b, col0=0):
            for kc in range(kc_n):
                s.op("pe", lambda e, kc=kc: e.matmul(
                    ps[b][:, col0:col0 + ncols], actT[:, kc, i * 128:(i + 1) * 128], wv[:, kc, 0:ncols],
                    start=(kc == 0), stop=(kc == kc_n - 1)),
                    reads=[rw, Ract], writes=[Rps[b]])

        def residual_update(i, banks, wi, fsrc=None, Rf=None):
            a = ab_of_tile(i)
            if fsrc is None:
                for hh, b in enumerate(banks):
                    s.op("act", lambda e, b=b, hh=hh: e.activation(junk_b[:, 0:512], ps[b][:], AF.Square,
                                                                   scale=1.0 / 32.0, accum_out=stat[:, 8 + hh:9 + hh]),
                         reads=[Rps[b]], writes=[Rjunk, Rstat])
                s.op("dve", lambda e: e.tensor_tensor(stat[:, 10:11], stat[:, 8:9], stat[:, 9:10], ALU.add),
                     reads=[Rstat], writes=[Rstat])
            else:
                s.op("act", lambda e: e.activation(junk_b[:], fsrc, AF.Square, scale=1.0 / 32.0,
                                                   accum_out=stat[:, 10:11]),
                     reads=[Rf], writes=[Rjunk, Rstat])
            s.op("dve", lambda e: e.tensor_scalar(stat[:, 11:12], stat[:, 10:11], 1e-6, None, ALU.add),
                 reads=[Rstat], writes=[Rstat])
            s.op("act", lambda e: e.activation(stat[:, 11:12], stat[:, 11:12], AF.Sqrt), reads=[Rstat], writes=[Rstat])
            s.op("dve", lambda e: e.reciprocal(stat[:, 12:13], stat[:, 11:12]), reads=[Rstat], writes=[Rstat])
            for hh in range(2):
                src = ps[banks[hh]][:] if fsrc is None else fsrc[:, hh * 512:(hh + 1) * 512]
                rr = [Rps[banks[hh]]] if fsrc is None else [Rf]
                si = stctr[0] % 2
                stctr[0] += 1
                s.op("dve", lambda e, src=src, hh=hh, si=si: e.scalar_tensor_tensor(
                    stage_f[:, si, :], src, stat[:, 12:13], gg_sb[:, wi, a, hh * 512:(hh + 1) * 512],
                    ALU.mult, ALU.mult),
                    reads=rr + [Rstat, Rgg], writes=[Rstage[si]])
                s.op("dve", lambda e, hh=hh, si=si, i=i: e.tensor_tensor(
                    x_sb[:, i, hh * 512:(hh + 1) * 512], x_sb[:, i, hh * 512:(hh + 1) * 512], stage_f[:, si, :], ALU.add),
                    reads=[Rstage[si], Rx[i]], writes=[Rx[i]])

        actT = carve(5120, FC * T, BF16).rearrange("p (c t) -> p c t", c=FC)
        RactT = ares("actT")
        graw = carve(19200, T + 2, F32)
        Rgraw = ares("graw")
        cbuf = carve(19200 + 1284, T, F32)
        Rcbuf = ares("cbuf")
        sbuf_s = carve(19200 + 1284 + 1280, T, F32)
        Rsbuf = ares("sbuf_s")
        fstageA = carve(19200, 5 * D, F32).rearrange("p (i n) -> p i n", i=5)
        fstageB = carve(0, 5 * D, F32).rearrange("p (i n) -> p i n", i=5)

        def fst(i):
            return fstageA[:, i, :] if i < 5 else fstageB[:, i - 5, :]

        def Rfst_of(i):
            return ares("fstage") if i < 5 else RhT
        bcorr = sb("bcorr", [128, 16])
        Rbcorr = Res("bcorr")

        def ffn(l):
            make_hT(l, 1)
            s.op("dve", lambda e: e.memset(graw[:, 0:1], 0.0), writes=[Rgraw])
            s.op("dve", lambda e: e.memset(graw[:, T + 1:T + 2], 0.0), writes=[Rgraw])
            for blk in range(0, FC, 2):
                wu, ru = wload(w_ffn_in[l][:, blk * 128: blk * 128 + 256], KC, 256)
                wg, rg = wload(w_ffn_in[l][:, DFF + blk * 128: DFF + blk * 128 + 256], KC, 256)
                for jj in range(2):
                    fc = blk + jj
                    def evac_g(b, t0, tn):
                        s.op("act", lambda e, b=b, t0=t0, tn=tn: e.copy(graw[:, 1 + t0:1 + t0 + tn], ps[b][:, 0:tn]),
                             reads=[Rps[b]], writes=[Rgraw])
                    linear_fm(wg, rg, jj * 128, hT, RhT, KC, evac_g)
                    w0 = cp("conv%d_0" % l, fc)
                    w1 = cp("conv%d_1" % l, fc)
                    w2 = cp("conv%d_2" % l, fc)
                    cb = cp("convb%d" % l, fc)
                    s.op("act", lambda e, w1=w1, cb=cb: e.activation(cbuf[:], graw[:, 1:T + 1], AF.Identity, bias=cb, scale=w1),
                         reads=[Rgraw, Rcolp], writes=[Rcbuf])
                    s.op("dve", lambda e, w0=w0: e.scalar_tensor_tensor(cbuf[:], graw[:, 0:T], w0, cbuf[:], ALU.mult, ALU.add),
                         reads=[Rgraw, Rcolp, Rcbuf], writes=[Rcbuf])
                    s.op("dve", lambda e, w2=w2: e.scalar_tensor_tensor(cbuf[:], graw[:, 2:T + 2], w2, cbuf[:], ALU.mult, ALU.add),
                         reads=[Rgraw, Rcolp, Rcbuf], writes=[Rcbuf])
                    gprev = graw[:, 256:256 + 1024].rearrange("p (u k) -> p u k", k=256)[:, :, 0]
                    gnext = graw[:, 257:257 + 1024].rearrange("p (u k) -> p u k", k=256)[:, :, 0]
                    s.op("dve", lambda e, gprev=gprev: e.tensor_tensor(bcorr[:, 0:4], gprev, rowp_sb[:, ROFF["bndL"] + 1:ROFF["bndL"] + 5], ALU.mult),
                         reads=[Rgraw, Rrowp], writes=[Rbcorr])
                    s.op("dve", lambda e, w0=w0: e.tensor_scalar(bcorr[:, 0:4], bcorr[:, 0:4], w0, None, ALU.mult),
                         reads=[Rbcorr, Rcolp], writes=[Rbcorr])
                    c_at = cbuf[:, 256:256 + 1024].rearrange("p (u k) -> p u k", k=256)[:, :, 0]
                    s.op("dve", lambda e, c_at=c_at: e.tensor_tensor(c_at, c_at, bcorr[:, 0:4], ALU.subtract),
                         reads=[Rbcorr, Rcbuf], writes=[Rcbuf])
                    s.op("dve", lambda e, gnext=gnext: e.tensor_tensor(bcorr[:, 4:8], gnext, rowp_sb[:, ROFF["bndL"] + 1:ROFF["bndL"] + 5], ALU.mult),
                         reads=[Rgraw, Rrowp], writes=[Rbcorr])
                    s.op("dve", lambda e, w2=w2: e.tensor_scalar(bcorr[:, 4:8], bcorr[:, 4:8], w2, None, ALU.mult),
                         reads=[Rbcorr, Rcolp], writes=[Rbcorr])
                    c_at2 = cbuf[:, 255:255 + 1024].rearrange("p (u k) -> p u k", k=256)[:, :, 0]
                    s.op("dve", lambda e, c_at2=c_at2: e.tensor_tensor(c_at2, c_at2, bcorr[:, 4:8], ALU.subtract),
                         reads=[Rbcorr, Rcbuf], writes=[Rcbuf])
                    s.op("act", lambda e: e.activation(sbuf_s[:], cbuf[:], AF.Silu), reads=[Rcbuf], writes=[Rsbuf])
                    def evac_u(b, t0, tn, fc=fc):
                        s.op("dve", lambda e, b=b, t0=t0, tn=tn: e.tensor_tensor(
                            actT[:, fc, t0:t0 + tn], ps[b][:, 0:tn], sbuf_s[:, t0:t0 + tn], ALU.mult),
                            reads=[Rps[b], Rsbuf], writes=[RactT])
                    linear_fm(wu, ru, jj * 128, hT, RhT, KC, evac_u)
            foA = carve(24320, FC * 256, BF16).rearrange("p (c n) -> p c n", c=FC)
            RfoA = ares("foA")
            foB0 = wbuf[0][:, 0:16 * 256].rearrange("p (c n) -> p c n", c=16)
            foB1 = wbuf[1][:, 0:6 * 256].rearrange("p (c n) -> p c n", c=6)
            for nb in range(4):
                src = w_ffn_out[l][:, nb * 256:(nb + 1) * 256].rearrange("(c p) n -> p c n", p=128)
                if nb % 2 == 0:
                    s.dma("pool", foA, src, writes=[RfoA])

                    def rhs_of(kc):
                        return foA[:, kc, :], RfoA
                else:
                    s.dma("pool", foB0, src[:, 0:16, :], writes=[Rw[0]])
                    s.dma("pool", foB1, src[:, 16:22, :], writes=[Rw[1]])

                    def rhs_of(kc):
                        return (foB0[:, kc, :], Rw[0]) if kc < 16 else (foB1[:, kc - 16, :], Rw[1])
                for i in range(NT):
                    b = bank()
                    for kc in range(FC):
                        rv, rr_ = rhs_of(kc)
                        s.op("pe", lambda e, b=b, kc=kc, i=i, rv=rv: e.matmul(
                            ps[b][:, 0:256], actT[:, kc, i * 128:(i + 1) * 128], rv,
                            start=(kc == 0), stop=(kc == FC - 1)),
                            reads=[rr_, RactT], writes=[Rps[b]])
                    s.op("act", lambda e, b=b, i=i, nb=nb: e.copy(fst(i)[:, nb * 256:(nb + 1) * 256], ps[b][:, 0:256]),
                         reads=[Rps[b]], writes=[Rfst_of(i)])
            for i in range(NT):
                residual_update(i, None, 1, fsrc=fst(i), Rf=Rfst_of(i))

        qT = carve(5120, 4 * T, BF16).rearrange("p (c t) -> p c t", c=4)
        RqT = ares("qT")
        kT = carve(7680, 4 * 1536, BF16).rearrange("p (c t) -> p c t", c=4)
        RkT = ares("kT")
        Vaug = carve(10752, 12 * 4 * 144, BF16).rearrange("p (k h d) -> p k h d", k=12, h=4)
        RV = ares("Vaug")
        yT = carve(0, 8 * T, BF16).rearrange("p (c t) -> p c t", c=8)
        fbT = carve(14208, 15 * (T + 2), BF16).rearrange("p (c t) -> p c t", c=15)
        RfbT = ares("fbT")
        RyT = RhT
        ropeC_sb = carve(23824, T, F32)
        ropeS_sb = carve(23824 + T, T, F32)
        Rrope = Res("rope")
        rtmp = sb("rtmp", [128, 2, 512])
        Rrtmp = Res("rtmp")
        raw_b = sb("raw_b", [128, 512], BF16)
        Rraw = Res("raw_b")

        def even_inproj():
            s.dma("sp", ropeC_sb[:], ropeC, writes=[Rrope])
            s.dma("sp", ropeS_sb[:], ropeS, writes=[Rrope])
            s.dma("pool", kT[:, :, 0:256], cacheKT.rearrange("(h p) t -> p h t", p=128), writes=[RkT])
            for kk_ in range(2):
                s.dma("pool", Vaug[:, kk_, :, 0:128],
                      cacheV[kk_ * 128:(kk_ + 1) * 128, :].rearrange("p (h d) -> p h d", h=4), writes=[RV])
            if "m" in sub:
                s.op("pool", lambda e: e.memset(Vaug[:, :, :, 128:129], 1.0), writes=[RV])
            for which, dst, doff in ((0, qT, 0), (1, kT, 256)):
                if "q" not in sub:
                    break
                wv, rw = wload(w_in_even[:, which * 512:(which + 1) * 512], KC, 512)
                Rdst = RqT if which == 0 else RkT
                for h in range(4):
                    def evac(b, t0, tn, h=h, dst=dst, doff=doff, Rdst=Rdst):
                        s.op("act", lambda e, b=b, tn=tn: e.copy(raw_b[:, 0:tn], ps[b][:, 0:tn]),
                             reads=[Rps[b]], writes=[Rraw])
                        b2 = bank()
                        s.op("pe", lambda e, b2=b2, tn=tn: e.matmul(ps[b2][:, 0:tn], rotT_b[:], raw_b[:, 0:tn], start=True, stop=True),
                             reads=[Rraw, Rconst], writes=[Rps[b2]])
                        s.op("dve", lambda e, t0=t0, tn=tn: e.tensor_tensor(rtmp[:, 0, 0:tn], raw_b[:, 0:tn], ropeC_sb[:, t0:t0 + tn], ALU.mult),
                             reads=[Rraw, Rrope], writes=[Rrtmp])
                        s.op("dve", lambda e, b2=b2, t0=t0, tn=tn: e.tensor_tensor(rtmp[:, 1, 0:tn], ps[b2][:, 0:tn], ropeS_sb[:, t0:t0 + tn], ALU.mult),
                             reads=[Rps[b2], Rrope], writes=[Rrtmp])
                        s.op("dve", lambda e, t0=t0, tn=tn: e.tensor_tensor(dst[:, h, doff + t0:doff + t0 + tn], rtmp[:, 0, 0:tn], rtmp[:, 1, 0:tn], ALU.add),
                             reads=[Rrtmp], writes=[Rdst])
                    linear_fm(wv, rw, h * 128, hT, RhT, KC, evac)
                if which == 1:
                    for i in range(NT):
                        b = bank()
                        linear_tm(wv, rw, 512, hT, RhT, KC, i, b)
                        si = stctr[0] % 2
                        stctr[0] += 1
                        s.op("act", lambda e, b=b, si=si: e.copy(stage_f[:, si, :], ps[b][:]), reads=[Rps[b]], writes=[Rstage[si]])
                        s.dma("sp", k_out[i * 128:(i + 1) * 128, :], stage_f[:, si, :], reads=[Rstage[si]])
            s.op("pool", lambda e: e.memset(fbT[:, :, 0:1], 0.0), writes=[RfbT])
            s.op("pool", lambda e: e.memset(fbT[:, :, T + 1:T + 2], 0.0), writes=[RfbT])
            for pc, (c0, ncols) in enumerate([(1536, 512), (2048, 512), (2560, 512), (3072, 384)]):
                wv, rw = wload(w_in_even[:, c0:c0 + ncols], KC, ncols)
                for j in range(ncols // 128):
                    ch = pc * 4 + j

                    def evac_fb(b, t0, tn, ch=ch):
                        s.op("act", lambda e, b=b, t0=t0, tn=tn: e.copy(fbT[:, ch, 1 + t0:1 + t0 + tn], ps[b][:, 0:tn]),
                             reads=[Rps[b]], writes=[RfbT])
                    linear_fm(wv, rw, j * 128, hT, RhT, KC, evac_fb)
            wv, rw = wload(w_in_even[:, 1024:1536], KC, 512)
            for i in range(NT if "v" in sub else 0):
                b = bank()
                linear_tm(wv, rw, 512, hT, RhT, KC, i, b)
                si = stctr[0] % 2
                stctr[0] += 1
                s.op("act", lambda e, b=b, si=si: e.copy(stage_f[:, si, :], ps[b][:]), reads=[Rps[b]], writes=[Rstage[si]])
                s.dma("sp", v_out[i * 128:(i + 1) * 128, :], stage_f[:, si, :], reads=[Rstage[si]])
                s.op("dve", lambda e, si=si, i=i: e.tensor_copy(Vaug[:, 2 + i, :, 0:128], stage_f[:, si, :].rearrange("p (h d) -> p h d", h=4)),
                     reads=[Rstage[si]], writes=[RV])


        NPB = 4
        PT = [[sb("PT%d%d" % (m, k), [128, 256], BF16) for k in range(NPB)] for m in range(2)]
        RPT = [[Res("PT%d%d" % (m, k)) for k in range(NPB)] for m in range(2)]
        ya_f = sb("ya_f", [128, 128])
        Rya = Res("ya_f")
        ya_b = sb("ya_b", [128, 128], BF16)
        Ryab = Res("ya_b")
        subln08 = sb("subln08", [128, 128])
        Rsub = Res("subln08")
        lamt = sb("lamt", [128, 64])
        Rlamt = Res("lamt")

        def attention():
            lo = ROFF["lam"]
            for k in range(2):
                s.op("dve", lambda e, k=k: e.tensor_tensor(lamt[:], rowp_sb[:, lo + 128 * k: lo + 128 * k + 64],
                                                          rowp_sb[:, lo + 128 * k + 64: lo + 128 * k + 128], ALU.mult),
                     reads=[Rrowp], writes=[Rlamt])
                s.op("dve", lambda e, k=k: e.reduce_sum(stat[:, 20 + k:21 + k], lamt[:], axis=AX.X),
                     reads=[Rlamt], writes=[Rstat])
            s.op("act", lambda e: e.activation(stat[:, 22:24], stat[:, 20:22], AF.Exp), reads=[Rstat], writes=[Rstat])
            s.op("dve", lambda e: e.tensor_tensor(stat[:, 24:25], stat[:, 22:23], stat[:, 23:24], ALU.subtract),
                 reads=[Rstat], writes=[Rstat])
            s.op("dve", lambda e: e.tensor_scalar(stat[:, 25:26], stat[:, 24:25], 0.2, -1.0, ALU.add, ALU.mult),
                 reads=[Rstat], writes=[Rstat])
            s.op("dve", lambda e: e.tensor_scalar(subln08[:], rowp_sb[:, ROFF["subln"]:ROFF["subln"] + 128], 0.8, None, ALU.mult),
                 reads=[Rrowp], writes=[Rsub])
            pctr = [0, 0]
            for h in range(4):
                for qu in range(5):
                    acc = [bank(), bank()]
                    reserved.update(acc)
                    pend = []
                    for kt in range(12):
                        ku = 0 if kt < 2 else 1 + (kt - 2) // 2
                        bcol = ROFF["abias"] + ku * 5 + qu
                        for m in range(2):
                            bs = bank()
                            s.op("pe", lambda e, bs=bs, m=m, h=h, kt=kt, qu=qu: e.matmul(
                                ps[bs][:, 0:256], kT[64 * m:64 * m + 64, h, kt * 128:(kt + 1) * 128],
                                qT[64 * m:64 * m + 64, h, qu * 256:(qu + 1) * 256], start=True, stop=True),
                                reads=[RkT, RqT], writes=[Rps[bs]])
                            pk = pctr[m] % NPB
                            pctr[m] += 1
                            s.op("act", lambda e, bs=bs, m=m, pk=pk, bcol=bcol: e.activation(
                                PT[m][pk][:], ps[bs][:, 0:256], AF.Exp, bias=rowp_sb[:, bcol:bcol + 1], scale=0.125),
                                reads=[Rps[bs], Rrowp], writes=[RPT[m][pk]])

                            def pv(m=m, pk=pk, kt=kt, h=h, acc=acc):
                                for qt in range(2):
                                    s.op("pe", lambda e, qt=qt: e.matmul(
                                        ps[acc[m]][:, qt * 129:qt * 129 + 129], PT[m][pk][:, qt * 128:(qt + 1) * 128],
                                        Vaug[:, kt, h, 0:129], start=(kt == 0 and qt == 0), stop=(kt == 11),
                                        skip_group_check=True),
                                        reads=[RPT[m][pk], RV], writes=[Rps[acc[m]]])
                            pend.append(pv)
                            if len(pend) > 2:
                                pend.pop(0)()
                    while pend:
                        pend.pop(0)()
                    reserved.difference_update(acc)
                    for qt in range(2):
                        i = qu * 2 + qt
                        c0 = qt * 129
                        s.op("dve", lambda e, c0=c0, acc=acc: e.reciprocal(stat[:, 30:31], ps[acc[0]][:, c0 + 128:c0 + 129]),
                             reads=[Rps[acc[0]]], writes=[Rstat])
                        s.op("dve", lambda e, c0=c0, acc=acc: e.reciprocal(stat[:, 31:32], ps[acc[1]][:, c0 + 128:c0 + 129]),
                             reads=[Rps[acc[1]]], writes=[Rstat])
                        s.op("dve", lambda e: e.tensor_tensor(stat[:, 32:33], stat[:, 31:32], stat[:, 25:26], ALU.mult),
                             reads=[Rstat], writes=[Rstat])
                        s.op("dve", lambda e, c0=c0, acc=acc: e.tensor_scalar(ya_f[:], ps[acc[0]][:, c0:c0 + 128], stat[:, 30:31], None, ALU.mult),
                             reads=[Rps[acc[0]], Rstat], writes=[Rya])
                        s.op("dve", lambda e, c0=c0, acc=acc: e.scalar_tensor_tensor(ya_f[:], ps[acc[1]][:, c0:c0 + 128], stat[:, 32:33], ya_f[:], ALU.mult, ALU.add),
                             reads=[Rps[acc[1]], Rstat, Rya], writes=[Rya])
                        s.op("act", lambda e: e.activation(junk_b[:, 0:128], ya_f[:], AF.Square, scale=1.0 / math.sqrt(128.0),
                                                           accum_out=stat[:, 33:34]),
                             reads=[Rya], writes=[Rjunk, Rstat])
                        s.op("dve", lambda e: e.tensor_scalar(stat[:, 34:35], stat[:, 33:34], 1e-6, None, ALU.add),
                             reads=[Rstat], writes=[Rstat])
                        s.op("act", lambda e: e.activation(stat[:, 34:35], stat[:, 34:35], AF.Sqrt), reads=[Rstat], writes=[Rstat])
                        s.op("dve", lambda e: e.reciprocal(stat[:, 35:36], stat[:, 34:35]), reads=[Rstat], writes=[Rstat])
                        s.op("dve", lambda e: e.scalar_tensor_tensor(ya_b[:], ya_f[:], stat[:, 35:36], subln08[:], ALU.mult, ALU.mult),
                             reads=[Rya, Rstat, Rsub], writes=[Ryab])
                        reserved.update(acc)
                        bt = bank()
                        reserved.difference_update(acc)
                        s.op("pe", lambda e, bt=bt: e.transpose(psb[bt][:, 0:128], ya_b[:], ident_b[:]),
                             reads=[Ryab, Rconst], writes=[Rps[bt]])
                        s.op("act", lambda e, bt=bt, h=h, i=i: e.copy(yT[:, h, i * 128:(i + 1) * 128], psb[bt][:, 0:128]),
                             reads=[Rps[bt]], writes=[RyT])

        def out_proj(W, actTv, Ract):
            pieces = [wload(W[:, hh * 512:(hh + 1) * 512], KC, 512) for hh in range(2)]
            for i in range(NT):
                bb = [bank(), bank()]
                for hh in range(2):
                    linear_tm(pieces[hh][0], pieces[hh][1], 512, actTv, Ract, KC, i, bb[hh])
                residual_update(i, bb, 0)

        Xc = carve(5120, NT * 8 * 256, BF16).rearrange("p (i g n) -> p i g n", i=NT, g=8)
        RXc = ares("Xc")
        chCS_b = sb("chCS_b", [128, 256], BF16)
        Rch = Res("chCS")

        def fnet():
            s.dma("pool", chCS_b[:], chCS, writes=[Rch])
            for i in range(NT):
                for gp in range(4):
                    b = bank()
                    for g2_ in range(2):
                        g = gp * 2 + g2_
                        s.op("pe", lambda e, b=b, g=g, g2_=g2_, i=i: e.matmul(
                            ps[b][:, g2_ * 256:(g2_ + 1) * 256], hT[:, g, i * 128:(i + 1) * 128], chCS_b[:], start=True, stop=True),
                            reads=[RhT, Rch], writes=[Rps[b]])
                    s.op("act", lambda e, b=b, gp=gp, i=i: e.copy(
                        Xc[:, i, gp * 2:gp * 2 + 2, :], ps[b][:].rearrange("p (g n) -> p g n", g=2)),
                        reads=[Rps[b]], writes=[RXc])
            for (t0, tn) in [(0, 384), (384, 384), (768, 384), (1152, 128)]:
                wc, rc = wload(dftC[:, t0:t0 + tn], NT, tn)
                wsn, rsn = wload(dftSn[:, t0:t0 + tn], NT, tn)
                for g in range(8):
                    b = bank()
                    for tc in range(NT):
                        s.op("pe", lambda e, b=b, g=g, tc=tc, tn=tn, wc=wc: e.matmul(
                            ps[b][:, 0:tn], Xc[:, tc, g, 0:128], wc[:, tc, 0:tn], start=(tc == 0), stop=False),
                            reads=[RXc, rc], writes=[Rps[b]])
                        s.op("pe", lambda e, b=b, g=g, tc=tc, tn=tn, wsn=wsn: e.matmul(
                            ps[b][:, 0:tn], Xc[:, tc, g, 128:256], wsn[:, tc, 0:tn], start=False, stop=(tc == NT - 1)),
                            reads=[RXc, rsn], writes=[Rps[b]])
                    s.op("act", lambda e, b=b, g=g, t0=t0, tn=tn: e.copy(yT[:, g, t0:t0 + tn], ps[b][:, 0:tn]),
                         reads=[Rps[b]], writes=[RyT])


        V_tm = carve(5120, NT * 512, BF16).rearrange("p (i n) -> p i n", i=NT)
        RVtm = ares("V_tm")
        YSb = carve(7680, NT * 512, BF16).rearrange("p (i n) -> p i n", i=NT)
        RYS = ares("YSb")
        Ystage = carve(10240, 128 * 16, F32)[0:64, :].rearrange("p (n k) -> p n k", k=16)
        RYst = ares("Ystage")
        Sst = [carve(12288, 512, F32), carve(12800, 512, F32)]
        RS = [ares("S0"), ares("S1")]
        tmpA = carve(13312, 512, F32)
        RtA = ares("tmpA")
        tmpB = carve(23824, 512, F32)
        RtB = ares("tmpB")
        w2t_b = carve(24336, 512, BF16)
        a2t_b = carve(24592, 512, BF16)
        g2_b = carve(24848, 512, BF16)
        Rlora = ares("lora")
        PT0 = 25104
        NPT = 12

        def pt(k, n=1):
            return carve(PT0 + 128 * k, 128 * n, F32)
        Rpt = [ares("pt%d" % k) for k in range(NPT)]
        lnx0_b = carve(26640, 512, BF16)
        lnx1_b = carve(26896, 512, BF16)
        Rlnx = ares("lnx")
        tb0 = junk_b[:, 0:128]
        tb1 = junk_b[:, 128:256]
        Rtb0 = Res("tb0")
        Rtb1 = Res("tb1")
        colx = sb("colx", [128, 64])
        Rcolx = Res("colx")
        tiny = sb("tiny", [128, 8])
        Rtiny = Res("tiny")
        BS = sb("BS", [128, NT, 8])
        RBS = Res("BS")
        wbf = [w[:].bitcast(F32) for w in wbuf]
        TW = wbf[0][:, 0:1024].rearrange("p (k n) -> p k n", k=8)
        TKK = wbf[0][:, 1024:2048].rearrange("p (k n) -> p k n", k=8)
        TNK = wbf[1][:, 0:1024].rearrange("p (k n) -> p k n", k=8)
        TKD = wbf[1][:, 1024:2048].rearrange("p (k n) -> p k n", k=8)
        TR2 = wbuf[2][:, 0:2048].rearrange("p (k n h) -> p k n h", k=8, h=2)
        tmpA_b = carve(10240, 512, BF16)
        RtAb = ares("tmpA_b")
        Sb16 = carve(10496, 512, BF16)
        RSb = ares("Sb16")
        CX = dict(cmu=0, nmu0=15, nmu1=30, omk1=45, keepL=49, keepR=54)

        def cx(name, c=0):
            o = CX[name] + c
            return colx[:, o:o + 1]

        def rwkv_setup():
            s.dma("pool", w2t_b, w2t_d, writes=[Rlora])
            s.dma("pool", a2t_b, a2t_d, writes=[Rlora])
            s.dma("pool", g2_b, g2_d, writes=[Rlora])
            s.dma("pool", lnx0_b, lnx_d[0, :].partition_broadcast(128), writes=[Rlnx])
            s.dma("pool", lnx1_b, lnx_d[1, :].partition_broadcast(128), writes=[Rlnx])
            mu0 = cp("mu0", 0, 15)
            mu1 = cp("mu1", 0, 15)
            s.op("dve", lambda e: e.tensor_tensor(colx[:, 0:15], mu0, mu1, ALU.add), reads=[Rcolp], writes=[Rcolx])
            s.op("dve", lambda e: e.tensor_scalar(colx[:, 0:15], colx[:, 0:15], -1.0, 1.0, ALU.mult, ALU.add),
                 reads=[Rcolx], writes=[Rcolx])
            s.op("dve", lambda e: e.tensor_scalar(colx[:, 15:30], mu0, -1.0, None, ALU.mult), reads=[Rcolp], writes=[Rcolx])
            s.op("dve", lambda e: e.tensor_scalar(colx[:, 30:45], mu1, -1.0, None, ALU.mult), reads=[Rcolp], writes=[Rcolx])
            s.op("dve", lambda e: e.tensor_scalar(colx[:, 45:49], cp("kvec1", 0, 4), -1.0, 1.0, ALU.mult, ALU.add),
                 reads=[Rcolp], writes=[Rcolx])
            s.op("dve", lambda e: e.tensor_scalar(colx[:, 49:54], rowp_sb[:, ROFF["bndL"]:ROFF["bndL"] + 5], -1.0, 1.0, ALU.mult, ALU.add),
                 reads=[Rrowp], writes=[Rcolx])
            s.op("dve", lambda e: e.tensor_scalar(colx[:, 54:59], rowp_sb[:, ROFF["bndR"]:ROFF["bndR"] + 5], -1.0, 1.0, ALU.mult, ALU.add),
                 reads=[Rrowp], writes=[Rcolx])
            s.op("dve", lambda e: e.memset(BS[:], 0.0), writes=[RBS])

        def shift(ch, ti, out, Rout):
            t0 = ti * 128
            f = fbT[:, ch, 1 + t0:1 + t0 + 128]
            fp = fbT[:, ch, t0:t0 + 128]
            fn = fbT[:, ch, 2 + t0:2 + t0 + 128]
            rr = [RfbT, Rcolp, Rcolx]
            s.op("dve", lambda e: e.tensor_scalar(out, f, cx("cmu", ch), None, ALU.mult), reads=rr, writes=[Rout])
            s.op("dve", lambda e: e.scalar_tensor_tensor(out, fp, cp("mu0", ch), out, ALU.mult, ALU.add),
                 reads=rr + [Rout], writes=[Rout])
            s.op("dve", lambda e: e.scalar_tensor_tensor(out, fn, cp("mu1", ch), out, ALU.mult, ALU.add),
                 reads=rr + [Rout], writes=[Rout])
            u = ti // 2
            if ti % 2 == 0:
                bl = rowp_sb[:, ROFF["bndL"] + u:ROFF["bndL"] + u + 1]
                s.op("dve", lambda e: e.tensor_tensor(tiny[:, 0:1], fbT[:, ch, t0:t0 + 1], bl, ALU.mult),
                     reads=[RfbT, Rrowp], writes=[Rtiny])
                s.op("dve", lambda e: e.scalar_tensor_tensor(out[:, 0:1], tiny[:, 0:1], cx("nmu0", ch), out[:, 0:1], ALU.mult, ALU.add),
                     reads=[Rtiny, Rcolx, Rout], writes=[Rout])
            else:
                br = rowp_sb[:, ROFF["bndR"] + u:ROFF["bndR"] + u + 1]
                s.op("dve", lambda e: e.tensor_tensor(tiny[:, 1:2], fbT[:, ch, 1 + t0 + 128:2 + t0 + 128], br, ALU.mult),
                     reads=[RfbT, Rrowp], writes=[Rtiny])
                s.op("dve", lambda e: e.scalar_tensor_tensor(out[:, 127:128], tiny[:, 1:2], cx("nmu1", ch), out[:, 127:128], ALU.mult, ALU.add),
                     reads=[Rtiny, Rcolx, Rout], writes=[Rout])

        def rwkv_prepass():
            for ti in range(NT):
                for c in range(4):
                    shift(8 + c, ti, pt(c), Rpt[c])
                    s.op("act", lambda e, c=c: e.copy(xn_b[:, c * 128:(c + 1) * 128], pt(c)), reads=[Rpt[c]], writes=[Rxn])
                b = bank()
                for c in range(4):
                    s.op("pe", lambda e, b=b, c=c: e.transpose(psb[b][:, c * 128:(c + 1) * 128], xn_b[:, c * 128:(c + 1) * 128], ident_b[:]),
                         reads=[Rxn, Rconst], writes=[Rps[b]])
                s.op("act", lambda e, b=b, ti=ti: e.copy(V_tm[:, ti, :], psb[b][:, 0:512]), reads=[Rps[b]], writes=[RVtm])

        def prep(d, ti):
            rev = (d == 1)

            def tab(T3, c):
                v = T3[:, d * 4 + c, :]
                return v[:, ::-1] if rev else v
            tabs_w = [Rw[0], Rw[1], Rw[2]]
            shift(12, ti, pt(0), Rpt[0])
            shift(13, ti, pt(1), Rpt[1])
            s.op("act", lambda e: e.activation(tb0, pt(0), AF.Tanh), reads=[Rpt[0]], writes=[Rtb0])
            s.op("act", lambda e: e.copy(tb1, pt(1)), reads=[Rpt[1]], writes=[Rtb1])
            lo, hi = 64 * d, 64 * d + 64
            for c in range(4):
                b = bank()
                s.op("pe", lambda e, b=b, c=c: e.matmul(ps[b][:, 0:128], w2t_b[lo:hi, c * 128:(c + 1) * 128], tb0[lo:hi, :], start=True, stop=True),
                     reads=[Rlora, Rtb0], writes=[Rps[b]])
                s.op("act", lambda e, b=b, c=c: e.activation(pt(2), ps[b][:, 0:128], AF.Sigmoid, bias=cp("w0_%d" % d, c)),
                     reads=[Rps[b], Rcolp], writes=[Rpt[2]])
                s.op("act", lambda e, c=c: e.activation(tab(TW, c), pt(2), AF.Exp, scale=-EXPM05),
                     reads=[Rpt[2]], writes=[Rw[0]])
                b = bank()
                s.op("pe", lambda e, b=b, c=c: e.matmul(ps[b][:, 0:128], a2t_b[lo:hi, c * 128:(c + 1) * 128], tb1[lo:hi, :], start=True, stop=True),
                     reads=[Rlora, Rtb1], writes=[Rps[b]])
                s.op("act", lambda e, b=b, c=c: e.activation(pt(3), ps[b][:, 0:128], AF.Sigmoid, bias=cp("a0_%d" % d, c)),
                     reads=[Rps[b], Rcolp], writes=[Rpt[3]])
                shift(4 + c, ti, pt(4), Rpt[4])
                s.op("dve", lambda e, c=c: e.tensor_scalar(pt(5), pt(4), cp("kvec0", c), None, ALU.mult),
                     reads=[Rpt[4], Rcolp], writes=[Rpt[5]])
                s.op("act", lambda e: e.activation(pt(6), pt(5), AF.Square), reads=[Rpt[5]], writes=[Rpt[6]])
                b = bank()
                s.op("pe", lambda e, b=b: e.matmul(ps[b][:, 0:128], bones_f[:], pt(6), start=True, stop=True),
                     reads=[Rconst, Rpt[6]], writes=[Rps[b]])
                s.op("dve", lambda e, b=b: e.tensor_scalar(pt(7), ps[b][:, 0:128], 1e-12, None, ALU.add),
                     reads=[Rps[b]], writes=[Rpt[7]])
                s.op("act", lambda e: e.activation(pt(7), pt(7), AF.Sqrt), reads=[Rpt[7]], writes=[Rpt[7]])
                s.op("dve", lambda e: e.reciprocal(pt(7), pt(7)), reads=[Rpt[7]], writes=[Rpt[7]])
                s.op("dve", lambda e: e.tensor_tensor(pt(8), pt(5), pt(7), ALU.mult), reads=[Rpt[5], Rpt[7]], writes=[Rpt[8]])
                s.op("act", lambda e, c=c: e.copy(tab(TKK, c), pt(8)), reads=[Rpt[8]], writes=[Rw[0]])
                s.op("dve", lambda e, c=c: e.scalar_tensor_tensor(tab(TNK, c), pt(8), -1.0, pt(3), ALU.mult, ALU.mult),
                     reads=[Rpt[8], Rpt[3]], writes=[Rw[1]])
                s.op("dve", lambda e, c=c: e.tensor_scalar(pt(9), pt(3), cp("kvec1", c), cx("omk1", c), ALU.mult, ALU.add),
                     reads=[Rpt[3], Rcolp, Rcolx], writes=[Rpt[9]])
                s.op("dve", lambda e: e.tensor_tensor(pt(9), pt(4), pt(9), ALU.mult), reads=[Rpt[4], Rpt[9]], writes=[Rpt[9]])
                s.op("act", lambda e, c=c: e.copy(tab(TKD, c), pt(9)), reads=[Rpt[9]], writes=[Rw[1]])
                shift(c, ti, pt(10), Rpt[10])
                for h2 in range(2):
                    o = TR2[:, d * 4 + c, :, h2]
                    if rev:
                        o = o[:, ::-1]
                    s.op("dve", lambda e, o=o, h2=h2: e.tensor_scalar(o, pt(10), halfsel_f[:, h2:h2 + 1], None, ALU.mult),
                         reads=[Rpt[10], Rconst], writes=[Rw[2]])
                s.op("dve", lambda e: e.tensor_tensor(pt(6), pt(10), pt(9), ALU.mult), reads=[Rpt[10], Rpt[9]], writes=[Rpt[6]])
                s.op("dve", lambda e, c=c: e.tensor_scalar(pt(6), pt(6), cp("kvec2", c), None, ALU.mult),
                     reads=[Rpt[6], Rcolp], writes=[Rpt[6]])
                b = bank()
                s.op("pe", lambda e, b=b: e.matmul(ps[b][:, 0:2], pt(6), halfsel_f[:, 0:2], start=True, stop=True),
                     reads=[Rpt[6], Rconst], writes=[Rps[b]])
                s.op("dve", lambda e, b=b, c=c, ti=ti: e.tensor_tensor(BS[:, ti, c * 2:c * 2 + 2], BS[:, ti, c * 2:c * 2 + 2], ps[b][:, 0:2], ALU.add),
                     reads=[Rps[b], RBS], writes=[RBS])

        yb = xn_b[:, 0:512]

        def finalize(ti, by):
            ys = pt(0, 4)
            Rys = [Rpt[0], Rpt[1], Rpt[2], Rpt[3]]
            t2 = pt(4, 4)
            Rt2 = [Rpt[4], Rpt[5], Rpt[6], Rpt[7]]
            ys3 = ys.rearrange("p (h i) -> p h i", h=8)
            t23 = t2.rearrange("p (h i) -> p h i", h=8)
            s.op("dve", lambda e: e.tensor_tensor(ys, YSb[:, ti, :], ps[by][:], ALU.add), reads=[RYS, Rps[by]], writes=Rys)
            s.op("dve", lambda e: e.reduce_sum(tiny[:, 0:8], ys3, axis=AX.X), reads=Rys, writes=[Rtiny])
            s.op("dve", lambda e: e.tensor_scalar(tiny[:, 0:8], tiny[:, 0:8], -1.0 / 64.0, None, ALU.mult), reads=[Rtiny], writes=[Rtiny])
            s.op("dve", lambda e: e.tensor_tensor(ys3, ys3, tiny[:, 0:8].unsqueeze(2).to_broadcast([128, 8, 64]), ALU.add),
                 reads=Rys + [Rtiny], writes=Rys)
            s.op("dve", lambda e: e.tensor_tensor(t2, ys, ys, ALU.mult), reads=Rys, writes=Rt2)
            s.op("dve", lambda e: e.reduce_sum(stat[:, 40:48], t23, axis=AX.X), reads=Rt2, writes=[Rstat])
            s.op("dve", lambda e: e.tensor_scalar(stat[:, 40:48], stat[:, 40:48], 1.0 / 64.0, 64e-5, ALU.mult, ALU.add),
                 reads=[Rstat], writes=[Rstat])
            s.op("act", lambda e: e.activation(stat[:, 40:48], stat[:, 40:48], AF.Sqrt), reads=[Rstat], writes=[Rstat])
            s.op("dve", lambda e: e.reciprocal(stat[:, 48:56], stat[:, 40:48]), reads=[Rstat], writes=[Rstat])
            s.op("dve", lambda e: e.tensor_tensor(ys3, ys3, stat[:, 48:56].unsqueeze(2).to_broadcast([128, 8, 64]), ALU.mult),
                 reads=Rys + [Rstat], writes=Rys)
            s.op("dve", lambda e: e.tensor_tensor(ys, ys, lnx0_b, ALU.mult), reads=Rys + [Rlnx], writes=Rys)
            s.op("dve", lambda e: e.tensor_tensor(ys, ys, lnx1_b, ALU.add), reads=Rys + [Rlnx], writes=Rys)
            s.op("dve", lambda e: e.tensor_tensor(t23, V_tm[:, ti, :].rearrange("p (h i) -> p h i", h=8),
                                                  BS[:, ti, :].unsqueeze(2).to_broadcast([128, 8, 64]), ALU.mult),
                 reads=[RVtm, RBS], writes=Rt2)
            s.op("dve", lambda e: e.tensor_tensor(ys, ys, t2, ALU.add), reads=Rys + Rt2, writes=Rys)
            shift(14, ti, pt(8), Rpt[8])
            s.op("act", lambda e: e.activation(tb0, pt(8), AF.Sigmoid), reads=[Rpt[8]], writes=[Rtb0])
            bg = bank()
            s.op("pe", lambda e, bg=bg: e.matmul(ps[bg][:], tb0, g2_b, start=True, stop=True),
                 reads=[Rtb0, Rlora], writes=[Rps[bg]])
            s.op("dve", lambda e, bg=bg: e.tensor_tensor(yb, ys, ps[bg][:], ALU.mult), reads=Rys + [Rps[bg]], writes=[Rxn])
            bt = bank()
            for c in range(4):
                s.op("pe", lambda e, bt=bt, c=c: e.transpose(psb[bt][:, c * 128:(c + 1) * 128], yb[:, c * 128:(c + 1) * 128], ident_b[:]),
                     reads=[Rxn, Rconst], writes=[Rps[bt]])
            s.op("act", lambda e, bt=bt, ti=ti: e.copy(yT[:, 4:8, ti * 128:(ti + 1) * 128],
                                                        psb[bt][:, 0:512].rearrange("p (c t) -> p c t", c=4)),
                 reads=[Rps[bt]], writes=[RyT])

        def rwkv_scan(nrounds=NT):
            S8 = [x_.rearrange("p (k i) -> p k i", k=8) for x_ in Sst]
            tA8 = tmpA.rearrange("p (k i) -> p k i", k=8)
            tB8 = tmpB.rearrange("p (k i) -> p k i", k=8)

            def bc(T3, n):
                return T3[:, :, n].unsqueeze(2).to_broadcast([128, 8, 64])
            for r in range(nrounds):
                tf, tbk = r, NT - 1 - r
                prep(0, tf)
                prep(1, tbk)
                if tf % 2 == 0:
                    u = tf // 2
                    if u == 0:
                        s.dma("sp", Sst[0][:, 0:256], initS[0], writes=[RS[0]])
                    else:
                        s.op("dve", lambda e, u=u: e.tensor_scalar(Sst[0][:, 0:256], Sst[0][:, 0:256], cx("keepL", u), None, ALU.mult),
                             reads=[RS[0], Rcolx], writes=[RS[0]])
                if tbk % 2 == 1:
                    u = tbk // 2
                    if u == 4:
                        s.op("dve", lambda e: e.memset(Sst[0][:, 256:512], 0.0), writes=[RS[0]])
                    elif u == 3:
                        s.dma("sp", tmpB[:, 0:256], initS[1], writes=[RtB])
                        s.op("dve", lambda e, u=u: e.scalar_tensor_tensor(Sst[0][:, 256:512], Sst[0][:, 256:512], cx("keepR", u),
                                                                          tmpB[:, 0:256], ALU.mult, ALU.add),
                             reads=[RS[0], Rcolx, RtB], writes=[RS[0]])
                    else:
                        s.op("dve", lambda e, u=u: e.tensor_scalar(Sst[0][:, 256:512], Sst[0][:, 256:512], cx("keepR", u), None, ALU.mult),
                             reads=[RS[0], Rcolx], writes=[RS[0]])
                by_ = None
                pending = []

                def flush():
                    for f_ in pending:
                        f_()
                    del pending[:]
                for n in range(128):
                    ci, ni = n % 2, (n + 1) % 2
                    Sc, Sn = S8[ci], S8[ni]
                    if n % 32 == 0:
                        by_ = bank()
                        reserved.add(by_)
                    tAb8 = tmpA_b.rearrange("p (k i) -> p k i", k=8)
                    s.op("dve", lambda e, n=n, Sc=Sc, tAb8=tAb8: e.tensor_tensor(tAb8, Sc, bc(TKK, n), ALU.mult),
                         reads=[RS[ci], Rw[0]], writes=[RtAb])
                    s.op("dve", lambda e, n=n, Sc=Sc, Sn=Sn: e.tensor_tensor(Sn, Sc, bc(TW, n), ALU.mult),
                         reads=[RS[ci], Rw[0]], writes=[RS[ni]])
                    bv = bank()
                    for d in range(2):
                        tile_d = tf if d == 0 else tbk
                        row = n if d == 0 else 127 - n
                        for h2 in range(2):
                            s.op("pe", lambda e, bv=bv, d=d, h2=h2, tile_d=tile_d, row=row: e.matmul(
                                ps[bv][64 * h2:64 * h2 + 64, d * 256:(d + 1) * 256].rearrange("p (c i) -> p c i", c=4),
                                ident_b[:, row:row + 1].to_broadcast([128, 64]),
                                V_tm[:, tile_d, :].rearrange("p (c h i) -> p c h i", c=4, h=2)[:, :, h2, :],
                                start=True, stop=True),
                                reads=[RVtm, Rconst], writes=[Rps[bv]])
                    bs_ = bank()
                    s.op("pe", lambda e, bs_=bs_: e.matmul(ps[bs_][:], bones_b[:], tmpA_b, start=True, stop=True),
                         reads=[RtAb, Rconst], writes=[Rps[bs_]])
                    flush()
                    s.op("dve", lambda e, n=n, bv=bv: e.tensor_tensor(tB8, ps[bv][:].rearrange("p (k i) -> p k i", k=8), bc(TKD, n), ALU.mult),
                         reads=[Rps[bv], Rw[1]], writes=[RtB])
                    s.op("dve", lambda e, ni=ni: e.tensor_tensor(Sst[ni], Sst[ni], tmpB, ALU.add),
                         reads=[RS[ni], RtB], writes=[RS[ni]])
                    s.op("dve", lambda e, n=n, bs_=bs_: e.tensor_tensor(tA8, ps[bs_][:].rearrange("p (k i) -> p k i", k=8), bc(TNK, n), ALU.mult),
                         reads=[Rps[bs_], Rw[1]], writes=[RtA])
                    s.op("dve", lambda e, ni=ni: e.tensor_tensor(Sst[ni], Sst[ni], tmpA, ALU.add),
                         reads=[RS[ni], RtA], writes=[RS[ni]])
                    s.op("act", lambda e, ni=ni: e.copy(Sb16, Sst[ni]), reads=[RS[ni]], writes=[RSb])
                    nl = n % 32

                    def ymm(by_=by_, nl=nl, n=n):
                        for dc in range(8):
                            s.op("pe", lambda e, dc=dc: e.matmul(
                                ps[by_][0:64, nl * 16 + dc * 2:nl * 16 + dc * 2 + 2], Sb16[:, dc * 64:(dc + 1) * 64], TR2[:, dc, n, :],
                                start=True, stop=True),
                                reads=[RSb, Rw[2]], writes=[Rps[by_]])
                    pending.append(ymm)
                    if nl == 31:
                        flush()
                        n0 = n - 31
                        src = ps[by_][0:64, :].rearrange("p (n k) -> p n k", k=16)
                        s.op("act", lambda e, src=src, n0=n0: e.copy(Ystage[:, n0:n0 + 32, 0:8], src[:, :, 0:8]),
                             reads=[Rps[by_]], writes=[RYst])
                        s.op("act", lambda e, src=src, n0=n0: e.copy(Ystage[:, 96 - n0:128 - n0, 8:16][:, ::-1, :], src[:, :, 8:16]),
                             reads=[Rps[by_]], writes=[RYst])
                        reserved.discard(by_)
                if tf % 2 == 1:
                    s.dma("sp", s_out[tf // 2, 0], Sst[0][:, 0:256], reads=[RS[0]])
                if tbk % 2 == 0:
                    s.dma("sp", s_out[tbk // 2, 1], Sst[0][:, 256:512], reads=[RS[0]])
                for d in range(2):
                    tile_d = tf if d == 0 else tbk
                    byy = bank()
                    for k8 in range(8):
                        s.op("pe", lambda e, byy=byy, k8=k8, d=d: e.matmul(
                            ps[byy][:, k8 * 64:(k8 + 1) * 64], Ystage[:, :, d * 8 + k8], ident_f[0:64, 0:64], start=True, stop=True),
                            reads=[RYst, Rconst], writes=[Rps[byy]])
                    first = (r < 5)
                    if first:
                        s.op("act", lambda e, byy=byy, tile_d=tile_d: e.copy(YSb[:, tile_d, :], ps[byy][:]),
                             reads=[Rps[byy]], writes=[RYS])
                    else:
                        finalize(tile_d, byy)


        def wbw(i, off_w, nelem):
            return wbuf[i][:, 2 * off_w:2 * off_w + nelem]
        Ast = carve(13200, 512, F32)
        RAst = ares("Ast")
        Abf = carve(13712, 512, BF16)
        RAbf = ares("Abf")
        WS = []
        fB = 14208 + 8 * 641
        for wsi in range(2):
            d_ = {}
            if wsi == 0:
                for k_, nm in enumerate(["Lt0", "Lt1", "L0", "L1", "Z0"]):
                    d_[nm] = carve(10240 + 512 * k_, 512, F32)
                d_["MT"] = carve(12800, 256, BF16)
                d_["RT"] = carve(12928, 512, BF16)
                d_["Z1"] = wbuf[0][:, 0:1024].bitcast(F32)
                for k_, nm in enumerate(["Lakt", "Mrbt", "Mrkt", "BKtm"]):
                    d_[nm] = wbw(0, 512 + 256 * k_, 512)
                d_["Zfb"] = carve(10240, 512, BF16)
            else:
                for k_, nm in enumerate(["Lt0", "Lt1", "L0", "L1", "Z0"]):
                    d_[nm] = carve(fB + 512 * k_, 512, F32)
                d_["Z1"] = wbuf[1][:, 0:1024].bitcast(F32)
                d_["Lakt"] = wbw(0, 1536, 512)
                d_["Mrbt"] = wbw(0, 1792, 512)
                d_["Mrkt"] = wbw(1, 512, 512)
                d_["MT"] = wbw(1, 768, 256)
                d_["BKtm"] = carve(23824, 512, BF16)
                d_["RT"] = carve(24080, 512, BF16)
                d_["Zfb"] = carve(fB, 512, BF16)
            for nm in ["Lt0", "Lt1", "L0", "L1", "Z0", "Z1", "Lakt", "Mrbt", "Mrkt", "BKtm", "MT", "RT"]:
                d_["R" + nm] = ares("ws%d_%s" % (wsi, nm))
            d_["RZfb"] = d_["RLt0"]
            WS.append(d_)
        masks = wbw(1, 896, 4 * 4 * 128).rearrange("p (m q t) -> p m q t", m=4, q=4)
        Rmask = ares("masks")
        TBL = []
        for par in range(2):
            d_ = {}
            for k_, nm in enumerate(["RH", "AH", "BH", "KH"]):
                d_[nm] = wbw(2, par * 1024 + 256 * k_, 512).rearrange("p (c t) -> p c t", c=4)
                d_["R" + nm] = ares("tb%d_%s" % (par, nm))
            TBL.append(d_)
        pcs = sb("pcs", [128, 2, 4])
        Rpcs = [Res("pcs0"), Res("pcs1")]

        def prep_chunk(d, ti, par):
            TB = TBL[par]
            shift(12, ti, pt(0), Rpt[0])
            shift(13, ti, pt(1), Rpt[1])
            s.op("act", lambda e: e.activation(tb0, pt(0), AF.Tanh), reads=[Rpt[0]], writes=[Rtb0])
            s.op("act", lambda e: e.copy(tb1, pt(1)), reads=[Rpt[1]], writes=[Rtb1])
            lo, hi = 64 * d, 64 * d + 64
            for c in range(4):
                b = bank()
                s.op("pe", lambda e, b=b, c=c: e.matmul(ps[b][:, 0:128], w2t_b[lo:hi, c * 128:(c + 1) * 128], tb0[lo:hi, :], start=True, stop=True),
                     reads=[Rlora, Rtb0], writes=[Rps[b]])
                s.op("act", lambda e, b=b, c=c: e.activation(pt(2), ps[b][:, 0:128], AF.Sigmoid, bias=cp("w0_%d" % d, c)),
                     reads=[Rps[b], Rcolp], writes=[Rpt[2]])
                b = bank()
                s.op("pe", lambda e, b=b, c=c: e.matmul(ps[b][:, 0:128], a2t_b[lo:hi, c * 128:(c + 1) * 128], tb1[lo:hi, :], start=True, stop=True),
                     reads=[Rlora, Rtb1], writes=[Rps[b]])
                s.op("act", lambda e, b=b, c=c: e.activation(pt(3), ps[b][:, 0:128], AF.Sigmoid, bias=cp("a0_%d" % d, c)),
                     reads=[Rps[b], Rcolp], writes=[Rpt[3]])
                s.op("dve", lambda e: e.tensor_scalar(pt(2), pt(2), -EXPM05, None, ALU.mult), reads=[Rpt[2]], writes=[Rpt[2]])
                if d == 0:
                    s.op("dve", lambda e: e.tensor_tensor_scan(pt(0), pt(11), pt(2), 0.0, ALU.mult, ALU.add),
                         reads=[Rpt[2], Rpt[11]], writes=[Rpt[0]])
                else:
                    s.op("dve", lambda e: e.tensor_tensor_scan(pt(0)[:, ::-1], pt(11), pt(2)[:, ::-1], 0.0, ALU.mult, ALU.add),
                         reads=[Rpt[2], Rpt[11]], writes=[Rpt[0]])
                s.op("act", lambda e: e.activation(pt(1), pt(0), AF.Exp), reads=[Rpt[0]], writes=[Rpt[1]])
                s.op("dve", lambda e: e.tensor_tensor(pt(2), pt(0), pt(2), ALU.subtract), reads=[Rpt[0], Rpt[2]], writes=[Rpt[2]])
                s.op("act", lambda e: e.activation(pt(2), pt(2), AF.Exp), reads=[Rpt[2]], writes=[Rpt[2]])
                s.op("act", lambda e: e.activation(pt(0), pt(0), AF.Exp, scale=-1.0), reads=[Rpt[0]], writes=[Rpt[0]])
                yield
                pcol = 127 if d == 0 else 0
                s.op("dve", lambda e, c=c, pcol=pcol: e.tensor_copy(pcs[:, par, c:c + 1], pt(1)[:, pcol:pcol + 1]),
                     reads=[Rpt[1]], writes=[Rpcs[par]])
                yield
                shift(4 + c, ti, pt(4), Rpt[4])
                s.op("dve", lambda e, c=c: e.tensor_scalar(pt(5), pt(4), cp("kvec0", c), None, ALU.mult),
                     reads=[Rpt[4], Rcolp], writes=[Rpt[5]])
                s.op("act", lambda e: e.activation(pt(6), pt(5), AF.Square), reads=[Rpt[5]], writes=[Rpt[6]])
                b = bank()
                s.op("pe", lambda e, b=b: e.matmul(ps[b][:, 0:128], bones_f[:], pt(6), start=True, stop=True),
                     reads=[Rconst, Rpt[6]], writes=[Rps[b]])
                s.op("dve", lambda e, b=b: e.tensor_scalar(pt(7), ps[b][:, 0:128], 1e-12, None, ALU.add),
                     reads=[Rps[b]], writes=[Rpt[7]])
                s.op("act", lambda e: e.activation(pt(7), pt(7), AF.Ln), reads=[Rpt[7]], writes=[Rpt[7]])
                s.op("act", lambda e: e.activation(pt(7), pt(7), AF.Exp, scale=-0.5), reads=[Rpt[7]], writes=[Rpt[7]])
                s.op("dve", lambda e: e.tensor_tensor(pt(8), pt(5), pt(7), ALU.mult), reads=[Rpt[5], Rpt[7]], writes=[Rpt[8]])
                yield
                s.op("dve", lambda e, c=c: e.scalar_tensor_tensor(TB["AH"][:, c, :], pt(8), -1.0, pt(2), ALU.mult, ALU.mult),
                     reads=[Rpt[8], Rpt[2]], writes=[TB["RAH"]])
                s.op("dve", lambda e: e.tensor_tensor(pt(6), pt(8), pt(3), ALU.mult), reads=[Rpt[8], Rpt[3]], writes=[Rpt[6]])
                s.op("dve", lambda e, c=c: e.tensor_tensor(TB["BH"][:, c, :], pt(6), pt(0), ALU.mult),
                     reads=[Rpt[6], Rpt[0]], writes=[TB["RBH"]])
                yield
                s.op("dve", lambda e, c=c: e.tensor_scalar(pt(9), pt(3), cp("kvec1", c), cx("omk1", c), ALU.mult, ALU.add),
                     reads=[Rpt[3], Rcolp, Rcolx], writes=[Rpt[9]])
                s.op("dve", lambda e: e.tensor_tensor(pt(9), pt(4), pt(9), ALU.mult), reads=[Rpt[4], Rpt[9]], writes=[Rpt[9]])
                s.op("dve", lambda e, c=c: e.tensor_tensor(TB["KH"][:, c, :], pt(9), pt(0), ALU.mult),
                     reads=[Rpt[9], Rpt[0]], writes=[TB["RKH"]])
                yield
                shift(c, ti, pt(10), Rpt[10])
                s.op("dve", lambda e, c=c: e.tensor_tensor(TB["RH"][:, c, :], pt(10), pt(1), ALU.mult),
                     reads=[Rpt[10], Rpt[1]], writes=[TB["RRH"]])
                yield
                s.op("dve", lambda e: e.tensor_tensor(pt(6), pt(10), pt(9), ALU.mult), reads=[Rpt[10], Rpt[9]], writes=[Rpt[6]])
                s.op("dve", lambda e, c=c: e.tensor_scalar(pt(6), pt(6), cp("kvec2", c), None, ALU.mult),
                     reads=[Rpt[6], Rcolp], writes=[Rpt[6]])
                b = bank()
                s.op("pe", lambda e, b=b: e.matmul(ps[b][:, 0:2], pt(6), halfsel_f[:, 0:2], start=True, stop=True),
                     reads=[Rpt[6], Rconst], writes=[Rps[b]])
                s.op("dve", lambda e, b=b, c=c, ti=ti: e.tensor_tensor(BS[:, ti, c * 2:c * 2 + 2], BS[:, ti, c * 2:c * 2 + 2], ps[b][:, 0:2], ALU.add),
                     reads=[Rps[b], RBS], writes=[RBS])
                yield

        def rwkv_chunked(ntiles=NT):
            cut = int(os.environ.get("CUT", 99))
            for m in range(4):
                for q in range(4):
                    s.dma("pool", masks[:, m, q, :], cmat[4 + m], writes=[Rmask])
            s.op("dve", lambda e: e.memset(pt(11), 1.0), writes=[Rpt[11]])
            MSK = {0: (0, 1, 2), 1: (1, 0, 3)}
            seq = []
            for d in range(2):
                order = list(range(NT)) if d == 0 else list(range(NT - 1, -1, -1))
                seq += [(d, ti) for ti in order[:ntiles]]
            for _ in prep_chunk(seq[0][0], seq[0][1], 0):
                pass
            if True:
                def tile_body(n):
                    d, ti = seq[n]
                    par = n % 2
                    Ad = Ast[:, d * 256:(d + 1) * 256]
                    Ad3 = Ad.rearrange("p (c i) -> p c i", c=4)
                    Abd = Abf[:, d * 256:(d + 1) * 256]
                    Abd3 = Abd.rearrange("p (c i) -> p c i", c=4)
                    mTs, mS, mTi = MSK[d]
                    TB = TBL[par]
                    nxt = prep_chunk(seq[n + 1][0], seq[n + 1][1], (n + 1) % 2) if n + 1 < len(seq) else iter(())
                    u = ti // 2
                    isstart = (ti % 2 == 0) if d == 0 else (ti % 2 == 1)
                    if isstart:
                        if d == 0:
                            if u == 0:
                                s.dma("sp", Ad, initS[0], writes=[RAst])
                            else:
                                s.op("dve", lambda e, u=u: e.tensor_scalar(Ad, Ad, cx("keepL", u), None, ALU.mult),
                                     reads=[RAst, Rcolx], writes=[RAst])
                        else:
                            if u == 4:
                                s.op("dve", lambda e: e.memset(Ad, 0.0), writes=[RAst])
                            elif u == 3:
                                s.dma("sp", stage_f[:, 0, 0:256], initS[1], writes=[Rstage[0]])
                                s.op("dve", lambda e, u=u: e.scalar_tensor_tensor(Ad, Ad, cx("keepR", u), stage_f[:, 0, 0:256], ALU.mult, ALU.add),
                                     reads=[RAst, Rcolx, Rstage[0]], writes=[RAst])
                            else:
                                s.op("dve", lambda e, u=u: e.tensor_scalar(Ad, Ad, cx("keepR", u), None, ALU.mult),
                                     reads=[RAst, Rcolx], writes=[RAst])
                        s.op("act", lambda e: e.copy(Abd, Ad), reads=[RAst], writes=[RAbf])
                    if cut < 2:
                        return
                    fy = bank()
                    reserved.add(fy)
                    first_fy = [True]
                    def group_body(g):
                        W = WS[g]
                        heads = [(c_, g) for c_ in range(4)]

                        def fm(T3, c, h2):
                            return T3[64 * h2:64 * h2 + 64, c, :]

                        def q4(tl):
                            return tl.rearrange("p (q t) -> p q t", q=4)
                        specs = [("Lt0", "BH", "AH", mTs), ("L0", "AH", "BH", mS), ("Lakt", "KH", "AH", mTs),
                                 ("Mrbt", "BH", "RH", mTi), ("Mrkt", "KH", "RH", mTi)]
                        lim = int(os.environ.get("LIM", 99))
                        for (dst, la, rb, mk) in specs[:lim]:
                            b = bank()
                            for q, (c, h2) in enumerate(heads[:int(os.environ.get("LIMH", 4))]):
                                s.op("pe", lambda e, b=b, q=q, c=c, h2=h2, la=la, rb=rb: e.matmul(
                                    ps[b][:, q * 128:(q + 1) * 128], fm(TB[la], c, h2), fm(TB[rb], c, h2), start=True, stop=True),
                                    reads=[TB["R" + la], TB["R" + rb]], writes=[Rps[b]])
                            if os.environ.get("NOEV") == "1":
                                continue
                            if os.environ.get("NOEV") == "2":
                                s.op("dve", lambda e, b=b, dst=dst, mk=mk: e.tensor_copy(W[dst], ps[b][:]),
                                     reads=[Rps[b], Rmask], writes=[W["R" + dst]])
                                continue
                            s.op("dve", lambda e, b=b, dst=dst, mk=mk: e.tensor_tensor(
                                q4(W[dst]), ps[b][:].rearrange("p (q t) -> p q t", q=4), masks[:, mk, :, :], ALU.mult),
                                reads=[Rps[b], Rmask], writes=[W["R" + dst]])
                        yield
                        if cut < 3:
                            return
                        b = bank()
                        for q, (c, h2) in enumerate(heads):
                            idn = ident_b[64 * h2:64 * h2 + 64, 64 * h2:64 * h2 + 64]
                            for k_, nm in enumerate(["AH", "BH", "KH"]):
                                s.op("pe", lambda e, b=b, q=q, c=c, h2=h2, nm=nm, k_=k_, idn=idn: e.transpose(
                                    psb[b][:, k_ * 256 + q * 64:k_ * 256 + q * 64 + 64], fm(TB[nm], c, h2), idn),
                                    reads=[TB["R" + nm], Rconst], writes=[Rps[b]])
                        Z0q = q4(W["Z0"])
                        BKq = q4(W["BKtm"])
                        s.op("act", lambda e, b=b, Z0q=Z0q: e.copy(Z0q[:, :, 0:64], psb[b][:, 0:256].rearrange("p (q j) -> p q j", q=4)),
                             reads=[Rps[b]], writes=[W["RZ0"]])
                        s.op("act", lambda e, b=b, BKq=BKq: e.copy(BKq[:, :, 0:64], psb[b][:, 256:512].rearrange("p (q j) -> p q j", q=4)),
                             reads=[Rps[b]], writes=[W["RBKtm"]])
                        s.op("act", lambda e, b=b, BKq=BKq: e.copy(BKq[:, :, 64:128], psb[b][:, 512:768].rearrange("p (q j) -> p q j", q=4)),
                             reads=[Rps[b]], writes=[W["RBKtm"]])

                        def Vq(c, h2):
                            return V_tm[:, ti, (c * 2 + h2) * 64:(c * 2 + h2) * 64 + 64]
                        if cut < 4:
                            return
                        b = bank()
                        Lakq = q4(W["Lakt"])
                        for q, (c, h2) in enumerate(heads):
                            s.op("pe", lambda e, b=b, q=q, c=c, h2=h2: e.matmul(
                                ps[b][:, q * 64:(q + 1) * 64], Lakq[:, q, :], Vq(c, h2), start=True, stop=True),
                                reads=[W["RLakt"], RVtm], writes=[Rps[b]])
                        s.op("act", lambda e, b=b, Z0q=Z0q: e.copy(Z0q[:, :, 64:128], ps[b][:, 0:256].rearrange("p (q j) -> p q j", q=4)),
                             reads=[Rps[b]], writes=[W["RZ0"]])
                        yield
                        if cut < 5:
                            return
                        zc, lc = 0, 0
                        for k in range(7):
                            Zc, Zn = W["Z%d" % zc], W["Z%d" % (1 - zc)]
                            RZc, RZn = W["RZ%d" % zc], W["RZ%d" % (1 - zc)]
                            Ltc, Lc = W["Lt%d" % lc], W["L%d" % lc]
                            RLtc, RLc = W["RLt%d" % lc], W["RL%d" % lc]
                            b = bank()
                            for q in range(4):
                                s.op("pe", lambda e, b=b, q=q, Zc=Zc, Ltc=Ltc: e.matmul(
                                    ps[b][:, q * 128:(q + 1) * 128], q4(Ltc)[:, q, :], q4(Zc)[:, q, :], start=True, stop=True),
                                    reads=[RZc, RLtc], writes=[Rps[b]])
                            s.op("dve", lambda e, b=b, Zn=Zn, Zc=Zc: e.tensor_tensor(Zn, ps[b][:], Zc, ALU.add),
                                 reads=[Rps[b], RZc], writes=[RZn])
                            if k < 6:
                                Ltn, Ln = W["Lt%d" % (1 - lc)], W["L%d" % (1 - lc)]
                                RLtn, RLn = W["RLt%d" % (1 - lc)], W["RL%d" % (1 - lc)]
                                b2 = bank()
                                for q in range(4):
                                    s.op("pe", lambda e, b2=b2, q=q, Lc=Lc, Ltc=Ltc: e.matmul(
                                        ps[b2][:, q * 128:(q + 1) * 128], q4(Lc)[:, q, :], q4(Ltc)[:, q, :], start=True, stop=True),
                                        reads=[RLc, RLtc], writes=[Rps[b2]])
                                s.op("act", lambda e, b2=b2, Ltn=Ltn: e.copy(Ltn, ps[b2][:]), reads=[Rps[b2]], writes=[RLtn])
                                if k < 5:
                                    b3 = bank()
                                    for q in range(4):
                                        s.op("pe", lambda e, b3=b3, q=q, Lc=Lc, Ltc=Ltc: e.matmul(
                                            ps[b3][:, q * 128:(q + 1) * 128], q4(Ltc)[:, q, :], q4(Lc)[:, q, :], start=True, stop=True),
                                            reads=[RLc, RLtc], writes=[Rps[b3]])
                                    s.op("act", lambda e, b3=b3, Ln=Ln: e.copy(Ln, ps[b3][:]), reads=[Rps[b3]], writes=[RLn])
                                lc = 1 - lc
                            zc = 1 - zc
                            yield
                        s.op("act", lambda e, zc=zc: e.copy(W["Zfb"], W["Z%d" % zc]), reads=[W["RZ%d" % zc]], writes=[W["RZfb"]])
                        Zf = q4(W["Zfb"])
                        RZf = W["RZfb"]
                        Mrbq = q4(W["Mrbt"])
                        Mrkq = q4(W["Mrkt"])
                        if cut < 6:
                            return
                        bM = bank()
                        bN = bank()
                        bR = bank()
                        for q, (c, h2) in enumerate(heads):
                            cl_ = c
                            prt = slice(64 * h2, 64 * h2 + 64)
                            s.op("pe", lambda e, q=q, cl_=cl_, prt=prt: e.matmul(
                                ps[bM][prt, cl_ * 64:(cl_ + 1) * 64], Zf[:, q, 0:64], BKq[:, q, 0:64],
                                start=(q == 0), stop=False, skip_group_check=True),
                                reads=[RZf, W["RBKtm"]], writes=[Rps[bM]])
                            s.op("pe", lambda e, q=q, cl_=cl_, prt=prt: e.matmul(
                                ps[bM][prt, cl_ * 64:(cl_ + 1) * 64], ident_b[0:64, 0:64], ident_b[0:64, 0:64],
                                start=False, stop=True, skip_group_check=True),
                                reads=[Rconst], writes=[Rps[bM]])
                            s.op("pe", lambda e, q=q, cl_=cl_, prt=prt: e.matmul(
                                ps[bN][prt, cl_ * 64:(cl_ + 1) * 64], BKq[:, q, 0:64], Zf[:, q, 64:128],
                                start=(q == 0), stop=False, skip_group_check=True),
                                reads=[RZf, W["RBKtm"]], writes=[Rps[bN]])
                            s.op("pe", lambda e, q=q, cl_=cl_, prt=prt, c=c, h2=h2: e.matmul(
                                ps[bN][prt, cl_ * 64:(cl_ + 1) * 64], BKq[:, q, 64:128], Vq(c, h2),
                                start=False, stop=False, skip_group_check=True),
                                reads=[W["RBKtm"], RVtm], writes=[Rps[bN]])
                            s.op("pe", lambda e, q=q, cl_=cl_, prt=prt: e.matmul(
                                ps[bR][prt, cl_ * 128:(cl_ + 1) * 128], Zf[:, q, 0:64], Mrbq[:, q, :],
                                start=(q == 0), stop=False, skip_group_check=True),
                                reads=[RZf, W["RMrbt"]], writes=[Rps[bR]])
                            s.op("pe", lambda e, q=q, cl_=cl_, prt=prt, c=c, h2=h2: e.matmul(
                                ps[bR][prt, cl_ * 128:(cl_ + 1) * 128], ident_b[prt, prt], fm(TB["RH"], c, h2),
                                start=False, stop=True, skip_group_check=True),
                                reads=[Rconst, TB["RRH"]], writes=[Rps[bR]])
                            st_ = first_fy[0]
                            first_fy[0] = False
                            hh = c * 2 + h2
                            s.op("pe", lambda e, q=q, hh=hh, st_=st_: e.matmul(
                                ps[fy][:, hh * 64:(hh + 1) * 64], Mrbq[:, q, :], Zf[:, q, 64:128],
                                start=st_, stop=False, skip_group_check=True),
                                reads=[RZf, W["RMrbt"]], writes=[Rps[fy]])
                            s.op("pe", lambda e, q=q, hh=hh, c=c, h2=h2: e.matmul(
                                ps[fy][:, hh * 64:(hh + 1) * 64], Mrkq[:, q, :], Vq(c, h2),
                                start=False, stop=False, skip_group_check=True),
                                reads=[W["RMrkt"], RVtm], writes=[Rps[fy]])
                        MTv = W["MT"].rearrange("p (c j) -> p c j", c=4)
                        RTv = W["RT"].rearrange("p (c t) -> p c t", c=4)
                        pg = slice(64 * g, 64 * g + 64)
                        s.op("act", lambda e, MTv=MTv, pg=pg: e.copy(MTv[pg], ps[bM][pg, 0:256].rearrange("p (c j) -> p c j", c=4)),
                             reads=[Rps[bM]], writes=[W["RMT"]])
                        s.op("act", lambda e, RTv=RTv, pg=pg: e.copy(RTv[pg], ps[bR][pg, 0:512].rearrange("p (c t) -> p c t", c=4)),
                             reads=[Rps[bR]], writes=[W["RRT"]])
                        for q, (c, h2) in enumerate(heads):
                            cl_ = c
                            prt = slice(64 * h2, 64 * h2 + 64)
                            hh = c * 2 + h2
                            s.op("pe", lambda e, cl_=cl_, prt=prt, hh=hh, c=c, RTv=RTv: e.matmul(
                                ps[fy][:, hh * 64:(hh + 1) * 64], RTv[prt, cl_, :], Abd3[prt, c, :],
                                start=False, stop=True, skip_group_check=True),
                                reads=[W["RRT"], RAbf], writes=[Rps[fy]])
                            s.op("pe", lambda e, cl_=cl_, prt=prt, c=c, MTv=MTv: e.matmul(
                                ps[bN][prt, cl_ * 64:(cl_ + 1) * 64], MTv[prt, cl_, :], Abd3[prt, c, :],
                                start=False, stop=True, skip_group_check=True),
                                reads=[W["RMT"], RAbf], writes=[Rps[bN]])
                        s.op("dve", lambda e, pg=pg: e.tensor_tensor(
                            Ad3[pg, :, :], ps[bN][pg, 0:256].rearrange("p (c i) -> p c i", c=4),
                            pcs[pg, par, :].unsqueeze(2).to_broadcast([64, 4, 64]), ALU.mult),
                            reads=[Rps[bN], Rpcs[par], RAbf], writes=[RAst])
                    gens = [group_body(0), group_body(1), nxt]
                    while gens:
                        for gn in list(gens):
                            try:
                                next(gn)
                            except StopIteration:
                                gens.remove(gn)
                    if cut < 6:
                        reserved.discard(fy)
                        return
                    s.op("act", lambda e: e.copy(Abd, Ad), reads=[RAst], writes=[RAbf])
                    isend = (ti % 2 == 1) if d == 0 else (ti % 2 == 0)
                    if isend:
                        s.dma("sp", s_out[u, d], Ad, reads=[RAst])
                    reserved.discard(fy)
                    if d == 0:
                        s.op("act", lambda e, ti=ti: e.copy(YSb[:, ti, :], ps[fy][:]), reads=[Rps[fy]], writes=[RYS])
                    else:
                        finalize(ti, fy)
                for n_ in range(len(seq)):
                    tile_body(n_)

        if stage >= 1:
            if "a" in sub:
                make_gg(0)
            if "b" in sub:
                make_hT(0, 0)
            if "c" in sub:
                even_inproj()
        if stage >= 3:
            attention()
            barrier()
            if stage >= 5:
                rwkv_setup()
                rwkv_prepass()
                rwkv_chunked(NT if stage >= 6 else 2)
            else:
                s.op("pool", lambda e: e.memset(yT[:, 4:8, :], 0.0), writes=[RyT])
            barrier()
            out_proj(w_out_even, yT, RyT)
        if stage >= 2:
            barrier()
            ffn(0)
            barrier()
        if stage >= 4:
            make_gg(1)
            make_hT(1, 0)
            fnet()
            out_proj(w_out_odd, yT, RyT)
            barrier()
            ffn(1)
        for i in range(NT):
            s.dma("sp", y_out[i * 128:(i + 1) * 128, :], x_sb[:, i, :], reads=[Rx[i]])
        s.emit()
    return nc


def _core_units(c):
    if c < 6:
        return [("p", 5 * c + u) for u in range(5)]
    b = c - 6
    return [("s", b, u) for u in range(4)] + [("p", 30 + b)]


def _host_prep(inp):
    f32 = np.float32
    COFF, NCOL = colp_layout()
    ROFF, NROW = rowp_layout()
    g = {k: np.asarray(v) for k, v in inp.items()}
    colp = np.zeros((128, NCOL), f32)

    def put(name, vec):
        c = cols_of(vec)
        colp[:, COFF[name]:COFF[name] + c.shape[1]] = c
    for l in range(2):
        put("bada%d" % l, g["b_ada"][l])
        for k in range(4):
            put("gain%d_%d" % (l, k), g["norm_gains"][l, k])
        for k in range(3):
            put("conv%d_%d" % (l, k), g["ffn_conv"][l, k])
        put("convb%d" % l, g["ffn_conv_b"][l])
    put("mu0", g["rwkv_shift_mu"][0, 0])
    put("mu1", g["rwkv_shift_mu"][0, 1])
    for d in range(2):
        put("w0_%d" % d, g["rwkv_w0"][0, d])
        put("a0_%d" % d, g["rwkv_a0"][0, d])
    for k in range(3):
        put("kvec%d" % k, g["rwkv_kvec"][0, k])

    cmat = np.zeros((8, 128, 128), f32)
    cmat[0] = np.eye(128, dtype=f32)
    for i in range(64):
        cmat[1][2 * i, 2 * i + 1] = 1.0
        cmat[1][2 * i + 1, 2 * i] = -1.0
    cmat[2][:64, :64] = 1.0
    cmat[2][64:, 64:] = 1.0
    cmat[3][:64, 0] = 1.0
    cmat[3][64:, 1] = 1.0
    rr_, cc_ = np.meshgrid(np.arange(128), np.arange(128), indexing="ij")
    cmat[4] = (rr_ < cc_)
    cmat[5] = (rr_ > cc_)
    cmat[6] = (rr_ <= cc_)
    cmat[7] = (rr_ >= cc_)
    cc = np.arange(128)
    chang = 2.0 * np.pi * ((cc[:, None] * cc[None, :]) % 128) / 128.0
    chCS = np.concatenate([np.cos(chang), np.sin(chang)], axis=1).astype(f32)

    inv = (10000.0 ** (-np.arange(16, dtype=np.float32) / 16)).astype(f32)

    shared = dict(
        colp=colp, w_ada=g["w_ada"], w_in_even=g["w_in_even"][0], w_out_even=g["w_out_even"][0],
        w_out_odd=g["w_out_odd"][0], w_ffn_in=g["w_ffn_in"], w_ffn_out=g["w_ffn_out"],
        w2t=np.ascontiguousarray(g["rwkv_w2"][0].reshape(128, 512)),
        a2t=np.ascontiguousarray(g["rwkv_a2"][0].reshape(128, 512)),
        g2=np.ascontiguousarray(g["rwkv_g2"][0]), chCS=chCS, cmat=cmat,
        lnx=np.ascontiguousarray(g["rwkv_lnx"][0]))
    maps = []
    for c in range(NCORES):
        units = _core_units(c)
        xs = []
        for un in units:
            if un[0] == "p":
                xs.append(g["x_prompt"][un[1]])
            else:
                xs.append(g["x_sample"][un[1], un[2] * 256:(un[2] + 1) * 256])
        x_in = np.ascontiguousarray(np.concatenate(xs, axis=0), dtype=f32)
        is_s = c >= 6
        condA = g["c"][c - 6] if is_s else g["c_ctx"]
        condB = g["c_ctx"]
        cond = np.stack([condA, condB], axis=0).astype(f32)
        condT = np.ascontiguousarray(cond.reshape(2, 8, 128).transpose(2, 1, 0).reshape(128, 16))
        rowp = np.zeros((1, NROW), f32)
        rowp[0, ROFF["lam"]:ROFF["lam"] + 256] = g["diff_lambda"][0].reshape(-1)
        rowp[0, ROFF["subln"]:ROFF["subln"] + 128] = g["diff_subln"][0]
        ab = np.full((6, 5), -30000.0, f32)
        if is_s:
            ab[0:5, 0:4] = 0.0
            ab[5, 4] = 0.0
            bndL = [1, 0, 0, 0, 1]
            bndR = [0, 0, 0, 1, 1]
        else:
            for u in range(5):
                ab[u + 1, u] = 0.0
            bndL = [1] * 5
            bndR = [1] * 5
        rowp[0, ROFF["abias"]:ROFF["abias"] + 30] = ab.reshape(-1)
        rowp[0, ROFF["bndL"]:ROFF["bndL"] + 5] = bndL
        rowp[0, ROFF["bndR"]:ROFF["bndR"] + 5] = bndR
        ropeC = np.ones((128, T), f32)
        ropeS = np.zeros((128, T), f32)
        if is_s:
            t = np.arange(1024)
            row = (t // 64).astype(f32)
            col = (t % 64).astype(f32)
            ang = np.concatenate([row[:, None] * inv[None, :], col[:, None] * inv[None, :]], axis=1).astype(f32)
            pidx = (np.arange(128) % 64) // 2
            ropeC[:, :1024] = np.cos(ang)[:, pidx].T
            ropeS[:, :1024] = np.sin(ang)[:, pidx].T
        cacheKT = np.zeros((512, 256), f32)
        cacheV = np.zeros((256, 512), f32)
        initS = np.zeros((2, 128, 256), f32)
        if is_s:
            b = c - 6
            cacheKT[:] = g["cache_k"][b, 0].reshape(256, 512).T
            cacheV[:] = g["cache_v"][b, 0].reshape(256, 512)
            st = g["state_wkv"][b, 0]
            initS[:] = st.reshape(2, 4, 2, 64, 64).transpose(0, 2, 4, 1, 3).reshape(2, 128, 256)
        dC = np.zeros((T, T), np.float64)
        dS = np.zeros((T, T), np.float64)
        blocks = [(0, 1024), (1024, 256)] if is_s else [(256 * u, 256) for u in range(5)]
        for (a0, L) in blocks:
            ll = np.arange(L)
            ang = 2.0 * np.pi * ((ll[:, None] * ll[None, :]) % L) / L
            sc = 1.0 / math.sqrt(L * 128.0)
            dC[a0:a0 + L, a0:a0 + L] = np.cos(ang) * sc
            dS[a0:a0 + L, a0:a0 + L] = -np.sin(ang) * sc
        m = dict(shared)
        m.update(x_in=x_in, condT=condT, rowp=rowp, ropeC=ropeC, ropeS=ropeS, cacheKT=cacheKT, cacheV=cacheV,
                 initS=initS, dftC=dC.astype(f32), dftSn=dS.astype(f32))
        maps.append(m)
    return maps


_NC_CACHE = {}


def kernel(**inputs):
    maps = _host_prep(inputs)
    if "nc" not in _NC_CACHE:
        _NC_CACHE["nc"] = build()
    nc = _NC_CACHE["nc"]
    import os
    ncr = int(os.environ.get("NCR", NCORES))
    res = run_bass_kernel_spmd(nc, maps[:ncr], core_ids=list(range(ncr)))
    outs = list(res.results) + [res.results[0]] * (NCORES - ncr)
    y_prompt = np.zeros((32, 256, D), np.float32)
    y_sample = np.zeros((2, 1024, D), np.float32)
    nk = np.zeros((32, 1, 256, 4, 128), np.float32)
    nv = np.zeros((32, 1, 256, 4, 128), np.float32)
    ns = np.zeros((32, 1, 2, 8, 64, 64), np.float32)
    for c in range(NCORES):
        r = outs[c]
        for u, un in enumerate(_core_units(c)):
            sl = slice(u * 256, (u + 1) * 256)
            if un[0] == "p":
                bi = un[1]
                y_prompt[bi] = r["y_out"][sl]
                nk[bi, 0] = r["k_out"][sl].reshape(256, 4, 128)
                nv[bi, 0] = r["v_out"][sl].reshape(256, 4, 128)
                stt = r["s_out"][u]
                ns[bi, 0] = stt.reshape(2, 2, 64, 4, 64).transpose(0, 3, 1, 4, 2).reshape(2, 8, 64, 64)
            else:
                y_sample[un[1], un[2] * 256:(un[2] + 1) * 256] = r["y_out"][sl]
    return (y_prompt, y_sample, nk, nv, ns)
```

```python
import contextlib
import math
import numpy as np
import concourse.bass as bass
import concourse.mybir as mybir
from concourse.bass_utils import run_bass_kernel_spmd

F32 = mybir.dt.float32
BF16 = mybir.dt.bfloat16
AF = mybir.ActivationFunctionType
ALU = mybir.AluOpType
AX = mybir.AxisListType

T = 1280
NT = 10
U = 5
D = 1024
KC = 8
DFF = 2816
FC = 22
NCORES = 8
ARENA_W = 27200
EXPM05 = math.exp(-0.5)


class Res:
    __slots__ = ("name", "writer", "readers", "excl")

    def __init__(self, name, excl=False):
        self.name = name
        self.writer = None
        self.readers = []
        self.excl = excl


class Sched:
    ENGS = ("pe", "act", "dve", "pool", "sp")
    NDMA = 6

    def __init__(self, nc):
        self.nc = nc
        self.prog = {e: [] for e in self.ENGS}
        self.signal = {e: set() for e in self.ENGS}
        self.ndma = {e: 0 for e in self.ENGS}

    def _collect(self, reads, writes, eng=None):
        deps = []
        for r in reads:
            if r.writer is not None:
                deps.append(r.writer)
            if r.excl:
                deps.extend(t for t in r.readers if t[1] != eng)
        for w in writes:
            if w.writer is not None:
                deps.append(w.writer)
            deps.extend(w.readers)
        return deps

    def _commit(self, tok, reads, writes):
        for r in reads:
            r.readers.append(tok)
        for w in writes:
            w.writer = tok
            w.readers = []

    def op(self, eng, fn, reads=(), writes=()):
        deps = self._collect(reads, writes, eng)
        idx = len(self.prog[eng])
        if eng == "pe":
            deps = [d for d in deps if not (d[0] == "c" and d[1] == "pe")]
        for d in deps:
            if d[0] == "c":
                self.signal[d[1]].add(d[2])
        self.prog[eng].append(dict(fn=fn, deps=deps, kind="c"))
        tok = ("c", eng, idx)
        self._commit(tok, reads, writes)
        return tok

    def dma(self, eng, out, in_, reads=(), writes=()):
        deps = self._collect(reads, writes, eng)
        n = self.ndma[eng]
        self.ndma[eng] += 1
        if n >= self.NDMA:
            deps.append(("d", eng, n - self.NDMA))
        for d in deps:
            if d[0] == "c":
                self.signal[d[1]].add(d[2])
        self.prog[eng].append(dict(out=out, in_=in_, deps=deps, kind="d", n=n))
        tok = ("d", eng, n)
        self._commit(tok, reads, writes)
        return tok

    def emit(self):
        nc = self.nc
        with contextlib.ExitStack() as st:
            csem = {e: st.enter_context(nc.semaphore("c_" + e)) for e in self.ENGS}
            dsem = {e: [st.enter_context(nc.semaphore("d_%s_%d" % (e, i))) for i in range(self.NDMA)]
                    for e in self.ENGS if self.ndma[e] > 0}
            sigval = {}
            for e in self.ENGS:
                cnt = 0
                m = {}
                for i in range(len(self.prog[e])):
                    if i in self.signal[e]:
                        cnt += 1
                        m[i] = cnt
                sigval[e] = m

            def resolve(tok):
                if tok[0] == "c":
                    return csem[tok[1]], sigval[tok[1]][tok[2]], ("c", tok[1])
                e, n = tok[1], tok[2]
                return dsem[e][n % self.NDMA], 16 * (n // self.NDMA + 1), ("d", e, n % self.NDMA)

            def run_engine(e, h):
                waited = {}
                for i, ins in enumerate(self.prog[e]):
                    need = {}
                    for d in ins["deps"]:
                        sem, val, key = resolve(d)
                        if waited.get(key, 0) >= val:
                            continue
                        if key not in need or need[key][1] < val:
                            need[key] = (sem, val)
                    for key, (sem, val) in need.items():
                        h.wait_ge(sem, val)
                        waited[key] = val
                    if ins["kind"] == "c":
                        bi = ins["fn"](h)
                        if i in self.signal[e]:
                            bi.then_inc(csem[e], 1)
                    else:
                        n = ins["n"]
                        h.dma_start(out=ins["out"], in_=ins["in_"]).then_inc(dsem[e][n % self.NDMA], 16)
                if self.ndma[e] > 0:
                    n = self.ndma[e]
                    for slot in range(self.NDMA):
                        cnt = (n - slot + self.NDMA - 1) // self.NDMA if n > slot else 0
                        if cnt > 0:
                            h.wait_ge(dsem[e][slot], 16 * cnt)

            with nc.Block() as block:
                @block.tensor
                def _(eng):
                    run_engine("pe", eng)

                @block.scalar
                def _(eng):
                    run_engine("act", eng)

                @block.vector
                def _(eng):
                    run_engine("dve", eng)

                @block.gpsimd
                def _(eng):
                    run_engine("pool", eng)

                @block.sync
                def _(eng):
                    run_engine("sp", eng)


def colp_layout():
    off = {}
    n = 0

    def add(name, cols):
        nonlocal n
        off[name] = n
        n += cols
    for l in range(2):
        add("bada%d" % l, 48)
        for g in range(4):
            add("gain%d_%d" % (l, g), 8)
        for k in range(3):
            add("conv%d_%d" % (l, k), FC)
        add("convb%d" % l, FC)
    add("mu0", 15)
    add("mu1", 15)
    for d in range(2):
        add("w0_%d" % d, 4)
        add("a0_%d" % d, 4)
    for k in range(3):
        add("kvec%d" % k, 4)
    return off, n


def rowp_layout():
    off = {}
    n = 0

    def add(name, cols):
        nonlocal n
        off[name] = n
        n += cols
    add("lam", 256)
    add("subln", 128)
    add("abias", 30)
    add("bndL", 5)
    add("bndR", 5)
    return off, n


def cols_of(vec):
    v = np.asarray(vec, np.float32).reshape(-1, 128)
    return np.ascontiguousarray(v.T)


def build(stage=99):
    nc = bass.Bass("TRN2", target_bir_lowering=False)
    COFF, NCOL = colp_layout()
    ROFF, NROW = rowp_layout()

    def din(name, shape):
        return nc.dram_tensor(name, list(shape), F32, kind="ExternalInput").ap()

    def dout(name, shape):
        return nc.dram_tensor(name, list(shape), F32, kind="ExternalOutput").ap()

    x_in = din("x_in", [T, D])
    condT = din("condT", [128, 16])
    colp = din("colp", [128, NCOL])
    rowp = din("rowp", [1, NROW])
    w_ada = din("w_ada", [2, D, 6 * D])
    w_in_even = din("w_in_even", [D, 3456])
    w_out_even = din("w_out_even", [D, D])
    w_out_odd = din("w_out_odd", [D, D])
    w_ffn_in = din("w_ffn_in", [2, D, 2 * DFF])
    w_ffn_out = din("w_ffn_out", [2, DFF, D])
    w2t_d = din("w2t", [128, 512])
    a2t_d = din("a2t", [128, 512])
    g2_d = din("g2", [128, 512])
    cacheKT = din("cacheKT", [512, 256])
    cacheV = din("cacheV", [256, 512])
    ropeC = din("ropeC", [128, T])
    ropeS = din("ropeS", [128, T])
    initS = din("initS", [2, 128, 256])
    dftC = din("dftC", [T, T])
    dftSn = din("dftSn", [T, T])
    chCS = din("chCS", [128, 256])
    cmat = din("cmat", [8, 128, 128])
    lnx_d = din("lnx", [2, 512])

    y_out = dout("y_out", [T, D])
    k_out = dout("k_out", [T, 512])
    v_out = dout("v_out", [T, 512])
    s_out = dout("s_out", [U, 2, 128, 256])

    import os
    sub = os.environ.get("SUB", "abcmqv")
    with contextlib.ExitStack() as st:
        s = Sched(nc)

        def sb(name, shape, dt=F32):
            return st.enter_context(nc.sbuf_tensor(name, list(shape), dt))

        x_sb = sb("x_sb", [128, NT, D])
        Rx = [Res("x%d" % i) for i in range(NT)]
        WB = 4096
        wbuf = [sb("wb%d" % i, [128, WB], BF16) for i in range(3)]
        Rw = [Res("wb%d" % i) for i in range(3)]
        wctr = [0]
        colp_sb = sb("colp_sb", [128, NCOL])
        Rcolp = Res("colp")
        rowp_sb = sb("rowp_sb", [128, NROW])
        Rrowp = Res("rowp")
        ident_f = sb("ident_f", [128, 128])
        ident_b = sb("ident_b", [128, 128], BF16)
        rotT_b = sb("rotT_b", [128, 128], BF16)
        bones_f = sb("bones_f", [128, 128])
        halfsel_f = sb("halfsel_f", [128, 128])
        bones_b = sb("bones_b", [128, 128], BF16)
        Rconst = Res("const")
        gg_sb = sb("gg_sb", [128, 2, 2, D], BF16)
        Rgg = Res("gg")
        scond = sb("scond", [128, 16], BF16)
        condf = sb("condf", [128, 16])
        Rcond = Res("cond")
        modcol = sb("modcol", [128, 2, 96])
        Rmod = Res("modcol")
        scsh = sb("scsh", [128, 2, 2, 2, 16])
        Rscsh = Res("scsh")
        stat = sb("stat", [128, 64])
        Rstat = Res("stat")
        junk_b = sb("junk_b", [128, D], BF16)
        Rjunk = Res("junk")
        xn_b = sb("xn_b", [128, D], BF16)
        Rxn = Res("xn")
        stage_f = sb("stage_f", [128, 2, 512])
        Rstage = [Res("stage0"), Res("stage1")]
        stctr = [0]
        arena = sb("arena", [128, ARENA_W])
        Rar = {}

        def ares(name):
            if name not in Rar:
                Rar[name] = Res("ar_" + name)
            return Rar[name]

        def carve(off_w, nelem, dt):
            if dt == F32:
                return arena[:, off_w:off_w + nelem]
            return arena[:, off_w:off_w + (nelem + 1) // 2].bitcast(BF16)[:, 0:nelem]

        ps = [st.enter_context(nc.psum_tensor("ps%d" % b, [128, 512], F32)) for b in range(8)]
        psb = [p.bitcast(BF16) for p in ps]
        Rps = [Res("ps%d" % b, excl=True) for b in range(8)]
        bctr = [0]

        reserved = set()

        def bank():
            while True:
                b = bctr[0] % 8
                bctr[0] += 1
                if b not in reserved:
                    return b

        def barrier():
            allr = list(Rar.values()) + list(Rw)
            s.op("dve", lambda e: e.memset(stat[:, 63:64], 0.0), writes=allr + [Rstat])

        def wload(src_ap, kc, ncols):
            i = wctr[0] % 3
            wctr[0] += 1
            view = wbuf[i][:, 0:kc * ncols].rearrange("p (c n) -> p c n", c=kc)
            s.dma("pool", view, src_ap.rearrange("(c p) n -> p c n", p=128), writes=[Rw[i]])
            return view, Rw[i]

        s.dma("sp", colp_sb[:], colp, writes=[Rcolp])
        s.dma("sp", rowp_sb[:], rowp[0, :].partition_broadcast(128), writes=[Rrowp])
        s.dma("sp", ident_f[:], cmat[0], writes=[Rconst])
        s.dma("sp", bones_f[:], cmat[2], writes=[Rconst])
        s.dma("sp", halfsel_f[:], cmat[3], writes=[Rconst])
        s.dma("pool", ident_b[:], cmat[0], writes=[Rconst])
        s.dma("pool", rotT_b[:], cmat[1], writes=[Rconst])
        s.dma("pool", bones_b[:], cmat[2], writes=[Rconst])
        s.dma("sp", condf[:], condT, writes=[Rcond])
        for i in range(NT):
            s.dma("sp", x_sb[:, i, :], x_in[i * 128:(i + 1) * 128, :], writes=[Rx[i]])
        s.op("act", lambda e: e.activation(scond[:], condf[:], AF.Silu), reads=[Rcond], writes=[Rcond])

        def cp(name, c=0, n=1):
            o = COFF[name] + c
            return colp_sb[:, o:o + n]

        def p0_gen(l):
            b = bank()
            reserved.add(b)
            for v in range(6):
                for half in range(2):
                    wv, rw = wload(w_ada[l][:, v * 1024 + half * 512: v * 1024 + half * 512 + 512], KC, 512)
                    for j in range(4):
                        col = (v * 8 + half * 4 + j) * 2
                        for kc in range(KC):
                            s.op("pe", lambda e, wv=wv, j=j, kc=kc, col=col, b=b: e.matmul(
                                ps[b][:, col:col + 2], wv[:, kc, j * 128:(j + 1) * 128],
                                scond[:, kc * 2:kc * 2 + 2], start=(kc == 0), stop=(kc == KC - 1)),
                                reads=[rw, Rcond], writes=[Rps[b]])
                    yield
            s.op("dve", lambda e, l=l, b=b: e.tensor_tensor(
                modcol[:, l, :].rearrange("p (c a) -> p c a", a=2),
                ps[b][:, 0:96].rearrange("p (c a) -> p c a", a=2),
                cp("bada%d" % l, 0, 48).unsqueeze(2).to_broadcast([128, 48, 2]), ALU.add),
                reads=[Rps[b], Rcolp], writes=[Rmod])
            reserved.discard(b)

            def mv(v, l=l):
                return modcol[:, l, v * 16:(v + 1) * 16].rearrange("p (c a) -> p c a", a=2)

            def gain(g, l=l):
                return cp("gain%d_%d" % (l, g), 0, 8).unsqueeze(2).to_broadcast([128, 8, 2])
            for wi, (vs, vsh, g) in enumerate([(1, 0, 0), (4, 3, 2)]):
                sc = scsh[:, l, wi, 0, :].rearrange("p (c a) -> p c a", a=2)
                sh = scsh[:, l, wi, 1, :].rearrange("p (c a) -> p c a", a=2)
                s.op("dve", lambda e, sc=sc, vs=vs, mv=mv: e.tensor_scalar(sc, mv(vs), 1.0, None, ALU.add),
                     reads=[Rmod], writes=[Rscsh])
                s.op("dve", lambda e, sc=sc, g=g, gain=gain: e.tensor_tensor(sc, sc, gain(g), ALU.mult),
                     reads=[Rscsh, Rcolp], writes=[Rscsh])
                s.op("dve", lambda e, sh=sh, vsh=vsh, mv=mv: e.tensor_copy(sh, mv(vsh)),
                     reads=[Rmod], writes=[Rscsh])

        for _ in p0_gen(0):
            pass
        P0L1 = [p0_gen(1)]

        ggcol = sb("ggcol", [128, 2, 16])
        Rggcol = Res("ggcol")

        def make_gg(l):
            for wi, (vg, g) in enumerate([(2, 1), (5, 3)]):
                gc = ggcol[:, wi, :].rearrange("p (c a) -> p c a", a=2)
                s.op("dve", lambda e, gc=gc, vg=vg, g=g, l=l: e.tensor_tensor(
                    gc, modcol[:, l, vg * 16:(vg + 1) * 16].rearrange("p (c a) -> p c a", a=2),
                    cp("gain%d_%d" % (l, g), 0, 8).unsqueeze(2).to_broadcast([128, 8, 2]), ALU.mult),
                    reads=[Rmod, Rcolp], writes=[Rggcol])
                for a in range(2):
                    for hh in range(2):
                        b = bank()
                        for j in range(4):
                            c = hh * 4 + j
                            s.op("pe", lambda e, b=b, wi=wi, c=c, a=a, j=j: e.matmul(
                                ps[b][:, j * 128:(j + 1) * 128],
                                ggcol[:, wi, c * 2 + a:c * 2 + a + 1].to_broadcast([128, 128]),
                                ident_f[:], start=True, stop=True),
                                reads=[Rggcol, Rconst], writes=[Rps[b]])
                        s.op("act", lambda e, b=b, wi=wi, a=a, hh=hh: e.copy(
                            gg_sb[:, wi, a, hh * 512:(hh + 1) * 512], ps[b][:]),
                            reads=[Rps[b]], writes=[Rgg])

        hT = carve(0, 8 * T, BF16).rearrange("p (c t) -> p c t", c=8)
        RhT = ares("hT")

        def ab_of_tile(i):
            return 0 if i < 8 else 1

        def make_hT(l, wi):
            for i in range(NT):
                a = ab_of_tile(i)
                s.op("act", lambda e, i=i: e.activation(junk_b[:], x_sb[:, i, :], AF.Square, scale=1.0 / 32.0,
                                                        accum_out=stat[:, 0:1]),
                     reads=[Rx[i]], writes=[Rjunk, Rstat])
                s.op("dve", lambda e: e.tensor_scalar(stat[:, 1:2], stat[:, 0:1], 1e-6, None, ALU.add),
                     reads=[Rstat], writes=[Rstat])
                s.op("act", lambda e: e.activation(stat[:, 1:2], stat[:, 1:2], AF.Sqrt), reads=[Rstat], writes=[Rstat])
                s.op("dve", lambda e: e.reciprocal(stat[:, 2:3], stat[:, 1:2]), reads=[Rstat], writes=[Rstat])
                s.op("dve", lambda e, i=i: e.tensor_scalar(xn_b[:], x_sb[:, i, :], stat[:, 2:3], None, ALU.mult),
                     reads=[Rx[i], Rstat], writes=[Rxn])
                b = bank()
                for c in range(8):
                    s.op("pe", lambda e, b=b, c=c: e.transpose(psb[b][:, c * 128:(c + 1) * 128],
                                                               xn_b[:, c * 128:(c + 1) * 128], ident_b[:]),
                         reads=[Rxn, Rconst], writes=[Rps[b]])
                for c in range(8):
                    s.op("act", lambda e, b=b, c=c, i=i, a=a: e.activation(
                        hT[:, c, i * 128:(i + 1) * 128], psb[b][:, c * 128:(c + 1) * 128], AF.Identity,
                        bias=scsh[:, l, wi, 1, c * 2 + a:c * 2 + a + 1],
                        scale=scsh[:, l, wi, 0, c * 2 + a:c * 2 + a + 1]),
                        reads=[Rps[b], Rscsh], writes=[RhT])

        TOKCH = [(0, 512), (512, 512), (1024, 256)]

        def linear_fm(wv, rw, ncol0, actT, Ract, kc_n, evac):
            for (t0, tn) in TOKCH:
                b = bank()
                for kc in range(kc_n):
                    s.op("pe", lambda e, b=b, kc=kc, t0=t0, tn=tn: e.matmul(
                        ps[b][:, 0:tn], wv[:, kc, ncol0:ncol0 + 128], actT[:, kc, t0:t0 + tn],
                        start=(kc == 0), stop=(kc == kc_n - 1)),
                        reads=[rw, Ract], writes=[Rps[b]])
                evac(b, t0, tn)

        def linear_tm(wv, rw, ncols, actT, Ract, kc_n, i, b, col0=0):
            for kc in range(kc_n):
                s.op("pe", lambda e, kc=kc: e.matmul(
                    ps[b][:, col0:col0 + ncols], actT[:, kc, i * 128:(i + 1) * 128], wv[:, kc, 0:ncols],
                    start=(kc == 0), stop=(kc == kc_n - 1)),
                    reads=[rw, Ract], writes=[Rps[b]])

        def residual_update(i, banks, wi, fsrc=None, Rf=None):
            a = ab_of_tile(i)
            if fsrc is None:
                for hh, b in enumerate(banks):
                    s.op("act", lambda e, b=b, hh=hh: e.activation(junk_b[:, 0:512], ps[b][:], AF.Square,
                                                                   scale=1.0 / 32.0, accum_out=stat[:, 8 + hh:9 + hh]),
                         reads=[Rps[b]], writes=[Rjunk, Rstat])
                s.op("dve", lambda e: e.tensor_tensor(stat[:, 10:11], stat[:, 8:9], stat[:, 9:10], ALU.add),
                     reads=[Rstat], writes=[Rstat])
            else:
                s.op("act", lambda e: e.activation(junk_b[:], fsrc, AF.Square, scale=1.0 / 32.0,
                                                   accum_out=stat[:, 10:11]),
                     reads=[Rf], writes=[Rjunk, Rstat])
            s.op("dve", lambda e: e.tensor_scalar(stat[:, 11:12], stat[:, 10:11], 1e-6, None, ALU.add),
                 reads=[Rstat], writes=[Rstat])
            s.op("act", lambda e: e.activation(stat[:, 11:12], stat[:, 11:12], AF.Sqrt), reads=[Rstat], writes=[Rstat])
            s.op("dve", lambda e: e.reciprocal(stat[:, 12:13], stat[:, 11:12]), reads=[Rstat], writes=[Rstat])
            for hh in range(2):
                src = ps[banks[hh]][:] if fsrc is None else fsrc[:, hh * 512:(hh + 1) * 512]
                rr = [Rps[banks[hh]]] if fsrc is None else [Rf]
                si = stctr[0] % 2
                stctr[0] += 1
                s.op("dve", lambda e, src=src, hh=hh, si=si: e.scalar_tensor_tensor(
                    stage_f[:, si, :], src, stat[:, 12:13], gg_sb[:, wi, a, hh * 512:(hh + 1) * 512],
                    ALU.mult, ALU.mult),
                    reads=rr + [Rstat, Rgg], writes=[Rstage[si]])
                s.op("dve", lambda e, hh=hh, si=si, i=i: e.tensor_tensor(
                    x_sb[:, i, hh * 512:(hh + 1) * 512], x_sb[:, i, hh * 512:(hh + 1) * 512], stage_f[:, si, :], ALU.add),
                    reads=[Rstage[si], Rx[i]], writes=[Rx[i]])

        actT = carve(5120, FC * T, BF16).rearrange("p (c t) -> p c t", c=FC)
        RactT = ares("actT")
        graw = carve(19200, T + 2, F32)
        Rgraw = ares("graw")
        cbuf = carve(19200 + 1284, T, F32)
        Rcbuf = ares("cbuf")
        sbuf_s = carve(19200 + 1284 + 1280, T, F32)
        Rsbuf = ares("sbuf_s")
        fstageA = carve(19200, 5 * D, F32).rearrange("p (i n) -> p i n", i=5)
        fstageB = carve(0, 5 * D, F32).rearrange("p (i n) -> p i n", i=5)

        def fst(i):
            return fstageA[:, i, :] if i < 5 else fstageB[:, i - 5, :]

        def Rfst_of(i):
            return ares("fstage") if i < 5 else RhT
        bcorr = sb("bcorr", [128, 16])
        Rbcorr = Res("bcorr")

        def ffn(l):
            make_hT(l, 1)
            s.op("dve", lambda e: e.memset(graw[:, 0:1], 0.0), writes=[Rgraw])
            s.op("dve", lambda e: e.memset(graw[:, T + 1:T + 2], 0.0), writes=[Rgraw])
            for blk in range(0, FC, 2):
                wu, ru = wload(w_ffn_in[l][:, blk * 128: blk * 128 + 256], KC, 256)
                wg, rg = wload(w_ffn_in[l][:, DFF + blk * 128: DFF + blk * 128 + 256], KC, 256)
                for jj in range(2):
                    fc = blk + jj
                    def evac_g(b, t0, tn):
                        s.op("act", lambda e, b=b, t0=t0, tn=tn: e.copy(graw[:, 1 + t0:1 + t0 + tn], ps[b][:, 0:tn]),
                             reads=[Rps[b]], writes=[Rgraw])
                    linear_fm(wg, rg, jj * 128, hT, RhT, KC, evac_g)
                    w0 = cp("conv%d_0" % l, fc)
                    w1 = cp("conv%d_1" % l, fc)
                    w2 = cp("conv%d_2" % l, fc)
                    cb = cp("convb%d" % l, fc)
                    s.op("act", lambda e, w1=w1, cb=cb: e.activation(cbuf[:], graw[:, 1:T + 1], AF.Identity, bias=cb, scale=w1),
                         reads=[Rgraw, Rcolp], writes=[Rcbuf])
                    s.op("dve", lambda e, w0=w0: e.scalar_tensor_tensor(cbuf[:], graw[:, 0:T], w0, cbuf[:], ALU.mult, ALU.add),
                         reads=[Rgraw, Rcolp, Rcbuf], writes=[Rcbuf])
                    s.op("dve", lambda e, w2=w2: e.scalar_tensor_tensor(cbuf[:], graw[:, 2:T + 2], w2, cbuf[:], ALU.mult, ALU.add),
                         reads=[Rgraw, Rcolp, Rcbuf], writes=[Rcbuf])
                    gprev = graw[:, 256:256 + 1024].rearrange("p (u k) -> p u k", k=256)[:, :, 0]
                    gnext = graw[:, 257:257 + 1024].rearrange("p (u k) -> p u k", k=256)[:, :, 0]
                    s.op("dve", lambda e, gprev=gprev: e.tensor_tensor(bcorr[:, 0:4], gprev, rowp_sb[:, ROFF["bndL"] + 1:ROFF["bndL"] + 5], ALU.mult),
                         reads=[Rgraw, Rrowp], writes=[Rbcorr])
                    s.op("dve", lambda e, w0=w0: e.tensor_scalar(bcorr[:, 0:4], bcorr[:, 0:4], w0, None, ALU.mult),
                         reads=[Rbcorr, Rcolp], writes=[Rbcorr])
                    c_at = cbuf[:, 256:256 + 1024].rearrange("p (u k) -> p u k", k=256)[:, :, 0]
                    s.op("dve", lambda e, c_at=c_at: e.tensor_tensor(c_at, c_at, bcorr[:, 0:4], ALU.subtract),
                         reads=[Rbcorr, Rcbuf], writes=[Rcbuf])
                    s.op("dve", lambda e, gnext=gnext: e.tensor_tensor(bcorr[:, 4:8], gnext, rowp_sb[:, ROFF["bndL"] + 1:ROFF["bndL"] + 5], ALU.mult),
                         reads=[Rgraw, Rrowp], writes=[Rbcorr])
                    s.op("dve", lambda e, w2=w2: e.tensor_scalar(bcorr[:, 4:8], bcorr[:, 4:8], w2, None, ALU.mult),
                         reads=[Rbcorr, Rcolp], writes=[Rbcorr])
                    c_at2 = cbuf[:, 255:255 + 1024].rearrange("p (u k) -> p u k", k=256)[:, :, 0]
                    s.op("dve", lambda e, c_at2=c_at2: e.tensor_tensor(c_at2, c_at2, bcorr[:, 4:8], ALU.subtract),
                         reads=[Rbcorr, Rcbuf], writes=[Rcbuf])
                    s.op("act", lambda e: e.activation(sbuf_s[:], cbuf[:], AF.Silu), reads=[Rcbuf], writes=[Rsbuf])
                    def evac_u(b, t0, tn, fc=fc):
                        s.op("dve", lambda e, b=b, t0=t0, tn=tn: e.tensor_tensor(
                            actT[:, fc, t0:t0 + tn], ps[b][:, 0:tn], sbuf_s[:, t0:t0 + tn], ALU.mult),
                            reads=[Rps[b], Rsbuf], writes=[RactT])
                    linear_fm(wu, ru, jj * 128, hT, RhT, KC, evac_u)
            foA = carve(24320, FC * 256, BF16).rearrange("p (c n) -> p c n", c=FC)
            RfoA = ares("foA")
            foB0 = wbuf[0][:, 0:16 * 256].rearrange("p (c n) -> p c n", c=16)
            foB1 = wbuf[1][:, 0:6 * 256].rearrange("p (c n) -> p c n", c=6)
            for nb in range(4):
                src = w_ffn_out[l][:, nb * 256:(nb + 1) * 256].rearrange("(c p) n -> p c n", p=128)
                if nb % 2 == 0:
                    s.dma("pool", foA, src, writes=[RfoA])

                    def rhs_of(kc):
                        return foA[:, kc, :], RfoA
                else:
                    s.dma("pool", foB0, src[:, 0:16, :], writes=[Rw[0]])
                    s.dma("pool", foB1, src[:, 16:22, :], writes=[Rw[1]])

                    def rhs_of(kc):
                        return (foB0[:, kc, :], Rw[0]) if kc < 16 else (foB1[:, kc - 16, :], Rw[1])
                for i in range(NT):
                    b = bank()
                    for kc in range(FC):
                        rv, rr_ = rhs_of(kc)
                        s.op("pe", lambda e, b=b, kc=kc, i=i, rv=rv: e.matmul(
                            ps[b][:, 0:256], actT[:, kc, i * 128:(i + 1) * 128], rv,
                            start=(kc == 0), stop=(kc == FC - 1)),
                            reads=[rr_, RactT], writes=[Rps[b]])
                    s.op("act", lambda e, b=b, i=i, nb=nb: e.copy(fst(i)[:, nb * 256:(nb + 1) * 256], ps[b][:, 0:256]),
                         reads=[Rps[b]], writes=[Rfst_of(i)])
            for i in range(NT):
                residual_update(i, None, 1, fsrc=fst(i), Rf=Rfst_of(i))

        qT = carve(5120, 4 * T, BF16).rearrange("p (c t) -> p c t", c=4)
        RqT = ares("qT")
        kT = carve(7680, 4 * 1536, BF16).rearrange("p (c t) -> p c t", c=4)
        RkT = ares("kT")
        Vaug = carve(10752, 12 * 4 * 144, BF16).rearrange("p (k h d) -> p k h d", k=12, h=4)
        RV = ares("Vaug")
        yT = carve(0, 8 * T, BF16).rearrange("p (c t) -> p c t", c=8)
        fbT = carve(14208, 15 * (T + 2), BF16).rearrange("p (c t) -> p c t", c=15)
        RfbT = ares("fbT")
        RyT = RhT
        ropeC_sb = carve(23824, T, F32)
        ropeS_sb = carve(23824 + T, T, F32)
        Rrope = Res("rope")
        rtmp = sb("rtmp", [128, 2, 512])
        Rrtmp = Res("rtmp")
        raw_b = sb("raw_b", [128, 512], BF16)
        Rraw = Res("raw_b")

        def even_inproj():
            s.dma("sp", ropeC_sb[:], ropeC, writes=[Rrope])
            s.dma("sp", ropeS_sb[:], ropeS, writes=[Rrope])
            s.dma("pool", kT[:, :, 0:256], cacheKT.rearrange("(h p) t -> p h t", p=128), writes=[RkT])
            for kk_ in range(2):
                s.dma("pool", Vaug[:, kk_, :, 0:128],
                      cacheV[kk_ * 128:(kk_ + 1) * 128, :].rearrange("p (h d) -> p h d", h=4), writes=[RV])
            if "m" in sub:
                s.op("pool", lambda e: e.memset(Vaug[:, :, :, 128:129], 1.0), writes=[RV])
            for which, dst, doff in ((0, qT, 0), (1, kT, 256)):
                if "q" not in sub:
                    break
                wv, rw = wload(w_in_even[:, which * 512:(which + 1) * 512], KC, 512)
                Rdst = RqT if which == 0 else RkT
                for h in range(4):
                    def evac(b, t0, tn, h=h, dst=dst, doff=doff, Rdst=Rdst):
                        s.op("act", lambda e, b=b, tn=tn: e.copy(raw_b[:, 0:tn], ps[b][:, 0:tn]),
                             reads=[Rps[b]], writes=[Rraw])
                        b2 = bank()
                        s.op("pe", lambda e, b2=b2, tn=tn: e.matmul(ps[b2][:, 0:tn], rotT_b[:], raw_b[:, 0:tn], start=True, stop=True),
                             reads=[Rraw, Rconst], writes=[Rps[b2]])
                        s.op("dve", lambda e, t0=t0, tn=tn: e.tensor_tensor(rtmp[:, 0, 0:tn], raw_b[:, 0:tn], ropeC_sb[:, t0:t0 + tn], ALU.mult),
                             reads=[Rraw, Rrope], writes=[Rrtmp])
                        s.op("dve", lambda e, b2=b2, t0=t0, tn=tn: e.tensor_tensor(rtmp[:, 1, 0:tn], ps[b2][:, 0:tn], ropeS_sb[:, t0:t0 + tn], ALU.mult),
                             reads=[Rps[b2], Rrope], writes=[Rrtmp])
                        s.op("dve", lambda e, t0=t0, tn=tn: e.tensor_tensor(dst[:, h, doff + t0:doff + t0 + tn], rtmp[:, 0, 0:tn], rtmp[:, 1, 0:tn], ALU.add),
                             reads=[Rrtmp], writes=[Rdst])
                    linear_fm(wv, rw, h * 128, hT, RhT, KC, evac)
                if which == 1:
                    for i in range(NT):
                        b = bank()
                        linear_tm(wv, rw, 512, hT, RhT, KC, i, b)
                        si = stctr[0] % 2
                        stctr[0] += 1
                        s.op("act", lambda e, b=b, si=si: e.copy(stage_f[:, si, :], ps[b][:]), reads=[Rps[b]], writes=[Rstage[si]])
                        s.dma("sp", k_out[i * 128:(i + 1) * 128, :], stage_f[:, si, :], reads=[Rstage[si]])
            s.op("pool", lambda e: e.memset(fbT[:, :, 0:1], 0.0), writes=[RfbT])
            s.op("pool", lambda e: e.memset(fbT[:, :, T + 1:T + 2], 0.0), writes=[RfbT])
            for pc, (c0, ncols) in enumerate([(1536, 512), (2048, 512), (2560, 512), (3072, 384)]):
                wv, rw = wload(w_in_even[:, c0:c0 + ncols], KC, ncols)
                for j in range(ncols // 128):
                    ch = pc * 4 + j

                    def evac_fb(b, t0, tn, ch=ch):
                        s.op("act", lambda e, b=b, t0=t0, tn=tn: e.copy(fbT[:, ch, 1 + t0:1 + t0 + tn], ps[b][:, 0:tn]),
                             reads=[Rps[b]], writes=[RfbT])
                    linear_fm(wv, rw, j * 128, hT, RhT, KC, evac_fb)
            wv, rw = wload(w_in_even[:, 1024:1536], KC, 512)
            for i in range(NT if "v" in sub else 0):
                b = bank()
                linear_tm(wv, rw, 512, hT, RhT, KC, i, b)
                si = stctr[0] % 2
                stctr[0] += 1
                s.op("act", lambda e, b=b, si=si: e.copy(stage_f[:, si, :], ps[b][:]), reads=[Rps[b]], writes=[Rstage[si]])
                s.dma("sp", v_out[i * 128:(i + 1) * 128, :], stage_f[:, si, :], reads=[Rstage[si]])
                s.op("dve", lambda e, si=si, i=i: e.tensor_copy(Vaug[:, 2 + i, :, 0:128], stage_f[:, si, :].rearrange("p (h d) -> p h d", h=4)),
                     reads=[Rstage[si]], writes=[RV])


        NPB = 4
        PT = [[sb("PT%d%d" % (m, k), [128, 256], BF16) for k in range(NPB)] for m in range(2)]
        RPT = [[Res("PT%d%d" % (m, k)) for k in range(NPB)] for m in range(2)]
        ya_f = sb("ya_f", [128, 128])
        Rya = Res("ya_f")
        ya_b = sb("ya_b", [128, 128], BF16)
        Ryab = Res("ya_b")
        subln08 = sb("subln08", [128, 128])
        Rsub = Res("subln08")
        lamt = sb("lamt", [128, 64])
        Rlamt = Res("lamt")

        def attention():
            lo = ROFF["lam"]
            for k in range(2):
                s.op("dve", lambda e, k=k: e.tensor_tensor(lamt[:], rowp_sb[:, lo + 128 * k: lo + 128 * k + 64],
                                                          rowp_sb[:, lo + 128 * k + 64: lo + 128 * k + 128], ALU.mult),
                     reads=[Rrowp], writes=[Rlamt])
                s.op("dve", lambda e, k=k: e.reduce_sum(stat[:, 20 + k:21 + k], lamt[:], axis=AX.X),
                     reads=[Rlamt], writes=[Rstat])
            s.op("act", lambda e: e.activation(stat[:, 22:24], stat[:, 20:22], AF.Exp), reads=[Rstat], writes=[Rstat])
            s.op("dve", lambda e: e.tensor_tensor(stat[:, 24:25], stat[:, 22:23], stat[:, 23:24], ALU.subtract),
                 reads=[Rstat], writes=[Rstat])
            s.op("dve", lambda e: e.tensor_scalar(stat[:, 25:26], stat[:, 24:25], 0.2, -1.0, ALU.add, ALU.mult),
                 reads=[Rstat], writes=[Rstat])
            s.op("dve", lambda e: e.tensor_scalar(subln08[:], rowp_sb[:, ROFF["subln"]:ROFF["subln"] + 128], 0.8, None, ALU.mult),
                 reads=[Rrowp], writes=[Rsub])
            pctr = [0, 0]
            for h in range(4):
                for qu in range(5):
                    if P0L1:
                        try:
                            next(P0L1[0])
                        except StopIteration:
                            del P0L1[:]
                    acc = [bank(), bank()]
                    reserved.update(acc)
                    pend = []
                    for kt in range(12):
                        ku = 0 if kt < 2 else 1 + (kt - 2) // 2
                        bcol = ROFF["abias"] + ku * 5 + qu
                        for m in range(2):
                            bs = bank()
                            s.op("pe", lambda e, bs=bs, m=m, h=h, kt=kt, qu=qu: e.matmul(
                                ps[bs][:, 0:256], kT[64 * m:64 * m + 64, h, kt * 128:(kt + 1) * 128],
                                qT[64 * m:64 * m + 64, h, qu * 256:(qu + 1) * 256], start=True, stop=True),
                                reads=[RkT, RqT], writes=[Rps[bs]])
                            pk = pctr[m] % NPB
                            pctr[m] += 1
                            s.op("act", lambda e, bs=bs, m=m, pk=pk, bcol=bcol: e.activation(
                                PT[m][pk][:], ps[bs][:, 0:256], AF.Exp, bias=rowp_sb[:, bcol:bcol + 1], scale=0.125),
                                reads=[Rps[bs], Rrowp], writes=[RPT[m][pk]])

                            def pv(m=m, pk=pk, kt=kt, h=h, acc=acc):
                                for qt in range(2):
                                    s.op("pe", lambda e, qt=qt: e.matmul(
                                        ps[acc[m]][:, qt * 129:qt * 129 + 129], PT[m][pk][:, qt * 128:(qt + 1) * 128],
                                        Vaug[:, kt, h, 0:129], start=(kt == 0 and qt == 0), stop=(kt == 11),
                                        skip_group_check=True),
                                        reads=[RPT[m][pk], RV], writes=[Rps[acc[m]]])
                            pend.append(pv)
                            if len(pend) > 2:
                                pend.pop(0)()
                    while pend:
                        pend.pop(0)()
                    reserved.difference_update(acc)
                    for qt in range(2):
                        i = qu * 2 + qt
                        c0 = qt * 129
                        s.op("dve", lambda e, c0=c0, acc=acc: e.reciprocal(stat[:, 30:31], ps[acc[0]][:, c0 + 128:c0 + 129]),
                             reads=[Rps[acc[0]]], writes=[Rstat])
                        s.op("dve", lambda e, c0=c0, acc=acc: e.reciprocal(stat[:, 31:32], ps[acc[1]][:, c0 + 128:c0 + 129]),
                             reads=[Rps[acc[1]]], writes=[Rstat])
                        s.op("dve", lambda e: e.tensor_tensor(stat[:, 32:33], stat[:, 31:32], stat[:, 25:26], ALU.mult),
                             reads=[Rstat], writes=[Rstat])
                        s.op("dve", lambda e, c0=c0, acc=acc: e.tensor_scalar(ya_f[:], ps[acc[0]][:, c0:c0 + 128], stat[:, 30:31], None, ALU.mult),
                             reads=[Rps[acc[0]], Rstat], writes=[Rya])
                        s.op("dve", lambda e, c0=c0, acc=acc: e.scalar_tensor_tensor(ya_f[:], ps[acc[1]][:, c0:c0 + 128], stat[:, 32:33], ya_f[:], ALU.mult, ALU.add),
                             reads=[Rps[acc[1]], Rstat, Rya], writes=[Rya])
                        s.op("act", lambda e: e.activation(junk_b[:, 0:128], ya_f[:], AF.Square, scale=1.0 / math.sqrt(128.0),
                                                           accum_out=stat[:, 33:34]),
                             reads=[Rya], writes=[Rjunk, Rstat])
                        s.op("dve", lambda e: e.tensor_scalar(stat[:, 34:35], stat[:, 33:34], 1e-6, None, ALU.add),
                             reads=[Rstat], writes=[Rstat])
                        s.op("act", lambda e: e.activation(stat[:, 34:35], stat[:, 34:35], AF.Sqrt), reads=[Rstat], writes=[Rstat])
                        s.op("dve", lambda e: e.reciprocal(stat[:, 35:36], stat[:, 34:35]), reads=[Rstat], writes=[Rstat])
                        s.op("dve", lambda e: e.scalar_tensor_tensor(ya_b[:], ya_f[:], stat[:, 35:36], subln08[:], ALU.mult, ALU.mult),
                             reads=[Rya, Rstat, Rsub], writes=[Ryab])
                        reserved.update(acc)
                        bt = bank()
                        reserved.difference_update(acc)
                        s.op("pe", lambda e, bt=bt: e.transpose(psb[bt][:, 0:128], ya_b[:], ident_b[:]),
                             reads=[Ryab, Rconst], writes=[Rps[bt]])
                        s.op("act", lambda e, bt=bt, h=h, i=i: e.copy(yT[:, h, i * 128:(i + 1) * 128], psb[bt][:, 0:128]),
                             reads=[Rps[bt]], writes=[RyT])

        def out_proj(W, actTv, Ract):
            pieces = [wload(W[:, hh * 512:(hh + 1) * 512], KC, 512) for hh in range(2)]
            for i in range(NT):
                bb = [bank(), bank()]
                for hh in range(2):
                    linear_tm(pieces[hh][0], pieces[hh][1], 512, actTv, Ract, KC, i, bb[hh])
                residual_update(i, bb, 0)

        Xc = carve(5120, NT * 8 * 256, BF16).rearrange("p (i g n) -> p i g n", i=NT, g=8)
        RXc = ares("Xc")
        chCS_b = sb("chCS_b", [128, 256], BF16)
        Rch = Res("chCS")

        def fnet():
            s.dma("pool", chCS_b[:], chCS, writes=[Rch])
            for i in range(NT):
                for gp in range(4):
                    b = bank()
                    for g2_ in range(2):
                        g = gp * 2 + g2_
                        s.op("pe", lambda e, b=b, g=g, g2_=g2_, i=i: e.matmul(
                            ps[b][:, g2_ * 256:(g2_ + 1) * 256], hT[:, g, i * 128:(i + 1) * 128], chCS_b[:], start=True, stop=True),
                            reads=[RhT, Rch], writes=[Rps[b]])
                    s.op("act", lambda e, b=b, gp=gp, i=i: e.copy(
                        Xc[:, i, gp * 2:gp * 2 + 2, :], ps[b][:].rearrange("p (g n) -> p g n", g=2)),
                        reads=[Rps[b]], writes=[RXc])
            for (t0, tn) in [(0, 384), (384, 384), (768, 384), (1152, 128)]:
                wc, rc = wload(dftC[:, t0:t0 + tn], NT, tn)
                wsn, rsn = wload(dftSn[:, t0:t0 + tn], NT, tn)
                for g in range(8):
                    b = bank()
                    for tc in range(NT):
                        s.op("pe", lambda e, b=b, g=g, tc=tc, tn=tn, wc=wc: e.matmul(
                            ps[b][:, 0:tn], Xc[:, tc, g, 0:128], wc[:, tc, 0:tn], start=(tc == 0), stop=False),
                            reads=[RXc, rc], writes=[Rps[b]])
                        s.op("pe", lambda e, b=b, g=g, tc=tc, tn=tn, wsn=wsn: e.matmul(
                            ps[b][:, 0:tn], Xc[:, tc, g, 128:256], wsn[:, tc, 0:tn], start=False, stop=(tc == NT - 1)),
                            reads=[RXc, rsn], writes=[Rps[b]])
                    s.op("act", lambda e, b=b, g=g, t0=t0, tn=tn: e.copy(yT[:, g, t0:t0 + tn], ps[b][:, 0:tn]),
                         reads=[Rps[b]], writes=[RyT])


        V_tm = carve(5120, NT * 512, BF16).rearrange("p (i n) -> p i n", i=NT)
        RVtm = ares("V_tm")
        YSb = carve(7680, NT * 512, BF16).rearrange("p (i n) -> p i n", i=NT)
        RYS = ares("YSb")
        Ystage = carve(10240, 128 * 16, F32)[0:64, :].rearrange("p (n k) -> p n k", k=16)
        RYst = ares("Ystage")
        Sst = [carve(12288, 512, F32), carve(12800, 512, F32)]
        RS = [ares("S0"), ares("S1")]
        tmpA = carve(13312, 512, F32)
        RtA = ares("tmpA")
        tmpB = carve(23824, 512, F32)
        RtB = ares("tmpB")
        w2t_b = carve(24336, 512, BF16)
        a2t_b = carve(24592, 512, BF16)
        g2_b = carve(24848, 512, BF16)
        Rlora = ares("lora")
        PT0 = 25104
        NPT = 12

        def pt(k, n=1):
            return carve(PT0 + 128 * k, 128 * n, F32)
        Rpt = [ares("pt%d" % k) for k in range(NPT)]
        lnx0_b = carve(26640, 512, BF16)
        lnx1_b = carve(26896, 512, BF16)
        Rlnx = ares("lnx")
        tb0 = junk_b[:, 0:128]
        tb1 = junk_b[:, 128:256]
        Rtb0 = Res("tb0")
        Rtb1 = Res("tb1")
        colx = sb("colx", [128, 64])
        Rcolx = Res("colx")
        tiny = sb("tiny", [128, 8])
        Rtiny = Res("tiny")
        BS = sb("BS", [128, NT, 8])
        RBS = Res("BS")
        wbf = [w[:].bitcast(F32) for w in wbuf]
        TW = wbf[0][:, 0:1024].rearrange("p (k n) -> p k n", k=8)
        TKK = wbf[0][:, 1024:2048].rearrange("p (k n) -> p k n", k=8)
        TNK = wbf[1][:, 0:1024].rearrange("p (k n) -> p k n", k=8)
        TKD = wbf[1][:, 1024:2048].rearrange("p (k n) -> p k n", k=8)
        TR2 = wbuf[2][:, 0:2048].rearrange("p (k n h) -> p k n h", k=8, h=2)
        tmpA_b = carve(10240, 512, BF16)
        RtAb = ares("tmpA_b")
        Sb16 = carve(10496, 512, BF16)
        RSb = ares("Sb16")
        CX = dict(cmu=0, nmu0=15, nmu1=30, omk1=45, keepL=49, keepR=54)

        def cx(name, c=0):
            o = CX[name] + c
            return colx[:, o:o + 1]

        def rwkv_setup():
            s.dma("pool", w2t_b, w2t_d, writes=[Rlora])
            s.dma("pool", a2t_b, a2t_d, writes=[Rlora])
            s.dma("pool", g2_b, g2_d, writes=[Rlora])
            s.dma("pool", lnx0_b, lnx_d[0, :].partition_broadcast(128), writes=[Rlnx])
            s.dma("pool", lnx1_b, lnx_d[1, :].partition_broadcast(128), writes=[Rlnx])
            mu0 = cp("mu0", 0, 15)
            mu1 = cp("mu1", 0, 15)
            s.op("dve", lambda e: e.tensor_tensor(colx[:, 0:15], mu0, mu1, ALU.add), reads=[Rcolp], writes=[Rcolx])
            s.op("dve", lambda e: e.tensor_scalar(colx[:, 0:15], colx[:, 0:15], -1.0, 1.0, ALU.mult, ALU.add),
                 reads=[Rcolx], writes=[Rcolx])
            s.op("dve", lambda e: e.tensor_scalar(colx[:, 15:30], mu0, -1.0, None, ALU.mult), reads=[Rcolp], writes=[Rcolx])
            s.op("dve", lambda e: e.tensor_scalar(colx[:, 30:45], mu1, -1.0, None, ALU.mult), reads=[Rcolp], writes=[Rcolx])
            s.op("dve", lambda e: e.tensor_scalar(colx[:, 45:49], cp("kvec1", 0, 4), -1.0, 1.0, ALU.mult, ALU.add),
                 reads=[Rcolp], writes=[Rcolx])
            s.op("dve", lambda e: e.tensor_scalar(colx[:, 49:54], rowp_sb[:, ROFF["bndL"]:ROFF["bndL"] + 5], -1.0, 1.0, ALU.mult, ALU.add),
                 reads=[Rrowp], writes=[Rcolx])
            s.op("dve", lambda e: e.tensor_scalar(colx[:, 54:59], rowp_sb[:, ROFF["bndR"]:ROFF["bndR"] + 5], -1.0, 1.0, ALU.mult, ALU.add),
                 reads=[Rrowp], writes=[Rcolx])
            s.op("dve", lambda e: e.memset(BS[:], 0.0), writes=[RBS])

        def shift(ch, ti, out, Rout):
            t0 = ti * 128
            f = fbT[:, ch, 1 + t0:1 + t0 + 128]
            fp = fbT[:, ch, t0:t0 + 128]
            fn = fbT[:, ch, 2 + t0:2 + t0 + 128]
            rr = [RfbT, Rcolp, Rcolx]
            s.op("dve", lambda e: e.tensor_scalar(out, f, cx("cmu", ch), None, ALU.mult), reads=rr, writes=[Rout])
            s.op("dve", lambda e: e.scalar_tensor_tensor(out, fp, cp("mu0", ch), out, ALU.mult, ALU.add),
                 reads=rr + [Rout], writes=[Rout])
            s.op("dve", lambda e: e.scalar_tensor_tensor(out, fn, cp("mu1", ch), out, ALU.mult, ALU.add),
                 reads=rr + [Rout], writes=[Rout])
            u = ti // 2
            if ti % 2 == 0:
                bl = rowp_sb[:, ROFF["bndL"] + u:ROFF["bndL"] + u + 1]
                s.op("dve", lambda e: e.tensor_tensor(tiny[:, 0:1], fbT[:, ch, t0:t0 + 1], bl, ALU.mult),
                     reads=[RfbT, Rrowp], writes=[Rtiny])
                s.op("dve", lambda e: e.scalar_tensor_tensor(out[:, 0:1], tiny[:, 0:1], cx("nmu0", ch), out[:, 0:1], ALU.mult, ALU.add),
                     reads=[Rtiny, Rcolx, Rout], writes=[Rout])
            else:
                br = rowp_sb[:, ROFF["bndR"] + u:ROFF["bndR"] + u + 1]
                s.op("dve", lambda e: e.tensor_tensor(tiny[:, 1:2], fbT[:, ch, 1 + t0 + 128:2 + t0 + 128], br, ALU.mult),
                     reads=[RfbT, Rrowp], writes=[Rtiny])
                s.op("dve", lambda e: e.scalar_tensor_tensor(out[:, 127:128], tiny[:, 1:2], cx("nmu1", ch), out[:, 127:128], ALU.mult, ALU.add),
                     reads=[Rtiny, Rcolx, Rout], writes=[Rout])

        def rwkv_prepass():
            for ti in range(NT):
                for c in range(4):
                    shift(8 + c, ti, pt(c), Rpt[c])
                    s.op("act", lambda e, c=c: e.copy(xn_b[:, c * 128:(c + 1) * 128], pt(c)), reads=[Rpt[c]], writes=[Rxn])
                b = bank()
                for c in range(4):
                    s.op("pe", lambda e, b=b, c=c: e.transpose(psb[b][:, c * 128:(c + 1) * 128], xn_b[:, c * 128:(c + 1) * 128], ident_b[:]),
                         reads=[Rxn, Rconst], writes=[Rps[b]])
                s.op("act", lambda e, b=b, ti=ti: e.copy(V_tm[:, ti, :], psb[b][:, 0:512]), reads=[Rps[b]], writes=[RVtm])

        def prep(d, ti):
            rev = (d == 1)

            def tab(T3, c):
                v = T3[:, d * 4 + c, :]
                return v[:, ::-1] if rev else v
            tabs_w = [Rw[0], Rw[1], Rw[2]]
            shift(12, ti, pt(0), Rpt[0])
            shift(13, ti, pt(1), Rpt[1])
            s.op("act", lambda e: e.activation(tb0, pt(0), AF.Tanh), reads=[Rpt[0]], writes=[Rtb0])
            s.op("act", lambda e: e.copy(tb1, pt(1)), reads=[Rpt[1]], writes=[Rtb1])
            lo, hi = 64 * d, 64 * d + 64
            for c in range(4):
                b = bank()
                s.op("pe", lambda e, b=b, c=c: e.matmul(ps[b][:, 0:128], w2t_b[lo:hi, c * 128:(c + 1) * 128], tb0[lo:hi, :], start=True, stop=True),
                     reads=[Rlora, Rtb0], writes=[Rps[b]])
                s.op("act", lambda e, b=b, c=c: e.activation(pt(2), ps[b][:, 0:128], AF.Sigmoid, bias=cp("w0_%d" % d, c)),
                     reads=[Rps[b], Rcolp], writes=[Rpt[2]])
                s.op("act", lambda e, c=c: e.activation(tab(TW, c), pt(2), AF.Exp, scale=-EXPM05),
                     reads=[Rpt[2]], writes=[Rw[0]])
                b = bank()
                s.op("pe", lambda e, b=b, c=c: e.matmul(ps[b][:, 0:128], a2t_b[lo:hi, c * 128:(c + 1) * 128], tb1[lo:hi, :], start=True, stop=True),
                     reads=[Rlora, Rtb1], writes=[Rps[b]])
                s.op("act", lambda e, b=b, c=c: e.activation(pt(3), ps[b][:, 0:128], AF.Sigmoid, bias=cp("a0_%d" % d, c)),
                     reads=[Rps[b], Rcolp], writes=[Rpt[3]])
                shift(4 + c, ti, pt(4), Rpt[4])
                s.op("dve", lambda e, c=c: e.tensor_scalar(pt(5), pt(4), cp("kvec0", c), None, ALU.mult),
                     reads=[Rpt[4], Rcolp], writes=[Rpt[5]])
                s.op("act", lambda e: e.activation(pt(6), pt(5), AF.Square), reads=[Rpt[5]], writes=[Rpt[6]])
                b = bank()
                s.op("pe", lambda e, b=b: e.matmul(ps[b][:, 0:128], bones_f[:], pt(6), start=True, stop=True),
                     reads=[Rconst, Rpt[6]], writes=[Rps[b]])
                s.op("dve", lambda e, b=b: e.tensor_scalar(pt(7), ps[b][:, 0:128], 1e-12, None, ALU.add),
                     reads=[Rps[b]], writes=[Rpt[7]])
                s.op("act", lambda e: e.activation(pt(7), pt(7), AF.Sqrt), reads=[Rpt[7]], writes=[Rpt[7]])
                s.op("dve", lambda e: e.reciprocal(pt(7), pt(7)), reads=[Rpt[7]], writes=[Rpt[7]])
                s.op("dve", lambda e: e.tensor_tensor(pt(8), pt(5), pt(7), ALU.mult), reads=[Rpt[5], Rpt[7]], writes=[Rpt[8]])
                s.op("act", lambda e, c=c: e.copy(tab(TKK, c), pt(8)), reads=[Rpt[8]], writes=[Rw[0]])
                s.op("dve", lambda e, c=c: e.scalar_tensor_tensor(tab(TNK, c), pt(8), -1.0, pt(3), ALU.mult, ALU.mult),
                     reads=[Rpt[8], Rpt[3]], writes=[Rw[1]])
                s.op("dve", lambda e, c=c: e.tensor_scalar(pt(9), pt(3), cp("kvec1", c), cx("omk1", c), ALU.mult, ALU.add),
                     reads=[Rpt[3], Rcolp, Rcolx], writes=[Rpt[9]])
                s.op("dve", lambda e: e.tensor_tensor(pt(9), pt(4), pt(9), ALU.mult), reads=[Rpt[4], Rpt[9]], writes=[Rpt[9]])
                s.op("act", lambda e, c=c: e.copy(tab(TKD, c), pt(9)), reads=[Rpt[9]], writes=[Rw[1]])
                shift(c, ti, pt(10), Rpt[10])
                for h2 in range(2):
                    o = TR2[:, d * 4 + c, :, h2]
                    if rev:
                        o = o[:, ::-1]
                    s.op("dve", lambda e, o=o, h2=h2: e.tensor_scalar(o, pt(10), halfsel_f[:, h2:h2 + 1], None, ALU.mult),
                         reads=[Rpt[10], Rconst], writes=[Rw[2]])
                s.op("dve", lambda e: e.tensor_tensor(pt(6), pt(10), pt(9), ALU.mult), reads=[Rpt[10], Rpt[9]], writes=[Rpt[6]])
                s.op("dve", lambda e, c=c: e.tensor_scalar(pt(6), pt(6), cp("kvec2", c), None, ALU.mult),
                     reads=[Rpt[6], Rcolp], writes=[Rpt[6]])
                b = bank()
                s.op("pe", lambda e, b=b: e.matmul(ps[b][:, 0:2], pt(6), halfsel_f[:, 0:2], start=True, stop=True),
                     reads=[Rpt[6], Rconst], writes=[Rps[b]])
                s.op("dve", lambda e, b=b, c=c, ti=ti: e.tensor_tensor(BS[:, ti, c * 2:c * 2 + 2], BS[:, ti, c * 2:c * 2 + 2], ps[b][:, 0:2], ALU.add),
                     reads=[Rps[b], RBS], writes=[RBS])

        yb = xn_b[:, 0:512]

        def finalize(ti, by):
            ys = pt(0, 4)
            Rys = [Rpt[0], Rpt[1], Rpt[2], Rpt[3]]
            t2 = pt(4, 4)
            Rt2 = [Rpt[4], Rpt[5], Rpt[6], Rpt[7]]
            ys3 = ys.rearrange("p (h i) -> p h i", h=8)
            t23 = t2.rearrange("p (h i) -> p h i", h=8)
            s.op("dve", lambda e: e.tensor_tensor(ys, YSb[:, ti, :], ps[by][:], ALU.add), reads=[RYS, Rps[by]], writes=Rys)
            s.op("dve", lambda e: e.reduce_sum(tiny[:, 0:8], ys3, axis=AX.X), reads=Rys, writes=[Rtiny])
            s.op("dve", lambda e: e.tensor_scalar(tiny[:, 0:8], tiny[:, 0:8], -1.0 / 64.0, None, ALU.mult), reads=[Rtiny], writes=[Rtiny])
            s.op("dve", lambda e: e.tensor_tensor(ys3, ys3, tiny[:, 0:8].unsqueeze(2).to_broadcast([128, 8, 64]), ALU.add),
                 reads=Rys + [Rtiny], writes=Rys)
            s.op("dve", lambda e: e.tensor_tensor(t2, ys, ys, ALU.mult), reads=Rys, writes=Rt2)
            s.op("dve", lambda e: e.reduce_sum(stat[:, 40:48], t23, axis=AX.X), reads=Rt2, writes=[Rstat])
            s.op("dve", lambda e: e.tensor_scalar(stat[:, 40:48], stat[:, 40:48], 1.0 / 64.0, 64e-5, ALU.mult, ALU.add),
                 reads=[Rstat], writes=[Rstat])
            s.op("act", lambda e: e.activation(stat[:, 40:48], stat[:, 40:48], AF.Sqrt), reads=[Rstat], writes=[Rstat])
            s.op("dve", lambda e: e.reciprocal(stat[:, 48:56], stat[:, 40:48]), reads=[Rstat], writes=[Rstat])
            s.op("dve", lambda e: e.tensor_tensor(ys3, ys3, stat[:, 48:56].unsqueeze(2).to_broadcast([128, 8, 64]), ALU.mult),
                 reads=Rys + [Rstat], writes=Rys)
            s.op("dve", lambda e: e.tensor_tensor(ys, ys, lnx0_b, ALU.mult), reads=Rys + [Rlnx], writes=Rys)
            s.op("dve", lambda e: e.tensor_tensor(ys, ys, lnx1_b, ALU.add), reads=Rys + [Rlnx], writes=Rys)
            s.op("dve", lambda e: e.tensor_tensor(t23, V_tm[:, ti, :].rearrange("p (h i) -> p h i", h=8),
                                                  BS[:, ti, :].unsqueeze(2).to_broadcast([128, 8, 64]), ALU.mult),
                 reads=[RVtm, RBS], writes=Rt2)
            s.op("dve", lambda e: e.tensor_tensor(ys, ys, t2, ALU.add), reads=Rys + Rt2, writes=Rys)
            shift(14, ti, pt(8), Rpt[8])
            s.op("act", lambda e: e.activation(tb0, pt(8), AF.Sigmoid), reads=[Rpt[8]], writes=[Rtb0])
            bg = bank()
            s.op("pe", lambda e, bg=bg: e.matmul(ps[bg][:], tb0, g2_b, start=True, stop=True),
                 reads=[Rtb0, Rlora], writes=[Rps[bg]])
            s.op("dve", lambda e, bg=bg: e.tensor_tensor(yb, ys, ps[bg][:], ALU.mult), reads=Rys + [Rps[bg]], writes=[Rxn])
            bt = bank()
            for c in range(4):
                s.op("pe", lambda e, bt=bt, c=c: e.transpose(psb[bt][:, c * 128:(c + 1) * 128], yb[:, c * 128:(c + 1) * 128], ident_b[:]),
                     reads=[Rxn, Rconst], writes=[Rps[bt]])
            s.op("act", lambda e, bt=bt, ti=ti: e.copy(yT[:, 4:8, ti * 128:(ti + 1) * 128],
                                                        psb[bt][:, 0:512].rearrange("p (c t) -> p c t", c=4)),
                 reads=[Rps[bt]], writes=[RyT])

        def rwkv_scan(nrounds=NT):
            S8 = [x_.rearrange("p (k i) -> p k i", k=8) for x_ in Sst]
            tA8 = tmpA.rearrange("p (k i) -> p k i", k=8)
            tB8 = tmpB.rearrange("p (k i) -> p k i", k=8)

            def bc(T3, n):
                return T3[:, :, n].unsqueeze(2).to_broadcast([128, 8, 64])
            for r in range(nrounds):
                tf, tbk = r, NT - 1 - r
                prep(0, tf)
                prep(1, tbk)
                if tf % 2 == 0:
                    u = tf // 2
                    if u == 0:
                        s.dma("sp", Sst[0][:, 0:256], initS[0], writes=[RS[0]])
                    else:
                        s.op("dve", lambda e, u=u: e.tensor_scalar(Sst[0][:, 0:256], Sst[0][:, 0:256], cx("keepL", u), None, ALU.mult),
                             reads=[RS[0], Rcolx], writes=[RS[0]])
                if tbk % 2 == 1:
                    u = tbk // 2
                    if u == 4:
                        s.op("dve", lambda e: e.memset(Sst[0][:, 256:512], 0.0), writes=[RS[0]])
                    elif u == 3:
                        s.dma("sp", tmpB[:, 0:256], initS[1], writes=[RtB])
                        s.op("dve", lambda e, u=u: e.scalar_tensor_tensor(Sst[0][:, 256:512], Sst[0][:, 256:512], cx("keepR", u),
                                                                          tmpB[:, 0:256], ALU.mult, ALU.add),
                             reads=[RS[0], Rcolx, RtB], writes=[RS[0]])
                    else:
                        s.op("dve", lambda e, u=u: e.tensor_scalar(Sst[0][:, 256:512], Sst[0][:, 256:512], cx("keepR", u), None, ALU.mult),
                             reads=[RS[0], Rcolx], writes=[RS[0]])
                by_ = None
                pending = []

                def flush():
                    for f_ in pending:
                        f_()
                    del pending[:]
                for n in range(128):
                    ci, ni = n % 2, (n + 1) % 2
                    Sc, Sn = S8[ci], S8[ni]
                    if n % 32 == 0:
                        by_ = bank()
                        reserved.add(by_)
                    tAb8 = tmpA_b.rearrange("p (k i) -> p k i", k=8)
                    s.op("dve", lambda e, n=n, Sc=Sc, tAb8=tAb8: e.tensor_tensor(tAb8, Sc, bc(TKK, n), ALU.mult),
                         reads=[RS[ci], Rw[0]], writes=[RtAb])
                    s.op("dve", lambda e, n=n, Sc=Sc, Sn=Sn: e.tensor_tensor(Sn, Sc, bc(TW, n), ALU.mult),
                         reads=[RS[ci], Rw[0]], writes=[RS[ni]])
                    bv = bank()
                    for d in range(2):
                        tile_d = tf if d == 0 else tbk
                        row = n if d == 0 else 127 - n
                        for h2 in range(2):
                            s.op("pe", lambda e, bv=bv, d=d, h2=h2, tile_d=tile_d, row=row: e.matmul(
                                ps[bv][64 * h2:64 * h2 + 64, d * 256:(d + 1) * 256].rearrange("p (c i) -> p c i", c=4),
                                ident_b[:, row:row + 1].to_broadcast([128, 64]),
                                V_tm[:, tile_d, :].rearrange("p (c h i) -> p c h i", c=4, h=2)[:, :, h2, :],
                                start=True, stop=True),
                                reads=[RVtm, Rconst], writes=[Rps[bv]])
                    bs_ = bank()
                    s.op("pe", lambda e, bs_=bs_: e.matmul(ps[bs_][:], bones_b[:], tmpA_b, start=True, stop=True),
                         reads=[RtAb, Rconst], writes=[Rps[bs_]])
                    flush()
                    s.op("dve", lambda e, n=n, bv=bv: e.tensor_tensor(tB8, ps[bv][:].rearrange("p (k i) -> p k i", k=8), bc(TKD, n), ALU.mult),
                         reads=[Rps[bv], Rw[1]], writes=[RtB])
                    s.op("dve", lambda e, ni=ni: e.tensor_tensor(Sst[ni], Sst[ni], tmpB, ALU.add),
                         reads=[RS[ni], RtB], writes=[RS[ni]])
                    s.op("dve", lambda e, n=n, bs_=bs_: e.tensor_tensor(tA8, ps[bs_][:].rearrange("p (k i) -> p k i", k=8), bc(TNK, n), ALU.mult),
                         reads=[Rps[bs_], Rw[1]], writes=[RtA])
                    s.op("dve", lambda e, ni=ni: e.tensor_tensor(Sst[ni], Sst[ni], tmpA, ALU.add),
                         reads=[RS[ni], RtA], writes=[RS[ni]])
                    s.op("act", lambda e, ni=ni: e.copy(Sb16, Sst[ni]), reads=[RS[ni]], writes=[RSb])
                    nl = n % 32

                    def ymm(by_=by_, nl=nl, n=n):
                        for dc in range(8):
                            s.op("pe", lambda e, dc=dc: e.matmul(
                                ps[by_][0:64, nl * 16 + dc * 2:nl * 16 + dc * 2 + 2], Sb16[:, dc * 64:(dc + 1) * 64], TR2[:, dc, n, :],
                                start=True, stop=True),
                                reads=[RSb, Rw[2]], writes=[Rps[by_]])
                    pending.append(ymm)
                    if nl == 31:
                        flush()
                        n0 = n - 31
                        src = ps[by_][0:64, :].rearrange("p (n k) -> p n k", k=16)
                        s.op("act", lambda e, src=src, n0=n0: e.copy(Ystage[:, n0:n0 + 32, 0:8], src[:, :, 0:8]),
                             reads=[Rps[by_]], writes=[RYst])
                        s.op("act", lambda e, src=src, n0=n0: e.copy(Ystage[:, 96 - n0:128 - n0, 8:16][:, ::-1, :], src[:, :, 8:16]),
                             reads=[Rps[by_]], writes=[RYst])
                        reserved.discard(by_)
                if tf % 2 == 1:
                    s.dma("sp", s_out[tf // 2, 0], Sst[0][:, 0:256], reads=[RS[0]])
                if tbk % 2 == 0:
                    s.dma("sp", s_out[tbk // 2, 1], Sst[0][:, 256:512], reads=[RS[0]])
                for d in range(2):
                    tile_d = tf if d == 0 else tbk
                    byy = bank()
                    for k8 in range(8):
                        s.op("pe", lambda e, byy=byy, k8=k8, d=d: e.matmul(
                            ps[byy][:, k8 * 64:(k8 + 1) * 64], Ystage[:, :, d * 8 + k8], ident_f[0:64, 0:64], start=True, stop=True),
                            reads=[RYst, Rconst], writes=[Rps[byy]])
                    first = (r < 5)
                    if first:
                        s.op("act", lambda e, byy=byy, tile_d=tile_d: e.copy(YSb[:, tile_d, :], ps[byy][:]),
                             reads=[Rps[byy]], writes=[RYS])
                    else:
                        finalize(tile_d, byy)


        def wbw(i, off_w, nelem):
            return wbuf[i][:, 2 * off_w:2 * off_w + nelem]
        Ast = carve(13200, 512, F32)
        RAst = ares("Ast")
        Abf = carve(13712, 512, BF16)
        RAbf = ares("Abf")
        WS = []
        fB = 14208 + 8 * 641
        for wsi in range(2):
            d_ = {}
            if wsi == 0:
                for k_, nm in enumerate(["Lt0", "Lt1", "L0", "L1", "Z0"]):
                    d_[nm] = carve(10240 + 512 * k_, 512, F32)
                d_["MT"] = carve(12800, 256, BF16)
                d_["RT"] = carve(12928, 512, BF16)
                d_["Z1"] = wbuf[0][:, 0:1024].bitcast(F32)
                for k_, nm in enumerate(["Lakt", "Mrbt", "Mrkt", "BKtm"]):
                    d_[nm] = wbw(0, 512 + 256 * k_, 512)
                d_["Zfb"] = carve(10240, 512, BF16)
            else:
                for k_, nm in enumerate(["Lt0", "Lt1", "L0", "L1", "Z0"]):
                    d_[nm] = carve(fB + 512 * k_, 512, F32)
                d_["Z1"] = wbuf[1][:, 0:1024].bitcast(F32)
                d_["Lakt"] = wbw(0, 1536, 512)
                d_["Mrbt"] = wbw(0, 1792, 512)
                d_["Mrkt"] = wbw(1, 512, 512)
                d_["MT"] = wbw(1, 768, 256)
                d_["BKtm"] = carve(23824, 512, BF16)
                d_["RT"] = carve(24080, 512, BF16)
                d_["Zfb"] = carve(fB, 512, BF16)
            for nm in ["Lt0", "Lt1", "L0", "L1", "Z0", "Z1", "Lakt", "Mrbt", "Mrkt", "BKtm", "MT", "RT"]:
                d_["R" + nm] = ares("ws%d_%s" % (wsi, nm))
            d_["RZfb"] = d_["RLt0"]
            WS.append(d_)
        masks = wbw(1, 896, 4 * 4 * 128).rearrange("p (m q t) -> p m q t", m=4, q=4)
        Rmask = ares("masks")
        TBL = []
        for par in range(2):
            d_ = {}
            for k_, nm in enumerate(["RH", "AH", "BH", "KH"]):
                d_[nm] = wbw(2, par * 1024 + 256 * k_, 512).rearrange("p (c t) -> p c t", c=4)
                d_["R" + nm] = ares("tb%d_%s" % (par, nm))
            TBL.append(d_)
        pcs = sb("pcs", [128, 2, 4])
        Rpcs = [Res("pcs0"), Res("pcs1")]

        def prep_chunk(d, ti, par):
            TB = TBL[par]
            shift(12, ti, pt(0), Rpt[0])
            shift(13, ti, pt(1), Rpt[1])
            s.op("act", lambda e: e.activation(tb0, pt(0), AF.Tanh), reads=[Rpt[0]], writes=[Rtb0])
            s.op("act", lambda e: e.copy(tb1, pt(1)), reads=[Rpt[1]], writes=[Rtb1])
            lo, hi = 64 * d, 64 * d + 64
            for c in range(4):
                b = bank()
                s.op("pe", lambda e, b=b, c=c: e.matmul(ps[b][:, 0:128], w2t_b[lo:hi, c * 128:(c + 1) * 128], tb0[lo:hi, :], start=True, stop=True),
                     reads=[Rlora, Rtb0], writes=[Rps[b]])
                s.op("act", lambda e, b=b, c=c: e.activation(pt(2), ps[b][:, 0:128], AF.Sigmoid, bias=cp("w0_%d" % d, c)),
                     reads=[Rps[b], Rcolp], writes=[Rpt[2]])
                b = bank()
                s.op("pe", lambda e, b=b, c=c: e.matmul(ps[b][:, 0:128], a2t_b[lo:hi, c * 128:(c + 1) * 128], tb1[lo:hi, :], start=True, stop=True),
                     reads=[Rlora, Rtb1], writes=[Rps[b]])
                s.op("act", lambda e, b=b, c=c: e.activation(pt(3), ps[b][:, 0:128], AF.Sigmoid, bias=cp("a0_%d" % d, c)),
                     reads=[Rps[b], Rcolp], writes=[Rpt[3]])
                s.op("dve", lambda e: e.tensor_scalar(pt(2), pt(2), -EXPM05, None, ALU.mult), reads=[Rpt[2]], writes=[Rpt[2]])
                if d == 0:
                    s.op("dve", lambda e: e.tensor_tensor_scan(pt(0), pt(11), pt(2), 0.0, ALU.mult, ALU.add),
                         reads=[Rpt[2], Rpt[11]], writes=[Rpt[0]])
                else:
                    s.op("dve", lambda e: e.tensor_tensor_scan(pt(0)[:, ::-1], pt(11), pt(2)[:, ::-1], 0.0, ALU.mult, ALU.add),
                         reads=[Rpt[2], Rpt[11]], writes=[Rpt[0]])
                s.op("act", lambda e: e.activation(pt(1), pt(0), AF.Exp), reads=[Rpt[0]], writes=[Rpt[1]])
                s.op("dve", lambda e: e.tensor_tensor(pt(2), pt(0), pt(2), ALU.subtract), reads=[Rpt[0], Rpt[2]], writes=[Rpt[2]])
                s.op("act", lambda e: e.activation(pt(2), pt(2), AF.Exp), reads=[Rpt[2]], writes=[Rpt[2]])
                s.op("act", lambda e: e.activation(pt(0), pt(0), AF.Exp, scale=-1.0), reads=[Rpt[0]], writes=[Rpt[0]])
                yield
                pcol = 127 if d == 0 else 0
                s.op("dve", lambda e, c=c, pcol=pcol: e.tensor_copy(pcs[:, par, c:c + 1], pt(1)[:, pcol:pcol + 1]),
                     reads=[Rpt[1]], writes=[Rpcs[par]])
                yield
                shift(4 + c, ti, pt(4), Rpt[4])
                s.op("dve", lambda e, c=c: e.tensor_scalar(pt(5), pt(4), cp("kvec0", c), None, ALU.mult),
                     reads=[Rpt[4], Rcolp], writes=[Rpt[5]])
                s.op("act", lambda e: e.activation(pt(6), pt(5), AF.Square), reads=[Rpt[5]], writes=[Rpt[6]])
                b = bank()
                s.op("pe", lambda e, b=b: e.matmul(ps[b][:, 0:128], bones_f[:], pt(6), start=True, stop=True),
                     reads=[Rconst, Rpt[6]], writes=[Rps[b]])
                s.op("dve", lambda e, b=b: e.tensor_scalar(pt(7), ps[b][:, 0:128], 1e-12, None, ALU.add),
                     reads=[Rps[b]], writes=[Rpt[7]])
                s.op("act", lambda e: e.activation(pt(7), pt(7), AF.Ln), reads=[Rpt[7]], writes=[Rpt[7]])
                s.op("act", lambda e: e.activation(pt(7), pt(7), AF.Exp, scale=-0.5), reads=[Rpt[7]], writes=[Rpt[7]])
                s.op("dve", lambda e: e.tensor_tensor(pt(8), pt(5), pt(7), ALU.mult), reads=[Rpt[5], Rpt[7]], writes=[Rpt[8]])
                yield
                s.op("dve", lambda e, c=c: e.scalar_tensor_tensor(TB["AH"][:, c, :], pt(8), -1.0, pt(2), ALU.mult, ALU.mult),
                     reads=[Rpt[8], Rpt[2]], writes=[TB["RAH"]])
                s.op("dve", lambda e: e.tensor_tensor(pt(6), pt(8), pt(3), ALU.mult), reads=[Rpt[8], Rpt[3]], writes=[Rpt[6]])
                s.op("dve", lambda e, c=c: e.tensor_tensor(TB["BH"][:, c, :], pt(6), pt(0), ALU.mult),
                     reads=[Rpt[6], Rpt[0]], writes=[TB["RBH"]])
                yield
                s.op("dve", lambda e, c=c: e.tensor_scalar(pt(9), pt(3), cp("kvec1", c), cx("omk1", c), ALU.mult, ALU.add),
                     reads=[Rpt[3], Rcolp, Rcolx], writes=[Rpt[9]])
                s.op("dve", lambda e: e.tensor_tensor(pt(9), pt(4), pt(9), ALU.mult), reads=[Rpt[4], Rpt[9]], writes=[Rpt[9]])
                s.op("dve", lambda e, c=c: e.tensor_tensor(TB["KH"][:, c, :], pt(9), pt(0), ALU.mult),
                     reads=[Rpt[9], Rpt[0]], writes=[TB["RKH"]])
                yield
                shift(c, ti, pt(10), Rpt[10])
                s.op("dve", lambda e, c=c: e.tensor_tensor(TB["RH"][:, c, :], pt(10), pt(1), ALU.mult),
                     reads=[Rpt[10], Rpt[1]], writes=[TB["RRH"]])
                yield
                s.op("dve", lambda e: e.tensor_tensor(pt(6), pt(10), pt(9), ALU.mult), reads=[Rpt[10], Rpt[9]], writes=[Rpt[6]])
                s.op("dve", lambda e, c=c: e.tensor_scalar(pt(6), pt(6), cp("kvec2", c), None, ALU.mult),
                     reads=[Rpt[6], Rcolp], writes=[Rpt[6]])
                b = bank()
                s.op("pe", lambda e, b=b: e.matmul(ps[b][:, 0:2], pt(6), halfsel_f[:, 0:2], start=True, stop=True),
                     reads=[Rpt[6], Rconst], writes=[Rps[b]])
                s.op("dve", lambda e, b=b, c=c, ti=ti: e.tensor_tensor(BS[:, ti, c * 2:c * 2 + 2], BS[:, ti, c * 2:c * 2 + 2], ps[b][:, 0:2], ALU.add),
                     reads=[Rps[b], RBS], writes=[RBS])
                yield

        def rwkv_chunked(ntiles=NT):
            cut = int(os.environ.get("CUT", 99))
            for m in range(4):
                for q in range(4):
                    s.dma("pool", masks[:, m, q, :], cmat[4 + m], writes=[Rmask])
            s.op("dve", lambda e: e.memset(pt(11), 1.0), writes=[Rpt[11]])
            MSK = {0: (0, 1, 2), 1: (1, 0, 3)}
            seq = []
            for d in range(2):
                order = list(range(NT)) if d == 0 else list(range(NT - 1, -1, -1))
                seq += [(d, ti) for ti in order[:ntiles]]
            for _ in prep_chunk(seq[0][0], seq[0][1], 0):
                pass
            if True:
                def tile_body(n):
                    d, ti = seq[n]
                    par = n % 2
                    Ad = Ast[:, d * 256:(d + 1) * 256]
                    Ad3 = Ad.rearrange("p (c i) -> p c i", c=4)
                    Abd = Abf[:, d * 256:(d + 1) * 256]
                    Abd3 = Abd.rearrange("p (c i) -> p c i", c=4)
                    mTs, mS, mTi = MSK[d]
                    TB = TBL[par]
                    nxt = prep_chunk(seq[n + 1][0], seq[n + 1][1], (n + 1) % 2) if n + 1 < len(seq) else iter(())
                    u = ti // 2
                    isstart = (ti % 2 == 0) if d == 0 else (ti % 2 == 1)
                    if isstart:
                        if d == 0:
                            if u == 0:
                                s.dma("sp", Ad, initS[0], writes=[RAst])
                            else:
                                s.op("dve", lambda e, u=u: e.tensor_scalar(Ad, Ad, cx("keepL", u), None, ALU.mult),
                                     reads=[RAst, Rcolx], writes=[RAst])
                        else:
                            if u == 4:
                                s.op("dve", lambda e: e.memset(Ad, 0.0), writes=[RAst])
                            elif u == 3:
                                s.dma("sp", stage_f[:, 0, 0:256], initS[1], writes=[Rstage[0]])
                                s.op("dve", lambda e, u=u: e.scalar_tensor_tensor(Ad, Ad, cx("keepR", u), stage_f[:, 0, 0:256], ALU.mult, ALU.add),
                                     reads=[RAst, Rcolx, Rstage[0]], writes=[RAst])
                            else:
                                s.op("dve", lambda e, u=u: e.tensor_scalar(Ad, Ad, cx("keepR", u), None, ALU.mult),
                                     reads=[RAst, Rcolx], writes=[RAst])
                        s.op("act", lambda e: e.copy(Abd, Ad), reads=[RAst], writes=[RAbf])
                    if cut < 2:
                        return
                    fy = bank()
                    reserved.add(fy)
                    first_fy = [True]
                    def group_body(g):
                        W = WS[g]
                        heads = [(c_, g) for c_ in range(4)]

                        def fm(T3, c, h2):
                            return T3[64 * h2:64 * h2 + 64, c, :]

                        def q4(tl):
                            return tl.rearrange("p (q t) -> p q t", q=4)
                        specs = [("Lt0", "BH", "AH", mTs), ("L0", "AH", "BH", mS), ("Lakt", "KH", "AH", mTs),
                                 ("Mrbt", "BH", "RH", mTi), ("Mrkt", "KH", "RH", mTi)]
                        lim = int(os.environ.get("LIM", 99))
                        for (dst, la, rb, mk) in specs[:lim]:
                            b = bank()
                            for q, (c, h2) in enumerate(heads[:int(os.environ.get("LIMH", 4))]):
                                s.op("pe", lambda e, b=b, q=q, c=c, h2=h2, la=la, rb=rb: e.matmul(
                                    ps[b][:, q * 128:(q + 1) * 128], fm(TB[la], c, h2), fm(TB[rb], c, h2), start=True, stop=True),
                                    reads=[TB["R" + la], TB["R" + rb]], writes=[Rps[b]])
                            if os.environ.get("NOEV") == "1":
                                continue
                            if os.environ.get("NOEV") == "2":
                                s.op("dve", lambda e, b=b, dst=dst, mk=mk: e.tensor_copy(W[dst], ps[b][:]),
                                     reads=[Rps[b], Rmask], writes=[W["R" + dst]])
                                continue
                            s.op("dve", lambda e, b=b, dst=dst, mk=mk: e.tensor_tensor(
                                q4(W[dst]), ps[b][:].rearrange("p (q t) -> p q t", q=4), masks[:, mk, :, :], ALU.mult),
                                reads=[Rps[b], Rmask], writes=[W["R" + dst]])
                        yield
                        if cut < 3:
                            return
                        b = bank()
                        for q, (c, h2) in enumerate(heads):
                            idn = ident_b[64 * h2:64 * h2 + 64, 64 * h2:64 * h2 + 64]
                            for k_, nm in enumerate(["AH", "BH", "KH"]):
                                s.op("pe", lambda e, b=b, q=q, c=c, h2=h2, nm=nm, k_=k_, idn=idn: e.transpose(
                                    psb[b][:, k_ * 256 + q * 64:k_ * 256 + q * 64 + 64], fm(TB[nm], c, h2), idn),
                                    reads=[TB["R" + nm], Rconst], writes=[Rps[b]])
                        Z0q = q4(W["Z0"])
                        BKq = q4(W["BKtm"])
                        s.op("act", lambda e, b=b, Z0q=Z0q: e.copy(Z0q[:, :, 0:64], psb[b][:, 0:256].rearrange("p (q j) -> p q j", q=4)),
                             reads=[Rps[b]], writes=[W["RZ0"]])
                        s.op("act", lambda e, b=b, BKq=BKq: e.copy(BKq[:, :, 0:64], psb[b][:, 256:512].rearrange("p (q j) -> p q j", q=4)),
                             reads=[Rps[b]], writes=[W["RBKtm"]])
                        s.op("act", lambda e, b=b, BKq=BKq: e.copy(BKq[:, :, 64:128], psb[b][:, 512:768].rearrange("p (q j) -> p q j", q=4)),
                             reads=[Rps[b]], writes=[W["RBKtm"]])

                        def Vq(c, h2):
                            return V_tm[:, ti, (c * 2 + h2) * 64:(c * 2 + h2) * 64 + 64]
                        if cut < 4:
                            return
                        b = bank()
                        Lakq = q4(W["Lakt"])
                        for q, (c, h2) in enumerate(heads):
                            s.op("pe", lambda e, b=b, q=q, c=c, h2=h2: e.matmul(
                                ps[b][:, q * 64:(q + 1) * 64], Lakq[:, q, :], Vq(c, h2), start=True, stop=True),
                                reads=[W["RLakt"], RVtm], writes=[Rps[b]])
                        s.op("act", lambda e, b=b, Z0q=Z0q: e.copy(Z0q[:, :, 64:128], ps[b][:, 0:256].rearrange("p (q j) -> p q j", q=4)),
                             reads=[Rps[b]], writes=[W["RZ0"]])
                        yield
                        if cut < 5:
                            return
                        zc, lc = 0, 0
                        for k in range(7):
                            Zc, Zn = W["Z%d" % zc], W["Z%d" % (1 - zc)]
                            RZc, RZn = W["RZ%d" % zc], W["RZ%d" % (1 - zc)]
                            Ltc, Lc = W["Lt%d" % lc], W["L%d" % lc]
                            RLtc, RLc = W["RLt%d" % lc], W["RL%d" % lc]
                            b = bank()
                            for q in range(4):
                                s.op("pe", lambda e, b=b, q=q, Zc=Zc, Ltc=Ltc: e.matmul(
                                    ps[b][:, q * 128:(q + 1) * 128], q4(Ltc)[:, q, :], q4(Zc)[:, q, :], start=True, stop=True),
                                    reads=[RZc, RLtc], writes=[Rps[b]])
                            s.op("dve", lambda e, b=b, Zn=Zn, Zc=Zc: e.tensor_tensor(Zn, ps[b][:], Zc, ALU.add),
                                 reads=[Rps[b], RZc], writes=[RZn])
                            if k < 6:
                                Ltn, Ln = W["Lt%d" % (1 - lc)], W["L%d" % (1 - lc)]
                                RLtn, RLn = W["RLt%d" % (1 - lc)], W["RL%d" % (1 - lc)]
                                b2 = bank()
                                for q in range(4):
                                    s.op("pe", lambda e, b2=b2, q=q, Lc=Lc, Ltc=Ltc: e.matmul(
                                        ps[b2][:, q * 128:(q + 1) * 128], q4(Lc)[:, q, :], q4(Ltc)[:, q, :], start=True, stop=True),
                                        reads=[RLc, RLtc], writes=[Rps[b2]])
                                s.op("act", lambda e, b2=b2, Ltn=Ltn: e.copy(Ltn, ps[b2][:]), reads=[Rps[b2]], writes=[RLtn])
                                if k < 5:
                                    b3 = bank()
                                    for q in range(4):
                                        s.op("pe", lambda e, b3=b3, q=q, Lc=Lc, Ltc=Ltc: e.matmul(
                                            ps[b3][:, q * 128:(q + 1) * 128], q4(Ltc)[:, q, :], q4(Lc)[:, q, :], start=True, stop=True),
                                            reads=[RLc, RLtc], writes=[Rps[b3]])
                                    s.op("act", lambda e, b3=b3, Ln=Ln: e.copy(Ln, ps[b3][:]), reads=[Rps[b3]], writes=[RLn])
                                lc = 1 - lc
                            zc = 1 - zc
                            yield
                        s.op("act", lambda e, zc=zc: e.copy(W["Zfb"], W["Z%d" % zc]), reads=[W["RZ%d" % zc]], writes=[W["RZfb"]])
                        Zf = q4(W["Zfb"])
                        RZf = W["RZfb"]
                        Mrbq = q4(W["Mrbt"])
                        Mrkq = q4(W["Mrkt"])
                        if cut < 6:
                            return
                        bM = bank()
                        bN = bank()
                        bR = bank()
                        for q, (c, h2) in enumerate(heads):
                            cl_ = c
                            prt = slice(64 * h2, 64 * h2 + 64)
                            s.op("pe", lambda e, q=q, cl_=cl_, prt=prt: e.matmul(
                                ps[bM][prt, cl_ * 64:(cl_ + 1) * 64], Zf[:, q, 0:64], BKq[:, q, 0:64],
                                start=(q == 0), stop=False, skip_group_check=True),
                                reads=[RZf, W["RBKtm"]], writes=[Rps[bM]])
                            s.op("pe", lambda e, q=q, cl_=cl_, prt=prt: e.matmul(
                                ps[bM][prt, cl_ * 64:(cl_ + 1) * 64], ident_b[0:64, 0:64], ident_b[0:64, 0:64],
                                start=False, stop=True, skip_group_check=True),
                                reads=[Rconst], writes=[Rps[bM]])
                            s.op("pe", lambda e, q=q, cl_=cl_, prt=prt: e.matmul(
                                ps[bN][prt, cl_ * 64:(cl_ + 1) * 64], BKq[:, q, 0:64], Zf[:, q, 64:128],
                                start=(q == 0), stop=False, skip_group_check=True),
                                reads=[RZf, W["RBKtm"]], writes=[Rps[bN]])
                            s.op("pe", lambda e, q=q, cl_=cl_, prt=prt, c=c, h2=h2: e.matmul(
                                ps[bN][prt, cl_ * 64:(cl_ + 1) * 64], BKq[:, q, 64:128], Vq(c, h2),
                                start=False, stop=False, skip_group_check=True),
                                reads=[W["RBKtm"], RVtm], writes=[Rps[bN]])
                            s.op("pe", lambda e, q=q, cl_=cl_, prt=prt: e.matmul(
                                ps[bR][prt, cl_ * 128:(cl_ + 1) * 128], Zf[:, q, 0:64], Mrbq[:, q, :],
                                start=(q == 0), stop=False, skip_group_check=True),
                                reads=[RZf, W["RMrbt"]], writes=[Rps[bR]])
                            s.op("pe", lambda e, q=q, cl_=cl_, prt=prt, c=c, h2=h2: e.matmul(
                                ps[bR][prt, cl_ * 128:(cl_ + 1) * 128], ident_b[prt, prt], fm(TB["RH"], c, h2),
                                start=False, stop=True, skip_group_check=True),
                                reads=[Rconst, TB["RRH"]], writes=[Rps[bR]])
                            st_ = first_fy[0]
                            first_fy[0] = False
                            hh = c * 2 + h2
                            s.op("pe", lambda e, q=q, hh=hh, st_=st_: e.matmul(
                                ps[fy][:, hh * 64:(hh + 1) * 64], Mrbq[:, q, :], Zf[:, q, 64:128],
                                start=st_, stop=False, skip_group_check=True),
                                reads=[RZf, W["RMrbt"]], writes=[Rps[fy]])
                            s.op("pe", lambda e, q=q, hh=hh, c=c, h2=h2: e.matmul(
                                ps[fy][:, hh * 64:(hh + 1) * 64], Mrkq[:, q, :], Vq(c, h2),
                                start=False, stop=False, skip_group_check=True),
                                reads=[W["RMrkt"], RVtm], writes=[Rps[fy]])
                        MTv = W["MT"].rearrange("p (c j) -> p c j", c=4)
                        RTv = W["RT"].rearrange("p (c t) -> p c t", c=4)
                        pg = slice(64 * g, 64 * g + 64)
                        s.op("act", lambda e, MTv=MTv, pg=pg: e.copy(MTv[pg], ps[bM][pg, 0:256].rearrange("p (c j) -> p c j", c=4)),
                             reads=[Rps[bM]], writes=[W["RMT"]])
                        s.op("act", lambda e, RTv=RTv, pg=pg: e.copy(RTv[pg], ps[bR][pg, 0:512].rearrange("p (c t) -> p c t", c=4)),
                             reads=[Rps[bR]], writes=[W["RRT"]])
                        for q, (c, h2) in enumerate(heads):
                            cl_ = c
                            prt = slice(64 * h2, 64 * h2 + 64)
                            hh = c * 2 + h2
                            s.op("pe", lambda e, cl_=cl_, prt=prt, hh=hh, c=c, RTv=RTv: e.matmul(
                                ps[fy][:, hh * 64:(hh + 1) * 64], RTv[prt, cl_, :], Abd3[prt, c, :],
                                start=False, stop=True, skip_group_check=True),
                                reads=[W["RRT"], RAbf], writes=[Rps[fy]])
                            s.op("pe", lambda e, cl_=cl_, prt=prt, c=c, MTv=MTv: e.matmul(
                                ps[bN][prt, cl_ * 64:(cl_ + 1) * 64], MTv[prt, cl_, :], Abd3[prt, c, :],
                                start=False, stop=True, skip_group_check=True),
                                reads=[W["RMT"], RAbf], writes=[Rps[bN]])
                        s.op("dve", lambda e, pg=pg: e.tensor_tensor(
                            Ad3[pg, :, :], ps[bN][pg, 0:256].rearrange("p (c i) -> p c i", c=4),
                            pcs[pg, par, :].unsqueeze(2).to_broadcast([64, 4, 64]), ALU.mult),
                            reads=[Rps[bN], Rpcs[par], RAbf], writes=[RAst])
                    gens = [group_body(0), group_body(1), nxt]
                    while gens:
                        for gn in list(gens):
                            try:
                                next(gn)
                            except StopIteration:
                                gens.remove(gn)
                    if cut < 6:
                        reserved.discard(fy)
                        return
                    s.op("act", lambda e: e.copy(Abd, Ad), reads=[RAst], writes=[RAbf])
                    isend = (ti % 2 == 1) if d == 0 else (ti % 2 == 0)
                    if isend:
                        s.dma("sp", s_out[u, d], Ad, reads=[RAst])
                    reserved.discard(fy)
                    if d == 0:
                        s.op("act", lambda e, ti=ti: e.copy(YSb[:, ti, :], ps[fy][:]), reads=[Rps[fy]], writes=[RYS])
                    else:
                        finalize(ti, fy)
                for n_ in range(len(seq)):
                    tile_body(n_)

        if stage >= 1:
            if "a" in sub:
                make_gg(0)
            if "b" in sub:
                make_hT(0, 0)
            if "c" in sub:
                even_inproj()
        if stage >= 3:
            attention()
            if P0L1:
                for _ in P0L1[0]:
                    pass
                del P0L1[:]
            barrier()
            if stage >= 5:
                rwkv_setup()
                rwkv_prepass()
                rwkv_chunked(NT if stage >= 6 else 2)
            else:
                s.op("pool", lambda e: e.memset(yT[:, 4:8, :], 0.0), writes=[RyT])
            barrier()
            out_proj(w_out_even, yT, RyT)
        if stage >= 2:
            barrier()
            ffn(0)
            barrier()
        if stage >= 4:
            if P0L1:
                for _ in P0L1[0]:
                    pass
                del P0L1[:]
            make_gg(1)
            make_hT(1, 0)
            fnet()
            out_proj(w_out_odd, yT, RyT)
            barrier()
            ffn(1)
        for i in range(NT):
            s.dma("sp", y_out[i * 128:(i + 1) * 128, :], x_sb[:, i, :], reads=[Rx[i]])
        s.emit()
    return nc


def _core_units(c):
    if c < 6:
        return [("p", 5 * c + u) for u in range(5)]
    b = c - 6
    return [("s", b, u) for u in range(4)] + [("p", 30 + b)]


def _host_prep(inp):
    f32 = np.float32
    COFF, NCOL = colp_layout()
    ROFF, NROW = rowp_layout()
    g = {k: np.asarray(v) for k, v in inp.items()}
    colp = np.zeros((128, NCOL), f32)

    def put(name, vec):
        c = cols_of(vec)
        colp[:, COFF[name]:COFF[name] + c.shape[1]] = c
    for l in range(2):
        put("bada%d" % l, g["b_ada"][l])
        for k in range(4):
            put("gain%d_%d" % (l, k), g["norm_gains"][l, k])
        for k in range(3):
            put("conv%d_%d" % (l, k), g["ffn_conv"][l, k])
        put("convb%d" % l, g["ffn_conv_b"][l])
    put("mu0", g["rwkv_shift_mu"][0, 0])
    put("mu1", g["rwkv_shift_mu"][0, 1])
    for d in range(2):
        put("w0_%d" % d, g["rwkv_w0"][0, d])
        put("a0_%d" % d, g["rwkv_a0"][0, d])
    for k in range(3):
        put("kvec%d" % k, g["rwkv_kvec"][0, k])

    cmat = np.zeros((8, 128, 128), f32)
    cmat[0] = np.eye(128, dtype=f32)
    for i in range(64):
        cmat[1][2 * i, 2 * i + 1] = 1.0
        cmat[1][2 * i + 1, 2 * i] = -1.0
    cmat[2][:64, :64] = 1.0
    cmat[2][64:, 64:] = 1.0
    cmat[3][:64, 0] = 1.0
    cmat[3][64:, 1] = 1.0
    rr_, cc_ = np.meshgrid(np.arange(128), np.arange(128), indexing="ij")
    cmat[4] = (rr_ < cc_)
    cmat[5] = (rr_ > cc_)
    cmat[6] = (rr_ <= cc_)
    cmat[7] = (rr_ >= cc_)
    cc = np.arange(128)
    chang = 2.0 * np.pi * ((cc[:, None] * cc[None, :]) % 128) / 128.0
    chCS = np.concatenate([np.cos(chang), np.sin(chang)], axis=1).astype(f32)

    inv = (10000.0 ** (-np.arange(16, dtype=np.float32) / 16)).astype(f32)

    shared = dict(
        colp=colp, w_ada=g["w_ada"], w_in_even=g["w_in_even"][0], w_out_even=g["w_out_even"][0],
        w_out_odd=g["w_out_odd"][0], w_ffn_in=g["w_ffn_in"], w_ffn_out=g["w_ffn_out"],
        w2t=np.ascontiguousarray(g["rwkv_w2"][0].reshape(128, 512)),
        a2t=np.ascontiguousarray(g["rwkv_a2"][0].reshape(128, 512)),
        g2=np.ascontiguousarray(g["rwkv_g2"][0]), chCS=chCS, cmat=cmat,
        lnx=np.ascontiguousarray(g["rwkv_lnx"][0]))
    maps = []
    for c in range(NCORES):
        units = _core_units(c)
        xs = []
        for un in units:
            if un[0] == "p":
                xs.append(g["x_prompt"][un[1]])
            else:
                xs.append(g["x_sample"][un[1], un[2] * 256:(un[2] + 1) * 256])
        x_in = np.ascontiguousarray(np.concatenate(xs, axis=0), dtype=f32)
        is_s = c >= 6
        condA = g["c"][c - 6] if is_s else g["c_ctx"]
        condB = g["c_ctx"]
        cond = np.stack([condA, condB], axis=0).astype(f32)
        condT = np.ascontiguousarray(cond.reshape(2, 8, 128).transpose(2, 1, 0).reshape(128, 16))
        rowp = np.zeros((1, NROW), f32)
        rowp[0, ROFF["lam"]:ROFF["lam"] + 256] = g["diff_lambda"][0].reshape(-1)
        rowp[0, ROFF["subln"]:ROFF["subln"] + 128] = g["diff_subln"][0]
        ab = np.full((6, 5), -30000.0, f32)
        if is_s:
            ab[0:5, 0:4] = 0.0
            ab[5, 4] = 0.0
            bndL = [1, 0, 0, 0, 1]
            bndR = [0, 0, 0, 1, 1]
        else:
            for u in range(5):
                ab[u + 1, u] = 0.0
            bndL = [1] * 5
            bndR = [1] * 5
        rowp[0, ROFF["abias"]:ROFF["abias"] + 30] = ab.reshape(-1)
        rowp[0, ROFF["bndL"]:ROFF["bndL"] + 5] = bndL
        rowp[0, ROFF["bndR"]:ROFF["bndR"] + 5] = bndR
        ropeC = np.ones((128, T), f32)
        ropeS = np.zeros((128, T), f32)
        if is_s:
            t = np.arange(1024)
            row = (t // 64).astype(f32)
            col = (t % 64).astype(f32)
            ang = np.concatenate([row[:, None] * inv[None, :], col[:, None] * inv[None, :]], axis=1).astype(f32)
            pidx = (np.arange(128) % 64) // 2
            ropeC[:, :1024] = np.cos(ang)[:, pidx].T
            ropeS[:, :1024] = np.sin(ang)[:, pidx].T
        cacheKT = np.zeros((512, 256), f32)
        cacheV = np.zeros((256, 512), f32)
        initS = np.zeros((2, 128, 256), f32)
        if is_s:
            b = c - 6
            cacheKT[:] = g["cache_k"][b, 0].reshape(256, 512).T
            cacheV[:] = g["cache_v"][b, 0].reshape(256, 512)
            st = g["state_wkv"][b, 0]
            initS[:] = st.reshape(2, 4, 2, 64, 64).transpose(0, 2, 4, 1, 3).reshape(2, 128, 256)
        dC = np.zeros((T, T), np.float64)
        dS = np.zeros((T, T), np.float64)
        blocks = [(0, 1024), (1024, 256)] if is_s else [(256 * u, 256) for u in range(5)]
        for (a0, L) in blocks:
            ll = np.arange(L)
            ang = 2.0 * np.pi * ((ll[:, None] * ll[None, :]) % L) / L
            sc = 1.0 / math.sqrt(L * 128.0)
            dC[a0:a0 + L, a0:a0 + L] = np.cos(ang) * sc
            dS[a0:a0 + L, a0:a0 + L] = -np.sin(ang) * sc
        m = dict(shared)
        m.update(x_in=x_in, condT=condT, rowp=rowp, ropeC=ropeC, ropeS=ropeS, cacheKT=cacheKT, cacheV=cacheV,
                 initS=initS, dftC=dC.astype(f32), dftSn=dS.astype(f32))
        maps.append(m)
    return maps


_NC_CACHE = {}


def kernel(**inputs):
    maps = _host_prep(inputs)
    if "nc" not in _NC_CACHE:
        _NC_CACHE["nc"] = build()
    nc = _NC_CACHE["nc"]
    import os
    ncr = int(os.environ.get("NCR", NCORES))
    res = run_bass_kernel_spmd(nc, maps[:ncr], core_ids=list(range(ncr)))
    outs = list(res.results) + [res.results[0]] * (NCORES - ncr)
    y_prompt = np.zeros((32, 256, D), np.float32)
    y_sample = np.zeros((2, 1024, D), np.float32)
    nk = np.zeros((32, 1, 256, 4, 128), np.float32)
    nv = np.zeros((32, 1, 256, 4, 128), np.float32)
    ns = np.zeros((32, 1, 2, 8, 64, 64), np.float32)
    for c in range(NCORES):
        r = outs[c]
        for u, un in enumerate(_core_units(c)):
            sl = slice(u * 256, (u + 1) * 256)
            if un[0] == "p":
                bi = un[1]
                y_prompt[bi] = r["y_out"][sl]
                nk[bi, 0] = r["k_out"][sl].reshape(256, 4, 128)
                nv[bi, 0] = r["v_out"][sl].reshape(256, 4, 128)
                stt = r["s_out"][u]
                ns[bi, 0] = stt.reshape(2, 2, 64, 4, 64).transpose(0, 3, 1, 4, 2).reshape(2, 8, 64, 64)
            else:
                y_sample[un[1], un[2] * 256:(un[2] + 1) * 256] = r["y_out"][sl]
    return (y_prompt, y_sample, nk, nv, ns)
```

```python
import contextlib
import math
import numpy as np
import concourse.bass as bass
import concourse.mybir as mybir
from concourse.bass_utils import run_bass_kernel_spmd

F32 = mybir.dt.float32
BF16 = mybir.dt.bfloat16
AF = mybir.ActivationFunctionType
ALU = mybir.AluOpType
AX = mybir.AxisListType

T = 1280
NT = 10
U = 5
D = 1024
KC = 8
DFF = 2816
FC = 22
NCORES = 8
ARENA_W = 27200
EXPM05 = math.exp(-0.5)


class Res:
    __slots__ = ("name", "writer", "readers", "excl")

    def __init__(self, name, excl=False):
        self.name = name
        self.writer = None
        self.readers = []
        self.excl = excl


class Sched:
    ENGS = ("pe", "act", "dve", "pool", "sp")
    NDMA = 6

    def __init__(self, nc):
        self.nc = nc
        self.prog = {e: [] for e in self.ENGS}
        self.signal = {e: set() for e in self.ENGS}
        self.ndma = {e: 0 for e in self.ENGS}

    def _collect(self, reads, writes, eng=None):
        deps = []
        for r in reads:
            if r.writer is not None:
                deps.append(r.writer)
            if r.excl:
                deps.extend(t for t in r.readers if t[1] != eng)
        for w in writes:
            if w.writer is not None:
                deps.append(w.writer)
            deps.extend(w.readers)
        return deps

    def _commit(self, tok, reads, writes):
        for r in reads:
            r.readers.append(tok)
        for w in writes:
            w.writer = tok
            w.readers = []

    def op(self, eng, fn, reads=(), writes=()):
        deps = self._collect(reads, writes, eng)
        idx = len(self.prog[eng])
        if eng == "pe":
            deps = [d for d in deps if not (d[0] == "c" and d[1] == "pe")]
        for d in deps:
            if d[0] == "c":
                self.signal[d[1]].add(d[2])
        self.prog[eng].append(dict(fn=fn, deps=deps, kind="c"))
        tok = ("c", eng, idx)
        self._commit(tok, reads, writes)
        return tok

    def dma(self, eng, out, in_, reads=(), writes=()):
        deps = self._collect(reads, writes, eng)
        n = self.ndma[eng]
        self.ndma[eng] += 1
        if n >= self.NDMA:
            deps.append(("d", eng, n - self.NDMA))
        for d in deps:
            if d[0] == "c":
                self.signal[d[1]].add(d[2])
        self.prog[eng].append(dict(out=out, in_=in_, deps=deps, kind="d", n=n))
        tok = ("d", eng, n)
        self._commit(tok, reads, writes)
        return tok

    def emit(self):
        nc = self.nc
        with contextlib.ExitStack() as st:
            csem = {e: st.enter_context(nc.semaphore("c_" + e)) for e in self.ENGS}
            dsem = {e: [st.enter_context(nc.semaphore("d_%s_%d" % (e, i))) for i in range(self.NDMA)]
                    for e in self.ENGS if self.ndma[e] > 0}
            sigval = {}
            for e in self.ENGS:
                cnt = 0
                m = {}
                for i in range(len(self.prog[e])):
                    if i in self.signal[e]:
                        cnt += 1
                        m[i] = cnt
                sigval[e] = m

            def resolve(tok):
                if tok[0] == "c":
                    return csem[tok[1]], sigval[tok[1]][tok[2]], ("c", tok[1])
                e, n = tok[1], tok[2]
                return dsem[e][n % self.NDMA], 16 * (n // self.NDMA + 1), ("d", e, n % self.NDMA)

            def run_engine(e, h):
                waited = {}
                for i, ins in enumerate(self.prog[e]):
                    need = {}
                    for d in ins["deps"]:
                        sem, val, key = resolve(d)
                        if waited.get(key, 0) >= val:
                            continue
                        if key not in need or need[key][1] < val:
                            need[key] = (sem, val)
                    for key, (sem, val) in need.items():
                        h.wait_ge(sem, val)
                        waited[key] = val
                    if ins["kind"] == "c":
                        bi = ins["fn"](h)
                        if i in self.signal[e]:
                            bi.then_inc(csem[e], 1)
                    else:
                        n = ins["n"]
                        h.dma_start(out=ins["out"], in_=ins["in_"]).then_inc(dsem[e][n % self.NDMA], 16)
                if self.ndma[e] > 0:
                    n = self.ndma[e]
                    for slot in range(self.NDMA):
                        cnt = (n - slot + self.NDMA - 1) // self.NDMA if n > slot else 0
                        if cnt > 0:
                            h.wait_ge(dsem[e][slot], 16 * cnt)

            with nc.Block() as block:
                @block.tensor
                def _(eng):
                    run_engine("pe", eng)

                @block.scalar
                def _(eng):
                    run_engine("act", eng)

                @block.vector
                def _(eng):
                    run_engine("dve", eng)

                @block.gpsimd
                def _(eng):
                    run_engine("pool", eng)

                @block.sync
                def _(eng):
                    run_engine("sp", eng)


def colp_layout():
    off = {}
    n = 0

    def add(name, cols):
        nonlocal n
        off[name] = n
        n += cols
    for l in range(2):
        add("bada%d" % l, 48)
        for g in range(4):
            add("gain%d_%d" % (l, g), 8)
        for k in range(3):
            add("conv%d_%d" % (l, k), FC)
        add("convb%d" % l, FC)
    add("mu0", 15)
    add("mu1", 15)
    for d in range(2):
        add("w0_%d" % d, 4)
        add("a0_%d" % d, 4)
    for k in range(3):
        add("kvec%d" % k, 4)
    return off, n


def rowp_layout():
    off = {}
    n = 0

    def add(name, cols):
        nonlocal n
        off[name] = n
        n += cols
    add("lam", 256)
    add("subln", 128)
    add("abias", 30)
    add("bndL", 5)
    add("bndR", 5)
    return off, n


def cols_of(vec):
    v = np.asarray(vec, np.float32).reshape(-1, 128)
    return np.ascontiguousarray(v.T)


def build(stage=99):
    nc = bass.Bass("TRN2", target_bir_lowering=False)
    COFF, NCOL = colp_layout()
    ROFF, NROW = rowp_layout()

    def din(name, shape):
        return nc.dram_tensor(name, list(shape), F32, kind="ExternalInput").ap()

    def dout(name, shape):
        return nc.dram_tensor(name, list(shape), F32, kind="ExternalOutput").ap()

    x_in = din("x_in", [T, D])
    condT = din("condT", [128, 16])
    colp = din("colp", [128, NCOL])
    rowp = din("rowp", [1, NROW])
    w_ada = din("w_ada", [2, D, 6 * D])
    w_in_even = din("w_in_even", [D, 3456])
    w_out_even = din("w_out_even", [D, D])
    w_out_odd = din("w_out_odd", [D, D])
    w_ffn_in = din("w_ffn_in", [2, D, 2 * DFF])
    w_ffn_out = din("w_ffn_out", [2, DFF, D])
    w2t_d = din("w2t", [128, 512])
    a2t_d = din("a2t", [128, 512])
    g2_d = din("g2", [128, 512])
    cacheKT = din("cacheKT", [512, 256])
    cacheV = din("cacheV", [256, 512])
    ropeC = din("ropeC", [128, T])
    ropeS = din("ropeS", [128, T])
    initS = din("initS", [2, 128, 256])
    dftC = din("dftC", [T, T])
    dftSn = din("dftSn", [T, T])
    chCS = din("chCS", [128, 256])
    cmat = din("cmat", [8, 128, 128])
    lnx_d = din("lnx", [2, 512])

    y_out = dout("y_out", [T, D])
    k_out = dout("k_out", [T, 512])
    v_out = dout("v_out", [T, 512])
    s_out = dout("s_out", [U, 2, 128, 256])

    import os
    sub = os.environ.get("SUB", "abcmqv")
    with contextlib.ExitStack() as st:
        s = Sched(nc)

        def sb(name, shape, dt=F32):
            return st.enter_context(nc.sbuf_tensor(name, list(shape), dt))

        x_sb = sb("x_sb", [128, NT, D])
        Rx = [Res("x%d" % i) for i in range(NT)]
        WB = 4096
        wbuf = [sb("wb%d" % i, [128, WB], BF16) for i in range(3)]
        Rw = [Res("wb%d" % i) for i in range(3)]
        wctr = [0]
        colp_sb = sb("colp_sb", [128, NCOL])
        Rcolp = Res("colp")
        rowp_sb = sb("rowp_sb", [128, NROW])
        Rrowp = Res("rowp")
        ident_f = sb("ident_f", [128, 128])
        ident_b = sb("ident_b", [128, 128], BF16)
        rotT_b = sb("rotT_b", [128, 128], BF16)
        bones_f = sb("bones_f", [128, 128])
        halfsel_f = sb("halfsel_f", [128, 128])
        bones_b = sb("bones_b", [128, 128], BF16)
        Rconst = Res("const")
        gg_sb = sb("gg_sb", [128, 2, 2, D], BF16)
        Rgg = Res("gg")
        scond = sb("scond", [128, 16], BF16)
        condf = sb("condf", [128, 16])
        Rcond = Res("cond")
        modcol = sb("modcol", [128, 2, 96])
        Rmod = Res("modcol")
        scsh = sb("scsh", [128, 2, 2, 2, 16])
        Rscsh = Res("scsh")
        stat = sb("stat", [128, 64])
        Rstat = Res("stat")
        junk_b = sb("junk_b", [128, D], BF16)
        Rjunk = Res("junk")
        xn_b = sb("xn_b", [128, D], BF16)
        Rxn = Res("xn")
        stage_f = sb("stage_f", [128, 2, 512])
        Rstage = [Res("stage0"), Res("stage1")]
        stctr = [0]
        arena = sb("arena", [128, ARENA_W])
        Rar = {}

        def ares(name):
            if name not in Rar:
                Rar[name] = Res("ar_" + name)
            return Rar[name]

        def carve(off_w, nelem, dt):
            if dt == F32:
                return arena[:, off_w:off_w + nelem]
            return arena[:, off_w:off_w + (nelem + 1) // 2].bitcast(BF16)[:, 0:nelem]

        ps = [st.enter_context(nc.psum_tensor("ps%d" % b, [128, 512], F32)) for b in range(8)]
        psb = [p.bitcast(BF16) for p in ps]
        Rps = [Res("ps%d" % b, excl=True) for b in range(8)]
        bctr = [0]

        reserved = set()

        def bank():
            while True:
                b = bctr[0] % 8
                bctr[0] += 1
                if b not in reserved:
                    return b

        def barrier():
            allr = list(Rar.values()) + list(Rw)
            s.op("dve", lambda e: e.memset(stat[:, 63:64], 0.0), writes=allr + [Rstat])

        def wload(src_ap, kc, ncols):
            i = wctr[0] % 3
            wctr[0] += 1
            view = wbuf[i][:, 0:kc * ncols].rearrange("p (c n) -> p c n", c=kc)
            s.dma("pool", view, src_ap.rearrange("(c p) n -> p c n", p=128), writes=[Rw[i]])
            return view, Rw[i]

        s.dma("sp", colp_sb[:], colp, writes=[Rcolp])
        s.dma("sp", rowp_sb[:], rowp[0, :].partition_broadcast(128), writes=[Rrowp])
        s.dma("sp", ident_f[:], cmat[0], writes=[Rconst])
        s.dma("sp", bones_f[:], cmat[2], writes=[Rconst])
        s.dma("sp", halfsel_f[:], cmat[3], writes=[Rconst])
        s.dma("pool", ident_b[:], cmat[0], writes=[Rconst])
        s.dma("pool", rotT_b[:], cmat[1], writes=[Rconst])
        s.dma("pool", bones_b[:], cmat[2], writes=[Rconst])
        s.dma("sp", condf[:], condT, writes=[Rcond])
        for i in range(NT):
            s.dma("sp", x_sb[:, i, :], x_in[i * 128:(i + 1) * 128, :], writes=[Rx[i]])
        s.op("act", lambda e: e.activation(scond[:], condf[:], AF.Silu), reads=[Rcond], writes=[Rcond])

        def cp(name, c=0, n=1):
            o = COFF[name] + c
            return colp_sb[:, o:o + n]

        def p0_gen(l):
            b = bank()
            reserved.add(b)
            for v in range(6):
                for half in range(2):
                    wv, rw = wload(w_ada[l][:, v * 1024 + half * 512: v * 1024 + half * 512 + 512], KC, 512)
                    for j in range(4):
                        col = (v * 8 + half * 4 + j) * 2
                        for kc in range(KC):
                            s.op("pe", lambda e, wv=wv, j=j, kc=kc, col=col, b=b: e.matmul(
                                ps[b][:, col:col + 2], wv[:, kc, j * 128:(j + 1) * 128],
                                scond[:, kc * 2:kc * 2 + 2], start=(kc == 0), stop=(kc == KC - 1)),
                                reads=[rw, Rcond], writes=[Rps[b]])
                    yield
            s.op("dve", lambda e, l=l, b=b: e.tensor_tensor(
                modcol[:, l, :].rearrange("p (c a) -> p c a", a=2),
                ps[b][:, 0:96].rearrange("p (c a) -> p c a", a=2),
                cp("bada%d" % l, 0, 48).unsqueeze(2).to_broadcast([128, 48, 2]), ALU.add),
                reads=[Rps[b], Rcolp], writes=[Rmod])
            reserved.discard(b)

            def mv(v, l=l):
                return modcol[:, l, v * 16:(v + 1) * 16].rearrange("p (c a) -> p c a", a=2)

            def gain(g, l=l):
                return cp("gain%d_%d" % (l, g), 0, 8).unsqueeze(2).to_broadcast([128, 8, 2])
            for wi, (vs, vsh, g) in enumerate([(1, 0, 0), (4, 3, 2)]):
                sc = scsh[:, l, wi, 0, :].rearrange("p (c a) -> p c a", a=2)
                sh = scsh[:, l, wi, 1, :].rearrange("p (c a) -> p c a", a=2)
                s.op("dve", lambda e, sc=sc, vs=vs, mv=mv: e.tensor_scalar(sc, mv(vs), 1.0, None, ALU.add),
                     reads=[Rmod], writes=[Rscsh])
                s.op("dve", lambda e, sc=sc, g=g, gain=gain: e.tensor_tensor(sc, sc, gain(g), ALU.mult),
                     reads=[Rscsh, Rcolp], writes=[Rscsh])
                s.op("dve", lambda e, sh=sh, vsh=vsh, mv=mv: e.tensor_copy(sh, mv(vsh)),
                     reads=[Rmod], writes=[Rscsh])

        for _ in p0_gen(0):
            pass
        P0L1 = [p0_gen(1)]

        ggcol = sb("ggcol", [128, 2, 16])
        Rggcol = Res("ggcol")

        def make_gg(l):
            for wi, (vg, g) in enumerate([(2, 1), (5, 3)]):
                gc = ggcol[:, wi, :].rearrange("p (c a) -> p c a", a=2)
                s.op("dve", lambda e, gc=gc, vg=vg, g=g, l=l: e.tensor_tensor(
                    gc, modcol[:, l, vg * 16:(vg + 1) * 16].rearrange("p (c a) -> p c a", a=2),
                    cp("gain%d_%d" % (l, g), 0, 8).unsqueeze(2).to_broadcast([128, 8, 2]), ALU.mult),
                    reads=[Rmod, Rcolp], writes=[Rggcol])
                for a in range(2):
                    for hh in range(2):
                        b = bank()
                        for j in range(4):
                            c = hh * 4 + j
                            s.op("pe", lambda e, b=b, wi=wi, c=c, a=a, j=j: e.matmul(
                                ps[b][:, j * 128:(j + 1) * 128],
                                ggcol[:, wi, c * 2 + a:c * 2 + a + 1].to_broadcast([128, 128]),
                                ident_f[:], start=True, stop=True),
                                reads=[Rggcol, Rconst], writes=[Rps[b]])
                        s.op("act", lambda e, b=b, wi=wi, a=a, hh=hh: e.copy(
                            gg_sb[:, wi, a, hh * 512:(hh + 1) * 512], ps[b][:]),
                            reads=[Rps[b]], writes=[Rgg])

        hT = carve(0, 8 * T, BF16).rearrange("p (c t) -> p c t", c=8)
        RhT = ares("hT")

        def ab_of_tile(i):
            return 0 if i < 8 else 1

        def make_hT(l, wi):
            for i in range(NT):
                a = ab_of_tile(i)
                s.op("act", lambda e, i=i: e.activation(junk_b[:], x_sb[:, i, :], AF.Square, scale=1.0 / 32.0,
                                                        accum_out=stat[:, 0:1]),
                     reads=[Rx[i]], writes=[Rjunk, Rstat])
                s.op("dve", lambda e: e.tensor_scalar(stat[:, 1:2], stat[:, 0:1], 1e-6, None, ALU.add),
                     reads=[Rstat], writes=[Rstat])
                s.op("act", lambda e: e.activation(stat[:, 1:2], stat[:, 1:2], AF.Ln), reads=[Rstat], writes=[Rstat])
                s.op("act", lambda e: e.activation(stat[:, 2:3], stat[:, 1:2], AF.Exp, scale=-0.5), reads=[Rstat], writes=[Rstat])
                s.op("dve", lambda e, i=i: e.tensor_scalar(xn_b[:], x_sb[:, i, :], stat[:, 2:3], None, ALU.mult),
                     reads=[Rx[i], Rstat], writes=[Rxn])
                b = bank()
                for c in range(8):
                    s.op("pe", lambda e, b=b, c=c: e.transpose(psb[b][:, c * 128:(c + 1) * 128],
                                                               xn_b[:, c * 128:(c + 1) * 128], ident_b[:]),
                         reads=[Rxn, Rconst], writes=[Rps[b]])
                for c in range(8):
                    s.op("act", lambda e, b=b, c=c, i=i, a=a: e.activation(
                        hT[:, c, i * 128:(i + 1) * 128], psb[b][:, c * 128:(c + 1) * 128], AF.Identity,
                        bias=scsh[:, l, wi, 1, c * 2 + a:c * 2 + a + 1],
                        scale=scsh[:, l, wi, 0, c * 2 + a:c * 2 + a + 1]),
                        reads=[Rps[b], Rscsh], writes=[RhT])

        TOKCH = [(0, 512), (512, 512), (1024, 256)]

        def linear_fm(wv, rw, ncol0, actT, Ract, kc_n, evac):
            for (t0, tn) in TOKCH:
                b = bank()
                for kc in range(kc_n):
                    s.op("pe", lambda e, b=b, kc=kc, t0=t0, tn=tn: e.matmul(
                        ps[b][:, 0:tn], wv[:, kc, ncol0:ncol0 + 128], actT[:, kc, t0:t0 + tn],
                        start=(kc == 0), stop=(kc == kc_n - 1)),
                        reads=[rw, Ract], writes=[Rps[b]])
                evac(b, t0, tn)

        def linear_tm(wv, rw, ncols, actT, Ract, kc_n, i, b, col0=0):
            for kc in range(kc_n):
                s.op("pe", lambda e, kc=kc: e.matmul(
                    ps[b][:, col0:col0 + ncols], actT[:, kc, i * 128:(i + 1) * 128], wv[:, kc, 0:ncols],
                    start=(kc == 0), stop=(kc == kc_n - 1)),
                    reads=[rw, Ract], writes=[Rps[b]])

        def residual_update(i, banks, wi, fsrc=None, Rf=None):
            a = ab_of_tile(i)
            if fsrc is None:
                for hh, b in enumerate(banks):
                    s.op("act", lambda e, b=b, hh=hh: e.activation(junk_b[:, 0:512], ps[b][:], AF.Square,
                                                                   scale=1.0 / 32.0, accum_out=stat[:, 8 + hh:9 + hh]),
                         reads=[Rps[b]], writes=[Rjunk, Rstat])
                s.op("dve", lambda e: e.tensor_tensor(stat[:, 10:11], stat[:, 8:9], stat[:, 9:10], ALU.add),
                     reads=[Rstat], writes=[Rstat])
            else:
                s.op("act", lambda e: e.activation(junk_b[:], fsrc, AF.Square, scale=1.0 / 32.0,
                                                   accum_out=stat[:, 10:11]),
                     reads=[Rf], writes=[Rjunk, Rstat])
            s.op("dve", lambda e: e.tensor_scalar(stat[:, 11:12], stat[:, 10:11], 1e-6, None, ALU.add),
                 reads=[Rstat], writes=[Rstat])
            s.op("act", lambda e: e.activation(stat[:, 11:12], stat[:, 11:12], AF.Ln), reads=[Rstat], writes=[Rstat])
            s.op("act", lambda e: e.activation(stat[:, 12:13], stat[:, 11:12], AF.Exp, scale=-0.5), reads=[Rstat], writes=[Rstat])
            for hh in range(2):
                src = ps[banks[hh]][:] if fsrc is None else fsrc[:, hh * 512:(hh + 1) * 512]
                rr = [Rps[banks[hh]]] if fsrc is None else [Rf]
                si = stctr[0] % 2
                stctr[0] += 1
                s.op("dve", lambda e, src=src, hh=hh, si=si: e.scalar_tensor_tensor(
                    stage_f[:, si, :], src, stat[:, 12:13], gg_sb[:, wi, a, hh * 512:(hh + 1) * 512],
                    ALU.mult, ALU.mult),
                    reads=rr + [Rstat, Rgg], writes=[Rstage[si]])
                s.op("dve", lambda e, hh=hh, si=si, i=i: e.tensor_tensor(
                    x_sb[:, i, hh * 512:(hh + 1) * 512], x_sb[:, i, hh * 512:(hh + 1) * 512], stage_f[:, si, :], ALU.add),
                    reads=[Rstage[si], Rx[i]], writes=[Rx[i]])

        actT = carve(5120, FC * T, BF16).rearrange("p (c t) -> p c t", c=FC)
        RactT = ares("actT")
        graw = carve(19200, T + 2, F32)
        Rgraw = ares("graw")
        cbuf = carve(19200 + 1284, T, F32)
        Rcbuf = ares("cbuf")
        sbuf_s = carve(19200 + 1284 + 1280, T, F32)
        Rsbuf = ares("sbuf_s")
        fstageA = carve(19200, 5 * D, F32).rearrange("p (i n) -> p i n", i=5)
        fstageB = carve(0, 5 * D, F32).rearrange("p (i n) -> p i n", i=5)

        def fst(i):
            return fstageA[:, i, :] if i < 5 else fstageB[:, i - 5, :]

        def Rfst_of(i):
            return ares("fstage") if i < 5 else RhT
        bcorr = sb("bcorr", [128, 16])
        Rbcorr = Res("bcorr")

        def ffn(l):
            make_hT(l, 1)
            s.op("dve", lambda e: e.memset(graw[:, 0:1], 0.0), writes=[Rgraw])
            s.op("dve", lambda e: e.memset(graw[:, T + 1:T + 2], 0.0), writes=[Rgraw])
            for blk in range(0, FC, 2):
                wu, ru = wload(w_ffn_in[l][:, blk * 128: blk * 128 + 256], KC, 256)
                wg, rg = wload(w_ffn_in[l][:, DFF + blk * 128: DFF + blk * 128 + 256], KC, 256)
                for jj in range(2):
                    fc = blk + jj
                    def evac_g(b, t0, tn):
                        s.op("act", lambda e, b=b, t0=t0, tn=tn: e.copy(graw[:, 1 + t0:1 + t0 + tn], ps[b][:, 0:tn]),
                             reads=[Rps[b]], writes=[Rgraw])
                    linear_fm(wg, rg, jj * 128, hT, RhT, KC, evac_g)
                    w0 = cp("conv%d_0" % l, fc)
                    w1 = cp("conv%d_1" % l, fc)
                    w2 = cp("conv%d_2" % l, fc)
                    cb = cp("convb%d" % l, fc)
                    s.op("act", lambda e, w1=w1, cb=cb: e.activation(cbuf[:], graw[:, 1:T + 1], AF.Identity, bias=cb, scale=w1),
                         reads=[Rgraw, Rcolp], writes=[Rcbuf])
                    s.op("dve", lambda e, w0=w0: e.scalar_tensor_tensor(cbuf[:], graw[:, 0:T], w0, cbuf[:], ALU.mult, ALU.add),
                         reads=[Rgraw, Rcolp, Rcbuf], writes=[Rcbuf])
                    s.op("dve", lambda e, w2=w2: e.scalar_tensor_tensor(cbuf[:], graw[:, 2:T + 2], w2, cbuf[:], ALU.mult, ALU.add),
                         reads=[Rgraw, Rcolp, Rcbuf], writes=[Rcbuf])
                    gprev = graw[:, 256:256 + 1024].rearrange("p (u k) -> p u k", k=256)[:, :, 0]
                    gnext = graw[:, 257:257 + 1024].rearrange("p (u k) -> p u k", k=256)[:, :, 0]
                    s.op("dve", lambda e, gprev=gprev: e.tensor_tensor(bcorr[:, 0:4], gprev, rowp_sb[:, ROFF["bndL"] + 1:ROFF["bndL"] + 5], ALU.mult),
                         reads=[Rgraw, Rrowp], writes=[Rbcorr])
                    s.op("dve", lambda e, w0=w0: e.tensor_scalar(bcorr[:, 0:4], bcorr[:, 0:4], w0, None, ALU.mult),
                         reads=[Rbcorr, Rcolp], writes=[Rbcorr])
                    c_at = cbuf[:, 256:256 + 1024].rearrange("p (u k) -> p u k", k=256)[:, :, 0]
                    s.op("dve", lambda e, c_at=c_at: e.tensor_tensor(c_at, c_at, bcorr[:, 0:4], ALU.subtract),
                         reads=[Rbcorr, Rcbuf], writes=[Rcbuf])
                    s.op("dve", lambda e, gnext=gnext: e.tensor_tensor(bcorr[:, 4:8], gnext, rowp_sb[:, ROFF["bndL"] + 1:ROFF["bndL"] + 5], ALU.mult),
                         reads=[Rgraw, Rrowp], writes=[Rbcorr])
                    s.op("dve", lambda e, w2=w2: e.tensor_scalar(bcorr[:, 4:8], bcorr[:, 4:8], w2, None, ALU.mult),
                         reads=[Rbcorr, Rcolp], writes=[Rbcorr])
                    c_at2 = cbuf[:, 255:255 + 1024].rearrange("p (u k) -> p u k", k=256)[:, :, 0]
                    s.op("dve", lambda e, c_at2=c_at2: e.tensor_tensor(c_at2, c_at2, bcorr[:, 4:8], ALU.subtract),
                         reads=[Rbcorr, Rcbuf], writes=[Rcbuf])
                    s.op("act", lambda e: e.activation(sbuf_s[:], cbuf[:], AF.Silu), reads=[Rcbuf], writes=[Rsbuf])
                    def evac_u(b, t0, tn, fc=fc):
                        s.op("dve", lambda e, b=b, t0=t0, tn=tn: e.tensor_tensor(
                            actT[:, fc, t0:t0 + tn], ps[b][:, 0:tn], sbuf_s[:, t0:t0 + tn], ALU.mult),
                            reads=[Rps[b], Rsbuf], writes=[RactT])
                    linear_fm(wu, ru, jj * 128, hT, RhT, KC, evac_u)
            foA = carve(24320, FC * 256, BF16).rearrange("p (c n) -> p c n", c=FC)
            RfoA = ares("foA")
            foB0 = wbuf[0][:, 0:16 * 256].rearrange("p (c n) -> p c n", c=16)
            foB1 = wbuf[1][:, 0:6 * 256].rearrange("p (c n) -> p c n", c=6)
            for nb in range(4):
                src = w_ffn_out[l][:, nb * 256:(nb + 1) * 256].rearrange("(c p) n -> p c n", p=128)
                if nb % 2 == 0:
                    s.dma("pool", foA, src, writes=[RfoA])

                    def rhs_of(kc):
                        return foA[:, kc, :], RfoA
                else:
                    s.dma("pool", foB0, src[:, 0:16, :], writes=[Rw[0]])
                    s.dma("pool", foB1, src[:, 16:22, :], writes=[Rw[1]])

                    def rhs_of(kc):
                        return (foB0[:, kc, :], Rw[0]) if kc < 16 else (foB1[:, kc - 16, :], Rw[1])
                for i in range(NT):
                    b = bank()
                    for kc in range(FC):
                        rv, rr_ = rhs_of(kc)
                        s.op("pe", lambda e, b=b, kc=kc, i=i, rv=rv: e.matmul(
                            ps[b][:, 0:256], actT[:, kc, i * 128:(i + 1) * 128], rv,
                            start=(kc == 0), stop=(kc == FC - 1)),
                            reads=[rr_, RactT], writes=[Rps[b]])
                    s.op("act", lambda e, b=b, i=i, nb=nb: e.copy(fst(i)[:, nb * 256:(nb + 1) * 256], ps[b][:, 0:256]),
                         reads=[Rps[b]], writes=[Rfst_of(i)])
            for i in range(NT):
                residual_update(i, None, 1, fsrc=fst(i), Rf=Rfst_of(i))

        qT = carve(5120, 4 * T, BF16).rearrange("p (c t) -> p c t", c=4)
        RqT = ares("qT")
        kT = carve(7680, 4 * 1536, BF16).rearrange("p (c t) -> p c t", c=4)
        RkT = ares("kT")
        Vaug = carve(10752, 12 * 4 * 144, BF16).rearrange("p (k h d) -> p k h d", k=12, h=4)
        RV = ares("Vaug")
        yT = carve(0, 8 * T, BF16).rearrange("p (c t) -> p c t", c=8)
        fbT = carve(14208, 15 * (T + 2), BF16).rearrange("p (c t) -> p c t", c=15)
        RfbT = ares("fbT")
        RyT = RhT
        ropeC_sb = carve(23824, T, F32)
        ropeS_sb = carve(23824 + T, T, F32)
        Rrope = Res("rope")
        rtmp = sb("rtmp", [128, 2, 512])
        Rrtmp = Res("rtmp")
        raw_b = sb("raw_b", [128, 512], BF16)
        Rraw = Res("raw_b")

        def even_inproj():
            s.dma("sp", ropeC_sb[:], ropeC, writes=[Rrope])
            s.dma("sp", ropeS_sb[:], ropeS, writes=[Rrope])
            s.dma("pool", kT[:, :, 0:256], cacheKT.rearrange("(h p) t -> p h t", p=128), writes=[RkT])
            for kk_ in range(2):
                s.dma("pool", Vaug[:, kk_, :, 0:128],
                      cacheV[kk_ * 128:(kk_ + 1) * 128, :].rearrange("p (h d) -> p h d", h=4), writes=[RV])
            if "m" in sub:
                s.op("pool", lambda e: e.memset(Vaug[:, :, :, 128:129], 1.0), writes=[RV])
            for which, dst, doff in ((0, qT, 0), (1, kT, 256)):
                if "q" not in sub:
                    break
                wv, rw = wload(w_in_even[:, which * 512:(which + 1) * 512], KC, 512)
                Rdst = RqT if which == 0 else RkT
                for h in range(4):
                    def evac(b, t0, tn, h=h, dst=dst, doff=doff, Rdst=Rdst):
                        s.op("act", lambda e, b=b, tn=tn: e.copy(raw_b[:, 0:tn], ps[b][:, 0:tn]),
                             reads=[Rps[b]], writes=[Rraw])
                        b2 = bank()
                        s.op("pe", lambda e, b2=b2, tn=tn: e.matmul(ps[b2][:, 0:tn], rotT_b[:], raw_b[:, 0:tn], start=True, stop=True),
                             reads=[Rraw, Rconst], writes=[Rps[b2]])
                        s.op("dve", lambda e, t0=t0, tn=tn: e.tensor_tensor(rtmp[:, 0, 0:tn], raw_b[:, 0:tn], ropeC_sb[:, t0:t0 + tn], ALU.mult),
                             reads=[Rraw, Rrope], writes=[Rrtmp])
                        s.op("dve", lambda e, b2=b2, t0=t0, tn=tn: e.tensor_tensor(rtmp[:, 1, 0:tn], ps[b2][:, 0:tn], ropeS_sb[:, t0:t0 + tn], ALU.mult),
                             reads=[Rps[b2], Rrope], writes=[Rrtmp])
                        s.op("dve", lambda e, t0=t0, tn=tn: e.tensor_tensor(dst[:, h, doff + t0:doff + t0 + tn], rtmp[:, 0, 0:tn], rtmp[:, 1, 0:tn], ALU.add),
                             reads=[Rrtmp], writes=[Rdst])
                    linear_fm(wv, rw, h * 128, hT, RhT, KC, evac)
                if which == 1:
                    for i in range(NT):
                        b = bank()
                        linear_tm(wv, rw, 512, hT, RhT, KC, i, b)
                        si = stctr[0] % 2
                        stctr[0] += 1
                        s.op("act", lambda e, b=b, si=si: e.copy(stage_f[:, si, :], ps[b][:]), reads=[Rps[b]], writes=[Rstage[si]])
                        s.dma("sp", k_out[i * 128:(i + 1) * 128, :], stage_f[:, si, :], reads=[Rstage[si]])
            s.op("pool", lambda e: e.memset(fbT[:, :, 0:1], 0.0), writes=[RfbT])
            s.op("pool", lambda e: e.memset(fbT[:, :, T + 1:T + 2], 0.0), writes=[RfbT])
            for pc, (c0, ncols) in enumerate([(1536, 512), (2048, 512), (2560, 512), (3072, 384)]):
                wv, rw = wload(w_in_even[:, c0:c0 + ncols], KC, ncols)
                for j in range(ncols // 128):
                    ch = pc * 4 + j

                    def evac_fb(b, t0, tn, ch=ch):
                        s.op("act", lambda e, b=b, t0=t0, tn=tn: e.copy(fbT[:, ch, 1 + t0:1 + t0 + tn], ps[b][:, 0:tn]),
                             reads=[Rps[b]], writes=[RfbT])
                    linear_fm(wv, rw, j * 128, hT, RhT, KC, evac_fb)
            wv, rw = wload(w_in_even[:, 1024:1536], KC, 512)
            for i in range(NT if "v" in sub else 0):
                b = bank()
                linear_tm(wv, rw, 512, hT, RhT, KC, i, b)
                si = stctr[0] % 2
                stctr[0] += 1
                s.op("act", lambda e, b=b, si=si: e.copy(stage_f[:, si, :], ps[b][:]), reads=[Rps[b]], writes=[Rstage[si]])
                s.dma("sp", v_out[i * 128:(i + 1) * 128, :], stage_f[:, si, :], reads=[Rstage[si]])
                s.op("dve", lambda e, si=si, i=i: e.tensor_copy(Vaug[:, 2 + i, :, 0:128], stage_f[:, si, :].rearrange("p (h d) -> p h d", h=4)),
                     reads=[Rstage[si]], writes=[RV])


        NPB = 4
        PT = [[sb("PT%d%d" % (m, k), [128, 256], BF16) for k in range(NPB)] for m in range(2)]
        RPT = [[Res("PT%d%d" % (m, k)) for k in range(NPB)] for m in range(2)]
        ya_f = sb("ya_f", [128, 128])
        Rya = Res("ya_f")
        ya_b = sb("ya_b", [128, 128], BF16)
        Ryab = Res("ya_b")
        subln08 = sb("subln08", [128, 128])
        Rsub = Res("subln08")
        lamt = sb("lamt", [128, 64])
        Rlamt = Res("lamt")

        def attention():
            lo = ROFF["lam"]
            for k in range(2):
                s.op("dve", lambda e, k=k: e.tensor_tensor(lamt[:], rowp_sb[:, lo + 128 * k: lo + 128 * k + 64],
                                                          rowp_sb[:, lo + 128 * k + 64: lo + 128 * k + 128], ALU.mult),
                     reads=[Rrowp], writes=[Rlamt])
                s.op("dve", lambda e, k=k: e.reduce_sum(stat[:, 20 + k:21 + k], lamt[:], axis=AX.X),
                     reads=[Rlamt], writes=[Rstat])
            s.op("act", lambda e: e.activation(stat[:, 22:24], stat[:, 20:22], AF.Exp), reads=[Rstat], writes=[Rstat])
            s.op("dve", lambda e: e.tensor_tensor(stat[:, 24:25], stat[:, 22:23], stat[:, 23:24], ALU.subtract),
                 reads=[Rstat], writes=[Rstat])
            s.op("dve", lambda e: e.tensor_scalar(stat[:, 25:26], stat[:, 24:25], 0.2, -1.0, ALU.add, ALU.mult),
                 reads=[Rstat], writes=[Rstat])
            s.op("dve", lambda e: e.tensor_scalar(subln08[:], rowp_sb[:, ROFF["subln"]:ROFF["subln"] + 128], 0.8, None, ALU.mult),
                 reads=[Rrowp], writes=[Rsub])
            pctr = [0, 0]
            for h in range(4):
                for qu in range(5):
                    if P0L1:
                        try:
                            next(P0L1[0])
                        except StopIteration:
                            del P0L1[:]
                    acc = [bank(), bank()]
                    reserved.update(acc)
                    pend = []
                    for kt in range(12):
                        ku = 0 if kt < 2 else 1 + (kt - 2) // 2
                        bcol = ROFF["abias"] + ku * 5 + qu
                        for m in range(2):
                            bs = bank()
                            s.op("pe", lambda e, bs=bs, m=m, h=h, kt=kt, qu=qu: e.matmul(
                                ps[bs][:, 0:256], kT[64 * m:64 * m + 64, h, kt * 128:(kt + 1) * 128],
                                qT[64 * m:64 * m + 64, h, qu * 256:(qu + 1) * 256], start=True, stop=True),
                                reads=[RkT, RqT], writes=[Rps[bs]])
                            pk = pctr[m] % NPB
                            pctr[m] += 1
                            s.op("act", lambda e, bs=bs, m=m, pk=pk, bcol=bcol: e.activation(
                                PT[m][pk][:], ps[bs][:, 0:256], AF.Exp, bias=rowp_sb[:, bcol:bcol + 1], scale=0.125),
                                reads=[Rps[bs], Rrowp], writes=[RPT[m][pk]])

                            def pv(m=m, pk=pk, kt=kt, h=h, acc=acc):
                                for qt in range(2):
                                    s.op("pe", lambda e, qt=qt: e.matmul(
                                        ps[acc[m]][:, qt * 129:qt * 129 + 129], PT[m][pk][:, qt * 128:(qt + 1) * 128],
                                        Vaug[:, kt, h, 0:129], start=(kt == 0 and qt == 0), stop=(kt == 11),
                                        skip_group_check=True),
                                        reads=[RPT[m][pk], RV], writes=[Rps[acc[m]]])
                            pend.append(pv)
                            if len(pend) > 2:
                                pend.pop(0)()
                    while pend:
                        pend.pop(0)()
                    reserved.difference_update(acc)
                    for qt in range(2):
                        i = qu * 2 + qt
                        c0 = qt * 129
                        s.op("dve", lambda e, c0=c0, acc=acc: e.reciprocal(stat[:, 30:31], ps[acc[0]][:, c0 + 128:c0 + 129]),
                             reads=[Rps[acc[0]]], writes=[Rstat])
                        s.op("dve", lambda e, c0=c0, acc=acc: e.reciprocal(stat[:, 31:32], ps[acc[1]][:, c0 + 128:c0 + 129]),
                             reads=[Rps[acc[1]]], writes=[Rstat])
                        s.op("dve", lambda e: e.tensor_tensor(stat[:, 32:33], stat[:, 31:32], stat[:, 25:26], ALU.mult),
                             reads=[Rstat], writes=[Rstat])
                        s.op("dve", lambda e, c0=c0, acc=acc: e.tensor_scalar(ya_f[:], ps[acc[0]][:, c0:c0 + 128], stat[:, 30:31], None, ALU.mult),
                             reads=[Rps[acc[0]], Rstat], writes=[Rya])
                        s.op("dve", lambda e, c0=c0, acc=acc: e.scalar_tensor_tensor(ya_f[:], ps[acc[1]][:, c0:c0 + 128], stat[:, 32:33], ya_f[:], ALU.mult, ALU.add),
                             reads=[Rps[acc[1]], Rstat, Rya], writes=[Rya])
                        s.op("act", lambda e: e.activation(junk_b[:, 0:128], ya_f[:], AF.Square, scale=1.0 / math.sqrt(128.0),
                                                           accum_out=stat[:, 33:34]),
                             reads=[Rya], writes=[Rjunk, Rstat])
                        s.op("dve", lambda e: e.tensor_scalar(stat[:, 34:35], stat[:, 33:34], 1e-6, None, ALU.add),
                             reads=[Rstat], writes=[Rstat])
                        s.op("act", lambda e: e.activation(stat[:, 34:35], stat[:, 34:35], AF.Ln), reads=[Rstat], writes=[Rstat])
                        s.op("act", lambda e: e.activation(stat[:, 35:36], stat[:, 34:35], AF.Exp, scale=-0.5), reads=[Rstat], writes=[Rstat])
                        s.op("dve", lambda e: e.scalar_tensor_tensor(ya_b[:], ya_f[:], stat[:, 35:36], subln08[:], ALU.mult, ALU.mult),
                             reads=[Rya, Rstat, Rsub], writes=[Ryab])
                        reserved.update(acc)
                        bt = bank()
                        reserved.difference_update(acc)
                        s.op("pe", lambda e, bt=bt: e.transpose(psb[bt][:, 0:128], ya_b[:], ident_b[:]),
                             reads=[Ryab, Rconst], writes=[Rps[bt]])
                        s.op("act", lambda e, bt=bt, h=h, i=i: e.copy(yT[:, h, i * 128:(i + 1) * 128], psb[bt][:, 0:128]),
                             reads=[Rps[bt]], writes=[RyT])

        def out_proj(W, actTv, Ract):
            pieces = [wload(W[:, hh * 512:(hh + 1) * 512], KC, 512) for hh in range(2)]
            for i in range(NT):
                bb = [bank(), bank()]
                for hh in range(2):
                    linear_tm(pieces[hh][0], pieces[hh][1], 512, actTv, Ract, KC, i, bb[hh])
                residual_update(i, bb, 0)

        Xc = carve(5120, NT * 8 * 256, BF16).rearrange("p (i g n) -> p i g n", i=NT, g=8)
        RXc = ares("Xc")
        chCS_b = sb("chCS_b", [128, 256], BF16)
        Rch = Res("chCS")

        def fnet():
            s.dma("pool", chCS_b[:], chCS, writes=[Rch])
            for i in range(NT):
                for gp in range(4):
                    b = bank()
                    for g2_ in range(2):
                        g = gp * 2 + g2_
                        s.op("pe", lambda e, b=b, g=g, g2_=g2_, i=i: e.matmul(
                            ps[b][:, g2_ * 256:(g2_ + 1) * 256], hT[:, g, i * 128:(i + 1) * 128], chCS_b[:], start=True, stop=True),
                            reads=[RhT, Rch], writes=[Rps[b]])
                    s.op("act", lambda e, b=b, gp=gp, i=i: e.copy(
                        Xc[:, i, gp * 2:gp * 2 + 2, :], ps[b][:].rearrange("p (g n) -> p g n", g=2)),
                        reads=[Rps[b]], writes=[RXc])
            for (t0, tn) in [(0, 384), (384, 384), (768, 384), (1152, 128)]:
                wc, rc = wload(dftC[:, t0:t0 + tn], NT, tn)
                wsn, rsn = wload(dftSn[:, t0:t0 + tn], NT, tn)
                for g in range(8):
                    b = bank()
                    for tc in range(NT):
                        s.op("pe", lambda e, b=b, g=g, tc=tc, tn=tn, wc=wc: e.matmul(
                            ps[b][:, 0:tn], Xc[:, tc, g, 0:128], wc[:, tc, 0:tn], start=(tc == 0), stop=False),
                            reads=[RXc, rc], writes=[Rps[b]])
                        s.op("pe", lambda e, b=b, g=g, tc=tc, tn=tn, wsn=wsn: e.matmul(
                            ps[b][:, 0:tn], Xc[:, tc, g, 128:256], wsn[:, tc, 0:tn], start=False, stop=(tc == NT - 1)),
                            reads=[RXc, rsn], writes=[Rps[b]])
                    s.op("act", lambda e, b=b, g=g, t0=t0, tn=tn: e.copy(yT[:, g, t0:t0 + tn], ps[b][:, 0:tn]),
                         reads=[Rps[b]], writes=[RyT])


        V_tm = carve(5120, NT * 512, BF16).rearrange("p (i n) -> p i n", i=NT)
        RVtm = ares("V_tm")
        YSb = carve(7680, NT * 512, BF16).rearrange("p (i n) -> p i n", i=NT)
        RYS = ares("YSb")
        Ystage = carve(10240, 128 * 16, F32)[0:64, :].rearrange("p (n k) -> p n k", k=16)
        RYst = ares("Ystage")
        Sst = [carve(12288, 512, F32), carve(12800, 512, F32)]
        RS = [ares("S0"), ares("S1")]
        tmpA = carve(13312, 512, F32)
        RtA = ares("tmpA")
        tmpB = carve(23824, 512, F32)
        RtB = ares("tmpB")
        w2t_b = carve(24336, 512, BF16)
        a2t_b = carve(24592, 512, BF16)
        g2_b = carve(24848, 512, BF16)
        Rlora = ares("lora")
        PT0 = 25104
        NPT = 12

        def pt(k, n=1):
            return carve(PT0 + 128 * k, 128 * n, F32)
        Rpt = [ares("pt%d" % k) for k in range(NPT)]
        lnx0_b = carve(26640, 512, BF16)
        lnx1_b = carve(26896, 512, BF16)
        Rlnx = ares("lnx")
        tb0 = junk_b[:, 0:128]
        tb1 = junk_b[:, 128:256]
        Rtb0 = Res("tb0")
        Rtb1 = Res("tb1")
        colx = sb("colx", [128, 64])
        Rcolx = Res("colx")
        tiny = sb("tiny", [128, 8])
        Rtiny = Res("tiny")
        BS = sb("BS", [128, NT, 8])
        RBS = Res("BS")
        wbf = [w[:].bitcast(F32) for w in wbuf]
        TW = wbf[0][:, 0:1024].rearrange("p (k n) -> p k n", k=8)
        TKK = wbf[0][:, 1024:2048].rearrange("p (k n) -> p k n", k=8)
        TNK = wbf[1][:, 0:1024].rearrange("p (k n) -> p k n", k=8)
        TKD = wbf[1][:, 1024:2048].rearrange("p (k n) -> p k n", k=8)
        TR2 = wbuf[2][:, 0:2048].rearrange("p (k n h) -> p k n h", k=8, h=2)
        tmpA_b = carve(10240, 512, BF16)
        RtAb = ares("tmpA_b")
        Sb16 = carve(10496, 512, BF16)
        RSb = ares("Sb16")
        CX = dict(cmu=0, nmu0=15, nmu1=30, omk1=45, keepL=49, keepR=54)

        def cx(name, c=0):
            o = CX[name] + c
            return colx[:, o:o + 1]

        def rwkv_setup():
            s.dma("pool", w2t_b, w2t_d, writes=[Rlora])
            s.dma("pool", a2t_b, a2t_d, writes=[Rlora])
            s.dma("pool", g2_b, g2_d, writes=[Rlora])
            s.dma("pool", lnx0_b, lnx_d[0, :].partition_broadcast(128), writes=[Rlnx])
            s.dma("pool", lnx1_b, lnx_d[1, :].partition_broadcast(128), writes=[Rlnx])
            mu0 = cp("mu0", 0, 15)
            mu1 = cp("mu1", 0, 15)
            s.op("dve", lambda e: e.tensor_tensor(colx[:, 0:15], mu0, mu1, ALU.add), reads=[Rcolp], writes=[Rcolx])
            s.op("dve", lambda e: e.tensor_scalar(colx[:, 0:15], colx[:, 0:15], -1.0, 1.0, ALU.mult, ALU.add),
                 reads=[Rcolx], writes=[Rcolx])
            s.op("dve", lambda e: e.tensor_scalar(colx[:, 15:30], mu0, -1.0, None, ALU.mult), reads=[Rcolp], writes=[Rcolx])
            s.op("dve", lambda e: e.tensor_scalar(colx[:, 30:45], mu1, -1.0, None, ALU.mult), reads=[Rcolp], writes=[Rcolx])
            s.op("dve", lambda e: e.tensor_scalar(colx[:, 45:49], cp("kvec1", 0, 4), -1.0, 1.0, ALU.mult, ALU.add),
                 reads=[Rcolp], writes=[Rcolx])
            s.op("dve", lambda e: e.tensor_scalar(colx[:, 49:54], rowp_sb[:, ROFF["bndL"]:ROFF["bndL"] + 5], -1.0, 1.0, ALU.mult, ALU.add),
                 reads=[Rrowp], writes=[Rcolx])
            s.op("dve", lambda e: e.tensor_scalar(colx[:, 54:59], rowp_sb[:, ROFF["bndR"]:ROFF["bndR"] + 5], -1.0, 1.0, ALU.mult, ALU.add),
                 reads=[Rrowp], writes=[Rcolx])
            s.op("dve", lambda e: e.memset(BS[:], 0.0), writes=[RBS])

        def shift(ch, ti, out, Rout):
            t0 = ti * 128
            f = fbT[:, ch, 1 + t0:1 + t0 + 128]
            fp = fbT[:, ch, t0:t0 + 128]
            fn = fbT[:, ch, 2 + t0:2 + t0 + 128]
            rr = [RfbT, Rcolp, Rcolx]
            s.op("dve", lambda e: e.tensor_scalar(out, f, cx("cmu", ch), None, ALU.mult), reads=rr, writes=[Rout])
            s.op("dve", lambda e: e.scalar_tensor_tensor(out, fp, cp("mu0", ch), out, ALU.mult, ALU.add),
                 reads=rr + [Rout], writes=[Rout])
            s.op("dve", lambda e: e.scalar_tensor_tensor(out, fn, cp("mu1", ch), out, ALU.mult, ALU.add),
                 reads=rr + [Rout], writes=[Rout])
            u = ti // 2
            if ti % 2 == 0:
                bl = rowp_sb[:, ROFF["bndL"] + u:ROFF["bndL"] + u + 1]
                s.op("dve", lambda e: e.tensor_tensor(tiny[:, 0:1], fbT[:, ch, t0:t0 + 1], bl, ALU.mult),
                     reads=[RfbT, Rrowp], writes=[Rtiny])
                s.op("dve", lambda e: e.scalar_tensor_tensor(out[:, 0:1], tiny[:, 0:1], cx("nmu0", ch), out[:, 0:1], ALU.mult, ALU.add),
                     reads=[Rtiny, Rcolx, Rout], writes=[Rout])
            else:
                br = rowp_sb[:, ROFF["bndR"] + u:ROFF["bndR"] + u + 1]
                s.op("dve", lambda e: e.tensor_tensor(tiny[:, 1:2], fbT[:, ch, 1 + t0 + 128:2 + t0 + 128], br, ALU.mult),
                     reads=[RfbT, Rrowp], writes=[Rtiny])
                s.op("dve", lambda e: e.scalar_tensor_tensor(out[:, 127:128], tiny[:, 1:2], cx("nmu1", ch), out[:, 127:128], ALU.mult, ALU.add),
                     reads=[Rtiny, Rcolx, Rout], writes=[Rout])

        def rwkv_prepass():
            for ti in range(NT):
                for c in range(4):
                    shift(8 + c, ti, pt(c), Rpt[c])
                    s.op("act", lambda e, c=c: e.copy(xn_b[:, c * 128:(c + 1) * 128], pt(c)), reads=[Rpt[c]], writes=[Rxn])
                b = bank()
                for c in range(4):
                    s.op("pe", lambda e, b=b, c=c: e.transpose(psb[b][:, c * 128:(c + 1) * 128], xn_b[:, c * 128:(c + 1) * 128], ident_b[:]),
                         reads=[Rxn, Rconst], writes=[Rps[b]])
                s.op("act", lambda e, b=b, ti=ti: e.copy(V_tm[:, ti, :], psb[b][:, 0:512]), reads=[Rps[b]], writes=[RVtm])

        def prep(d, ti):
            rev = (d == 1)

            def tab(T3, c):
                v = T3[:, d * 4 + c, :]
                return v[:, ::-1] if rev else v
            tabs_w = [Rw[0], Rw[1], Rw[2]]
            shift(12, ti, pt(0), Rpt[0])
            shift(13, ti, pt(1), Rpt[1])
            s.op("act", lambda e: e.activation(tb0, pt(0), AF.Tanh), reads=[Rpt[0]], writes=[Rtb0])
            s.op("act", lambda e: e.copy(tb1, pt(1)), reads=[Rpt[1]], writes=[Rtb1])
            lo, hi = 64 * d, 64 * d + 64
            for c in range(4):
                b = bank()
                s.op("pe", lambda e, b=b, c=c: e.matmul(ps[b][:, 0:128], w2t_b[lo:hi, c * 128:(c + 1) * 128], tb0[lo:hi, :], start=True, stop=True),
                     reads=[Rlora, Rtb0], writes=[Rps[b]])
                s.op("act", lambda e, b=b, c=c: e.activation(pt(2), ps[b][:, 0:128], AF.Sigmoid, bias=cp("w0_%d" % d, c)),
                     reads=[Rps[b], Rcolp], writes=[Rpt[2]])
                s.op("act", lambda e, c=c: e.activation(tab(TW, c), pt(2), AF.Exp, scale=-EXPM05),
                     reads=[Rpt[2]], writes=[Rw[0]])
                b = bank()
                s.op("pe", lambda e, b=b, c=c: e.matmul(ps[b][:, 0:128], a2t_b[lo:hi, c * 128:(c + 1) * 128], tb1[lo:hi, :], start=True, stop=True),
                     reads=[Rlora, Rtb1], writes=[Rps[b]])
                s.op("act", lambda e, b=b, c=c: e.activation(pt(3), ps[b][:, 0:128], AF.Sigmoid, bias=cp("a0_%d" % d, c)),
                     reads=[Rps[b], Rcolp], writes=[Rpt[3]])
                shift(4 + c, ti, pt(4), Rpt[4])
                s.op("dve", lambda e, c=c: e.tensor_scalar(pt(5), pt(4), cp("kvec0", c), None, ALU.mult),
                     reads=[Rpt[4], Rcolp], writes=[Rpt[5]])
                s.op("act", lambda e: e.activation(pt(6), pt(5), AF.Square), reads=[Rpt[5]], writes=[Rpt[6]])
                b = bank()
                s.op("pe", lambda e, b=b: e.matmul(ps[b][:, 0:128], bones_f[:], pt(6), start=True, stop=True),
                     reads=[Rconst, Rpt[6]], writes=[Rps[b]])
                s.op("dve", lambda e, b=b: e.tensor_scalar(pt(7), ps[b][:, 0:128], 1e-12, None, ALU.add),
                     reads=[Rps[b]], writes=[Rpt[7]])
                s.op("act", lambda e: e.activation(pt(7), pt(7), AF.Sqrt), reads=[Rpt[7]], writes=[Rpt[7]])
                s.op("dve", lambda e: e.reciprocal(pt(7), pt(7)), reads=[Rpt[7]], writes=[Rpt[7]])
                s.op("dve", lambda e: e.tensor_tensor(pt(8), pt(5), pt(7), ALU.mult), reads=[Rpt[5], Rpt[7]], writes=[Rpt[8]])
                s.op("act", lambda e, c=c: e.copy(tab(TKK, c), pt(8)), reads=[Rpt[8]], writes=[Rw[0]])
                s.op("dve", lambda e, c=c: e.scalar_tensor_tensor(tab(TNK, c), pt(8), -1.0, pt(3), ALU.mult, ALU.mult),
                     reads=[Rpt[8], Rpt[3]], writes=[Rw[1]])
                s.op("dve", lambda e, c=c: e.tensor_scalar(pt(9), pt(3), cp("kvec1", c), cx("omk1", c), ALU.mult, ALU.add),
                     reads=[Rpt[3], Rcolp, Rcolx], writes=[Rpt[9]])
                s.op("dve", lambda e: e.tensor_tensor(pt(9), pt(4), pt(9), ALU.mult), reads=[Rpt[4], Rpt[9]], writes=[Rpt[9]])
                s.op("act", lambda e, c=c: e.copy(tab(TKD, c), pt(9)), reads=[Rpt[9]], writes=[Rw[1]])
                shift(c, ti, pt(10), Rpt[10])
                for h2 in range(2):
                    o = TR2[:, d * 4 + c, :, h2]
                    if rev:
                        o = o[:, ::-1]
                    s.op("dve", lambda e, o=o, h2=h2: e.tensor_scalar(o, pt(10), halfsel_f[:, h2:h2 + 1], None, ALU.mult),
                         reads=[Rpt[10], Rconst], writes=[Rw[2]])
                s.op("dve", lambda e: e.tensor_tensor(pt(6), pt(10), pt(9), ALU.mult), reads=[Rpt[10], Rpt[9]], writes=[Rpt[6]])
                s.op("dve", lambda e, c=c: e.tensor_scalar(pt(6), pt(6), cp("kvec2", c), None, ALU.mult),
                     reads=[Rpt[6], Rcolp], writes=[Rpt[6]])
                b = bank()
                s.op("pe", lambda e, b=b: e.matmul(ps[b][:, 0:2], pt(6), halfsel_f[:, 0:2], start=True, stop=True),
                     reads=[Rpt[6], Rconst], writes=[Rps[b]])
                s.op("dve", lambda e, b=b, c=c, ti=ti: e.tensor_tensor(BS[:, ti, c * 2:c * 2 + 2], BS[:, ti, c * 2:c * 2 + 2], ps[b][:, 0:2], ALU.add),
                     reads=[Rps[b], RBS], writes=[RBS])

        yb = xn_b[:, 0:512]

        def finalize(ti, by):
            ys = pt(0, 4)
            Rys = [Rpt[0], Rpt[1], Rpt[2], Rpt[3]]
            t2 = pt(4, 4)
            Rt2 = [Rpt[4], Rpt[5], Rpt[6], Rpt[7]]
            ys3 = ys.rearrange("p (h i) -> p h i", h=8)
            t23 = t2.rearrange("p (h i) -> p h i", h=8)
            s.op("dve", lambda e: e.tensor_tensor(ys, YSb[:, ti, :], ps[by][:], ALU.add), reads=[RYS, Rps[by]], writes=Rys)
            s.op("dve", lambda e: e.reduce_sum(tiny[:, 0:8], ys3, axis=AX.X), reads=Rys, writes=[Rtiny])
            s.op("dve", lambda e: e.tensor_scalar(tiny[:, 0:8], tiny[:, 0:8], -1.0 / 64.0, None, ALU.mult), reads=[Rtiny], writes=[Rtiny])
            s.op("dve", lambda e: e.tensor_tensor(ys3, ys3, tiny[:, 0:8].unsqueeze(2).to_broadcast([128, 8, 64]), ALU.add),
                 reads=Rys + [Rtiny], writes=Rys)
            s.op("dve", lambda e: e.tensor_tensor(t2, ys, ys, ALU.mult), reads=Rys, writes=Rt2)
            s.op("dve", lambda e: e.reduce_sum(stat[:, 40:48], t23, axis=AX.X), reads=Rt2, writes=[Rstat])
            s.op("dve", lambda e: e.tensor_scalar(stat[:, 40:48], stat[:, 40:48], 1.0 / 64.0, 64e-5, ALU.mult, ALU.add),
                 reads=[Rstat], writes=[Rstat])
            s.op("act", lambda e: e.activation(stat[:, 40:48], stat[:, 40:48], AF.Ln), reads=[Rstat], writes=[Rstat])
            s.op("act", lambda e: e.activation(stat[:, 48:56], stat[:, 40:48], AF.Exp, scale=-0.5), reads=[Rstat], writes=[Rstat])
            s.op("dve", lambda e: e.tensor_tensor(ys3, ys3, stat[:, 48:56].unsqueeze(2).to_broadcast([128, 8, 64]), ALU.mult),
                 reads=Rys + [Rstat], writes=Rys)
            s.op("dve", lambda e: e.tensor_tensor(ys, ys, lnx0_b, ALU.mult), reads=Rys + [Rlnx], writes=Rys)
            s.op("dve", lambda e: e.tensor_tensor(ys, ys, lnx1_b, ALU.add), reads=Rys + [Rlnx], writes=Rys)
            s.op("dve", lambda e: e.tensor_tensor(t23, V_tm[:, ti, :].rearrange("p (h i) -> p h i", h=8),
                                                  BS[:, ti, :].unsqueeze(2).to_broadcast([128, 8, 64]), ALU.mult),
                 reads=[RVtm, RBS], writes=Rt2)
            s.op("dve", lambda e: e.tensor_tensor(ys, ys, t2, ALU.add), reads=Rys + Rt2, writes=Rys)
            shift(14, ti, pt(8), Rpt[8])
            s.op("act", lambda e: e.activation(tb0, pt(8), AF.Sigmoid), reads=[Rpt[8]], writes=[Rtb0])
            bg = bank()
            s.op("pe", lambda e, bg=bg: e.matmul(ps[bg][:], tb0, g2_b, start=True, stop=True),
                 reads=[Rtb0, Rlora], writes=[Rps[bg]])
            s.op("dve", lambda e, bg=bg: e.tensor_tensor(yb, ys, ps[bg][:], ALU.mult), reads=Rys + [Rps[bg]], writes=[Rxn])
            bt = bank()
            for c in range(4):
                s.op("pe", lambda e, bt=bt, c=c: e.transpose(psb[bt][:, c * 128:(c + 1) * 128], yb[:, c * 128:(c + 1) * 128], ident_b[:]),
                     reads=[Rxn, Rconst], writes=[Rps[bt]])
            s.op("act", lambda e, bt=bt, ti=ti: e.copy(yT[:, 4:8, ti * 128:(ti + 1) * 128],
                                                        psb[bt][:, 0:512].rearrange("p (c t) -> p c t", c=4)),
                 reads=[Rps[bt]], writes=[RyT])

        def rwkv_scan(nrounds=NT):
            S8 = [x_.rearrange("p (k i) -> p k i", k=8) for x_ in Sst]
            tA8 = tmpA.rearrange("p (k i) -> p k i", k=8)
            tB8 = tmpB.rearrange("p (k i) -> p k i", k=8)

            def bc(T3, n):
                return T3[:, :, n].unsqueeze(2).to_broadcast([128, 8, 64])
            for r in range(nrounds):
                tf, tbk = r, NT - 1 - r
                prep(0, tf)
                prep(1, tbk)
                if tf % 2 == 0:
                    u = tf // 2
                    if u == 0:
                        s.dma("sp", Sst[0][:, 0:256], initS[0], writes=[RS[0]])
                    else:
                        s.op("dve", lambda e, u=u: e.tensor_scalar(Sst[0][:, 0:256], Sst[0][:, 0:256], cx("keepL", u), None, ALU.mult),
                             reads=[RS[0], Rcolx], writes=[RS[0]])
                if tbk % 2 == 1:
                    u = tbk // 2
                    if u == 4:
                        s.op("dve", lambda e: e.memset(Sst[0][:, 256:512], 0.0), writes=[RS[0]])
                    elif u == 3:
                        s.dma("sp", tmpB[:, 0:256], initS[1], writes=[RtB])
                        s.op("dve", lambda e, u=u: e.scalar_tensor_tensor(Sst[0][:, 256:512], Sst[0][:, 256:512], cx("keepR", u),
                                                                          tmpB[:, 0:256], ALU.mult, ALU.add),
                             reads=[RS[0], Rcolx, RtB], writes=[RS[0]])
                    else:
                        s.op("dve", lambda e, u=u: e.tensor_scalar(Sst[0][:, 256:512], Sst[0][:, 256:512], cx("keepR", u), None, ALU.mult),
                             reads=[RS[0], Rcolx], writes=[RS[0]])
                by_ = None
                pending = []

                def flush():
                    for f_ in pending:
                        f_()
                    del pending[:]
                for n in range(128):
                    ci, ni = n % 2, (n + 1) % 2
                    Sc, Sn = S8[ci], S8[ni]
                    if n % 32 == 0:
                        by_ = bank()
                        reserved.add(by_)
                    tAb8 = tmpA_b.rearrange("p (k i) -> p k i", k=8)
                    s.op("dve", lambda e, n=n, Sc=Sc, tAb8=tAb8: e.tensor_tensor(tAb8, Sc, bc(TKK, n), ALU.mult),
                         reads=[RS[ci], Rw[0]], writes=[RtAb])
                    s.op("dve", lambda e, n=n, Sc=Sc, Sn=Sn: e.tensor_tensor(Sn, Sc, bc(TW, n), ALU.mult),
                         reads=[RS[ci], Rw[0]], writes=[RS[ni]])
                    bv = bank()
                    for d in range(2):
                        tile_d = tf if d == 0 else tbk
                        row = n if d == 0 else 127 - n
                        for h2 in range(2):
                            s.op("pe", lambda e, bv=bv, d=d, h2=h2, tile_d=tile_d, row=row: e.matmul(
                                ps[bv][64 * h2:64 * h2 + 64, d * 256:(d + 1) * 256].rearrange("p (c i) -> p c i", c=4),
                                ident_b[:, row:row + 1].to_broadcast([128, 64]),
                                V_tm[:, tile_d, :].rearrange("p (c h i) -> p c h i", c=4, h=2)[:, :, h2, :],
                                start=True, stop=True),
                                reads=[RVtm, Rconst], writes=[Rps[bv]])
                    bs_ = bank()
                    s.op("pe", lambda e, bs_=bs_: e.matmul(ps[bs_][:], bones_b[:], tmpA_b, start=True, stop=True),
                         reads=[RtAb, Rconst], writes=[Rps[bs_]])
                    flush()
                    s.op("dve", lambda e, n=n, bv=bv: e.tensor_tensor(tB8, ps[bv][:].rearrange("p (k i) -> p k i", k=8), bc(TKD, n), ALU.mult),
                         reads=[Rps[bv], Rw[1]], writes=[RtB])
                    s.op("dve", lambda e, ni=ni: e.tensor_tensor(Sst[ni], Sst[ni], tmpB, ALU.add),
                         reads=[RS[ni], RtB], writes=[RS[ni]])
                    s.op("dve", lambda e, n=n, bs_=bs_: e.tensor_tensor(tA8, ps[bs_][:].rearrange("p (k i) -> p k i", k=8), bc(TNK, n), ALU.mult),
                         reads=[Rps[bs_], Rw[1]], writes=[RtA])
                    s.op("dve", lambda e, ni=ni: e.tensor_tensor(Sst[ni], Sst[ni], tmpA, ALU.add),
                         reads=[RS[ni], RtA], writes=[RS[ni]])
                    s.op("act", lambda e, ni=ni: e.copy(Sb16, Sst[ni]), reads=[RS[ni]], writes=[RSb])
                    nl = n % 32

                    def ymm(by_=by_, nl=nl, n=n):
                        for dc in range(8):
                            s.op("pe", lambda e, dc=dc: e.matmul(
                                ps[by_][0:64, nl * 16 + dc * 2:nl * 16 + dc * 2 + 2], Sb16[:, dc * 64:(dc + 1) * 64], TR2[:, dc, n, :],
                                start=True, stop=True),
                                reads=[RSb, Rw[2]], writes=[Rps[by_]])
                    pending.append(ymm)
                    if nl == 31:
                        flush()
                        n0 = n - 31
                        src = ps[by_][0:64, :].rearrange("p (n k) -> p n k", k=16)
                        s.op("act", lambda e, src=src, n0=n0: e.copy(Ystage[:, n0:n0 + 32, 0:8], src[:, :, 0:8]),
                             reads=[Rps[by_]], writes=[RYst])
                        s.op("act", lambda e, src=src, n0=n0: e.copy(Ystage[:, 96 - n0:128 - n0, 8:16][:, ::-1, :], src[:, :, 8:16]),
                             reads=[Rps[by_]], writes=[RYst])
                        reserved.discard(by_)
                if tf % 2 == 1:
                    s.dma("sp", s_out[tf // 2, 0], Sst[0][:, 0:256], reads=[RS[0]])
                if tbk % 2 == 0:
                    s.dma("sp", s_out[tbk // 2, 1], Sst[0][:, 256:512], reads=[RS[0]])
                for d in range(2):
                    tile_d = tf if d == 0 else tbk
                    byy = bank()
                    for k8 in range(8):
                        s.op("pe", lambda e, byy=byy, k8=k8, d=d: e.matmul(
                            ps[byy][:, k8 * 64:(k8 + 1) * 64], Ystage[:, :, d * 8 + k8], ident_f[0:64, 0:64], start=True, stop=True),
                            reads=[RYst, Rconst], writes=[Rps[byy]])
                    first = (r < 5)
                    if first:
                        s.op("act", lambda e, byy=byy, tile_d=tile_d: e.copy(YSb[:, tile_d, :], ps[byy][:]),
                             reads=[Rps[byy]], writes=[RYS])
                    else:
                        finalize(tile_d, byy)


        def wbw(i, off_w, nelem):
            return wbuf[i][:, 2 * off_w:2 * off_w + nelem]
        Ast = carve(13200, 512, F32)
        RAst = ares("Ast")
        Abf = carve(13712, 512, BF16)
        RAbf = ares("Abf")
        WS = []
        fB = 14208 + 8 * 641
        for wsi in range(2):
            d_ = {}
            if wsi == 0:
                for k_, nm in enumerate(["Lt0", "Lt1", "L0", "L1", "Z0"]):
                    d_[nm] = carve(10240 + 512 * k_, 512, F32)
                d_["MT"] = carve(12800, 256, BF16)
                d_["RT"] = carve(12928, 512, BF16)
                d_["Z1"] = wbuf[0][:, 0:1024].bitcast(F32)
                for k_, nm in enumerate(["Lakt", "Mrbt", "Mrkt", "BKtm"]):
                    d_[nm] = wbw(0, 512 + 256 * k_, 512)
                d_["Zfb"] = carve(10240, 512, BF16)
            else:
                for k_, nm in enumerate(["Lt0", "Lt1", "L0", "L1", "Z0"]):
                    d_[nm] = carve(fB + 512 * k_, 512, F32)
                d_["Z1"] = wbuf[1][:, 0:1024].bitcast(F32)
                d_["Lakt"] = wbw(0, 1536, 512)
                d_["Mrbt"] = wbw(0, 1792, 512)
                d_["Mrkt"] = wbw(1, 512, 512)
                d_["MT"] = wbw(1, 768, 256)
                d_["BKtm"] = carve(23824, 512, BF16)
                d_["RT"] = carve(24080, 512, BF16)
                d_["Zfb"] = carve(fB, 512, BF16)
            for nm in ["Lt0", "Lt1", "L0", "L1", "Z0", "Z1", "Lakt", "Mrbt", "Mrkt", "BKtm", "MT", "RT"]:
                d_["R" + nm] = ares("ws%d_%s" % (wsi, nm))
            d_["RZfb"] = d_["RLt0"]
            WS.append(d_)
        masks = wbw(1, 896, 4 * 4 * 128).rearrange("p (m q t) -> p m q t", m=4, q=4)
        Rmask = ares("masks")
        TBL = []
        for par in range(2):
            d_ = {}
            for k_, nm in enumerate(["RH", "AH", "BH", "KH"]):
                d_[nm] = wbw(2, par * 1024 + 256 * k_, 512).rearrange("p (c t) -> p c t", c=4)
                d_["R" + nm] = ares("tb%d_%s" % (par, nm))
            TBL.append(d_)
        pcs = sb("pcs", [128, 2, 4])
        Rpcs = [Res("pcs0"), Res("pcs1")]

        def prep_chunk(d, ti, par):
            TB = TBL[par]
            shift(12, ti, pt(0), Rpt[0])
            shift(13, ti, pt(1), Rpt[1])
            s.op("act", lambda e: e.activation(tb0, pt(0), AF.Tanh), reads=[Rpt[0]], writes=[Rtb0])
            s.op("act", lambda e: e.copy(tb1, pt(1)), reads=[Rpt[1]], writes=[Rtb1])
            lo, hi = 64 * d, 64 * d + 64
            for c in range(4):
                b = bank()
                s.op("pe", lambda e, b=b, c=c: e.matmul(ps[b][:, 0:128], w2t_b[lo:hi, c * 128:(c + 1) * 128], tb0[lo:hi, :], start=True, stop=True),
                     reads=[Rlora, Rtb0], writes=[Rps[b]])
                s.op("act", lambda e, b=b, c=c: e.activation(pt(2), ps[b][:, 0:128], AF.Sigmoid, bias=cp("w0_%d" % d, c)),
                     reads=[Rps[b], Rcolp], writes=[Rpt[2]])
                b = bank()
                s.op("pe", lambda e, b=b, c=c: e.matmul(ps[b][:, 0:128], a2t_b[lo:hi, c * 128:(c + 1) * 128], tb1[lo:hi, :], start=True, stop=True),
                     reads=[Rlora, Rtb1], writes=[Rps[b]])
                s.op("act", lambda e, b=b, c=c: e.activation(pt(3), ps[b][:, 0:128], AF.Sigmoid, bias=cp("a0_%d" % d, c)),
                     reads=[Rps[b], Rcolp], writes=[Rpt[3]])
                s.op("dve", lambda e: e.tensor_scalar(pt(2), pt(2), -EXPM05, None, ALU.mult), reads=[Rpt[2]], writes=[Rpt[2]])
                if d == 0:
                    s.op("dve", lambda e: e.tensor_tensor_scan(pt(0), pt(11), pt(2), 0.0, ALU.mult, ALU.add),
                         reads=[Rpt[2], Rpt[11]], writes=[Rpt[0]])
                else:
                    s.op("dve", lambda e: e.tensor_tensor_scan(pt(0)[:, ::-1], pt(11), pt(2)[:, ::-1], 0.0, ALU.mult, ALU.add),
                         reads=[Rpt[2], Rpt[11]], writes=[Rpt[0]])
                s.op("act", lambda e: e.activation(pt(1), pt(0), AF.Exp), reads=[Rpt[0]], writes=[Rpt[1]])
                s.op("dve", lambda e: e.tensor_tensor(pt(2), pt(0), pt(2), ALU.subtract), reads=[Rpt[0], Rpt[2]], writes=[Rpt[2]])
                s.op("act", lambda e: e.activation(pt(2), pt(2), AF.Exp), reads=[Rpt[2]], writes=[Rpt[2]])
                s.op("act", lambda e: e.activation(pt(0), pt(0), AF.Exp, scale=-1.0), reads=[Rpt[0]], writes=[Rpt[0]])
                yield
                pcol = 127 if d == 0 else 0
                s.op("dve", lambda e, c=c, pcol=pcol: e.tensor_copy(pcs[:, par, c:c + 1], pt(1)[:, pcol:pcol + 1]),
                     reads=[Rpt[1]], writes=[Rpcs[par]])
                yield
                shift(4 + c, ti, pt(4), Rpt[4])
                s.op("dve", lambda e, c=c: e.tensor_scalar(pt(5), pt(4), cp("kvec0", c), None, ALU.mult),
                     reads=[Rpt[4], Rcolp], writes=[Rpt[5]])
                s.op("act", lambda e: e.activation(pt(6), pt(5), AF.Square), reads=[Rpt[5]], writes=[Rpt[6]])
                b = bank()
                s.op("pe", lambda e, b=b: e.matmul(ps[b][:, 0:128], bones_f[:], pt(6), start=True, stop=True),
                     reads=[Rconst, Rpt[6]], writes=[Rps[b]])
                s.op("dve", lambda e, b=b: e.tensor_scalar(pt(7), ps[b][:, 0:128], 1e-12, None, ALU.add),
                     reads=[Rps[b]], writes=[Rpt[7]])
                s.op("act", lambda e: e.activation(pt(7), pt(7), AF.Ln), reads=[Rpt[7]], writes=[Rpt[7]])
                s.op("act", lambda e: e.activation(pt(7), pt(7), AF.Exp, scale=-0.5), reads=[Rpt[7]], writes=[Rpt[7]])
                s.op("dve", lambda e: e.tensor_tensor(pt(8), pt(5), pt(7), ALU.mult), reads=[Rpt[5], Rpt[7]], writes=[Rpt[8]])
                yield
                s.op("dve", lambda e, c=c: e.scalar_tensor_tensor(TB["AH"][:, c, :], pt(8), -1.0, pt(2), ALU.mult, ALU.mult),
                     reads=[Rpt[8], Rpt[2]], writes=[TB["RAH"]])
                s.op("dve", lambda e: e.tensor_tensor(pt(6), pt(8), pt(3), ALU.mult), reads=[Rpt[8], Rpt[3]], writes=[Rpt[6]])
                s.op("dve", lambda e, c=c: e.tensor_tensor(TB["BH"][:, c, :], pt(6), pt(0), ALU.mult),
                     reads=[Rpt[6], Rpt[0]], writes=[TB["RBH"]])
                yield
                s.op("dve", lambda e, c=c: e.tensor_scalar(pt(9), pt(3), cp("kvec1", c), cx("omk1", c), ALU.mult, ALU.add),
                     reads=[Rpt[3], Rcolp, Rcolx], writes=[Rpt[9]])
                s.op("dve", lambda e: e.tensor_tensor(pt(9), pt(4), pt(9), ALU.mult), reads=[Rpt[4], Rpt[9]], writes=[Rpt[9]])
                s.op("dve", lambda e, c=c: e.tensor_tensor(TB["KH"][:, c, :], pt(9), pt(0), ALU.mult),
                     reads=[Rpt[9], Rpt[0]], writes=[TB["RKH"]])
                yield
                shift(c, ti, pt(10), Rpt[10])
                s.op("dve", lambda e, c=c: e.tensor_tensor(TB["RH"][:, c, :], pt(10), pt(1), ALU.mult),
                     reads=[Rpt[10], Rpt[1]], writes=[TB["RRH"]])
                yield
                s.op("dve", lambda e: e.tensor_tensor(pt(6), pt(10), pt(9), ALU.mult), reads=[Rpt[10], Rpt[9]], writes=[Rpt[6]])
                s.op("dve", lambda e, c=c: e.tensor_scalar(pt(6), pt(6), cp("kvec2", c), None, ALU.mult),
                     reads=[Rpt[6], Rcolp], writes=[Rpt[6]])
                b = bank()
                s.op("pe", lambda e, b=b: e.matmul(ps[b][:, 0:2], pt(6), halfsel_f[:, 0:2], start=True, stop=True),
                     reads=[Rpt[6], Rconst], writes=[Rps[b]])
                s.op("dve", lambda e, b=b, c=c, ti=ti: e.tensor_tensor(BS[:, ti, c * 2:c * 2 + 2], BS[:, ti, c * 2:c * 2 + 2], ps[b][:, 0:2], ALU.add),
                     reads=[Rps[b], RBS], writes=[RBS])
                yield

        def rwkv_chunked(ntiles=NT):
            cut = int(os.environ.get("CUT", 99))
            for m in range(4):
                for q in range(4):
                    s.dma("pool", masks[:, m, q, :], cmat[4 + m], writes=[Rmask])
            s.op("dve", lambda e: e.memset(pt(11), 1.0), writes=[Rpt[11]])
            MSK = {0: (0, 1, 2), 1: (1, 0, 3)}
            seq = []
            for d in range(2):
                order = list(range(NT)) if d == 0 else list(range(NT - 1, -1, -1))
                seq += [(d, ti) for ti in order[:ntiles]]
            for _ in prep_chunk(seq[0][0], seq[0][1], 0):
                pass
            if True:
                def tile_body(n):
                    d, ti = seq[n]
                    par = n % 2
                    Ad = Ast[:, d * 256:(d + 1) * 256]
                    Ad3 = Ad.rearrange("p (c i) -> p c i", c=4)
                    Abd = Abf[:, d * 256:(d + 1) * 256]
                    Abd3 = Abd.rearrange("p (c i) -> p c i", c=4)
                    mTs, mS, mTi = MSK[d]
                    TB = TBL[par]
                    nxt = prep_chunk(seq[n + 1][0], seq[n + 1][1], (n + 1) % 2) if n + 1 < len(seq) else iter(())
                    u = ti // 2
                    isstart = (ti % 2 == 0) if d == 0 else (ti % 2 == 1)
                    if isstart:
                        if d == 0:
                            if u == 0:
                                s.dma("sp", Ad, initS[0], writes=[RAst])
                            else:
                                s.op("dve", lambda e, u=u: e.tensor_scalar(Ad, Ad, cx("keepL", u), None, ALU.mult),
                                     reads=[RAst, Rcolx], writes=[RAst])
                        else:
                            if u == 4:
                                s.op("dve", lambda e: e.memset(Ad, 0.0), writes=[RAst])
                            elif u == 3:
                                s.dma("sp", stage_f[:, 0, 0:256], initS[1], writes=[Rstage[0]])
                                s.op("dve", lambda e, u=u: e.scalar_tensor_tensor(Ad, Ad, cx("keepR", u), stage_f[:, 0, 0:256], ALU.mult, ALU.add),
                                     reads=[RAst, Rcolx, Rstage[0]], writes=[RAst])
                            else:
                                s.op("dve", lambda e, u=u: e.tensor_scalar(Ad, Ad, cx("keepR", u), None, ALU.mult),
                                     reads=[RAst, Rcolx], writes=[RAst])
                        s.op("act", lambda e: e.copy(Abd, Ad), reads=[RAst], writes=[RAbf])
                    if cut < 2:
                        return
                    fy = bank()
                    reserved.add(fy)
                    first_fy = [True]
                    def group_body(g):
                        W = WS[g]
                        heads = [(c_, g) for c_ in range(4)]

                        def fm(T3, c, h2):
                            return T3[64 * h2:64 * h2 + 64, c, :]

                        def q4(tl):
                            return tl.rearrange("p (q t) -> p q t", q=4)
                        specs = [("Lt0", "BH", "AH", mTs), ("L0", "AH", "BH", mS), ("Lakt", "KH", "AH", mTs),
                                 ("Mrbt", "BH", "RH", mTi), ("Mrkt", "KH", "RH", mTi)]
                        lim = int(os.environ.get("LIM", 99))
                        for (dst, la, rb, mk) in specs[:lim]:
                            b = bank()
                            for q, (c, h2) in enumerate(heads[:int(os.environ.get("LIMH", 4))]):
                                s.op("pe", lambda e, b=b, q=q, c=c, h2=h2, la=la, rb=rb: e.matmul(
                                    ps[b][:, q * 128:(q + 1) * 128], fm(TB[la], c, h2), fm(TB[rb], c, h2), start=True, stop=True),
                                    reads=[TB["R" + la], TB["R" + rb]], writes=[Rps[b]])
                            if os.environ.get("NOEV") == "1":
                                continue
                            if os.environ.get("NOEV") == "2":
                                s.op("dve", lambda e, b=b, dst=dst, mk=mk: e.tensor_copy(W[dst], ps[b][:]),
                                     reads=[Rps[b], Rmask], writes=[W["R" + dst]])
                                continue
                            s.op("dve", lambda e, b=b, dst=dst, mk=mk: e.tensor_tensor(
                                q4(W[dst]), ps[b][:].rearrange("p (q t) -> p q t", q=4), masks[:, mk, :, :], ALU.mult),
                                reads=[Rps[b], Rmask], writes=[W["R" + dst]])
                        yield
                        if cut < 3:
                            return
                        b = bank()
                        for q, (c, h2) in enumerate(heads):
                            idn = ident_b[64 * h2:64 * h2 + 64, 64 * h2:64 * h2 + 64]
                            for k_, nm in enumerate(["AH", "BH", "KH"]):
                                s.op("pe", lambda e, b=b, q=q, c=c, h2=h2, nm=nm, k_=k_, idn=idn: e.transpose(
                                    psb[b][:, k_ * 256 + q * 64:k_ * 256 + q * 64 + 64], fm(TB[nm], c, h2), idn),
                                    reads=[TB["R" + nm], Rconst], writes=[Rps[b]])
                        Z0q = q4(W["Z0"])
                        BKq = q4(W["BKtm"])
                        s.op("act", lambda e, b=b, Z0q=Z0q: e.copy(Z0q[:, :, 0:64], psb[b][:, 0:256].rearrange("p (q j) -> p q j", q=4)),
                             reads=[Rps[b]], writes=[W["RZ0"]])
                        s.op("act", lambda e, b=b, BKq=BKq: e.copy(BKq[:, :, 0:64], psb[b][:, 256:512].rearrange("p (q j) -> p q j", q=4)),
                             reads=[Rps[b]], writes=[W["RBKtm"]])
                        s.op("act", lambda e, b=b, BKq=BKq: e.copy(BKq[:, :, 64:128], psb[b][:, 512:768].rearrange("p (q j) -> p q j", q=4)),
                             reads=[Rps[b]], writes=[W["RBKtm"]])

                        def Vq(c, h2):
                            return V_tm[:, ti, (c * 2 + h2) * 64:(c * 2 + h2) * 64 + 64]
                        if cut < 4:
                            return
                        b = bank()
                        Lakq = q4(W["Lakt"])
                        for q, (c, h2) in enumerate(heads):
                            s.op("pe", lambda e, b=b, q=q, c=c, h2=h2: e.matmul(
                                ps[b][:, q * 64:(q + 1) * 64], Lakq[:, q, :], Vq(c, h2), start=True, stop=True),
                                reads=[W["RLakt"], RVtm], writes=[Rps[b]])
                        s.op("act", lambda e, b=b, Z0q=Z0q: e.copy(Z0q[:, :, 64:128], ps[b][:, 0:256].rearrange("p (q j) -> p q j", q=4)),
                             reads=[Rps[b]], writes=[W["RZ0"]])
                        yield
                        if cut < 5:
                            return
                        zc, lc = 0, 0
                        for k in range(7):
                            Zc, Zn = W["Z%d" % zc], W["Z%d" % (1 - zc)]
                            RZc, RZn = W["RZ%d" % zc], W["RZ%d" % (1 - zc)]
                            Ltc, Lc = W["Lt%d" % lc], W["L%d" % lc]
                            RLtc, RLc = W["RLt%d" % lc], W["RL%d" % lc]
                            b = bank()
                            for q in range(4):
                                s.op("pe", lambda e, b=b, q=q, Zc=Zc, Ltc=Ltc: e.matmul(
                                    ps[b][:, q * 128:(q + 1) * 128], q4(Ltc)[:, q, :], q4(Zc)[:, q, :], start=True, stop=True),
                                    reads=[RZc, RLtc], writes=[Rps[b]])
                            s.op("dve", lambda e, b=b, Zn=Zn, Zc=Zc: e.tensor_tensor(Zn, ps[b][:], Zc, ALU.add),
                                 reads=[Rps[b], RZc], writes=[RZn])
                            if k < 6:
                                Ltn, Ln = W["Lt%d" % (1 - lc)], W["L%d" % (1 - lc)]
                                RLtn, RLn = W["RLt%d" % (1 - lc)], W["RL%d" % (1 - lc)]
                                b2 = bank()
                                for q in range(4):
                                    s.op("pe", lambda e, b2=b2, q=q, Lc=Lc, Ltc=Ltc: e.matmul(
                                        ps[b2][:, q * 128:(q + 1) * 128], q4(Lc)[:, q, :], q4(Ltc)[:, q, :], start=True, stop=True),
                                        reads=[RLc, RLtc], writes=[Rps[b2]])
                                s.op("act", lambda e, b2=b2, Ltn=Ltn: e.copy(Ltn, ps[b2][:]), reads=[Rps[b2]], writes=[RLtn])
                                if k < 5:
                                    b3 = bank()
                                    for q in range(4):
                                        s.op("pe", lambda e, b3=b3, q=q, Lc=Lc, Ltc=Ltc: e.matmul(
                                            ps[b3][:, q * 128:(q + 1) * 128], q4(Ltc)[:, q, :], q4(Lc)[:, q, :], start=True, stop=True),
                                            reads=[RLc, RLtc], writes=[Rps[b3]])
                                    s.op("act", lambda e, b3=b3, Ln=Ln: e.copy(Ln, ps[b3][:]), reads=[Rps[b3]], writes=[RLn])
                                lc = 1 - lc
                            zc = 1 - zc
                            yield
                        s.op("act", lambda e, zc=zc: e.copy(W["Zfb"], W["Z%d" % zc]), reads=[W["RZ%d" % zc]], writes=[W["RZfb"]])
                        Zf = q4(W["Zfb"])
                        RZf = W["RZfb"]
                        Mrbq = q4(W["Mrbt"])
                        Mrkq = q4(W["Mrkt"])
                        if cut < 6:
                            return
                        bM = bank()
                        bN = bank()
                        bR = bank()
                        for q, (c, h2) in enumerate(heads):
                            cl_ = c
                            prt = slice(64 * h2, 64 * h2 + 64)
                            s.op("pe", lambda e, q=q, cl_=cl_, prt=prt: e.matmul(
                                ps[bM][prt, cl_ * 64:(cl_ + 1) * 64], Zf[:, q, 0:64], BKq[:, q, 0:64],
                                start=(q == 0), stop=False, skip_group_check=True),
                                reads=[RZf, W["RBKtm"]], writes=[Rps[bM]])
                            s.op("pe", lambda e, q=q, cl_=cl_, prt=prt: e.matmul(
                                ps[bM][prt, cl_ * 64:(cl_ + 1) * 64], ident_b[0:64, 0:64], ident_b[0:64, 0:64],
                                start=False, stop=True, skip_group_check=True),
                                reads=[Rconst], writes=[Rps[bM]])
                            s.op("pe", lambda e, q=q, cl_=cl_, prt=prt: e.matmul(
                                ps[bN][prt, cl_ * 64:(cl_ + 1) * 64], BKq[:, q, 0:64], Zf[:, q, 64:128],
                                start=(q == 0), stop=False, skip_group_check=True),
                                reads=[RZf, W["RBKtm"]], writes=[Rps[bN]])
                            s.op("pe", lambda e, q=q, cl_=cl_, prt=prt, c=c, h2=h2: e.matmul(
                                ps[bN][prt, cl_ * 64:(cl_ + 1) * 64], BKq[:, q, 64:128], Vq(c, h2),
                                start=False, stop=False, skip_group_check=True),
                                reads=[W["RBKtm"], RVtm], writes=[Rps[bN]])
                            s.op("pe", lambda e, q=q, cl_=cl_, prt=prt: e.matmul(
                                ps[bR][prt, cl_ * 128:(cl_ + 1) * 128], Zf[:, q, 0:64], Mrbq[:, q, :],
                                start=(q == 0), stop=False, skip_group_check=True),
                                reads=[RZf, W["RMrbt"]], writes=[Rps[bR]])
                            s.op("pe", lambda e, q=q, cl_=cl_, prt=prt, c=c, h2=h2: e.matmul(
                                ps[bR][prt, cl_ * 128:(cl_ + 1) * 128], ident_b[prt, prt], fm(TB["RH"], c, h2),
                                start=False, stop=True, skip_group_check=True),
                                reads=[Rconst, TB["RRH"]], writes=[Rps[bR]])
                            st_ = first_fy[0]
                            first_fy[0] = False
                            hh = c * 2 + h2
                            s.op("pe", lambda e, q=q, hh=hh, st_=st_: e.matmul(
                                ps[fy][:, hh * 64:(hh + 1) * 64], Mrbq[:, q, :], Zf[:, q, 64:128],
                                start=st_, stop=False, skip_group_check=True),
                                reads=[RZf, W["RMrbt"]], writes=[Rps[fy]])
                            s.op("pe", lambda e, q=q, hh=hh, c=c, h2=h2: e.matmul(
                                ps[fy][:, hh * 64:(hh + 1) * 64], Mrkq[:, q, :], Vq(c, h2),
                                start=False, stop=False, skip_group_check=True),
                                reads=[W["RMrkt"], RVtm], writes=[Rps[fy]])
                        MTv = W["MT"].rearrange("p (c j) -> p c j", c=4)
                        RTv = W["RT"].rearrange("p (c t) -> p c t", c=4)
                        pg = slice(64 * g, 64 * g + 64)
                        s.op("act", lambda e, MTv=MTv, pg=pg: e.copy(MTv[pg], ps[bM][pg, 0:256].rearrange("p (c j) -> p c j", c=4)),
                             reads=[Rps[bM]], writes=[W["RMT"]])
                        s.op("act", lambda e, RTv=RTv, pg=pg: e.copy(RTv[pg], ps[bR][pg, 0:512].rearrange("p (c t) -> p c t", c=4)),
                             reads=[Rps[bR]], writes=[W["RRT"]])
                        for q, (c, h2) in enumerate(heads):
                            cl_ = c
                            prt = slice(64 * h2, 64 * h2 + 64)
                            hh = c * 2 + h2
                            s.op("pe", lambda e, cl_=cl_, prt=prt, hh=hh, c=c, RTv=RTv: e.matmul(
                                ps[fy][:, hh * 64:(hh + 1) * 64], RTv[prt, cl_, :], Abd3[prt, c, :],
                                start=False, stop=True, skip_group_check=True),
                                reads=[W["RRT"], RAbf], writes=[Rps[fy]])
                            s.op("pe", lambda e, cl_=cl_, prt=prt, c=c, MTv=MTv: e.matmul(
                                ps[bN][prt, cl_ * 64:(cl_ + 1) * 64], MTv[prt, cl_, :], Abd3[prt, c, :],
                                start=False, stop=True, skip_group_check=True),
                                reads=[W["RMT"], RAbf], writes=[Rps[bN]])
                        s.op("dve", lambda e, pg=pg: e.tensor_tensor(
                            Ad3[pg, :, :], ps[bN][pg, 0:256].rearrange("p (c i) -> p c i", c=4),
                            pcs[pg, par, :].unsqueeze(2).to_broadcast([64, 4, 64]), ALU.mult),
                            reads=[Rps[bN], Rpcs[par], RAbf], writes=[RAst])
                    gens = [group_body(0), group_body(1), nxt]
                    while gens:
                        for gn in list(gens):
                            try:
                                next(gn)
                            except StopIteration:
                                gens.remove(gn)
                    if cut < 6:
                        reserved.discard(fy)
                        return
                    s.op("act", lambda e: e.copy(Abd, Ad), reads=[RAst], writes=[RAbf])
                    isend = (ti % 2 == 1) if d == 0 else (ti % 2 == 0)
                    if isend:
                        s.dma("sp", s_out[u, d], Ad, reads=[RAst])
                    reserved.discard(fy)
                    if d == 0:
                        s.op("act", lambda e, ti=ti: e.copy(YSb[:, ti, :], ps[fy][:]), reads=[Rps[fy]], writes=[RYS])
                    else:
                        finalize(ti, fy)
                for n_ in range(len(seq)):
                    tile_body(n_)

        if stage >= 1:
            if "a" in sub:
                make_gg(0)
            if "b" in sub:
                make_hT(0, 0)
            if "c" in sub:
                even_inproj()
        if stage >= 3:
            attention()
            if P0L1:
                for _ in P0L1[0]:
                    pass
                del P0L1[:]
            barrier()
            if stage >= 5:
                rwkv_setup()
                rwkv_prepass()
                rwkv_chunked(NT if stage >= 6 else 2)
            else:
                s.op("pool", lambda e: e.memset(yT[:, 4:8, :], 0.0), writes=[RyT])
            barrier()
            out_proj(w_out_even, yT, RyT)
        if stage >= 2:
            barrier()
            ffn(0)
            barrier()
        if stage >= 4:
            if P0L1:
                for _ in P0L1[0]:
                    pass
                del P0L1[:]
            make_gg(1)
            make_hT(1, 0)
            fnet()
            out_proj(w_out_odd, yT, RyT)
            barrier()
            ffn(1)
        for i in range(NT):
            s.dma("sp", y_out[i * 128:(i + 1) * 128, :], x_sb[:, i, :], reads=[Rx[i]])
        s.emit()
    return nc


def _core_units(c):
    if c < 6:
        return [("p", 5 * c + u) for u in range(5)]
    b = c - 6
    return [("s", b, u) for u in range(4)] + [("p", 30 + b)]


def _host_prep(inp):
    f32 = np.float32
    COFF, NCOL = colp_layout()
    ROFF, NROW = rowp_layout()
    g = {k: np.asarray(v) for k, v in inp.items()}
    colp = np.zeros((128, NCOL), f32)

    def put(name, vec):
        c = cols_of(vec)
        colp[:, COFF[name]:COFF[name] + c.shape[1]] = c
    for l in range(2):
        put("bada%d" % l, g["b_ada"][l])
        for k in range(4):
            put("gain%d_%d" % (l, k), g["norm_gains"][l, k])
        for k in range(3):
            put("conv%d_%d" % (l, k), g["ffn_conv"][l, k])
        put("convb%d" % l, g["ffn_conv_b"][l])
    put("mu0", g["rwkv_shift_mu"][0, 0])
    put("mu1", g["rwkv_shift_mu"][0, 1])
    for d in range(2):
        put("w0_%d" % d, g["rwkv_w0"][0, d])
        put("a0_%d" % d, g["rwkv_a0"][0, d])
    for k in range(3):
        put("kvec%d" % k, g["rwkv_kvec"][0, k])

    cmat = np.zeros((8, 128, 128), f32)
    cmat[0] = np.eye(128, dtype=f32)
    for i in range(64):
        cmat[1][2 * i, 2 * i + 1] = 1.0
        cmat[1][2 * i + 1, 2 * i] = -1.0
    cmat[2][:64, :64] = 1.0
    cmat[2][64:, 64:] = 1.0
    cmat[3][:64, 0] = 1.0
    cmat[3][64:, 1] = 1.0
    rr_, cc_ = np.meshgrid(np.arange(128), np.arange(128), indexing="ij")
    cmat[4] = (rr_ < cc_)
    cmat[5] = (rr_ > cc_)
    cmat[6] = (rr_ <= cc_)
    cmat[7] = (rr_ >= cc_)
    cc = np.arange(128)
    chang = 2.0 * np.pi * ((cc[:, None] * cc[None, :]) % 128) / 128.0
    chCS = np.concatenate([np.cos(chang), np.sin(chang)], axis=1).astype(f32)

    inv = (10000.0 ** (-np.arange(16, dtype=np.float32) / 16)).astype(f32)

    shared = dict(
        colp=colp, w_ada=g["w_ada"], w_in_even=g["w_in_even"][0], w_out_even=g["w_out_even"][0],
        w_out_odd=g["w_out_odd"][0], w_ffn_in=g["w_ffn_in"], w_ffn_out=g["w_ffn_out"],
        w2t=np.ascontiguousarray(g["rwkv_w2"][0].reshape(128, 512)),
        a2t=np.ascontiguousarray(g["rwkv_a2"][0].reshape(128, 512)),
        g2=np.ascontiguousarray(g["rwkv_g2"][0]), chCS=chCS, cmat=cmat,
        lnx=np.ascontiguousarray(g["rwkv_lnx"][0]))
    maps = []
    for c in range(NCORES):
        units = _core_units(c)
        xs = []
        for un in units:
            if un[0] == "p":
                xs.append(g["x_prompt"][un[1]])
            else:
                xs.append(g["x_sample"][un[1], un[2] * 256:(un[2] + 1) * 256])
        x_in = np.ascontiguousarray(np.concatenate(xs, axis=0), dtype=f32)
        is_s = c >= 6
        condA = g["c"][c - 6] if is_s else g["c_ctx"]
        condB = g["c_ctx"]
        cond = np.stack([condA, condB], axis=0).astype(f32)
        condT = np.ascontiguousarray(cond.reshape(2, 8, 128).transpose(2, 1, 0).reshape(128, 16))
        rowp = np.zeros((1, NROW), f32)
        rowp[0, ROFF["lam"]:ROFF["lam"] + 256] = g["diff_lambda"][0].reshape(-1)
        rowp[0, ROFF["subln"]:ROFF["subln"] + 128] = g["diff_subln"][0]
        ab = np.full((6, 5), -30000.0, f32)
        if is_s:
            ab[0:5, 0:4] = 0.0
            ab[5, 4] = 0.0
            bndL = [1, 0, 0, 0, 1]
            bndR = [0, 0, 0, 1, 1]
        else:
            for u in range(5):
                ab[u + 1, u] = 0.0
            bndL = [1] * 5
            bndR = [1] * 5
        rowp[0, ROFF["abias"]:ROFF["abias"] + 30] = ab.reshape(-1)
        rowp[0, ROFF["bndL"]:ROFF["bndL"] + 5] = bndL
        rowp[0, ROFF["bndR"]:ROFF["bndR"] + 5] = bndR
        ropeC = np.ones((128, T), f32)
        ropeS = np.zeros((128, T), f32)
        if is_s:
            t = np.arange(1024)
            row = (t // 64).astype(f32)
            col = (t % 64).astype(f32)
            ang = np.concatenate([row[:, None] * inv[None, :], col[:, None] * inv[None, :]], axis=1).astype(f32)
            pidx = (np.arange(128) % 64) // 2
            ropeC[:, :1024] = np.cos(ang)[:, pidx].T
            ropeS[:, :1024] = np.sin(ang)[:, pidx].T
        cacheKT = np.zeros((512, 256), f32)
        cacheV = np.zeros((256, 512), f32)
        initS = np.zeros((2, 128, 256), f32)
        if is_s:
            b = c - 6
            cacheKT[:] = g["cache_k"][b, 0].reshape(256, 512).T
            cacheV[:] = g["cache_v"][b, 0].reshape(256, 512)
            st = g["state_wkv"][b, 0]
            initS[:] = st.reshape(2, 4, 2, 64, 64).transpose(0, 2, 4, 1, 3).reshape(2, 128, 256)
        dC = np.zeros((T, T), np.float64)
        dS = np.zeros((T, T), np.float64)
        blocks = [(0, 1024), (1024, 256)] if is_s else [(256 * u, 256) for u in range(5)]
        for (a0, L) in blocks:
            ll = np.arange(L)
            ang = 2.0 * np.pi * ((ll[:, None] * ll[None, :]) % L) / L
            sc = 1.0 / math.sqrt(L * 128.0)
            dC[a0:a0 + L, a0:a0 + L] = np.cos(ang) * sc
            dS[a0:a0 + L, a0:a0 + L] = -np.sin(ang) * sc
        m = dict(shared)
        m.update(x_in=x_in, condT=condT, rowp=rowp, ropeC=ropeC, ropeS=ropeS, cacheKT=cacheKT, cacheV=cacheV,
                 initS=initS, dftC=dC.astype(f32), dftSn=dS.astype(f32))
        maps.append(m)
    return maps


_NC_CACHE = {}


def kernel(**inputs):
    maps = _host_prep(inputs)
    if "nc" not in _NC_CACHE:
        _NC_CACHE["nc"] = build()
    nc = _NC_CACHE["nc"]
    import os
    ncr = int(os.environ.get("NCR", NCORES))
    res = run_bass_kernel_spmd(nc, maps[:ncr], core_ids=list(range(ncr)))
    outs = list(res.results) + [res.results[0]] * (NCORES - ncr)
    y_prompt = np.zeros((32, 256, D), np.float32)
    y_sample = np.zeros((2, 1024, D), np.float32)
    nk = np.zeros((32, 1, 256, 4, 128), np.float32)
    nv = np.zeros((32, 1, 256, 4, 128), np.float32)
    ns = np.zeros((32, 1, 2, 8, 64, 64), np.float32)
    for c in range(NCORES):
        r = outs[c]
        for u, un in enumerate(_core_units(c)):
            sl = slice(u * 256, (u + 1) * 256)
            if un[0] == "p":
                bi = un[1]
                y_prompt[bi] = r["y_out"][sl]
                nk[bi, 0] = r["k_out"][sl].reshape(256, 4, 128)
                nv[bi, 0] = r["v_out"][sl].reshape(256, 4, 128)
                stt = r["s_out"][u]
                ns[bi, 0] = stt.reshape(2, 2, 64, 4, 64).transpose(0, 3, 1, 4, 2).reshape(2, 8, 64, 64)
            else:
                y_sample[un[1], un[2] * 256:(un[2] + 1) * 256] = r["y_out"][sl]
    return (y_prompt, y_sample, nk, nv, ns)
```

```python
import contextlib
import math
import numpy as np
import concourse.bass as bass
import concourse.mybir as mybir
from concourse.bass_utils import run_bass_kernel_spmd

F32 = mybir.dt.float32
BF16 = mybir.dt.bfloat16
AF = mybir.ActivationFunctionType
ALU = mybir.AluOpType
AX = mybir.AxisListType

T = 1280
NT = 10
U = 5
D = 1024
KC = 8
DFF = 2816
FC = 22
NCORES = 8
ARENA_W = 27200
EXPM05 = math.exp(-0.5)


class Res:
    __slots__ = ("name", "writer", "readers", "excl")

    def __init__(self, name, excl=False):
        self.name = name
        self.writer = None
        self.readers = []
        self.excl = excl


class Sched:
    ENGS = ("pe", "act", "dve", "pool", "sp")
    NDMA = 6

    def __init__(self, nc):
        self.nc = nc
        self.prog = {e: [] for e in self.ENGS}
        self.signal = {e: set() for e in self.ENGS}
        self.ndma = {e: 0 for e in self.ENGS}

    def _collect(self, reads, writes, eng=None):
        deps = []
        for r in reads:
            if r.writer is not None:
                deps.append(r.writer)
            if r.excl:
                deps.extend(t for t in r.readers if t[1] != eng)
        for w in writes:
            if w.writer is not None:
                deps.append(w.writer)
            deps.extend(w.readers)
        return deps

    def _commit(self, tok, reads, writes):
        for r in reads:
            r.readers.append(tok)
        for w in writes:
            w.writer = tok
            w.readers = []

    def op(self, eng, fn, reads=(), writes=()):
        deps = self._collect(reads, writes, eng)
        idx = len(self.prog[eng])
        if eng == "pe":
            deps = [d for d in deps if not (d[0] == "c" and d[1] == "pe")]
        for d in deps:
            if d[0] == "c":
                self.signal[d[1]].add(d[2])
        self.prog[eng].append(dict(fn=fn, deps=deps, kind="c"))
        tok = ("c", eng, idx)
        self._commit(tok, reads, writes)
        return tok

    def dma(self, eng, out, in_, reads=(), writes=()):
        deps = self._collect(reads, writes, eng)
        n = self.ndma[eng]
        self.ndma[eng] += 1
        if n >= self.NDMA:
            deps.append(("d", eng, n - self.NDMA))
        for d in deps:
            if d[0] == "c":
                self.signal[d[1]].add(d[2])
        self.prog[eng].append(dict(out=out, in_=in_, deps=deps, kind="d", n=n))
        tok = ("d", eng, n)
        self._commit(tok, reads, writes)
        return tok

    def emit(self):
        nc = self.nc
        with contextlib.ExitStack() as st:
            csem = {e: st.enter_context(nc.semaphore("c_" + e)) for e in self.ENGS}
            dsem = {e: [st.enter_context(nc.semaphore("d_%s_%d" % (e, i))) for i in range(self.NDMA)]
                    for e in self.ENGS if self.ndma[e] > 0}
            sigval = {}
            for e in self.ENGS:
                cnt = 0
                m = {}
                for i in range(len(self.prog[e])):
                    if i in self.signal[e]:
                        cnt += 1
                        m[i] = cnt
                sigval[e] = m

            def resolve(tok):
                if tok[0] == "c":
                    return csem[tok[1]], sigval[tok[1]][tok[2]], ("c", tok[1])
                e, n = tok[1], tok[2]
                return dsem[e][n % self.NDMA], 16 * (n // self.NDMA + 1), ("d", e, n % self.NDMA)

            def run_engine(e, h):
                waited = {}
                for i, ins in enumerate(self.prog[e]):
                    need = {}
                    for d in ins["deps"]:
                        sem, val, key = resolve(d)
                        if waited.get(key, 0) >= val:
                            continue
                        if key not in need or need[key][1] < val:
                            need[key] = (sem, val)
                    for key, (sem, val) in need.items():
                        h.wait_ge(sem, val)
                        waited[key] = val
                    if ins["kind"] == "c":
                        bi = ins["fn"](h)
                        if i in self.signal[e]:
                            bi.then_inc(csem[e], 1)
                    else:
                        n = ins["n"]
                        h.dma_start(out=ins["out"], in_=ins["in_"]).then_inc(dsem[e][n % self.NDMA], 16)
                if self.ndma[e] > 0:
                    n = self.ndma[e]
                    for slot in range(self.NDMA):
                        cnt = (n - slot + self.NDMA - 1) // self.NDMA if n > slot else 0
                        if cnt > 0:
                            h.wait_ge(dsem[e][slot], 16 * cnt)

            with nc.Block() as block:
                @block.tensor
                def _(eng):
                    run_engine("pe", eng)

                @block.scalar
                def _(eng):
                    run_engine("act", eng)

                @block.vector
                def _(eng):
                    run_engine("dve", eng)

                @block.gpsimd
                def _(eng):
                    run_engine("pool", eng)

                @block.sync
                def _(eng):
                    run_engine("sp", eng)


def colp_layout():
    off = {}
    n = 0

    def add(name, cols):
        nonlocal n
        off[name] = n
        n += cols
    for l in range(2):
        add("bada%d" % l, 48)
        for g in range(4):
            add("gain%d_%d" % (l, g), 8)
        for k in range(3):
            add("conv%d_%d" % (l, k), FC)
        add("convb%d" % l, FC)
    add("mu0", 15)
    add("mu1", 15)
    for d in range(2):
        add("w0_%d" % d, 4)
        add("a0_%d" % d, 4)
    for k in range(3):
        add("kvec%d" % k, 4)
    return off, n


def rowp_layout():
    off = {}
    n = 0

    def add(name, cols):
        nonlocal n
        off[name] = n
        n += cols
    add("lam", 256)
    add("subln", 128)
    add("abias", 30)
    add("bndL", 5)
    add("bndR", 5)
    return off, n


def cols_of(vec):
    v = np.asarray(vec, np.float32).reshape(-1, 128)
    return np.ascontiguousarray(v.T)


def build(stage=99):
    nc = bass.Bass("TRN2", target_bir_lowering=False)
    COFF, NCOL = colp_layout()
    ROFF, NROW = rowp_layout()

    def din(name, shape):
        return nc.dram_tensor(name, list(shape), F32, kind="ExternalInput").ap()

    def dout(name, shape):
        return nc.dram_tensor(name, list(shape), F32, kind="ExternalOutput").ap()

    x_in = din("x_in", [T, D])
    condT = din("condT", [128, 16])
    colp = din("colp", [128, NCOL])
    rowp = din("rowp", [1, NROW])
    w_ada = din("w_ada", [2, D, 6 * D])
    w_in_even = din("w_in_even", [D, 3456])
    w_out_even = din("w_out_even", [D, D])
    w_out_odd = din("w_out_odd", [D, D])
    w_ffn_in = din("w_ffn_in", [2, D, 2 * DFF])
    w_ffn_out = din("w_ffn_out", [2, DFF, D])
    w2t_d = din("w2t", [128, 512])
    a2t_d = din("a2t", [128, 512])
    g2_d = din("g2", [128, 512])
    cacheKT = din("cacheKT", [512, 256])
    cacheV = din("cacheV", [256, 512])
    ropeC = din("ropeC", [128, T])
    ropeS = din("ropeS", [128, T])
    initS = din("initS", [2, 128, 256])
    dftC = din("dftC", [T, T])
    dftSn = din("dftSn", [T, T])
    chCS = din("chCS", [128, 256])
    cmat = din("cmat", [8, 128, 128])
    lnx_d = din("lnx", [2, 512])

    y_out = dout("y_out", [T, D])
    k_out = dout("k_out", [T, 512])
    v_out = dout("v_out", [T, 512])
    s_out = dout("s_out", [U, 2, 128, 256])

    import os
    sub = os.environ.get("SUB", "abcmqv")
    with contextlib.ExitStack() as st:
        s = Sched(nc)

        def sb(name, shape, dt=F32):
            return st.enter_context(nc.sbuf_tensor(name, list(shape), dt))

        x_sb = sb("x_sb", [128, NT, D])
        Rx = [Res("x%d" % i) for i in range(NT)]
        WB = 4096
        wbuf = [sb("wb%d" % i, [128, WB], BF16) for i in range(3)]
        Rw = [Res("wb%d" % i) for i in range(3)]
        wctr = [0]
        colp_sb = sb("colp_sb", [128, NCOL])
        Rcolp = Res("colp")
        rowp_sb = sb("rowp_sb", [128, NROW])
        Rrowp = Res("rowp")
        ident_f = sb("ident_f", [128, 128])
        ident_b = sb("ident_b", [128, 128], BF16)
        rotT_b = sb("rotT_b", [128, 128], BF16)
        bones_f = sb("bones_f", [128, 128])
        halfsel_f = sb("halfsel_f", [128, 128])
        bones_b = sb("bones_b", [128, 128], BF16)
        Rconst = Res("const")
        gg_sb = sb("gg_sb", [128, 2, 2, D], BF16)
        Rgg = Res("gg")
        scond = sb("scond", [128, 16], BF16)
        condf = sb("condf", [128, 16])
        Rcond = Res("cond")
        modcol = sb("modcol", [128, 2, 96])
        Rmod = Res("modcol")
        scsh = sb("scsh", [128, 2, 2, 2, 16])
        Rscsh = Res("scsh")
        stat = sb("stat", [128, 64])
        Rstat = Res("stat")
        junk_b = sb("junk_b", [128, D], BF16)
        Rjunk = Res("junk")
        xn_b = sb("xn_b", [128, D], BF16)
        Rxn = Res("xn")
        stage_f = sb("stage_f", [128, 2, 512])
        Rstage = [Res("stage0"), Res("stage1")]
        stctr = [0]
        arena = sb("arena", [128, ARENA_W])
        Rar = {}

        def ares(name):
            if name not in Rar:
                Rar[name] = Res("ar_" + name)
            return Rar[name]

        def carve(off_w, nelem, dt):
            if dt == F32:
                return arena[:, off_w:off_w + nelem]
            return arena[:, off_w:off_w + (nelem + 1) // 2].bitcast(BF16)[:, 0:nelem]

        ps = [st.enter_context(nc.psum_tensor("ps%d" % b, [128, 512], F32)) for b in range(8)]
        psb = [p.bitcast(BF16) for p in ps]
        Rps = [Res("ps%d" % b, excl=True) for b in range(8)]
        bctr = [0]

        reserved = set()

        def bank():
            while True:
                b = bctr[0] % 8
                bctr[0] += 1
                if b not in reserved:
                    return b

        def barrier():
            allr = list(Rar.values()) + list(Rw)
            s.op("dve", lambda e: e.memset(stat[:, 63:64], 0.0), writes=allr + [Rstat])

        def wload(src_ap, kc, ncols):
            i = wctr[0] % 3
            wctr[0] += 1
            view = wbuf[i][:, 0:kc * ncols].rearrange("p (c n) -> p c n", c=kc)
            s.dma("pool", view, src_ap.rearrange("(c p) n -> p c n", p=128), writes=[Rw[i]])
            return view, Rw[i]

        s.dma("sp", colp_sb[:], colp, writes=[Rcolp])
        s.dma("sp", rowp_sb[:], rowp[0, :].partition_broadcast(128), writes=[Rrowp])
        s.dma("sp", ident_f[:], cmat[0], writes=[Rconst])
        s.dma("sp", bones_f[:], cmat[2], writes=[Rconst])
        s.dma("sp", halfsel_f[:], cmat[3], writes=[Rconst])
        s.dma("pool", ident_b[:], cmat[0], writes=[Rconst])
        s.dma("pool", rotT_b[:], cmat[1], writes=[Rconst])
        s.dma("pool", bones_b[:], cmat[2], writes=[Rconst])
        s.dma("sp", condf[:], condT, writes=[Rcond])
        for i in range(NT):
            s.dma("sp", x_sb[:, i, :], x_in[i * 128:(i + 1) * 128, :], writes=[Rx[i]])
        s.op("act", lambda e: e.activation(scond[:], condf[:], AF.Silu), reads=[Rcond], writes=[Rcond])

        def cp(name, c=0, n=1):
            o = COFF[name] + c
            return colp_sb[:, o:o + n]

        def p0_gen(l):
            b = bank()
            reserved.add(b)
            for v in range(6):
                for half in range(2):
                    wv, rw = wload(w_ada[l][:, v * 1024 + half * 512: v * 1024 + half * 512 + 512], KC, 512)
                    for j in range(4):
                        col = (v * 8 + half * 4 + j) * 2
                        for kc in range(KC):
                            s.op("pe", lambda e, wv=wv, j=j, kc=kc, col=col, b=b: e.matmul(
                                ps[b][:, col:col + 2], wv[:, kc, j * 128:(j + 1) * 128],
                                scond[:, kc * 2:kc * 2 + 2], start=(kc == 0), stop=(kc == KC - 1)),
                                reads=[rw, Rcond], writes=[Rps[b]])
                    yield
            s.op("dve", lambda e, l=l, b=b: e.tensor_tensor(
                modcol[:, l, :].rearrange("p (c a) -> p c a", a=2),
                ps[b][:, 0:96].rearrange("p (c a) -> p c a", a=2),
                cp("bada%d" % l, 0, 48).unsqueeze(2).to_broadcast([128, 48, 2]), ALU.add),
                reads=[Rps[b], Rcolp], writes=[Rmod])
            reserved.discard(b)

            def mv(v, l=l):
                return modcol[:, l, v * 16:(v + 1) * 16].rearrange("p (c a) -> p c a", a=2)

            def gain(g, l=l):
                return cp("gain%d_%d" % (l, g), 0, 8).unsqueeze(2).to_broadcast([128, 8, 2])
            for wi, (vs, vsh, g) in enumerate([(1, 0, 0), (4, 3, 2)]):
                sc = scsh[:, l, wi, 0, :].rearrange("p (c a) -> p c a", a=2)
                sh = scsh[:, l, wi, 1, :].rearrange("p (c a) -> p c a", a=2)
                s.op("dve", lambda e, sc=sc, vs=vs, mv=mv: e.tensor_scalar(sc, mv(vs), 1.0, None, ALU.add),
                     reads=[Rmod], writes=[Rscsh])
                s.op("dve", lambda e, sc=sc, g=g, gain=gain: e.tensor_tensor(sc, sc, gain(g), ALU.mult),
                     reads=[Rscsh, Rcolp], writes=[Rscsh])
                s.op("dve", lambda e, sh=sh, vsh=vsh, mv=mv: e.tensor_copy(sh, mv(vsh)),
                     reads=[Rmod], writes=[Rscsh])

        for _ in p0_gen(0):
            pass
        P0L1 = [p0_gen(1)]

        ggcol = sb("ggcol", [128, 2, 16])
        Rggcol = Res("ggcol")

        def make_gg(l):
            for wi, (vg, g) in enumerate([(2, 1), (5, 3)]):
                gc = ggcol[:, wi, :].rearrange("p (c a) -> p c a", a=2)
                s.op("dve", lambda e, gc=gc, vg=vg, g=g, l=l: e.tensor_tensor(
                    gc, modcol[:, l, vg * 16:(vg + 1) * 16].rearrange("p (c a) -> p c a", a=2),
                    cp("gain%d_%d" % (l, g), 0, 8).unsqueeze(2).to_broadcast([128, 8, 2]), ALU.mult),
                    reads=[Rmod, Rcolp], writes=[Rggcol])
                for a in range(2):
                    for hh in range(2):
                        b = bank()
                        for j in range(4):
                            c = hh * 4 + j
                            s.op("pe", lambda e, b=b, wi=wi, c=c, a=a, j=j: e.matmul(
                                ps[b][:, j * 128:(j + 1) * 128],
                                ggcol[:, wi, c * 2 + a:c * 2 + a + 1].to_broadcast([128, 128]),
                                ident_f[:], start=True, stop=True),
                                reads=[Rggcol, Rconst], writes=[Rps[b]])
                        s.op("act", lambda e, b=b, wi=wi, a=a, hh=hh: e.copy(
                            gg_sb[:, wi, a, hh * 512:(hh + 1) * 512], ps[b][:]),
                            reads=[Rps[b]], writes=[Rgg])

        hT = carve(0, 8 * T, BF16).rearrange("p (c t) -> p c t", c=8)
        RhT = ares("hT")

        def ab_of_tile(i):
            return 0 if i < 8 else 1

        def make_hT(l, wi):
            for i in range(NT):
                a = ab_of_tile(i)
                s.op("act", lambda e, i=i: e.activation(junk_b[:], x_sb[:, i, :], AF.Square, scale=1.0 / 32.0,
                                                        accum_out=stat[:, 0:1]),
                     reads=[Rx[i]], writes=[Rjunk, Rstat])
                s.op("dve", lambda e: e.tensor_scalar(stat[:, 1:2], stat[:, 0:1], 1e-6, None, ALU.add),
                     reads=[Rstat], writes=[Rstat])
                s.op("act", lambda e: e.activation(stat[:, 1:2], stat[:, 1:2], AF.Ln), reads=[Rstat], writes=[Rstat])
                s.op("act", lambda e: e.activation(stat[:, 2:3], stat[:, 1:2], AF.Exp, scale=-0.5), reads=[Rstat], writes=[Rstat])
                s.op("dve", lambda e, i=i: e.tensor_scalar(xn_b[:], x_sb[:, i, :], stat[:, 2:3], None, ALU.mult),
                     reads=[Rx[i], Rstat], writes=[Rxn])
                b = bank()
                for c in range(8):
                    s.op("pe", lambda e, b=b, c=c: e.transpose(psb[b][:, c * 128:(c + 1) * 128],
                                                               xn_b[:, c * 128:(c + 1) * 128], ident_b[:]),
                         reads=[Rxn, Rconst], writes=[Rps[b]])
                for c in range(8):
                    s.op("act", lambda e, b=b, c=c, i=i, a=a: e.activation(
                        hT[:, c, i * 128:(i + 1) * 128], psb[b][:, c * 128:(c + 1) * 128], AF.Identity,
                        bias=scsh[:, l, wi, 1, c * 2 + a:c * 2 + a + 1],
                        scale=scsh[:, l, wi, 0, c * 2 + a:c * 2 + a + 1]),
                        reads=[Rps[b], Rscsh], writes=[RhT])

        TOKCH = [(0, 512), (512, 512), (1024, 256)]

        def linear_fm(wv, rw, ncol0, actT, Ract, kc_n, evac):
            for (t0, tn) in TOKCH:
                b = bank()
                for kc in range(kc_n):
                    s.op("pe", lambda e, b=b, kc=kc, t0=t0, tn=tn: e.matmul(
                        ps[b][:, 0:tn], wv[:, kc, ncol0:ncol0 + 128], actT[:, kc, t0:t0 + tn],
                        start=(kc == 0), stop=(kc == kc_n - 1)),
                        reads=[rw, Ract], writes=[Rps[b]])
                evac(b, t0, tn)

        def linear_tm(wv, rw, ncols, actT, Ract, kc_n, i, b, col0=0):
            for kc in range(kc_n):
                s.op("pe", lambda e, kc=kc: e.matmul(
                    ps[b][:, col0:col0 + ncols], actT[:, kc, i * 128:(i + 1) * 128], wv[:, kc, 0:ncols],
                    start=(kc == 0), stop=(kc == kc_n - 1)),
                    reads=[rw, Ract], writes=[Rps[b]])

        def residual_update(i, banks, wi, fsrc=None, Rf=None):
            a = ab_of_tile(i)
            if fsrc is None:
                for hh, b in enumerate(banks):
                    s.op("act", lambda e, b=b, hh=hh: e.activation(junk_b[:, 0:512], ps[b][:], AF.Square,
                                                                   scale=1.0 / 32.0, accum_out=stat[:, 8 + hh:9 + hh]),
                         reads=[Rps[b]], writes=[Rjunk, Rstat])
                s.op("dve", lambda e: e.tensor_tensor(stat[:, 10:11], stat[:, 8:9], stat[:, 9:10], ALU.add),
                     reads=[Rstat], writes=[Rstat])
            else:
                s.op("act", lambda e: e.activation(junk_b[:], fsrc, AF.Square, scale=1.0 / 32.0,
                                                   accum_out=stat[:, 10:11]),
                     reads=[Rf], writes=[Rjunk, Rstat])
            s.op("dve", lambda e: e.tensor_scalar(stat[:, 11:12], stat[:, 10:11], 1e-6, None, ALU.add),
                 reads=[Rstat], writes=[Rstat])
            s.op("act", lambda e: e.activation(stat[:, 11:12], stat[:, 11:12], AF.Ln), reads=[Rstat], writes=[Rstat])
            s.op("act", lambda e: e.activation(stat[:, 12:13], stat[:, 11:12], AF.Exp, scale=-0.5), reads=[Rstat], writes=[Rstat])
            for hh in range(2):
                src = ps[banks[hh]][:] if fsrc is None else fsrc[:, hh * 512:(hh + 1) * 512]
                rr = [Rps[banks[hh]]] if fsrc is None else [Rf]
                si = stctr[0] % 2
                stctr[0] += 1
                s.op("dve", lambda e, src=src, hh=hh, si=si: e.scalar_tensor_tensor(
                    stage_f[:, si, :], src, stat[:, 12:13], gg_sb[:, wi, a, hh * 512:(hh + 1) * 512],
                    ALU.mult, ALU.mult),
                    reads=rr + [Rstat, Rgg], writes=[Rstage[si]])
                s.op("dve", lambda e, hh=hh, si=si, i=i: e.tensor_tensor(
                    x_sb[:, i, hh * 512:(hh + 1) * 512], x_sb[:, i, hh * 512:(hh + 1) * 512], stage_f[:, si, :], ALU.add),
                    reads=[Rstage[si], Rx[i]], writes=[Rx[i]])

        actT = carve(5120, FC * T, BF16).rearrange("p (c t) -> p c t", c=FC)
        RactT = ares("actT")
        graw = carve(19200, T + 2, F32)
        Rgraw = ares("graw")
        cbuf = carve(19200 + 1284, T, F32)
        Rcbuf = ares("cbuf")
        sbuf_s = carve(19200 + 1284 + 1280, T, F32)
        Rsbuf = ares("sbuf_s")
        fstageA = carve(19200, 5 * D, F32).rearrange("p (i n) -> p i n", i=5)
        fstageB = carve(0, 5 * D, F32).rearrange("p (i n) -> p i n", i=5)

        def fst(i):
            return fstageA[:, i, :] if i < 5 else fstageB[:, i - 5, :]

        def Rfst_of(i):
            return ares("fstage") if i < 5 else RhT
        bcorr = sb("bcorr", [128, 16])
        Rbcorr = Res("bcorr")

        def ffn(l, emit_out=False):
            make_hT(l, 1)
            s.op("dve", lambda e: e.memset(graw[:, 0:1], 0.0), writes=[Rgraw])
            s.op("dve", lambda e: e.memset(graw[:, T + 1:T + 2], 0.0), writes=[Rgraw])
            for blk in range(0, FC, 2):
                wu, ru = wload(w_ffn_in[l][:, blk * 128: blk * 128 + 256], KC, 256)
                wg, rg = wload(w_ffn_in[l][:, DFF + blk * 128: DFF + blk * 128 + 256], KC, 256)
                for jj in range(2):
                    fc = blk + jj
                    def evac_g(b, t0, tn):
                        s.op("act", lambda e, b=b, t0=t0, tn=tn: e.copy(graw[:, 1 + t0:1 + t0 + tn], ps[b][:, 0:tn]),
                             reads=[Rps[b]], writes=[Rgraw])
                    linear_fm(wg, rg, jj * 128, hT, RhT, KC, evac_g)
                    w0 = cp("conv%d_0" % l, fc)
                    w1 = cp("conv%d_1" % l, fc)
                    w2 = cp("conv%d_2" % l, fc)
                    cb = cp("convb%d" % l, fc)
                    s.op("act", lambda e, w1=w1, cb=cb: e.activation(cbuf[:], graw[:, 1:T + 1], AF.Identity, bias=cb, scale=w1),
                         reads=[Rgraw, Rcolp], writes=[Rcbuf])
                    s.op("dve", lambda e, w0=w0: e.scalar_tensor_tensor(cbuf[:], graw[:, 0:T], w0, cbuf[:], ALU.mult, ALU.add),
                         reads=[Rgraw, Rcolp, Rcbuf], writes=[Rcbuf])
                    s.op("dve", lambda e, w2=w2: e.scalar_tensor_tensor(cbuf[:], graw[:, 2:T + 2], w2, cbuf[:], ALU.mult, ALU.add),
                         reads=[Rgraw, Rcolp, Rcbuf], writes=[Rcbuf])
                    gprev = graw[:, 256:256 + 1024].rearrange("p (u k) -> p u k", k=256)[:, :, 0]
                    gnext = graw[:, 257:257 + 1024].rearrange("p (u k) -> p u k", k=256)[:, :, 0]
                    s.op("dve", lambda e, gprev=gprev: e.tensor_tensor(bcorr[:, 0:4], gprev, rowp_sb[:, ROFF["bndL"] + 1:ROFF["bndL"] + 5], ALU.mult),
                         reads=[Rgraw, Rrowp], writes=[Rbcorr])
                    s.op("dve", lambda e, w0=w0: e.tensor_scalar(bcorr[:, 0:4], bcorr[:, 0:4], w0, None, ALU.mult),
                         reads=[Rbcorr, Rcolp], writes=[Rbcorr])
                    c_at = cbuf[:, 256:256 + 1024].rearrange("p (u k) -> p u k", k=256)[:, :, 0]
                    s.op("dve", lambda e, c_at=c_at: e.tensor_tensor(c_at, c_at, bcorr[:, 0:4], ALU.subtract),
                         reads=[Rbcorr, Rcbuf], writes=[Rcbuf])
                    s.op("dve", lambda e, gnext=gnext: e.tensor_tensor(bcorr[:, 4:8], gnext, rowp_sb[:, ROFF["bndL"] + 1:ROFF["bndL"] + 5], ALU.mult),
                         reads=[Rgraw, Rrowp], writes=[Rbcorr])
                    s.op("dve", lambda e, w2=w2: e.tensor_scalar(bcorr[:, 4:8], bcorr[:, 4:8], w2, None, ALU.mult),
                         reads=[Rbcorr, Rcolp], writes=[Rbcorr])
                    c_at2 = cbuf[:, 255:255 + 1024].rearrange("p (u k) -> p u k", k=256)[:, :, 0]
                    s.op("dve", lambda e, c_at2=c_at2: e.tensor_tensor(c_at2, c_at2, bcorr[:, 4:8], ALU.subtract),
                         reads=[Rbcorr, Rcbuf], writes=[Rcbuf])
                    s.op("act", lambda e: e.activation(sbuf_s[:], cbuf[:], AF.Silu), reads=[Rcbuf], writes=[Rsbuf])
                    def evac_u(b, t0, tn, fc=fc):
                        s.op("dve", lambda e, b=b, t0=t0, tn=tn: e.tensor_tensor(
                            actT[:, fc, t0:t0 + tn], ps[b][:, 0:tn], sbuf_s[:, t0:t0 + tn], ALU.mult),
                            reads=[Rps[b], Rsbuf], writes=[RactT])
                    linear_fm(wu, ru, jj * 128, hT, RhT, KC, evac_u)
            foA = carve(24320, FC * 256, BF16).rearrange("p (c n) -> p c n", c=FC)
            RfoA = ares("foA")
            foB0 = wbuf[0][:, 0:16 * 256].rearrange("p (c n) -> p c n", c=16)
            foB1 = wbuf[1][:, 0:6 * 256].rearrange("p (c n) -> p c n", c=6)
            for nb in range(4):
                src = w_ffn_out[l][:, nb * 256:(nb + 1) * 256].rearrange("(c p) n -> p c n", p=128)
                if nb % 2 == 0:
                    s.dma("pool", foA, src, writes=[RfoA])

                    def rhs_of(kc):
                        return foA[:, kc, :], RfoA
                else:
                    s.dma("pool", foB0, src[:, 0:16, :], writes=[Rw[0]])
                    s.dma("pool", foB1, src[:, 16:22, :], writes=[Rw[1]])

                    def rhs_of(kc):
                        return (foB0[:, kc, :], Rw[0]) if kc < 16 else (foB1[:, kc - 16, :], Rw[1])
                for i in range(NT):
                    b = bank()
                    for kc in range(FC):
                        rv, rr_ = rhs_of(kc)
                        s.op("pe", lambda e, b=b, kc=kc, i=i, rv=rv: e.matmul(
                            ps[b][:, 0:256], actT[:, kc, i * 128:(i + 1) * 128], rv,
                            start=(kc == 0), stop=(kc == FC - 1)),
                            reads=[rr_, RactT], writes=[Rps[b]])
                    s.op("act", lambda e, b=b, i=i, nb=nb: e.copy(fst(i)[:, nb * 256:(nb + 1) * 256], ps[b][:, 0:256]),
                         reads=[Rps[b]], writes=[Rfst_of(i)])
            for i in range(NT):
                residual_update(i, None, 1, fsrc=fst(i), Rf=Rfst_of(i))
                if emit_out:
                    s.dma("sp", y_out[i * 128:(i + 1) * 128, :], x_sb[:, i, :], reads=[Rx[i]])

        qT = carve(5120, 4 * T, BF16).rearrange("p (c t) -> p c t", c=4)
        RqT = ares("qT")
        kT = carve(7680, 4 * 1536, BF16).rearrange("p (c t) -> p c t", c=4)
        RkT = ares("kT")
        Vaug = carve(10752, 12 * 4 * 144, BF16).rearrange("p (k h d) -> p k h d", k=12, h=4)
        RV = ares("Vaug")
        yT = carve(0, 8 * T, BF16).rearrange("p (c t) -> p c t", c=8)
        fbT = carve(14208, 15 * (T + 2), BF16).rearrange("p (c t) -> p c t", c=15)
        RfbT = ares("fbT")
        RyT = RhT
        ropeC_sb = carve(23824, T, F32)
        ropeS_sb = carve(23824 + T, T, F32)
        Rrope = Res("rope")
        rtmp = sb("rtmp", [128, 2, 512])
        Rrtmp = Res("rtmp")
        raw_b = sb("raw_b", [128, 512], BF16)
        Rraw = Res("raw_b")

        def even_inproj():
            s.dma("sp", ropeC_sb[:], ropeC, writes=[Rrope])
            s.dma("sp", ropeS_sb[:], ropeS, writes=[Rrope])
            s.dma("pool", kT[:, :, 0:256], cacheKT.rearrange("(h p) t -> p h t", p=128), writes=[RkT])
            for kk_ in range(2):
                s.dma("pool", Vaug[:, kk_, :, 0:128],
                      cacheV[kk_ * 128:(kk_ + 1) * 128, :].rearrange("p (h d) -> p h d", h=4), writes=[RV])
            if "m" in sub:
                s.op("pool", lambda e: e.memset(Vaug[:, :, :, 128:129], 1.0), writes=[RV])
            for which, dst, doff in ((0, qT, 0), (1, kT, 256)):
                if "q" not in sub:
                    break
                wv, rw = wload(w_in_even[:, which * 512:(which + 1) * 512], KC, 512)
                Rdst = RqT if which == 0 else RkT
                for h in range(4):
                    def evac(b, t0, tn, h=h, dst=dst, doff=doff, Rdst=Rdst):
                        s.op("act", lambda e, b=b, tn=tn: e.copy(raw_b[:, 0:tn], ps[b][:, 0:tn]),
                             reads=[Rps[b]], writes=[Rraw])
                        b2 = bank()
                        s.op("pe", lambda e, b2=b2, tn=tn: e.matmul(ps[b2][:, 0:tn], rotT_b[:], raw_b[:, 0:tn], start=True, stop=True),
                             reads=[Rraw, Rconst], writes=[Rps[b2]])
                        s.op("dve", lambda e, t0=t0, tn=tn: e.tensor_tensor(rtmp[:, 0, 0:tn], raw_b[:, 0:tn], ropeC_sb[:, t0:t0 + tn], ALU.mult),
                             reads=[Rraw, Rrope], writes=[Rrtmp])
                        s.op("dve", lambda e, b2=b2, t0=t0, tn=tn: e.tensor_tensor(rtmp[:, 1, 0:tn], ps[b2][:, 0:tn], ropeS_sb[:, t0:t0 + tn], ALU.mult),
                             reads=[Rps[b2], Rrope], writes=[Rrtmp])
                        s.op("dve", lambda e, t0=t0, tn=tn: e.tensor_tensor(dst[:, h, doff + t0:doff + t0 + tn], rtmp[:, 0, 0:tn], rtmp[:, 1, 0:tn], ALU.add),
                             reads=[Rrtmp], writes=[Rdst])
                    linear_fm(wv, rw, h * 128, hT, RhT, KC, evac)
                if which == 1:
                    for i in range(NT):
                        b = bank()
                        linear_tm(wv, rw, 512, hT, RhT, KC, i, b)
                        si = stctr[0] % 2
                        stctr[0] += 1
                        s.op("act", lambda e, b=b, si=si: e.copy(stage_f[:, si, :], ps[b][:]), reads=[Rps[b]], writes=[Rstage[si]])
                        s.dma("sp", k_out[i * 128:(i + 1) * 128, :], stage_f[:, si, :], reads=[Rstage[si]])
            s.op("pool", lambda e: e.memset(fbT[:, :, 0:1], 0.0), writes=[RfbT])
            s.op("pool", lambda e: e.memset(fbT[:, :, T + 1:T + 2], 0.0), writes=[RfbT])
            for pc, (c0, ncols) in enumerate([(1536, 512), (2048, 512), (2560, 512), (3072, 384)]):
                wv, rw = wload(w_in_even[:, c0:c0 + ncols], KC, ncols)
                for j in range(ncols // 128):
                    ch = pc * 4 + j

                    def evac_fb(b, t0, tn, ch=ch):
                        s.op("act", lambda e, b=b, t0=t0, tn=tn: e.copy(fbT[:, ch, 1 + t0:1 + t0 + tn], ps[b][:, 0:tn]),
                             reads=[Rps[b]], writes=[RfbT])
                    linear_fm(wv, rw, j * 128, hT, RhT, KC, evac_fb)
            wv, rw = wload(w_in_even[:, 1024:1536], KC, 512)
            for i in range(NT if "v" in sub else 0):
                b = bank()
                linear_tm(wv, rw, 512, hT, RhT, KC, i, b)
                si = stctr[0] % 2
                stctr[0] += 1
                s.op("act", lambda e, b=b, si=si: e.copy(stage_f[:, si, :], ps[b][:]), reads=[Rps[b]], writes=[Rstage[si]])
                s.dma("sp", v_out[i * 128:(i + 1) * 128, :], stage_f[:, si, :], reads=[Rstage[si]])
                s.op("dve", lambda e, si=si, i=i: e.tensor_copy(Vaug[:, 2 + i, :, 0:128], stage_f[:, si, :].rearrange("p (h d) -> p h d", h=4)),
                     reads=[Rstage[si]], writes=[RV])


        NPB = 4
        PT = [[sb("PT%d%d" % (m, k), [128, 256], BF16) for k in range(NPB)] for m in range(2)]
        RPT = [[Res("PT%d%d" % (m, k)) for k in range(NPB)] for m in range(2)]
        ya_f = sb("ya_f", [128, 128])
        Rya = Res("ya_f")
        ya_b = sb("ya_b", [128, 128], BF16)
        Ryab = Res("ya_b")
        subln08 = sb("subln08", [128, 128])
        Rsub = Res("subln08")
        lamt = sb("lamt", [128, 64])
        Rlamt = Res("lamt")

        def attention():
            lo = ROFF["lam"]
            for k in range(2):
                s.op("dve", lambda e, k=k: e.tensor_tensor(lamt[:], rowp_sb[:, lo + 128 * k: lo + 128 * k + 64],
                                                          rowp_sb[:, lo + 128 * k + 64: lo + 128 * k + 128], ALU.mult),
                     reads=[Rrowp], writes=[Rlamt])
                s.op("dve", lambda e, k=k: e.reduce_sum(stat[:, 20 + k:21 + k], lamt[:], axis=AX.X),
                     reads=[Rlamt], writes=[Rstat])
            s.op("act", lambda e: e.activation(stat[:, 22:24], stat[:, 20:22], AF.Exp), reads=[Rstat], writes=[Rstat])
            s.op("dve", lambda e: e.tensor_tensor(stat[:, 24:25], stat[:, 22:23], stat[:, 23:24], ALU.subtract),
                 reads=[Rstat], writes=[Rstat])
            s.op("dve", lambda e: e.tensor_scalar(stat[:, 25:26], stat[:, 24:25], 0.2, -1.0, ALU.add, ALU.mult),
                 reads=[Rstat], writes=[Rstat])
            s.op("dve", lambda e: e.tensor_scalar(subln08[:], rowp_sb[:, ROFF["subln"]:ROFF["subln"] + 128], 0.8, None, ALU.mult),
                 reads=[Rrowp], writes=[Rsub])
            pctr = [0, 0]
            for h in range(4):
                for qu in range(5):
                    if P0L1:
                        try:
                            next(P0L1[0])
                        except StopIteration:
                            del P0L1[:]
                    acc = [bank(), bank()]
                    reserved.update(acc)
                    pend = []
                    for kt in range(12):
                        ku = 0 if kt < 2 else 1 + (kt - 2) // 2
                        bcol = ROFF["abias"] + ku * 5 + qu
                        for m in range(2):
                            bs = bank()
                            s.op("pe", lambda e, bs=bs, m=m, h=h, kt=kt, qu=qu: e.matmul(
                                ps[bs][:, 0:256], kT[64 * m:64 * m + 64, h, kt * 128:(kt + 1) * 128],
                                qT[64 * m:64 * m + 64, h, qu * 256:(qu + 1) * 256], start=True, stop=True),
                                reads=[RkT, RqT], writes=[Rps[bs]])
                            pk = pctr[m] % NPB
                            pctr[m] += 1
                            s.op("act", lambda e, bs=bs, m=m, pk=pk, bcol=bcol: e.activation(
                                PT[m][pk][:], ps[bs][:, 0:256], AF.Exp, bias=rowp_sb[:, bcol:bcol + 1], scale=0.125),
                                reads=[Rps[bs], Rrowp], writes=[RPT[m][pk]])

                            def pv(m=m, pk=pk, kt=kt, h=h, acc=acc):
                                for qt in range(2):
                                    s.op("pe", lambda e, qt=qt: e.matmul(
                                        ps[acc[m]][:, qt * 129:qt * 129 + 129], PT[m][pk][:, qt * 128:(qt + 1) * 128],
                                        Vaug[:, kt, h, 0:129], start=(kt == 0 and qt == 0), stop=(kt == 11),
                                        skip_group_check=True),
                                        reads=[RPT[m][pk], RV], writes=[Rps[acc[m]]])
                            pend.append(pv)
                            if len(pend) > 3:
                                pend.pop(0)()
                    while pend:
                        pend.pop(0)()
                    reserved.difference_update(acc)
                    for qt in range(2):
                        i = qu * 2 + qt
                        c0 = qt * 129
                        s.op("dve", lambda e, c0=c0, acc=acc: e.reciprocal(stat[:, 30:31], ps[acc[0]][:, c0 + 128:c0 + 129]),
                             reads=[Rps[acc[0]]], writes=[Rstat])
                        s.op("dve", lambda e, c0=c0, acc=acc: e.reciprocal(stat[:, 31:32], ps[acc[1]][:, c0 + 128:c0 + 129]),
                             reads=[Rps[acc[1]]], writes=[Rstat])
                        s.op("dve", lambda e: e.tensor_tensor(stat[:, 32:33], stat[:, 31:32], stat[:, 25:26], ALU.mult),
                             reads=[Rstat], writes=[Rstat])
                        s.op("dve", lambda e, c0=c0, acc=acc: e.tensor_scalar(ya_f[:], ps[acc[0]][:, c0:c0 + 128], stat[:, 30:31], None, ALU.mult),
                             reads=[Rps[acc[0]], Rstat], writes=[Rya])
                        s.op("dve", lambda e, c0=c0, acc=acc: e.scalar_tensor_tensor(ya_f[:], ps[acc[1]][:, c0:c0 + 128], stat[:, 32:33], ya_f[:], ALU.mult, ALU.add),
                             reads=[Rps[acc[1]], Rstat, Rya], writes=[Rya])
                        s.op("act", lambda e: e.activation(junk_b[:, 0:128], ya_f[:], AF.Square, scale=1.0 / math.sqrt(128.0),
                                                           accum_out=stat[:, 33:34]),
                             reads=[Rya], writes=[Rjunk, Rstat])
                        s.op("dve", lambda e: e.tensor_scalar(stat[:, 34:35], stat[:, 33:34], 1e-6, None, ALU.add),
                             reads=[Rstat], writes=[Rstat])
                        s.op("act", lambda e: e.activation(stat[:, 34:35], stat[:, 34:35], AF.Ln), reads=[Rstat], writes=[Rstat])
                        s.op("act", lambda e: e.activation(stat[:, 35:36], stat[:, 34:35], AF.Exp, scale=-0.5), reads=[Rstat], writes=[Rstat])
                        s.op("dve", lambda e: e.scalar_tensor_tensor(ya_b[:], ya_f[:], stat[:, 35:36], subln08[:], ALU.mult, ALU.mult),
                             reads=[Rya, Rstat, Rsub], writes=[Ryab])
                        reserved.update(acc)
                        bt = bank()
                        reserved.difference_update(acc)
                        s.op("pe", lambda e, bt=bt: e.transpose(psb[bt][:, 0:128], ya_b[:], ident_b[:]),
                             reads=[Ryab, Rconst], writes=[Rps[bt]])
                        s.op("act", lambda e, bt=bt, h=h, i=i: e.copy(yT[:, h, i * 128:(i + 1) * 128], psb[bt][:, 0:128]),
                             reads=[Rps[bt]], writes=[RyT])

        def out_proj(W, actTv, Ract):
            pieces = [wload(W[:, hh * 512:(hh + 1) * 512], KC, 512) for hh in range(2)]
            for i in range(NT):
                bb = [bank(), bank()]
                for hh in range(2):
                    linear_tm(pieces[hh][0], pieces[hh][1], 512, actTv, Ract, KC, i, bb[hh])
                residual_update(i, bb, 0)

        Xc = carve(5120, NT * 8 * 256, BF16).rearrange("p (i g n) -> p i g n", i=NT, g=8)
        RXc = ares("Xc")
        chCS_b = sb("chCS_b", [128, 256], BF16)
        Rch = Res("chCS")

        def fnet():
            s.dma("pool", chCS_b[:], chCS, writes=[Rch])
            for i in range(NT):
                for gp in range(4):
                    b = bank()
                    for g2_ in range(2):
                        g = gp * 2 + g2_
                        s.op("pe", lambda e, b=b, g=g, g2_=g2_, i=i: e.matmul(
                            ps[b][:, g2_ * 256:(g2_ + 1) * 256], hT[:, g, i * 128:(i + 1) * 128], chCS_b[:], start=True, stop=True),
                            reads=[RhT, Rch], writes=[Rps[b]])
                    s.op("act", lambda e, b=b, gp=gp, i=i: e.copy(
                        Xc[:, i, gp * 2:gp * 2 + 2, :], ps[b][:].rearrange("p (g n) -> p g n", g=2)),
                        reads=[Rps[b]], writes=[RXc])
            for (t0, tn) in [(0, 384), (384, 384), (768, 384), (1152, 128)]:
                wc, rc = wload(dftC[:, t0:t0 + tn], NT, tn)
                wsn, rsn = wload(dftSn[:, t0:t0 + tn], NT, tn)
                for g in range(8):
                    b = bank()
                    for tc in range(NT):
                        s.op("pe", lambda e, b=b, g=g, tc=tc, tn=tn, wc=wc: e.matmul(
                            ps[b][:, 0:tn], Xc[:, tc, g, 0:128], wc[:, tc, 0:tn], start=(tc == 0), stop=False),
                            reads=[RXc, rc], writes=[Rps[b]])
                        s.op("pe", lambda e, b=b, g=g, tc=tc, tn=tn, wsn=wsn: e.matmul(
                            ps[b][:, 0:tn], Xc[:, tc, g, 128:256], wsn[:, tc, 0:tn], start=False, stop=(tc == NT - 1)),
                            reads=[RXc, rsn], writes=[Rps[b]])
                    s.op("act", lambda e, b=b, g=g, t0=t0, tn=tn: e.copy(yT[:, g, t0:t0 + tn], ps[b][:, 0:tn]),
                         reads=[Rps[b]], writes=[RyT])


        V_tm = carve(5120, NT * 512, BF16).rearrange("p (i n) -> p i n", i=NT)
        RVtm = ares("V_tm")
        YSb = carve(7680, NT * 512, BF16).rearrange("p (i n) -> p i n", i=NT)
        RYS = ares("YSb")
        Ystage = carve(10240, 128 * 16, F32)[0:64, :].rearrange("p (n k) -> p n k", k=16)
        RYst = ares("Ystage")
        Sst = [carve(12288, 512, F32), carve(12800, 512, F32)]
        RS = [ares("S0"), ares("S1")]
        tmpA = carve(13312, 512, F32)
        RtA = ares("tmpA")
        tmpB = carve(23824, 512, F32)
        RtB = ares("tmpB")
        w2t_b = carve(24336, 512, BF16)
        a2t_b = carve(24592, 512, BF16)
        g2_b = carve(24848, 512, BF16)
        Rlora = ares("lora")
        PT0 = 25104
        NPT = 12

        def pt(k, n=1):
            return carve(PT0 + 128 * k, 128 * n, F32)
        Rpt = [ares("pt%d" % k) for k in range(NPT)]
        lnx0_b = carve(26640, 512, BF16)
        lnx1_b = carve(26896, 512, BF16)
        Rlnx = ares("lnx")
        tb0 = junk_b[:, 0:128]
        tb1 = junk_b[:, 128:256]
        Rtb0 = Res("tb0")
        Rtb1 = Res("tb1")
        colx = sb("colx", [128, 64])
        Rcolx = Res("colx")
        tiny = sb("tiny", [128, 8])
        Rtiny = Res("tiny")
        BS = sb("BS", [128, NT, 8])
        RBS = Res("BS")
        wbf = [w[:].bitcast(F32) for w in wbuf]
        TW = wbf[0][:, 0:1024].rearrange("p (k n) -> p k n", k=8)
        TKK = wbf[0][:, 1024:2048].rearrange("p (k n) -> p k n", k=8)
        TNK = wbf[1][:, 0:1024].rearrange("p (k n) -> p k n", k=8)
        TKD = wbf[1][:, 1024:2048].rearrange("p (k n) -> p k n", k=8)
        TR2 = wbuf[2][:, 0:2048].rearrange("p (k n h) -> p k n h", k=8, h=2)
        tmpA_b = carve(10240, 512, BF16)
        RtAb = ares("tmpA_b")
        Sb16 = carve(10496, 512, BF16)
        RSb = ares("Sb16")
        CX = dict(cmu=0, nmu0=15, nmu1=30, omk1=45, keepL=49, keepR=54)

        def cx(name, c=0):
            o = CX[name] + c
            return colx[:, o:o + 1]

        def rwkv_setup():
            s.dma("pool", w2t_b, w2t_d, writes=[Rlora])
            s.dma("pool", a2t_b, a2t_d, writes=[Rlora])
            s.dma("pool", g2_b, g2_d, writes=[Rlora])
            s.dma("pool", lnx0_b, lnx_d[0, :].partition_broadcast(128), writes=[Rlnx])
            s.dma("pool", lnx1_b, lnx_d[1, :].partition_broadcast(128), writes=[Rlnx])
            mu0 = cp("mu0", 0, 15)
            mu1 = cp("mu1", 0, 15)
            s.op("dve", lambda e: e.tensor_tensor(colx[:, 0:15], mu0, mu1, ALU.add), reads=[Rcolp], writes=[Rcolx])
            s.op("dve", lambda e: e.tensor_scalar(colx[:, 0:15], colx[:, 0:15], -1.0, 1.0, ALU.mult, ALU.add),
                 reads=[Rcolx], writes=[Rcolx])
            s.op("dve", lambda e: e.tensor_scalar(colx[:, 15:30], mu0, -1.0, None, ALU.mult), reads=[Rcolp], writes=[Rcolx])
            s.op("dve", lambda e: e.tensor_scalar(colx[:, 30:45], mu1, -1.0, None, ALU.mult), reads=[Rcolp], writes=[Rcolx])
            s.op("dve", lambda e: e.tensor_scalar(colx[:, 45:49], cp("kvec1", 0, 4), -1.0, 1.0, ALU.mult, ALU.add),
                 reads=[Rcolp], writes=[Rcolx])
            s.op("dve", lambda e: e.tensor_scalar(colx[:, 49:54], rowp_sb[:, ROFF["bndL"]:ROFF["bndL"] + 5], -1.0, 1.0, ALU.mult, ALU.add),
                 reads=[Rrowp], writes=[Rcolx])
            s.op("dve", lambda e: e.tensor_scalar(colx[:, 54:59], rowp_sb[:, ROFF["bndR"]:ROFF["bndR"] + 5], -1.0, 1.0, ALU.mult, ALU.add),
                 reads=[Rrowp], writes=[Rcolx])
            s.op("dve", lambda e: e.memset(BS[:], 0.0), writes=[RBS])

        def shift(ch, ti, out, Rout):
            t0 = ti * 128
            f = fbT[:, ch, 1 + t0:1 + t0 + 128]
            fp = fbT[:, ch, t0:t0 + 128]
            fn = fbT[:, ch, 2 + t0:2 + t0 + 128]
            rr = [RfbT, Rcolp, Rcolx]
            s.op("dve", lambda e: e.tensor_scalar(out, f, cx("cmu", ch), None, ALU.mult), reads=rr, writes=[Rout])
            s.op("dve", lambda e: e.scalar_tensor_tensor(out, fp, cp("mu0", ch), out, ALU.mult, ALU.add),
                 reads=rr + [Rout], writes=[Rout])
            s.op("dve", lambda e: e.scalar_tensor_tensor(out, fn, cp("mu1", ch), out, ALU.mult, ALU.add),
                 reads=rr + [Rout], writes=[Rout])
            u = ti // 2
            if ti % 2 == 0:
                bl = rowp_sb[:, ROFF["bndL"] + u:ROFF["bndL"] + u + 1]
                s.op("dve", lambda e: e.tensor_tensor(tiny[:, 0:1], fbT[:, ch, t0:t0 + 1], bl, ALU.mult),
                     reads=[RfbT, Rrowp], writes=[Rtiny])
                s.op("dve", lambda e: e.scalar_tensor_tensor(out[:, 0:1], tiny[:, 0:1], cx("nmu0", ch), out[:, 0:1], ALU.mult, ALU.add),
                     reads=[Rtiny, Rcolx, Rout], writes=[Rout])
            else:
                br = rowp_sb[:, ROFF["bndR"] + u:ROFF["bndR"] + u + 1]
                s.op("dve", lambda e: e.tensor_tensor(tiny[:, 1:2], fbT[:, ch, 1 + t0 + 128:2 + t0 + 128], br, ALU.mult),
                     reads=[RfbT, Rrowp], writes=[Rtiny])
                s.op("dve", lambda e: e.scalar_tensor_tensor(out[:, 127:128], tiny[:, 1:2], cx("nmu1", ch), out[:, 127:128], ALU.mult, ALU.add),
                     reads=[Rtiny, Rcolx, Rout], writes=[Rout])

        def rwkv_prepass():
            for ti in range(NT):
                for c in range(4):
                    shift(8 + c, ti, pt(c), Rpt[c])
                    s.op("act", lambda e, c=c: e.copy(xn_b[:, c * 128:(c + 1) * 128], pt(c)), reads=[Rpt[c]], writes=[Rxn])
                b = bank()
                for c in range(4):
                    s.op("pe", lambda e, b=b, c=c: e.transpose(psb[b][:, c * 128:(c + 1) * 128], xn_b[:, c * 128:(c + 1) * 128], ident_b[:]),
                         reads=[Rxn, Rconst], writes=[Rps[b]])
                s.op("act", lambda e, b=b, ti=ti: e.copy(V_tm[:, ti, :], psb[b][:, 0:512]), reads=[Rps[b]], writes=[RVtm])

        def prep(d, ti):
            rev = (d == 1)

            def tab(T3, c):
                v = T3[:, d * 4 + c, :]
                return v[:, ::-1] if rev else v
            tabs_w = [Rw[0], Rw[1], Rw[2]]
            shift(12, ti, pt(0), Rpt[0])
            shift(13, ti, pt(1), Rpt[1])
            s.op("act", lambda e: e.activation(tb0, pt(0), AF.Tanh), reads=[Rpt[0]], writes=[Rtb0])
            s.op("act", lambda e: e.copy(tb1, pt(1)), reads=[Rpt[1]], writes=[Rtb1])
            lo, hi = 64 * d, 64 * d + 64
            for c in range(4):
                b = bank()
                s.op("pe", lambda e, b=b, c=c: e.matmul(ps[b][:, 0:128], w2t_b[lo:hi, c * 128:(c + 1) * 128], tb0[lo:hi, :], start=True, stop=True),
                     reads=[Rlora, Rtb0], writes=[Rps[b]])
                s.op("act", lambda e, b=b, c=c: e.activation(pt(2), ps[b][:, 0:128], AF.Sigmoid, bias=cp("w0_%d" % d, c)),
                     reads=[Rps[b], Rcolp], writes=[Rpt[2]])
                s.op("act", lambda e, c=c: e.activation(tab(TW, c), pt(2), AF.Exp, scale=-EXPM05),
                     reads=[Rpt[2]], writes=[Rw[0]])
                b = bank()
                s.op("pe", lambda e, b=b, c=c: e.matmul(ps[b][:, 0:128], a2t_b[lo:hi, c * 128:(c + 1) * 128], tb1[lo:hi, :], start=True, stop=True),
                     reads=[Rlora, Rtb1], writes=[Rps[b]])
                s.op("act", lambda e, b=b, c=c: e.activation(pt(3), ps[b][:, 0:128], AF.Sigmoid, bias=cp("a0_%d" % d, c)),
                     reads=[Rps[b], Rcolp], writes=[Rpt[3]])
                shift(4 + c, ti, pt(4), Rpt[4])
                s.op("dve", lambda e, c=c: e.tensor_scalar(pt(5), pt(4), cp("kvec0", c), None, ALU.mult),
                     reads=[Rpt[4], Rcolp], writes=[Rpt[5]])
                s.op("act", lambda e: e.activation(pt(6), pt(5), AF.Square), reads=[Rpt[5]], writes=[Rpt[6]])
                b = bank()
                s.op("pe", lambda e, b=b: e.matmul(ps[b][:, 0:128], bones_f[:], pt(6), start=True, stop=True),
                     reads=[Rconst, Rpt[6]], writes=[Rps[b]])
                s.op("dve", lambda e, b=b: e.tensor_scalar(pt(7), ps[b][:, 0:128], 1e-12, None, ALU.add),
                     reads=[Rps[b]], writes=[Rpt[7]])
                s.op("act", lambda e: e.activation(pt(7), pt(7), AF.Sqrt), reads=[Rpt[7]], writes=[Rpt[7]])
                s.op("dve", lambda e: e.reciprocal(pt(7), pt(7)), reads=[Rpt[7]], writes=[Rpt[7]])
                s.op("dve", lambda e: e.tensor_tensor(pt(8), pt(5), pt(7), ALU.mult), reads=[Rpt[5], Rpt[7]], writes=[Rpt[8]])
                s.op("act", lambda e, c=c: e.copy(tab(TKK, c), pt(8)), reads=[Rpt[8]], writes=[Rw[0]])
                s.op("dve", lambda e, c=c: e.scalar_tensor_tensor(tab(TNK, c), pt(8), -1.0, pt(3), ALU.mult, ALU.mult),
                     reads=[Rpt[8], Rpt[3]], writes=[Rw[1]])
                s.op("dve", lambda e, c=c: e.tensor_scalar(pt(9), pt(3), cp("kvec1", c), cx("omk1", c), ALU.mult, ALU.add),
                     reads=[Rpt[3], Rcolp, Rcolx], writes=[Rpt[9]])
                s.op("dve", lambda e: e.tensor_tensor(pt(9), pt(4), pt(9), ALU.mult), reads=[Rpt[4], Rpt[9]], writes=[Rpt[9]])
                s.op("act", lambda e, c=c: e.copy(tab(TKD, c), pt(9)), reads=[Rpt[9]], writes=[Rw[1]])
                shift(c, ti, pt(10), Rpt[10])
                for h2 in range(2):
                    o = TR2[:, d * 4 + c, :, h2]
                    if rev:
                        o = o[:, ::-1]
                    s.op("dve", lambda e, o=o, h2=h2: e.tensor_scalar(o, pt(10), halfsel_f[:, h2:h2 + 1], None, ALU.mult),
                         reads=[Rpt[10], Rconst], writes=[Rw[2]])
                s.op("dve", lambda e: e.tensor_tensor(pt(6), pt(10), pt(9), ALU.mult), reads=[Rpt[10], Rpt[9]], writes=[Rpt[6]])
                s.op("dve", lambda e, c=c: e.tensor_scalar(pt(6), pt(6), cp("kvec2", c), None, ALU.mult),
                     reads=[Rpt[6], Rcolp], writes=[Rpt[6]])
                b = bank()
                s.op("pe", lambda e, b=b: e.matmul(ps[b][:, 0:2], pt(6), halfsel_f[:, 0:2], start=True, stop=True),
                     reads=[Rpt[6], Rconst], writes=[Rps[b]])
                s.op("dve", lambda e, b=b, c=c, ti=ti: e.tensor_tensor(BS[:, ti, c * 2:c * 2 + 2], BS[:, ti, c * 2:c * 2 + 2], ps[b][:, 0:2], ALU.add),
                     reads=[Rps[b], RBS], writes=[RBS])

        yb = xn_b[:, 0:512]

        def finalize(ti, by):
            ys = pt(0, 4)
            Rys = [Rpt[0], Rpt[1], Rpt[2], Rpt[3]]
            t2 = pt(4, 4)
            Rt2 = [Rpt[4], Rpt[5], Rpt[6], Rpt[7]]
            ys3 = ys.rearrange("p (h i) -> p h i", h=8)
            t23 = t2.rearrange("p (h i) -> p h i", h=8)
            s.op("dve", lambda e: e.tensor_tensor(ys, YSb[:, ti, :], ps[by][:], ALU.add), reads=[RYS, Rps[by]], writes=Rys)
            s.op("dve", lambda e: e.reduce_sum(tiny[:, 0:8], ys3, axis=AX.X), reads=Rys, writes=[Rtiny])
            s.op("dve", lambda e: e.tensor_scalar(tiny[:, 0:8], tiny[:, 0:8], -1.0 / 64.0, None, ALU.mult), reads=[Rtiny], writes=[Rtiny])
            s.op("dve", lambda e: e.tensor_tensor(ys3, ys3, tiny[:, 0:8].unsqueeze(2).to_broadcast([128, 8, 64]), ALU.add),
                 reads=Rys + [Rtiny], writes=Rys)
            s.op("dve", lambda e: e.tensor_tensor(t2, ys, ys, ALU.mult), reads=Rys, writes=Rt2)
            s.op("dve", lambda e: e.reduce_sum(stat[:, 40:48], t23, axis=AX.X), reads=Rt2, writes=[Rstat])
            s.op("dve", lambda e: e.tensor_scalar(stat[:, 40:48], stat[:, 40:48], 1.0 / 64.0, 64e-5, ALU.mult, ALU.add),
                 reads=[Rstat], writes=[Rstat])
            s.op("act", lambda e: e.activation(stat[:, 40:48], stat[:, 40:48], AF.Ln), reads=[Rstat], writes=[Rstat])
            s.op("act", lambda e: e.activation(stat[:, 48:56], stat[:, 40:48], AF.Exp, scale=-0.5), reads=[Rstat], writes=[Rstat])
            s.op("dve", lambda e: e.tensor_tensor(ys3, ys3, stat[:, 48:56].unsqueeze(2).to_broadcast([128, 8, 64]), ALU.mult),
                 reads=Rys + [Rstat], writes=Rys)
            s.op("dve", lambda e: e.tensor_tensor(ys, ys, lnx0_b, ALU.mult), reads=Rys + [Rlnx], writes=Rys)
            s.op("dve", lambda e: e.tensor_tensor(ys, ys, lnx1_b, ALU.add), reads=Rys + [Rlnx], writes=Rys)
            s.op("dve", lambda e: e.tensor_tensor(t23, V_tm[:, ti, :].rearrange("p (h i) -> p h i", h=8),
                                                  BS[:, ti, :].unsqueeze(2).to_broadcast([128, 8, 64]), ALU.mult),
                 reads=[RVtm, RBS], writes=Rt2)
            s.op("dve", lambda e: e.tensor_tensor(ys, ys, t2, ALU.add), reads=Rys + Rt2, writes=Rys)
            shift(14, ti, pt(8), Rpt[8])
            s.op("act", lambda e: e.activation(tb0, pt(8), AF.Sigmoid), reads=[Rpt[8]], writes=[Rtb0])
            bg = bank()
            s.op("pe", lambda e, bg=bg: e.matmul(ps[bg][:], tb0, g2_b, start=True, stop=True),
                 reads=[Rtb0, Rlora], writes=[Rps[bg]])
            s.op("dve", lambda e, bg=bg: e.tensor_tensor(yb, ys, ps[bg][:], ALU.mult), reads=Rys + [Rps[bg]], writes=[Rxn])
            bt = bank()
            for c in range(4):
                s.op("pe", lambda e, bt=bt, c=c: e.transpose(psb[bt][:, c * 128:(c + 1) * 128], yb[:, c * 128:(c + 1) * 128], ident_b[:]),
                     reads=[Rxn, Rconst], writes=[Rps[bt]])
            s.op("act", lambda e, bt=bt, ti=ti: e.copy(yT[:, 4:8, ti * 128:(ti + 1) * 128],
                                                        psb[bt][:, 0:512].rearrange("p (c t) -> p c t", c=4)),
                 reads=[Rps[bt]], writes=[RyT])

        def rwkv_scan(nrounds=NT):
            S8 = [x_.rearrange("p (k i) -> p k i", k=8) for x_ in Sst]
            tA8 = tmpA.rearrange("p (k i) -> p k i", k=8)
            tB8 = tmpB.rearrange("p (k i) -> p k i", k=8)

            def bc(T3, n):
                return T3[:, :, n].unsqueeze(2).to_broadcast([128, 8, 64])
            for r in range(nrounds):
                tf, tbk = r, NT - 1 - r
                prep(0, tf)
                prep(1, tbk)
                if tf % 2 == 0:
                    u = tf // 2
                    if u == 0:
                        s.dma("sp", Sst[0][:, 0:256], initS[0], writes=[RS[0]])
                    else:
                        s.op("dve", lambda e, u=u: e.tensor_scalar(Sst[0][:, 0:256], Sst[0][:, 0:256], cx("keepL", u), None, ALU.mult),
                             reads=[RS[0], Rcolx], writes=[RS[0]])
                if tbk % 2 == 1:
                    u = tbk // 2
                    if u == 4:
                        s.op("dve", lambda e: e.memset(Sst[0][:, 256:512], 0.0), writes=[RS[0]])
                    elif u == 3:
                        s.dma("sp", tmpB[:, 0:256], initS[1], writes=[RtB])
                        s.op("dve", lambda e, u=u: e.scalar_tensor_tensor(Sst[0][:, 256:512], Sst[0][:, 256:512], cx("keepR", u),
                                                                          tmpB[:, 0:256], ALU.mult, ALU.add),
                             reads=[RS[0], Rcolx, RtB], writes=[RS[0]])
                    else:
                        s.op("dve", lambda e, u=u: e.tensor_scalar(Sst[0][:, 256:512], Sst[0][:, 256:512], cx("keepR", u), None, ALU.mult),
                             reads=[RS[0], Rcolx], writes=[RS[0]])
                by_ = None
                pending = []

                def flush():
                    for f_ in pending:
                        f_()
                    del pending[:]
                for n in range(128):
                    ci, ni = n % 2, (n + 1) % 2
                    Sc, Sn = S8[ci], S8[ni]
                    if n % 32 == 0:
                        by_ = bank()
                        reserved.add(by_)
                    tAb8 = tmpA_b.rearrange("p (k i) -> p k i", k=8)
                    s.op("dve", lambda e, n=n, Sc=Sc, tAb8=tAb8: e.tensor_tensor(tAb8, Sc, bc(TKK, n), ALU.mult),
                         reads=[RS[ci], Rw[0]], writes=[RtAb])
                    s.op("dve", lambda e, n=n, Sc=Sc, Sn=Sn: e.tensor_tensor(Sn, Sc, bc(TW, n), ALU.mult),
                         reads=[RS[ci], Rw[0]], writes=[RS[ni]])
                    bv = bank()
                    for d in range(2):
                        tile_d = tf if d == 0 else tbk
                        row = n if d == 0 else 127 - n
                        for h2 in range(2):
                            s.op("pe", lambda e, bv=bv, d=d, h2=h2, tile_d=tile_d, row=row: e.matmul(
                                ps[bv][64 * h2:64 * h2 + 64, d * 256:(d + 1) * 256].rearrange("p (c i) -> p c i", c=4),
                                ident_b[:, row:row + 1].to_broadcast([128, 64]),
                                V_tm[:, tile_d, :].rearrange("p (c h i) -> p c h i", c=4, h=2)[:, :, h2, :],
                                start=True, stop=True),
                                reads=[RVtm, Rconst], writes=[Rps[bv]])
                    bs_ = bank()
                    s.op("pe", lambda e, bs_=bs_: e.matmul(ps[bs_][:], bones_b[:], tmpA_b, start=True, stop=True),
                         reads=[RtAb, Rconst], writes=[Rps[bs_]])
                    flush()
                    s.op("dve", lambda e, n=n, bv=bv: e.tensor_tensor(tB8, ps[bv][:].rearrange("p (k i) -> p k i", k=8), bc(TKD, n), ALU.mult),
                         reads=[Rps[bv], Rw[1]], writes=[RtB])
                    s.op("dve", lambda e, ni=ni: e.tensor_tensor(Sst[ni], Sst[ni], tmpB, ALU.add),
                         reads=[RS[ni], RtB], writes=[RS[ni]])
                    s.op("dve", lambda e, n=n, bs_=bs_: e.tensor_tensor(tA8, ps[bs_][:].rearrange("p (k i) -> p k i", k=8), bc(TNK, n), ALU.mult),
                         reads=[Rps[bs_], Rw[1]], writes=[RtA])
                    s.op("dve", lambda e, ni=ni: e.tensor_tensor(Sst[ni], Sst[ni], tmpA, ALU.add),
                         reads=[RS[ni], RtA], writes=[RS[ni]])
                    s.op("act", lambda e, ni=ni: e.copy(Sb16, Sst[ni]), reads=[RS[ni]], writes=[RSb])
                    nl = n % 32

                    def ymm(by_=by_, nl=nl, n=n):
                        for dc in range(8):
                            s.op("pe", lambda e, dc=dc: e.matmul(
                                ps[by_][0:64, nl * 16 + dc * 2:nl * 16 + dc * 2 + 2], Sb16[:, dc * 64:(dc + 1) * 64], TR2[:, dc, n, :],
                                start=True, stop=True),
                                reads=[RSb, Rw[2]], writes=[Rps[by_]])
                    pending.append(ymm)
                    if nl == 31:
                        flush()
                        n0 = n - 31
                        src = ps[by_][0:64, :].rearrange("p (n k) -> p n k", k=16)
                        s.op("act", lambda e, src=src, n0=n0: e.copy(Ystage[:, n0:n0 + 32, 0:8], src[:, :, 0:8]),
                             reads=[Rps[by_]], writes=[RYst])
                        s.op("act", lambda e, src=src, n0=n0: e.copy(Ystage[:, 96 - n0:128 - n0, 8:16][:, ::-1, :], src[:, :, 8:16]),
                             reads=[Rps[by_]], writes=[RYst])
                        reserved.discard(by_)
                if tf % 2 == 1:
                    s.dma("sp", s_out[tf // 2, 0], Sst[0][:, 0:256], reads=[RS[0]])
                if tbk % 2 == 0:
                    s.dma("sp", s_out[tbk // 2, 1], Sst[0][:, 256:512], reads=[RS[0]])
                for d in range(2):
                    tile_d = tf if d == 0 else tbk
                    byy = bank()
                    for k8 in range(8):
                        s.op("pe", lambda e, byy=byy, k8=k8, d=d: e.matmul(
                            ps[byy][:, k8 * 64:(k8 + 1) * 64], Ystage[:, :, d * 8 + k8], ident_f[0:64, 0:64], start=True, stop=True),
                            reads=[RYst, Rconst], writes=[Rps[byy]])
                    first = (r < 5)
                    if first:
                        s.op("act", lambda e, byy=byy, tile_d=tile_d: e.copy(YSb[:, tile_d, :], ps[byy][:]),
                             reads=[Rps[byy]], writes=[RYS])
                    else:
                        finalize(tile_d, byy)


        def wbw(i, off_w, nelem):
            return wbuf[i][:, 2 * off_w:2 * off_w + nelem]
        Ast = carve(13200, 512, F32)
        RAst = ares("Ast")
        Abf = carve(13712, 512, BF16)
        RAbf = ares("Abf")
        WS = []
        fB = 14208 + 8 * 641
        for wsi in range(2):
            d_ = {}
            if wsi == 0:
                for k_, nm in enumerate(["Lt0", "Lt1", "L0", "L1", "Z0"]):
                    d_[nm] = carve(10240 + 512 * k_, 512, F32)
                d_["MT"] = carve(12800, 256, BF16)
                d_["RT"] = carve(12928, 512, BF16)
                d_["Z1"] = wbuf[0][:, 0:1024].bitcast(F32)
                for k_, nm in enumerate(["Lakt", "Mrbt", "Mrkt", "BKtm"]):
                    d_[nm] = wbw(0, 512 + 256 * k_, 512)
                d_["Zfb"] = carve(10240, 512, BF16)
            else:
                for k_, nm in enumerate(["Lt0", "Lt1", "L0", "L1", "Z0"]):
                    d_[nm] = carve(fB + 512 * k_, 512, F32)
                d_["Z1"] = wbuf[1][:, 0:1024].bitcast(F32)
                d_["Lakt"] = wbw(0, 1536, 512)
                d_["Mrbt"] = wbw(0, 1792, 512)
                d_["Mrkt"] = wbw(1, 512, 512)
                d_["MT"] = wbw(1, 768, 256)
                d_["BKtm"] = carve(23824, 512, BF16)
                d_["RT"] = carve(24080, 512, BF16)
                d_["Zfb"] = carve(fB, 512, BF16)
            for nm in ["Lt0", "Lt1", "L0", "L1", "Z0", "Z1", "Lakt", "Mrbt", "Mrkt", "BKtm", "MT", "RT"]:
                d_["R" + nm] = ares("ws%d_%s" % (wsi, nm))
            d_["RZfb"] = d_["RLt0"]
            WS.append(d_)
        masks = wbw(1, 896, 4 * 4 * 128).rearrange("p (m q t) -> p m q t", m=4, q=4)
        Rmask = ares("masks")
        TBL = []
        for par in range(2):
            d_ = {}
            for k_, nm in enumerate(["RH", "AH", "BH", "KH"]):
                d_[nm] = wbw(2, par * 1024 + 256 * k_, 512).rearrange("p (c t) -> p c t", c=4)
                d_["R" + nm] = ares("tb%d_%s" % (par, nm))
            TBL.append(d_)
        pcs = sb("pcs", [128, 2, 4])
        Rpcs = [Res("pcs0"), Res("pcs1")]

        def prep_chunk(d, ti, par):
            TB = TBL[par]
            shift(12, ti, pt(0), Rpt[0])
            shift(13, ti, pt(1), Rpt[1])
            s.op("act", lambda e: e.activation(tb0, pt(0), AF.Tanh), reads=[Rpt[0]], writes=[Rtb0])
            s.op("act", lambda e: e.copy(tb1, pt(1)), reads=[Rpt[1]], writes=[Rtb1])
            lo, hi = 64 * d, 64 * d + 64
            for c in range(4):
                b = bank()
                s.op("pe", lambda e, b=b, c=c: e.matmul(ps[b][:, 0:128], w2t_b[lo:hi, c * 128:(c + 1) * 128], tb0[lo:hi, :], start=True, stop=True),
                     reads=[Rlora, Rtb0], writes=[Rps[b]])
                s.op("act", lambda e, b=b, c=c: e.activation(pt(2), ps[b][:, 0:128], AF.Sigmoid, bias=cp("w0_%d" % d, c)),
                     reads=[Rps[b], Rcolp], writes=[Rpt[2]])
                b = bank()
                s.op("pe", lambda e, b=b, c=c: e.matmul(ps[b][:, 0:128], a2t_b[lo:hi, c * 128:(c + 1) * 128], tb1[lo:hi, :], start=True, stop=True),
                     reads=[Rlora, Rtb1], writes=[Rps[b]])
                s.op("act", lambda e, b=b, c=c: e.activation(pt(3), ps[b][:, 0:128], AF.Sigmoid, bias=cp("a0_%d" % d, c)),
                     reads=[Rps[b], Rcolp], writes=[Rpt[3]])
                s.op("dve", lambda e: e.tensor_scalar(pt(2), pt(2), -EXPM05, None, ALU.mult), reads=[Rpt[2]], writes=[Rpt[2]])
                if d == 0:
                    s.op("dve", lambda e: e.tensor_tensor_scan(pt(0), pt(11), pt(2), 0.0, ALU.mult, ALU.add),
                         reads=[Rpt[2], Rpt[11]], writes=[Rpt[0]])
                else:
                    s.op("dve", lambda e: e.tensor_tensor_scan(pt(0)[:, ::-1], pt(11), pt(2)[:, ::-1], 0.0, ALU.mult, ALU.add),
                         reads=[Rpt[2], Rpt[11]], writes=[Rpt[0]])
                s.op("act", lambda e: e.activation(pt(1), pt(0), AF.Exp), reads=[Rpt[0]], writes=[Rpt[1]])
                s.op("dve", lambda e: e.tensor_tensor(pt(2), pt(0), pt(2), ALU.subtract), reads=[Rpt[0], Rpt[2]], writes=[Rpt[2]])
                s.op("act", lambda e: e.activation(pt(2), pt(2), AF.Exp), reads=[Rpt[2]], writes=[Rpt[2]])
                s.op("act", lambda e: e.activation(pt(0), pt(0), AF.Exp, scale=-1.0), reads=[Rpt[0]], writes=[Rpt[0]])
                yield
                pcol = 127 if d == 0 else 0
                s.op("dve", lambda e, c=c, pcol=pcol: e.tensor_copy(pcs[:, par, c:c + 1], pt(1)[:, pcol:pcol + 1]),
                     reads=[Rpt[1]], writes=[Rpcs[par]])
                yield
                shift(4 + c, ti, pt(4), Rpt[4])
                s.op("dve", lambda e, c=c: e.tensor_scalar(pt(5), pt(4), cp("kvec0", c), None, ALU.mult),
                     reads=[Rpt[4], Rcolp], writes=[Rpt[5]])
                s.op("act", lambda e: e.activation(pt(6), pt(5), AF.Square), reads=[Rpt[5]], writes=[Rpt[6]])
                b = bank()
                s.op("pe", lambda e, b=b: e.matmul(ps[b][:, 0:128], bones_f[:], pt(6), start=True, stop=True),
                     reads=[Rconst, Rpt[6]], writes=[Rps[b]])
                s.op("dve", lambda e, b=b: e.tensor_scalar(pt(7), ps[b][:, 0:128], 1e-12, None, ALU.add),
                     reads=[Rps[b]], writes=[Rpt[7]])
                s.op("act", lambda e: e.activation(pt(7), pt(7), AF.Ln), reads=[Rpt[7]], writes=[Rpt[7]])
                s.op("act", lambda e: e.activation(pt(7), pt(7), AF.Exp, scale=-0.5), reads=[Rpt[7]], writes=[Rpt[7]])
                s.op("dve", lambda e: e.tensor_tensor(pt(8), pt(5), pt(7), ALU.mult), reads=[Rpt[5], Rpt[7]], writes=[Rpt[8]])
                yield
                s.op("dve", lambda e, c=c: e.scalar_tensor_tensor(TB["AH"][:, c, :], pt(8), -1.0, pt(2), ALU.mult, ALU.mult),
                     reads=[Rpt[8], Rpt[2]], writes=[TB["RAH"]])
                s.op("dve", lambda e: e.tensor_tensor(pt(6), pt(8), pt(3), ALU.mult), reads=[Rpt[8], Rpt[3]], writes=[Rpt[6]])
                s.op("dve", lambda e, c=c: e.tensor_tensor(TB["BH"][:, c, :], pt(6), pt(0), ALU.mult),
                     reads=[Rpt[6], Rpt[0]], writes=[TB["RBH"]])
                yield
                s.op("dve", lambda e, c=c: e.tensor_scalar(pt(9), pt(3), cp("kvec1", c), cx("omk1", c), ALU.mult, ALU.add),
                     reads=[Rpt[3], Rcolp, Rcolx], writes=[Rpt[9]])
                s.op("dve", lambda e: e.tensor_tensor(pt(9), pt(4), pt(9), ALU.mult), reads=[Rpt[4], Rpt[9]], writes=[Rpt[9]])
                s.op("dve", lambda e, c=c: e.tensor_tensor(TB["KH"][:, c, :], pt(9), pt(0), ALU.mult),
                     reads=[Rpt[9], Rpt[0]], writes=[TB["RKH"]])
                yield
                shift(c, ti, pt(10), Rpt[10])
                s.op("dve", lambda e, c=c: e.tensor_tensor(TB["RH"][:, c, :], pt(10), pt(1), ALU.mult),
                     reads=[Rpt[10], Rpt[1]], writes=[TB["RRH"]])
                yield
                s.op("dve", lambda e: e.tensor_tensor(pt(6), pt(10), pt(9), ALU.mult), reads=[Rpt[10], Rpt[9]], writes=[Rpt[6]])
                s.op("dve", lambda e, c=c: e.tensor_scalar(pt(6), pt(6), cp("kvec2", c), None, ALU.mult),
                     reads=[Rpt[6], Rcolp], writes=[Rpt[6]])
                b = bank()
                s.op("pe", lambda e, b=b: e.matmul(ps[b][:, 0:2], pt(6), halfsel_f[:, 0:2], start=True, stop=True),
                     reads=[Rpt[6], Rconst], writes=[Rps[b]])
                s.op("dve", lambda e, b=b, c=c, ti=ti: e.tensor_tensor(BS[:, ti, c * 2:c * 2 + 2], BS[:, ti, c * 2:c * 2 + 2], ps[b][:, 0:2], ALU.add),
                     reads=[Rps[b], RBS], writes=[RBS])
                yield

        def rwkv_chunked(ntiles=NT):
            cut = int(os.environ.get("CUT", 99))
            for m in range(4):
                for q in range(4):
                    s.dma("pool", masks[:, m, q, :], cmat[4 + m], writes=[Rmask])
            s.op("dve", lambda e: e.memset(pt(11), 1.0), writes=[Rpt[11]])
            MSK = {0: (0, 1, 2), 1: (1, 0, 3)}
            seq = []
            for d in range(2):
                order = list(range(NT)) if d == 0 else list(range(NT - 1, -1, -1))
                seq += [(d, ti) for ti in order[:ntiles]]
            for _ in prep_chunk(seq[0][0], seq[0][1], 0):
                pass
            if True:
                def tile_body(n):
                    d, ti = seq[n]
                    par = n % 2
                    Ad = Ast[:, d * 256:(d + 1) * 256]
                    Ad3 = Ad.rearrange("p (c i) -> p c i", c=4)
                    Abd = Abf[:, d * 256:(d + 1) * 256]
                    Abd3 = Abd.rearrange("p (c i) -> p c i", c=4)
                    mTs, mS, mTi = MSK[d]
                    TB = TBL[par]
                    nxt = prep_chunk(seq[n + 1][0], seq[n + 1][1], (n + 1) % 2) if n + 1 < len(seq) else iter(())
                    u = ti // 2
                    isstart = (ti % 2 == 0) if d == 0 else (ti % 2 == 1)
                    if isstart:
                        if d == 0:
                            if u == 0:
                                s.dma("sp", Ad, initS[0], writes=[RAst])
                            else:
                                s.op("dve", lambda e, u=u: e.tensor_scalar(Ad, Ad, cx("keepL", u), None, ALU.mult),
                                     reads=[RAst, Rcolx], writes=[RAst])
                        else:
                            if u == 4:
                                s.op("dve", lambda e: e.memset(Ad, 0.0), writes=[RAst])
                            elif u == 3:
                                s.dma("sp", stage_f[:, 0, 0:256], initS[1], writes=[Rstage[0]])
                                s.op("dve", lambda e, u=u: e.scalar_tensor_tensor(Ad, Ad, cx("keepR", u), stage_f[:, 0, 0:256], ALU.mult, ALU.add),
                                     reads=[RAst, Rcolx, Rstage[0]], writes=[RAst])
                            else:
                                s.op("dve", lambda e, u=u: e.tensor_scalar(Ad, Ad, cx("keepR", u), None, ALU.mult),
                                     reads=[RAst, Rcolx], writes=[RAst])
                        s.op("act", lambda e: e.copy(Abd, Ad), reads=[RAst], writes=[RAbf])
                    if cut < 2:
                        return
                    fy = bank()
                    reserved.add(fy)
                    first_fy = [True]
                    def group_body(g):
                        W = WS[g]
                        heads = [(c_, g) for c_ in range(4)]

                        def fm(T3, c, h2):
                            return T3[64 * h2:64 * h2 + 64, c, :]

                        def q4(tl):
                            return tl.rearrange("p (q t) -> p q t", q=4)
                        specs = [("Lt0", "BH", "AH", mTs), ("L0", "AH", "BH", mS), ("Lakt", "KH", "AH", mTs),
                                 ("Mrbt", "BH", "RH", mTi), ("Mrkt", "KH", "RH", mTi)]
                        lim = int(os.environ.get("LIM", 99))
                        for (dst, la, rb, mk) in specs[:lim]:
                            b = bank()
                            for q, (c, h2) in enumerate(heads[:int(os.environ.get("LIMH", 4))]):
                                s.op("pe", lambda e, b=b, q=q, c=c, h2=h2, la=la, rb=rb: e.matmul(
                                    ps[b][:, q * 128:(q + 1) * 128], fm(TB[la], c, h2), fm(TB[rb], c, h2), start=True, stop=True),
                                    reads=[TB["R" + la], TB["R" + rb]], writes=[Rps[b]])
                            if os.environ.get("NOEV") == "1":
                                continue
                            if os.environ.get("NOEV") == "2":
                                s.op("dve", lambda e, b=b, dst=dst, mk=mk: e.tensor_copy(W[dst], ps[b][:]),
                                     reads=[Rps[b], Rmask], writes=[W["R" + dst]])
                                continue
                            s.op("dve", lambda e, b=b, dst=dst, mk=mk: e.tensor_tensor(
                                q4(W[dst]), ps[b][:].rearrange("p (q t) -> p q t", q=4), masks[:, mk, :, :], ALU.mult),
                                reads=[Rps[b], Rmask], writes=[W["R" + dst]])
                        yield
                        if cut < 3:
                            return
                        b = bank()
                        for q, (c, h2) in enumerate(heads):
                            idn = ident_b[64 * h2:64 * h2 + 64, 64 * h2:64 * h2 + 64]
                            for k_, nm in enumerate(["AH", "BH", "KH"]):
                                s.op("pe", lambda e, b=b, q=q, c=c, h2=h2, nm=nm, k_=k_, idn=idn: e.transpose(
                                    psb[b][:, k_ * 256 + q * 64:k_ * 256 + q * 64 + 64], fm(TB[nm], c, h2), idn),
                                    reads=[TB["R" + nm], Rconst], writes=[Rps[b]])
                        Z0q = q4(W["Z0"])
                        BKq = q4(W["BKtm"])
                        s.op("act", lambda e, b=b, Z0q=Z0q: e.copy(Z0q[:, :, 0:64], psb[b][:, 0:256].rearrange("p (q j) -> p q j", q=4)),
                             reads=[Rps[b]], writes=[W["RZ0"]])
                        s.op("act", lambda e, b=b, BKq=BKq: e.copy(BKq[:, :, 0:64], psb[b][:, 256:512].rearrange("p (q j) -> p q j", q=4)),
                             reads=[Rps[b]], writes=[W["RBKtm"]])
                        s.op("act", lambda e, b=b, BKq=BKq: e.copy(BKq[:, :, 64:128], psb[b][:, 512:768].rearrange("p (q j) -> p q j", q=4)),
                             reads=[Rps[b]], writes=[W["RBKtm"]])

                        def Vq(c, h2):
                            return V_tm[:, ti, (c * 2 + h2) * 64:(c * 2 + h2) * 64 + 64]
                        if cut < 4:
                            return
                        b = bank()
                        Lakq = q4(W["Lakt"])
                        for q, (c, h2) in enumerate(heads):
                            s.op("pe", lambda e, b=b, q=q, c=c, h2=h2: e.matmul(
                                ps[b][:, q * 64:(q + 1) * 64], Lakq[:, q, :], Vq(c, h2), start=True, stop=True),
                                reads=[W["RLakt"], RVtm], writes=[Rps[b]])
                        s.op("act", lambda e, b=b, Z0q=Z0q: e.copy(Z0q[:, :, 64:128], ps[b][:, 0:256].rearrange("p (q j) -> p q j", q=4)),
                             reads=[Rps[b]], writes=[W["RZ0"]])
                        yield
                        if cut < 5:
                            return
                        zc, lc = 0, 0
                        for k in range(7):
                            Zc, Zn = W["Z%d" % zc], W["Z%d" % (1 - zc)]
                            RZc, RZn = W["RZ%d" % zc], W["RZ%d" % (1 - zc)]
                            Ltc, Lc = W["Lt%d" % lc], W["L%d" % lc]
                            RLtc, RLc = W["RLt%d" % lc], W["RL%d" % lc]
                            b = bank()
                            for q in range(4):
                                s.op("pe", lambda e, b=b, q=q, Zc=Zc, Ltc=Ltc: e.matmul(
                                    ps[b][:, q * 128:(q + 1) * 128], q4(Ltc)[:, q, :], q4(Zc)[:, q, :], start=True, stop=True),
                                    reads=[RZc, RLtc], writes=[Rps[b]])
                            s.op("dve", lambda e, b=b, Zn=Zn, Zc=Zc: e.tensor_tensor(Zn, ps[b][:], Zc, ALU.add),
                                 reads=[Rps[b], RZc], writes=[RZn])
                            if k < 6:
                                Ltn, Ln = W["Lt%d" % (1 - lc)], W["L%d" % (1 - lc)]
                                RLtn, RLn = W["RLt%d" % (1 - lc)], W["RL%d" % (1 - lc)]
                                b2 = bank()
                                for q in range(4):
                                    s.op("pe", lambda e, b2=b2, q=q, Lc=Lc, Ltc=Ltc: e.matmul(
                                        ps[b2][:, q * 128:(q + 1) * 128], q4(Lc)[:, q, :], q4(Ltc)[:, q, :], start=True, stop=True),
                                        reads=[RLc, RLtc], writes=[Rps[b2]])
                                s.op("act", lambda e, b2=b2, Ltn=Ltn: e.copy(Ltn, ps[b2][:]), reads=[Rps[b2]], writes=[RLtn])
                                if k < 5:
                                    b3 = bank()
                                    for q in range(4):
                                        s.op("pe", lambda e, b3=b3, q=q, Lc=Lc, Ltc=Ltc: e.matmul(
                                            ps[b3][:, q * 128:(q + 1) * 128], q4(Ltc)[:, q, :], q4(Lc)[:, q, :], start=True, stop=True),
                                            reads=[RLc, RLtc], writes=[Rps[b3]])
                                    s.op("act", lambda e, b3=b3, Ln=Ln: e.copy(Ln, ps[b3][:]), reads=[Rps[b3]], writes=[RLn])
                                lc = 1 - lc
                            zc = 1 - zc
                            yield
                        s.op("act", lambda e, zc=zc: e.copy(W["Zfb"], W["Z%d" % zc]), reads=[W["RZ%d" % zc]], writes=[W["RZfb"]])
                        Zf = q4(W["Zfb"])
                        RZf = W["RZfb"]
                        Mrbq = q4(W["Mrbt"])
                        Mrkq = q4(W["Mrkt"])
                        if cut < 6:
                            return
                        bM = bank()
                        bN = bank()
                        bR = bank()
                        for q, (c, h2) in enumerate(heads):
                            cl_ = c
                            prt = slice(64 * h2, 64 * h2 + 64)
                            s.op("pe", lambda e, q=q, cl_=cl_, prt=prt: e.matmul(
                                ps[bM][prt, cl_ * 64:(cl_ + 1) * 64], Zf[:, q, 0:64], BKq[:, q, 0:64],
                                start=(q == 0), stop=False, skip_group_check=True),
                                reads=[RZf, W["RBKtm"]], writes=[Rps[bM]])
                            s.op("pe", lambda e, q=q, cl_=cl_, prt=prt: e.matmul(
                                ps[bM][prt, cl_ * 64:(cl_ + 1) * 64], ident_b[0:64, 0:64], ident_b[0:64, 0:64],
                                start=False, stop=True, skip_group_check=True),
                                reads=[Rconst], writes=[Rps[bM]])
                            s.op("pe", lambda e, q=q, cl_=cl_, prt=prt: e.matmul(
                                ps[bN][prt, cl_ * 64:(cl_ + 1) * 64], BKq[:, q, 0:64], Zf[:, q, 64:128],
                                start=(q == 0), stop=False, skip_group_check=True),
                                reads=[RZf, W["RBKtm"]], writes=[Rps[bN]])
                            s.op("pe", lambda e, q=q, cl_=cl_, prt=prt, c=c, h2=h2: e.matmul(
                                ps[bN][prt, cl_ * 64:(cl_ + 1) * 64], BKq[:, q, 64:128], Vq(c, h2),
                                start=False, stop=False, skip_group_check=True),
                                reads=[W["RBKtm"], RVtm], writes=[Rps[bN]])
                            s.op("pe", lambda e, q=q, cl_=cl_, prt=prt: e.matmul(
                                ps[bR][prt, cl_ * 128:(cl_ + 1) * 128], Zf[:, q, 0:64], Mrbq[:, q, :],
                                start=(q == 0), stop=False, skip_group_check=True),
                                reads=[RZf, W["RMrbt"]], writes=[Rps[bR]])
                            s.op("pe", lambda e, q=q, cl_=cl_, prt=prt, c=c, h2=h2: e.matmul(
                                ps[bR][prt, cl_ * 128:(cl_ + 1) * 128], ident_b[prt, prt], fm(TB["RH"], c, h2),
                                start=False, stop=True, skip_group_check=True),
                                reads=[Rconst, TB["RRH"]], writes=[Rps[bR]])
                            st_ = first_fy[0]
                            first_fy[0] = False
                            hh = c * 2 + h2
                            s.op("pe", lambda e, q=q, hh=hh, st_=st_: e.matmul(
                                ps[fy][:, hh * 64:(hh + 1) * 64], Mrbq[:, q, :], Zf[:, q, 64:128],
                                start=st_, stop=False, skip_group_check=True),
                                reads=[RZf, W["RMrbt"]], writes=[Rps[fy]])
                            s.op("pe", lambda e, q=q, hh=hh, c=c, h2=h2: e.matmul(
                                ps[fy][:, hh * 64:(hh + 1) * 64], Mrkq[:, q, :], Vq(c, h2),
                                start=False, stop=False, skip_group_check=True),
                                reads=[W["RMrkt"], RVtm], writes=[Rps[fy]])
                        MTv = W["MT"].rearrange("p (c j) -> p c j", c=4)
                        RTv = W["RT"].rearrange("p (c t) -> p c t", c=4)
                        pg = slice(64 * g, 64 * g + 64)
                        s.op("act", lambda e, MTv=MTv, pg=pg: e.copy(MTv[pg], ps[bM][pg, 0:256].rearrange("p (c j) -> p c j", c=4)),
                             reads=[Rps[bM]], writes=[W["RMT"]])
                        s.op("act", lambda e, RTv=RTv, pg=pg: e.copy(RTv[pg], ps[bR][pg, 0:512].rearrange("p (c t) -> p c t", c=4)),
                             reads=[Rps[bR]], writes=[W["RRT"]])
                        for q, (c, h2) in enumerate(heads):
                            cl_ = c
                            prt = slice(64 * h2, 64 * h2 + 64)
                            hh = c * 2 + h2
                            s.op("pe", lambda e, cl_=cl_, prt=prt, hh=hh, c=c, RTv=RTv: e.matmul(
                                ps[fy][:, hh * 64:(hh + 1) * 64], RTv[prt, cl_, :], Abd3[prt, c, :],
                                start=False, stop=True, skip_group_check=True),
                                reads=[W["RRT"], RAbf], writes=[Rps[fy]])
                            s.op("pe", lambda e, cl_=cl_, prt=prt, c=c, MTv=MTv: e.matmul(
                                ps[bN][prt, cl_ * 64:(cl_ + 1) * 64], MTv[prt, cl_, :], Abd3[prt, c, :],
                                start=False, stop=True, skip_group_check=True),
                                reads=[W["RMT"], RAbf], writes=[Rps[bN]])
                        s.op("dve", lambda e, pg=pg: e.tensor_tensor(
                            Ad3[pg, :, :], ps[bN][pg, 0:256].rearrange("p (c i) -> p c i", c=4),
                            pcs[pg, par, :].unsqueeze(2).to_broadcast([64, 4, 64]), ALU.mult),
                            reads=[Rps[bN], Rpcs[par], RAbf], writes=[RAst])
                    gens = [group_body(0), group_body(1), nxt]
                    while gens:
                        for gn in list(gens):
                            try:
                                next(gn)
                            except StopIteration:
                                gens.remove(gn)
                    if cut < 6:
                        reserved.discard(fy)
                        return
                    s.op("act", lambda e: e.copy(Abd, Ad), reads=[RAst], writes=[RAbf])
                    isend = (ti % 2 == 1) if d == 0 else (ti % 2 == 0)
                    if isend:
                        s.dma("sp", s_out[u, d], Ad, reads=[RAst])
                    reserved.discard(fy)
                    if d == 0:
                        s.op("act", lambda e, ti=ti: e.copy(YSb[:, ti, :], ps[fy][:]), reads=[Rps[fy]], writes=[RYS])
                    else:
                        finalize(ti, fy)
                for n_ in range(len(seq)):
                    tile_body(n_)

        if stage >= 1:
            if "a" in sub:
                make_gg(0)
            if "b" in sub:
                make_hT(0, 0)
            if "c" in sub:
                even_inproj()
        if stage >= 3:
            attention()
            if P0L1:
                for _ in P0L1[0]:
                    pass
                del P0L1[:]
            barrier()
            if stage >= 5:
                rwkv_setup()
                rwkv_prepass()
                rwkv_chunked(NT if stage >= 6 else 2)
            else:
                s.op("pool", lambda e: e.memset(yT[:, 4:8, :], 0.0), writes=[RyT])
            barrier()
            out_proj(w_out_even, yT, RyT)
        if stage >= 2:
            barrier()
            ffn(0)
            barrier()
        if stage >= 4:
            if P0L1:
                for _ in P0L1[0]:
                    pass
                del P0L1[:]
            make_gg(1)
            make_hT(1, 0)
            fnet()
            out_proj(w_out_odd, yT, RyT)
            barrier()
            ffn(1, emit_out=True)
        for i in range(NT if stage < 4 else 0):
            s.dma("sp", y_out[i * 128:(i + 1) * 128, :], x_sb[:, i, :], reads=[Rx[i]])
        s.emit()
    return nc


def _core_units(c):
    if c < 6:
        return [("p", 5 * c + u) for u in range(5)]
    b = c - 6
    return [("s", b, u) for u in range(4)] + [("p", 30 + b)]


def _host_prep(inp):
    f32 = np.float32
    COFF, NCOL = colp_layout()
    ROFF, NROW = rowp_layout()
    g = {k: np.asarray(v) for k, v in inp.items()}
    colp = np.zeros((128, NCOL), f32)

    def put(name, vec):
        c = cols_of(vec)
        colp[:, COFF[name]:COFF[name] + c.shape[1]] = c
    for l in range(2):
        put("bada%d" % l, g["b_ada"][l])
        for k in range(4):
            put("gain%d_%d" % (l, k), g["norm_gains"][l, k])
        for k in range(3):
            put("conv%d_%d" % (l, k), g["ffn_conv"][l, k])
        put("convb%d" % l, g["ffn_conv_b"][l])
    put("mu0", g["rwkv_shift_mu"][0, 0])
    put("mu1", g["rwkv_shift_mu"][0, 1])
    for d in range(2):
        put("w0_%d" % d, g["rwkv_w0"][0, d])
        put("a0_%d" % d, g["rwkv_a0"][0, d])
    for k in range(3):
        put("kvec%d" % k, g["rwkv_kvec"][0, k])

    cmat = np.zeros((8, 128, 128), f32)
    cmat[0] = np.eye(128, dtype=f32)
    for i in range(64):
        cmat[1][2 * i, 2 * i + 1] = 1.0
        cmat[1][2 * i + 1, 2 * i] = -1.0
    cmat[2][:64, :64] = 1.0
    cmat[2][64:, 64:] = 1.0
    cmat[3][:64, 0] = 1.0
    cmat[3][64:, 1] = 1.0
    rr_, cc_ = np.meshgrid(np.arange(128), np.arange(128), indexing="ij")
    cmat[4] = (rr_ < cc_)
    cmat[5] = (rr_ > cc_)
    cmat[6] = (rr_ <= cc_)
    cmat[7] = (rr_ >= cc_)
    cc = np.arange(128)
    chang = 2.0 * np.pi * ((cc[:, None] * cc[None, :]) % 128) / 128.0
    chCS = np.concatenate([np.cos(chang), np.sin(chang)], axis=1).astype(f32)

    inv = (10000.0 ** (-np.arange(16, dtype=np.float32) / 16)).astype(f32)

    shared = dict(
        colp=colp, w_ada=g["w_ada"], w_in_even=g["w_in_even"][0], w_out_even=g["w_out_even"][0],
        w_out_odd=g["w_out_odd"][0], w_ffn_in=g["w_ffn_in"], w_ffn_out=g["w_ffn_out"],
        w2t=np.ascontiguousarray(g["rwkv_w2"][0].reshape(128, 512)),
        a2t=np.ascontiguousarray(g["rwkv_a2"][0].reshape(128, 512)),
        g2=np.ascontiguousarray(g["rwkv_g2"][0]), chCS=chCS, cmat=cmat,
        lnx=np.ascontiguousarray(g["rwkv_lnx"][0]))
    maps = []
    for c in range(NCORES):
        units = _core_units(c)
        xs = []
        for un in units:
            if un[0] == "p":
                xs.append(g["x_prompt"][un[1]])
            else:
                xs.append(g["x_sample"][un[1], un[2] * 256:(un[2] + 1) * 256])
        x_in = np.ascontiguousarray(np.concatenate(xs, axis=0), dtype=f32)
        is_s = c >= 6
        condA = g["c"][c - 6] if is_s else g["c_ctx"]
        condB = g["c_ctx"]
        cond = np.stack([condA, condB], axis=0).astype(f32)
        condT = np.ascontiguousarray(cond.reshape(2, 8, 128).transpose(2, 1, 0).reshape(128, 16))
        rowp = np.zeros((1, NROW), f32)
        rowp[0, ROFF["lam"]:ROFF["lam"] + 256] = g["diff_lambda"][0].reshape(-1)
        rowp[0, ROFF["subln"]:ROFF["subln"] + 128] = g["diff_subln"][0]
        ab = np.full((6, 5), -30000.0, f32)
        if is_s:
            ab[0:5, 0:4] = 0.0
            ab[5, 4] = 0.0
            bndL = [1, 0, 0, 0, 1]
            bndR = [0, 0, 0, 1, 1]
        else:
            for u in range(5):
                ab[u + 1, u] = 0.0
            bndL = [1] * 5
            bndR = [1] * 5
        rowp[0, ROFF["abias"]:ROFF["abias"] + 30] = ab.reshape(-1)
        rowp[0, ROFF["bndL"]:ROFF["bndL"] + 5] = bndL
        rowp[0, ROFF["bndR"]:ROFF["bndR"] + 5] = bndR
        ropeC = np.ones((128, T), f32)
        ropeS = np.zeros((128, T), f32)
        if is_s:
            t = np.arange(1024)
            row = (t // 64).astype(f32)
            col = (t % 64).astype(f32)
            ang = np.concatenate([row[:, None] * inv[None, :], col[:, None] * inv[None, :]], axis=1).astype(f32)
            pidx = (np.arange(128) % 64) // 2
            ropeC[:, :1024] = np.cos(ang)[:, pidx].T
            ropeS[:, :1024] = np.sin(ang)[:, pidx].T
        cacheKT = np.zeros((512, 256), f32)
        cacheV = np.zeros((256, 512), f32)
        initS = np.zeros((2, 128, 256), f32)
        if is_s:
            b = c - 6
            cacheKT[:] = g["cache_k"][b, 0].reshape(256, 512).T
            cacheV[:] = g["cache_v"][b, 0].reshape(256, 512)
            st = g["state_wkv"][b, 0]
            initS[:] = st.reshape(2, 4, 2, 64, 64).transpose(0, 2, 4, 1, 3).reshape(2, 128, 256)
        dC = np.zeros((T, T), np.float64)
        dS = np.zeros((T, T), np.float64)
        blocks = [(0, 1024), (1024, 256)] if is_s else [(256 * u, 256) for u in range(5)]
        for (a0, L) in blocks:
            ll = np.arange(L)
            ang = 2.0 * np.pi * ((ll[:, None] * ll[None, :]) % L) / L
            sc = 1.0 / math.sqrt(L * 128.0)
            dC[a0:a0 + L, a0:a0 + L] = np.cos(ang) * sc
            dS[a0:a0 + L, a0:a0 + L] = -np.sin(ang) * sc
        m = dict(shared)
        m.update(x_in=x_in, condT=condT, rowp=rowp, ropeC=ropeC, ropeS=ropeS, cacheKT=cacheKT, cacheV=cacheV,
                 initS=initS, dftC=dC.astype(f32), dftSn=dS.astype(f32))
        maps.append(m)
    return maps


_NC_CACHE = {}


def kernel(**inputs):
    maps = _host_prep(inputs)
    if "nc" not in _NC_CACHE:
        _NC_CACHE["nc"] = build()
    nc = _NC_CACHE["nc"]
    import os
    ncr = int(os.environ.get("NCR", NCORES))
    res = run_bass_kernel_spmd(nc, maps[:ncr], core_ids=list(range(ncr)))
    outs = list(res.results) + [res.results[0]] * (NCORES - ncr)
    y_prompt = np.zeros((32, 256, D), np.float32)
    y_sample = np.zeros((2, 1024, D), np.float32)
    nk = np.zeros((32, 1, 256, 4, 128), np.float32)
    nv = np.zeros((32, 1, 256, 4, 128), np.float32)
    ns = np.zeros((32, 1, 2, 8, 64, 64), np.float32)
    for c in range(NCORES):
        r = outs[c]
        for u, un in enumerate(_core_units(c)):
            sl = slice(u * 256, (u + 1) * 256)
            if un[0] == "p":
                bi = un[1]
                y_prompt[bi] = r["y_out"][sl]
                nk[bi, 0] = r["k_out"][sl].reshape(256, 4, 128)
                nv[bi, 0] = r["v_out"][sl].reshape(256, 4, 128)
                stt = r["s_out"][u]
                ns[bi, 0] = stt.reshape(2, 2, 64, 4, 64).transpose(0, 3, 1, 4, 2).reshape(2, 8, 64, 64)
            else:
                y_sample[un[1], un[2] * 256:(un[2] + 1) * 256] = r["y_out"][sl]
    return (y_prompt, y_sample, nk, nv, ns)
```

```python
import contextlib
import math
import numpy as np
import concourse.bass as bass
import concourse.mybir as mybir
from concourse.bass_utils import run_bass_kernel_spmd

F32 = mybir.dt.float32
BF16 = mybir.dt.bfloat16
AF = mybir.ActivationFunctionType
ALU = mybir.AluOpType
AX = mybir.AxisListType

T = 1280
NT = 10
U = 5
D = 1024
KC = 8
DFF = 2816
FC = 22
NCORES = 8
ARENA_W = 27200
EXPM05 = math.exp(-0.5)


class Res:
    __slots__ = ("name", "writer", "readers", "excl")

    def __init__(self, name, excl=False):
        self.name = name
        self.writer = None
        self.readers = []
        self.excl = excl


class Sched:
    ENGS = ("pe", "act", "dve", "pool", "sp")
    NDMA = 6

    def __init__(self, nc):
        self.nc = nc
        self.prog = {e: [] for e in self.ENGS}
        self.signal = {e: set() for e in self.ENGS}
        self.ndma = {e: 0 for e in self.ENGS}

    def _collect(self, reads, writes, eng=None):
        deps = []
        for r in reads:
            if r.writer is not None:
                deps.append(r.writer)
            if r.excl:
                deps.extend(t for t in r.readers if t[1] != eng)
        for w in writes:
            if w.writer is not None:
                deps.append(w.writer)
            deps.extend(w.readers)
        return deps

    def _commit(self, tok, reads, writes):
        for r in reads:
            r.readers.append(tok)
        for w in writes:
            w.writer = tok
            w.readers = []

    def op(self, eng, fn, reads=(), writes=()):
        deps = self._collect(reads, writes, eng)
        idx = len(self.prog[eng])
        if eng == "pe":
            deps = [d for d in deps if not (d[0] == "c" and d[1] == "pe")]
        for d in deps:
            if d[0] == "c":
                self.signal[d[1]].add(d[2])
        self.prog[eng].append(dict(fn=fn, deps=deps, kind="c"))
        tok = ("c", eng, idx)
        self._commit(tok, reads, writes)
        return tok

    def dma(self, eng, out, in_, reads=(), writes=()):
        deps = self._collect(reads, writes, eng)
        n = self.ndma[eng]
        self.ndma[eng] += 1
        if n >= self.NDMA:
            deps.append(("d", eng, n - self.NDMA))
        for d in deps:
            if d[0] == "c":
                self.signal[d[1]].add(d[2])
        self.prog[eng].append(dict(out=out, in_=in_, deps=deps, kind="d", n=n))
        tok = ("d", eng, n)
        self._commit(tok, reads, writes)
        return tok

    def emit(self):
        nc = self.nc
        with contextlib.ExitStack() as st:
            csem = {e: st.enter_context(nc.semaphore("c_" + e)) for e in self.ENGS}
            dsem = {e: [st.enter_context(nc.semaphore("d_%s_%d" % (e, i))) for i in range(self.NDMA)]
                    for e in self.ENGS if self.ndma[e] > 0}
            sigval = {}
            for e in self.ENGS:
                cnt = 0
                m = {}
                for i in range(len(self.prog[e])):
                    if i in self.signal[e]:
                        cnt += 1
                        m[i] = cnt
                sigval[e] = m

            def resolve(tok):
                if tok[0] == "c":
                    return csem[tok[1]], sigval[tok[1]][tok[2]], ("c", tok[1])
                e, n = tok[1], tok[2]
                return dsem[e][n % self.NDMA], 16 * (n // self.NDMA + 1), ("d", e, n % self.NDMA)

            def run_engine(e, h):
                waited = {}
                for i, ins in enumerate(self.prog[e]):
                    need = {}
                    for d in ins["deps"]:
                        sem, val, key = resolve(d)
                        if waited.get(key, 0) >= val:
                            continue
                        if key not in need or need[key][1] < val:
                            need[key] = (sem, val)
                    for key, (sem, val) in need.items():
                        h.wait_ge(sem, val)
                        waited[key] = val
                    if ins["kind"] == "c":
                        bi = ins["fn"](h)
                        if i in self.signal[e]:
                            bi.then_inc(csem[e], 1)
                    else:
                        n = ins["n"]
                        h.dma_start(out=ins["out"], in_=ins["in_"]).then_inc(dsem[e][n % self.NDMA], 16)
                if self.ndma[e] > 0:
                    n = self.ndma[e]
                    for slot in range(self.NDMA):
                        cnt = (n - slot + self.NDMA - 1) // self.NDMA if n > slot else 0
                        if cnt > 0:
                            h.wait_ge(dsem[e][slot], 16 * cnt)

            with nc.Block() as block:
                @block.tensor
                def _(eng):
                    run_engine("pe", eng)

                @block.scalar
                def _(eng):
                    run_engine("act", eng)

                @block.vector
                def _(eng):
                    run_engine("dve", eng)

                @block.gpsimd
                def _(eng):
                    run_engine("pool", eng)

                @block.sync
                def _(eng):
                    run_engine("sp", eng)


def colp_layout():
    off = {}
    n = 0

    def add(name, cols):
        nonlocal n
        off[name] = n
        n += cols
    for l in range(2):
        add("bada%d" % l, 48)
        for g in range(4):
            add("gain%d_%d" % (l, g), 8)
        for k in range(3):
            add("conv%d_%d" % (l, k), FC)
        add("convb%d" % l, FC)
    add("mu0", 15)
    add("mu1", 15)
    for d in range(2):
        add("w0_%d" % d, 4)
        add("a0_%d" % d, 4)
    for k in range(3):
        add("kvec%d" % k, 4)
    return off, n


def rowp_layout():
    off = {}
    n = 0

    def add(name, cols):
        nonlocal n
        off[name] = n
        n += cols
    add("lam", 256)
    add("subln", 128)
    add("abias", 30)
    add("bndL", 5)
    add("bndR", 5)
    return off, n


def cols_of(vec):
    v = np.asarray(vec, np.float32).reshape(-1, 128)
    return np.ascontiguousarray(v.T)


def build(stage=99):
    nc = bass.Bass("TRN2", target_bir_lowering=False)
    COFF, NCOL = colp_layout()
    ROFF, NROW = rowp_layout()

    def din(name, shape):
        return nc.dram_tensor(name, list(shape), F32, kind="ExternalInput").ap()

    def dout(name, shape):
        return nc.dram_tensor(name, list(shape), F32, kind="ExternalOutput").ap()

    x_in = din("x_in", [T, D])
    condT = din("condT", [128, 16])
    colp = din("colp", [128, NCOL])
    rowp = din("rowp", [1, NROW])
    w_ada = din("w_ada", [2, D, 6 * D])
    w_in_even = din("w_in_even", [D, 3456])
    w_out_even = din("w_out_even", [D, D])
    w_out_odd = din("w_out_odd", [D, D])
    w_ffn_in = din("w_ffn_in", [2, D, 2 * DFF])
    w_ffn_out = din("w_ffn_out", [2, DFF, D])
    w2t_d = din("w2t", [128, 512])
    a2t_d = din("a2t", [128, 512])
    g2_d = din("g2", [128, 512])
    cacheKT = din("cacheKT", [512, 256])
    cacheV = din("cacheV", [256, 512])
    ropeC = din("ropeC", [128, T])
    ropeS = din("ropeS", [128, T])
    initS = din("initS", [2, 128, 256])
    dftC = din("dftC", [T, T])
    dftSn = din("dftSn", [T, T])
    chCS = din("chCS", [128, 256])
    cmat = din("cmat", [8, 128, 128])
    lnx_d = din("lnx", [2, 512])

    y_out = dout("y_out", [T, D])
    k_out = dout("k_out", [T, 512])
    v_out = dout("v_out", [T, 512])
    s_out = dout("s_out", [U, 2, 128, 256])

    import os
    sub = os.environ.get("SUB", "abcmqv")
    with contextlib.ExitStack() as st:
        s = Sched(nc)

        def sb(name, shape, dt=F32):
            return st.enter_context(nc.sbuf_tensor(name, list(shape), dt))

        x_sb = sb("x_sb", [128, NT, D])
        Rx = [Res("x%d" % i) for i in range(NT)]
        WB = 4096
        wbuf = [sb("wb%d" % i, [128, WB], BF16) for i in range(3)]
        Rw = [Res("wb%d" % i) for i in range(3)]
        wctr = [0]
        colp_sb = sb("colp_sb", [128, NCOL])
        Rcolp = Res("colp")
        rowp_sb = sb("rowp_sb", [128, NROW])
        Rrowp = Res("rowp")
        ident_f = sb("ident_f", [128, 128])
        ident_b = sb("ident_b", [128, 128], BF16)
        rotT_b = sb("rotT_b", [128, 128], BF16)
        bones_f = sb("bones_f", [128, 128])
        halfsel_f = sb("halfsel_f", [128, 128])
        bones_b = sb("bones_b", [128, 128], BF16)
        Rconst = Res("const")
        gg_sb = sb("gg_sb", [128, 2, 2, D], BF16)
        Rgg = Res("gg")
        scond = sb("scond", [128, 16], BF16)
        condf = sb("condf", [128, 16])
        Rcond = Res("cond")
        modcol = sb("modcol", [128, 2, 96])
        Rmod = Res("modcol")
        scsh = sb("scsh", [128, 2, 2, 2, 16])
        Rscsh = Res("scsh")
        stat = sb("stat", [128, 64])
        Rstat = Res("stat")
        junk_b = sb("junk_b", [128, D], BF16)
        Rjunk = Res("junk")
        xn_b = sb("xn_b", [128, D], BF16)
        Rxn = Res("xn")
        stage_f = sb("stage_f", [128, 2, 512])
        Rstage = [Res("stage0"), Res("stage1")]
        stctr = [0]
        arena = sb("arena", [128, ARENA_W])
        Rar = {}

        def ares(name):
            if name not in Rar:
                Rar[name] = Res("ar_" + name)
            return Rar[name]

        def carve(off_w, nelem, dt):
            if dt == F32:
                return arena[:, off_w:off_w + nelem]
            return arena[:, off_w:off_w + (nelem + 1) // 2].bitcast(BF16)[:, 0:nelem]

        ps = [st.enter_context(nc.psum_tensor("ps%d" % b, [128, 512], F32)) for b in range(8)]
        psb = [p.bitcast(BF16) for p in ps]
        Rps = [Res("ps%d" % b, excl=True) for b in range(8)]
        bctr = [0]

        reserved = set()

        def bank():
            while True:
                b = bctr[0] % 8
                bctr[0] += 1
                if b not in reserved:
                    return b

        def barrier():
            allr = list(Rar.values()) + list(Rw)
            s.op("dve", lambda e: e.memset(stat[:, 63:64], 0.0), writes=allr + [Rstat])

        def wload(src_ap, kc, ncols):
            i = wctr[0] % 3
            wctr[0] += 1
            view = wbuf[i][:, 0:kc * ncols].rearrange("p (c n) -> p c n", c=kc)
            s.dma("pool", view, src_ap.rearrange("(c p) n -> p c n", p=128), writes=[Rw[i]])
            return view, Rw[i]

        s.dma("sp", colp_sb[:], colp, writes=[Rcolp])
        s.dma("sp", rowp_sb[:], rowp[0, :].partition_broadcast(128), writes=[Rrowp])
        s.dma("sp", ident_f[:], cmat[0], writes=[Rconst])
        s.dma("sp", bones_f[:], cmat[2], writes=[Rconst])
        s.dma("sp", halfsel_f[:], cmat[3], writes=[Rconst])
        s.dma("pool", ident_b[:], cmat[0], writes=[Rconst])
        s.dma("pool", rotT_b[:], cmat[1], writes=[Rconst])
        s.dma("pool", bones_b[:], cmat[2], writes=[Rconst])
        s.dma("sp", condf[:], condT, writes=[Rcond])
        for i in range(NT):
            s.dma("sp", x_sb[:, i, :], x_in[i * 128:(i + 1) * 128, :], writes=[Rx[i]])
        s.op("act", lambda e: e.activation(scond[:], condf[:], AF.Silu), reads=[Rcond], writes=[Rcond])

        def cp(name, c=0, n=1):
            o = COFF[name] + c
            return colp_sb[:, o:o + n]

        def p0_gen(l):
            b = bank()
            reserved.add(b)
            for v in range(6):
                for half in range(2):
                    wv, rw = wload(w_ada[l][:, v * 1024 + half * 512: v * 1024 + half * 512 + 512], KC, 512)
                    for j in range(4):
                        col = (v * 8 + half * 4 + j) * 2
                        for kc in range(KC):
                            s.op("pe", lambda e, wv=wv, j=j, kc=kc, col=col, b=b: e.matmul(
                                ps[b][:, col:col + 2], wv[:, kc, j * 128:(j + 1) * 128],
                                scond[:, kc * 2:kc * 2 + 2], start=(kc == 0), stop=(kc == KC - 1)),
                                reads=[rw, Rcond], writes=[Rps[b]])
                    yield
            s.op("dve", lambda e, l=l, b=b: e.tensor_tensor(
                modcol[:, l, :].rearrange("p (c a) -> p c a", a=2),
                ps[b][:, 0:96].rearrange("p (c a) -> p c a", a=2),
                cp("bada%d" % l, 0, 48).unsqueeze(2).to_broadcast([128, 48, 2]), ALU.add),
                reads=[Rps[b], Rcolp], writes=[Rmod])
            reserved.discard(b)

            def mv(v, l=l):
                return modcol[:, l, v * 16:(v + 1) * 16].rearrange("p (c a) -> p c a", a=2)

            def gain(g, l=l):
                return cp("gain%d_%d" % (l, g), 0, 8).unsqueeze(2).to_broadcast([128, 8, 2])
            for wi, (vs, vsh, g) in enumerate([(1, 0, 0), (4, 3, 2)]):
                sc = scsh[:, l, wi, 0, :].rearrange("p (c a) -> p c a", a=2)
                sh = scsh[:, l, wi, 1, :].rearrange("p (c a) -> p c a", a=2)
                s.op("dve", lambda e, sc=sc, vs=vs, mv=mv: e.tensor_scalar(sc, mv(vs), 1.0, None, ALU.add),
                     reads=[Rmod], writes=[Rscsh])
                s.op("dve", lambda e, sc=sc, g=g, gain=gain: e.tensor_tensor(sc, sc, gain(g), ALU.mult),
                     reads=[Rscsh, Rcolp], writes=[Rscsh])
                s.op("dve", lambda e, sh=sh, vsh=vsh, mv=mv: e.tensor_copy(sh, mv(vsh)),
                     reads=[Rmod], writes=[Rscsh])

        for _ in p0_gen(0):
            pass
        P0L1 = [p0_gen(1)]

        ggcol = sb("ggcol", [128, 2, 16])
        Rggcol = Res("ggcol")

        def make_gg(l):
            for wi, (vg, g) in enumerate([(2, 1), (5, 3)]):
                gc = ggcol[:, wi, :].rearrange("p (c a) -> p c a", a=2)
                s.op("dve", lambda e, gc=gc, vg=vg, g=g, l=l: e.tensor_tensor(
                    gc, modcol[:, l, vg * 16:(vg + 1) * 16].rearrange("p (c a) -> p c a", a=2),
                    cp("gain%d_%d" % (l, g), 0, 8).unsqueeze(2).to_broadcast([128, 8, 2]), ALU.mult),
                    reads=[Rmod, Rcolp], writes=[Rggcol])
                for a in range(2):
                    for hh in range(2):
                        b = bank()
                        for j in range(4):
                            c = hh * 4 + j
                            s.op("pe", lambda e, b=b, wi=wi, c=c, a=a, j=j: e.matmul(
                                ps[b][:, j * 128:(j + 1) * 128],
                                ggcol[:, wi, c * 2 + a:c * 2 + a + 1].to_broadcast([128, 128]),
                                ident_f[:], start=True, stop=True),
                                reads=[Rggcol, Rconst], writes=[Rps[b]])
                        s.op("act", lambda e, b=b, wi=wi, a=a, hh=hh: e.copy(
                            gg_sb[:, wi, a, hh * 512:(hh + 1) * 512], ps[b][:]),
                            reads=[Rps[b]], writes=[Rgg])

        hT = carve(0, 8 * T, BF16).rearrange("p (c t) -> p c t", c=8)
        RhT = ares("hT")

        def ab_of_tile(i):
            return 0 if i < 8 else 1

        def make_hT(l, wi):
            for i in range(NT):
                a = ab_of_tile(i)
                s.op("act", lambda e, i=i: e.activation(junk_b[:], x_sb[:, i, :], AF.Square, scale=1.0 / 32.0,
                                                        accum_out=stat[:, 0:1]),
                     reads=[Rx[i]], writes=[Rjunk, Rstat])
                s.op("dve", lambda e: e.tensor_scalar(stat[:, 1:2], stat[:, 0:1], 1e-6, None, ALU.add),
                     reads=[Rstat], writes=[Rstat])
                s.op("act", lambda e: e.activation(stat[:, 1:2], stat[:, 1:2], AF.Ln), reads=[Rstat], writes=[Rstat])
                s.op("act", lambda e: e.activation(stat[:, 2:3], stat[:, 1:2], AF.Exp, scale=-0.5), reads=[Rstat], writes=[Rstat])
                s.op("dve", lambda e, i=i: e.tensor_scalar(xn_b[:], x_sb[:, i, :], stat[:, 2:3], None, ALU.mult),
                     reads=[Rx[i], Rstat], writes=[Rxn])
                b = bank()
                for c in range(8):
                    s.op("pe", lambda e, b=b, c=c: e.transpose(psb[b][:, c * 128:(c + 1) * 128],
                                                               xn_b[:, c * 128:(c + 1) * 128], ident_b[:]),
                         reads=[Rxn, Rconst], writes=[Rps[b]])
                for c in range(8):
                    s.op("act", lambda e, b=b, c=c, i=i, a=a: e.activation(
                        hT[:, c, i * 128:(i + 1) * 128], psb[b][:, c * 128:(c + 1) * 128], AF.Identity,
                        bias=scsh[:, l, wi, 1, c * 2 + a:c * 2 + a + 1],
                        scale=scsh[:, l, wi, 0, c * 2 + a:c * 2 + a + 1]),
                        reads=[Rps[b], Rscsh], writes=[RhT])

        TOKCH = [(0, 512), (512, 512), (1024, 256)]

        def linear_fm(wv, rw, ncol0, actT, Ract, kc_n, evac):
            for (t0, tn) in TOKCH:
                b = bank()
                for kc in range(kc_n):
                    s.op("pe", lambda e, b=b, kc=kc, t0=t0, tn=tn: e.matmul(
                        ps[b][:, 0:tn], wv[:, kc, ncol0:ncol0 + 128], actT[:, kc, t0:t0 + tn],
                        start=(kc == 0), stop=(kc == kc_n - 1)),
                        reads=[rw, Ract], writes=[Rps[b]])
                evac(b, t0, tn)

        def linear_tm(wv, rw, ncols, actT, Ract, kc_n, i, b, col0=0):
            for kc in range(kc_n):
                s.op("pe", lambda e, kc=kc: e.matmul(
                    ps[b][:, col0:col0 + ncols], actT[:, kc, i * 128:(i + 1) * 128], wv[:, kc, 0:ncols],
                    start=(kc == 0), stop=(kc == kc_n - 1)),
                    reads=[rw, Ract], writes=[Rps[b]])

        def residual_update(i, banks, wi, fsrc=None, Rf=None):
            a = ab_of_tile(i)
            if fsrc is None:
                for hh, b in enumerate(banks):
                    s.op("act", lambda e, b=b, hh=hh: e.activation(junk_b[:, 0:512], ps[b][:], AF.Square,
                                                                   scale=1.0 / 32.0, accum_out=stat[:, 8 + hh:9 + hh]),
                         reads=[Rps[b]], writes=[Rjunk, Rstat])
                s.op("dve", lambda e: e.tensor_tensor(stat[:, 10:11], stat[:, 8:9], stat[:, 9:10], ALU.add),
                     reads=[Rstat], writes=[Rstat])
            else:
                s.op("act", lambda e: e.activation(junk_b[:], fsrc, AF.Square, scale=1.0 / 32.0,
                                                   accum_out=stat[:, 10:11]),
                     reads=[Rf], writes=[Rjunk, Rstat])
            s.op("dve", lambda e: e.tensor_scalar(stat[:, 11:12], stat[:, 10:11], 1e-6, None, ALU.add),
                 reads=[Rstat], writes=[Rstat])
            s.op("act", lambda e: e.activation(stat[:, 11:12], stat[:, 11:12], AF.Ln), reads=[Rstat], writes=[Rstat])
            s.op("act", lambda e: e.activation(stat[:, 12:13], stat[:, 11:12], AF.Exp, scale=-0.5), reads=[Rstat], writes=[Rstat])
            for hh in range(2):
                src = ps[banks[hh]][:] if fsrc is None else fsrc[:, hh * 512:(hh + 1) * 512]
                rr = [Rps[banks[hh]]] if fsrc is None else [Rf]
                si = stctr[0] % 2
                stctr[0] += 1
                s.op("dve", lambda e, src=src, hh=hh, si=si: e.scalar_tensor_tensor(
                    stage_f[:, si, :], src, stat[:, 12:13], gg_sb[:, wi, a, hh * 512:(hh + 1) * 512],
                    ALU.mult, ALU.mult),
                    reads=rr + [Rstat, Rgg], writes=[Rstage[si]])
                s.op("dve", lambda e, hh=hh, si=si, i=i: e.tensor_tensor(
                    x_sb[:, i, hh * 512:(hh + 1) * 512], x_sb[:, i, hh * 512:(hh + 1) * 512], stage_f[:, si, :], ALU.add),
                    reads=[Rstage[si], Rx[i]], writes=[Rx[i]])

        actT = carve(5120, FC * T, BF16).rearrange("p (c t) -> p c t", c=FC)
        RactT = ares("actT")
        graw = carve(19200, T + 2, F32)
        Rgraw = ares("graw")
        cbuf = carve(19200 + 1284, T, F32)
        Rcbuf = ares("cbuf")
        sbuf_s = carve(19200 + 1284 + 1280, T, F32)
        Rsbuf = ares("sbuf_s")
        fstageA = carve(19200, 5 * D, F32).rearrange("p (i n) -> p i n", i=5)
        fstageB = carve(0, 5 * D, F32).rearrange("p (i n) -> p i n", i=5)

        def fst(i):
            return fstageA[:, i, :] if i < 5 else fstageB[:, i - 5, :]

        def Rfst_of(i):
            return ares("fstage") if i < 5 else RhT
        bcorr = sb("bcorr", [128, 16])
        Rbcorr = Res("bcorr")

        def ffn(l, emit_out=False):
            make_hT(l, 1)
            s.op("dve", lambda e: e.memset(graw[:, 0:1], 0.0), writes=[Rgraw])
            s.op("dve", lambda e: e.memset(graw[:, T + 1:T + 2], 0.0), writes=[Rgraw])
            for blk in range(0, FC, 2):
                wu, ru = wload(w_ffn_in[l][:, blk * 128: blk * 128 + 256], KC, 256)
                wg, rg = wload(w_ffn_in[l][:, DFF + blk * 128: DFF + blk * 128 + 256], KC, 256)
                for jj in range(2):
                    fc = blk + jj
                    def evac_g(b, t0, tn):
                        s.op("act", lambda e, b=b, t0=t0, tn=tn: e.copy(graw[:, 1 + t0:1 + t0 + tn], ps[b][:, 0:tn]),
                             reads=[Rps[b]], writes=[Rgraw])
                    linear_fm(wg, rg, jj * 128, hT, RhT, KC, evac_g)
                    w0 = cp("conv%d_0" % l, fc)
                    w1 = cp("conv%d_1" % l, fc)
                    w2 = cp("conv%d_2" % l, fc)
                    cb = cp("convb%d" % l, fc)
                    s.op("act", lambda e, w1=w1, cb=cb: e.activation(cbuf[:], graw[:, 1:T + 1], AF.Identity, bias=cb, scale=w1),
                         reads=[Rgraw, Rcolp], writes=[Rcbuf])
                    s.op("dve", lambda e, w0=w0: e.scalar_tensor_tensor(cbuf[:], graw[:, 0:T], w0, cbuf[:], ALU.mult, ALU.add),
                         reads=[Rgraw, Rcolp, Rcbuf], writes=[Rcbuf])
                    s.op("dve", lambda e, w2=w2: e.scalar_tensor_tensor(cbuf[:], graw[:, 2:T + 2], w2, cbuf[:], ALU.mult, ALU.add),
                         reads=[Rgraw, Rcolp, Rcbuf], writes=[Rcbuf])
                    gprev = graw[:, 256:256 + 1024].rearrange("p (u k) -> p u k", k=256)[:, :, 0]
                    gnext = graw[:, 257:257 + 1024].rearrange("p (u k) -> p u k", k=256)[:, :, 0]
                    s.op("dve", lambda e, gprev=gprev: e.tensor_tensor(bcorr[:, 0:4], gprev, rowp_sb[:, ROFF["bndL"] + 1:ROFF["bndL"] + 5], ALU.mult),
                         reads=[Rgraw, Rrowp], writes=[Rbcorr])
                    s.op("dve", lambda e, w0=w0: e.tensor_scalar(bcorr[:, 0:4], bcorr[:, 0:4], w0, None, ALU.mult),
                         reads=[Rbcorr, Rcolp], writes=[Rbcorr])
                    c_at = cbuf[:, 256:256 + 1024].rearrange("p (u k) -> p u k", k=256)[:, :, 0]
                    s.op("dve", lambda e, c_at=c_at: e.tensor_tensor(c_at, c_at, bcorr[:, 0:4], ALU.subtract),
                         reads=[Rbcorr, Rcbuf], writes=[Rcbuf])
                    s.op("dve", lambda e, gnext=gnext: e.tensor_tensor(bcorr[:, 4:8], gnext, rowp_sb[:, ROFF["bndL"] + 1:ROFF["bndL"] + 5], ALU.mult),
                         reads=[Rgraw, Rrowp], writes=[Rbcorr])
                    s.op("dve", lambda e, w2=w2: e.tensor_scalar(bcorr[:, 4:8], bcorr[:, 4:8], w2, None, ALU.mult),
                         reads=[Rbcorr, Rcolp], writes=[Rbcorr])
                    c_at2 = cbuf[:, 255:255 + 1024].rearrange("p (u k) -> p u k", k=256)[:, :, 0]
                    s.op("dve", lambda e, c_at2=c_at2: e.tensor_tensor(c_at2, c_at2, bcorr[:, 4:8], ALU.subtract),
                         reads=[Rbcorr, Rcbuf], writes=[Rcbuf])
                    s.op("act", lambda e: e.activation(sbuf_s[:], cbuf[:], AF.Silu), reads=[Rcbuf], writes=[Rsbuf])
                    def evac_u(b, t0, tn, fc=fc):
                        s.op("dve", lambda e, b=b, t0=t0, tn=tn: e.tensor_tensor(
                            actT[:, fc, t0:t0 + tn], ps[b][:, 0:tn], sbuf_s[:, t0:t0 + tn], ALU.mult),
                            reads=[Rps[b], Rsbuf], writes=[RactT])
                    linear_fm(wu, ru, jj * 128, hT, RhT, KC, evac_u)
            foA = carve(24320, FC * 256, BF16).rearrange("p (c n) -> p c n", c=FC)
            RfoA = ares("foA")
            foB0 = wbuf[0][:, 0:16 * 256].rearrange("p (c n) -> p c n", c=16)
            foB1 = wbuf[1][:, 0:6 * 256].rearrange("p (c n) -> p c n", c=6)
            for nb in range(4):
                src = w_ffn_out[l][:, nb * 256:(nb + 1) * 256].rearrange("(c p) n -> p c n", p=128)
                if nb % 2 == 0:
                    s.dma("pool", foA, src, writes=[RfoA])

                    def rhs_of(kc):
                        return foA[:, kc, :], RfoA
                else:
                    s.dma("pool", foB0, src[:, 0:16, :], writes=[Rw[0]])
                    s.dma("pool", foB1, src[:, 16:22, :], writes=[Rw[1]])

                    def rhs_of(kc):
                        return (foB0[:, kc, :], Rw[0]) if kc < 16 else (foB1[:, kc - 16, :], Rw[1])
                for i in range(NT):
                    b = bank()
                    for kc in range(FC):
                        rv, rr_ = rhs_of(kc)
                        s.op("pe", lambda e, b=b, kc=kc, i=i, rv=rv: e.matmul(
                            ps[b][:, 0:256], actT[:, kc, i * 128:(i + 1) * 128], rv,
                            start=(kc == 0), stop=(kc == FC - 1)),
                            reads=[rr_, RactT], writes=[Rps[b]])
                    s.op("act", lambda e, b=b, i=i, nb=nb: e.copy(fst(i)[:, nb * 256:(nb + 1) * 256], ps[b][:, 0:256]),
                         reads=[Rps[b]], writes=[Rfst_of(i)])
            for i in range(NT):
                residual_update(i, None, 1, fsrc=fst(i), Rf=Rfst_of(i))
                if emit_out:
                    s.dma("sp", y_out[i * 128:(i + 1) * 128, :], x_sb[:, i, :], reads=[Rx[i]])

        qT = carve(5120, 4 * T, BF16).rearrange("p (c t) -> p c t", c=4)
        RqT = ares("qT")
        kT = carve(7680, 4 * 1536, BF16).rearrange("p (c t) -> p c t", c=4)
        RkT = ares("kT")
        Vaug = carve(10752, 12 * 4 * 144, BF16).rearrange("p (k h d) -> p k h d", k=12, h=4)
        RV = ares("Vaug")
        yT = carve(0, 8 * T, BF16).rearrange("p (c t) -> p c t", c=8)
        fbT = carve(14208, 15 * (T + 2), BF16).rearrange("p (c t) -> p c t", c=15)
        RfbT = ares("fbT")
        RyT = RhT
        ropeC_sb = carve(23824, T, F32)
        ropeS_sb = carve(23824 + T, T, F32)
        Rrope = Res("rope")
        rtmp = sb("rtmp", [128, 2, 512])
        Rrtmp = Res("rtmp")
        raw_b = sb("raw_b", [128, 512], BF16)
        Rraw = Res("raw_b")

        def even_inproj():
            s.dma("sp", ropeC_sb[:], ropeC, writes=[Rrope])
            s.dma("sp", ropeS_sb[:], ropeS, writes=[Rrope])
            s.dma("pool", kT[:, :, 0:256], cacheKT.rearrange("(h p) t -> p h t", p=128), writes=[RkT])
            for kk_ in range(2):
                s.dma("pool", Vaug[:, kk_, :, 0:128],
                      cacheV[kk_ * 128:(kk_ + 1) * 128, :].rearrange("p (h d) -> p h d", h=4), writes=[RV])
            if "m" in sub:
                s.op("pool", lambda e: e.memset(Vaug[:, :, :, 128:129], 1.0), writes=[RV])
            for which, dst, doff in ((0, qT, 0), (1, kT, 256)):
                if "q" not in sub:
                    break
                wv, rw = wload(w_in_even[:, which * 512:(which + 1) * 512], KC, 512)
                Rdst = RqT if which == 0 else RkT
                for h in range(4):
                    def evac(b, t0, tn, h=h, dst=dst, doff=doff, Rdst=Rdst):
                        s.op("act", lambda e, b=b, tn=tn: e.copy(raw_b[:, 0:tn], ps[b][:, 0:tn]),
                             reads=[Rps[b]], writes=[Rraw])
                        b2 = bank()
                        s.op("pe", lambda e, b2=b2, tn=tn: e.matmul(ps[b2][:, 0:tn], rotT_b[:], raw_b[:, 0:tn], start=True, stop=True),
                             reads=[Rraw, Rconst], writes=[Rps[b2]])
                        s.op("dve", lambda e, t0=t0, tn=tn: e.tensor_tensor(rtmp[:, 0, 0:tn], raw_b[:, 0:tn], ropeC_sb[:, t0:t0 + tn], ALU.mult),
                             reads=[Rraw, Rrope], writes=[Rrtmp])
                        s.op("dve", lambda e, b2=b2, t0=t0, tn=tn: e.tensor_tensor(rtmp[:, 1, 0:tn], ps[b2][:, 0:tn], ropeS_sb[:, t0:t0 + tn], ALU.mult),
                             reads=[Rps[b2], Rrope], writes=[Rrtmp])
                        s.op("dve", lambda e, t0=t0, tn=tn: e.tensor_tensor(dst[:, h, doff + t0:doff + t0 + tn], rtmp[:, 0, 0:tn], rtmp[:, 1, 0:tn], ALU.add),
                             reads=[Rrtmp], writes=[Rdst])
                    linear_fm(wv, rw, h * 128, hT, RhT, KC, evac)
                if which == 1:
                    for i in range(NT):
                        b = bank()
                        linear_tm(wv, rw, 512, hT, RhT, KC, i, b)
                        si = stctr[0] % 2
                        stctr[0] += 1
                        s.op("act", lambda e, b=b, si=si: e.copy(stage_f[:, si, :], ps[b][:]), reads=[Rps[b]], writes=[Rstage[si]])
                        s.dma("sp", k_out[i * 128:(i + 1) * 128, :], stage_f[:, si, :], reads=[Rstage[si]])
            s.op("pool", lambda e: e.memset(fbT[:, :, 0:1], 0.0), writes=[RfbT])
            s.op("pool", lambda e: e.memset(fbT[:, :, T + 1:T + 2], 0.0), writes=[RfbT])
            for pc, (c0, ncols) in enumerate([(1536, 512), (2048, 512), (2560, 512), (3072, 384)]):
                wv, rw = wload(w_in_even[:, c0:c0 + ncols], KC, ncols)
                for j in range(ncols // 128):
                    ch = pc * 4 + j

                    def evac_fb(b, t0, tn, ch=ch):
                        s.op("act", lambda e, b=b, t0=t0, tn=tn: e.copy(fbT[:, ch, 1 + t0:1 + t0 + tn], ps[b][:, 0:tn]),
                             reads=[Rps[b]], writes=[RfbT])
                    linear_fm(wv, rw, j * 128, hT, RhT, KC, evac_fb)
            wv, rw = wload(w_in_even[:, 1024:1536], KC, 512)
            for i in range(NT if "v" in sub else 0):
                b = bank()
                linear_tm(wv, rw, 512, hT, RhT, KC, i, b)
                si = stctr[0] % 2
                stctr[0] += 1
                s.op("act", lambda e, b=b, si=si: e.copy(stage_f[:, si, :], ps[b][:]), reads=[Rps[b]], writes=[Rstage[si]])
                s.dma("sp", v_out[i * 128:(i + 1) * 128, :], stage_f[:, si, :], reads=[Rstage[si]])
                s.op("dve", lambda e, si=si, i=i: e.tensor_copy(Vaug[:, 2 + i, :, 0:128], stage_f[:, si, :].rearrange("p (h d) -> p h d", h=4)),
                     reads=[Rstage[si]], writes=[RV])


        NPB = 4
        PT = [[sb("PT%d%d" % (m, k), [128, 256], BF16) for k in range(NPB)] for m in range(2)]
        RPT = [[Res("PT%d%d" % (m, k)) for k in range(NPB)] for m in range(2)]
        ya_f = sb("ya_f", [128, 128])
        Rya = Res("ya_f")
        ya_b = sb("ya_b", [128, 128], BF16)
        Ryab = Res("ya_b")
        subln08 = sb("subln08", [128, 128])
        Rsub = Res("subln08")
        lamt = sb("lamt", [128, 64])
        Rlamt = Res("lamt")

        def attention():
            lo = ROFF["lam"]
            for k in range(2):
                s.op("dve", lambda e, k=k: e.tensor_tensor(lamt[:], rowp_sb[:, lo + 128 * k: lo + 128 * k + 64],
                                                          rowp_sb[:, lo + 128 * k + 64: lo + 128 * k + 128], ALU.mult),
                     reads=[Rrowp], writes=[Rlamt])
                s.op("dve", lambda e, k=k: e.reduce_sum(stat[:, 20 + k:21 + k], lamt[:], axis=AX.X),
                     reads=[Rlamt], writes=[Rstat])
            s.op("act", lambda e: e.activation(stat[:, 22:24], stat[:, 20:22], AF.Exp), reads=[Rstat], writes=[Rstat])
            s.op("dve", lambda e: e.tensor_tensor(stat[:, 24:25], stat[:, 22:23], stat[:, 23:24], ALU.subtract),
                 reads=[Rstat], writes=[Rstat])
            s.op("dve", lambda e: e.tensor_scalar(stat[:, 25:26], stat[:, 24:25], 0.2, -1.0, ALU.add, ALU.mult),
                 reads=[Rstat], writes=[Rstat])
            s.op("dve", lambda e: e.tensor_scalar(subln08[:], rowp_sb[:, ROFF["subln"]:ROFF["subln"] + 128], 0.8, None, ALU.mult),
                 reads=[Rrowp], writes=[Rsub])
            pctr = [0, 0]
            for h in range(4):
                for qu in range(5):
                    if P0L1:
                        try:
                            next(P0L1[0])
                        except StopIteration:
                            del P0L1[:]
                    acc = [bank(), bank()]
                    reserved.update(acc)
                    pend = []
                    for kt in range(12):
                        ku = 0 if kt < 2 else 1 + (kt - 2) // 2
                        bcol = ROFF["abias"] + ku * 5 + qu
                        for m in range(2):
                            bs = bank()
                            s.op("pe", lambda e, bs=bs, m=m, h=h, kt=kt, qu=qu: e.matmul(
                                ps[bs][:, 0:256], kT[64 * m:64 * m + 64, h, kt * 128:(kt + 1) * 128],
                                qT[64 * m:64 * m + 64, h, qu * 256:(qu + 1) * 256], start=True, stop=True),
                                reads=[RkT, RqT], writes=[Rps[bs]])
                            pk = pctr[m] % NPB
                            pctr[m] += 1
                            s.op("act", lambda e, bs=bs, m=m, pk=pk, bcol=bcol: e.activation(
                                PT[m][pk][:], ps[bs][:, 0:256], AF.Exp, bias=rowp_sb[:, bcol:bcol + 1], scale=0.125),
                                reads=[Rps[bs], Rrowp], writes=[RPT[m][pk]])

                            def pv(m=m, pk=pk, kt=kt, h=h, acc=acc):
                                for qt in range(2):
                                    s.op("pe", lambda e, qt=qt: e.matmul(
                                        ps[acc[m]][:, qt * 129:qt * 129 + 129], PT[m][pk][:, qt * 128:(qt + 1) * 128],
                                        Vaug[:, kt, h, 0:129], start=(kt == 0 and qt == 0), stop=(kt == 11),
                                        skip_group_check=True),
                                        reads=[RPT[m][pk], RV], writes=[Rps[acc[m]]])
                            pend.append(pv)
                            if len(pend) > 5:
                                pend.pop(0)()
                    while pend:
                        pend.pop(0)()
                    reserved.difference_update(acc)
                    for qt in range(2):
                        i = qu * 2 + qt
                        c0 = qt * 129
                        s.op("dve", lambda e, c0=c0, acc=acc: e.reciprocal(stat[:, 30:31], ps[acc[0]][:, c0 + 128:c0 + 129]),
                             reads=[Rps[acc[0]]], writes=[Rstat])
                        s.op("dve", lambda e, c0=c0, acc=acc: e.reciprocal(stat[:, 31:32], ps[acc[1]][:, c0 + 128:c0 + 129]),
                             reads=[Rps[acc[1]]], writes=[Rstat])
                        s.op("dve", lambda e: e.tensor_tensor(stat[:, 32:33], stat[:, 31:32], stat[:, 25:26], ALU.mult),
                             reads=[Rstat], writes=[Rstat])
                        s.op("dve", lambda e, c0=c0, acc=acc: e.tensor_scalar(ya_f[:], ps[acc[0]][:, c0:c0 + 128], stat[:, 30:31], None, ALU.mult),
                             reads=[Rps[acc[0]], Rstat], writes=[Rya])
                        s.op("dve", lambda e, c0=c0, acc=acc: e.scalar_tensor_tensor(ya_f[:], ps[acc[1]][:, c0:c0 + 128], stat[:, 32:33], ya_f[:], ALU.mult, ALU.add),
                             reads=[Rps[acc[1]], Rstat, Rya], writes=[Rya])
                        s.op("act", lambda e: e.activation(junk_b[:, 0:128], ya_f[:], AF.Square, scale=1.0 / math.sqrt(128.0),
                                                           accum_out=stat[:, 33:34]),
                             reads=[Rya], writes=[Rjunk, Rstat])
                        s.op("dve", lambda e: e.tensor_scalar(stat[:, 34:35], stat[:, 33:34], 1e-6, None, ALU.add),
                             reads=[Rstat], writes=[Rstat])
                        s.op("act", lambda e: e.activation(stat[:, 34:35], stat[:, 34:35], AF.Ln), reads=[Rstat], writes=[Rstat])
                        s.op("act", lambda e: e.activation(stat[:, 35:36], stat[:, 34:35], AF.Exp, scale=-0.5), reads=[Rstat], writes=[Rstat])
                        s.op("dve", lambda e: e.scalar_tensor_tensor(ya_b[:], ya_f[:], stat[:, 35:36], subln08[:], ALU.mult, ALU.mult),
                             reads=[Rya, Rstat, Rsub], writes=[Ryab])
                        reserved.update(acc)
                        bt = bank()
                        reserved.difference_update(acc)
                        s.op("pe", lambda e, bt=bt: e.transpose(psb[bt][:, 0:128], ya_b[:], ident_b[:]),
                             reads=[Ryab, Rconst], writes=[Rps[bt]])
                        s.op("act", lambda e, bt=bt, h=h, i=i: e.copy(yT[:, h, i * 128:(i + 1) * 128], psb[bt][:, 0:128]),
                             reads=[Rps[bt]], writes=[RyT])

        def out_proj(W, actTv, Ract):
            pieces = [wload(W[:, hh * 512:(hh + 1) * 512], KC, 512) for hh in range(2)]
            for i in range(NT):
                bb = [bank(), bank()]
                for hh in range(2):
                    linear_tm(pieces[hh][0], pieces[hh][1], 512, actTv, Ract, KC, i, bb[hh])
                residual_update(i, bb, 0)

        Xc = carve(5120, NT * 8 * 256, BF16).rearrange("p (i g n) -> p i g n", i=NT, g=8)
        RXc = ares("Xc")
        chCS_b = sb("chCS_b", [128, 256], BF16)
        Rch = Res("chCS")

        def fnet():
            s.dma("pool", chCS_b[:], chCS, writes=[Rch])
            for i in range(NT):
                for gp in range(4):
                    b = bank()
                    for g2_ in range(2):
                        g = gp * 2 + g2_
                        s.op("pe", lambda e, b=b, g=g, g2_=g2_, i=i: e.matmul(
                            ps[b][:, g2_ * 256:(g2_ + 1) * 256], hT[:, g, i * 128:(i + 1) * 128], chCS_b[:], start=True, stop=True),
                            reads=[RhT, Rch], writes=[Rps[b]])
                    s.op("act", lambda e, b=b, gp=gp, i=i: e.copy(
                        Xc[:, i, gp * 2:gp * 2 + 2, :], ps[b][:].rearrange("p (g n) -> p g n", g=2)),
                        reads=[Rps[b]], writes=[RXc])
            for (t0, tn) in [(0, 384), (384, 384), (768, 384), (1152, 128)]:
                wc, rc = wload(dftC[:, t0:t0 + tn], NT, tn)
                wsn, rsn = wload(dftSn[:, t0:t0 + tn], NT, tn)
                for g in range(8):
                    b = bank()
                    for tc in range(NT):
                        s.op("pe", lambda e, b=b, g=g, tc=tc, tn=tn, wc=wc: e.matmul(
                            ps[b][:, 0:tn], Xc[:, tc, g, 0:128], wc[:, tc, 0:tn], start=(tc == 0), stop=False),
                            reads=[RXc, rc], writes=[Rps[b]])
                        s.op("pe", lambda e, b=b, g=g, tc=tc, tn=tn, wsn=wsn: e.matmul(
                            ps[b][:, 0:tn], Xc[:, tc, g, 128:256], wsn[:, tc, 0:tn], start=False, stop=(tc == NT - 1)),
                            reads=[RXc, rsn], writes=[Rps[b]])
                    s.op("act", lambda e, b=b, g=g, t0=t0, tn=tn: e.copy(yT[:, g, t0:t0 + tn], ps[b][:, 0:tn]),
                         reads=[Rps[b]], writes=[RyT])


        V_tm = carve(5120, NT * 512, BF16).rearrange("p (i n) -> p i n", i=NT)
        RVtm = ares("V_tm")
        YSb = carve(7680, NT * 512, BF16).rearrange("p (i n) -> p i n", i=NT)
        RYS = ares("YSb")
        Ystage = carve(10240, 128 * 16, F32)[0:64, :].rearrange("p (n k) -> p n k", k=16)
        RYst = ares("Ystage")
        Sst = [carve(12288, 512, F32), carve(12800, 512, F32)]
        RS = [ares("S0"), ares("S1")]
        tmpA = carve(13312, 512, F32)
        RtA = ares("tmpA")
        tmpB = carve(23824, 512, F32)
        RtB = ares("tmpB")
        w2t_b = carve(24336, 512, BF16)
        a2t_b = carve(24592, 512, BF16)
        g2_b = carve(24848, 512, BF16)
        Rlora = ares("lora")
        PT0 = 25104
        NPT = 12

        def pt(k, n=1):
            return carve(PT0 + 128 * k, 128 * n, F32)
        Rpt = [ares("pt%d" % k) for k in range(NPT)]
        lnx0_b = carve(26640, 512, BF16)
        lnx1_b = carve(26896, 512, BF16)
        Rlnx = ares("lnx")
        tb0 = junk_b[:, 0:128]
        tb1 = junk_b[:, 128:256]
        Rtb0 = Res("tb0")
        Rtb1 = Res("tb1")
        colx = sb("colx", [128, 64])
        Rcolx = Res("colx")
        tiny = sb("tiny", [128, 8])
        Rtiny = Res("tiny")
        BS = sb("BS", [128, NT, 8])
        RBS = Res("BS")
        wbf = [w[:].bitcast(F32) for w in wbuf]
        TW = wbf[0][:, 0:1024].rearrange("p (k n) -> p k n", k=8)
        TKK = wbf[0][:, 1024:2048].rearrange("p (k n) -> p k n", k=8)
        TNK = wbf[1][:, 0:1024].rearrange("p (k n) -> p k n", k=8)
        TKD = wbf[1][:, 1024:2048].rearrange("p (k n) -> p k n", k=8)
        TR2 = wbuf[2][:, 0:2048].rearrange("p (k n h) -> p k n h", k=8, h=2)
        tmpA_b = carve(10240, 512, BF16)
        RtAb = ares("tmpA_b")
        Sb16 = carve(10496, 512, BF16)
        RSb = ares("Sb16")
        CX = dict(cmu=0, nmu0=15, nmu1=30, omk1=45, keepL=49, keepR=54)

        def cx(name, c=0):
            o = CX[name] + c
            return colx[:, o:o + 1]

        def rwkv_setup():
            s.dma("pool", w2t_b, w2t_d, writes=[Rlora])
            s.dma("pool", a2t_b, a2t_d, writes=[Rlora])
            s.dma("pool", g2_b, g2_d, writes=[Rlora])
            s.dma("pool", lnx0_b, lnx_d[0, :].partition_broadcast(128), writes=[Rlnx])
            s.dma("pool", lnx1_b, lnx_d[1, :].partition_broadcast(128), writes=[Rlnx])
            mu0 = cp("mu0", 0, 15)
            mu1 = cp("mu1", 0, 15)
            s.op("dve", lambda e: e.tensor_tensor(colx[:, 0:15], mu0, mu1, ALU.add), reads=[Rcolp], writes=[Rcolx])
            s.op("dve", lambda e: e.tensor_scalar(colx[:, 0:15], colx[:, 0:15], -1.0, 1.0, ALU.mult, ALU.add),
                 reads=[Rcolx], writes=[Rcolx])
            s.op("dve", lambda e: e.tensor_scalar(colx[:, 15:30], mu0, -1.0, None, ALU.mult), reads=[Rcolp], writes=[Rcolx])
            s.op("dve", lambda e: e.tensor_scalar(colx[:, 30:45], mu1, -1.0, None, ALU.mult), reads=[Rcolp], writes=[Rcolx])
            s.op("dve", lambda e: e.tensor_scalar(colx[:, 45:49], cp("kvec1", 0, 4), -1.0, 1.0, ALU.mult, ALU.add),
                 reads=[Rcolp], writes=[Rcolx])
            s.op("dve", lambda e: e.tensor_scalar(colx[:, 49:54], rowp_sb[:, ROFF["bndL"]:ROFF["bndL"] + 5], -1.0, 1.0, ALU.mult, ALU.add),
                 reads=[Rrowp], writes=[Rcolx])
            s.op("dve", lambda e: e.tensor_scalar(colx[:, 54:59], rowp_sb[:, ROFF["bndR"]:ROFF["bndR"] + 5], -1.0, 1.0, ALU.mult, ALU.add),
                 reads=[Rrowp], writes=[Rcolx])
            s.op("dve", lambda e: e.memset(BS[:], 0.0), writes=[RBS])

        def shift(ch, ti, out, Rout):
            t0 = ti * 128
            f = fbT[:, ch, 1 + t0:1 + t0 + 128]
            fp = fbT[:, ch, t0:t0 + 128]
            fn = fbT[:, ch, 2 + t0:2 + t0 + 128]
            rr = [RfbT, Rcolp, Rcolx]
            s.op("dve", lambda e: e.tensor_scalar(out, f, cx("cmu", ch), None, ALU.mult), reads=rr, writes=[Rout])
            s.op("dve", lambda e: e.scalar_tensor_tensor(out, fp, cp("mu0", ch), out, ALU.mult, ALU.add),
                 reads=rr + [Rout], writes=[Rout])
            s.op("dve", lambda e: e.scalar_tensor_tensor(out, fn, cp("mu1", ch), out, ALU.mult, ALU.add),
                 reads=rr + [Rout], writes=[Rout])
            u = ti // 2
            if ti % 2 == 0:
                bl = rowp_sb[:, ROFF["bndL"] + u:ROFF["bndL"] + u + 1]
                s.op("dve", lambda e: e.tensor_tensor(tiny[:, 0:1], fbT[:, ch, t0:t0 + 1], bl, ALU.mult),
                     reads=[RfbT, Rrowp], writes=[Rtiny])
                s.op("dve", lambda e: e.scalar_tensor_tensor(out[:, 0:1], tiny[:, 0:1], cx("nmu0", ch), out[:, 0:1], ALU.mult, ALU.add),
                     reads=[Rtiny, Rcolx, Rout], writes=[Rout])
            else:
                br = rowp_sb[:, ROFF["bndR"] + u:ROFF["bndR"] + u + 1]
                s.op("dve", lambda e: e.tensor_tensor(tiny[:, 1:2], fbT[:, ch, 1 + t0 + 128:2 + t0 + 128], br, ALU.mult),
                     reads=[RfbT, Rrowp], writes=[Rtiny])
                s.op("dve", lambda e: e.scalar_tensor_tensor(out[:, 127:128], tiny[:, 1:2], cx("nmu1", ch), out[:, 127:128], ALU.mult, ALU.add),
                     reads=[Rtiny, Rcolx, Rout], writes=[Rout])

        def rwkv_prepass():
            for ti in range(NT):
                for c in range(4):
                    shift(8 + c, ti, pt(c), Rpt[c])
                    s.op("act", lambda e, c=c: e.copy(xn_b[:, c * 128:(c + 1) * 128], pt(c)), reads=[Rpt[c]], writes=[Rxn])
                b = bank()
                for c in range(4):
                    s.op("pe", lambda e, b=b, c=c: e.transpose(psb[b][:, c * 128:(c + 1) * 128], xn_b[:, c * 128:(c + 1) * 128], ident_b[:]),
                         reads=[Rxn, Rconst], writes=[Rps[b]])
                s.op("act", lambda e, b=b, ti=ti: e.copy(V_tm[:, ti, :], psb[b][:, 0:512]), reads=[Rps[b]], writes=[RVtm])

        def prep(d, ti):
            rev = (d == 1)

            def tab(T3, c):
                v = T3[:, d * 4 + c, :]
                return v[:, ::-1] if rev else v
            tabs_w = [Rw[0], Rw[1], Rw[2]]
            shift(12, ti, pt(0), Rpt[0])
            shift(13, ti, pt(1), Rpt[1])
            s.op("act", lambda e: e.activation(tb0, pt(0), AF.Tanh), reads=[Rpt[0]], writes=[Rtb0])
            s.op("act", lambda e: e.copy(tb1, pt(1)), reads=[Rpt[1]], writes=[Rtb1])
            lo, hi = 64 * d, 64 * d + 64
            for c in range(4):
                b = bank()
                s.op("pe", lambda e, b=b, c=c: e.matmul(ps[b][:, 0:128], w2t_b[lo:hi, c * 128:(c + 1) * 128], tb0[lo:hi, :], start=True, stop=True),
                     reads=[Rlora, Rtb0], writes=[Rps[b]])
                s.op("act", lambda e, b=b, c=c: e.activation(pt(2), ps[b][:, 0:128], AF.Sigmoid, bias=cp("w0_%d" % d, c)),
                     reads=[Rps[b], Rcolp], writes=[Rpt[2]])
                s.op("act", lambda e, c=c: e.activation(tab(TW, c), pt(2), AF.Exp, scale=-EXPM05),
                     reads=[Rpt[2]], writes=[Rw[0]])
                b = bank()
                s.op("pe", lambda e, b=b, c=c: e.matmul(ps[b][:, 0:128], a2t_b[lo:hi, c * 128:(c + 1) * 128], tb1[lo:hi, :], start=True, stop=True),
                     reads=[Rlora, Rtb1], writes=[Rps[b]])
                s.op("act", lambda e, b=b, c=c: e.activation(pt(3), ps[b][:, 0:128], AF.Sigmoid, bias=cp("a0_%d" % d, c)),
                     reads=[Rps[b], Rcolp], writes=[Rpt[3]])
                shift(4 + c, ti, pt(4), Rpt[4])
                s.op("dve", lambda e, c=c: e.tensor_scalar(pt(5), pt(4), cp("kvec0", c), None, ALU.mult),
                     reads=[Rpt[4], Rcolp], writes=[Rpt[5]])
                s.op("act", lambda e: e.activation(pt(6), pt(5), AF.Square), reads=[Rpt[5]], writes=[Rpt[6]])
                b = bank()
                s.op("pe", lambda e, b=b: e.matmul(ps[b][:, 0:128], bones_f[:], pt(6), start=True, stop=True),
                     reads=[Rconst, Rpt[6]], writes=[Rps[b]])
                s.op("dve", lambda e, b=b: e.tensor_scalar(pt(7), ps[b][:, 0:128], 1e-12, None, ALU.add),
                     reads=[Rps[b]], writes=[Rpt[7]])
                s.op("act", lambda e: e.activation(pt(7), pt(7), AF.Sqrt), reads=[Rpt[7]], writes=[Rpt[7]])
                s.op("dve", lambda e: e.reciprocal(pt(7), pt(7)), reads=[Rpt[7]], writes=[Rpt[7]])
                s.op("dve", lambda e: e.tensor_tensor(pt(8), pt(5), pt(7), ALU.mult), reads=[Rpt[5], Rpt[7]], writes=[Rpt[8]])
                s.op("act", lambda e, c=c: e.copy(tab(TKK, c), pt(8)), reads=[Rpt[8]], writes=[Rw[0]])
                s.op("dve", lambda e, c=c: e.scalar_tensor_tensor(tab(TNK, c), pt(8), -1.0, pt(3), ALU.mult, ALU.mult),
                     reads=[Rpt[8], Rpt[3]], writes=[Rw[1]])
                s.op("dve", lambda e, c=c: e.tensor_scalar(pt(9), pt(3), cp("kvec1", c), cx("omk1", c), ALU.mult, ALU.add),
                     reads=[Rpt[3], Rcolp, Rcolx], writes=[Rpt[9]])
                s.op("dve", lambda e: e.tensor_tensor(pt(9), pt(4), pt(9), ALU.mult), reads=[Rpt[4], Rpt[9]], writes=[Rpt[9]])
                s.op("act", lambda e, c=c: e.copy(tab(TKD, c), pt(9)), reads=[Rpt[9]], writes=[Rw[1]])
                shift(c, ti, pt(10), Rpt[10])
                for h2 in range(2):
                    o = TR2[:, d * 4 + c, :, h2]
                    if rev:
                        o = o[:, ::-1]
                    s.op("dve", lambda e, o=o, h2=h2: e.tensor_scalar(o, pt(10), halfsel_f[:, h2:h2 + 1], None, ALU.mult),
                         reads=[Rpt[10], Rconst], writes=[Rw[2]])
                s.op("dve", lambda e: e.tensor_tensor(pt(6), pt(10), pt(9), ALU.mult), reads=[Rpt[10], Rpt[9]], writes=[Rpt[6]])
                s.op("dve", lambda e, c=c: e.tensor_scalar(pt(6), pt(6), cp("kvec2", c), None, ALU.mult),
                     reads=[Rpt[6], Rcolp], writes=[Rpt[6]])
                b = bank()
                s.op("pe", lambda e, b=b: e.matmul(ps[b][:, 0:2], pt(6), halfsel_f[:, 0:2], start=True, stop=True),
                     reads=[Rpt[6], Rconst], writes=[Rps[b]])
                s.op("dve", lambda e, b=b, c=c, ti=ti: e.tensor_tensor(BS[:, ti, c * 2:c * 2 + 2], BS[:, ti, c * 2:c * 2 + 2], ps[b][:, 0:2], ALU.add),
                     reads=[Rps[b], RBS], writes=[RBS])

        yb = xn_b[:, 0:512]

        def finalize(ti, by):
            ys = pt(0, 4)
            Rys = [Rpt[0], Rpt[1], Rpt[2], Rpt[3]]
            t2 = pt(4, 4)
            Rt2 = [Rpt[4], Rpt[5], Rpt[6], Rpt[7]]
            ys3 = ys.rearrange("p (h i) -> p h i", h=8)
            t23 = t2.rearrange("p (h i) -> p h i", h=8)
            s.op("dve", lambda e: e.tensor_tensor(ys, YSb[:, ti, :], ps[by][:], ALU.add), reads=[RYS, Rps[by]], writes=Rys)
            s.op("dve", lambda e: e.reduce_sum(tiny[:, 0:8], ys3, axis=AX.X), reads=Rys, writes=[Rtiny])
            s.op("dve", lambda e: e.tensor_scalar(tiny[:, 0:8], tiny[:, 0:8], -1.0 / 64.0, None, ALU.mult), reads=[Rtiny], writes=[Rtiny])
            s.op("dve", lambda e: e.tensor_tensor(ys3, ys3, tiny[:, 0:8].unsqueeze(2).to_broadcast([128, 8, 64]), ALU.add),
                 reads=Rys + [Rtiny], writes=Rys)
            s.op("dve", lambda e: e.tensor_tensor(t2, ys, ys, ALU.mult), reads=Rys, writes=Rt2)
            s.op("dve", lambda e: e.reduce_sum(stat[:, 40:48], t23, axis=AX.X), reads=Rt2, writes=[Rstat])
            s.op("dve", lambda e: e.tensor_scalar(stat[:, 40:48], stat[:, 40:48], 1.0 / 64.0, 64e-5, ALU.mult, ALU.add),
                 reads=[Rstat], writes=[Rstat])
            s.op("act", lambda e: e.activation(stat[:, 40:48], stat[:, 40:48], AF.Ln), reads=[Rstat], writes=[Rstat])
            s.op("act", lambda e: e.activation(stat[:, 48:56], stat[:, 40:48], AF.Exp, scale=-0.5), reads=[Rstat], writes=[Rstat])
            s.op("dve", lambda e: e.tensor_tensor(ys3, ys3, stat[:, 48:56].unsqueeze(2).to_broadcast([128, 8, 64]), ALU.mult),
                 reads=Rys + [Rstat], writes=Rys)
            s.op("dve", lambda e: e.tensor_tensor(ys, ys, lnx0_b, ALU.mult), reads=Rys + [Rlnx], writes=Rys)
            s.op("dve", lambda e: e.tensor_tensor(ys, ys, lnx1_b, ALU.add), reads=Rys + [Rlnx], writes=Rys)
            s.op("dve", lambda e: e.tensor_tensor(t23, V_tm[:, ti, :].rearrange("p (h i) -> p h i", h=8),
                                                  BS[:, ti, :].unsqueeze(2).to_broadcast([128, 8, 64]), ALU.mult),
                 reads=[RVtm, RBS], writes=Rt2)
            s.op("dve", lambda e: e.tensor_tensor(ys, ys, t2, ALU.add), reads=Rys + Rt2, writes=Rys)
            shift(14, ti, pt(8), Rpt[8])
            s.op("act", lambda e: e.activation(tb0, pt(8), AF.Sigmoid), reads=[Rpt[8]], writes=[Rtb0])
            bg = bank()
            s.op("pe", lambda e, bg=bg: e.matmul(ps[bg][:], tb0, g2_b, start=True, stop=True),
                 reads=[Rtb0, Rlora], writes=[Rps[bg]])
            s.op("dve", lambda e, bg=bg: e.tensor_tensor(yb, ys, ps[bg][:], ALU.mult), reads=Rys + [Rps[bg]], writes=[Rxn])
            bt = bank()
            for c in range(4):
                s.op("pe", lambda e, bt=bt, c=c: e.transpose(psb[bt][:, c * 128:(c + 1) * 128], yb[:, c * 128:(c + 1) * 128], ident_b[:]),
                     reads=[Rxn, Rconst], writes=[Rps[bt]])
            s.op("act", lambda e, bt=bt, ti=ti: e.copy(yT[:, 4:8, ti * 128:(ti + 1) * 128],
                                                        psb[bt][:, 0:512].rearrange("p (c t) -> p c t", c=4)),
                 reads=[Rps[bt]], writes=[RyT])

        def rwkv_scan(nrounds=NT):
            S8 = [x_.rearrange("p (k i) -> p k i", k=8) for x_ in Sst]
            tA8 = tmpA.rearrange("p (k i) -> p k i", k=8)
            tB8 = tmpB.rearrange("p (k i) -> p k i", k=8)

            def bc(T3, n):
                return T3[:, :, n].unsqueeze(2).to_broadcast([128, 8, 64])
            for r in range(nrounds):
                tf, tbk = r, NT - 1 - r
                prep(0, tf)
                prep(1, tbk)
                if tf % 2 == 0:
                    u = tf // 2
                    if u == 0:
                        s.dma("sp", Sst[0][:, 0:256], initS[0], writes=[RS[0]])
                    else:
                        s.op("dve", lambda e, u=u: e.tensor_scalar(Sst[0][:, 0:256], Sst[0][:, 0:256], cx("keepL", u), None, ALU.mult),
                             reads=[RS[0], Rcolx], writes=[RS[0]])
                if tbk % 2 == 1:
                    u = tbk // 2
                    if u == 4:
                        s.op("dve", lambda e: e.memset(Sst[0][:, 256:512], 0.0), writes=[RS[0]])
                    elif u == 3:
                        s.dma("sp", tmpB[:, 0:256], initS[1], writes=[RtB])
                        s.op("dve", lambda e, u=u: e.scalar_tensor_tensor(Sst[0][:, 256:512], Sst[0][:, 256:512], cx("keepR", u),
                                                                          tmpB[:, 0:256], ALU.mult, ALU.add),
                             reads=[RS[0], Rcolx, RtB], writes=[RS[0]])
                    else:
                        s.op("dve", lambda e, u=u: e.tensor_scalar(Sst[0][:, 256:512], Sst[0][:, 256:512], cx("keepR", u), None, ALU.mult),
                             reads=[RS[0], Rcolx], writes=[RS[0]])
                by_ = None
                pending = []

                def flush():
                    for f_ in pending:
                        f_()
                    del pending[:]
                for n in range(128):
                    ci, ni = n % 2, (n + 1) % 2
                    Sc, Sn = S8[ci], S8[ni]
                    if n % 32 == 0:
                        by_ = bank()
                        reserved.add(by_)
                    tAb8 = tmpA_b.rearrange("p (k i) -> p k i", k=8)
                    s.op("dve", lambda e, n=n, Sc=Sc, tAb8=tAb8: e.tensor_tensor(tAb8, Sc, bc(TKK, n), ALU.mult),
                         reads=[RS[ci], Rw[0]], writes=[RtAb])
                    s.op("dve", lambda e, n=n, Sc=Sc, Sn=Sn: e.tensor_tensor(Sn, Sc, bc(TW, n), ALU.mult),
                         reads=[RS[ci], Rw[0]], writes=[RS[ni]])
                    bv = bank()
                    for d in range(2):
                        tile_d = tf if d == 0 else tbk
                        row = n if d == 0 else 127 - n
                        for h2 in range(2):
                            s.op("pe", lambda e, bv=bv, d=d, h2=h2, tile_d=tile_d, row=row: e.matmul(
                                ps[bv][64 * h2:64 * h2 + 64, d * 256:(d + 1) * 256].rearrange("p (c i) -> p c i", c=4),
                                ident_b[:, row:row + 1].to_broadcast([128, 64]),
                                V_tm[:, tile_d, :].rearrange("p (c h i) -> p c h i", c=4, h=2)[:, :, h2, :],
                                start=True, stop=True),
                                reads=[RVtm, Rconst], writes=[Rps[bv]])
                    bs_ = bank()
                    s.op("pe", lambda e, bs_=bs_: e.matmul(ps[bs_][:], bones_b[:], tmpA_b, start=True, stop=True),
                         reads=[RtAb, Rconst], writes=[Rps[bs_]])
                    flush()
                    s.op("dve", lambda e, n=n, bv=bv: e.tensor_tensor(tB8, ps[bv][:].rearrange("p (k i) -> p k i", k=8), bc(TKD, n), ALU.mult),
                         reads=[Rps[bv], Rw[1]], writes=[RtB])
                    s.op("dve", lambda e, ni=ni: e.tensor_tensor(Sst[ni], Sst[ni], tmpB, ALU.add),
                         reads=[RS[ni], RtB], writes=[RS[ni]])
                    s.op("dve", lambda e, n=n, bs_=bs_: e.tensor_tensor(tA8, ps[bs_][:].rearrange("p (k i) -> p k i", k=8), bc(TNK, n), ALU.mult),
                         reads=[Rps[bs_], Rw[1]], writes=[RtA])
                    s.op("dve", lambda e, ni=ni: e.tensor_tensor(Sst[ni], Sst[ni], tmpA, ALU.add),
                         reads=[RS[ni], RtA], writes=[RS[ni]])
                    s.op("act", lambda e, ni=ni: e.copy(Sb16, Sst[ni]), reads=[RS[ni]], writes=[RSb])
                    nl = n % 32

                    def ymm(by_=by_, nl=nl, n=n):
                        for dc in range(8):
                            s.op("pe", lambda e, dc=dc: e.matmul(
                                ps[by_][0:64, nl * 16 + dc * 2:nl * 16 + dc * 2 + 2], Sb16[:, dc * 64:(dc + 1) * 64], TR2[:, dc, n, :],
                                start=True, stop=True),
                                reads=[RSb, Rw[2]], writes=[Rps[by_]])
                    pending.append(ymm)
                    if nl == 31:
                        flush()
                        n0 = n - 31
                        src = ps[by_][0:64, :].rearrange("p (n k) -> p n k", k=16)
                        s.op("act", lambda e, src=src, n0=n0: e.copy(Ystage[:, n0:n0 + 32, 0:8], src[:, :, 0:8]),
                             reads=[Rps[by_]], writes=[RYst])
                        s.op("act", lambda e, src=src, n0=n0: e.copy(Ystage[:, 96 - n0:128 - n0, 8:16][:, ::-1, :], src[:, :, 8:16]),
                             reads=[Rps[by_]], writes=[RYst])
                        reserved.discard(by_)
                if tf % 2 == 1:
                    s.dma("sp", s_out[tf // 2, 0], Sst[0][:, 0:256], reads=[RS[0]])
                if tbk % 2 == 0:
                    s.dma("sp", s_out[tbk // 2, 1], Sst[0][:, 256:512], reads=[RS[0]])
                for d in range(2):
                    tile_d = tf if d == 0 else tbk
                    byy = bank()
                    for k8 in range(8):
                        s.op("pe", lambda e, byy=byy, k8=k8, d=d: e.matmul(
                            ps[byy][:, k8 * 64:(k8 + 1) * 64], Ystage[:, :, d * 8 + k8], ident_f[0:64, 0:64], start=True, stop=True),
                            reads=[RYst, Rconst], writes=[Rps[byy]])
                    first = (r < 5)
                    if first:
                        s.op("act", lambda e, byy=byy, tile_d=tile_d: e.copy(YSb[:, tile_d, :], ps[byy][:]),
                             reads=[Rps[byy]], writes=[RYS])
                    else:
                        finalize(tile_d, byy)


        def wbw(i, off_w, nelem):
            return wbuf[i][:, 2 * off_w:2 * off_w + nelem]
        Ast = carve(13200, 512, F32)
        RAst = ares("Ast")
        Abf = carve(13712, 512, BF16)
        RAbf = ares("Abf")
        WS = []
        fB = 14208 + 8 * 641
        for wsi in range(2):
            d_ = {}
            if wsi == 0:
                for k_, nm in enumerate(["Lt0", "Lt1", "L0", "L1", "Z0"]):
                    d_[nm] = carve(10240 + 512 * k_, 512, F32)
                d_["MT"] = carve(12800, 256, BF16)
                d_["RT"] = carve(12928, 512, BF16)
                d_["Z1"] = wbuf[0][:, 0:1024].bitcast(F32)
                for k_, nm in enumerate(["Lakt", "Mrbt", "Mrkt", "BKtm"]):
                    d_[nm] = wbw(0, 512 + 256 * k_, 512)
                d_["Zfb"] = carve(10240, 512, BF16)
            else:
                for k_, nm in enumerate(["Lt0", "Lt1", "L0", "L1", "Z0"]):
                    d_[nm] = carve(fB + 512 * k_, 512, F32)
                d_["Z1"] = wbuf[1][:, 0:1024].bitcast(F32)
                d_["Lakt"] = wbw(0, 1536, 512)
                d_["Mrbt"] = wbw(0, 1792, 512)
                d_["Mrkt"] = wbw(1, 512, 512)
                d_["MT"] = wbw(1, 768, 256)
                d_["BKtm"] = carve(23824, 512, BF16)
                d_["RT"] = carve(24080, 512, BF16)
                d_["Zfb"] = carve(fB, 512, BF16)
            for nm in ["Lt0", "Lt1", "L0", "L1", "Z0", "Z1", "Lakt", "Mrbt", "Mrkt", "BKtm", "MT", "RT"]:
                d_["R" + nm] = ares("ws%d_%s" % (wsi, nm))
            d_["RZfb"] = d_["RLt0"]
            WS.append(d_)
        masks = wbw(1, 896, 4 * 4 * 128).rearrange("p (m q t) -> p m q t", m=4, q=4)
        Rmask = ares("masks")
        TBL = []
        for par in range(2):
            d_ = {}
            for k_, nm in enumerate(["RH", "AH", "BH", "KH"]):
                d_[nm] = wbw(2, par * 1024 + 256 * k_, 512).rearrange("p (c t) -> p c t", c=4)
                d_["R" + nm] = ares("tb%d_%s" % (par, nm))
            TBL.append(d_)
        pcs = sb("pcs", [128, 2, 4])
        Rpcs = [Res("pcs0"), Res("pcs1")]

        def prep_chunk(d, ti, par):
            TB = TBL[par]
            shift(12, ti, pt(0), Rpt[0])
            shift(13, ti, pt(1), Rpt[1])
            s.op("act", lambda e: e.activation(tb0, pt(0), AF.Tanh), reads=[Rpt[0]], writes=[Rtb0])
            s.op("act", lambda e: e.copy(tb1, pt(1)), reads=[Rpt[1]], writes=[Rtb1])
            lo, hi = 64 * d, 64 * d + 64
            for c in range(4):
                b = bank()
                s.op("pe", lambda e, b=b, c=c: e.matmul(ps[b][:, 0:128], w2t_b[lo:hi, c * 128:(c + 1) * 128], tb0[lo:hi, :], start=True, stop=True),
                     reads=[Rlora, Rtb0], writes=[Rps[b]])
                s.op("act", lambda e, b=b, c=c: e.activation(pt(2), ps[b][:, 0:128], AF.Sigmoid, bias=cp("w0_%d" % d, c)),
                     reads=[Rps[b], Rcolp], writes=[Rpt[2]])
                b = bank()
                s.op("pe", lambda e, b=b, c=c: e.matmul(ps[b][:, 0:128], a2t_b[lo:hi, c * 128:(c + 1) * 128], tb1[lo:hi, :], start=True, stop=True),
                     reads=[Rlora, Rtb1], writes=[Rps[b]])
                s.op("act", lambda e, b=b, c=c: e.activation(pt(3), ps[b][:, 0:128], AF.Sigmoid, bias=cp("a0_%d" % d, c)),
                     reads=[Rps[b], Rcolp], writes=[Rpt[3]])
                s.op("dve", lambda e: e.tensor_scalar(pt(2), pt(2), -EXPM05, None, ALU.mult), reads=[Rpt[2]], writes=[Rpt[2]])
                if d == 0:
                    s.op("dve", lambda e: e.tensor_tensor_scan(pt(0), pt(11), pt(2), 0.0, ALU.mult, ALU.add),
                         reads=[Rpt[2], Rpt[11]], writes=[Rpt[0]])
                else:
                    s.op("dve", lambda e: e.tensor_tensor_scan(pt(0)[:, ::-1], pt(11), pt(2)[:, ::-1], 0.0, ALU.mult, ALU.add),
                         reads=[Rpt[2], Rpt[11]], writes=[Rpt[0]])
                s.op("act", lambda e: e.activation(pt(1), pt(0), AF.Exp), reads=[Rpt[0]], writes=[Rpt[1]])
                s.op("dve", lambda e: e.tensor_tensor(pt(2), pt(0), pt(2), ALU.subtract), reads=[Rpt[0], Rpt[2]], writes=[Rpt[2]])
                s.op("act", lambda e: e.activation(pt(2), pt(2), AF.Exp), reads=[Rpt[2]], writes=[Rpt[2]])
                s.op("act", lambda e: e.activation(pt(0), pt(0), AF.Exp, scale=-1.0), reads=[Rpt[0]], writes=[Rpt[0]])
                yield
                pcol = 127 if d == 0 else 0
                s.op("dve", lambda e, c=c, pcol=pcol: e.tensor_copy(pcs[:, par, c:c + 1], pt(1)[:, pcol:pcol + 1]),
                     reads=[Rpt[1]], writes=[Rpcs[par]])
                yield
                shift(4 + c, ti, pt(4), Rpt[4])
                s.op("dve", lambda e, c=c: e.tensor_scalar(pt(5), pt(4), cp("kvec0", c), None, ALU.mult),
                     reads=[Rpt[4], Rcolp], writes=[Rpt[5]])
                s.op("act", lambda e: e.activation(pt(6), pt(5), AF.Square), reads=[Rpt[5]], writes=[Rpt[6]])
                b = bank()
                s.op("pe", lambda e, b=b: e.matmul(ps[b][:, 0:128], bones_f[:], pt(6), start=True, stop=True),
                     reads=[Rconst, Rpt[6]], writes=[Rps[b]])
                s.op("dve", lambda e, b=b: e.tensor_scalar(pt(7), ps[b][:, 0:128], 1e-12, None, ALU.add),
                     reads=[Rps[b]], writes=[Rpt[7]])
                s.op("act", lambda e: e.activation(pt(7), pt(7), AF.Ln), reads=[Rpt[7]], writes=[Rpt[7]])
                s.op("act", lambda e: e.activation(pt(7), pt(7), AF.Exp, scale=-0.5), reads=[Rpt[7]], writes=[Rpt[7]])
                s.op("dve", lambda e: e.tensor_tensor(pt(8), pt(5), pt(7), ALU.mult), reads=[Rpt[5], Rpt[7]], writes=[Rpt[8]])
                yield
                s.op("dve", lambda e, c=c: e.scalar_tensor_tensor(TB["AH"][:, c, :], pt(8), -1.0, pt(2), ALU.mult, ALU.mult),
                     reads=[Rpt[8], Rpt[2]], writes=[TB["RAH"]])
                s.op("dve", lambda e: e.tensor_tensor(pt(6), pt(8), pt(3), ALU.mult), reads=[Rpt[8], Rpt[3]], writes=[Rpt[6]])
                s.op("dve", lambda e, c=c: e.tensor_tensor(TB["BH"][:, c, :], pt(6), pt(0), ALU.mult),
                     reads=[Rpt[6], Rpt[0]], writes=[TB["RBH"]])
                yield
                s.op("dve", lambda e, c=c: e.tensor_scalar(pt(9), pt(3), cp("kvec1", c), cx("omk1", c), ALU.mult, ALU.add),
                     reads=[Rpt[3], Rcolp, Rcolx], writes=[Rpt[9]])
                s.op("dve", lambda e: e.tensor_tensor(pt(9), pt(4), pt(9), ALU.mult), reads=[Rpt[4], Rpt[9]], writes=[Rpt[9]])
                s.op("dve", lambda e, c=c: e.tensor_tensor(TB["KH"][:, c, :], pt(9), pt(0), ALU.mult),
                     reads=[Rpt[9], Rpt[0]], writes=[TB["RKH"]])
                yield
                shift(c, ti, pt(10), Rpt[10])
                s.op("dve", lambda e, c=c: e.tensor_tensor(TB["RH"][:, c, :], pt(10), pt(1), ALU.mult),
                     reads=[Rpt[10], Rpt[1]], writes=[TB["RRH"]])
                yield
                s.op("dve", lambda e: e.tensor_tensor(pt(6), pt(10), pt(9), ALU.mult), reads=[Rpt[10], Rpt[9]], writes=[Rpt[6]])
                s.op("dve", lambda e, c=c: e.tensor_scalar(pt(6), pt(6), cp("kvec2", c), None, ALU.mult),
                     reads=[Rpt[6], Rcolp], writes=[Rpt[6]])
                b = bank()
                s.op("pe", lambda e, b=b: e.matmul(ps[b][:, 0:2], pt(6), halfsel_f[:, 0:2], start=True, stop=True),
                     reads=[Rpt[6], Rconst], writes=[Rps[b]])
                s.op("dve", lambda e, b=b, c=c, ti=ti: e.tensor_tensor(BS[:, ti, c * 2:c * 2 + 2], BS[:, ti, c * 2:c * 2 + 2], ps[b][:, 0:2], ALU.add),
                     reads=[Rps[b], RBS], writes=[RBS])
                yield

        def rwkv_chunked(ntiles=NT):
            cut = int(os.environ.get("CUT", 99))
            for m in range(4):
                for q in range(4):
                    s.dma("pool", masks[:, m, q, :], cmat[4 + m], writes=[Rmask])
            s.op("dve", lambda e: e.memset(pt(11), 1.0), writes=[Rpt[11]])
            MSK = {0: (0, 1, 2), 1: (1, 0, 3)}
            seq = []
            for d in range(2):
                order = list(range(NT)) if d == 0 else list(range(NT - 1, -1, -1))
                seq += [(d, ti) for ti in order[:ntiles]]
            for _ in prep_chunk(seq[0][0], seq[0][1], 0):
                pass
            if True:
                def tile_body(n):
                    d, ti = seq[n]
                    par = n % 2
                    Ad = Ast[:, d * 256:(d + 1) * 256]
                    Ad3 = Ad.rearrange("p (c i) -> p c i", c=4)
                    Abd = Abf[:, d * 256:(d + 1) * 256]
                    Abd3 = Abd.rearrange("p (c i) -> p c i", c=4)
                    mTs, mS, mTi = MSK[d]
                    TB = TBL[par]
                    nxt = prep_chunk(seq[n + 1][0], seq[n + 1][1], (n + 1) % 2) if n + 1 < len(seq) else iter(())
                    u = ti // 2
                    isstart = (ti % 2 == 0) if d == 0 else (ti % 2 == 1)
                    if isstart:
                        if d == 0:
                            if u == 0:
                                s.dma("sp", Ad, initS[0], writes=[RAst])
                            else:
                                s.op("dve", lambda e, u=u: e.tensor_scalar(Ad, Ad, cx("keepL", u), None, ALU.mult),
                                     reads=[RAst, Rcolx], writes=[RAst])
                        else:
                            if u == 4:
                                s.op("dve", lambda e: e.memset(Ad, 0.0), writes=[RAst])
                            elif u == 3:
                                s.dma("sp", stage_f[:, 0, 0:256], initS[1], writes=[Rstage[0]])
                                s.op("dve", lambda e, u=u: e.scalar_tensor_tensor(Ad, Ad, cx("keepR", u), stage_f[:, 0, 0:256], ALU.mult, ALU.add),
                                     reads=[RAst, Rcolx, Rstage[0]], writes=[RAst])
                            else:
                                s.op("dve", lambda e, u=u: e.tensor_scalar(Ad, Ad, cx("keepR", u), None, ALU.mult),
                                     reads=[RAst, Rcolx], writes=[RAst])
                        s.op("act", lambda e: e.copy(Abd, Ad), reads=[RAst], writes=[RAbf])
                    if cut < 2:
                        return
                    fy = bank()
                    reserved.add(fy)
                    first_fy = [True]
                    def group_body(g):
                        W = WS[g]
                        heads = [(c_, g) for c_ in range(4)]

                        def fm(T3, c, h2):
                            return T3[64 * h2:64 * h2 + 64, c, :]

                        def q4(tl):
                            return tl.rearrange("p (q t) -> p q t", q=4)
                        specs = [("Lt0", "BH", "AH", mTs), ("L0", "AH", "BH", mS), ("Lakt", "KH", "AH", mTs),
                                 ("Mrbt", "BH", "RH", mTi), ("Mrkt", "KH", "RH", mTi)]
                        lim = int(os.environ.get("LIM", 99))
                        for (dst, la, rb, mk) in specs[:lim]:
                            b = bank()
                            for q, (c, h2) in enumerate(heads[:int(os.environ.get("LIMH", 4))]):
                                s.op("pe", lambda e, b=b, q=q, c=c, h2=h2, la=la, rb=rb: e.matmul(
                                    ps[b][:, q * 128:(q + 1) * 128], fm(TB[la], c, h2), fm(TB[rb], c, h2), start=True, stop=True),
                                    reads=[TB["R" + la], TB["R" + rb]], writes=[Rps[b]])
                            if os.environ.get("NOEV") == "1":
                                continue
                            if os.environ.get("NOEV") == "2":
                                s.op("dve", lambda e, b=b, dst=dst, mk=mk: e.tensor_copy(W[dst], ps[b][:]),
                                     reads=[Rps[b], Rmask], writes=[W["R" + dst]])
                                continue
                            s.op("dve", lambda e, b=b, dst=dst, mk=mk: e.tensor_tensor(
                                q4(W[dst]), ps[b][:].rearrange("p (q t) -> p q t", q=4), masks[:, mk, :, :], ALU.mult),
                                reads=[Rps[b], Rmask], writes=[W["R" + dst]])
                        yield
                        if cut < 3:
                            return
                        b = bank()
                        for q, (c, h2) in enumerate(heads):
                            idn = ident_b[64 * h2:64 * h2 + 64, 64 * h2:64 * h2 + 64]
                            for k_, nm in enumerate(["AH", "BH", "KH"]):
                                s.op("pe", lambda e, b=b, q=q, c=c, h2=h2, nm=nm, k_=k_, idn=idn: e.transpose(
                                    psb[b][:, k_ * 256 + q * 64:k_ * 256 + q * 64 + 64], fm(TB[nm], c, h2), idn),
                                    reads=[TB["R" + nm], Rconst], writes=[Rps[b]])
                        Z0q = q4(W["Z0"])
                        BKq = q4(W["BKtm"])
                        s.op("act", lambda e, b=b, Z0q=Z0q: e.copy(Z0q[:, :, 0:64], psb[b][:, 0:256].rearrange("p (q j) -> p q j", q=4)),
                             reads=[Rps[b]], writes=[W["RZ0"]])
                        s.op("act", lambda e, b=b, BKq=BKq: e.copy(BKq[:, :, 0:64], psb[b][:, 256:512].rearrange("p (q j) -> p q j", q=4)),
                             reads=[Rps[b]], writes=[W["RBKtm"]])
                        s.op("act", lambda e, b=b, BKq=BKq: e.copy(BKq[:, :, 64:128], psb[b][:, 512:768].rearrange("p (q j) -> p q j", q=4)),
                             reads=[Rps[b]], writes=[W["RBKtm"]])

                        def Vq(c, h2):
                            return V_tm[:, ti, (c * 2 + h2) * 64:(c * 2 + h2) * 64 + 64]
                        if cut < 4:
                            return
                        b = bank()
                        Lakq = q4(W["Lakt"])
                        for q, (c, h2) in enumerate(heads):
                            s.op("pe", lambda e, b=b, q=q, c=c, h2=h2: e.matmul(
                                ps[b][:, q * 64:(q + 1) * 64], Lakq[:, q, :], Vq(c, h2), start=True, stop=True),
                                reads=[W["RLakt"], RVtm], writes=[Rps[b]])
                        s.op("act", lambda e, b=b, Z0q=Z0q: e.copy(Z0q[:, :, 64:128], ps[b][:, 0:256].rearrange("p (q j) -> p q j", q=4)),
                             reads=[Rps[b]], writes=[W["RZ0"]])
                        yield
                        if cut < 5:
                            return
                        zc, lc = 0, 0
                        for k in range(7):
                            Zc, Zn = W["Z%d" % zc], W["Z%d" % (1 - zc)]
                            RZc, RZn = W["RZ%d" % zc], W["RZ%d" % (1 - zc)]
                            Ltc, Lc = W["Lt%d" % lc], W["L%d" % lc]
                            RLtc, RLc = W["RLt%d" % lc], W["RL%d" % lc]
                            b = bank()
                            for q in range(4):
                                s.op("pe", lambda e, b=b, q=q, Zc=Zc, Ltc=Ltc: e.matmul(
                                    ps[b][:, q * 128:(q + 1) * 128], q4(Ltc)[:, q, :], q4(Zc)[:, q, :], start=True, stop=True),
                                    reads=[RZc, RLtc], writes=[Rps[b]])
                            s.op("dve", lambda e, b=b, Zn=Zn, Zc=Zc: e.tensor_tensor(Zn, ps[b][:], Zc, ALU.add),
                                 reads=[Rps[b], RZc], writes=[RZn])
                            if k < 6:
                                Ltn, Ln = W["Lt%d" % (1 - lc)], W["L%d" % (1 - lc)]
                                RLtn, RLn = W["RLt%d" % (1 - lc)], W["RL%d" % (1 - lc)]
                                b2 = bank()
                                for q in range(4):
                                    s.op("pe", lambda e, b2=b2, q=q, Lc=Lc, Ltc=Ltc: e.matmul(
                                        ps[b2][:, q * 128:(q + 1) * 128], q4(Lc)[:, q, :], q4(Ltc)[:, q, :], start=True, stop=True),
                                        reads=[RLc, RLtc], writes=[Rps[b2]])
                                s.op("act", lambda e, b2=b2, Ltn=Ltn: e.copy(Ltn, ps[b2][:]), reads=[Rps[b2]], writes=[RLtn])
                                if k < 5:
                                    b3 = bank()
                                    for q in range(4):
                                        s.op("pe", lambda e, b3=b3, q=q, Lc=Lc, Ltc=Ltc: e.matmul(
                                            ps[b3][:, q * 128:(q + 1) * 128], q4(Ltc)[:, q, :], q4(Lc)[:, q, :], start=True, stop=True),
                                            reads=[RLc, RLtc], writes=[Rps[b3]])
                                    s.op("act", lambda e, b3=b3, Ln=Ln: e.copy(Ln, ps[b3][:]), reads=[Rps[b3]], writes=[RLn])
                                lc = 1 - lc
                            zc = 1 - zc
                            yield
                        s.op("act", lambda e, zc=zc: e.copy(W["Zfb"], W["Z%d" % zc]), reads=[W["RZ%d" % zc]], writes=[W["RZfb"]])
                        Zf = q4(W["Zfb"])
                        RZf = W["RZfb"]
                        Mrbq = q4(W["Mrbt"])
                        Mrkq = q4(W["Mrkt"])
                        if cut < 6:
                            return
                        bM = bank()
                        bN = bank()
                        bR = bank()
                        for q, (c, h2) in enumerate(heads):
                            cl_ = c
                            prt = slice(64 * h2, 64 * h2 + 64)
                            s.op("pe", lambda e, q=q, cl_=cl_, prt=prt: e.matmul(
                                ps[bM][prt, cl_ * 64:(cl_ + 1) * 64], Zf[:, q, 0:64], BKq[:, q, 0:64],
                                start=(q == 0), stop=False, skip_group_check=True),
                                reads=[RZf, W["RBKtm"]], writes=[Rps[bM]])
                            s.op("pe", lambda e, q=q, cl_=cl_, prt=prt: e.matmul(
                                ps[bM][prt, cl_ * 64:(cl_ + 1) * 64], ident_b[0:64, 0:64], ident_b[0:64, 0:64],
                                start=False, stop=True, skip_group_check=True),
                                reads=[Rconst], writes=[Rps[bM]])
                            s.op("pe", lambda e, q=q, cl_=cl_, prt=prt: e.matmul(
                                ps[bN][prt, cl_ * 64:(cl_ + 1) * 64], BKq[:, q, 0:64], Zf[:, q, 64:128],
                                start=(q == 0), stop=False, skip_group_check=True),
                                reads=[RZf, W["RBKtm"]], writes=[Rps[bN]])
                            s.op("pe", lambda e, q=q, cl_=cl_, prt=prt, c=c, h2=h2: e.matmul(
                                ps[bN][prt, cl_ * 64:(cl_ + 1) * 64], BKq[:, q, 64:128], Vq(c, h2),
                                start=False, stop=False, skip_group_check=True),
                                reads=[W["RBKtm"], RVtm], writes=[Rps[bN]])
                            s.op("pe", lambda e, q=q, cl_=cl_, prt=prt: e.matmul(
                                ps[bR][prt, cl_ * 128:(cl_ + 1) * 128], Zf[:, q, 0:64], Mrbq[:, q, :],
                                start=(q == 0), stop=False, skip_group_check=True),
                                reads=[RZf, W["RMrbt"]], writes=[Rps[bR]])
                            s.op("pe", lambda e, q=q, cl_=cl_, prt=prt, c=c, h2=h2: e.matmul(
                                ps[bR][prt, cl_ * 128:(cl_ + 1) * 128], ident_b[prt, prt], fm(TB["RH"], c, h2),
                                start=False, stop=True, skip_group_check=True),
                                reads=[Rconst, TB["RRH"]], writes=[Rps[bR]])
                            st_ = first_fy[0]
                            first_fy[0] = False
                            hh = c * 2 + h2
                            s.op("pe", lambda e, q=q, hh=hh, st_=st_: e.matmul(
                                ps[fy][:, hh * 64:(hh + 1) * 64], Mrbq[:, q, :], Zf[:, q, 64:128],
                                start=st_, stop=False, skip_group_check=True),
                                reads=[RZf, W["RMrbt"]], writes=[Rps[fy]])
                            s.op("pe", lambda e, q=q, hh=hh, c=c, h2=h2: e.matmul(
                                ps[fy][:, hh * 64:(hh + 1) * 64], Mrkq[:, q, :], Vq(c, h2),
                                start=False, stop=False, skip_group_check=True),
                                reads=[W["RMrkt"], RVtm], writes=[Rps[fy]])
                        MTv = W["MT"].rearrange("p (c j) -> p c j", c=4)
                        RTv = W["RT"].rearrange("p (c t) -> p c t", c=4)
                        pg = slice(64 * g, 64 * g + 64)
                        s.op("act", lambda e, MTv=MTv, pg=pg: e.copy(MTv[pg], ps[bM][pg, 0:256].rearrange("p (c j) -> p c j", c=4)),
                             reads=[Rps[bM]], writes=[W["RMT"]])
                        s.op("act", lambda e, RTv=RTv, pg=pg: e.copy(RTv[pg], ps[bR][pg, 0:512].rearrange("p (c t) -> p c t", c=4)),
                             reads=[Rps[bR]], writes=[W["RRT"]])
                        for q, (c, h2) in enumerate(heads):
                            cl_ = c
                            prt = slice(64 * h2, 64 * h2 + 64)
                            hh = c * 2 + h2
                            s.op("pe", lambda e, cl_=cl_, prt=prt, hh=hh, c=c, RTv=RTv: e.matmul(
                                ps[fy][:, hh * 64:(hh + 1) * 64], RTv[prt, cl_, :], Abd3[prt, c, :],
                                start=False, stop=True, skip_group_check=True),
                                reads=[W["RRT"], RAbf], writes=[Rps[fy]])
                            s.op("pe", lambda e, cl_=cl_, prt=prt, c=c, MTv=MTv: e.matmul(
                                ps[bN][prt, cl_ * 64:(cl_ + 1) * 64], MTv[prt, cl_, :], Abd3[prt, c, :],
                                start=False, stop=True, skip_group_check=True),
                                reads=[W["RMT"], RAbf], writes=[Rps[bN]])
                        s.op("dve", lambda e, pg=pg: e.tensor_tensor(
                            Ad3[pg, :, :], ps[bN][pg, 0:256].rearrange("p (c i) -> p c i", c=4),
                            pcs[pg, par, :].unsqueeze(2).to_broadcast([64, 4, 64]), ALU.mult),
                            reads=[Rps[bN], Rpcs[par], RAbf], writes=[RAst])
                    gens = [group_body(0), group_body(1), nxt]
                    while gens:
                        for gn in list(gens):
                            try:
                                next(gn)
                            except StopIteration:
                                gens.remove(gn)
                    if cut < 6:
                        reserved.discard(fy)
                        return
                    s.op("act", lambda e: e.copy(Abd, Ad), reads=[RAst], writes=[RAbf])
                    isend = (ti % 2 == 1) if d == 0 else (ti % 2 == 0)
                    if isend:
                        s.dma("sp", s_out[u, d], Ad, reads=[RAst])
                    reserved.discard(fy)
                    if d == 0:
                        s.op("act", lambda e, ti=ti: e.copy(YSb[:, ti, :], ps[fy][:]), reads=[Rps[fy]], writes=[RYS])
                    else:
                        finalize(ti, fy)
                for n_ in range(len(seq)):
                    tile_body(n_)

        if stage >= 1:
            if "a" in sub:
                make_gg(0)
            if "b" in sub:
                make_hT(0, 0)
            if "c" in sub:
                even_inproj()
        if stage >= 3:
            attention()
            if P0L1:
                for _ in P0L1[0]:
                    pass
                del P0L1[:]
            barrier()
            if stage >= 5:
                rwkv_setup()
                rwkv_prepass()
                rwkv_chunked(NT if stage >= 6 else 2)
            else:
                s.op("pool", lambda e: e.memset(yT[:, 4:8, :], 0.0), writes=[RyT])
            barrier()
            out_proj(w_out_even, yT, RyT)
        if stage >= 2:
            barrier()
            ffn(0)
            barrier()
        if stage >= 4:
            if P0L1:
                for _ in P0L1[0]:
                    pass
                del P0L1[:]
            make_gg(1)
            make_hT(1, 0)
            fnet()
            out_proj(w_out_odd, yT, RyT)
            barrier()
            ffn(1, emit_out=True)
        for i in range(NT if stage < 4 else 0):
            s.dma("sp", y_out[i * 128:(i + 1) * 128, :], x_sb[:, i, :], reads=[Rx[i]])
        s.emit()
    return nc


def _core_units(c):
    if c < 6:
        return [("p", 5 * c + u) for u in range(5)]
    b = c - 6
    return [("s", b, u) for u in range(4)] + [("p", 30 + b)]


def _host_prep(inp):
    f32 = np.float32
    COFF, NCOL = colp_layout()
    ROFF, NROW = rowp_layout()
    g = {k: np.asarray(v) for k, v in inp.items()}
    colp = np.zeros((128, NCOL), f32)

    def put(name, vec):
        c = cols_of(vec)
        colp[:, COFF[name]:COFF[name] + c.shape[1]] = c
    for l in range(2):
        put("bada%d" % l, g["b_ada"][l])
        for k in range(4):
            put("gain%d_%d" % (l, k), g["norm_gains"][l, k])
        for k in range(3):
            put("conv%d_%d" % (l, k), g["ffn_conv"][l, k])
        put("convb%d" % l, g["ffn_conv_b"][l])
    put("mu0", g["rwkv_shift_mu"][0, 0])
    put("mu1", g["rwkv_shift_mu"][0, 1])
    for d in range(2):
        put("w0_%d" % d, g["rwkv_w0"][0, d])
        put("a0_%d" % d, g["rwkv_a0"][0, d])
    for k in range(3):
        put("kvec%d" % k, g["rwkv_kvec"][0, k])

    cmat = np.zeros((8, 128, 128), f32)
    cmat[0] = np.eye(128, dtype=f32)
    for i in range(64):
        cmat[1][2 * i, 2 * i + 1] = 1.0
        cmat[1][2 * i + 1, 2 * i] = -1.0
    cmat[2][:64, :64] = 1.0
    cmat[2][64:, 64:] = 1.0
    cmat[3][:64, 0] = 1.0
    cmat[3][64:, 1] = 1.0
    rr_, cc_ = np.meshgrid(np.arange(128), np.arange(128), indexing="ij")
    cmat[4] = (rr_ < cc_)
    cmat[5] = (rr_ > cc_)
    cmat[6] = (rr_ <= cc_)
    cmat[7] = (rr_ >= cc_)
    cc = np.arange(128)
    chang = 2.0 * np.pi * ((cc[:, None] * cc[None, :]) % 128) / 128.0
    chCS = np.concatenate([np.cos(chang), np.sin(chang)], axis=1).astype(f32)

    inv = (10000.0 ** (-np.arange(16, dtype=np.float32) / 16)).astype(f32)

    shared = dict(
        colp=colp, w_ada=g["w_ada"], w_in_even=g["w_in_even"][0], w_out_even=g["w_out_even"][0],
        w_out_odd=g["w_out_odd"][0], w_ffn_in=g["w_ffn_in"], w_ffn_out=g["w_ffn_out"],
        w2t=np.ascontiguousarray(g["rwkv_w2"][0].reshape(128, 512)),
        a2t=np.ascontiguousarray(g["rwkv_a2"][0].reshape(128, 512)),
        g2=np.ascontiguousarray(g["rwkv_g2"][0]), chCS=chCS, cmat=cmat,
        lnx=np.ascontiguousarray(g["rwkv_lnx"][0]))
    maps = []
    for c in range(NCORES):
        units = _core_units(c)
        xs = []
        for un in units:
            if un[0] == "p":
                xs.append(g["x_prompt"][un[1]])
            else:
                xs.append(g["x_sample"][un[1], un[2] * 256:(un[2] + 1) * 256])
        x_in = np.ascontiguousarray(np.concatenate(xs, axis=0), dtype=f32)
        is_s = c >= 6
        condA = g["c"][c - 6] if is_s else g["c_ctx"]
        condB = g["c_ctx"]
        cond = np.stack([condA, condB], axis=0).astype(f32)
        condT = np.ascontiguousarray(cond.reshape(2, 8, 128).transpose(2, 1, 0).reshape(128, 16))
        rowp = np.zeros((1, NROW), f32)
        rowp[0, ROFF["lam"]:ROFF["lam"] + 256] = g["diff_lambda"][0].reshape(-1)
        rowp[0, ROFF["subln"]:ROFF["subln"] + 128] = g["diff_subln"][0]
        ab = np.full((6, 5), -30000.0, f32)
        if is_s:
            ab[0:5, 0:4] = 0.0
            ab[5, 4] = 0.0
            bndL = [1, 0, 0, 0, 1]
            bndR = [0, 0, 0, 1, 1]
        else:
            for u in range(5):
                ab[u + 1, u] = 0.0
            bndL = [1] * 5
            bndR = [1] * 5
        rowp[0, ROFF["abias"]:ROFF["abias"] + 30] = ab.reshape(-1)
        rowp[0, ROFF["bndL"]:ROFF["bndL"] + 5] = bndL
        rowp[0, ROFF["bndR"]:ROFF["bndR"] + 5] = bndR
        ropeC = np.ones((128, T), f32)
        ropeS = np.zeros((128, T), f32)
        if is_s:
            t = np.arange(1024)
            row = (t // 64).astype(f32)
            col = (t % 64).astype(f32)
            ang = np.concatenate([row[:, None] * inv[None, :], col[:, None] * inv[None, :]], axis=1).astype(f32)
            pidx = (np.arange(128) % 64) // 2
            ropeC[:, :1024] = np.cos(ang)[:, pidx].T
            ropeS[:, :1024] = np.sin(ang)[:, pidx].T
        cacheKT = np.zeros((512, 256), f32)
        cacheV = np.zeros((256, 512), f32)
        initS = np.zeros((2, 128, 256), f32)
        if is_s:
            b = c - 6
            cacheKT[:] = g["cache_k"][b, 0].reshape(256, 512).T
            cacheV[:] = g["cache_v"][b, 0].reshape(256, 512)
            st = g["state_wkv"][b, 0]
            initS[:] = st.reshape(2, 4, 2, 64, 64).transpose(0, 2, 4, 1, 3).reshape(2, 128, 256)
        dC = np.zeros((T, T), np.float64)
        dS = np.zeros((T, T), np.float64)
        blocks = [(0, 1024), (1024, 256)] if is_s else [(256 * u, 256) for u in range(5)]
        for (a0, L) in blocks:
            ll = np.arange(L)
            ang = 2.0 * np.pi * ((ll[:, None] * ll[None, :]) % L) / L
            sc = 1.0 / math.sqrt(L * 128.0)
            dC[a0:a0 + L, a0:a0 + L] = np.cos(ang) * sc
            dS[a0:a0 + L, a0:a0 + L] = -np.sin(ang) * sc
        m = dict(shared)
        m.update(x_in=x_in, condT=condT, rowp=rowp, ropeC=ropeC, ropeS=ropeS, cacheKT=cacheKT, cacheV=cacheV,
                 initS=initS, dftC=dC.astype(f32), dftSn=dS.astype(f32))
        maps.append(m)
    return maps


_NC_CACHE = {}


def kernel(**inputs):
    maps = _host_prep(inputs)
    if "nc" not in _NC_CACHE:
        _NC_CACHE["nc"] = build()
    nc = _NC_CACHE["nc"]
    import os
    ncr = int(os.environ.get("NCR", NCORES))
    res = run_bass_kernel_spmd(nc, maps[:ncr], core_ids=list(range(ncr)))
    outs = list(res.results) + [res.results[0]] * (NCORES - ncr)
    y_prompt = np.zeros((32, 256, D), np.float32)
    y_sample = np.zeros((2, 1024, D), np.float32)
    nk = np.zeros((32, 1, 256, 4, 128), np.float32)
    nv = np.zeros((32, 1, 256, 4, 128), np.float32)
    ns = np.zeros((32, 1, 2, 8, 64, 64), np.float32)
    for c in range(NCORES):
        r = outs[c]
        for u, un in enumerate(_core_units(c)):
            sl = slice(u * 256, (u + 1) * 256)
            if un[0] == "p":
                bi = un[1]
                y_prompt[bi] = r["y_out"][sl]
                nk[bi, 0] = r["k_out"][sl].reshape(256, 4, 128)
                nv[bi, 0] = r["v_out"][sl].reshape(256, 4, 128)
                stt = r["s_out"][u]
                ns[bi, 0] = stt.reshape(2, 2, 64, 4, 64).transpose(0, 3, 1, 4, 2).reshape(2, 8, 64, 64)
            else:
                y_sample[un[1], un[2] * 256:(un[2] + 1) * 256] = r["y_out"][sl]
    return (y_prompt, y_sample, nk, nv, ns)
```
